# Optimizing a Trainium2 kernel written in Bass

```python
import math
import jax, jax.numpy as jnp
from jax import lax
import numpy as np

D_MODEL = 1024
BATCH = 2
SEQ = 8192
DEPTH = 1

HEAD_DIM = 128
N_HEADS = D_MODEL // HEAD_DIM
GDN_HEADS = N_HEADS // 2
NSA_HEADS = N_HEADS - GDN_HEADS
GDN_WIDTH = GDN_HEADS * HEAD_DIM
NSA_WIDTH = NSA_HEADS * HEAD_DIM
GDN_CONV = 4
GDN_CHUNK = 64
CMP_BLOCK = 32
CMP_STRIDE = 16
SEL_BLOCK = 64
N_SELECT = 16
WINDOW = 512
Q_BLOCK = 128
D_FF = 2816
FFN_CONV = 3
N_GATES = 3
IN_SPLITS = (GDN_WIDTH, GDN_WIDTH, GDN_WIDTH, GDN_WIDTH, GDN_HEADS, GDN_HEADS,
             NSA_WIDTH, HEAD_DIM, HEAD_DIM, HEAD_DIM, HEAD_DIM, HEAD_DIM, HEAD_DIM,
             N_GATES * NSA_HEADS)
N_IN = 4 * GDN_WIDTH + 2 * GDN_HEADS + NSA_WIDTH + 6 * HEAD_DIM + N_GATES * NSA_HEADS
EPS = 1e-6
NEG_INF = -1e30
FORCE_BONUS = 1e4

kernel_name = "hybrid_gdn_nsa_convffn_block"


def rmsnorm(x, w):
    xf = x.astype(jnp.float32)
    y = xf * lax.rsqrt(jnp.mean(xf * xf, axis=-1, keepdims=True) + EPS)
    return (y * w.astype(jnp.float32)).astype(x.dtype)


def l2norm(x):
    xf = x.astype(jnp.float32)
    return xf * lax.rsqrt(jnp.sum(xf * xf, axis=-1, keepdims=True) + EPS)


def causal_dwconv(x, w, b=None):
    k = w.shape[0]
    y = lax.conv_general_dilated(x, w[:, None, :].astype(x.dtype), window_strides=(1,),
                                 padding=[(k - 1, 0)], dimension_numbers=('NWC', 'WIO', 'NWC'),
                                 feature_group_count=x.shape[-1])
    if b is not None:
        y = y + b
    return y


def masked_softmax(s, mask):
    p = jax.nn.softmax(jnp.where(mask, s, NEG_INF), axis=-1)
    return jnp.where(mask, p, 0.0)


def gated_delta_chunked(q, k, v, g, beta):
    B, S, H, dk = q.shape
    dv = v.shape[-1]
    n = S // GDN_CHUNK
    f32 = jnp.float32

    def to_chunks(t):
        t = t.astype(f32).reshape((B, n, GDN_CHUNK) + t.shape[2:])
        return jnp.moveaxis(t, 3, 1)

    q, k, v, g, beta = (to_chunks(t) for t in (q, k, v, g, beta))
    q = q * (dk ** -0.5)
    gc = jnp.cumsum(g, axis=-1)
    idx = jnp.arange(GDN_CHUNK)
    incl = idx[:, None] >= idx[None, :]
    strict = idx[:, None] > idx[None, :]
    decay = jnp.exp(jnp.where(incl, gc[..., :, None] - gc[..., None, :], NEG_INF))
    k_beta = k * beta[..., None]
    a_mat = jnp.where(strict, jnp.einsum('bhncd,bhnsd->bhncs', k_beta, k) * decay, 0.0)
    eye = jnp.eye(GDN_CHUNK, dtype=f32)
    t_mat = lax.linalg.triangular_solve(eye + a_mat, jnp.broadcast_to(eye, a_mat.shape),
                                        left_side=True, lower=True, unit_diagonal=True)
    u = t_mat @ (v * beta[..., None])
    w = t_mat @ (k_beta * jnp.exp(gc)[..., None])
    attn_intra = jnp.where(incl, jnp.einsum('bhncd,bhnsd->bhncs', q, k) * decay, 0.0)
    q_dec = q * jnp.exp(gc)[..., None]
    k_dec = k * jnp.exp(gc[..., -1:] - gc)[..., None]
    g_last = jnp.exp(gc[..., -1])

    def step(state, inp):
        u_i, w_i, qd_i, kd_i, a_i, gl_i = inp
        v_new = u_i - w_i @ state
        o_i = qd_i @ state + a_i @ v_new
        state = state * gl_i[..., None, None] + jnp.swapaxes(kd_i, -1, -2) @ v_new
        return state, o_i

    xs = tuple(jnp.moveaxis(t, 2, 0) for t in (u, w, q_dec, k_dec, attn_intra, g_last))
    state0 = jnp.zeros((B, H, dk, dv), f32)
    _, o = lax.scan(step, state0, xs)
    return jnp.transpose(o, (1, 0, 3, 2, 4)).reshape(B, S, H, dv)


def nsa_compress(kv, pos, w1, w2):
    B, S, D = kv.shape
    n_cmp = (S - CMP_BLOCK) // CMP_STRIDE + 1
    idx = jnp.arange(n_cmp)[:, None] * CMP_STRIDE + jnp.arange(CMP_BLOCK)[None, :]
    blocks = kv[:, idx] + pos
    hid = jax.nn.silu(blocks.reshape(B, n_cmp, CMP_BLOCK * D) @ w1)
    return hid @ w2


def nsa_attention(q, k_cmp, v_cmp, k_slc, v_slc, k_win, v_win, gates):
    f32 = jnp.float32
    q, k_cmp, v_cmp, k_slc, v_slc, k_win, v_win, gates = (
        t.astype(f32) for t in (q, k_cmp, v_cmp, k_slc, v_slc, k_win, v_win, gates))
    B, S, H, D = q.shape
    scale = D ** -0.5
    n_cmp = k_cmp.shape[1]
    n_sel = S // SEL_BLOCK
    n_pick = min(N_SELECT, n_sel)
    cs = jnp.arange(n_cmp) * CMP_STRIDE
    ss = jnp.arange(n_sel) * SEL_BLOCK
    ov = jnp.minimum(cs[:, None] + CMP_BLOCK, ss[None, :] + SEL_BLOCK) - jnp.maximum(cs[:, None], ss[None, :])
    overlap = jnp.clip(ov, 0, None).astype(f32) / CMP_BLOCK
    cmp_end = cs + CMP_BLOCK - 1
    k_blocks = k_slc.reshape(B, n_sel, SEL_BLOCK, D)
    v_blocks = v_slc.reshape(B, n_sel, SEL_BLOCK, D)
    k_win_pad = jnp.pad(k_win, ((0, 0), (WINDOW, 0), (0, 0)))
    v_win_pad = jnp.pad(v_win, ((0, 0), (WINDOW, 0), (0, 0)))
    blk = jnp.arange(n_sel)
    gather = jax.vmap(lambda kb, ib: kb[ib])

    def block(qi):
        t0 = qi * Q_BLOCK
        qb = lax.dynamic_slice_in_dim(q, t0, Q_BLOCK, axis=1)
        gb = lax.dynamic_slice_in_dim(gates, t0, Q_BLOCK, axis=1)
        t = t0 + jnp.arange(Q_BLOCK)
        mask_c = cmp_end[None, :] <= t[:, None]
        s_c = jnp.einsum('bqhd,bkd->bqhk', qb, k_cmp) * scale
        p_c = masked_softmax(s_c, mask_c[None, :, None, :])
        o_c = jnp.einsum('bqhk,bkd->bqhd', p_c, v_cmp)
        imp = jnp.einsum('bqhk,kj->bqj', p_c, overlap)
        cur = t // SEL_BLOCK
        valid = blk[None, :] <= cur[:, None]
        forced = (blk[None, :] == 0) | (blk[None, :] == cur[:, None]) | (blk[None, :] == cur[:, None] - 1)
        imp = jnp.where(valid[None], imp + jnp.where(forced, FORCE_BONUS, 0.0)[None], NEG_INF)
        _, sel = lax.top_k(imp, n_pick)
        ks = gather(k_blocks, sel).reshape(B, Q_BLOCK, n_pick * SEL_BLOCK, D)
        vs = gather(v_blocks, sel).reshape(B, Q_BLOCK, n_pick * SEL_BLOCK, D)
        pos_s = (sel[..., None] * SEL_BLOCK + jnp.arange(SEL_BLOCK)).reshape(B, Q_BLOCK, n_pick * SEL_BLOCK)
        mask_s = pos_s <= t[None, :, None]
        s_s = jnp.einsum('bqhd,bqkd->bqhk', qb, ks) * scale
        p_s = masked_softmax(s_s, mask_s[:, :, None, :])
        o_s = jnp.einsum('bqhk,bqkd->bqhd', p_s, vs)
        kw = lax.dynamic_slice_in_dim(k_win_pad, t0, Q_BLOCK + WINDOW, axis=1)
        vw = lax.dynamic_slice_in_dim(v_win_pad, t0, Q_BLOCK + WINDOW, axis=1)
        pos_w = t0 - WINDOW + jnp.arange(Q_BLOCK + WINDOW)
        mask_w = (pos_w[None, :] <= t[:, None]) & (pos_w[None, :] > t[:, None] - WINDOW) & (pos_w[None, :] >= 0)
        s_w = jnp.einsum('bqhd,bkd->bqhk', qb, kw) * scale
        p_w = masked_softmax(s_w, mask_w[None, :, None, :])
        o_w = jnp.einsum('bqhk,bkd->bqhd', p_w, vw)
        return gb[..., 0:1] * o_c + gb[..., 1:2] * o_s + gb[..., 2:3] * o_w

    out = lax.map(block, jnp.arange(S // Q_BLOCK))
    return jnp.moveaxis(out, 0, 1).reshape(B, S, H, D)


def hybrid_layer(x, c, ada_w, ada_b, norm1_w, w_in, gdn_conv_w, gdn_A_log, gdn_dt_bias,
                 gdn_out_norm_w, nsa_q_norm_w, nsa_k_norm_cmp, nsa_k_norm_slc, nsa_k_norm_win,
                 cmp_k_pos, cmp_k_w1, cmp_k_w2, cmp_v_pos, cmp_v_w1, cmp_v_w2, w_out,
                 norm2_w, ffn_w_up, ffn_conv_w, ffn_conv_b, ffn_w_down):
    f32 = jnp.float32
    B, S, _ = x.shape
    mod = jax.nn.silu(c) @ ada_w + ada_b
    shift1, scale1, gate1, shift2, scale2, gate2 = jnp.split(mod[:, None, :], 6, axis=-1)

    h = rmsnorm(x, norm1_w) * (1 + scale1) + shift1
    proj = h @ w_in
    offsets = np.cumsum(IN_SPLITS)[:-1].tolist()
    (gq, gk, gv, gz, ga, gbeta, nq, kc, vc, ksl, vsl, kwn, vwn, ng) = jnp.split(proj, offsets, axis=-1)

    qkv = jax.nn.silu(causal_dwconv(jnp.concatenate([gq, gk, gv], axis=-1), gdn_conv_w))
    gq, gk, gv = jnp.split(qkv, 3, axis=-1)
    hd = (B, S, GDN_HEADS, HEAD_DIM)
    q_g = l2norm(gq.reshape(hd))
    k_g = l2norm(gk.reshape(hd))
    v_g = gv.reshape(hd)
    beta = jax.nn.sigmoid(gbeta.astype(f32))
    log_decay = -jnp.exp(gdn_A_log.astype(f32)) * jax.nn.softplus(ga.astype(f32) + gdn_dt_bias.astype(f32))
    o_g = gated_delta_chunked(q_g, k_g, v_g, log_decay, beta)
    o_g = rmsnorm(o_g, gdn_out_norm_w) * jax.nn.silu(gz.reshape(hd).astype(f32))

    q_n = rmsnorm(nq.reshape(B, S, NSA_HEADS, HEAD_DIM), nsa_q_norm_w)
    k_cmp = rmsnorm(nsa_compress(kc, cmp_k_pos, cmp_k_w1, cmp_k_w2), nsa_k_norm_cmp)
    v_cmp = nsa_compress(vc, cmp_v_pos, cmp_v_w1, cmp_v_w2)
    k_slc = rmsnorm(ksl, nsa_k_norm_slc)
    k_win = rmsnorm(kwn, nsa_k_norm_win)
    gates = jax.nn.sigmoid(ng.reshape(B, S, NSA_HEADS, N_GATES))
    o_n = nsa_attention(q_n, k_cmp, v_cmp, k_slc, vsl, k_win, vwn, gates)

    mixed = jnp.concatenate([o_g.reshape(B, S, GDN_WIDTH), o_n.reshape(B, S, NSA_WIDTH)], axis=-1)
    x = x + gate1 * (mixed.astype(x.dtype) @ w_out)

    h2 = rmsnorm(x, norm2_w) * (1 + scale2) + shift2
    up = causal_dwconv(h2 @ ffn_w_up, ffn_conv_w, ffn_conv_b)
    a, b = jnp.split(up, 2, axis=-1)
    x = x + gate2 * ((jax.nn.silu(a) * b) @ ffn_w_down)
    return x


def setup_inputs(seed: int = 0) -> dict:
    key = jax.random.key(seed)
    ks = jax.random.split(key, 32)
    f32 = jnp.float32
    L = DEPTH

    def nrm(k, shape, s):
        return jax.random.normal(k, shape, f32) * s

    def gain(k, n):
        return 1.0 + 0.02 * jax.random.normal(k, (L, n), f32)

    dt = jnp.exp(jax.random.uniform(ks[8], (L, GDN_HEADS), f32, math.log(1e-3), math.log(1e-1)))
    return {
        'x': nrm(ks[0], (BATCH, SEQ, D_MODEL), 1.0),
        'c': nrm(ks[1], (BATCH, D_MODEL), 1.0),
        'ada_w': nrm(ks[2], (L, D_MODEL, 6 * D_MODEL), 0.5 * D_MODEL ** -0.5),
        'ada_b': nrm(ks[3], (L, 6 * D_MODEL), 0.02),
        'norm1_w': gain(ks[4], D_MODEL),
        'w_in': nrm(ks[5], (L, D_MODEL, N_IN), D_MODEL ** -0.5),
        'gdn_conv_w': nrm(ks[6], (L, GDN_CONV, 3 * GDN_WIDTH), GDN_CONV ** -0.5),
        'gdn_A_log': jnp.log(jax.random.uniform(ks[7], (L, GDN_HEADS), f32, 1.0, 16.0)),
        'gdn_dt_bias': dt + jnp.log(-jnp.expm1(-dt)),
        'gdn_out_norm_w': gain(ks[9], HEAD_DIM),
        'nsa_q_norm_w': gain(ks[10], HEAD_DIM),
        'nsa_k_norm_cmp': gain(ks[11], HEAD_DIM),
        'nsa_k_norm_slc': gain(ks[12], HEAD_DIM),
        'nsa_k_norm_win': gain(ks[13], HEAD_DIM),
        'cmp_k_pos': nrm(ks[14], (L, CMP_BLOCK, HEAD_DIM), 0.02),
        'cmp_k_w1': nrm(ks[15], (L, CMP_BLOCK * HEAD_DIM, HEAD_DIM), (CMP_BLOCK * HEAD_DIM) ** -0.5),
        'cmp_k_w2': nrm(ks[16], (L, HEAD_DIM, HEAD_DIM), HEAD_DIM ** -0.5),
        'cmp_v_pos': nrm(ks[17], (L, CMP_BLOCK, HEAD_DIM), 0.02),
        'cmp_v_w1': nrm(ks[18], (L, CMP_BLOCK * HEAD_DIM, HEAD_DIM), (CMP_BLOCK * HEAD_DIM) ** -0.5),
        'cmp_v_w2': nrm(ks[19], (L, HEAD_DIM, HEAD_DIM), HEAD_DIM ** -0.5),
        'w_out': nrm(ks[20], (L, D_MODEL, D_MODEL), D_MODEL ** -0.5),
        'norm2_w': gain(ks[21], D_MODEL),
        'ffn_w_up': nrm(ks[22], (L, D_MODEL, 2 * D_FF), D_MODEL ** -0.5),
        'ffn_conv_w': nrm(ks[23], (L, FFN_CONV, 2 * D_FF), FFN_CONV ** -0.5),
        'ffn_conv_b': nrm(ks[24], (L, 2 * D_FF), 0.02),
        'ffn_w_down': nrm(ks[25], (L, D_FF, D_MODEL), D_FF ** -0.5),
    }


def reference(x, c, ada_w, ada_b, norm1_w, w_in, gdn_conv_w, gdn_A_log, gdn_dt_bias,
              gdn_out_norm_w, nsa_q_norm_w, nsa_k_norm_cmp, nsa_k_norm_slc, nsa_k_norm_win,
              cmp_k_pos, cmp_k_w1, cmp_k_w2, cmp_v_pos, cmp_v_w1, cmp_v_w2, w_out,
              norm2_w, ffn_w_up, ffn_conv_w, ffn_conv_b, ffn_w_down):
    for l in range(DEPTH):
        x = hybrid_layer(x, c, ada_w[l], ada_b[l], norm1_w[l], w_in[l], gdn_conv_w[l], gdn_A_log[l],
                         gdn_dt_bias[l], gdn_out_norm_w[l], nsa_q_norm_w[l], nsa_k_norm_cmp[l],
                         nsa_k_norm_slc[l], nsa_k_norm_win[l], cmp_k_pos[l], cmp_k_w1[l], cmp_k_w2[l],
                         cmp_v_pos[l], cmp_v_w1[l], cmp_v_w2[l], w_out[l], norm2_w[l], ffn_w_up[l],
                         ffn_conv_w[l], ffn_conv_b[l], ffn_w_down[l])
    return x
```

```python
import numpy as np
from contextlib import ExitStack
import concourse.bass as bass
import concourse.mybir as mybir
from concourse.bass_utils import run_bass_kernel_spmd

F32 = mybir.dt.float32
BF16 = mybir.dt.bfloat16
AF = mybir.ActivationFunctionType
ALU = mybir.AluOpType

D = 1024
S_LEN = 8192
NT = 16
NB = 64
N_IN = 3348
DFF = 2816
EPS = 1e-6
O_GQ, O_GK, O_GV, O_GZ, O_GA, O_GB, O_NQ, O_KC, O_VC, O_KSL, O_VSL, O_KWN, O_VWN, O_NG = (
    0, 512, 1024, 1536, 2048, 2052, 2056, 2568, 2696, 2824, 2952, 3080, 3208, 3336)
BIG = 30000.0


class Buf:
    __slots__ = ("name", "w", "rs", "const", "excl")

    def __init__(self, name, const=False, excl=False):
        self.name = name
        self.w = None
        self.rs = []
        self.const = const
        self.excl = excl


class V:
    __slots__ = ("ap", "bs")

    def __init__(self, ap, bs):
        self.ap = ap
        self.bs = bs if isinstance(bs, (list, tuple)) else [bs]

    def bitcast(self, dt):
        return V(self.ap.bitcast(dt), self.bs)

    def re(self, pat, **kw):
        return V(self.ap.rearrange(pat, **kw), self.bs)

    def bc(self, shape):
        return V(self.ap.broadcast_to(shape), self.bs)

    def __getitem__(self, k):
        return V(self.ap[k], self.bs)


class Tl:
    def __init__(self, t, b):
        self.t = t
        self.b = b

    def __getitem__(self, k):
        return V(self.t[k], self.b)


class Op:
    __slots__ = ("eng", "fn", "deps", "dmaw", "signal", "sigval", "grp")


class DGroup:
    def __init__(self, name, sem, full=False):
        self.name = name
        self.sem = sem
        self.count = 0
        self.full = full


ENGS = ("pe", "act", "dve", "pool", "sp")


class Sched:
    def __init__(self, nc, es):
        self.nc = nc
        self.es = es
        self.eng = {"pe": nc.tensor, "act": nc.scalar, "dve": nc.vector, "pool": nc.gpsimd, "sp": nc.sync}
        self.sem = {e: es.enter_context(nc.semaphore("s_" + e)) for e in ENGS}
        self.cnt = {e: 0 for e in ENGS}
        self.seen = {e: {} for e in ENGS}
        self.ops = []
        self.bufs = []
        self.groups = []
        self.nins = 0

    def buf(self, name, const=False, excl=False):
        b = Buf(name, const, excl)
        self.bufs.append(b)
        return b

    def group(self, name, full=False):
        g = DGroup(name, self.es.enter_context(self.nc.semaphore("g_" + name)), full)
        self.groups.append(g)
        return g

    def op(self, eng, fn, r=(), w=(), grp=None):
        o = Op()
        o.eng = eng
        o.fn = fn
        o.signal = False
        o.sigval = None
        o.grp = grp
        deps = []
        for b in r:
            if b.w is not None:
                deps.append(b.w)
            if b.excl:
                deps.extend(x for x in b.rs if x.eng != eng)
        for b in w:
            if b.w is not None:
                deps.append(b.w)
            deps.extend(b.rs)
        seen = set()
        dd = []
        for d in deps:
            if id(d) in seen or d is o:
                continue
            seen.add(id(d))
            if d.eng == "pe" and eng == "pe":
                continue
            if grp is not None and grp.full and d.grp is grp:
                continue
            dd.append(d)
        o.deps = dd
        o.dmaw = {}
        for d in dd:
            if d.grp is None:
                d.signal = True
            else:
                o.dmaw[d.grp.name] = d.grp.count
        if grp is not None:
            grp.count += 1
        for b in w:
            b.w = o
            b.rs = []
        for b in r:
            if not b.const and b.w is not o:
                b.rs.append(o)
        self.ops.append(o)
        return o

    def flush(self, barrier=True):
        if barrier:
            last = {}
            for o in self.ops:
                last[o.eng] = o
            for e, o in last.items():
                if o.grp is None:
                    o.signal = True
        for o in self.ops:
            e = self.eng[o.eng]
            for d in o.deps:
                if d.grp is not None:
                    key = "g_" + d.grp.name
                    val = 16 * (d.grp.count if d.grp.full else o.dmaw[d.grp.name])
                    sem = d.grp.sem
                else:
                    key = d.eng
                    val = d.sigval
                    sem = self.sem[d.eng]
                    assert val is not None, (o.eng, d.eng)
                if self.seen[o.eng].get(key, 0) >= val:
                    continue
                e.wait_ge(sem, val)
                self.seen[o.eng][key] = val
            ins = o.fn(e)
            self.nins += 1
            if o.grp is not None:
                ins.then_inc(o.grp.sem, 16)
            elif o.signal:
                self.cnt[o.eng] += 1
                o.sigval = self.cnt[o.eng]
                ins.then_inc(self.sem[o.eng], 1)
        self.ops = []
        if barrier:
            for en in ENGS:
                e = self.eng[en]
                for e2 in ENGS:
                    if e2 == en or e2 == "sp":
                        continue
                    if self.cnt[e2] > self.seen[en].get(e2, 0):
                        e.wait_ge(self.sem[e2], self.cnt[e2])
                        self.seen[en][e2] = self.cnt[e2]
                for g in self.groups:
                    key = "g_" + g.name
                    if 16 * g.count > self.seen[en].get(key, 0):
                        e.wait_ge(g.sem, 16 * g.count)
                        self.seen[en][key] = 16 * g.count
            for b in self.bufs:
                b.w = None
                b.rs = []

    @staticmethod
    def _bs(*vs):
        out = []
        for v in vs:
            if isinstance(v, V):
                for b in v.bs:
                    if b not in out:
                        out.append(b)
        return out

    @staticmethod
    def _a(v):
        return v.ap if isinstance(v, V) else v

    def mm(self, out, lhsT, rhs, start=True, stop=True):
        o_, l_, r_ = out.ap, lhsT.ap, rhs.ap
        self.op("pe", lambda e: e.matmul(o_, lhsT=l_, rhs=r_, start=start, stop=stop),
                r=self._bs(lhsT, rhs), w=self._bs(out))

    def mm_(self, out, lhsT, rhs, start=True, stop=True, skip=False):
        o_, l_, r_ = out.ap, lhsT.ap, rhs.ap
        self.op("pe", lambda e: e.matmul(o_, lhsT=l_, rhs=r_, start=start, stop=stop, skip_group_check=skip),
                r=self._bs(lhsT, rhs), w=self._bs(out))

    def tr(self, out, in_, ident):
        o_, i_, d_ = out.ap, in_.ap, ident.ap
        self.op("pe", lambda e: e.transpose(o_, i_, d_), r=self._bs(in_, ident), w=self._bs(out))

    def act(self, out, in_, func, bias=None, scale=None, accum=None, eng="act"):
        kw = {}
        if bias is not None:
            kw["bias"] = self._a(bias)
        if scale is not None:
            kw["scale"] = self._a(scale)
        if accum is not None:
            kw["accum_out"] = accum.ap
        o_, i_ = out.ap, in_.ap
        self.op("act", lambda e: e.activation(out=o_, in_=i_, func=func, **kw),
                r=self._bs(in_, bias, scale), w=self._bs(out, accum))

    def tt(self, eng, out, in0, in1, op):
        o_, a_, b_ = out.ap, in0.ap, in1.ap
        self.op(eng, lambda e: e.tensor_tensor(out=o_, in0=a_, in1=b_, op=op),
                r=self._bs(in0, in1), w=self._bs(out))

    def ts(self, eng, out, in0, s1, s2=None, op0=ALU.mult, op1=None):
        o_, a_ = out.ap, in0.ap
        s1_, s2_ = self._a(s1), self._a(s2)
        kw = {}
        if op1 is not None:
            kw["op1"] = op1
        self.op(eng, lambda e: e.tensor_scalar(out=o_, in0=a_, scalar1=s1_, scalar2=s2_, op0=op0, **kw),
                r=self._bs(in0, s1, s2), w=self._bs(out))

    def stt(self, out, in0, scalar, in1, op0, op1):
        o_, a_, b_ = out.ap, in0.ap, in1.ap
        s_ = self._a(scalar)
        self.op("dve", lambda e: e.scalar_tensor_tensor(out=o_, in0=a_, scalar=s_, in1=b_, op0=op0, op1=op1),
                r=self._bs(in0, scalar, in1), w=self._bs(out))

    def copy(self, eng, out, in_):
        o_, i_ = out.ap, in_.ap
        if eng == "act":
            self.op("act", lambda e: e.copy(out=o_, in_=i_), r=self._bs(in_), w=self._bs(out))
        else:
            self.op(eng, lambda e: e.tensor_copy(out=o_, in_=i_), r=self._bs(in_), w=self._bs(out))

    def recip(self, out, in_):
        o_, i_ = out.ap, in_.ap
        self.op("dve", lambda e: e.reciprocal(out=o_, in_=i_), r=self._bs(in_), w=self._bs(out))

    def memset(self, eng, out, val):
        o_ = out.ap
        self.op(eng, lambda e: e.memset(o_, val), r=[], w=self._bs(out))

    def dma(self, out, in_, grp, eng="sp"):
        o_, i_ = self._a(out), self._a(in_)
        self.op(eng, lambda e: e.dma_start(out=o_, in_=i_), r=self._bs(in_), w=self._bs(out), grp=grp)


class Bank:
    def __init__(self, t, q):
        self.t = t
        self.q = q

    def f(self, c0, c1, p0=0, p1=128):
        return V(self.t[p0:p1, c0:c1], self.q[0:1])

    def b(self, c0, c1, p0=0, p1=128):
        return V(self.t[p0:p1, :].bitcast(BF16)[:, c0:c1], self.q[0:1])


W1_SEGS = ((0, 1536, 0), (2048, 2056, 1536), (2568, 3336, 1544))
W1_N = 2312
C_AB, C_KC, C_VC, C_KSL, C_VSL, C_KWN, C_VWN = 1536, 1544, 1672, 1800, 1928, 2056, 2184


class Builder:
    def __init__(self, debug=False, ntiles=NT):
        self.debug = debug
        self.ntiles = ntiles
        import os
        self.stop = int(os.environ.get('K_STOP', '0'))
        self.var = int(os.environ.get('K_VAR', '0'))
        self.halo = int(os.environ.get('K_HALO', '0'))
        self.nc = bass.Bass("TRN2", target_bir_lowering=False)
        self.din = {}
        self.dout = {}

    def inp(self, name, shape, dt=F32):
        self.din[name] = self.nc.dram_tensor(name, list(shape), dt, kind="ExternalInput").ap()
        return self.din[name]

    def outp(self, name, shape, dt=F32):
        self.dout[name] = self.nc.dram_tensor(name, list(shape), dt, kind="ExternalOutput").ap()
        return self.dout[name]

    def scratch(self, name, shape, dt=F32):
        if self.debug:
            return self.outp(name, shape, dt)
        return self.nc.dram_tensor(name, list(shape), dt, kind="Internal").ap()

    def sb(self, es, name, shape, dt=F32, const=False):
        t = es.enter_context(self.nc.sbuf_tensor(name, list(shape), dt))
        return Tl(t, self.S.buf(name, const))

    def cload(self, es, name, shape, grp):
        d = self.inp(name, shape)
        t = self.sb(es, "c_" + name, shape, F32, const=True)
        idx = tuple(slice(None) for _ in shape)
        self.S.dma(t[idx], d[idx], grp)
        return t

    def build(self, upto=9):
        nc = self.nc
        I = self.inp
        self.xb = I("xb", [S_LEN, D])
        self.xo = I("xo", [2048, D])
        cT = I("cT", [128, 8])
        ada_w = I("ada_w", [D, 6 * D])
        self.w_in = I("w_in", [D, N_IN])
        self.out = self.outp("out", [2048, D])
        self.o_own = self.scratch("o_own", [16, 128, 512])
        self.kslT_d = self.scratch("kslT_d", [128, S_LEN], BF16)
        self.kwnT_d = self.scratch("kwnT_d", [128, S_LEN], BF16)
        self.kcT_d = self.scratch("kcT_d", [128, S_LEN], BF16)
        self.vcT_d = self.scratch("vcT_d", [128, S_LEN], BF16)
        self.vsl_d = self.scratch("vsl_d", [128, NB, 128], BF16)
        self.vwn_d = self.scratch("vwn_d", [128, NB, 128], BF16)
        self.mix_d = self.scratch("mix_d", [128, 8, 2048], BF16)
        self.oprev_d = self.scratch("oprev_d", [16, 2, 512])
        self.mixh_d = self.scratch("mixh_d", [128, 8, 32], BF16)

        with ExitStack() as top:
            S = self.S = Sched(nc, top)
            self.g_const = g_const = S.group("const", full=True)
            self.g_out = S.group("outw")
            self.idf = idf = self.cload(top, "identf", [128, 128], g_const)
            self.idb = idb = self.sb(top, "idb", [128, 128], BF16)
            S.copy("dve", idb[:, :], idf[:, :])
            self.modT = modT = self.sb(top, "modT", [128, 48])
            self.s1 = s1 = self.sb(top, "s1", [128, 8])
            self.s2 = s2 = self.sb(top, "s2", [128, 8])
            n1 = self.cload(top, "n1w", [128, 8], g_const)
            n2 = self.cload(top, "n2w", [128, 8], g_const)
            self.pb = []
            for i in range(8):
                t = top.enter_context(nc.psum_tensor(f"pb{i}", [128, 512], F32))
                self.pb.append(Bank(t, [S.buf(f"pb{i}", excl=True)]))
            pb = self.pb
            self.bankA = [pb[2], pb[3]]
            self.bankB = [pb[4], pb[5]]
            self.bankC = [pb[0], pb[1]]

            with ExitStack() as es:
                ct = self.sb(es, "ct", [128, 8])
                sc = self.sb(es, "sc", [128, 8])
                abT = self.cload(es, "ada_bT", [128, 48], g_const)
                S.dma(ct[:, :], cT[:, :], g_const)
                S.act(sc[:, :], ct[:, :], AF.Silu)
                aw = [self.sb(es, f"aw{i}", [128, 6 * D]) for i in range(2)]
                g_aw = [S.group(f"aw{i}") for i in range(2)]
                for k in range(8):
                    sl = k % 2
                    for hh in range(4):
                        S.dma(aw[sl][:, hh * 1536:(hh + 1) * 1536],
                              ada_w[k * 128:(k + 1) * 128, hh * 1536:(hh + 1) * 1536], g_aw[sl])
                    pm = pb[k % 2]
                    for cc in range(48):
                        S.mm(pm.f(cc, cc + 1), lhsT=aw[sl][:, cc * 128:(cc + 1) * 128], rhs=sc[:, k:k + 1])
                    S.tt("dve", modT[:, :], pm.f(0, 48), (abT if k == 0 else modT)[:, :], ALU.add)
                S.stt(s1[:, :], modT[:, 8:16], 1.0, n1[:, :], ALU.add, ALU.mult)
                S.stt(s2[:, :], modT[:, 32:40], 1.0, n2[:, :], ALU.add, ALU.mult)
                S.flush()

            if upto >= 1:
                self.phase1()
            keep = dict(qnT=self.sb(top, "qnT", [128, 4, 2048], BF16), gates=self.sb(top, "gates", [128, 16, 12]),
                        qnTh=self.sb(top, "qnTh", [128, 4, 32], BF16), gatesh=self.sb(top, "gatesh", [32, 12]))
            if upto >= 2:
                self.phase2(keep)
                if self.debug:
                    S.dma(self.outp("d_qnT", [128, 4, 2048], BF16)[:, :, :], keep["qnT"][:, :, :], self.g_out)
                    S.dma(self.outp("d_gates", [128, 16, 12])[:, :, :], keep["gates"][:, :, :], self.g_out)
            if upto >= 3:
                S.flush()
                self.phase3(keep)
            if upto >= 4:
                S.flush()
                if self.debug:
                    self.dx1 = self.outp("d_x1", [4, 128, 4, D])
                self.phase45()
            S.flush()
        return nc

    def front(self, F, xt, nb, hT, c0, scol, sh_c0, b0=0):
        S = self.S
        pb = self.pb
        ssq, rt, rstd, junk, xs = F["ssq"], F["rt"], F["rstd"], F["junk"], F["xs"]
        for b2 in range(nb):
            S.act(junk[:, :], xt[:, b0 + b2, :], AF.Square, accum=ssq[:, b2:b2 + 1])
        S.act(rt[:, 0:nb], ssq[:, 0:nb], AF.Sqrt, bias=EPS, scale=1.0 / D)
        S.recip(rstd[:, 0:nb], rt[:, 0:nb])
        for b2 in range(nb):
            S.ts("pool", xs[:, b2, :], xt[:, b0 + b2, :], rstd[:, b2:b2 + 1], 1.0, ALU.mult, ALU.mult)
        w = nb * 128
        for half in range(2):
            bank = pb[half]
            for k in range(half * 4, half * 4 + 4):
                off = (k % 4) * 256
                for b2 in range(nb):
                    S.tr(bank.b(off + b2 * 128, off + (b2 + 1) * 128), xs[:, b2, k * 128:(k + 1) * 128],
                         self.idb[:, :])
            for k in range(half * 4, half * 4 + 4):
                off = (k % 4) * 256
                S.act(hT[:, k, c0:c0 + w], bank.b(off, off + w), AF.Identity,
                      bias=self.modT[:, sh_c0 + k:sh_c0 + k + 1], scale=scol[:, k:k + 1])

    def front32(self, F, xt, hT, scol, sh_c0):
        S = self.S
        bank = self.pb[0]
        ssq, rt, rstd, junk, xs = F["ssq"], F["rt"], F["rstd"], F["junk"], F["xs"]
        S.act(junk[0:32, :], xt[:, :], AF.Square, accum=ssq[0:32, 0:1])
        S.act(rt[0:32, 0:1], ssq[0:32, 0:1], AF.Sqrt, bias=EPS, scale=1.0 / D)
        S.recip(rstd[0:32, 0:1], rt[0:32, 0:1])
        S.ts("pool", xs[0:32, 0, :], xt[:, :], rstd[0:32, 0:1], 1.0, ALU.mult, ALU.mult)
        for k in range(8):
            S.tr(bank.b(k * 32, (k + 1) * 32), xs[0:32, 0, k * 128:(k + 1) * 128], self.idb[0:32, 0:32])
        for k in range(8):
            S.act(hT[:, k, :], bank.b(k * 32, (k + 1) * 32), AF.Identity,
                  bias=self.modT[:, sh_c0 + k:sh_c0 + k + 1], scale=scol[:, k:k + 1])

    def phase1(self):
        S = self.S
        nc = self.nc
        pb = self.pb
        gc_ = S.group("const1", full=True)
        idb, idf = self.idb, self.idf
        with ExitStack() as es:
            tri2 = self.cload(es, "tri2", [128, 128], gc_)
            blk2 = self.cload(es, "blk2", [128, 128], gc_)
            onesf = self.cload(es, "onesf", [128, 128], gc_)
            cind = self.cload(es, "cind", [128, 2], gc_)
            ma4 = self.cload(es, "ma4", [128, 512], gc_)
            mb4 = self.cload(es, "mb4", [128, 512], gc_)
            sel = self.cload(es, "sel", [128, 4], gc_)
            hsel = self.cload(es, "hsel", [128, 4], gc_)
            cw = self.cload(es, "gcw", [128, 48], gc_)
            dtb = self.cload(es, "dtb", [128, 4], gc_)
            alog = self.cload(es, "alog", [128, 4], gc_)
            kslw = self.cload(es, "kslw", [128, 1], gc_)
            kwnw = self.cload(es, "kwnw", [128, 1], gc_)
            negones = self.sb(es, "negones", [128, 128])
            S.ts("dve", negones[:, :], onesf[:, :], -1.0, None, ALU.mult)
            onesb = self.sb(es, "onesb", [128, 128], BF16)
            S.copy("dve", onesb[:, :], onesf[:, :])
            i4b = self.sb(es, "i4b", [128, 512], BF16)
            for h in range(4):
                S.copy("dve", i4b[:, h * 128:(h + 1) * 128], idf[:, :])
            negA = self.sb(es, "negA", [128, 4])
            S.act(negA[:, :], alog[:, :], AF.Exp)
            S.ts("dve", negA[:, :], negA[:, :], -1.0, None, ALU.mult)

            winb = self.sb(es, "winb", [128, 8, W1_N], BF16)
            with ExitStack() as es2:
                wst = [self.sb(es2, f"wst{i}", [128, W1_N]) for i in range(2)]
                g_w = [S.group(f"wst{i}") for i in range(2)]
                for k in range(8):
                    sl = k % 2
                    for (a, b_, o) in W1_SEGS:
                        S.dma(wst[sl][:, o:o + (b_ - a)], self.w_in[k * 128:(k + 1) * 128, a:b_], g_w[sl])
                    S.copy("dve", winb[:, k, 0:1024], wst[sl][:, 0:1024])
                    S.copy("pool", winb[:, k, 1024:W1_N], wst[sl][:, 1024:W1_N])
                S.flush()

            F = dict(ssq=self.sb(es, "f_ssq", [128, 4]), rt=self.sb(es, "f_rt", [128, 4]),
                     rstd=self.sb(es, "f_rstd", [128, 4]), junk=self.sb(es, "f_junk", [128, 1024], BF16),
                     xs=self.sb(es, "f_xs", [128, 2, 1024], BF16))
            xt = [self.sb(es, f"xt{i}", [128, 2, 1024]) for i in range(2)]
            g_x = [S.group(f"xt{i}") for i in range(2)]
            hT = self.sb(es, "hT", [128, 8, 512], BF16)
            pre = [self.sb(es, f"pre{i}", [128, 515]) for i in range(2)]
            hist = self.sb(es, "hist", [128, 12, 3])
            S.memset("pool", hist[:, :, :], 0.0)
            acc = [self.sb(es, f"cacc{i}", [128, 512]) for i in range(2)]
            yf = [self.sb(es, f"yf{i}", [128, 512]) for i in range(2)]
            sq = [self.sb(es, f"sq{i}", [128, 512], BF16) for i in range(2)]
            rtt = [self.sb(es, f"rtt{i}", [128, 512]) for i in range(2)]
            qT = self.sb(es, "qT", [128, 4, 512], BF16)
            kT = self.sb(es, "kT", [128, 4, 512], BF16)
            vT = self.sb(es, "vT", [128, 4, 512], BF16)
            st_f = [[self.sb(es, f"stf{c}_{i}", [128, 512], BF16) for i in range(2)] for c in range(4)]
            st_v = [[self.sb(es, f"stv{c}_{i}", [128, 4, 128], BF16) for i in range(2)] for c in range(2)]
            g_sf = [[S.group(f"sf{c}_{i}") for i in range(2)] for c in range(4)]
            g_sv = [[S.group(f"sv{c}_{i}") for i in range(2)] for c in range(2)]
            ab = self.sb(es, "ab", [128, 4, 8])
            S32 = self.sb(es, "S32", [128, 4, 128])
            Sb = self.sb(es, "Sb", [128, 4, 128], BF16)
            S.memset("pool", S32[:, :, :], 0.0)
            S.memset("pool", Sb[:, :, :], 0.0)
            oacc = [self.sb(es, f"oacc{i}", [128, 512]) for i in range(2)]
            g_o = [S.group(f"oacc{i}") for i in range(2)]
            oph = [self.sb(es, f"oph{i}", [128, 512]) for i in range(2)]
            g_oh = [S.group(f"oph{i}") for i in range(2)]
            G = []
            for s_ in range(2):
                g = {}
                for nm, shp, dt in (("sm", [128, 64], F32), ("TG", [128, 4, 128], F32), ("Xs", [128, 512], F32),
                                    ("XA", [128, 512], F32), ("XB", [128, 512], F32), ("EG", [128, 4, 128], F32),
                                    ("Nb", [128, 4, 128], BF16), ("aT", [128, 4, 128], BF16),
                                    ("qdT", [128, 4, 128], BF16), ("kw", [128, 4, 128], BF16),
                                    ("kd", [128, 4, 128], BF16), ("vb", [128, 4, 128], BF16),
                                    ("NTb", [128, 4, 128], BF16), ("RTb", [128, 4, 128], BF16),
                                    ("P0", [128, 4, 128], BF16), ("P1", [128, 4, 128], BF16),
                                    ("Q0", [128, 4, 128], BF16), ("Q1", [128, 4, 128], BF16),
                                    ("u", [128, 4, 128], F32), ("wTb", [128, 4, 128], BF16),
                                    ("vn", [128, 4, 128], BF16)):
                    g[nm] = self.sb(es, f"g{s_}_{nm}", shp, dt)
                G.append(g)

            xrows = self.xb.rearrange("(n b p) d -> n p b d", b=2, p=128)

            def load_x(n):
                S.dma(xt[n % 2][:, :, :], xrows[n], g_x[n % 2])

            load_x(0)
            for ti in range(self.ntiles):
                for hf in range(2):
                    n = 2 * ti + hf
                    if n + 1 < 2 * NT:
                        load_x(n + 1)
                    self.front(F, xt[n % 2], 2, hT, hf * 256, self.s1, 0)
                if self.stop == 1:
                    continue
                pi = 0
                for c in range(12):
                    bank = pb[2 + (pi % 2)]
                    pi += 1
                    for k in range(8):
                        S.mm(bank.f(0, 512), lhsT=winb[:, k, c * 128:(c + 1) * 128], rhs=hT[:, k, :],
                             start=(k == 0), stop=(k == 7))
                    p_ = pre[c % 2]
                    S.copy("pool", p_[:, 0:3], hist[:, c, :])
                    S.copy("act", p_[:, 3:515], bank.f(0, 512))
                    S.copy("pool", hist[:, c, :], p_[:, 512:515])
                    a_ = acc[c % 2]
                    S.ts("pool", a_[:, :], p_[:, 0:512], cw[:, c * 4:c * 4 + 1], 1.0, ALU.mult, ALU.mult)
                    for j in range(1, 4):
                        S.stt(a_[:, :], p_[:, j:j + 512], cw[:, c * 4 + j:c * 4 + j + 1], a_[:, :], ALU.mult, ALU.add)
                    h = c % 4
                    if c >= 8:
                        S.act(vT[:, h, :], a_[:, :], AF.Silu)
                    else:
                        y_ = yf[c % 2]
                        S.act(y_[:, :], a_[:, :], AF.Silu)
                        S.act(sq[c % 2][:, :], y_[:, :], AF.Square)
                        bk2 = pb[4 + (c % 2)]
                        S.mm(bk2.f(0, 512), lhsT=onesb[:, :], rhs=sq[c % 2][:, :])
                        r_ = rtt[c % 2]
                        if c < 4:
                            S.act(r_[:, :], bk2.f(0, 512), AF.Sqrt, bias=EPS * 128.0, scale=128.0)
                        else:
                            S.act(r_[:, :], bk2.f(0, 512), AF.Sqrt, bias=EPS, scale=1.0)
                        S.recip(r_[:, :], r_[:, :])
                        S.tt("pool", (qT if c < 4 else kT)[:, h, :], y_[:, :], r_[:, :], ALU.mult)
                if self.stop == 2:
                    continue
                for ci, (coff, dst, wcol) in enumerate(((C_KC, self.kcT_d, None), (C_VC, self.vcT_d, None),
                                                       (C_KSL, self.kslT_d, kslw), (C_KWN, self.kwnT_d, kwnw))):
                    bank = pb[2 + (pi % 2)]
                    pi += 1
                    for k in range(8):
                        S.mm(bank.f(0, 512), lhsT=winb[:, k, coff:coff + 128], rhs=hT[:, k, :],
                             start=(k == 0), stop=(k == 7))
                    st = st_f[ci][ti % 2]
                    if wcol is None:
                        S.copy("act", st[:, :], bank.f(0, 512))
                    else:
                        y_ = yf[ci % 2]
                        S.copy("act", y_[:, :], bank.f(0, 512))
                        S.act(sq[ci % 2][:, :], bank.f(0, 512), AF.Square)
                        bk2 = pb[4 + (ci % 2)]
                        S.mm(bk2.f(0, 512), lhsT=onesb[:, :], rhs=sq[ci % 2][:, :])
                        r_ = rtt[ci % 2]
                        S.act(r_[:, :], bk2.f(0, 512), AF.Sqrt, bias=EPS, scale=1.0 / 128.0)
                        S.recip(r_[:, :], r_[:, :])
                        S.stt(st[:, :], y_[:, :], wcol[:, 0:1], r_[:, :], ALU.mult, ALU.mult)
                    S.dma(dst[:, ti * 512:(ti + 1) * 512], st[:, :], g_sf[ci][ti % 2])
                if self.stop == 3:
                    continue
                for blk in range(4):
                    bank = pb[6]
                    for vi, coff in enumerate((C_VSL, C_VWN)):
                        for k in range(8):
                            if self.var == 4:
                                break
                            S.mm(bank.f(vi * 128, (vi + 1) * 128), lhsT=hT[:, k, blk * 128:(blk + 1) * 128],
                                 rhs=winb[:, k, coff:coff + 128], start=(k == 0), stop=(k == 7))
                    for k in range(8):
                        if self.var == 1:
                            break
                        S.mm(bank.f(256, 264), lhsT=hT[:, k, blk * 128:(blk + 1) * 128],
                             rhs=winb[:, k, C_AB:C_AB + 8], start=(k == 0), stop=(k == 7))
                    if self.var != 3:
                        S.copy("act", st_v[0][ti % 2][:, blk, :], bank.f(0, 128))
                        S.copy("act", st_v[1][ti % 2][:, blk, :], bank.f(128, 256))
                    if self.var != 5:
                        S.copy("dve", ab[:, blk, :], bank.f(256, 264))
                if self.var != 2:
                    S.dma(self.vsl_d[:, ti * 4:(ti + 1) * 4, :], st_v[0][ti % 2][:, :, :], g_sv[0][ti % 2])
                    S.dma(self.vwn_d[:, ti * 4:(ti + 1) * 4, :], st_v[1][ti % 2][:, :, :], g_sv[1][ti % 2])

                if self.stop == 4:
                    continue
                for pair in range(2):
                    blks = (2 * pair, 2 * pair + 1)
                    for s_, blk in enumerate(blks):
                        self.gdn_local_1(G[s_], s_, blk, ab, tri2, blk2, onesf, negones, cind, ma4, mb4, dtb, negA,
                                         qT, kT, vT)
                    if self.stop == 5:
                        continue
                    self.gdn_solve(G, blks, i4b)
                    if self.stop == 6:
                        continue
                    for s_, blk in enumerate(blks):
                        self.gdn_uw(G[s_], s_)
                    if self.stop == 7:
                        continue
                    for s_, blk in enumerate(blks):
                        oa = oacc[ti % 2]
                        self.gdn_recur(G[s_], S32, Sb, sel, blk, oa, hsel, oph[ti % 2])
                S.dma(self.o_own[ti], oacc[ti % 2][:, :], g_o[ti % 2])
                S.dma(self.oprev_d[ti], oph[ti % 2][126:128, :], g_oh[ti % 2])
            S.flush()

    def gdn_local_1(self, g, s_, blk, ab, tri2, blk2, onesf, negones, cind, ma4, mb4, dtb, negA, qT, kT, vT):
        S = self.S
        pb = self.pb
        sm = g["sm"]
        cs = slice(blk * 128, (blk + 1) * 128)
        x_, t_, gg, be, nb_, gci, gcv, gl, egc, ekd, bw, glS = (
            sm[:, 0:4], sm[:, 4:8], sm[:, 8:12], sm[:, 12:16], sm[:, 16:20], sm[:, 20:28], sm[:, 28:32],
            sm[:, 32:36], sm[:, 36:40], sm[:, 40:44], sm[:, 44:48], sm[:, 48:56])
        S.tt("dve", x_, ab[:, blk, 0:4], dtb[:, :], ALU.add)
        S.stt(t_, x_, -1.0, x_, ALU.mult, ALU.max)
        S.act(t_, t_, AF.Exp, scale=-1.0)
        S.act(t_, t_, AF.Ln, bias=1.0)
        S.stt(t_, x_, 0.0, t_, ALU.max, ALU.add)
        S.tt("dve", gg, t_, negA[:, :], ALU.mult)
        S.act(be, ab[:, blk, 4:8], AF.Sigmoid)
        S.ts("dve", nb_, be, -1.0, None, ALU.mult)
        gci3 = gci.re("p (h i) -> p h i", i=2)
        for i in range(2):
            S.ts("dve", gci3[:, :, i], gg, cind[:, i:i + 1], None, ALU.mult)
        for h in range(4):
            S.ts("pool", g["TG"][:, h, :], tri2[:, :], gg[:, h:h + 1], 1.0, ALU.mult, ALU.mult)
        bS = pb[6]
        o0 = 384 + s_ * 32
        S.mm(bS.f(o0, o0 + 4), lhsT=tri2[:, :], rhs=gg)
        S.mm(bS.f(o0 + 4, o0 + 8), lhsT=blk2[:, :], rhs=gg)
        S.mm(bS.f(o0 + 8, o0 + 16), lhsT=onesf[:, :], rhs=gci)
        bX = self.bankA[s_]
        for h in range(4):
            S.mm(bX.f(h * 128, (h + 1) * 128), lhsT=onesf[:, :], rhs=g["TG"][:, h, :], start=True, stop=False)
            S.mm(bX.f(h * 128, (h + 1) * 128), lhsT=g["TG"][:, h, :], rhs=negones[:, :], start=False, stop=True)
        S.copy("dve", sm[:, 28:36], bS.f(o0, o0 + 8))
        S.act(glS, bS.f(o0 + 8, o0 + 16), AF.Exp)
        S.act(egc, gcv, AF.Exp)
        S.tt("dve", ekd, gl, gcv, ALU.subtract)
        S.act(ekd, ekd, AF.Exp)
        S.tt("dve", bw, be, egc, ALU.mult)
        S.copy("act", g["Xs"][:, :], bX.f(0, 512))
        for h in range(4):
            S.act(g["EG"][:, h, :], bX.f(h * 128, (h + 1) * 128), AF.Exp, bias=gcv[:, h:h + 1])
        S.tt("pool", g["XA"][:, :], g["Xs"][:, :], ma4[:, :], ALU.add)
        S.tt("pool", g["XB"][:, :], g["Xs"][:, :], mb4[:, :], ALU.add)
        S.act(g["XA"][:, :], g["XA"][:, :], AF.Exp, scale=-1.0)
        S.act(g["XB"][:, :], g["XB"][:, :], AF.Exp)
        bK = self.bankA[s_]
        bQ = self.bankB[s_]
        for h in range(4):
            S.mm(bK.f(h * 128, (h + 1) * 128), lhsT=kT[:, h, cs], rhs=kT[:, h, cs])
        for h in range(4):
            S.mm(bQ.f(h * 128, (h + 1) * 128), lhsT=kT[:, h, cs], rhs=qT[:, h, cs])
        for h in range(4):
            S.stt(g["Nb"][:, h, :], bK.f(h * 128, (h + 1) * 128), nb_[:, h:h + 1],
                  g["XA"][:, h * 128:(h + 1) * 128], ALU.mult, ALU.mult)
        S.tt("dve", g["aT"][:, :, :].re("p h c -> p (h c)"), bQ.f(0, 512), g["XB"][:, :], ALU.mult)
        S.tt("pool", g["qdT"][:, :, :], qT[:, :, cs], g["EG"][:, :, :], ALU.mult)
        bT = pb[7]
        for h in range(4):
            S.tr(bT.b(h * 128, (h + 1) * 128), kT[:, h, cs], self.idb[:, :])
        for h in range(4):
            S.tr(bT.b(512 + h * 128, 512 + (h + 1) * 128), vT[:, h, cs], self.idb[:, :])
        kt3 = bT.b(0, 512).re("p (h d) -> p h d", h=4)
        vt3 = bT.b(512, 1024).re("p (h d) -> p h d", h=4)
        S.tt("dve", g["kw"][:, :, :], kt3, V(bw.ap.unsqueeze(2).broadcast_to([128, 4, 128]), bw.bs), ALU.mult)
        S.tt("dve", g["kd"][:, :, :], kt3, V(ekd.ap.unsqueeze(2).broadcast_to([128, 4, 128]), ekd.bs), ALU.mult)
        S.tt("dve", g["vb"][:, :, :], vt3, V(be.ap.unsqueeze(2).broadcast_to([128, 4, 128]), be.bs), ALU.mult)

    def gdn_solve(self, G, blks, i4b):
        S = self.S
        idb = self.idb
        for s_ in range(len(blks)):
            g = G[s_]
            bA, bC = self.bankA[s_], self.bankC[s_]
            for h in range(4):
                S.mm(bA.f(h * 128, (h + 1) * 128), lhsT=g["Nb"][:, h, :], rhs=idb[:, :])
            S.copy("act", g["NTb"][:, :, :].re("p h c -> p (h c)"), bA.f(0, 512))
            S.mm(bC.f(0, 512), lhsT=idb[:, :], rhs=i4b[:, :], start=True, stop=False)
            for h in range(4):
                S.mm(bC.f(h * 128, (h + 1) * 128), lhsT=g["Nb"][:, h, :], rhs=idb[:, :], start=False, stop=(h == 3))
            S.copy("act", g["RTb"][:, :, :].re("p h c -> p (h c)"), bC.f(0, 512))
        P = [G[s_]["Nb"] for s_ in range(len(blks))]
        Q = [G[s_]["NTb"] for s_ in range(len(blks))]
        for k in range(1, 6):
            for s_ in range(len(blks)):
                g = G[s_]
                bA, bB, bC = self.bankA[s_], self.bankB[s_], self.bankC[s_]
                Pn = g["P%d" % (k % 2)]
                Qn = g["Q%d" % (k % 2)]
                for h in range(4):
                    S.mm(bA.f(h * 128, (h + 1) * 128), lhsT=Q[s_][:, h, :], rhs=P[s_][:, h, :])
                if k < 5:
                    for h in range(4):
                        S.mm(bB.f(h * 128, (h + 1) * 128), lhsT=P[s_][:, h, :], rhs=Q[s_][:, h, :])
                S.copy("dve", Pn[:, :, :].re("p h c -> p (h c)"), bA.f(0, 512))
                if k < 5:
                    S.copy("act", Qn[:, :, :].re("p h c -> p (h c)"), bB.f(0, 512))
                S.mm(bC.f(0, 512), lhsT=idb[:, :], rhs=g["RTb"][:, :, :].re("p h c -> p (h c)"), start=True, stop=False)
                for h in range(4):
                    S.mm(bC.f(h * 128, (h + 1) * 128), lhsT=Pn[:, h, :], rhs=g["RTb"][:, h, :],
                         start=False, stop=(h == 3))
                S.copy("act", g["RTb"][:, :, :].re("p h c -> p (h c)"), bC.f(0, 512))
                P[s_] = Pn
                Q[s_] = Qn

    def gdn_uw(self, g, s_):
        S = self.S
        bA, bB = self.bankA[s_], self.bankB[s_]
        for h in range(4):
            S.mm(bA.f(h * 128, (h + 1) * 128), lhsT=g["RTb"][:, h, :], rhs=g["vb"][:, h, :])
        for h in range(4):
            S.mm(bB.f(h * 128, (h + 1) * 128), lhsT=g["kw"][:, h, :], rhs=g["RTb"][:, h, :])
        S.copy("act", g["u"][:, :, :].re("p h c -> p (h c)"), bA.f(0, 512))
        S.copy("dve", g["wTb"][:, :, :].re("p h c -> p (h c)"), bB.f(0, 512))

    def gdn_recur(self, g, S32, Sb, sel, blk, oa, hsel, oh):
        S = self.S
        pb = self.pb
        bV, bO, bS_ = pb[2], pb[3], pb[4]
        glS = g["sm"][:, 48:56].re("p (h i) -> p h i", i=2)
        for i in range(2):
            r0, r1 = 64 * i, 64 * i + 64
            for h in range(4):
                S.mm(bV.f(h * 128, (h + 1) * 128, r0, r1), lhsT=g["wTb"][:, h, r0:r1], rhs=Sb[:, h, :])
            S.tt("dve", g["vn"][r0:r1, :, :].re("p h c -> p (h c)"), g["u"][r0:r1, :, :].re("p h c -> p (h c)"),
                 bV.f(0, 512, r0, r1), ALU.subtract)
            for h in range(4):
                S.mm(bO.f(h * 128, (h + 1) * 128, r0, r1), lhsT=g["qdT"][:, h, r0:r1], rhs=Sb[:, h, :],
                     start=True, stop=False)
                S.mm(bO.f(h * 128, (h + 1) * 128, r0, r1), lhsT=g["aT"][r0:r1, h, r0:r1], rhs=g["vn"][r0:r1, h, :],
                     start=False, stop=True)
            for h in range(4):
                S.mm(bS_.f(h * 128, (h + 1) * 128), lhsT=g["kd"][r0:r1, h, :], rhs=g["vn"][r0:r1, h, :])
            glb = V(glS.ap[:, :, i].unsqueeze(2).broadcast_to([128, 4, 128]), glS.bs)
            S.tt("pool", S32[:, :, :], S32[:, :, :], glb, ALU.mult)
            S.tt("dve", S32[:, :, :].re("p h c -> p (h c)"), S32[:, :, :].re("p h c -> p (h c)"), bS_.f(0, 512), ALU.add)
            S.copy("act", Sb[:, :, :], S32[:, :, :])
        r_ = blk % 4
        if r_ == 0:
            S.ts("dve", oa[:, :], bO.f(0, 512), sel[:, 0:1], None, ALU.mult)
        else:
            S.stt(oa[:, :], bO.f(0, 512), sel[:, r_:r_ + 1], oa[:, :], ALU.mult, ALU.add)
        if r_ == 0:
            S.ts("dve", oh[64:128, :], bO.f(0, 512, 64, 128), hsel[64:128, 0:1], None, ALU.mult)
        else:
            S.stt(oh[64:128, :], bO.f(0, 512, 64, 128), hsel[64:128, r_:r_ + 1], oh[64:128, :], ALU.mult, ALU.add)

    def phase2(self, keep):
        S = self.S
        pb = self.pb
        gc_ = S.group("const2", full=True)
        qnT, gates = keep["qnT"], keep["gates"]
        with ExitStack() as es:
            qnw = self.cload(es, "qnw", [128, 1], gc_)
            gonw4 = self.cload(es, "gonw4", [128, 512], gc_)
            onesf = self.cload(es, "onesf2", [128, 128], gc_)
            onesb = self.sb(es, "onesb2", [128, 128], BF16)
            S.copy("dve", onesb[:, :], onesf[:, :])
            winb2 = self.sb(es, "winb2", [128, 8, 1036], BF16)
            with ExitStack() as es2:
                wst = [self.sb(es2, f"w2st{i}", [128, 1036]) for i in range(2)]
                g_w = [S.group(f"w2st{i}") for i in range(2)]
                for k in range(8):
                    sl = k % 2
                    for (a, b_, o) in ((O_GZ, O_GZ + 512, 0), (O_NQ, O_NQ + 512, 512), (O_NG, O_NG + 12, 1024)):
                        S.dma(wst[sl][:, o:o + (b_ - a)], self.w_in[k * 128:(k + 1) * 128, a:b_], g_w[sl])
                    S.copy("dve" if k % 2 else "pool", winb2[:, k, :], wst[sl][:, :])
                S.flush()
            F = dict(ssq=self.sb(es, "f2_ssq", [128, 4]), rt=self.sb(es, "f2_rt", [128, 4]),
                     rstd=self.sb(es, "f2_rstd", [128, 4]), junk=self.sb(es, "f2_junk", [128, 1024], BF16),
                     xs=self.sb(es, "f2_xs", [128, 2, 1024], BF16))
            xt = [self.sb(es, f"x2t{i}", [128, 2, 1024]) for i in range(2)]
            g_x = [S.group(f"x2t{i}") for i in range(2)]
            hT = self.sb(es, "h2T_", [128, 8, 512], BF16)
            yf = [self.sb(es, f"y2f{i}", [128, 512]) for i in range(2)]
            sq = [self.sb(es, f"s2q{i}", [128, 512], BF16) for i in range(2)]
            rtt = [self.sb(es, f"r2tt{i}", [128, 512]) for i in range(2)]
            og = [self.sb(es, f"og{i}", [128, 512]) for i in range(2)]
            g_og = [S.group(f"og{i}") for i in range(2)]
            zs = [self.sb(es, f"zs{i}", [128, 512]) for i in range(2)]
            t1 = [self.sb(es, f"p2t{i}", [128, 512]) for i in range(2)]
            mg = [self.sb(es, f"mg{i}", [128, 512], BF16) for i in range(2)]
            mgT = [self.sb(es, f"mgT{i}", [128, 4, 128], BF16) for i in range(2)]
            g_mg = [S.group(f"mgT{i}") for i in range(2)]
            osq = self.sb(es, "osq", [128, 8])
            xrows = self.xo.rearrange("(n b p) d -> n p b d", b=2, p=128)

            def load_x(n):
                S.dma(xt[n % 2][:, :, :], xrows[n], g_x[n % 2])

            load_x(0)
            for t in range(4):
                for hf in range(2):
                    n = 2 * t + hf
                    if n + 1 < 8:
                        load_x(n + 1)
                    self.front(F, xt[n % 2], 2, hT, hf * 256, self.s1, 0)
                for h in range(4):
                    bank = pb[2 + (h % 2)]
                    for k in range(8):
                        S.mm(bank.f(0, 512), lhsT=winb2[:, k, 512 + h * 128:512 + (h + 1) * 128], rhs=hT[:, k, :],
                             start=(k == 0), stop=(k == 7))
                    y_ = yf[h % 2]
                    S.copy("act", y_[:, :], bank.f(0, 512))
                    S.act(sq[h % 2][:, :], bank.f(0, 512), AF.Square)
                    bk2 = pb[4 + (h % 2)]
                    S.mm(bk2.f(0, 512), lhsT=onesb[:, :], rhs=sq[h % 2][:, :])
                    r_ = rtt[h % 2]
                    S.act(r_[:, :], bk2.f(0, 512), AF.Sqrt, bias=EPS, scale=1.0 / 128.0)
                    S.recip(r_[:, :], r_[:, :])
                    S.stt(qnT[:, h, t * 512:(t + 1) * 512], y_[:, :], qnw[:, 0:1], r_[:, :], ALU.mult, ALU.mult)
                for blk in range(4):
                    j = 4 * t + blk
                    cs = slice(blk * 128, (blk + 1) * 128)
                    bg = pb[6]
                    for k in range(8):
                        S.mm(bg.f(0, 12), lhsT=hT[:, k, cs], rhs=winb2[:, k, 1024:1036], start=(k == 0), stop=(k == 7))
                    S.act(gates[:, j, :], bg.f(0, 12), AF.Sigmoid)
                    bz = pb[7]
                    for k in range(8):
                        S.mm(bz.f(0, 512), lhsT=hT[:, k, cs], rhs=winb2[:, k, 0:512], start=(k == 0), stop=(k == 7))
                    z_ = zs[j % 2]
                    S.act(z_[:, :], bz.f(0, 512), AF.Silu)
                    o_ = og[j % 2]
                    S.dma(o_[:, :], self.o_own[j], g_og[j % 2])
                    for h in range(4):
                        S.act(t1[j % 2][:, h * 128:(h + 1) * 128], o_[:, h * 128:(h + 1) * 128], AF.Square,
                              accum=osq[:, h:h + 1])
                    S.act(osq[:, 4:8], osq[:, 0:4], AF.Sqrt, bias=EPS, scale=1.0 / 128.0)
                    S.recip(osq[:, 4:8], osq[:, 4:8])
                    rb = V(osq.t[:, 4:8].unsqueeze(2).broadcast_to([128, 4, 128]), [osq.b])
                    S.tt("dve", t1[j % 2][:, :].re("p (h d) -> p h d", h=4), o_[:, :].re("p (h d) -> p h d", h=4), rb,
                         ALU.mult)
                    S.tt("pool", t1[j % 2][:, :], t1[j % 2][:, :], gonw4[:, :], ALU.mult)
                    S.tt("dve", mg[j % 2][:, :], t1[j % 2][:, :], z_[:, :], ALU.mult)
                    bt = pb[0]
                    for c in range(4):
                        S.tr(bt.b(c * 128, (c + 1) * 128), mg[j % 2][:, c * 128:(c + 1) * 128], self.idb[:, :])
                    S.copy("act", mgT[j % 2][:, :, :].re("p c t -> p (c t)"), bt.b(0, 512))
                    S.dma(self.mix_d[:, 0:4, j * 128:(j + 1) * 128], mgT[j % 2][:, :, :], g_mg[j % 2])
            gh = S.group("p2halo", full=True)
            selAB = self.cload(es, "selAB", [128, 2], gh)
            xht = self.sb(es, "xht", [32, D])
            S.dma(xht[:, :], self.inp("xh", [32, D])[:, :], gh)
            ca = self.sb(es, "hca", [32, 512])
            cb = self.sb(es, "hcb", [32, 512])
            S.memset("pool", cb[0:2, :], 0.0)
            opv = self.oprev_d.rearrange("j i c -> (j i) c")
            S.dma(ca[:, :], opv[0:32, :], gh)
            S.dma(cb[2:32, :], opv[0:30, :], gh)
            hTh = self.sb(es, "hTh", [128, 8, 32], BF16)
            self.front32(F, xht, hTh, self.s1, 0)
            qnTh, gatesh = keep["qnTh"], keep["gatesh"]
            bq = pb[2]
            for h in range(4):
                for k in range(8):
                    S.mm(bq.f(h * 32, (h + 1) * 32), lhsT=winb2[:, k, 512 + h * 128:512 + (h + 1) * 128], rhs=hTh[:, k, :],
                         start=(k == 0), stop=(k == 7))
            S.copy("act", yf[0][:, 0:128], bq.f(0, 128))
            S.act(sq[0][:, 0:128], bq.f(0, 128), AF.Square)
            S.mm(pb[4].f(0, 128), lhsT=onesb[:, :], rhs=sq[0][:, 0:128])
            S.act(rtt[0][:, 0:128], pb[4].f(0, 128), AF.Sqrt, bias=EPS, scale=1.0 / 128.0)
            S.recip(rtt[0][:, 0:128], rtt[0][:, 0:128])
            S.stt(qnTh[:, :, :].re("p h q -> p (h q)"), yf[0][:, 0:128], qnw[:, 0:1], rtt[0][:, 0:128], ALU.mult, ALU.mult)
            for k in range(8):
                S.mm(pb[6].f(0, 12, 0, 32), lhsT=hTh[:, k, :], rhs=winb2[:, k, 1024:1036], start=(k == 0), stop=(k == 7))
            S.act(gatesh[:, :], pb[6].f(0, 12, 0, 32), AF.Sigmoid)
            for k in range(8):
                S.mm(pb[7].f(0, 512, 0, 32), lhsT=hTh[:, k, :], rhs=winb2[:, k, 0:512], start=(k == 0), stop=(k == 7))
            S.act(zs[0][0:32, :], pb[7].f(0, 512, 0, 32), AF.Silu)
            S.ts("dve", ca[:, :], ca[:, :], selAB[0:32, 0:1], None, ALU.mult)
            S.stt(ca[:, :], cb[:, :], selAB[0:32, 1:2], ca[:, :], ALU.mult, ALU.add)
            for h in range(4):
                S.act(t1[0][0:32, h * 128:(h + 1) * 128], ca[:, h * 128:(h + 1) * 128], AF.Square, accum=osq[0:32, h:h + 1])
            S.act(osq[0:32, 4:8], osq[0:32, 0:4], AF.Sqrt, bias=EPS, scale=1.0 / 128.0)
            S.recip(osq[0:32, 4:8], osq[0:32, 4:8])
            rb = V(osq.t[0:32, 4:8].unsqueeze(2).broadcast_to([32, 4, 128]), [osq.b])
            S.tt("dve", t1[0][0:32, :].re("p (h d) -> p h d", h=4), ca[:, :].re("p (h d) -> p h d", h=4), rb, ALU.mult)
            S.tt("pool", t1[0][0:32, :], t1[0][0:32, :], gonw4[0:32, :], ALU.mult)
            S.tt("dve", mg[0][0:32, :], t1[0][0:32, :], zs[0][0:32, :], ALU.mult)
            for c in range(4):
                S.tr(pb[0].b(c * 32, (c + 1) * 32), mg[0][0:32, c * 128:(c + 1) * 128], self.idb[0:32, 0:32])
            mghT = self.sb(es, "mghT", [128, 4, 32], BF16)
            S.copy("act", mghT[:, :, :].re("p c t -> p (c t)"), pb[0].b(0, 128))
            S.dma(self.mixh_d[:, 0:4, :], mghT[:, :, :], S.group("mghT"))
            S.flush()

    def phase3(self, keep):
        S = self.S
        pb = self.pb
        idb = self.idb
        qnT, gates = keep["qnT"], keep["gates"]
        SC = 128.0 ** -0.5
        with ExitStack() as es:
            gc_ = S.group("const3", full=True)
            kcmpT = self.sb(es, "kcmpT", [128, 512], BF16)
            vcx = self.sb(es, "vcx", [128, 4, 129], BF16)
            onesf = self.cload(es, "onesf3", [128, 128], gc_)
            onesb = self.sb(es, "onesb3", [128, 128], BF16)
            S.copy("dve", onesb[:, :], onesf[:, :])
            with ExitStack() as e2:
                g2 = S.group("const3b", full=True)
                kcw = self.cload(e2, "kcw", [128, 1], g2)
                pm511 = self.cload(e2, "pm511", [128, 1], g2)
                for which, src_d, w1n, w2n, posn in (("k", self.kcT_d, "cmp_k_w1", "cmp_k_w2", "cmp_k_posT"),
                                                    ("v", self.vcT_d, "cmp_v_w1", "cmp_v_w2", "cmp_v_posT")):
                    with ExitStack() as e3:
                        g3 = S.group("c3" + which, full=True)
                        xT = self.sb(e3, "cx" + which, [128, S_LEN], BF16)
                        S.dma(xT[:, 0:4096], src_d[:, 0:4096], g3)
                        S.dma(xT[:, 4096:8192], src_d[:, 4096:8192], g3)
                        w1d = self.inp(w1n, [128, 32, 128])
                        w1f = self.sb(e3, "w1f" + which, [128, 32, 128])
                        S.dma(w1f[:, 0:16, :], w1d[:, 0:16, :], g3)
                        S.dma(w1f[:, 16:32, :], w1d[:, 16:32, :], g3)
                        w1b = self.sb(e3, "w1b" + which, [128, 32, 128], BF16)
                        S.copy("dve", w1b[:, 0:16, :], w1f[:, 0:16, :])
                        S.copy("pool", w1b[:, 16:32, :], w1f[:, 16:32, :])
                        w2f = self.cload(e3, w2n, [128, 128], g3)
                        w2b = self.sb(e3, "w2b" + which, [128, 128], BF16)
                        S.copy("dve", w2b[:, :], w2f[:, :])
                        posf = self.cload(e3, posn, [128, 32], g3)
                        posb = self.sb(e3, "posb" + which, [128, 32], BF16)
                        S.copy("dve", posb[:, :], posf[:, :])
                        hid = self.sb(e3, "hid" + which, [128, 512], BF16)
                        bcol = self.sb(e3, "bcol" + which, [128, 1])
                        S.memset("pool", hid[:, :], 0.0)
                        bh, bb = pb[2], pb[3]
                        x3 = xT[:, :].re("p (n s) -> p n s", s=16)
                        for l in range(32):
                            S.mm(bh.f(0, 511), lhsT=w1b[:, l, :], rhs=x3[:, l // 16:l // 16 + 511, l % 16],
                                 start=(l == 0), stop=(l == 31))
                        for l in range(32):
                            S.mm(bb.f(0, 1), lhsT=w1b[:, l, :], rhs=posb[:, l:l + 1], start=(l == 0), stop=(l == 31))
                        S.copy("dve", bcol[:, :], bb.f(0, 1))
                        S.act(hid[:, 0:511], bh.f(0, 511), AF.Silu, bias=bcol[:, 0:1])
                        if which == "k":
                            bk = pb[4]
                            S.mm(bk.f(0, 512), lhsT=w2b[:, :], rhs=hid[:, :])
                            yk = self.sb(e3, "yk", [128, 512])
                            sqk = self.sb(e3, "sqk", [128, 512], BF16)
                            rk = self.sb(e3, "rk", [128, 512])
                            S.copy("act", yk[:, :], bk.f(0, 512))
                            S.act(sqk[:, :], bk.f(0, 512), AF.Square)
                            S.mm(pb[5].f(0, 512), lhsT=onesb[:, :], rhs=sqk[:, :])
                            S.act(rk[:, :], pb[5].f(0, 512), AF.Sqrt, bias=EPS, scale=1.0 / 128.0)
                            S.recip(rk[:, :], rk[:, :])
                            S.stt(kcmpT[:, :], yk[:, :], kcw[:, 0:1], rk[:, :], ALU.mult, ALU.mult)
                            S.memset("dve", kcmpT[:, 511:512], 0.0)
                        else:
                            bv = pb[4]
                            for nt in range(4):
                                S.mm(bv.f(nt * 128, (nt + 1) * 128), lhsT=hid[:, nt * 128:(nt + 1) * 128], rhs=w2b[:, :])
                            S.memset("pool", vcx[:, :, 128:129], 1.0)
                            S.copy("act", vcx[:, :, 0:128], bv.f(0, 512).re("p (n d) -> p n d", n=4))
                            S.ts("dve", vcx[:, 3, :], vcx[:, 3, :], pm511[:, 0:1], None, ALU.mult)
                        S.flush()
            if self.debug:
                S.dma(self.outp("d_kcmpT", [128, 512], BF16)[:, :], kcmpT[:, :], self.g_out)
                S.dma(self.outp("d_vcx", [128, 4, 129], BF16)[:, :, :], vcx[:, :, :], self.g_out)
            if self.stop == 31:
                S.flush()
                return
            kslT = self.sb(es, "kslT", [128, S_LEN], BF16)
            kwnT = self.sb(es, "kwnT", [128, S_LEN], BF16)
            vslx = self.sb(es, "vslx", [128, NB, 129], BF16)
            vwnx = self.sb(es, "vwnx", [128, NB, 129], BF16)
            for i in range(2):
                S.dma(kslT[:, i * 4096:(i + 1) * 4096], self.kslT_d[:, i * 4096:(i + 1) * 4096], gc_)
                S.dma(kwnT[:, i * 4096:(i + 1) * 4096], self.kwnT_d[:, i * 4096:(i + 1) * 4096], gc_)
                S.dma(vslx[:, i * 32:(i + 1) * 32, 0:128], self.vsl_d[:, i * 32:(i + 1) * 32, :], gc_)
                S.dma(vwnx[:, i * 32:(i + 1) * 32, 0:128], self.vwn_d[:, i * 32:(i + 1) * 32, :], gc_)
            S.memset("pool", vslx[:, :, 128:129], 1.0)
            S.memset("pool", vwnx[:, :, 128:129], 1.0)
            eall = self.sb(es, "eallb", [128, S_LEN], BF16)
            addm = self.cload(es, "addm", [128, 16, 128], gc_)
            ovb = self.sb(es, "ovb", [128, 4, 128], BF16)
            cmbb = self.sb(es, "cmbb", [128, 16, 2, 128], BF16)
            cbsb = self.sb(es, "cbsb", [128, 4, 4, 128], BF16)
            cbwb = self.sb(es, "cbwb", [128, 8, 4, 128], BF16)
            with ExitStack() as e2:
                g2 = S.group("const3c", full=True)
                ead = self.inp("eall", [128, S_LEN])
                est = [self.sb(e2, f"est{i}", [128, 2048]) for i in range(2)]
                g_e = [S.group(f"est{i}") for i in range(2)]
                for i in range(4):
                    S.dma(est[i % 2][:, :], ead[:, i * 2048:(i + 1) * 2048], g_e[i % 2])
                    S.copy("pool" if i % 2 else "dve", eall[:, i * 2048:(i + 1) * 2048], est[i % 2][:, :])
                ovf = self.cload(e2, "ovm", [128, 4, 128], g2)
                S.copy("dve", ovb[:, :, :], ovf[:, :, :])
                cmf = self.cload(e2, "cmb", [128, 16, 2, 128], g2)
                S.copy("pool", cmbb[:, :, :, :], cmf[:, :, :, :])
                cbsf = self.cload(e2, "cbs", [128, 4, 128], g2)
                cbwf = self.cload(e2, "cbw", [128, 8, 128], g2)
                for h in range(4):
                    S.copy("dve", cbsb[:, :, h, :], cbsf[:, :, :])
                    S.copy("pool", cbwb[:, :, h, :], cbwf[:, :, :])
                S.flush()
            Pc = [self.sb(es, f"Pc{i}", [128, 512], BF16) for i in range(4)]
            Pk = [self.sb(es, f"Pk{i}", [128, 512], BF16) for i in range(3)]
            cmrep = [self.sb(es, f"cmrep{i}", [128, 4, 128], BF16) for i in range(2)]
            ob = {nm: self.sb(es, "ob_" + nm, [128, 4, 129]) for nm in ("c", "s", "w")}
            rs = self.sb(es, "rs", [128, 12])
            cf = self.sb(es, "cf", [128, 12])
            imp = self.sb(es, "imp", [128, 128])
            imp2 = self.sb(es, "imp2", [128, 128])
            m8 = self.sb(es, "m8", [128, 16])
            selm = self.sb(es, "selm", [128, 128])
            biasT = self.sb(es, "biasT", [128, 4, 128], BF16)
            mixn = [self.sb(es, f"mixn{i}", [128, 4, 128], BF16) for i in range(2)]
            tmpn = self.sb(es, "tmpn", [128, 128])
            mnT = [self.sb(es, f"mnT{i}", [128, 4, 128], BF16) for i in range(2)]
            g_mn = [S.group(f"mnT{i}") for i in range(2)]
            bS = [pb[0], pb[1]]
            bO = [pb[2], pb[3]]
            bI, bT = pb[4], pb[5]
            si = [0]

            def nsa_block(Q, qv, cmp_plan, slc_plan, win_plan, addv, gatev, store):
                NQ = 4 * Q

                def scores(kT_tile, extra):
                    bank = bS[si[0] % 2]
                    out_ = Pk[si[0] % 3]
                    si[0] += 1
                    S.mm(bank.f(0, NQ), lhsT=kT_tile, rhs=qv, start=True, stop=(len(extra) == 0))
                    for i_, (l_, r_) in enumerate(extra):
                        S.mm(bank.f(0, NQ), lhsT=l_, rhs=r_, start=False, stop=(i_ == len(extra) - 1))
                    S.act(out_[:, 0:NQ], bank.f(0, NQ), AF.Exp, scale=SC)
                    return out_

                def pv(P_, vx, first, last):
                    for h in range(4):
                        S.mm_(bO[h // 2].f((h % 2) * 129, (h % 2) * 129 + 129, 0, Q), lhsT=P_[:, h * Q:(h + 1) * Q],
                              rhs=vx, start=(first and h % 2 == 0), stop=last, skip=True)

                def evac_o(dst):
                    for hh in range(2):
                        S.copy("act", dst[0:Q, 2 * hh:2 * hh + 2, :], bO[hh].f(0, 258, 0, Q).re("p (h c) -> p h c", h=2))

                ncp = len(cmp_plan)
                for i_, (nt, mk) in enumerate(cmp_plan):
                    bank = bS[i_ % 2]
                    S.mm(bank.f(0, NQ), lhsT=kcmpT[:, nt * 128:(nt + 1) * 128], rhs=qv, start=True, stop=(mk is None))
                    if mk is not None:
                        S.mm(bank.f(0, NQ), lhsT=idb[:, :], rhs=mk, start=False, stop=True)
                    S.act(Pc[i_][:, 0:NQ], bank.f(0, NQ), AF.Exp, scale=SC)
                for h in range(4):
                    for i_, (nt, mk) in enumerate(cmp_plan):
                        S.mm_(bO[h // 2].f((h % 2) * 129, (h % 2) * 129 + 129, 0, Q), lhsT=Pc[i_][:, h * Q:(h + 1) * Q],
                              rhs=vcx[:, nt, :], start=(i_ == 0 and h % 2 == 0), stop=(i_ == ncp - 1), skip=True)
                for h in range(4):
                    for i_, (nt, mk) in enumerate(cmp_plan):
                        S.mm_(bI.f(h * 128, (h + 1) * 128, 0, Q), lhsT=Pc[i_][:, h * Q:(h + 1) * Q], rhs=ovb[:, nt, :],
                              start=(i_ == 0 and h == 0), stop=(i_ == ncp - 1), skip=True)
                evac_o(ob["c"])
                S.ts("dve", rs[0:Q, 0:4], ob["c"][0:Q, :, 128], 1e-30, None, ALU.max)
                S.recip(rs[0:Q, 0:4], rs[0:Q, 0:4])
                S.ts("dve", imp[0:Q, :], bI.f(0, 128, 0, Q), rs[0:Q, 0:1], None, ALU.mult)
                for h in range(1, 4):
                    S.stt(imp[0:Q, :], bI.f(h * 128, (h + 1) * 128, 0, Q), rs[0:Q, h:h + 1], imp[0:Q, :], ALU.mult, ALU.add)
                S.tt("pool", imp[0:Q, :], imp[0:Q, :], addv, ALU.add)
                self.max8(m8[0:Q, 0:8], imp[0:Q, :])
                self.match_replace(imp2[0:Q, :], m8[0:Q, 0:8], imp[0:Q, :], -3.0e38)
                self.max8(m8[0:Q, 8:16], imp2[0:Q, :])
                S.ts("dve", selm[0:Q, :], imp[0:Q, :], m8[0:Q, 15:16], None, ALU.is_ge)
                S.mm(bT.f(0, Q), lhsT=selm[0:Q, :], rhs=self.idf[0:Q, 0:Q])
                S.ts("dve", bT2[:, 0:NQ].re("p (h q) -> p h q", h=4),
                     V(bT.t[:, 0:Q].unsqueeze(1).broadcast_to([128, 4, Q]), bT.q[0:1]), BIG, -BIG, ALU.mult, ALU.add)
                brhs = bT2[:, 0:NQ]
                for i_, (kt, extra) in enumerate(slc_plan):
                    ex = [(eall[:, kt * 128:(kt + 1) * 128], brhs)] + extra
                    P_ = scores(kslT[:, kt * 128:(kt + 1) * 128], ex)
                    pv(P_, vslx[:, kt, :], i_ == 0, i_ == len(slc_plan) - 1)
                evac_o(ob["s"])
                for i_, (kt, mk) in enumerate(win_plan):
                    P_ = scores(kwnT[:, kt * 128:(kt + 1) * 128], [(idb[:, :], mk)])
                    pv(P_, vwnx[:, kt, :], i_ == 0, i_ == len(win_plan) - 1)
                evac_o(ob["w"])
                S.ts("dve", rs[0:Q, 4:8], ob["s"][0:Q, :, 128], 1e-30, None, ALU.max)
                S.ts("dve", rs[0:Q, 8:12], ob["w"][0:Q, :, 128], 1e-30, None, ALU.max)
                S.recip(rs[0:Q, 4:12], rs[0:Q, 4:12])
                g3 = gatev.re("p (h g) -> p g h", g=3)
                S.tt("dve", cf[0:Q, :].re("p (g h) -> p g h", g=3), rs[0:Q, :].re("p (g h) -> p g h", g=3), g3, ALU.mult)
                mx = mixn[si[0] % 2]
                for h in range(4):
                    S.ts("pool", tmpn[0:Q, :], ob["c"][0:Q, h, 0:128], cf[0:Q, h:h + 1], 1.0, ALU.mult, ALU.mult)
                    S.stt(tmpn[0:Q, :], ob["s"][0:Q, h, 0:128], cf[0:Q, 4 + h:5 + h], tmpn[0:Q, :], ALU.mult, ALU.add)
                    S.stt(mx[0:Q, h, :], ob["w"][0:Q, h, 0:128], cf[0:Q, 8 + h:9 + h], tmpn[0:Q, :], ALU.mult, ALU.add)
                for h in range(4):
                    S.tr(bT.b(h * Q, (h + 1) * Q), mx[0:Q, h, :], idb[0:Q, 0:Q])
                store(bT.b(0, 4 * Q))

            bT2 = self.sb(es, "bT2", [128, 512], BF16)
            for j in range(16):
                qv = qnT[:, :, j * 128:(j + 1) * 128]
                nt_hi = min(3, (32 * j + 30) // 128)
                lo = max(0, 32 * j - 1) // 128
                cmp_plan = []
                for nt in range(nt_hi + 1):
                    mk = None
                    if nt >= lo:
                        slot = nt - lo
                        S.copy("pool", cmrep[slot][:, :, :],
                               V(cmbb.t[:, j, slot, :].unsqueeze(1).broadcast_to([128, 4, 128]), [cmbb.b]))
                        mk = cmrep[slot][:, :, :].re("p h q -> p (h q)")
                    cmp_plan.append((nt, mk))
                slc_plan = []
                for kt in range(4 * j + 4):
                    ex = []
                    if kt >= 4 * j:
                        ex.append((idb[:, :], cbsb[:, kt - 4 * j, :, :].re("p h q -> p (h q)")))
                    slc_plan.append((kt, ex))
                win_plan = [(4 * j - 4 + e, cbwb[:, e, :, :].re("p h q -> p (h q)")) for e in range(8) if 4 * j - 4 + e >= 0]

                def store(src, j=j):
                    S.copy("act", mnT[j % 2][:, :, :].re("p c t -> p (c t)"), src)
                    S.dma(self.mix_d[:, 4:8, j * 128:(j + 1) * 128], mnT[j % 2][:, :, :], g_mn[j % 2])

                nsa_block(128, qv, cmp_plan, slc_plan, win_plan, addm[:, j, :], gates[:, j, :], store)
            self.nsa_halo(es, nsa_block, keep, idb)
            S.flush()

    def nsa_halo(self, es, nsa_block, keep, idb):
        S = self.S
        with ExitStack() as e2:
            g2 = S.group("const3h", full=True)
            haddm = self.cload(e2, "haddm", [32, 128], g2)
            hcmb = self.sb(e2, "hcmb", [128, 4, 128], BF16)
            hsmb = self.sb(e2, "hsmb", [128, 64, 128], BF16)
            hwmb = self.sb(e2, "hwmb", [128, 64, 128], BF16)
            hcf = self.cload(e2, "hcm", [128, 4, 128], g2)
            S.copy("dve", hcmb[:, :, :], hcf[:, :, :])
            st = [self.sb(e2, f"hst{i}", [128, 16, 128]) for i in range(2)]
            g_s = [S.group(f"hst{i}") for i in range(2)]
            n = 0
            for nm, dst in (("hsm", hsmb), ("hwm", hwmb)):
                src = self.inp(nm, [128, 64, 128])
                for i in range(4):
                    S.dma(st[n % 2][:, :, :], src[:, i * 16:(i + 1) * 16, :], g_s[n % 2])
                    S.copy("pool" if n % 2 else "dve", dst[:, i * 16:(i + 1) * 16, :], st[n % 2][:, :, :])
                    n += 1
            mnTh = self.sb(e2, "mnTh", [128, 4, 32], BF16)
            g_m = S.group("mnTh")
            cmp_plan = [(nt, hcmb[:, nt, :]) for nt in range(4)]
            slc_plan = [(kt, [(idb[:, :], hsmb[:, kt, :])]) for kt in range(NB)]
            win_plan = [(kt, hwmb[:, kt, :]) for kt in range(NB)]

            def store(src):
                S.copy("act", mnTh[:, :, :].re("p c t -> p (c t)"), src)
                S.dma(self.mixh_d[:, 4:8, :], mnTh[:, :, :], g_m)

            nsa_block(32, keep["qnTh"][:, :, :], cmp_plan, slc_plan, win_plan, haddm[:, :], keep["gatesh"][:, :], store)
            S.flush()

    def max8(self, out, in_):
        o_, i_ = out.ap, in_.ap
        self.S.op("dve", lambda e: e.max(out=o_, in_=i_), r=self.S._bs(in_), w=self.S._bs(out))

    def match_replace(self, out, rep, vals, imm):
        o_, r_, v_ = out.ap, rep.ap, vals.ap
        self.S.op("dve", lambda e: e.match_replace(out=o_, in_to_replace=r_, in_values=v_, imm_value=imm),
                  r=self.S._bs(rep, vals), w=self.S._bs(out))

    def phase45(self):
        S = self.S
        pb = self.pb
        idb = self.idb
        w_out = self.inp("w_out", [D, D])
        w_up = self.inp("w_up", [D, 2 * DFF])
        w_dn = self.inp("w_dn", [DFF, D])
        wup_d = self.scratch("wup_d", [128, 44, 8, 128], BF16)
        with ExitStack() as e1:
            full = self.sb(e1, "wupfull", [128, 44, 8, 128], BF16)
            stg = [self.sb(e1, f"wupst{i}", [128, 2 * DFF]) for i in range(2)]
            g_s = [S.group(f"wupst{i}") for i in range(2)]
            g_f = S.group("wupfull")
            for k in range(8):
                st = stg[k % 2]
                for i in range(2):
                    S.dma(st[:, i * DFF:(i + 1) * DFF], w_up[k * 128:(k + 1) * 128, i * DFF:(i + 1) * DFF], g_s[k % 2])
                s3 = st[:, :].re("p (c n) -> p c n", n=128)
                S.copy("dve", full[:, 0:16, k, :], s3[:, 0:16, :])
                S.copy("pool", full[:, 16:30, k, :], s3[:, 16:30, :])
                S.copy("act", full[:, 30:44, k, :], s3[:, 30:44, :])
            for i in range(4):
                S.dma(wup_d[:, i * 11:(i + 1) * 11, :, :], full[:, i * 11:(i + 1) * 11, :, :], g_f)
            S.flush()
        with ExitStack() as es:
            gc_ = S.group("const4", full=True)
            fcw = self.cload(es, "fcw", [128, 44 * 3], gc_)
            fcb = self.cload(es, "fcb", [128, 44], gc_)
            wdnb = self.sb(es, "wdnb", [128, 22, D], BF16)
            woutb = self.sb(es, "woutb", [128, 8, D], BF16)
            g1row = self.sb(es, "g1row", [128, D])
            g2row = self.sb(es, "g2row", [128, D])
            with ExitStack() as e2:
                stg = [self.sb(e2, f"wst4_{i}", [128, 2, D]) for i in range(2)]
                g_s = [S.group(f"wst4_{i}") for i in range(2)]
                n = 0
                for (src, dst, nch) in ((w_dn, wdnb, 22), (w_out, woutb, 8)):
                    sv = src.rearrange("(c p) d -> p c d", p=128)
                    for c0 in range(0, nch, 2):
                        st = stg[n % 2]
                        S.dma(st[:, :, :], sv[:, c0:c0 + 2, :], g_s[n % 2])
                        S.copy(("dve", "pool", "act")[n % 3], dst[:, c0:c0 + 2, :], st[:, :, :])
                        n += 1
                gb = self.sb(e2, "gbc", [128, 128])
                for gi, (col0, dst) in enumerate(((16, g1row), (40, g2row))):
                    for cc in range(8):
                        S.copy("dve", gb[:, :], V(self.modT.t[:, col0 + cc:col0 + cc + 1].broadcast_to([128, 128]),
                                                  [self.modT.b]))
                        bank = pb[cc % 2]
                        S.mm(bank.f(0, 128), lhsT=gb[:, :], rhs=self.idf[:, :])
                        S.copy("act", dst[:, cc * 128:(cc + 1) * 128], bank.f(0, 128))
                S.flush()
            F = dict(ssq=self.sb(es, "f4_ssq", [128, 4]), rt=self.sb(es, "f4_rt", [128, 4]),
                     rstd=self.sb(es, "f4_rstd", [128, 4]), junk=self.sb(es, "f4_junk", [128, 1024], BF16),
                     xs=self.sb(es, "f4_xs", [128, 2, 1024], BF16))
            x1 = [self.sb(es, f"x1_{i}", [128, 4, D]) for i in range(2)]
            g_x = [S.group(f"x1_{i}") for i in range(2)]
            mixt = self.sb(es, "mixt", [128, 8, 512], BF16)
            g_m = S.group("mixt")
            h2T = self.sb(es, "h2T", [128, 8, 512], BF16)
            gT = self.sb(es, "gT", [128, 22, 512], BF16)
            wch = [self.sb(es, f"wch{i}", [128, 2, 8, 128], BF16) for i in range(3)]
            g_wc = [S.group(f"wch{i}") for i in range(3)]
            upa = [self.sb(es, f"upa{i}", [128, 4, 130]) for i in range(2)]
            upb = [self.sb(es, f"upb{i}", [128, 4, 130]) for i in range(2)]
            for u_ in upa + upb:
                S.memset("pool", u_[:, :, :], 0.0)
            aa = [self.sb(es, f"aa{i}", [128, 4, 128]) for i in range(2)]
            ab_ = [self.sb(es, f"ab_{i}", [128, 4, 128]) for i in range(2)]
            tmp = [self.sb(es, f"tmp4_{i}", [128, 512]) for i in range(2)]
            xrows = self.xo.rearrange("(t b p) d -> t p b d", b=4, p=128)
            orows = self.out.rearrange("(t b p) d -> t p b d", b=4, p=128)
            wi = 0
            gh = S.group("p4halo", full=True)
            hexr = self.cload(es, "hexr", [128, 32], gh)
            x1h = self.sb(es, "x1h", [32, D])
            mixh = self.sb(es, "mixh", [128, 8, 32], BF16)
            S.dma(x1h[:, :], self.din["xh"][:, :], gh)
            S.dma(mixh[:, :, :], self.mixh_d[:, :, :], gh)
            h2Th = self.sb(es, "h2Th", [128, 8, 32], BF16)
            for half in range(2):
                bank = pb[2 + half]
                hs = slice(half * 512, (half + 1) * 512)
                for c in range(8):
                    S.mm(bank.f(0, 512, 0, 32), lhsT=mixh[:, c, :], rhs=woutb[:, c, hs], start=(c == 0), stop=(c == 7))
                S.tt("dve", tmp[half][0:32, :], bank.f(0, 512, 0, 32), g1row[0:32, hs], ALU.mult)
                S.tt("pool", x1h[:, hs], x1h[:, hs], tmp[half][0:32, :], ALU.add)
            self.front32(F, x1h, h2Th, self.s2, 24)

            def conv(u_, c, dst):
                S.ts("pool", dst[:, :, :], u_[:, :, 2:130], fcw[:, c * 3 + 2:c * 3 + 3], fcb[:, c:c + 1], ALU.mult, ALU.add)
                S.stt(dst[:, :, :], u_[:, :, 1:129], fcw[:, c * 3 + 1:c * 3 + 2], dst[:, :, :], ALU.mult, ALU.add)
                S.stt(dst[:, :, :], u_[:, :, 0:128], fcw[:, c * 3:c * 3 + 1], dst[:, :, :], ALU.mult, ALU.add)

            for t in range(4):
                xx = x1[t % 2]
                S.dma(xx[:, :, :], xrows[t], g_x[t % 2])
                S.dma(mixt[:, :, :], self.mix_d[:, :, t * 512:(t + 1) * 512], g_m)
                for blk in range(4):
                    cs = slice(blk * 128, (blk + 1) * 128)
                    for half in range(2):
                        bank = pb[2 + half]
                        hs = slice(half * 512, (half + 1) * 512)
                        for c in range(8):
                            S.mm(bank.f(0, 512), lhsT=mixt[:, c, cs], rhs=woutb[:, c, hs], start=(c == 0), stop=(c == 7))
                        tm = tmp[half]
                        S.tt("dve", tm[:, :], bank.f(0, 512), g1row[:, hs], ALU.mult)
                        S.tt("pool", xx[:, blk, hs], xx[:, blk, hs], tm[:, :], ALU.add)
                if self.debug:
                    S.dma(self.dx1[t], xx[:, :, :], self.g_out)
                for hf in range(2):
                    self.front(F, xx, 2, h2T, hf * 256, self.s2, 24, b0=2 * hf)
                for c in range(22):
                    w_ = wch[wi % 3]
                    S.dma(w_[:, 0, :, :], wup_d[:, c, :, :], g_wc[wi % 3])
                    S.dma(w_[:, 1, :, :], wup_d[:, 22 + c, :, :], g_wc[wi % 3])
                    wi += 1
                    ua, ub = upa[c % 2], upb[c % 2]
                    for i_, (u_, bank) in enumerate(((ua, pb[4]), (ub, pb[5]))):
                        for k in range(8):
                            S.mm(bank.f(0, 512), lhsT=w_[:, i_, k, :], rhs=h2T[:, k, :], start=(k == 0), stop=(k == 7))
                        S.copy("act", u_[:, :, 2:130], bank.f(0, 512).re("p (b t) -> p b t", b=4))
                        bh_ = pb[6 + i_]
                        for k in range(8):
                            S.mm(bh_.f(0, 8), lhsT=w_[:, i_, k, :], rhs=h2Th[:, k, 8 * t:8 * t + 8], start=(k == 0),
                                 stop=(k == 7))
                        S.tt("dve", u_[:, :, 0:2], bh_.f(0, 8).re("p (b i) -> p b i", i=2),
                             hexr[:, 8 * t:8 * t + 8].re("p (b i) -> p b i", i=2), ALU.mult)
                    conv(ua, c, aa[c % 2])
                    conv(ub, 22 + c, ab_[c % 2])
                    S.act(aa[c % 2][:, :, :], aa[c % 2][:, :, :], AF.Silu)
                    S.tt("dve", gT[:, c, :].re("p (b t) -> p b t", b=4), aa[c % 2][:, :, :], ab_[c % 2][:, :, :], ALU.mult)
                for blk in range(4):
                    cs = slice(blk * 128, (blk + 1) * 128)
                    for half in range(2):
                        bank = pb[6 + half]
                        hs = slice(half * 512, (half + 1) * 512)
                        for c in range(22):
                            S.mm(bank.f(0, 512), lhsT=gT[:, c, cs], rhs=wdnb[:, c, hs], start=(c == 0), stop=(c == 21))
                        tm = tmp[half]
                        S.tt("dve", tm[:, :], bank.f(0, 512), g2row[:, hs], ALU.mult)
                        S.tt("pool", xx[:, blk, hs], xx[:, blk, hs], tm[:, :], ALU.add)
                S.dma(orows[t], xx[:, :, :], g_x[t % 2])
            S.flush()


def _colL(v, n):
    return np.ascontiguousarray(np.asarray(v, np.float32).reshape(n, 128).T)


def _rep(v, n=128):
    v = np.asarray(v, np.float32).reshape(1, -1)
    return np.ascontiguousarray(np.repeat(v, n, axis=0))


def _consts():
    p = np.arange(128)
    same = (p[:, None] // 64) == (p[None, :] // 64)
    c = {}
    c["identf"] = np.eye(128, dtype=np.float32)
    c["tri2"] = (same & (p[:, None] <= p[None, :])).astype(np.float32)
    c["blk2"] = same.astype(np.float32)
    c["onesf"] = np.ones((128, 128), np.float32)
    c["cind"] = np.stack([(p < 64), (p >= 64)], axis=1).astype(np.float32)
    ma = np.where(same & (p[None, :] < p[:, None]), 0.0, BIG).astype(np.float32)
    mb = np.where(same & (p[None, :] >= p[:, None]), 0.0, -BIG).astype(np.float32)
    c["ma4"] = np.ascontiguousarray(np.tile(ma, (1, 4)))
    c["mb4"] = np.ascontiguousarray(np.tile(mb, (1, 4)))
    return c


def _host_inputs(inputs):
    x = np.asarray(inputs["x"], np.float32)
    cst = _consts()
    g = lambda k: np.asarray(inputs[k][0], np.float32)
    gcw = g("gdn_conv_w")
    gcwT = np.ascontiguousarray(gcw.reshape(4, 12, 128).transpose(2, 1, 0).reshape(128, 48))
    shared = {
        "ada_w": np.ascontiguousarray(g("ada_w")),
        "ada_bT": _colL(g("ada_b"), 48),
        "n1w": _colL(g("norm1_w"), 8),
        "n2w": _colL(g("norm2_w"), 8),
        "w_in": np.ascontiguousarray(g("w_in")),
        "gcw": gcwT,
        "dtb": _rep(g("gdn_dt_bias")),
        "alog": _rep(g("gdn_A_log")),
        "kslw": _colL(g("nsa_k_norm_slc"), 1),
        "kwnw": _colL(g("nsa_k_norm_win"), 1),
        "qnw": _colL(g("nsa_q_norm_w"), 1),
        "gonw4": _rep(np.tile(g("gdn_out_norm_w"), 4)),
        "onesf2": np.ones((128, 128), np.float32),
    }
    for nm in ("k", "v"):
        shared[f"cmp_{nm}_w1"] = np.ascontiguousarray(g(f"cmp_{nm}_w1").reshape(32, 128, 128).transpose(1, 0, 2))
        shared[f"cmp_{nm}_w2"] = np.ascontiguousarray(g(f"cmp_{nm}_w2"))
        shared[f"cmp_{nm}_posT"] = np.ascontiguousarray(g(f"cmp_{nm}_pos").T)
    shared["kcw"] = _colL(g("nsa_k_norm_cmp"), 1)
    shared["onesf3"] = np.ones((128, 128), np.float32)
    pm = np.ones((128, 1), np.float32)
    pm[127, 0] = 0.0
    shared["pm511"] = pm
    keys = np.arange(S_LEN)
    shared["eall"] = (keys[None, :] // 64 == np.arange(128)[:, None]).astype(np.float32)
    n = np.arange(512)
    js = np.arange(128)
    ov = np.minimum(16 * n[:, None] + 32, 64 * js[None, :] + 64) - np.maximum(16 * n[:, None], 64 * js[None, :])
    ov = np.clip(ov, 0, None).astype(np.float32) / 32.0
    ov[511] = 0.0
    shared["ovm"] = np.ascontiguousarray(ov.reshape(4, 128, 128).transpose(1, 0, 2))
    fw = g("ffn_conv_w")
    shared["fcw"] = np.ascontiguousarray(fw.reshape(3, 44, 128).transpose(2, 1, 0).reshape(128, 132))
    shared["fcb"] = _colL(g("ffn_conv_b"), 44)
    shared["w_out"] = np.ascontiguousarray(g("w_out"))
    shared["w_up"] = np.ascontiguousarray(g("ffn_w_up"))
    shared["w_dn"] = np.ascontiguousarray(g("ffn_w_down"))
    shared.update(cst)
    maps = []
    for core in range(8):
        b, r = core // 4, core % 4
        xo = np.concatenate([x[b, 128 * (4 * j + r):128 * (4 * j + r) + 128] for j in range(16)], axis=0)
        m = dict(shared)
        m["xb"] = np.ascontiguousarray(x[b])
        m["xo"] = np.ascontiguousarray(xo)
        m["cT"] = _colL(inputs["c"][b], 8)
        sel = np.zeros((128, 4), np.float32)
        sel[:, r] = 1.0
        m["sel"] = sel
        p = np.arange(128)
        q = np.arange(128)
        addm = np.zeros((128, 16, 128), np.float32)
        cmb = np.zeros((128, 16, 2, 128), np.float32)
        for j in range(16):
            qi = 4 * j + r
            tq = 128 * qi + q
            cur = tq // 64
            jj = np.arange(128)[None, :]
            valid = jj <= cur[:, None]
            forced = (jj == 0) | (jj == cur[:, None]) | (jj == cur[:, None] - 1)
            addm[:, j, :] = np.where(valid, np.where(forced, 1.0e4, 0.0), -1.0e30)
            lo = max(0, 32 * j - 1) // 128
            for slot in range(2):
                nn = 128 * (lo + slot) + p
                ok = (16 * nn[:, None] + 31) <= tq[None, :]
                cmb[:, j, slot, :] = np.where(ok, 0.0, -BIG)
        m["addm"] = addm
        m["cmb"] = cmb
        cbs = np.zeros((128, 4, 128), np.float32)
        for d in range(4):
            ok = (128 * (d - r) + p[:, None]) <= q[None, :]
            cbs[:, d, :] = np.where(ok, 0.0, -BIG)
        m["cbs"] = cbs
        cbw = np.zeros((128, 8, 128), np.float32)
        for e in range(8):
            rel = 128 * (e - 4 - r) + p[:, None]
            ok = (rel <= q[None, :]) & (rel > q[None, :] - 512)
            cbw[:, e, :] = np.where(ok, 0.0, -BIG)
        m["cbw"] = cbw
        hs_ = np.zeros((128, 4), np.float32)
        hs_[:, (r - 1) % 4] = 1.0
        m["hsel"] = hs_
        sab = np.zeros((128, 2), np.float32)
        sab[:, 0] = 1.0 if r >= 1 else 0.0
        sab[:, 1] = 1.0 if r == 0 else 0.0
        m["selAB"] = sab
        tq = np.array([128 * (4 * j + r) - 2 + i for j in range(16) for i in range(2)])
        ex = tq >= 0
        xh = np.zeros((32, D), np.float32)
        xh[ex] = x[b, tq[ex]]
        m["xh"] = xh
        m["hexr"] = _rep(ex.astype(np.float32))
        cur = tq // 64
        jj = np.arange(128)[None, :]
        valid = (jj <= cur[:, None]) & ex[:, None]
        forced = (jj == 0) | (jj == cur[:, None]) | (jj == cur[:, None] - 1)
        m["haddm"] = np.where(valid, np.where(forced, 1.0e4, 0.0), -1.0e30).astype(np.float32)
        nn = np.arange(512)
        okc = ((16 * nn[:, None] + 31) <= tq[None, :]) & ex[None, :]
        hcm = np.where(okc, 0.0, -BIG).astype(np.float32).reshape(4, 128, 1, 32)
        m["hcm"] = np.ascontiguousarray(np.broadcast_to(hcm, (4, 128, 4, 32)).transpose(1, 0, 2, 3).reshape(128, 4, 128))
        pos = np.arange(S_LEN)
        oks = (pos[:, None] <= tq[None, :]) & ex[None, :]
        okw = oks & (pos[:, None] > tq[None, :] - 512)
        for nm, ok in (("hsm", oks), ("hwm", okw)):
            a = np.where(ok, 0.0, -BIG).astype(np.float32).reshape(64, 128, 1, 32)
            m[nm] = np.ascontiguousarray(np.broadcast_to(a, (64, 128, 4, 32)).transpose(1, 0, 2, 3).reshape(128, 64, 128))
        maps.append(m)
    return maps


def run(inputs, debug=False, upto=9, ntiles=NT):
    bld = Builder(debug, ntiles)
    nc = bld.build(upto)
    maps = _host_inputs(inputs)
    maps = [{k: v for k, v in m.items() if k in bld.din} for m in maps]
    missing = [k for k in bld.din if k not in maps[0]]
    assert not missing, missing
    res = run_bass_kernel_spmd(nc, maps, core_ids=list(range(8)))
    return res.results


def kernel(**inputs):
    results = run(inputs)
    outp = np.zeros((2, S_LEN, D), np.float32)
    for core in range(8):
        b, r = core // 4, core % 4
        o = results[core]["out"]
        for j in range(16):
            qi = 4 * j + r
            outp[b, 128 * qi:128 * qi + 128] = o[128 * j:128 * j + 128]
    return outp
```

```python
import numpy as np
from contextlib import ExitStack
import concourse.bass as bass
import concourse.mybir as mybir
from concourse.bass_utils import run_bass_kernel_spmd

F32 = mybir.dt.float32
BF16 = mybir.dt.bfloat16
AF = mybir.ActivationFunctionType
ALU = mybir.AluOpType

D = 1024
S_LEN = 8192
NT = 16
NB = 64
N_IN = 3348
DFF = 2816
EPS = 1e-6
O_GQ, O_GK, O_GV, O_GZ, O_GA, O_GB, O_NQ, O_KC, O_VC, O_KSL, O_VSL, O_KWN, O_VWN, O_NG = (
    0, 512, 1024, 1536, 2048, 2052, 2056, 2568, 2696, 2824, 2952, 3080, 3208, 3336)
BIG = 30000.0


class Buf:
    __slots__ = ("name", "w", "rs", "const", "excl")

    def __init__(self, name, const=False, excl=False):
        self.name = name
        self.w = None
        self.rs = []
        self.const = const
        self.excl = excl


class V:
    __slots__ = ("ap", "bs")

    def __init__(self, ap, bs):
        self.ap = ap
        self.bs = bs if isinstance(bs, (list, tuple)) else [bs]

    def bitcast(self, dt):
        return V(self.ap.bitcast(dt), self.bs)

    def re(self, pat, **kw):
        return V(self.ap.rearrange(pat, **kw), self.bs)

    def bc(self, shape):
        return V(self.ap.broadcast_to(shape), self.bs)

    def __getitem__(self, k):
        return V(self.ap[k], self.bs)


class Tl:
    def __init__(self, t, b):
        self.t = t
        self.b = b

    def __getitem__(self, k):
        return V(self.t[k], self.b)


class Op:
    __slots__ = ("eng", "fn", "deps", "dmaw", "signal", "sigval", "grp")


class DGroup:
    def __init__(self, name, sem, full=False):
        self.name = name
        self.sem = sem
        self.count = 0
        self.full = full


ENGS = ("pe", "act", "dve", "pool", "sp")


class Sched:
    def __init__(self, nc, es):
        self.nc = nc
        self.es = es
        self.eng = {"pe": nc.tensor, "act": nc.scalar, "dve": nc.vector, "pool": nc.gpsimd, "sp": nc.sync}
        self.sem = {e: es.enter_context(nc.semaphore("s_" + e)) for e in ENGS}
        self.cnt = {e: 0 for e in ENGS}
        self.seen = {e: {} for e in ENGS}
        self.ops = []
        self.bufs = []
        self.groups = []
        self.nins = 0

    def buf(self, name, const=False, excl=False):
        b = Buf(name, const, excl)
        self.bufs.append(b)
        return b

    def group(self, name, full=False):
        g = DGroup(name, self.es.enter_context(self.nc.semaphore("g_" + name)), full)
        self.groups.append(g)
        return g

    def op(self, eng, fn, r=(), w=(), grp=None):
        o = Op()
        o.eng = eng
        o.fn = fn
        o.signal = False
        o.sigval = None
        o.grp = grp
        deps = []
        for b in r:
            if b.w is not None:
                deps.append(b.w)
            if b.excl:
                deps.extend(x for x in b.rs if x.eng != eng)
        for b in w:
            if b.w is not None:
                deps.append(b.w)
            deps.extend(b.rs)
        seen = set()
        dd = []
        for d in deps:
            if id(d) in seen or d is o:
                continue
            seen.add(id(d))
            if d.eng == "pe" and eng == "pe":
                continue
            if grp is not None and grp.full and d.grp is grp:
                continue
            dd.append(d)
        o.deps = dd
        o.dmaw = {}
        for d in dd:
            if d.grp is None:
                d.signal = True
            else:
                o.dmaw[d.grp.name] = d.grp.count
        if grp is not None:
            grp.count += 1
        for b in w:
            b.w = o
            b.rs = []
        for b in r:
            if not b.const and b.w is not o:
                b.rs.append(o)
        self.ops.append(o)
        return o

    def flush(self, barrier=True):
        if barrier:
            last = {}
            for o in self.ops:
                last[o.eng] = o
            for e, o in last.items():
                if o.grp is None:
                    o.signal = True
        for o in self.ops:
            e = self.eng[o.eng]
            for d in o.deps:
                if d.grp is not None:
                    key = "g_" + d.grp.name
                    val = 16 * (d.grp.count if d.grp.full else o.dmaw[d.grp.name])
                    sem = d.grp.sem
                else:
                    key = d.eng
                    val = d.sigval
                    sem = self.sem[d.eng]
                    assert val is not None, (o.eng, d.eng)
                if self.seen[o.eng].get(key, 0) >= val:
                    continue
                e.wait_ge(sem, val)
                self.seen[o.eng][key] = val
            ins = o.fn(e)
            self.nins += 1
            if o.grp is not None:
                ins.then_inc(o.grp.sem, 16)
            elif o.signal:
                self.cnt[o.eng] += 1
                o.sigval = self.cnt[o.eng]
                ins.then_inc(self.sem[o.eng], 1)
        self.ops = []
        if barrier:
            for en in ENGS:
                e = self.eng[en]
                for e2 in ENGS:
                    if e2 == en or e2 == "sp":
                        continue
                    if self.cnt[e2] > self.seen[en].get(e2, 0):
                        e.wait_ge(self.sem[e2], self.cnt[e2])
                        self.seen[en][e2] = self.cnt[e2]
                for g in self.groups:
                    key = "g_" + g.name
                    if 16 * g.count > self.seen[en].get(key, 0):
                        e.wait_ge(g.sem, 16 * g.count)
                        self.seen[en][key] = 16 * g.count
            for b in self.bufs:
                b.w = None
                b.rs = []

    @staticmethod
    def _bs(*vs):
        out = []
        for v in vs:
            if isinstance(v, V):
                for b in v.bs:
                    if b not in out:
                        out.append(b)
        return out

    @staticmethod
    def _a(v):
        return v.ap if isinstance(v, V) else v

    def mm(self, out, lhsT, rhs, start=True, stop=True):
        o_, l_, r_ = out.ap, lhsT.ap, rhs.ap
        self.op("pe", lambda e: e.matmul(o_, lhsT=l_, rhs=r_, start=start, stop=stop),
                r=self._bs(lhsT, rhs), w=self._bs(out))

    def mm_(self, out, lhsT, rhs, start=True, stop=True, skip=False):
        o_, l_, r_ = out.ap, lhsT.ap, rhs.ap
        self.op("pe", lambda e: e.matmul(o_, lhsT=l_, rhs=r_, start=start, stop=stop, skip_group_check=skip),
                r=self._bs(lhsT, rhs), w=self._bs(out))

    def tr(self, out, in_, ident):
        o_, i_, d_ = out.ap, in_.ap, ident.ap
        self.op("pe", lambda e: e.transpose(o_, i_, d_), r=self._bs(in_, ident), w=self._bs(out))

    def act(self, out, in_, func, bias=None, scale=None, accum=None, eng="act"):
        kw = {}
        if bias is not None:
            kw["bias"] = self._a(bias)
        if scale is not None:
            kw["scale"] = self._a(scale)
        if accum is not None:
            kw["accum_out"] = accum.ap
        o_, i_ = out.ap, in_.ap
        self.op("act", lambda e: e.activation(out=o_, in_=i_, func=func, **kw),
                r=self._bs(in_, bias, scale), w=self._bs(out, accum))

    def tt(self, eng, out, in0, in1, op):
        o_, a_, b_ = out.ap, in0.ap, in1.ap
        self.op(eng, lambda e: e.tensor_tensor(out=o_, in0=a_, in1=b_, op=op),
                r=self._bs(in0, in1), w=self._bs(out))

    def ts(self, eng, out, in0, s1, s2=None, op0=ALU.mult, op1=None):
        o_, a_ = out.ap, in0.ap
        s1_, s2_ = self._a(s1), self._a(s2)
        kw = {}
        if op1 is not None:
            kw["op1"] = op1
        self.op(eng, lambda e: e.tensor_scalar(out=o_, in0=a_, scalar1=s1_, scalar2=s2_, op0=op0, **kw),
                r=self._bs(in0, s1, s2), w=self._bs(out))

    def stt(self, out, in0, scalar, in1, op0, op1):
        o_, a_, b_ = out.ap, in0.ap, in1.ap
        s_ = self._a(scalar)
        self.op("dve", lambda e: e.scalar_tensor_tensor(out=o_, in0=a_, scalar=s_, in1=b_, op0=op0, op1=op1),
                r=self._bs(in0, scalar, in1), w=self._bs(out))

    def copy(self, eng, out, in_):
        o_, i_ = out.ap, in_.ap
        if eng == "act":
            self.op("act", lambda e: e.copy(out=o_, in_=i_), r=self._bs(in_), w=self._bs(out))
        else:
            self.op(eng, lambda e: e.tensor_copy(out=o_, in_=i_), r=self._bs(in_), w=self._bs(out))

    def recip(self, out, in_):
        o_, i_ = out.ap, in_.ap
        self.op("dve", lambda e: e.reciprocal(out=o_, in_=i_), r=self._bs(in_), w=self._bs(out))

    def memset(self, eng, out, val):
        o_ = out.ap
        self.op(eng, lambda e: e.memset(o_, val), r=[], w=self._bs(out))

    def dma(self, out, in_, grp, eng="sp"):
        o_, i_ = self._a(out), self._a(in_)
        self.op(eng, lambda e: e.dma_start(out=o_, in_=i_), r=self._bs(in_), w=self._bs(out), grp=grp)


class Bank:
    def __init__(self, t, q):
        self.t = t
        self.q = q

    def f(self, c0, c1, p0=0, p1=128):
        return V(self.t[p0:p1, c0:c1], self.q[0:1])

    def b(self, c0, c1, p0=0, p1=128):
        return V(self.t[p0:p1, :].bitcast(BF16)[:, c0:c1], self.q[0:1])


W1_SEGS = ((0, 1536, 0), (2048, 2056, 1536), (2568, 3336, 1544))
W1_N = 2312
C_AB, C_KC, C_VC, C_KSL, C_VSL, C_KWN, C_VWN = 1536, 1544, 1672, 1800, 1928, 2056, 2184


class Builder:
    def __init__(self, debug=False, ntiles=NT):
        self.debug = debug
        self.ntiles = ntiles
        import os
        self.stop = int(os.environ.get('K_STOP', '0'))
        self.var = int(os.environ.get('K_VAR', '0'))
        self.halo = int(os.environ.get('K_HALO', '0'))
        self.nc = bass.Bass("TRN2", target_bir_lowering=False)
        self.din = {}
        self.dout = {}

    def inp(self, name, shape, dt=F32):
        self.din[name] = self.nc.dram_tensor(name, list(shape), dt, kind="ExternalInput").ap()
        return self.din[name]

    def outp(self, name, shape, dt=F32):
        self.dout[name] = self.nc.dram_tensor(name, list(shape), dt, kind="ExternalOutput").ap()
        return self.dout[name]

    def scratch(self, name, shape, dt=F32):
        if self.debug:
            return self.outp(name, shape, dt)
        return self.nc.dram_tensor(name, list(shape), dt, kind="Internal").ap()

    def sb(self, es, name, shape, dt=F32, const=False):
        t = es.enter_context(self.nc.sbuf_tensor(name, list(shape), dt))
        return Tl(t, self.S.buf(name, const))

    def cload(self, es, name, shape, grp):
        d = self.inp(name, shape)
        t = self.sb(es, "c_" + name, shape, F32, const=True)
        idx = tuple(slice(None) for _ in shape)
        self.S.dma(t[idx], d[idx], grp)
        return t

    def build(self, upto=9):
        nc = self.nc
        I = self.inp
        self.xb = I("xb", [S_LEN, D])
        self.xo = I("xo", [2048, D])
        cT = I("cT", [128, 8])
        ada_w = I("ada_w", [D, 6 * D])
        self.w_in = I("w_in", [D, N_IN])
        self.out = self.outp("out", [2048, D])
        self.o_own = self.scratch("o_own", [16, 128, 512])
        self.kslT_d = self.scratch("kslT_d", [128, S_LEN], BF16)
        self.kwnT_d = self.scratch("kwnT_d", [128, S_LEN], BF16)
        self.kcT_d = self.scratch("kcT_d", [128, S_LEN], BF16)
        self.vcT_d = self.scratch("vcT_d", [128, S_LEN], BF16)
        self.vsl_d = self.scratch("vsl_d", [128, NB, 128], BF16)
        self.vwn_d = self.scratch("vwn_d", [128, NB, 128], BF16)
        self.mix_d = self.scratch("mix_d", [128, 8, 2048], BF16)
        self.oprev_d = self.scratch("oprev_d", [16, 2, 512])
        self.mixh_d = self.scratch("mixh_d", [128, 8, 32], BF16)

        with ExitStack() as top:
            S = self.S = Sched(nc, top)
            self.g_const = g_const = S.group("const", full=True)
            self.g_out = S.group("outw")
            self.idf = idf = self.cload(top, "identf", [128, 128], g_const)
            self.idb = idb = self.sb(top, "idb", [128, 128], BF16)
            S.copy("dve", idb[:, :], idf[:, :])
            self.modT = modT = self.sb(top, "modT", [128, 48])
            self.s1 = s1 = self.sb(top, "s1", [128, 8])
            self.s2 = s2 = self.sb(top, "s2", [128, 8])
            n1 = self.cload(top, "n1w", [128, 8], g_const)
            n2 = self.cload(top, "n2w", [128, 8], g_const)
            self.pb = []
            for i in range(8):
                t = top.enter_context(nc.psum_tensor(f"pb{i}", [128, 512], F32))
                self.pb.append(Bank(t, [S.buf(f"pb{i}", excl=True)]))
            pb = self.pb
            self.bankA = [pb[2], pb[3]]
            self.bankB = [pb[4], pb[5]]
            self.bankC = [pb[0], pb[1]]

            with ExitStack() as es:
                ct = self.sb(es, "ct", [128, 8])
                sc = self.sb(es, "sc", [128, 8])
                abT = self.cload(es, "ada_bT", [128, 48], g_const)
                S.dma(ct[:, :], cT[:, :], g_const)
                S.act(sc[:, :], ct[:, :], AF.Silu)
                aw = [self.sb(es, f"aw{i}", [128, 6 * D]) for i in range(2)]
                g_aw = [S.group(f"aw{i}") for i in range(2)]
                for k in range(8):
                    sl = k % 2
                    for hh in range(4):
                        S.dma(aw[sl][:, hh * 1536:(hh + 1) * 1536],
                              ada_w[k * 128:(k + 1) * 128, hh * 1536:(hh + 1) * 1536], g_aw[sl])
                    pm = pb[k % 2]
                    for cc in range(48):
                        S.mm(pm.f(cc, cc + 1), lhsT=aw[sl][:, cc * 128:(cc + 1) * 128], rhs=sc[:, k:k + 1])
                    S.tt("dve", modT[:, :], pm.f(0, 48), (abT if k == 0 else modT)[:, :], ALU.add)
                S.stt(s1[:, :], modT[:, 8:16], 1.0, n1[:, :], ALU.add, ALU.mult)
                S.stt(s2[:, :], modT[:, 32:40], 1.0, n2[:, :], ALU.add, ALU.mult)
                S.flush()

            if upto >= 1:
                self.phase1()
            keep = dict(qnT=self.sb(top, "qnT", [128, 4, 2048], BF16), gates=self.sb(top, "gates", [128, 16, 12]),
                        qnTh=self.sb(top, "qnTh", [128, 4, 32], BF16), gatesh=self.sb(top, "gatesh", [32, 12]))
            if upto >= 2:
                self.phase2(keep)
                if self.debug:
                    S.dma(self.outp("d_qnT", [128, 4, 2048], BF16)[:, :, :], keep["qnT"][:, :, :], self.g_out)
                    S.dma(self.outp("d_gates", [128, 16, 12])[:, :, :], keep["gates"][:, :, :], self.g_out)
            if upto >= 3:
                S.flush()
                self.phase3(keep)
            if upto >= 4:
                S.flush()
                if self.debug:
                    self.dx1 = self.outp("d_x1", [4, 128, 4, D])
                self.phase45()
            S.flush()
        return nc

    def front(self, F, xt, nb, hT, c0, scol, sh_c0, b0=0):
        S = self.S
        pb = self.pb
        ssq, rt, rstd, junk, xs = F["ssq"], F["rt"], F["rstd"], F["junk"], F["xs"]
        for b2 in range(nb):
            S.act(junk[:, :], xt[:, b0 + b2, :], AF.Square, accum=ssq[:, b2:b2 + 1])
        S.act(rt[:, 0:nb], ssq[:, 0:nb], AF.Sqrt, bias=EPS, scale=1.0 / D)
        S.recip(rstd[:, 0:nb], rt[:, 0:nb])
        for b2 in range(nb):
            S.ts("pool", xs[:, b2, :], xt[:, b0 + b2, :], rstd[:, b2:b2 + 1], 1.0, ALU.mult, ALU.mult)
        w = nb * 128
        for half in range(2):
            bank = pb[half]
            for k in range(half * 4, half * 4 + 4):
                off = (k % 4) * 256
                for b2 in range(nb):
                    S.tr(bank.b(off + b2 * 128, off + (b2 + 1) * 128), xs[:, b2, k * 128:(k + 1) * 128],
                         self.idb[:, :])
            for k in range(half * 4, half * 4 + 4):
                off = (k % 4) * 256
                S.act(hT[:, k, c0:c0 + w], bank.b(off, off + w), AF.Identity,
                      bias=self.modT[:, sh_c0 + k:sh_c0 + k + 1], scale=scol[:, k:k + 1])

    def front32(self, F, xt, hT, scol, sh_c0):
        S = self.S
        bank = self.pb[0]
        ssq, rt, rstd, junk, xs = F["ssq"], F["rt"], F["rstd"], F["junk"], F["xs"]
        S.act(junk[0:32, :], xt[:, :], AF.Square, accum=ssq[0:32, 0:1])
        S.act(rt[0:32, 0:1], ssq[0:32, 0:1], AF.Sqrt, bias=EPS, scale=1.0 / D)
        S.recip(rstd[0:32, 0:1], rt[0:32, 0:1])
        S.ts("pool", xs[0:32, 0, :], xt[:, :], rstd[0:32, 0:1], 1.0, ALU.mult, ALU.mult)
        for k in range(8):
            S.tr(bank.b(k * 32, (k + 1) * 32), xs[0:32, 0, k * 128:(k + 1) * 128], self.idb[0:32, 0:32])
        for k in range(8):
            S.act(hT[:, k, :], bank.b(k * 32, (k + 1) * 32), AF.Identity,
                  bias=self.modT[:, sh_c0 + k:sh_c0 + k + 1], scale=scol[:, k:k + 1])

    def phase1(self):
        S = self.S
        nc = self.nc
        pb = self.pb
        gc_ = S.group("const1", full=True)
        idb, idf = self.idb, self.idf
        with ExitStack() as es:
            tri2 = self.cload(es, "tri2", [128, 128], gc_)
            blk2 = self.cload(es, "blk2", [128, 128], gc_)
            onesf = self.cload(es, "onesf", [128, 128], gc_)
            cind = self.cload(es, "cind", [128, 2], gc_)
            ma4 = self.cload(es, "ma4", [128, 512], gc_)
            mb4 = self.cload(es, "mb4", [128, 512], gc_)
            sel = self.cload(es, "sel", [128, 4], gc_)
            hsel = self.cload(es, "hsel", [128, 4], gc_)
            cw = self.cload(es, "gcw", [128, 48], gc_)
            dtb = self.cload(es, "dtb", [128, 4], gc_)
            alog = self.cload(es, "alog", [128, 4], gc_)
            kslw = self.cload(es, "kslw", [128, 1], gc_)
            kwnw = self.cload(es, "kwnw", [128, 1], gc_)
            negones = self.sb(es, "negones", [128, 128])
            S.ts("dve", negones[:, :], onesf[:, :], -1.0, None, ALU.mult)
            onesb = self.sb(es, "onesb", [128, 128], BF16)
            S.copy("dve", onesb[:, :], onesf[:, :])
            i4b = self.sb(es, "i4b", [128, 512], BF16)
            for h in range(4):
                S.copy("dve", i4b[:, h * 128:(h + 1) * 128], idf[:, :])
            negA = self.sb(es, "negA", [128, 4])
            S.act(negA[:, :], alog[:, :], AF.Exp)
            S.ts("dve", negA[:, :], negA[:, :], -1.0, None, ALU.mult)

            winb = self.sb(es, "winb", [128, 8, W1_N], BF16)
            with ExitStack() as es2:
                wst = [self.sb(es2, f"wst{i}", [128, W1_N]) for i in range(2)]
                g_w = [S.group(f"wst{i}") for i in range(2)]
                for k in range(8):
                    sl = k % 2
                    for (a, b_, o) in W1_SEGS:
                        S.dma(wst[sl][:, o:o + (b_ - a)], self.w_in[k * 128:(k + 1) * 128, a:b_], g_w[sl])
                    S.copy("dve", winb[:, k, 0:1024], wst[sl][:, 0:1024])
                    S.copy("pool", winb[:, k, 1024:W1_N], wst[sl][:, 1024:W1_N])
                S.flush()

            F = dict(ssq=self.sb(es, "f_ssq", [128, 4]), rt=self.sb(es, "f_rt", [128, 4]),
                     rstd=self.sb(es, "f_rstd", [128, 4]), junk=self.sb(es, "f_junk", [128, 1024], BF16),
                     xs=self.sb(es, "f_xs", [128, 2, 1024], BF16))
            xt = [self.sb(es, f"xt{i}", [128, 2, 1024]) for i in range(2)]
            g_x = [S.group(f"xt{i}") for i in range(2)]
            hT = self.sb(es, "hT", [128, 8, 512], BF16)
            pre = [self.sb(es, f"pre{i}", [128, 515]) for i in range(3)]
            hist = self.sb(es, "hist", [128, 12, 3])
            S.memset("pool", hist[:, :, :], 0.0)
            acc = [self.sb(es, f"cacc{i}", [128, 512]) for i in range(3)]
            yf = [self.sb(es, f"yf{i}", [128, 512]) for i in range(3)]
            sq = [self.sb(es, f"sq{i}", [128, 512], BF16) for i in range(3)]
            rtt = [self.sb(es, f"rtt{i}", [128, 512]) for i in range(3)]
            qT = self.sb(es, "qT", [128, 4, 512], BF16)
            kT = self.sb(es, "kT", [128, 4, 512], BF16)
            vT = self.sb(es, "vT", [128, 4, 512], BF16)
            st_f = [[self.sb(es, f"stf{c}_{i}", [128, 512], BF16) for i in range(2)] for c in range(4)]
            st_v = [[self.sb(es, f"stv{c}_{i}", [128, 4, 128], BF16) for i in range(2)] for c in range(2)]
            g_sf = [[S.group(f"sf{c}_{i}") for i in range(2)] for c in range(4)]
            g_sv = [[S.group(f"sv{c}_{i}") for i in range(2)] for c in range(2)]
            ab = self.sb(es, "ab", [128, 4, 8])
            S32 = self.sb(es, "S32", [128, 4, 128])
            Sb = self.sb(es, "Sb", [128, 4, 128], BF16)
            S.memset("pool", S32[:, :, :], 0.0)
            S.memset("pool", Sb[:, :, :], 0.0)
            oacc = [self.sb(es, f"oacc{i}", [128, 512]) for i in range(2)]
            g_o = [S.group(f"oacc{i}") for i in range(2)]
            oph = [self.sb(es, f"oph{i}", [128, 512]) for i in range(2)]
            g_oh = [S.group(f"oph{i}") for i in range(2)]
            G = []
            for s_ in range(2):
                g = {}
                for nm, shp, dt in (("sm", [128, 64], F32), ("TG", [128, 4, 128], F32), ("Xs", [128, 512], F32),
                                    ("XA", [128, 512], F32), ("XB", [128, 512], F32), ("EG", [128, 4, 128], F32),
                                    ("Nb", [128, 4, 128], BF16), ("aT", [128, 4, 128], BF16),
                                    ("qdT", [128, 4, 128], BF16), ("kw", [128, 4, 128], BF16),
                                    ("kd", [128, 4, 128], BF16), ("vb", [128, 4, 128], BF16),
                                    ("NTb", [128, 4, 128], BF16), ("RTb", [128, 4, 128], BF16),
                                    ("P0", [128, 4, 128], BF16), ("P1", [128, 4, 128], BF16),
                                    ("Q0", [128, 4, 128], BF16), ("Q1", [128, 4, 128], BF16),
                                    ("u", [128, 4, 128], F32), ("wTb", [128, 4, 128], BF16),
                                    ("vn", [128, 4, 128], BF16)):
                    g[nm] = self.sb(es, f"g{s_}_{nm}", shp, dt)
                G.append(g)

            xrows = self.xb.rearrange("(n b p) d -> n p b d", b=2, p=128)

            def load_x(n):
                S.dma(xt[n % 2][:, :, :], xrows[n], g_x[n % 2])

            load_x(0)
            for ti in range(self.ntiles):
                for hf in range(2):
                    n = 2 * ti + hf
                    if n + 1 < 2 * NT:
                        load_x(n + 1)
                    self.front(F, xt[n % 2], 2, hT, hf * 256, self.s1, 0)
                if self.stop == 1:
                    continue
                nsa_ch = ((C_KC, self.kcT_d, None), (C_VC, self.vcT_d, None), (C_KSL, self.kslT_d, kslw),
                          (C_KWN, self.kwnT_d, kwnw))

                def st1(c, pos):
                    bank = pb[2 + (pos % 2)]
                    coff = c * 128 if c < 12 else nsa_ch[c - 12][0]
                    for k in range(8):
                        S.mm(bank.f(0, 512), lhsT=winb[:, k, coff:coff + 128], rhs=hT[:, k, :],
                             start=(k == 0), stop=(k == 7))
                    if c < 12:
                        p_ = pre[pos % 3]
                        S.copy("pool", p_[:, 0:3], hist[:, c, :])
                        S.copy("act", p_[:, 3:515], bank.f(0, 512))
                        S.copy("pool", hist[:, c, :], p_[:, 512:515])
                        a_ = acc[pos % 3]
                        S.ts("pool", a_[:, :], p_[:, 0:512], cw[:, c * 4:c * 4 + 1], 1.0, ALU.mult, ALU.mult)
                        for j in range(1, 4):
                            S.stt(a_[:, :], p_[:, j:j + 512], cw[:, c * 4 + j:c * 4 + j + 1], a_[:, :], ALU.mult, ALU.add)
                    else:
                        ci = c - 12
                        wcol = nsa_ch[ci][2]
                        if wcol is None:
                            S.copy("act", st_f[ci][ti % 2][:, :], bank.f(0, 512))
                            S.dma(nsa_ch[ci][1][:, ti * 512:(ti + 1) * 512], st_f[ci][ti % 2][:, :], g_sf[ci][ti % 2])
                        else:
                            S.copy("act", yf[pos % 3][:, :], bank.f(0, 512))
                            S.act(sq[pos % 3][:, :], bank.f(0, 512), AF.Square)

                def st2(c, pos):
                    h = c % 4
                    if c >= 12 and nsa_ch[c - 12][2] is None:
                        return
                    if c >= 8 and c < 12:
                        S.act(vT[:, h, :], acc[pos % 3][:, :], AF.Silu)
                        return
                    if c < 8:
                        S.act(yf[pos % 3][:, :], acc[pos % 3][:, :], AF.Silu)
                        S.act(sq[pos % 3][:, :], yf[pos % 3][:, :], AF.Square)
                    S.mm(pb[4 + (pos % 2)].f(0, 512), lhsT=onesb[:, :], rhs=sq[pos % 3][:, :])

                def st3(c, pos):
                    h = c % 4
                    if (c >= 8 and c < 12) or (c >= 12 and nsa_ch[c - 12][2] is None):
                        return
                    bk2 = pb[4 + (pos % 2)]
                    r_ = rtt[pos % 3]
                    if c < 4:
                        S.act(r_[:, :], bk2.f(0, 512), AF.Sqrt, bias=EPS * 128.0, scale=128.0)
                    elif c < 8:
                        S.act(r_[:, :], bk2.f(0, 512), AF.Sqrt, bias=EPS, scale=1.0)
                    else:
                        S.act(r_[:, :], bk2.f(0, 512), AF.Sqrt, bias=EPS, scale=1.0 / 128.0)
                    S.recip(r_[:, :], r_[:, :])
                    if c < 8:
                        S.tt("pool", (qT if c < 4 else kT)[:, h, :], yf[pos % 3][:, :], r_[:, :], ALU.mult)
                    else:
                        ci = c - 12
                        st = st_f[ci][ti % 2]
                        S.stt(st[:, :], yf[pos % 3][:, :], nsa_ch[ci][2][:, 0:1], r_[:, :], ALU.mult, ALU.mult)
                        S.dma(nsa_ch[ci][1][:, ti * 512:(ti + 1) * 512], st[:, :], g_sf[ci][ti % 2])

                order = [12, 13, 14, 15] + list(range(12))
                for i in range(len(order) + 2):
                    if i < len(order):
                        st1(order[i], i)
                    if 0 <= i - 1 < len(order):
                        st2(order[i - 1], i - 1)
                    if 0 <= i - 2 < len(order):
                        st3(order[i - 2], i - 2)
                if self.stop == 3:
                    continue
                for blk in range(4):
                    bank = pb[6]
                    for vi, coff in enumerate((C_VSL, C_VWN)):
                        for k in range(8):
                            if self.var == 4:
                                break
                            S.mm(bank.f(vi * 128, (vi + 1) * 128), lhsT=hT[:, k, blk * 128:(blk + 1) * 128],
                                 rhs=winb[:, k, coff:coff + 128], start=(k == 0), stop=(k == 7))
                    for k in range(8):
                        if self.var == 1:
                            break
                        S.mm(bank.f(256, 264), lhsT=hT[:, k, blk * 128:(blk + 1) * 128],
                             rhs=winb[:, k, C_AB:C_AB + 8], start=(k == 0), stop=(k == 7))
                    if self.var != 3:
                        S.copy("act", st_v[0][ti % 2][:, blk, :], bank.f(0, 128))
                        S.copy("act", st_v[1][ti % 2][:, blk, :], bank.f(128, 256))
                    if self.var != 5:
                        S.copy("dve", ab[:, blk, :], bank.f(256, 264))
                if self.var != 2:
                    S.dma(self.vsl_d[:, ti * 4:(ti + 1) * 4, :], st_v[0][ti % 2][:, :, :], g_sv[0][ti % 2])
                    S.dma(self.vwn_d[:, ti * 4:(ti + 1) * 4, :], st_v[1][ti % 2][:, :, :], g_sv[1][ti % 2])

                if self.stop == 4:
                    continue
                for pair in range(2):
                    blks = (2 * pair, 2 * pair + 1)
                    for s_, blk in enumerate(blks):
                        self.gdn_local_1(G[s_], s_, blk, ab, tri2, blk2, onesf, negones, cind, ma4, mb4, dtb, negA,
                                         qT, kT, vT)
                    if self.stop == 5:
                        continue
                    self.gdn_solve(G, blks, i4b)
                    if self.stop == 6:
                        continue
                    for s_, blk in enumerate(blks):
                        self.gdn_uw(G[s_], s_)
                    if self.stop == 7:
                        continue
                    for s_, blk in enumerate(blks):
                        oa = oacc[ti % 2]
                        self.gdn_recur(G[s_], S32, Sb, sel, blk, oa, hsel, oph[ti % 2])
                S.dma(self.o_own[ti], oacc[ti % 2][:, :], g_o[ti % 2])
                S.dma(self.oprev_d[ti], oph[ti % 2][126:128, :], g_oh[ti % 2])
            S.flush()

    def gdn_local_1(self, g, s_, blk, ab, tri2, blk2, onesf, negones, cind, ma4, mb4, dtb, negA, qT, kT, vT):
        S = self.S
        pb = self.pb
        sm = g["sm"]
        cs = slice(blk * 128, (blk + 1) * 128)
        x_, t_, gg, be, nb_, gci, gcv, gl, egc, ekd, bw, glS = (
            sm[:, 0:4], sm[:, 4:8], sm[:, 8:12], sm[:, 12:16], sm[:, 16:20], sm[:, 20:28], sm[:, 28:32],
            sm[:, 32:36], sm[:, 36:40], sm[:, 40:44], sm[:, 44:48], sm[:, 48:56])
        S.tt("dve", x_, ab[:, blk, 0:4], dtb[:, :], ALU.add)
        S.stt(t_, x_, -1.0, x_, ALU.mult, ALU.max)
        S.act(t_, t_, AF.Exp, scale=-1.0)
        S.act(t_, t_, AF.Ln, bias=1.0)
        S.stt(t_, x_, 0.0, t_, ALU.max, ALU.add)
        S.tt("dve", gg, t_, negA[:, :], ALU.mult)
        S.act(be, ab[:, blk, 4:8], AF.Sigmoid)
        S.ts("dve", nb_, be, -1.0, None, ALU.mult)
        gci3 = gci.re("p (h i) -> p h i", i=2)
        for i in range(2):
            S.ts("dve", gci3[:, :, i], gg, cind[:, i:i + 1], None, ALU.mult)
        for h in range(4):
            S.ts("pool", g["TG"][:, h, :], tri2[:, :], gg[:, h:h + 1], 1.0, ALU.mult, ALU.mult)
        bS = pb[6]
        o0 = 384 + s_ * 32
        S.mm(bS.f(o0, o0 + 4), lhsT=tri2[:, :], rhs=gg)
        S.mm(bS.f(o0 + 4, o0 + 8), lhsT=blk2[:, :], rhs=gg)
        S.mm(bS.f(o0 + 8, o0 + 16), lhsT=onesf[:, :], rhs=gci)
        bX = self.bankA[s_]
        for h in range(4):
            S.mm(bX.f(h * 128, (h + 1) * 128), lhsT=onesf[:, :], rhs=g["TG"][:, h, :], start=True, stop=False)
            S.mm(bX.f(h * 128, (h + 1) * 128), lhsT=g["TG"][:, h, :], rhs=negones[:, :], start=False, stop=True)
        S.copy("dve", sm[:, 28:36], bS.f(o0, o0 + 8))
        S.act(glS, bS.f(o0 + 8, o0 + 16), AF.Exp)
        S.act(egc, gcv, AF.Exp)
        S.tt("dve", ekd, gl, gcv, ALU.subtract)
        S.act(ekd, ekd, AF.Exp)
        S.tt("dve", bw, be, egc, ALU.mult)
        S.copy("act", g["Xs"][:, :], bX.f(0, 512))
        for h in range(4):
            S.act(g["EG"][:, h, :], bX.f(h * 128, (h + 1) * 128), AF.Exp, bias=gcv[:, h:h + 1])
        S.tt("pool", g["XA"][:, :], g["Xs"][:, :], ma4[:, :], ALU.add)
        S.tt("pool", g["XB"][:, :], g["Xs"][:, :], mb4[:, :], ALU.add)
        S.act(g["XA"][:, :], g["XA"][:, :], AF.Exp, scale=-1.0)
        S.act(g["XB"][:, :], g["XB"][:, :], AF.Exp)
        bK = self.bankA[s_]
        bQ = self.bankB[s_]
        for h in range(4):
            S.mm(bK.f(h * 128, (h + 1) * 128), lhsT=kT[:, h, cs], rhs=kT[:, h, cs])
        for h in range(4):
            S.mm(bQ.f(h * 128, (h + 1) * 128), lhsT=kT[:, h, cs], rhs=qT[:, h, cs])
        for h in range(4):
            S.stt(g["Nb"][:, h, :], bK.f(h * 128, (h + 1) * 128), nb_[:, h:h + 1],
                  g["XA"][:, h * 128:(h + 1) * 128], ALU.mult, ALU.mult)
        S.tt("dve", g["aT"][:, :, :].re("p h c -> p (h c)"), bQ.f(0, 512), g["XB"][:, :], ALU.mult)
        S.tt("pool", g["qdT"][:, :, :], qT[:, :, cs], g["EG"][:, :, :], ALU.mult)
        bT = pb[7]
        for h in range(4):
            S.tr(bT.b(h * 128, (h + 1) * 128), kT[:, h, cs], self.idb[:, :])
        for h in range(4):
            S.tr(bT.b(512 + h * 128, 512 + (h + 1) * 128), vT[:, h, cs], self.idb[:, :])
        kt3 = bT.b(0, 512).re("p (h d) -> p h d", h=4)
        vt3 = bT.b(512, 1024).re("p (h d) -> p h d", h=4)
        S.tt("dve", g["kw"][:, :, :], kt3, V(bw.ap.unsqueeze(2).broadcast_to([128, 4, 128]), bw.bs), ALU.mult)
        S.tt("dve", g["kd"][:, :, :], kt3, V(ekd.ap.unsqueeze(2).broadcast_to([128, 4, 128]), ekd.bs), ALU.mult)
        S.tt("dve", g["vb"][:, :, :], vt3, V(be.ap.unsqueeze(2).broadcast_to([128, 4, 128]), be.bs), ALU.mult)

    def gdn_solve(self, G, blks, i4b):
        S = self.S
        idb = self.idb
        for s_ in range(len(blks)):
            g = G[s_]
            bA, bC = self.bankA[s_], self.bankC[s_]
            for h in range(4):
                S.mm(bA.f(h * 128, (h + 1) * 128), lhsT=g["Nb"][:, h, :], rhs=idb[:, :])
            S.copy("act", g["NTb"][:, :, :].re("p h c -> p (h c)"), bA.f(0, 512))
            S.mm(bC.f(0, 512), lhsT=idb[:, :], rhs=i4b[:, :], start=True, stop=False)
            for h in range(4):
                S.mm(bC.f(h * 128, (h + 1) * 128), lhsT=g["Nb"][:, h, :], rhs=idb[:, :], start=False, stop=(h == 3))
            S.copy("act", g["RTb"][:, :, :].re("p h c -> p (h c)"), bC.f(0, 512))
        P = [G[s_]["Nb"] for s_ in range(len(blks))]
        Q = [G[s_]["NTb"] for s_ in range(len(blks))]
        for k in range(1, 6):
            for s_ in range(len(blks)):
                g = G[s_]
                bA, bB, bC = self.bankA[s_], self.bankB[s_], self.bankC[s_]
                Pn = g["P%d" % (k % 2)]
                Qn = g["Q%d" % (k % 2)]
                for h in range(4):
                    S.mm(bA.f(h * 128, (h + 1) * 128), lhsT=Q[s_][:, h, :], rhs=P[s_][:, h, :])
                if k < 5:
                    for h in range(4):
                        S.mm(bB.f(h * 128, (h + 1) * 128), lhsT=P[s_][:, h, :], rhs=Q[s_][:, h, :])
                S.copy("dve", Pn[:, :, :].re("p h c -> p (h c)"), bA.f(0, 512))
                if k < 5:
                    S.copy("act", Qn[:, :, :].re("p h c -> p (h c)"), bB.f(0, 512))
                S.mm(bC.f(0, 512), lhsT=idb[:, :], rhs=g["RTb"][:, :, :].re("p h c -> p (h c)"), start=True, stop=False)
                for h in range(4):
                    S.mm(bC.f(h * 128, (h + 1) * 128), lhsT=Pn[:, h, :], rhs=g["RTb"][:, h, :],
                         start=False, stop=(h == 3))
                S.copy("act", g["RTb"][:, :, :].re("p h c -> p (h c)"), bC.f(0, 512))
                P[s_] = Pn
                Q[s_] = Qn

    def gdn_uw(self, g, s_):
        S = self.S
        bA, bB = self.bankA[s_], self.bankB[s_]
        for h in range(4):
            S.mm(bA.f(h * 128, (h + 1) * 128), lhsT=g["RTb"][:, h, :], rhs=g["vb"][:, h, :])
        for h in range(4):
            S.mm(bB.f(h * 128, (h + 1) * 128), lhsT=g["kw"][:, h, :], rhs=g["RTb"][:, h, :])
        S.copy("act", g["u"][:, :, :].re("p h c -> p (h c)"), bA.f(0, 512))
        S.copy("dve", g["wTb"][:, :, :].re("p h c -> p (h c)"), bB.f(0, 512))

    def gdn_recur(self, g, S32, Sb, sel, blk, oa, hsel, oh):
        S = self.S
        pb = self.pb
        bV, bO, bS_ = pb[2], pb[3], pb[4]
        glS = g["sm"][:, 48:56].re("p (h i) -> p h i", i=2)
        for i in range(2):
            r0, r1 = 64 * i, 64 * i + 64
            for h in range(4):
                S.mm(bV.f(h * 128, (h + 1) * 128, r0, r1), lhsT=g["wTb"][:, h, r0:r1], rhs=Sb[:, h, :])
            S.tt("dve", g["vn"][r0:r1, :, :].re("p h c -> p (h c)"), g["u"][r0:r1, :, :].re("p h c -> p (h c)"),
                 bV.f(0, 512, r0, r1), ALU.subtract)
            for h in range(4):
                S.mm(bO.f(h * 128, (h + 1) * 128, r0, r1), lhsT=g["qdT"][:, h, r0:r1], rhs=Sb[:, h, :],
                     start=True, stop=False)
                S.mm(bO.f(h * 128, (h + 1) * 128, r0, r1), lhsT=g["aT"][r0:r1, h, r0:r1], rhs=g["vn"][r0:r1, h, :],
                     start=False, stop=True)
            for h in range(4):
                S.mm(bS_.f(h * 128, (h + 1) * 128), lhsT=g["kd"][r0:r1, h, :], rhs=g["vn"][r0:r1, h, :])
            glb = V(glS.ap[:, :, i].unsqueeze(2).broadcast_to([128, 4, 128]), glS.bs)
            S.tt("pool", S32[:, :, :], S32[:, :, :], glb, ALU.mult)
            S.tt("dve", S32[:, :, :].re("p h c -> p (h c)"), S32[:, :, :].re("p h c -> p (h c)"), bS_.f(0, 512), ALU.add)
            S.copy("act", Sb[:, :, :], S32[:, :, :])
        r_ = blk % 4
        if r_ == 0:
            S.ts("dve", oa[:, :], bO.f(0, 512), sel[:, 0:1], None, ALU.mult)
        else:
            S.stt(oa[:, :], bO.f(0, 512), sel[:, r_:r_ + 1], oa[:, :], ALU.mult, ALU.add)
        if r_ == 0:
            S.ts("dve", oh[64:128, :], bO.f(0, 512, 64, 128), hsel[64:128, 0:1], None, ALU.mult)
        else:
            S.stt(oh[64:128, :], bO.f(0, 512, 64, 128), hsel[64:128, r_:r_ + 1], oh[64:128, :], ALU.mult, ALU.add)

    def phase2(self, keep):
        S = self.S
        pb = self.pb
        gc_ = S.group("const2", full=True)
        qnT, gates = keep["qnT"], keep["gates"]
        with ExitStack() as es:
            qnw = self.cload(es, "qnw", [128, 1], gc_)
            gonw4 = self.cload(es, "gonw4", [128, 512], gc_)
            onesf = self.cload(es, "onesf2", [128, 128], gc_)
            onesb = self.sb(es, "onesb2", [128, 128], BF16)
            S.copy("dve", onesb[:, :], onesf[:, :])
            winb2 = self.sb(es, "winb2", [128, 8, 1036], BF16)
            with ExitStack() as es2:
                wst = [self.sb(es2, f"w2st{i}", [128, 1036]) for i in range(2)]
                g_w = [S.group(f"w2st{i}") for i in range(2)]
                for k in range(8):
                    sl = k % 2
                    for (a, b_, o) in ((O_GZ, O_GZ + 512, 0), (O_NQ, O_NQ + 512, 512), (O_NG, O_NG + 12, 1024)):
                        S.dma(wst[sl][:, o:o + (b_ - a)], self.w_in[k * 128:(k + 1) * 128, a:b_], g_w[sl])
                    S.copy("dve" if k % 2 else "pool", winb2[:, k, :], wst[sl][:, :])
                S.flush()
            F = dict(ssq=self.sb(es, "f2_ssq", [128, 4]), rt=self.sb(es, "f2_rt", [128, 4]),
                     rstd=self.sb(es, "f2_rstd", [128, 4]), junk=self.sb(es, "f2_junk", [128, 1024], BF16),
                     xs=self.sb(es, "f2_xs", [128, 2, 1024], BF16))
            xt = [self.sb(es, f"x2t{i}", [128, 2, 1024]) for i in range(2)]
            g_x = [S.group(f"x2t{i}") for i in range(2)]
            hT = self.sb(es, "h2T_", [128, 8, 512], BF16)
            yf = [self.sb(es, f"y2f{i}", [128, 512]) for i in range(2)]
            sq = [self.sb(es, f"s2q{i}", [128, 512], BF16) for i in range(2)]
            rtt = [self.sb(es, f"r2tt{i}", [128, 512]) for i in range(2)]
            og = [self.sb(es, f"og{i}", [128, 512]) for i in range(2)]
            g_og = [S.group(f"og{i}") for i in range(2)]
            zs = [self.sb(es, f"zs{i}", [128, 512]) for i in range(2)]
            t1 = [self.sb(es, f"p2t{i}", [128, 512]) for i in range(2)]
            mg = [self.sb(es, f"mg{i}", [128, 512], BF16) for i in range(2)]
            mgT = [self.sb(es, f"mgT{i}", [128, 4, 128], BF16) for i in range(2)]
            g_mg = [S.group(f"mgT{i}") for i in range(2)]
            osq = self.sb(es, "osq", [128, 8])
            xrows = self.xo.rearrange("(n b p) d -> n p b d", b=2, p=128)

            def load_x(n):
                S.dma(xt[n % 2][:, :, :], xrows[n], g_x[n % 2])

            load_x(0)
            for t in range(4):
                for hf in range(2):
                    n = 2 * t + hf
                    if n + 1 < 8:
                        load_x(n + 1)
                    self.front(F, xt[n % 2], 2, hT, hf * 256, self.s1, 0)
                for h in range(4):
                    bank = pb[2 + (h % 2)]
                    for k in range(8):
                        S.mm(bank.f(0, 512), lhsT=winb2[:, k, 512 + h * 128:512 + (h + 1) * 128], rhs=hT[:, k, :],
                             start=(k == 0), stop=(k == 7))
                    y_ = yf[h % 2]
                    S.copy("act", y_[:, :], bank.f(0, 512))
                    S.act(sq[h % 2][:, :], bank.f(0, 512), AF.Square)
                    bk2 = pb[4 + (h % 2)]
                    S.mm(bk2.f(0, 512), lhsT=onesb[:, :], rhs=sq[h % 2][:, :])
                    r_ = rtt[h % 2]
                    S.act(r_[:, :], bk2.f(0, 512), AF.Sqrt, bias=EPS, scale=1.0 / 128.0)
                    S.recip(r_[:, :], r_[:, :])
                    S.stt(qnT[:, h, t * 512:(t + 1) * 512], y_[:, :], qnw[:, 0:1], r_[:, :], ALU.mult, ALU.mult)
                for blk in range(4):
                    j = 4 * t + blk
                    cs = slice(blk * 128, (blk + 1) * 128)
                    bg = pb[6]
                    for k in range(8):
                        S.mm(bg.f(0, 12), lhsT=hT[:, k, cs], rhs=winb2[:, k, 1024:1036], start=(k == 0), stop=(k == 7))
                    S.act(gates[:, j, :], bg.f(0, 12), AF.Sigmoid)
                    bz = pb[7]
                    for k in range(8):
                        S.mm(bz.f(0, 512), lhsT=hT[:, k, cs], rhs=winb2[:, k, 0:512], start=(k == 0), stop=(k == 7))
                    z_ = zs[j % 2]
                    S.act(z_[:, :], bz.f(0, 512), AF.Silu)
                    o_ = og[j % 2]
                    S.dma(o_[:, :], self.o_own[j], g_og[j % 2])
                    for h in range(4):
                        S.act(t1[j % 2][:, h * 128:(h + 1) * 128], o_[:, h * 128:(h + 1) * 128], AF.Square,
                              accum=osq[:, h:h + 1])
                    S.act(osq[:, 4:8], osq[:, 0:4], AF.Sqrt, bias=EPS, scale=1.0 / 128.0)
                    S.recip(osq[:, 4:8], osq[:, 4:8])
                    rb = V(osq.t[:, 4:8].unsqueeze(2).broadcast_to([128, 4, 128]), [osq.b])
                    S.tt("dve", t1[j % 2][:, :].re("p (h d) -> p h d", h=4), o_[:, :].re("p (h d) -> p h d", h=4), rb,
                         ALU.mult)
                    S.tt("pool", t1[j % 2][:, :], t1[j % 2][:, :], gonw4[:, :], ALU.mult)
                    S.tt("dve", mg[j % 2][:, :], t1[j % 2][:, :], z_[:, :], ALU.mult)
                    bt = pb[0]
                    for c in range(4):
                        S.tr(bt.b(c * 128, (c + 1) * 128), mg[j % 2][:, c * 128:(c + 1) * 128], self.idb[:, :])
                    S.copy("act", mgT[j % 2][:, :, :].re("p c t -> p (c t)"), bt.b(0, 512))
                    S.dma(self.mix_d[:, 0:4, j * 128:(j + 1) * 128], mgT[j % 2][:, :, :], g_mg[j % 2])
            gh = S.group("p2halo", full=True)
            selAB = self.cload(es, "selAB", [128, 2], gh)
            xht = self.sb(es, "xht", [32, D])
            S.dma(xht[:, :], self.inp("xh", [32, D])[:, :], gh)
            ca = self.sb(es, "hca", [32, 512])
            cb = self.sb(es, "hcb", [32, 512])
            S.memset("pool", cb[0:2, :], 0.0)
            opv = self.oprev_d.rearrange("j i c -> (j i) c")
            S.dma(ca[:, :], opv[0:32, :], gh)
            S.dma(cb[2:32, :], opv[0:30, :], gh)
            hTh = self.sb(es, "hTh", [128, 8, 32], BF16)
            self.front32(F, xht, hTh, self.s1, 0)
            qnTh, gatesh = keep["qnTh"], keep["gatesh"]
            bq = pb[2]
            for h in range(4):
                for k in range(8):
                    S.mm(bq.f(h * 32, (h + 1) * 32), lhsT=winb2[:, k, 512 + h * 128:512 + (h + 1) * 128], rhs=hTh[:, k, :],
                         start=(k == 0), stop=(k == 7))
            S.copy("act", yf[0][:, 0:128], bq.f(0, 128))
            S.act(sq[0][:, 0:128], bq.f(0, 128), AF.Square)
            S.mm(pb[4].f(0, 128), lhsT=onesb[:, :], rhs=sq[0][:, 0:128])
            S.act(rtt[0][:, 0:128], pb[4].f(0, 128), AF.Sqrt, bias=EPS, scale=1.0 / 128.0)
            S.recip(rtt[0][:, 0:128], rtt[0][:, 0:128])
            S.stt(qnTh[:, :, :].re("p h q -> p (h q)"), yf[0][:, 0:128], qnw[:, 0:1], rtt[0][:, 0:128], ALU.mult, ALU.mult)
            for k in range(8):
                S.mm(pb[6].f(0, 12, 0, 32), lhsT=hTh[:, k, :], rhs=winb2[:, k, 1024:1036], start=(k == 0), stop=(k == 7))
            S.act(gatesh[:, :], pb[6].f(0, 12, 0, 32), AF.Sigmoid)
            for k in range(8):
                S.mm(pb[7].f(0, 512, 0, 32), lhsT=hTh[:, k, :], rhs=winb2[:, k, 0:512], start=(k == 0), stop=(k == 7))
            S.act(zs[0][0:32, :], pb[7].f(0, 512, 0, 32), AF.Silu)
            S.ts("dve", ca[:, :], ca[:, :], selAB[0:32, 0:1], None, ALU.mult)
            S.stt(ca[:, :], cb[:, :], selAB[0:32, 1:2], ca[:, :], ALU.mult, ALU.add)
            for h in range(4):
                S.act(t1[0][0:32, h * 128:(h + 1) * 128], ca[:, h * 128:(h + 1) * 128], AF.Square, accum=osq[0:32, h:h + 1])
            S.act(osq[0:32, 4:8], osq[0:32, 0:4], AF.Sqrt, bias=EPS, scale=1.0 / 128.0)
            S.recip(osq[0:32, 4:8], osq[0:32, 4:8])
            rb = V(osq.t[0:32, 4:8].unsqueeze(2).broadcast_to([32, 4, 128]), [osq.b])
            S.tt("dve", t1[0][0:32, :].re("p (h d) -> p h d", h=4), ca[:, :].re("p (h d) -> p h d", h=4), rb, ALU.mult)
            S.tt("pool", t1[0][0:32, :], t1[0][0:32, :], gonw4[0:32, :], ALU.mult)
            S.tt("dve", mg[0][0:32, :], t1[0][0:32, :], zs[0][0:32, :], ALU.mult)
            for c in range(4):
                S.tr(pb[0].b(c * 32, (c + 1) * 32), mg[0][0:32, c * 128:(c + 1) * 128], self.idb[0:32, 0:32])
            mghT = self.sb(es, "mghT", [128, 4, 32], BF16)
            S.copy("act", mghT[:, :, :].re("p c t -> p (c t)"), pb[0].b(0, 128))
            S.dma(self.mixh_d[:, 0:4, :], mghT[:, :, :], S.group("mghT"))
            S.flush()

    def phase3(self, keep):
        S = self.S
        pb = self.pb
        idb = self.idb
        qnT, gates = keep["qnT"], keep["gates"]
        SC = 128.0 ** -0.5
        with ExitStack() as es:
            gc_ = S.group("const3", full=True)
            kcmpT = self.sb(es, "kcmpT", [128, 512], BF16)
            vcx = self.sb(es, "vcx", [128, 4, 129], BF16)
            onesf = self.cload(es, "onesf3", [128, 128], gc_)
            onesb = self.sb(es, "onesb3", [128, 128], BF16)
            S.copy("dve", onesb[:, :], onesf[:, :])
            with ExitStack() as e2:
                g2 = S.group("const3b", full=True)
                kcw = self.cload(e2, "kcw", [128, 1], g2)
                pm511 = self.cload(e2, "pm511", [128, 1], g2)
                for which, src_d, w1n, w2n, posn in (("k", self.kcT_d, "cmp_k_w1", "cmp_k_w2", "cmp_k_posT"),
                                                    ("v", self.vcT_d, "cmp_v_w1", "cmp_v_w2", "cmp_v_posT")):
                    with ExitStack() as e3:
                        g3 = S.group("c3" + which, full=True)
                        xT = self.sb(e3, "cx" + which, [128, S_LEN], BF16)
                        S.dma(xT[:, 0:4096], src_d[:, 0:4096], g3)
                        S.dma(xT[:, 4096:8192], src_d[:, 4096:8192], g3)
                        w1d = self.inp(w1n, [128, 32, 128])
                        w1f = self.sb(e3, "w1f" + which, [128, 32, 128])
                        S.dma(w1f[:, 0:16, :], w1d[:, 0:16, :], g3)
                        S.dma(w1f[:, 16:32, :], w1d[:, 16:32, :], g3)
                        w1b = self.sb(e3, "w1b" + which, [128, 32, 128], BF16)
                        S.copy("dve", w1b[:, 0:16, :], w1f[:, 0:16, :])
                        S.copy("pool", w1b[:, 16:32, :], w1f[:, 16:32, :])
                        w2f = self.cload(e3, w2n, [128, 128], g3)
                        w2b = self.sb(e3, "w2b" + which, [128, 128], BF16)
                        S.copy("dve", w2b[:, :], w2f[:, :])
                        posf = self.cload(e3, posn, [128, 32], g3)
                        posb = self.sb(e3, "posb" + which, [128, 32], BF16)
                        S.copy("dve", posb[:, :], posf[:, :])
                        hid = self.sb(e3, "hid" + which, [128, 512], BF16)
                        bcol = self.sb(e3, "bcol" + which, [128, 1])
                        S.memset("pool", hid[:, :], 0.0)
                        bh, bb = pb[2], pb[3]
                        x3 = xT[:, :].re("p (n s) -> p n s", s=16)
                        for l in range(32):
                            S.mm(bh.f(0, 511), lhsT=w1b[:, l, :], rhs=x3[:, l // 16:l // 16 + 511, l % 16],
                                 start=(l == 0), stop=(l == 31))
                        for l in range(32):
                            S.mm(bb.f(0, 1), lhsT=w1b[:, l, :], rhs=posb[:, l:l + 1], start=(l == 0), stop=(l == 31))
                        S.copy("dve", bcol[:, :], bb.f(0, 1))
                        S.act(hid[:, 0:511], bh.f(0, 511), AF.Silu, bias=bcol[:, 0:1])
                        if which == "k":
                            bk = pb[4]
                            S.mm(bk.f(0, 512), lhsT=w2b[:, :], rhs=hid[:, :])
                            yk = self.sb(e3, "yk", [128, 512])
                            sqk = self.sb(e3, "sqk", [128, 512], BF16)
                            rk = self.sb(e3, "rk", [128, 512])
                            S.copy("act", yk[:, :], bk.f(0, 512))
                            S.act(sqk[:, :], bk.f(0, 512), AF.Square)
                            S.mm(pb[5].f(0, 512), lhsT=onesb[:, :], rhs=sqk[:, :])
                            S.act(rk[:, :], pb[5].f(0, 512), AF.Sqrt, bias=EPS, scale=1.0 / 128.0)
                            S.recip(rk[:, :], rk[:, :])
                            S.stt(kcmpT[:, :], yk[:, :], kcw[:, 0:1], rk[:, :], ALU.mult, ALU.mult)
                            S.memset("dve", kcmpT[:, 511:512], 0.0)
                        else:
                            bv = pb[4]
                            for nt in range(4):
                                S.mm(bv.f(nt * 128, (nt + 1) * 128), lhsT=hid[:, nt * 128:(nt + 1) * 128], rhs=w2b[:, :])
                            S.memset("pool", vcx[:, :, 128:129], 1.0)
                            S.copy("act", vcx[:, :, 0:128], bv.f(0, 512).re("p (n d) -> p n d", n=4))
                            S.ts("dve", vcx[:, 3, :], vcx[:, 3, :], pm511[:, 0:1], None, ALU.mult)
                        S.flush()
            if self.debug:
                S.dma(self.outp("d_kcmpT", [128, 512], BF16)[:, :], kcmpT[:, :], self.g_out)
                S.dma(self.outp("d_vcx", [128, 4, 129], BF16)[:, :, :], vcx[:, :, :], self.g_out)
            if self.stop == 31:
                S.flush()
                return
            kslT = self.sb(es, "kslT", [128, S_LEN], BF16)
            kwnT = self.sb(es, "kwnT", [128, S_LEN], BF16)
            vslx = self.sb(es, "vslx", [128, NB, 129], BF16)
            vwnx = self.sb(es, "vwnx", [128, NB, 129], BF16)
            for i in range(2):
                S.dma(kslT[:, i * 4096:(i + 1) * 4096], self.kslT_d[:, i * 4096:(i + 1) * 4096], gc_)
                S.dma(kwnT[:, i * 4096:(i + 1) * 4096], self.kwnT_d[:, i * 4096:(i + 1) * 4096], gc_)
                S.dma(vslx[:, i * 32:(i + 1) * 32, 0:128], self.vsl_d[:, i * 32:(i + 1) * 32, :], gc_)
                S.dma(vwnx[:, i * 32:(i + 1) * 32, 0:128], self.vwn_d[:, i * 32:(i + 1) * 32, :], gc_)
            S.memset("pool", vslx[:, :, 128:129], 1.0)
            S.memset("pool", vwnx[:, :, 128:129], 1.0)
            eall = self.sb(es, "eallb", [128, S_LEN], BF16)
            addm = self.cload(es, "addm", [128, 16, 128], gc_)
            ovb = self.sb(es, "ovb", [128, 4, 128], BF16)
            cmbb = self.sb(es, "cmbb", [128, 16, 2, 128], BF16)
            cbsb = self.sb(es, "cbsb", [128, 4, 4, 128], BF16)
            cbwb = self.sb(es, "cbwb", [128, 8, 4, 128], BF16)
            with ExitStack() as e2:
                g2 = S.group("const3c", full=True)
                ead = self.inp("eall", [128, S_LEN])
                est = [self.sb(e2, f"est{i}", [128, 2048]) for i in range(2)]
                g_e = [S.group(f"est{i}") for i in range(2)]
                for i in range(4):
                    S.dma(est[i % 2][:, :], ead[:, i * 2048:(i + 1) * 2048], g_e[i % 2])
                    S.copy("pool" if i % 2 else "dve", eall[:, i * 2048:(i + 1) * 2048], est[i % 2][:, :])
                ovf = self.cload(e2, "ovm", [128, 4, 128], g2)
                S.copy("dve", ovb[:, :, :], ovf[:, :, :])
                cmf = self.cload(e2, "cmb", [128, 16, 2, 128], g2)
                S.copy("pool", cmbb[:, :, :, :], cmf[:, :, :, :])
                cbsf = self.cload(e2, "cbs", [128, 4, 128], g2)
                cbwf = self.cload(e2, "cbw", [128, 8, 128], g2)
                for h in range(4):
                    S.copy("dve", cbsb[:, :, h, :], cbsf[:, :, :])
                    S.copy("pool", cbwb[:, :, h, :], cbwf[:, :, :])
                S.flush()
            Pc = [self.sb(es, f"Pc{i}", [128, 512], BF16) for i in range(4)]
            Pk = [self.sb(es, f"Pk{i}", [128, 512], BF16) for i in range(3)]
            cmrep = [self.sb(es, f"cmrep{i}", [128, 4, 128], BF16) for i in range(2)]
            ob = {nm: self.sb(es, "ob_" + nm, [128, 4, 129]) for nm in ("c", "s", "w")}
            rs = self.sb(es, "rs", [128, 12])
            cf = self.sb(es, "cf", [128, 12])
            imp = self.sb(es, "imp", [128, 128])
            imp2 = self.sb(es, "imp2", [128, 128])
            m8 = self.sb(es, "m8", [128, 16])
            selm = self.sb(es, "selm", [128, 128])
            biasT = self.sb(es, "biasT", [128, 4, 128], BF16)
            mixn = [self.sb(es, f"mixn{i}", [128, 4, 128], BF16) for i in range(2)]
            tmpn = self.sb(es, "tmpn", [128, 128])
            mnT = [self.sb(es, f"mnT{i}", [128, 4, 128], BF16) for i in range(2)]
            g_mn = [S.group(f"mnT{i}") for i in range(2)]
            bS = [pb[0], pb[1]]
            bO = [pb[2], pb[3]]
            bI, bT = pb[4], pb[5]
            si = [0]

            def nsa_block(Q, qv, cmp_plan, slc_plan, win_plan, addv, gatev, store):
                NQ = 4 * Q

                def scores(kT_tile, extra):
                    bank = bS[si[0] % 2]
                    out_ = Pk[si[0] % 3]
                    si[0] += 1
                    S.mm(bank.f(0, NQ), lhsT=kT_tile, rhs=qv, start=True, stop=(len(extra) == 0))
                    for i_, (l_, r_) in enumerate(extra):
                        S.mm(bank.f(0, NQ), lhsT=l_, rhs=r_, start=False, stop=(i_ == len(extra) - 1))
                    S.act(out_[:, 0:NQ], bank.f(0, NQ), AF.Exp, scale=SC)
                    return out_

                def pv(P_, vx, first, last):
                    for h in range(4):
                        S.mm_(bO[h // 2].f((h % 2) * 129, (h % 2) * 129 + 129, 0, Q), lhsT=P_[:, h * Q:(h + 1) * Q],
                              rhs=vx, start=(first and h % 2 == 0), stop=last, skip=True)

                def evac_o(dst):
                    for hh in range(2):
                        S.copy("act", dst[0:Q, 2 * hh:2 * hh + 2, :], bO[hh].f(0, 258, 0, Q).re("p (h c) -> p h c", h=2))

                ncp = len(cmp_plan)
                for i_, (nt, mk) in enumerate(cmp_plan):
                    bank = bS[i_ % 2]
                    S.mm(bank.f(0, NQ), lhsT=kcmpT[:, nt * 128:(nt + 1) * 128], rhs=qv, start=True, stop=(mk is None))
                    if mk is not None:
                        S.mm(bank.f(0, NQ), lhsT=idb[:, :], rhs=mk, start=False, stop=True)
                    S.act(Pc[i_][:, 0:NQ], bank.f(0, NQ), AF.Exp, scale=SC)
                for h in range(4):
                    for i_, (nt, mk) in enumerate(cmp_plan):
                        S.mm_(bO[h // 2].f((h % 2) * 129, (h % 2) * 129 + 129, 0, Q), lhsT=Pc[i_][:, h * Q:(h + 1) * Q],
                              rhs=vcx[:, nt, :], start=(i_ == 0 and h % 2 == 0), stop=(i_ == ncp - 1), skip=True)
                for h in range(4):
                    for i_, (nt, mk) in enumerate(cmp_plan):
                        S.mm_(bI.f(h * 128, (h + 1) * 128, 0, Q), lhsT=Pc[i_][:, h * Q:(h + 1) * Q], rhs=ovb[:, nt, :],
                              start=(i_ == 0 and h == 0), stop=(i_ == ncp - 1), skip=True)
                evac_o(ob["c"])
                S.ts("dve", rs[0:Q, 0:4], ob["c"][0:Q, :, 128], 1e-30, None, ALU.max)
                S.recip(rs[0:Q, 0:4], rs[0:Q, 0:4])
                S.ts("dve", imp[0:Q, :], bI.f(0, 128, 0, Q), rs[0:Q, 0:1], None, ALU.mult)
                for h in range(1, 4):
                    S.stt(imp[0:Q, :], bI.f(h * 128, (h + 1) * 128, 0, Q), rs[0:Q, h:h + 1], imp[0:Q, :], ALU.mult, ALU.add)
                S.tt("pool", imp[0:Q, :], imp[0:Q, :], addv, ALU.add)
                self.max8(m8[0:Q, 0:8], imp[0:Q, :])
                self.match_replace(imp2[0:Q, :], m8[0:Q, 0:8], imp[0:Q, :], -3.0e38)
                self.max8(m8[0:Q, 8:16], imp2[0:Q, :])
                S.ts("dve", selm[0:Q, :], imp[0:Q, :], m8[0:Q, 15:16], None, ALU.is_ge)
                S.mm(bT.f(0, Q), lhsT=selm[0:Q, :], rhs=self.idf[0:Q, 0:Q])
                S.ts("dve", bT2[:, 0:NQ].re("p (h q) -> p h q", h=4),
                     V(bT.t[:, 0:Q].unsqueeze(1).broadcast_to([128, 4, Q]), bT.q[0:1]), BIG, -BIG, ALU.mult, ALU.add)
                brhs = bT2[:, 0:NQ]
                prev = None
                for i_, (kt, extra) in enumerate(slc_plan):
                    ex = [(eall[:, kt * 128:(kt + 1) * 128], brhs)] + extra
                    P_ = scores(kslT[:, kt * 128:(kt + 1) * 128], ex)
                    if prev is not None:
                        pv(prev[0], vslx[:, prev[1], :], prev[2] == 0, False)
                    prev = (P_, kt, i_)
                pv(prev[0], vslx[:, prev[1], :], prev[2] == 0, True)
                evac_o(ob["s"])
                prev = None
                for i_, (kt, mk) in enumerate(win_plan):
                    P_ = scores(kwnT[:, kt * 128:(kt + 1) * 128], [(idb[:, :], mk)])
                    if prev is not None:
                        pv(prev[0], vwnx[:, prev[1], :], prev[2] == 0, False)
                    prev = (P_, kt, i_)
                pv(prev[0], vwnx[:, prev[1], :], prev[2] == 0, True)
                evac_o(ob["w"])
                S.ts("dve", rs[0:Q, 4:8], ob["s"][0:Q, :, 128], 1e-30, None, ALU.max)
                S.ts("dve", rs[0:Q, 8:12], ob["w"][0:Q, :, 128], 1e-30, None, ALU.max)
                S.recip(rs[0:Q, 4:12], rs[0:Q, 4:12])
                g3 = gatev.re("p (h g) -> p g h", g=3)
                S.tt("dve", cf[0:Q, :].re("p (g h) -> p g h", g=3), rs[0:Q, :].re("p (g h) -> p g h", g=3), g3, ALU.mult)
                mx = mixn[si[0] % 2]
                for h in range(4):
                    S.ts("pool", tmpn[0:Q, :], ob["c"][0:Q, h, 0:128], cf[0:Q, h:h + 1], 1.0, ALU.mult, ALU.mult)
                    S.stt(tmpn[0:Q, :], ob["s"][0:Q, h, 0:128], cf[0:Q, 4 + h:5 + h], tmpn[0:Q, :], ALU.mult, ALU.add)
                    S.stt(mx[0:Q, h, :], ob["w"][0:Q, h, 0:128], cf[0:Q, 8 + h:9 + h], tmpn[0:Q, :], ALU.mult, ALU.add)
                for h in range(4):
                    S.tr(bT.b(h * Q, (h + 1) * Q), mx[0:Q, h, :], idb[0:Q, 0:Q])
                store(bT.b(0, 4 * Q))

            bT2 = self.sb(es, "bT2", [128, 512], BF16)
            for j in range(16):
                qv = qnT[:, :, j * 128:(j + 1) * 128]
                nt_hi = min(3, (32 * j + 30) // 128)
                lo = max(0, 32 * j - 1) // 128
                cmp_plan = []
                for nt in range(nt_hi + 1):
                    mk = None
                    if nt >= lo:
                        slot = nt - lo
                        S.copy("pool", cmrep[slot][:, :, :],
                               V(cmbb.t[:, j, slot, :].unsqueeze(1).broadcast_to([128, 4, 128]), [cmbb.b]))
                        mk = cmrep[slot][:, :, :].re("p h q -> p (h q)")
                    cmp_plan.append((nt, mk))
                slc_plan = []
                for kt in range(4 * j + 4):
                    ex = []
                    if kt >= 4 * j:
                        ex.append((idb[:, :], cbsb[:, kt - 4 * j, :, :].re("p h q -> p (h q)")))
                    slc_plan.append((kt, ex))
                win_plan = [(4 * j - 4 + e, cbwb[:, e, :, :].re("p h q -> p (h q)")) for e in range(8) if 4 * j - 4 + e >= 0]

                def store(src, j=j):
                    S.copy("act", mnT[j % 2][:, :, :].re("p c t -> p (c t)"), src)
                    S.dma(self.mix_d[:, 4:8, j * 128:(j + 1) * 128], mnT[j % 2][:, :, :], g_mn[j % 2])

                nsa_block(128, qv, cmp_plan, slc_plan, win_plan, addm[:, j, :], gates[:, j, :], store)
            self.nsa_halo(es, nsa_block, keep, idb)
            S.flush()

    def nsa_halo(self, es, nsa_block, keep, idb):
        S = self.S
        with ExitStack() as e2:
            g2 = S.group("const3h", full=True)
            haddm = self.cload(e2, "haddm", [32, 128], g2)
            hcmb = self.sb(e2, "hcmb", [128, 4, 128], BF16)
            hsmb = self.sb(e2, "hsmb", [128, 64, 128], BF16)
            hwmb = self.sb(e2, "hwmb", [128, 64, 128], BF16)
            hcf = self.cload(e2, "hcm", [128, 4, 128], g2)
            S.copy("dve", hcmb[:, :, :], hcf[:, :, :])
            st = [self.sb(e2, f"hst{i}", [128, 16, 128]) for i in range(2)]
            g_s = [S.group(f"hst{i}") for i in range(2)]
            n = 0
            for nm, dst in (("hsm", hsmb), ("hwm", hwmb)):
                src = self.inp(nm, [128, 64, 128])
                for i in range(4):
                    S.dma(st[n % 2][:, :, :], src[:, i * 16:(i + 1) * 16, :], g_s[n % 2])
                    S.copy("pool" if n % 2 else "dve", dst[:, i * 16:(i + 1) * 16, :], st[n % 2][:, :, :])
                    n += 1
            mnTh = self.sb(e2, "mnTh", [128, 4, 32], BF16)
            g_m = S.group("mnTh")
            cmp_plan = [(nt, hcmb[:, nt, :]) for nt in range(4)]
            slc_plan = [(kt, [(idb[:, :], hsmb[:, kt, :])]) for kt in range(NB)]
            win_plan = [(kt, hwmb[:, kt, :]) for kt in range(NB)]

            def store(src):
                S.copy("act", mnTh[:, :, :].re("p c t -> p (c t)"), src)
                S.dma(self.mixh_d[:, 4:8, :], mnTh[:, :, :], g_m)

            nsa_block(32, keep["qnTh"][:, :, :], cmp_plan, slc_plan, win_plan, haddm[:, :], keep["gatesh"][:, :], store)
            S.flush()

    def max8(self, out, in_):
        o_, i_ = out.ap, in_.ap
        self.S.op("dve", lambda e: e.max(out=o_, in_=i_), r=self.S._bs(in_), w=self.S._bs(out))

    def match_replace(self, out, rep, vals, imm):
        o_, r_, v_ = out.ap, rep.ap, vals.ap
        self.S.op("dve", lambda e: e.match_replace(out=o_, in_to_replace=r_, in_values=v_, imm_value=imm),
                  r=self.S._bs(rep, vals), w=self.S._bs(out))

    def phase45(self):
        S = self.S
        pb = self.pb
        idb = self.idb
        w_out = self.inp("w_out", [D, D])
        w_up = self.inp("w_up", [D, 2 * DFF])
        w_dn = self.inp("w_dn", [DFF, D])
        wup_d = self.scratch("wup_d", [128, 44, 8, 128], BF16)
        with ExitStack() as e1:
            full = self.sb(e1, "wupfull", [128, 44, 8, 128], BF16)
            stg = [self.sb(e1, f"wupst{i}", [128, 2 * DFF]) for i in range(2)]
            g_s = [S.group(f"wupst{i}") for i in range(2)]
            g_f = S.group("wupfull")
            for k in range(8):
                st = stg[k % 2]
                for i in range(2):
                    S.dma(st[:, i * DFF:(i + 1) * DFF], w_up[k * 128:(k + 1) * 128, i * DFF:(i + 1) * DFF], g_s[k % 2])
                s3 = st[:, :].re("p (c n) -> p c n", n=128)
                S.copy("dve", full[:, 0:16, k, :], s3[:, 0:16, :])
                S.copy("pool", full[:, 16:30, k, :], s3[:, 16:30, :])
                S.copy("act", full[:, 30:44, k, :], s3[:, 30:44, :])
            for i in range(4):
                S.dma(wup_d[:, i * 11:(i + 1) * 11, :, :], full[:, i * 11:(i + 1) * 11, :, :], g_f)
            S.flush()
        with ExitStack() as es:
            gc_ = S.group("const4", full=True)
            fcw = self.cload(es, "fcw", [128, 44 * 3], gc_)
            fcb = self.cload(es, "fcb", [128, 44], gc_)
            wdnb = self.sb(es, "wdnb", [128, 22, D], BF16)
            woutb = self.sb(es, "woutb", [128, 8, D], BF16)
            g1row = self.sb(es, "g1row", [128, D])
            g2row = self.sb(es, "g2row", [128, D])
            with ExitStack() as e2:
                stg = [self.sb(e2, f"wst4_{i}", [128, 2, D]) for i in range(2)]
                g_s = [S.group(f"wst4_{i}") for i in range(2)]
                n = 0
                for (src, dst, nch) in ((w_dn, wdnb, 22), (w_out, woutb, 8)):
                    sv = src.rearrange("(c p) d -> p c d", p=128)
                    for c0 in range(0, nch, 2):
                        st = stg[n % 2]
                        S.dma(st[:, :, :], sv[:, c0:c0 + 2, :], g_s[n % 2])
                        S.copy(("dve", "pool", "act")[n % 3], dst[:, c0:c0 + 2, :], st[:, :, :])
                        n += 1
                gb = self.sb(e2, "gbc", [128, 128])
                for gi, (col0, dst) in enumerate(((16, g1row), (40, g2row))):
                    for cc in range(8):
                        S.copy("dve", gb[:, :], V(self.modT.t[:, col0 + cc:col0 + cc + 1].broadcast_to([128, 128]),
                                                  [self.modT.b]))
                        bank = pb[cc % 2]
                        S.mm(bank.f(0, 128), lhsT=gb[:, :], rhs=self.idf[:, :])
                        S.copy("act", dst[:, cc * 128:(cc + 1) * 128], bank.f(0, 128))
                S.flush()
            F = dict(ssq=self.sb(es, "f4_ssq", [128, 4]), rt=self.sb(es, "f4_rt", [128, 4]),
                     rstd=self.sb(es, "f4_rstd", [128, 4]), junk=self.sb(es, "f4_junk", [128, 1024], BF16),
                     xs=self.sb(es, "f4_xs", [128, 2, 1024], BF16))
            x1 = [self.sb(es, f"x1_{i}", [128, 4, D]) for i in range(2)]
            g_x = [S.group(f"x1_{i}") for i in range(2)]
            mixt = self.sb(es, "mixt", [128, 8, 512], BF16)
            g_m = S.group("mixt")
            h2T = self.sb(es, "h2T", [128, 8, 512], BF16)
            gT = self.sb(es, "gT", [128, 22, 512], BF16)
            wch = [self.sb(es, f"wch{i}", [128, 2, 8, 128], BF16) for i in range(3)]
            g_wc = [S.group(f"wch{i}") for i in range(3)]
            upa = [self.sb(es, f"upa{i}", [128, 4, 130]) for i in range(2)]
            upb = [self.sb(es, f"upb{i}", [128, 4, 130]) for i in range(2)]
            for u_ in upa + upb:
                S.memset("pool", u_[:, :, :], 0.0)
            aa = [self.sb(es, f"aa{i}", [128, 4, 128]) for i in range(2)]
            ab_ = [self.sb(es, f"ab_{i}", [128, 4, 128]) for i in range(2)]
            tmp = [self.sb(es, f"tmp4_{i}", [128, 512]) for i in range(2)]
            xrows = self.xo.rearrange("(t b p) d -> t p b d", b=4, p=128)
            orows = self.out.rearrange("(t b p) d -> t p b d", b=4, p=128)
            wi = 0
            gh = S.group("p4halo", full=True)
            hexr = self.cload(es, "hexr", [128, 32], gh)
            x1h = self.sb(es, "x1h", [32, D])
            mixh = self.sb(es, "mixh", [128, 8, 32], BF16)
            S.dma(x1h[:, :], self.din["xh"][:, :], gh)
            S.dma(mixh[:, :, :], self.mixh_d[:, :, :], gh)
            h2Th = self.sb(es, "h2Th", [128, 8, 32], BF16)
            for half in range(2):
                bank = pb[2 + half]
                hs = slice(half * 512, (half + 1) * 512)
                for c in range(8):
                    S.mm(bank.f(0, 512, 0, 32), lhsT=mixh[:, c, :], rhs=woutb[:, c, hs], start=(c == 0), stop=(c == 7))
                S.tt("dve", tmp[half][0:32, :], bank.f(0, 512, 0, 32), g1row[0:32, hs], ALU.mult)
                S.tt("pool", x1h[:, hs], x1h[:, hs], tmp[half][0:32, :], ALU.add)
            self.front32(F, x1h, h2Th, self.s2, 24)

            def conv(u_, c, dst):
                S.ts("pool", dst[:, :, :], u_[:, :, 2:130], fcw[:, c * 3 + 2:c * 3 + 3], fcb[:, c:c + 1], ALU.mult, ALU.add)
                S.stt(dst[:, :, :], u_[:, :, 1:129], fcw[:, c * 3 + 1:c * 3 + 2], dst[:, :, :], ALU.mult, ALU.add)
                S.stt(dst[:, :, :], u_[:, :, 0:128], fcw[:, c * 3:c * 3 + 1], dst[:, :, :], ALU.mult, ALU.add)

            for t in range(4):
                xx = x1[t % 2]
                S.dma(xx[:, :, :], xrows[t], g_x[t % 2])
                S.dma(mixt[:, :, :], self.mix_d[:, :, t * 512:(t + 1) * 512], g_m)
                for blk in range(4):
                    cs = slice(blk * 128, (blk + 1) * 128)
                    for half in range(2):
                        bank = pb[2 + half]
                        hs = slice(half * 512, (half + 1) * 512)
                        for c in range(8):
                            S.mm(bank.f(0, 512), lhsT=mixt[:, c, cs], rhs=woutb[:, c, hs], start=(c == 0), stop=(c == 7))
                        tm = tmp[half]
                        S.tt("dve", tm[:, :], bank.f(0, 512), g1row[:, hs], ALU.mult)
                        S.tt("pool", xx[:, blk, hs], xx[:, blk, hs], tm[:, :], ALU.add)
                if self.debug:
                    S.dma(self.dx1[t], xx[:, :, :], self.g_out)
                for hf in range(2):
                    self.front(F, xx, 2, h2T, hf * 256, self.s2, 24, b0=2 * hf)
                for c in range(22):
                    w_ = wch[wi % 3]
                    S.dma(w_[:, 0, :, :], wup_d[:, c, :, :], g_wc[wi % 3])
                    S.dma(w_[:, 1, :, :], wup_d[:, 22 + c, :, :], g_wc[wi % 3])
                    wi += 1
                    ua, ub = upa[c % 2], upb[c % 2]
                    for i_, (u_, bank) in enumerate(((ua, pb[4]), (ub, pb[5]))):
                        for k in range(8):
                            S.mm(bank.f(0, 512), lhsT=w_[:, i_, k, :], rhs=h2T[:, k, :], start=(k == 0), stop=(k == 7))
                        S.copy("act", u_[:, :, 2:130], bank.f(0, 512).re("p (b t) -> p b t", b=4))
                        bh_ = pb[6 + i_]
                        for k in range(8):
                            S.mm(bh_.f(0, 8), lhsT=w_[:, i_, k, :], rhs=h2Th[:, k, 8 * t:8 * t + 8], start=(k == 0),
                                 stop=(k == 7))
                        S.tt("dve", u_[:, :, 0:2], bh_.f(0, 8).re("p (b i) -> p b i", i=2),
                             hexr[:, 8 * t:8 * t + 8].re("p (b i) -> p b i", i=2), ALU.mult)
                    conv(ua, c, aa[c % 2])
                    conv(ub, 22 + c, ab_[c % 2])
                    S.act(aa[c % 2][:, :, :], aa[c % 2][:, :, :], AF.Silu)
                    S.tt("dve", gT[:, c, :].re("p (b t) -> p b t", b=4), aa[c % 2][:, :, :], ab_[c % 2][:, :, :], ALU.mult)
                for blk in range(4):
                    cs = slice(blk * 128, (blk + 1) * 128)
                    for half in range(2):
                        bank = pb[6 + half]
                        hs = slice(half * 512, (half + 1) * 512)
                        for c in range(22):
                            S.mm(bank.f(0, 512), lhsT=gT[:, c, cs], rhs=wdnb[:, c, hs], start=(c == 0), stop=(c == 21))
                        tm = tmp[half]
                        S.tt("dve", tm[:, :], bank.f(0, 512), g2row[:, hs], ALU.mult)
                        S.tt("pool", xx[:, blk, hs], xx[:, blk, hs], tm[:, :], ALU.add)
                S.dma(orows[t], xx[:, :, :], g_x[t % 2])
            S.flush()


def _colL(v, n):
    return np.ascontiguousarray(np.asarray(v, np.float32).reshape(n, 128).T)


def _rep(v, n=128):
    v = np.asarray(v, np.float32).reshape(1, -1)
    return np.ascontiguousarray(np.repeat(v, n, axis=0))


def _consts():
    p = np.arange(128)
    same = (p[:, None] // 64) == (p[None, :] // 64)
    c = {}
    c["identf"] = np.eye(128, dtype=np.float32)
    c["tri2"] = (same & (p[:, None] <= p[None, :])).astype(np.float32)
    c["blk2"] = same.astype(np.float32)
    c["onesf"] = np.ones((128, 128), np.float32)
    c["cind"] = np.stack([(p < 64), (p >= 64)], axis=1).astype(np.float32)
    ma = np.where(same & (p[None, :] < p[:, None]), 0.0, BIG).astype(np.float32)
    mb = np.where(same & (p[None, :] >= p[:, None]), 0.0, -BIG).astype(np.float32)
    c["ma4"] = np.ascontiguousarray(np.tile(ma, (1, 4)))
    c["mb4"] = np.ascontiguousarray(np.tile(mb, (1, 4)))
    return c


def _host_inputs(inputs):
    x = np.asarray(inputs["x"], np.float32)
    cst = _consts()
    g = lambda k: np.asarray(inputs[k][0], np.float32)
    gcw = g("gdn_conv_w")
    gcwT = np.ascontiguousarray(gcw.reshape(4, 12, 128).transpose(2, 1, 0).reshape(128, 48))
    shared = {
        "ada_w": np.ascontiguousarray(g("ada_w")),
        "ada_bT": _colL(g("ada_b"), 48),
        "n1w": _colL(g("norm1_w"), 8),
        "n2w": _colL(g("norm2_w"), 8),
        "w_in": np.ascontiguousarray(g("w_in")),
        "gcw": gcwT,
        "dtb": _rep(g("gdn_dt_bias")),
        "alog": _rep(g("gdn_A_log")),
        "kslw": _colL(g("nsa_k_norm_slc"), 1),
        "kwnw": _colL(g("nsa_k_norm_win"), 1),
        "qnw": _colL(g("nsa_q_norm_w"), 1),
        "gonw4": _rep(np.tile(g("gdn_out_norm_w"), 4)),
        "onesf2": np.ones((128, 128), np.float32),
    }
    for nm in ("k", "v"):
        shared[f"cmp_{nm}_w1"] = np.ascontiguousarray(g(f"cmp_{nm}_w1").reshape(32, 128, 128).transpose(1, 0, 2))
        shared[f"cmp_{nm}_w2"] = np.ascontiguousarray(g(f"cmp_{nm}_w2"))
        shared[f"cmp_{nm}_posT"] = np.ascontiguousarray(g(f"cmp_{nm}_pos").T)
    shared["kcw"] = _colL(g("nsa_k_norm_cmp"), 1)
    shared["onesf3"] = np.ones((128, 128), np.float32)
    pm = np.ones((128, 1), np.float32)
    pm[127, 0] = 0.0
    shared["pm511"] = pm
    keys = np.arange(S_LEN)
    shared["eall"] = (keys[None, :] // 64 == np.arange(128)[:, None]).astype(np.float32)
    n = np.arange(512)
    js = np.arange(128)
    ov = np.minimum(16 * n[:, None] + 32, 64 * js[None, :] + 64) - np.maximum(16 * n[:, None], 64 * js[None, :])
    ov = np.clip(ov, 0, None).astype(np.float32) / 32.0
    ov[511] = 0.0
    shared["ovm"] = np.ascontiguousarray(ov.reshape(4, 128, 128).transpose(1, 0, 2))
    fw = g("ffn_conv_w")
    shared["fcw"] = np.ascontiguousarray(fw.reshape(3, 44, 128).transpose(2, 1, 0).reshape(128, 132))
    shared["fcb"] = _colL(g("ffn_conv_b"), 44)
    shared["w_out"] = np.ascontiguousarray(g("w_out"))
    shared["w_up"] = np.ascontiguousarray(g("ffn_w_up"))
    shared["w_dn"] = np.ascontiguousarray(g("ffn_w_down"))
    shared.update(cst)
    maps = []
    for core in range(8):
        b, r = core // 4, core % 4
        xo = np.concatenate([x[b, 128 * (4 * j + r):128 * (4 * j + r) + 128] for j in range(16)], axis=0)
        m = dict(shared)
        m["xb"] = np.ascontiguousarray(x[b])
        m["xo"] = np.ascontiguousarray(xo)
        m["cT"] = _colL(inputs["c"][b], 8)
        sel = np.zeros((128, 4), np.float32)
        sel[:, r] = 1.0
        m["sel"] = sel
        p = np.arange(128)
        q = np.arange(128)
        addm = np.zeros((128, 16, 128), np.float32)
        cmb = np.zeros((128, 16, 2, 128), np.float32)
        for j in range(16):
            qi = 4 * j + r
            tq = 128 * qi + q
            cur = tq // 64
            jj = np.arange(128)[None, :]
            valid = jj <= cur[:, None]
            forced = (jj == 0) | (jj == cur[:, None]) | (jj == cur[:, None] - 1)
            addm[:, j, :] = np.where(valid, np.where(forced, 1.0e4, 0.0), -1.0e30)
            lo = max(0, 32 * j - 1) // 128
            for slot in range(2):
                nn = 128 * (lo + slot) + p
                ok = (16 * nn[:, None] + 31) <= tq[None, :]
                cmb[:, j, slot, :] = np.where(ok, 0.0, -BIG)
        m["addm"] = addm
        m["cmb"] = cmb
        cbs = np.zeros((128, 4, 128), np.float32)
        for d in range(4):
            ok = (128 * (d - r) + p[:, None]) <= q[None, :]
            cbs[:, d, :] = np.where(ok, 0.0, -BIG)
        m["cbs"] = cbs
        cbw = np.zeros((128, 8, 128), np.float32)
        for e in range(8):
            rel = 128 * (e - 4 - r) + p[:, None]
            ok = (rel <= q[None, :]) & (rel > q[None, :] - 512)
            cbw[:, e, :] = np.where(ok, 0.0, -BIG)
        m["cbw"] = cbw
        hs_ = np.zeros((128, 4), np.float32)
        hs_[:, (r - 1) % 4] = 1.0
        m["hsel"] = hs_
        sab = np.zeros((128, 2), np.float32)
        sab[:, 0] = 1.0 if r >= 1 else 0.0
        sab[:, 1] = 1.0 if r == 0 else 0.0
        m["selAB"] = sab
        tq = np.array([128 * (4 * j + r) - 2 + i for j in range(16) for i in range(2)])
        ex = tq >= 0
        xh = np.zeros((32, D), np.float32)
        xh[ex] = x[b, tq[ex]]
        m["xh"] = xh
        m["hexr"] = _rep(ex.astype(np.float32))
        cur = tq // 64
        jj = np.arange(128)[None, :]
        valid = (jj <= cur[:, None]) & ex[:, None]
        forced = (jj == 0) | (jj == cur[:, None]) | (jj == cur[:, None] - 1)
        m["haddm"] = np.where(valid, np.where(forced, 1.0e4, 0.0), -1.0e30).astype(np.float32)
        nn = np.arange(512)
        okc = ((16 * nn[:, None] + 31) <= tq[None, :]) & ex[None, :]
        hcm = np.where(okc, 0.0, -BIG).astype(np.float32).reshape(4, 128, 1, 32)
        m["hcm"] = np.ascontiguousarray(np.broadcast_to(hcm, (4, 128, 4, 32)).transpose(1, 0, 2, 3).reshape(128, 4, 128))
        pos = np.arange(S_LEN)
        oks = (pos[:, None] <= tq[None, :]) & ex[None, :]
        okw = oks & (pos[:, None] > tq[None, :] - 512)
        for nm, ok in (("hsm", oks), ("hwm", okw)):
            a = np.where(ok, 0.0, -BIG).astype(np.float32).reshape(64, 128, 1, 32)
            m[nm] = np.ascontiguousarray(np.broadcast_to(a, (64, 128, 4, 32)).transpose(1, 0, 2, 3).reshape(128, 64, 128))
        maps.append(m)
    return maps


def run(inputs, debug=False, upto=9, ntiles=NT):
    bld = Builder(debug, ntiles)
    nc = bld.build(upto)
    maps = _host_inputs(inputs)
    maps = [{k: v for k, v in m.items() if k in bld.din} for m in maps]
    missing = [k for k in bld.din if k not in maps[0]]
    assert not missing, missing
    res = run_bass_kernel_spmd(nc, maps, core_ids=list(range(8)))
    return res.results


def kernel(**inputs):
    results = run(inputs)
    outp = np.zeros((2, S_LEN, D), np.float32)
    for core in range(8):
        b, r = core // 4, core % 4
        o = results[core]["out"]
        for j in range(16):
            qi = 4 * j + r
            outp[b, 128 * qi:128 * qi + 128] = o[128 * j:128 * j + 128]
    return outp
```

```python
import numpy as np
from contextlib import ExitStack
import concourse.bass as bass
import concourse.mybir as mybir
from concourse.bass_utils import run_bass_kernel_spmd

F32 = mybir.dt.float32
BF16 = mybir.dt.bfloat16
AF = mybir.ActivationFunctionType
ALU = mybir.AluOpType

D = 1024
S_LEN = 8192
NT = 16
NB = 64
N_IN = 3348
DFF = 2816
EPS = 1e-6
O_GQ, O_GK, O_GV, O_GZ, O_GA, O_GB, O_NQ, O_KC, O_VC, O_KSL, O_VSL, O_KWN, O_VWN, O_NG = (
    0, 512, 1024, 1536, 2048, 2052, 2056, 2568, 2696, 2824, 2952, 3080, 3208, 3336)
BIG = 30000.0


class Buf:
    __slots__ = ("name", "w", "rs", "const", "excl")

    def __init__(self, name, const=False, excl=False):
        self.name = name
        self.w = None
        self.rs = []
        self.const = const
        self.excl = excl


class V:
    __slots__ = ("ap", "bs")

    def __init__(self, ap, bs):
        self.ap = ap
        self.bs = bs if isinstance(bs, (list, tuple)) else [bs]

    def bitcast(self, dt):
        return V(self.ap.bitcast(dt), self.bs)

    def re(self, pat, **kw):
        return V(self.ap.rearrange(pat, **kw), self.bs)

    def bc(self, shape):
        return V(self.ap.broadcast_to(shape), self.bs)

    def __getitem__(self, k):
        return V(self.ap[k], self.bs)


class Tl:
    def __init__(self, t, b):
        self.t = t
        self.b = b

    def __getitem__(self, k):
        return V(self.t[k], self.b)


class Op:
    __slots__ = ("eng", "fn", "deps", "dmaw", "signal", "sigval", "grp")


class DGroup:
    def __init__(self, name, sem, full=False):
        self.name = name
        self.sem = sem
        self.count = 0
        self.full = full


ENGS = ("pe", "act", "dve", "pool", "sp")


class Sched:
    def __init__(self, nc, es):
        self.nc = nc
        self.es = es
        self.eng = {"pe": nc.tensor, "act": nc.scalar, "dve": nc.vector, "pool": nc.gpsimd, "sp": nc.sync}
        self.sem = {e: es.enter_context(nc.semaphore("s_" + e)) for e in ENGS}
        self.cnt = {e: 0 for e in ENGS}
        self.seen = {e: {} for e in ENGS}
        self.ops = []
        self.bufs = []
        self.groups = []
        self.nins = 0

    def buf(self, name, const=False, excl=False):
        b = Buf(name, const, excl)
        self.bufs.append(b)
        return b

    def group(self, name, full=False):
        g = DGroup(name, self.es.enter_context(self.nc.semaphore("g_" + name)), full)
        self.groups.append(g)
        return g

    def op(self, eng, fn, r=(), w=(), grp=None):
        o = Op()
        o.eng = eng
        o.fn = fn
        o.signal = False
        o.sigval = None
        o.grp = grp
        deps = []
        for b in r:
            if b.w is not None:
                deps.append(b.w)
            if b.excl:
                deps.extend(x for x in b.rs if x.eng != eng)
        for b in w:
            if b.w is not None:
                deps.append(b.w)
            deps.extend(b.rs)
        seen = set()
        dd = []
        for d in deps:
            if id(d) in seen or d is o:
                continue
            seen.add(id(d))
            if d.eng == "pe" and eng == "pe":
                continue
            if grp is not None and grp.full and d.grp is grp:
                continue
            dd.append(d)
        o.deps = dd
        o.dmaw = {}
        for d in dd:
            if d.grp is None:
                d.signal = True
            else:
                o.dmaw[d.grp.name] = d.grp.count
        if grp is not None:
            grp.count += 1
        for b in w:
            b.w = o
            b.rs = []
        for b in r:
            if not b.const and b.w is not o:
                b.rs.append(o)
        self.ops.append(o)
        return o

    def flush(self, barrier=True):
        if barrier:
            last = {}
            for o in self.ops:
                last[o.eng] = o
            for e, o in last.items():
                if o.grp is None:
                    o.signal = True
        for o in self.ops:
            e = self.eng[o.eng]
            for d in o.deps:
                if d.grp is not None:
                    key = "g_" + d.grp.name
                    val = 16 * (d.grp.count if d.grp.full else o.dmaw[d.grp.name])
                    sem = d.grp.sem
                else:
                    key = d.eng
                    val = d.sigval
                    sem = self.sem[d.eng]
                    assert val is not None, (o.eng, d.eng)
                if self.seen[o.eng].get(key, 0) >= val:
                    continue
                e.wait_ge(sem, val)
                self.seen[o.eng][key] = val
            ins = o.fn(e)
            self.nins += 1
            if o.grp is not None:
                ins.then_inc(o.grp.sem, 16)
            elif o.signal:
                self.cnt[o.eng] += 1
                o.sigval = self.cnt[o.eng]
                ins.then_inc(self.sem[o.eng], 1)
        self.ops = []
        if barrier:
            for en in ENGS:
                e = self.eng[en]
                for e2 in ENGS:
                    if e2 == en or e2 == "sp":
                        continue
                    if self.cnt[e2] > self.seen[en].get(e2, 0):
                        e.wait_ge(self.sem[e2], self.cnt[e2])
                        self.seen[en][e2] = self.cnt[e2]
                for g in self.groups:
                    key = "g_" + g.name
                    if 16 * g.count > self.seen[en].get(key, 0):
                        e.wait_ge(g.sem, 16 * g.count)
                        self.seen[en][key] = 16 * g.count
            for b in self.bufs:
                b.w = None
                b.rs = []

    @staticmethod
    def _bs(*vs):
        out = []
        for v in vs:
            if isinstance(v, V):
                for b in v.bs:
                    if b not in out:
                        out.append(b)
        return out

    @staticmethod
    def _a(v):
        return v.ap if isinstance(v, V) else v

    def mm(self, out, lhsT, rhs, start=True, stop=True):
        o_, l_, r_ = out.ap, lhsT.ap, rhs.ap
        self.op("pe", lambda e: e.matmul(o_, lhsT=l_, rhs=r_, start=start, stop=stop),
                r=self._bs(lhsT, rhs), w=self._bs(out))

    def mm_(self, out, lhsT, rhs, start=True, stop=True, skip=False):
        o_, l_, r_ = out.ap, lhsT.ap, rhs.ap
        self.op("pe", lambda e: e.matmul(o_, lhsT=l_, rhs=r_, start=start, stop=stop, skip_group_check=skip),
                r=self._bs(lhsT, rhs), w=self._bs(out))

    def tr(self, out, in_, ident):
        o_, i_, d_ = out.ap, in_.ap, ident.ap
        self.op("pe", lambda e: e.transpose(o_, i_, d_), r=self._bs(in_, ident), w=self._bs(out))

    def act(self, out, in_, func, bias=None, scale=None, accum=None, eng="act"):
        kw = {}
        if bias is not None:
            kw["bias"] = self._a(bias)
        if scale is not None:
            kw["scale"] = self._a(scale)
        if accum is not None:
            kw["accum_out"] = accum.ap
        o_, i_ = out.ap, in_.ap
        self.op("act", lambda e: e.activation(out=o_, in_=i_, func=func, **kw),
                r=self._bs(in_, bias, scale), w=self._bs(out, accum))

    def tt(self, eng, out, in0, in1, op):
        o_, a_, b_ = out.ap, in0.ap, in1.ap
        self.op(eng, lambda e: e.tensor_tensor(out=o_, in0=a_, in1=b_, op=op),
                r=self._bs(in0, in1), w=self._bs(out))

    def ts(self, eng, out, in0, s1, s2=None, op0=ALU.mult, op1=None):
        o_, a_ = out.ap, in0.ap
        s1_, s2_ = self._a(s1), self._a(s2)
        kw = {}
        if op1 is not None:
            kw["op1"] = op1
        self.op(eng, lambda e: e.tensor_scalar(out=o_, in0=a_, scalar1=s1_, scalar2=s2_, op0=op0, **kw),
                r=self._bs(in0, s1, s2), w=self._bs(out))

    def stt(self, out, in0, scalar, in1, op0, op1):
        o_, a_, b_ = out.ap, in0.ap, in1.ap
        s_ = self._a(scalar)
        self.op("dve", lambda e: e.scalar_tensor_tensor(out=o_, in0=a_, scalar=s_, in1=b_, op0=op0, op1=op1),
                r=self._bs(in0, scalar, in1), w=self._bs(out))

    def copy(self, eng, out, in_):
        o_, i_ = out.ap, in_.ap
        if eng == "act":
            self.op("act", lambda e: e.copy(out=o_, in_=i_), r=self._bs(in_), w=self._bs(out))
        else:
            self.op(eng, lambda e: e.tensor_copy(out=o_, in_=i_), r=self._bs(in_), w=self._bs(out))

    def recip(self, out, in_):
        o_, i_ = out.ap, in_.ap
        self.op("dve", lambda e: e.reciprocal(out=o_, in_=i_), r=self._bs(in_), w=self._bs(out))

    def memset(self, eng, out, val):
        o_ = out.ap
        self.op(eng, lambda e: e.memset(o_, val), r=[], w=self._bs(out))

    def dma(self, out, in_, grp, eng="sp"):
        o_, i_ = self._a(out), self._a(in_)
        self.op(eng, lambda e: e.dma_start(out=o_, in_=i_), r=self._bs(in_), w=self._bs(out), grp=grp)


class Bank:
    def __init__(self, t, q):
        self.t = t
        self.q = q

    def f(self, c0, c1, p0=0, p1=128):
        return V(self.t[p0:p1, c0:c1], self.q[0:1])

    def b(self, c0, c1, p0=0, p1=128):
        return V(self.t[p0:p1, :].bitcast(BF16)[:, c0:c1], self.q[0:1])


W1_SEGS = ((0, 1536, 0), (2048, 2056, 1536), (2568, 3336, 1544))
W1_N = 2312
C_AB, C_KC, C_VC, C_KSL, C_VSL, C_KWN, C_VWN = 1536, 1544, 1672, 1800, 1928, 2056, 2184


class Builder:
    def __init__(self, debug=False, ntiles=NT):
        self.debug = debug
        self.ntiles = ntiles
        import os
        self.stop = int(os.environ.get('K_STOP', '0'))
        self.var = int(os.environ.get('K_VAR', '0'))
        self.halo = int(os.environ.get('K_HALO', '0'))
        self.nc = bass.Bass("TRN2", target_bir_lowering=False)
        self.din = {}
        self.dout = {}

    def inp(self, name, shape, dt=F32):
        self.din[name] = self.nc.dram_tensor(name, list(shape), dt, kind="ExternalInput").ap()
        return self.din[name]

    def outp(self, name, shape, dt=F32):
        self.dout[name] = self.nc.dram_tensor(name, list(shape), dt, kind="ExternalOutput").ap()
        return self.dout[name]

    def scratch(self, name, shape, dt=F32):
        if self.debug:
            return self.outp(name, shape, dt)
        return self.nc.dram_tensor(name, list(shape), dt, kind="Internal").ap()

    def sb(self, es, name, shape, dt=F32, const=False):
        t = es.enter_context(self.nc.sbuf_tensor(name, list(shape), dt))
        return Tl(t, self.S.buf(name, const))

    def cload(self, es, name, shape, grp):
        d = self.inp(name, shape)
        t = self.sb(es, "c_" + name, shape, F32, const=True)
        idx = tuple(slice(None) for _ in shape)
        self.S.dma(t[idx], d[idx], grp)
        return t

    def build(self, upto=9):
        nc = self.nc
        I = self.inp
        self.xb = I("xb", [S_LEN, D])
        self.xo = I("xo", [2048, D])
        cT = I("cT", [128, 8])
        ada_w = I("ada_w", [D, 6 * D])
        self.w_in = I("w_in", [D, N_IN])
        self.out = self.outp("out", [2048, D])
        self.o_own = self.scratch("o_own", [16, 128, 512])
        self.kslT_d = self.scratch("kslT_d", [128, S_LEN], BF16)
        self.kwnT_d = self.scratch("kwnT_d", [128, S_LEN], BF16)
        self.kcT_d = self.scratch("kcT_d", [128, S_LEN], BF16)
        self.vcT_d = self.scratch("vcT_d", [128, S_LEN], BF16)
        self.vsl_d = self.scratch("vsl_d", [128, NB, 128], BF16)
        self.vwn_d = self.scratch("vwn_d", [128, NB, 128], BF16)
        self.mix_d = self.scratch("mix_d", [128, 8, 2048], BF16)
        self.oprev_d = self.scratch("oprev_d", [16, 2, 512])
        self.mixh_d = self.scratch("mixh_d", [128, 8, 32], BF16)

        with ExitStack() as top:
            S = self.S = Sched(nc, top)
            self.g_const = g_const = S.group("const", full=True)
            self.g_out = S.group("outw")
            self.idf = idf = self.cload(top, "identf", [128, 128], g_const)
            self.idb = idb = self.sb(top, "idb", [128, 128], BF16)
            S.copy("dve", idb[:, :], idf[:, :])
            self.modT = modT = self.sb(top, "modT", [128, 48])
            self.s1 = s1 = self.sb(top, "s1", [128, 8])
            self.s2 = s2 = self.sb(top, "s2", [128, 8])
            n1 = self.cload(top, "n1w", [128, 8], g_const)
            n2 = self.cload(top, "n2w", [128, 8], g_const)
            self.pb = []
            for i in range(8):
                t = top.enter_context(nc.psum_tensor(f"pb{i}", [128, 512], F32))
                self.pb.append(Bank(t, [S.buf(f"pb{i}", excl=True)]))
            pb = self.pb
            self.bankA = [pb[2], pb[3]]
            self.bankB = [pb[4], pb[5]]
            self.bankC = [pb[0], pb[1]]

            with ExitStack() as es:
                ct = self.sb(es, "ct", [128, 8])
                sc = self.sb(es, "sc", [128, 8])
                abT = self.cload(es, "ada_bT", [128, 48], g_const)
                S.dma(ct[:, :], cT[:, :], g_const)
                S.act(sc[:, :], ct[:, :], AF.Silu)
                aw = [self.sb(es, f"aw{i}", [128, 6 * D]) for i in range(2)]
                g_aw = [S.group(f"aw{i}") for i in range(2)]
                for k in range(8):
                    sl = k % 2
                    for hh in range(4):
                        S.dma(aw[sl][:, hh * 1536:(hh + 1) * 1536],
                              ada_w[k * 128:(k + 1) * 128, hh * 1536:(hh + 1) * 1536], g_aw[sl])
                    pm = pb[k % 2]
                    for cc in range(48):
                        S.mm(pm.f(cc, cc + 1), lhsT=aw[sl][:, cc * 128:(cc + 1) * 128], rhs=sc[:, k:k + 1])
                    S.tt("dve", modT[:, :], pm.f(0, 48), (abT if k == 0 else modT)[:, :], ALU.add)
                S.stt(s1[:, :], modT[:, 8:16], 1.0, n1[:, :], ALU.add, ALU.mult)
                S.stt(s2[:, :], modT[:, 32:40], 1.0, n2[:, :], ALU.add, ALU.mult)
                S.flush()

            if upto >= 1:
                self.phase1()
            keep = dict(qnT=self.sb(top, "qnT", [128, 4, 2048], BF16), gates=self.sb(top, "gates", [128, 16, 12]),
                        qnTh=self.sb(top, "qnTh", [128, 4, 32], BF16), gatesh=self.sb(top, "gatesh", [32, 12]))
            if upto >= 2:
                self.phase2(keep)
                if self.debug:
                    S.dma(self.outp("d_qnT", [128, 4, 2048], BF16)[:, :, :], keep["qnT"][:, :, :], self.g_out)
                    S.dma(self.outp("d_gates", [128, 16, 12])[:, :, :], keep["gates"][:, :, :], self.g_out)
            if upto >= 3:
                S.flush()
                self.phase3(keep)
            if upto >= 4:
                S.flush()
                if self.debug:
                    self.dx1 = self.outp("d_x1", [4, 128, 4, D])
                self.phase45()
            S.flush()
        return nc

    def front(self, F, xt, nb, hT, c0, scol, sh_c0, b0=0):
        S = self.S
        pb = self.pb
        ssq, rt, rstd, junk, xs = F["ssq"], F["rt"], F["rstd"], F["junk"], F["xs"]
        for b2 in range(nb):
            S.act(junk[:, :], xt[:, b0 + b2, :], AF.Square, accum=ssq[:, b2:b2 + 1])
        S.act(rt[:, 0:nb], ssq[:, 0:nb], AF.Ln, bias=EPS, scale=1.0 / D)
        S.act(rstd[:, 0:nb], rt[:, 0:nb], AF.Exp, scale=-0.5)
        for b2 in range(nb):
            S.ts("pool", xs[:, b2, :], xt[:, b0 + b2, :], rstd[:, b2:b2 + 1], 1.0, ALU.mult, ALU.mult)
        w = nb * 128
        for half in range(2):
            bank = pb[half]
            for k in range(half * 4, half * 4 + 4):
                off = (k % 4) * 256
                for b2 in range(nb):
                    S.tr(bank.b(off + b2 * 128, off + (b2 + 1) * 128), xs[:, b2, k * 128:(k + 1) * 128],
                         self.idb[:, :])
            for k in range(half * 4, half * 4 + 4):
                off = (k % 4) * 256
                S.ts("dve", hT[:, k, c0:c0 + w], bank.b(off, off + w), scol[:, k:k + 1],
                     self.modT[:, sh_c0 + k:sh_c0 + k + 1], ALU.mult, ALU.add)

    def front32(self, F, xt, hT, scol, sh_c0):
        S = self.S
        bank = self.pb[0]
        ssq, rt, rstd, junk, xs = F["ssq"], F["rt"], F["rstd"], F["junk"], F["xs"]
        S.act(junk[0:32, :], xt[:, :], AF.Square, accum=ssq[0:32, 0:1])
        S.act(rt[0:32, 0:1], ssq[0:32, 0:1], AF.Ln, bias=EPS, scale=1.0 / D)
        S.act(rstd[0:32, 0:1], rt[0:32, 0:1], AF.Exp, scale=-0.5)
        S.ts("pool", xs[0:32, 0, :], xt[:, :], rstd[0:32, 0:1], 1.0, ALU.mult, ALU.mult)
        for k in range(8):
            S.tr(bank.b(k * 32, (k + 1) * 32), xs[0:32, 0, k * 128:(k + 1) * 128], self.idb[0:32, 0:32])
        for k in range(8):
            S.ts("dve", hT[:, k, :], bank.b(k * 32, (k + 1) * 32), scol[:, k:k + 1],
                 self.modT[:, sh_c0 + k:sh_c0 + k + 1], ALU.mult, ALU.add)

    def phase1(self):
        S = self.S
        nc = self.nc
        pb = self.pb
        gc_ = S.group("const1", full=True)
        idb, idf = self.idb, self.idf
        with ExitStack() as es:
            tri2 = self.cload(es, "tri2", [128, 128], gc_)
            blk2 = self.cload(es, "blk2", [128, 128], gc_)
            onesf = self.cload(es, "onesf", [128, 128], gc_)
            cind = self.cload(es, "cind", [128, 2], gc_)
            ma4 = self.cload(es, "ma4", [128, 512], gc_)
            mb4 = self.cload(es, "mb4", [128, 512], gc_)
            sel = self.cload(es, "sel", [128, 4], gc_)
            hsel = self.cload(es, "hsel", [128, 4], gc_)
            cw = self.cload(es, "gcw", [128, 48], gc_)
            dtb = self.cload(es, "dtb", [128, 4], gc_)
            alog = self.cload(es, "alog", [128, 4], gc_)
            kslw = self.cload(es, "kslw", [128, 1], gc_)
            kwnw = self.cload(es, "kwnw", [128, 1], gc_)
            negones = self.sb(es, "negones", [128, 128])
            S.ts("dve", negones[:, :], onesf[:, :], -1.0, None, ALU.mult)
            onesb = self.sb(es, "onesb", [128, 128], BF16)
            S.copy("dve", onesb[:, :], onesf[:, :])
            i4b = self.sb(es, "i4b", [128, 512], BF16)
            for h in range(4):
                S.copy("dve", i4b[:, h * 128:(h + 1) * 128], idf[:, :])
            negA = self.sb(es, "negA", [128, 4])
            S.act(negA[:, :], alog[:, :], AF.Exp)
            S.ts("dve", negA[:, :], negA[:, :], -1.0, None, ALU.mult)

            winb = self.sb(es, "winb", [128, 8, W1_N], BF16)
            with ExitStack() as es2:
                wst = [self.sb(es2, f"wst{i}", [128, W1_N]) for i in range(2)]
                g_w = [S.group(f"wst{i}") for i in range(2)]
                for k in range(8):
                    sl = k % 2
                    for (a, b_, o) in W1_SEGS:
                        S.dma(wst[sl][:, o:o + (b_ - a)], self.w_in[k * 128:(k + 1) * 128, a:b_], g_w[sl])
                    S.copy("dve", winb[:, k, 0:1024], wst[sl][:, 0:1024])
                    S.copy("pool", winb[:, k, 1024:W1_N], wst[sl][:, 1024:W1_N])
                S.flush()

            F = dict(ssq=self.sb(es, "f_ssq", [128, 4]), rt=self.sb(es, "f_rt", [128, 4]),
                     rstd=self.sb(es, "f_rstd", [128, 4]), junk=self.sb(es, "f_junk", [128, 1024], BF16),
                     xs=self.sb(es, "f_xs", [128, 2, 1024], BF16))
            xt = [self.sb(es, f"xt{i}", [128, 2, 1024]) for i in range(2)]
            g_x = [S.group(f"xt{i}") for i in range(2)]
            hT = self.sb(es, "hT", [128, 8, 512], BF16)
            pre = [self.sb(es, f"pre{i}", [128, 515]) for i in range(3)]
            hist = self.sb(es, "hist", [128, 12, 3])
            S.memset("pool", hist[:, :, :], 0.0)
            acc = [self.sb(es, f"cacc{i}", [128, 512]) for i in range(3)]
            yfs = [self.sb(es, f"yfs{i}", [128, 512], BF16 if i < 8 else F32) for i in range(10)]
            sq = [self.sb(es, f"sq{i}", [128, 512], BF16) for i in range(3)]
            rtt = [self.sb(es, f"rtt{i}", [128, 512]) for i in range(3)]
            qT = self.sb(es, "qT", [128, 4, 512], BF16)
            kT = self.sb(es, "kT", [128, 4, 512], BF16)
            vT = self.sb(es, "vT", [128, 4, 512], BF16)
            st_f = [[self.sb(es, f"stf{c}_{i}", [128, 512], BF16) for i in range(2)] for c in range(4)]
            st_v = [[self.sb(es, f"stv{c}_{i}", [128, 4, 128], BF16) for i in range(2)] for c in range(2)]
            g_sf = [[S.group(f"sf{c}_{i}") for i in range(2)] for c in range(4)]
            g_sv = [[S.group(f"sv{c}_{i}") for i in range(2)] for c in range(2)]
            ab = self.sb(es, "ab", [128, 4, 8])
            S32 = self.sb(es, "S32", [128, 4, 128])
            Sb = self.sb(es, "Sb", [128, 4, 128], BF16)
            S.memset("pool", S32[:, :, :], 0.0)
            S.memset("pool", Sb[:, :, :], 0.0)
            oacc = [self.sb(es, f"oacc{i}", [128, 512]) for i in range(2)]
            g_o = [S.group(f"oacc{i}") for i in range(2)]
            oph = [self.sb(es, f"oph{i}", [128, 512]) for i in range(2)]
            g_oh = [S.group(f"oph{i}") for i in range(2)]
            G = []
            for s_ in range(2):
                g = {}
                for nm, shp, dt in (("sm", [128, 64], F32), ("TG", [128, 4, 128], F32), ("Xs", [128, 512], F32),
                                    ("XA", [128, 512], F32), ("XB", [128, 512], F32), ("EG", [128, 4, 128], F32),
                                    ("Nb", [128, 4, 128], BF16), ("aT", [128, 4, 128], BF16),
                                    ("qdT", [128, 4, 128], BF16), ("kw", [128, 4, 128], BF16),
                                    ("kd", [128, 4, 128], BF16), ("vb", [128, 4, 128], BF16),
                                    ("NTb", [128, 4, 128], BF16), ("RTb", [128, 4, 128], BF16),
                                    ("P0", [128, 4, 128], BF16), ("P1", [128, 4, 128], BF16),
                                    ("Q0", [128, 4, 128], BF16), ("Q1", [128, 4, 128], BF16),
                                    ("u", [128, 4, 128], F32), ("wTb", [128, 4, 128], BF16),
                                    ("vn", [128, 4, 128], BF16)):
                    g[nm] = self.sb(es, f"g{s_}_{nm}", shp, dt)
                G.append(g)

            xrows = self.xb.rearrange("(n b p) d -> n p b d", b=2, p=128)

            def load_x(n):
                S.dma(xt[n % 2][:, :, :], xrows[n], g_x[n % 2])

            load_x(0)
            for ti in range(self.ntiles):
                for hf in range(2):
                    n = 2 * ti + hf
                    if n + 1 < 2 * NT:
                        load_x(n + 1)
                    self.front(F, xt[n % 2], 2, hT, hf * 256, self.s1, 0)
                if self.stop == 1:
                    continue
                nsa_ch = ((C_KC, self.kcT_d, None), (C_VC, self.vcT_d, None), (C_KSL, self.kslT_d, kslw),
                          (C_KWN, self.kwnT_d, kwnw))

                def st1(c, pos):
                    bank = pb[2 + (pos % 2)]
                    coff = c * 128 if c < 12 else nsa_ch[c - 12][0]
                    for k in range(8):
                        S.mm(bank.f(0, 512), lhsT=winb[:, k, coff:coff + 128], rhs=hT[:, k, :],
                             start=(k == 0), stop=(k == 7))
                    if c < 12:
                        p_ = pre[pos % 3]
                        S.copy("pool", p_[:, 0:3], hist[:, c, :])
                        S.copy("act", p_[:, 3:515], bank.f(0, 512))
                        S.copy("pool", hist[:, c, :], p_[:, 512:515])
                        a_ = acc[pos % 3]
                        S.ts("pool", a_[:, :], p_[:, 0:512], cw[:, c * 4:c * 4 + 1], 1.0, ALU.mult, ALU.mult)
                        for j in range(1, 4):
                            S.stt(a_[:, :], p_[:, j:j + 512], cw[:, c * 4 + j:c * 4 + j + 1], a_[:, :], ALU.mult, ALU.add)
                    else:
                        ci = c - 12
                        wcol = nsa_ch[ci][2]
                        if wcol is None:
                            S.copy("act", st_f[ci][ti % 2][:, :], bank.f(0, 512))
                            S.dma(nsa_ch[ci][1][:, ti * 512:(ti + 1) * 512], st_f[ci][ti % 2][:, :], g_sf[ci][ti % 2])
                        else:
                            S.copy("act", yfs[c - 6][:, :], bank.f(0, 512))

                def st2(c, pos):
                    h = c % 4
                    if c >= 12:
                        return
                    if c >= 8:
                        S.act(vT[:, h, :], acc[pos % 3][:, :], AF.Silu)
                        return
                    S.act(yfs[c][:, :], acc[pos % 3][:, :], AF.Silu)

                def st3(c, i_):
                    h = c % 4
                    bk2 = pb[4 + (i_ % 2)]
                    yi = c if c < 8 else c - 6
                    S.act(sq[i_ % 3][:, :], yfs[yi][:, :], AF.Square)
                    S.mm(bk2.f(0, 512), lhsT=onesb[:, :], rhs=sq[i_ % 3][:, :])
                    r_ = rtt[i_ % 3]
                    if c < 4:
                        S.act(r_[:, :], bk2.f(0, 512), AF.Ln, bias=EPS * 128.0, scale=128.0)
                    elif c < 8:
                        S.act(r_[:, :], bk2.f(0, 512), AF.Ln, bias=EPS, scale=1.0)
                    else:
                        S.act(r_[:, :], bk2.f(0, 512), AF.Ln, bias=EPS, scale=1.0 / 128.0)
                    S.act(r_[:, :], r_[:, :], AF.Exp, scale=-0.5)
                    if c < 8:
                        S.tt("pool", (qT if c < 4 else kT)[:, h, :], yfs[yi][:, :], r_[:, :], ALU.mult)
                    else:
                        ci = c - 12
                        st = st_f[ci][ti % 2]
                        S.stt(st[:, :], yfs[yi][:, :], nsa_ch[ci][2][:, 0:1], r_[:, :], ALU.mult, ALU.mult)
                        S.dma(nsa_ch[ci][1][:, ti * 512:(ti + 1) * 512], st[:, :], g_sf[ci][ti % 2])

                order = [12, 13, 14, 15] + list(range(12))
                for i in range(len(order) + 1):
                    if i < len(order):
                        st1(order[i], i)
                    if 0 <= i - 1 < len(order):
                        st2(order[i - 1], i - 1)
                if self.stop == 3:
                    continue
                for blk in range(4):
                    bank = pb[6]
                    for vi, coff in enumerate((C_VSL, C_VWN)):
                        for k in range(8):
                            if self.var == 4:
                                break
                            S.mm(bank.f(vi * 128, (vi + 1) * 128), lhsT=hT[:, k, blk * 128:(blk + 1) * 128],
                                 rhs=winb[:, k, coff:coff + 128], start=(k == 0), stop=(k == 7))
                    for k in range(8):
                        if self.var == 1:
                            break
                        S.mm(bank.f(256, 264), lhsT=hT[:, k, blk * 128:(blk + 1) * 128],
                             rhs=winb[:, k, C_AB:C_AB + 8], start=(k == 0), stop=(k == 7))
                    if self.var != 3:
                        S.copy("act", st_v[0][ti % 2][:, blk, :], bank.f(0, 128))
                        S.copy("act", st_v[1][ti % 2][:, blk, :], bank.f(128, 256))
                    if self.var != 5:
                        S.copy("dve", ab[:, blk, :], bank.f(256, 264))
                if self.var != 2:
                    S.dma(self.vsl_d[:, ti * 4:(ti + 1) * 4, :], st_v[0][ti % 2][:, :, :], g_sv[0][ti % 2])
                    S.dma(self.vwn_d[:, ti * 4:(ti + 1) * 4, :], st_v[1][ti % 2][:, :, :], g_sv[1][ti % 2])

                for i_, c in enumerate([14, 15, 0, 1, 2, 3, 4, 5, 6, 7]):
                    st3(c, i_)
                if self.stop == 4:
                    continue
                for pair in range(2):
                    blks = (2 * pair, 2 * pair + 1)
                    for s_, blk in enumerate(blks):
                        self.gdn_local_1(G[s_], s_, blk, ab, tri2, blk2, onesf, negones, cind, ma4, mb4, dtb, negA,
                                         qT, kT, vT)
                    if self.stop == 5:
                        continue
                    self.gdn_solve(G, blks, i4b)
                    if self.stop == 6:
                        continue
                    for s_, blk in enumerate(blks):
                        self.gdn_uw(G[s_], s_)
                    if self.stop == 7:
                        continue
                    for s_, blk in enumerate(blks):
                        oa = oacc[ti % 2]
                        self.gdn_recur(G[s_], S32, Sb, sel, blk, oa, hsel, oph[ti % 2])
                S.dma(self.o_own[ti], oacc[ti % 2][:, :], g_o[ti % 2])
                S.dma(self.oprev_d[ti], oph[ti % 2][126:128, :], g_oh[ti % 2])
            S.flush()

    def gdn_local_1(self, g, s_, blk, ab, tri2, blk2, onesf, negones, cind, ma4, mb4, dtb, negA, qT, kT, vT):
        S = self.S
        pb = self.pb
        sm = g["sm"]
        cs = slice(blk * 128, (blk + 1) * 128)
        x_, t_, gg, be, nb_, gci, gcv, gl, egc, ekd, bw, glS = (
            sm[:, 0:4], sm[:, 4:8], sm[:, 8:12], sm[:, 12:16], sm[:, 16:20], sm[:, 20:28], sm[:, 28:32],
            sm[:, 32:36], sm[:, 36:40], sm[:, 40:44], sm[:, 44:48], sm[:, 48:56])
        S.tt("dve", x_, ab[:, blk, 0:4], dtb[:, :], ALU.add)
        S.stt(t_, x_, -1.0, x_, ALU.mult, ALU.max)
        S.act(t_, t_, AF.Exp, scale=-1.0)
        S.act(t_, t_, AF.Ln, bias=1.0)
        S.stt(t_, x_, 0.0, t_, ALU.max, ALU.add)
        S.tt("dve", gg, t_, negA[:, :], ALU.mult)
        S.act(be, ab[:, blk, 4:8], AF.Exp, scale=-1.0)
        S.ts("dve", be, be, 1.0, None, ALU.add)
        S.recip(be, be)
        S.ts("dve", nb_, be, -1.0, None, ALU.mult)
        gci3 = gci.re("p (h i) -> p h i", i=2)
        for i in range(2):
            S.ts("dve", gci3[:, :, i], gg, cind[:, i:i + 1], None, ALU.mult)
        for h in range(4):
            S.ts("pool", g["TG"][:, h, :], tri2[:, :], gg[:, h:h + 1], 1.0, ALU.mult, ALU.mult)
        bS = pb[6]
        o0 = 384 + s_ * 32
        S.mm(bS.f(o0, o0 + 4), lhsT=tri2[:, :], rhs=gg)
        S.mm(bS.f(o0 + 4, o0 + 8), lhsT=blk2[:, :], rhs=gg)
        S.mm(bS.f(o0 + 8, o0 + 16), lhsT=onesf[:, :], rhs=gci)
        bX = self.bankA[s_]
        for h in range(4):
            S.mm(bX.f(h * 128, (h + 1) * 128), lhsT=onesf[:, :], rhs=g["TG"][:, h, :], start=True, stop=False)
            S.mm(bX.f(h * 128, (h + 1) * 128), lhsT=g["TG"][:, h, :], rhs=negones[:, :], start=False, stop=True)
        S.copy("dve", sm[:, 28:36], bS.f(o0, o0 + 8))
        S.act(glS, bS.f(o0 + 8, o0 + 16), AF.Exp)
        S.act(egc, gcv, AF.Exp)
        S.tt("dve", ekd, gl, gcv, ALU.subtract)
        S.act(ekd, ekd, AF.Exp)
        S.tt("dve", bw, be, egc, ALU.mult)
        S.copy("act", g["Xs"][:, :], bX.f(0, 512))
        for h in range(4):
            S.act(g["EG"][:, h, :], bX.f(h * 128, (h + 1) * 128), AF.Exp, bias=gcv[:, h:h + 1])
        S.tt("pool", g["XA"][:, :], g["Xs"][:, :], ma4[:, :], ALU.add)
        S.tt("pool", g["XB"][:, :], g["Xs"][:, :], mb4[:, :], ALU.add)
        S.act(g["XA"][:, :], g["XA"][:, :], AF.Exp, scale=-1.0)
        S.act(g["XB"][:, :], g["XB"][:, :], AF.Exp)
        bK = self.bankA[s_]
        bQ = self.bankB[s_]
        for h in range(4):
            S.mm(bK.f(h * 128, (h + 1) * 128), lhsT=kT[:, h, cs], rhs=kT[:, h, cs])
        for h in range(4):
            S.mm(bQ.f(h * 128, (h + 1) * 128), lhsT=kT[:, h, cs], rhs=qT[:, h, cs])
        for h in range(4):
            S.stt(g["Nb"][:, h, :], bK.f(h * 128, (h + 1) * 128), nb_[:, h:h + 1],
                  g["XA"][:, h * 128:(h + 1) * 128], ALU.mult, ALU.mult)
        S.tt("dve", g["aT"][:, :, :].re("p h c -> p (h c)"), bQ.f(0, 512), g["XB"][:, :], ALU.mult)
        S.tt("pool", g["qdT"][:, :, :], qT[:, :, cs], g["EG"][:, :, :], ALU.mult)
        bT = pb[7]
        for h in range(4):
            S.tr(bT.b(h * 128, (h + 1) * 128), kT[:, h, cs], self.idb[:, :])
        for h in range(4):
            S.tr(bT.b(512 + h * 128, 512 + (h + 1) * 128), vT[:, h, cs], self.idb[:, :])
        kt3 = bT.b(0, 512).re("p (h d) -> p h d", h=4)
        vt3 = bT.b(512, 1024).re("p (h d) -> p h d", h=4)
        S.tt("dve", g["kw"][:, :, :], kt3, V(bw.ap.unsqueeze(2).broadcast_to([128, 4, 128]), bw.bs), ALU.mult)
        S.tt("dve", g["kd"][:, :, :], kt3, V(ekd.ap.unsqueeze(2).broadcast_to([128, 4, 128]), ekd.bs), ALU.mult)
        S.tt("dve", g["vb"][:, :, :], vt3, V(be.ap.unsqueeze(2).broadcast_to([128, 4, 128]), be.bs), ALU.mult)

    def gdn_solve(self, G, blks, i4b):
        S = self.S
        idb = self.idb
        for s_ in range(len(blks)):
            g = G[s_]
            bA, bC = self.bankA[s_], self.bankC[s_]
            for h in range(4):
                S.mm(bA.f(h * 128, (h + 1) * 128), lhsT=g["Nb"][:, h, :], rhs=idb[:, :])
            S.copy("act", g["NTb"][:, :, :].re("p h c -> p (h c)"), bA.f(0, 512))
            S.mm(bC.f(0, 512), lhsT=idb[:, :], rhs=i4b[:, :], start=True, stop=False)
            for h in range(4):
                S.mm(bC.f(h * 128, (h + 1) * 128), lhsT=g["Nb"][:, h, :], rhs=idb[:, :], start=False, stop=(h == 3))
            S.copy("act", g["RTb"][:, :, :].re("p h c -> p (h c)"), bC.f(0, 512))
        P = [G[s_]["Nb"] for s_ in range(len(blks))]
        Q = [G[s_]["NTb"] for s_ in range(len(blks))]
        for k in range(1, 6):
            for s_ in range(len(blks)):
                g = G[s_]
                bA, bB, bC = self.bankA[s_], self.bankB[s_], self.bankC[s_]
                Pn = g["P%d" % (k % 2)]
                Qn = g["Q%d" % (k % 2)]
                for h in range(4):
                    S.mm(bA.f(h * 128, (h + 1) * 128), lhsT=Q[s_][:, h, :], rhs=P[s_][:, h, :])
                if k < 5:
                    for h in range(4):
                        S.mm(bB.f(h * 128, (h + 1) * 128), lhsT=P[s_][:, h, :], rhs=Q[s_][:, h, :])
                S.copy("dve", Pn[:, :, :].re("p h c -> p (h c)"), bA.f(0, 512))
                if k < 5:
                    S.copy("act", Qn[:, :, :].re("p h c -> p (h c)"), bB.f(0, 512))
                S.mm(bC.f(0, 512), lhsT=idb[:, :], rhs=g["RTb"][:, :, :].re("p h c -> p (h c)"), start=True, stop=False)
                for h in range(4):
                    S.mm(bC.f(h * 128, (h + 1) * 128), lhsT=Pn[:, h, :], rhs=g["RTb"][:, h, :],
                         start=False, stop=(h == 3))
                S.copy("act", g["RTb"][:, :, :].re("p h c -> p (h c)"), bC.f(0, 512))
                P[s_] = Pn
                Q[s_] = Qn

    def gdn_uw(self, g, s_):
        S = self.S
        bA, bB = self.bankA[s_], self.bankB[s_]
        for h in range(4):
            S.mm(bA.f(h * 128, (h + 1) * 128), lhsT=g["RTb"][:, h, :], rhs=g["vb"][:, h, :])
        for h in range(4):
            S.mm(bB.f(h * 128, (h + 1) * 128), lhsT=g["kw"][:, h, :], rhs=g["RTb"][:, h, :])
        S.copy("act", g["u"][:, :, :].re("p h c -> p (h c)"), bA.f(0, 512))
        S.copy("dve", g["wTb"][:, :, :].re("p h c -> p (h c)"), bB.f(0, 512))

    def gdn_recur(self, g, S32, Sb, sel, blk, oa, hsel, oh):
        S = self.S
        pb = self.pb
        bV, bO, bS_ = pb[2], pb[3], pb[4]
        glS = g["sm"][:, 48:56].re("p (h i) -> p h i", i=2)
        for i in range(2):
            r0, r1 = 64 * i, 64 * i + 64
            for h in range(4):
                S.mm(bV.f(h * 128, (h + 1) * 128, r0, r1), lhsT=g["wTb"][:, h, r0:r1], rhs=Sb[:, h, :])
            S.tt("dve", g["vn"][r0:r1, :, :].re("p h c -> p (h c)"), g["u"][r0:r1, :, :].re("p h c -> p (h c)"),
                 bV.f(0, 512, r0, r1), ALU.subtract)
            for h in range(4):
                S.mm(bO.f(h * 128, (h + 1) * 128, r0, r1), lhsT=g["qdT"][:, h, r0:r1], rhs=Sb[:, h, :],
                     start=True, stop=False)
                S.mm(bO.f(h * 128, (h + 1) * 128, r0, r1), lhsT=g["aT"][r0:r1, h, r0:r1], rhs=g["vn"][r0:r1, h, :],
                     start=False, stop=True)
            for h in range(4):
                S.mm(bS_.f(h * 128, (h + 1) * 128), lhsT=g["kd"][r0:r1, h, :], rhs=g["vn"][r0:r1, h, :])
            glb = V(glS.ap[:, :, i].unsqueeze(2).broadcast_to([128, 4, 128]), glS.bs)
            S.tt("pool", S32[:, :, :], S32[:, :, :], glb, ALU.mult)
            S.tt("dve", S32[:, :, :].re("p h c -> p (h c)"), S32[:, :, :].re("p h c -> p (h c)"), bS_.f(0, 512), ALU.add)
            S.copy("act", Sb[:, :, :], S32[:, :, :])
        r_ = blk % 4
        if r_ == 0:
            S.ts("dve", oa[:, :], bO.f(0, 512), sel[:, 0:1], None, ALU.mult)
        else:
            S.stt(oa[:, :], bO.f(0, 512), sel[:, r_:r_ + 1], oa[:, :], ALU.mult, ALU.add)
        if r_ == 0:
            S.ts("dve", oh[64:128, :], bO.f(0, 512, 64, 128), hsel[64:128, 0:1], None, ALU.mult)
        else:
            S.stt(oh[64:128, :], bO.f(0, 512, 64, 128), hsel[64:128, r_:r_ + 1], oh[64:128, :], ALU.mult, ALU.add)

    def phase2(self, keep):
        S = self.S
        pb = self.pb
        gc_ = S.group("const2", full=True)
        qnT, gates = keep["qnT"], keep["gates"]
        with ExitStack() as es:
            qnw = self.cload(es, "qnw", [128, 1], gc_)
            gonw4 = self.cload(es, "gonw4", [128, 512], gc_)
            onesf = self.cload(es, "onesf2", [128, 128], gc_)
            onesb = self.sb(es, "onesb2", [128, 128], BF16)
            S.copy("dve", onesb[:, :], onesf[:, :])
            winb2 = self.sb(es, "winb2", [128, 8, 1036], BF16)
            with ExitStack() as es2:
                wst = [self.sb(es2, f"w2st{i}", [128, 1036]) for i in range(2)]
                g_w = [S.group(f"w2st{i}") for i in range(2)]
                for k in range(8):
                    sl = k % 2
                    for (a, b_, o) in ((O_GZ, O_GZ + 512, 0), (O_NQ, O_NQ + 512, 512), (O_NG, O_NG + 12, 1024)):
                        S.dma(wst[sl][:, o:o + (b_ - a)], self.w_in[k * 128:(k + 1) * 128, a:b_], g_w[sl])
                    S.copy("dve" if k % 2 else "pool", winb2[:, k, :], wst[sl][:, :])
                S.flush()
            F = dict(ssq=self.sb(es, "f2_ssq", [128, 4]), rt=self.sb(es, "f2_rt", [128, 4]),
                     rstd=self.sb(es, "f2_rstd", [128, 4]), junk=self.sb(es, "f2_junk", [128, 1024], BF16),
                     xs=self.sb(es, "f2_xs", [128, 2, 1024], BF16))
            xt = [self.sb(es, f"x2t{i}", [128, 2, 1024]) for i in range(2)]
            g_x = [S.group(f"x2t{i}") for i in range(2)]
            hT = self.sb(es, "h2T_", [128, 8, 512], BF16)
            yf = [self.sb(es, f"y2f{i}", [128, 512]) for i in range(2)]
            sq = [self.sb(es, f"s2q{i}", [128, 512], BF16) for i in range(2)]
            rtt = [self.sb(es, f"r2tt{i}", [128, 512]) for i in range(2)]
            og = [self.sb(es, f"og{i}", [128, 512]) for i in range(2)]
            g_og = [S.group(f"og{i}") for i in range(2)]
            zs = [self.sb(es, f"zs{i}", [128, 512]) for i in range(2)]
            t1 = [self.sb(es, f"p2t{i}", [128, 512]) for i in range(2)]
            mg = [self.sb(es, f"mg{i}", [128, 512], BF16) for i in range(2)]
            mgT = [self.sb(es, f"mgT{i}", [128, 4, 128], BF16) for i in range(2)]
            g_mg = [S.group(f"mgT{i}") for i in range(2)]
            osq = self.sb(es, "osq", [128, 8])
            xrows = self.xo.rearrange("(n b p) d -> n p b d", b=2, p=128)

            def load_x(n):
                S.dma(xt[n % 2][:, :, :], xrows[n], g_x[n % 2])

            load_x(0)
            for t in range(4):
                for hf in range(2):
                    n = 2 * t + hf
                    if n + 1 < 8:
                        load_x(n + 1)
                    self.front(F, xt[n % 2], 2, hT, hf * 256, self.s1, 0)
                for h in range(4):
                    bank = pb[2 + (h % 2)]
                    for k in range(8):
                        S.mm(bank.f(0, 512), lhsT=winb2[:, k, 512 + h * 128:512 + (h + 1) * 128], rhs=hT[:, k, :],
                             start=(k == 0), stop=(k == 7))
                    y_ = yf[h % 2]
                    S.copy("act", y_[:, :], bank.f(0, 512))
                    S.act(sq[h % 2][:, :], bank.f(0, 512), AF.Square)
                    bk2 = pb[4 + (h % 2)]
                    S.mm(bk2.f(0, 512), lhsT=onesb[:, :], rhs=sq[h % 2][:, :])
                    r_ = rtt[h % 2]
                    S.act(r_[:, :], bk2.f(0, 512), AF.Sqrt, bias=EPS, scale=1.0 / 128.0)
                    S.recip(r_[:, :], r_[:, :])
                    S.stt(qnT[:, h, t * 512:(t + 1) * 512], y_[:, :], qnw[:, 0:1], r_[:, :], ALU.mult, ALU.mult)
                for blk in range(4):
                    j = 4 * t + blk
                    cs = slice(blk * 128, (blk + 1) * 128)
                    bg = pb[6]
                    for k in range(8):
                        S.mm(bg.f(0, 12), lhsT=hT[:, k, cs], rhs=winb2[:, k, 1024:1036], start=(k == 0), stop=(k == 7))
                    S.act(gates[:, j, :], bg.f(0, 12), AF.Sigmoid)
                    bz = pb[7]
                    for k in range(8):
                        S.mm(bz.f(0, 512), lhsT=hT[:, k, cs], rhs=winb2[:, k, 0:512], start=(k == 0), stop=(k == 7))
                    z_ = zs[j % 2]
                    S.act(z_[:, :], bz.f(0, 512), AF.Silu)
                    o_ = og[j % 2]
                    S.dma(o_[:, :], self.o_own[j], g_og[j % 2])
                    for h in range(4):
                        S.act(t1[j % 2][:, h * 128:(h + 1) * 128], o_[:, h * 128:(h + 1) * 128], AF.Square,
                              accum=osq[:, h:h + 1])
                    S.act(osq[:, 4:8], osq[:, 0:4], AF.Sqrt, bias=EPS, scale=1.0 / 128.0)
                    S.recip(osq[:, 4:8], osq[:, 4:8])
                    rb = V(osq.t[:, 4:8].unsqueeze(2).broadcast_to([128, 4, 128]), [osq.b])
                    S.tt("dve", t1[j % 2][:, :].re("p (h d) -> p h d", h=4), o_[:, :].re("p (h d) -> p h d", h=4), rb,
                         ALU.mult)
                    S.tt("pool", t1[j % 2][:, :], t1[j % 2][:, :], gonw4[:, :], ALU.mult)
                    S.tt("dve", mg[j % 2][:, :], t1[j % 2][:, :], z_[:, :], ALU.mult)
                    bt = pb[0]
                    for c in range(4):
                        S.tr(bt.b(c * 128, (c + 1) * 128), mg[j % 2][:, c * 128:(c + 1) * 128], self.idb[:, :])
                    S.copy("act", mgT[j % 2][:, :, :].re("p c t -> p (c t)"), bt.b(0, 512))
                    S.dma(self.mix_d[:, 0:4, j * 128:(j + 1) * 128], mgT[j % 2][:, :, :], g_mg[j % 2])
            gh = S.group("p2halo", full=True)
            selAB = self.cload(es, "selAB", [128, 2], gh)
            xht = self.sb(es, "xht", [32, D])
            S.dma(xht[:, :], self.inp("xh", [32, D])[:, :], gh)
            ca = self.sb(es, "hca", [32, 512])
            cb = self.sb(es, "hcb", [32, 512])
            S.memset("pool", cb[0:2, :], 0.0)
            opv = self.oprev_d.rearrange("j i c -> (j i) c")
            S.dma(ca[:, :], opv[0:32, :], gh)
            S.dma(cb[2:32, :], opv[0:30, :], gh)
            hTh = self.sb(es, "hTh", [128, 8, 32], BF16)
            self.front32(F, xht, hTh, self.s1, 0)
            qnTh, gatesh = keep["qnTh"], keep["gatesh"]
            bq = pb[2]
            for h in range(4):
                for k in range(8):
                    S.mm(bq.f(h * 32, (h + 1) * 32), lhsT=winb2[:, k, 512 + h * 128:512 + (h + 1) * 128], rhs=hTh[:, k, :],
                         start=(k == 0), stop=(k == 7))
            S.copy("act", yf[0][:, 0:128], bq.f(0, 128))
            S.act(sq[0][:, 0:128], bq.f(0, 128), AF.Square)
            S.mm(pb[4].f(0, 128), lhsT=onesb[:, :], rhs=sq[0][:, 0:128])
            S.act(rtt[0][:, 0:128], pb[4].f(0, 128), AF.Sqrt, bias=EPS, scale=1.0 / 128.0)
            S.recip(rtt[0][:, 0:128], rtt[0][:, 0:128])
            S.stt(qnTh[:, :, :].re("p h q -> p (h q)"), yf[0][:, 0:128], qnw[:, 0:1], rtt[0][:, 0:128], ALU.mult, ALU.mult)
            for k in range(8):
                S.mm(pb[6].f(0, 12, 0, 32), lhsT=hTh[:, k, :], rhs=winb2[:, k, 1024:1036], start=(k == 0), stop=(k == 7))
            S.act(gatesh[:, :], pb[6].f(0, 12, 0, 32), AF.Sigmoid)
            for k in range(8):
                S.mm(pb[7].f(0, 512, 0, 32), lhsT=hTh[:, k, :], rhs=winb2[:, k, 0:512], start=(k == 0), stop=(k == 7))
            S.act(zs[0][0:32, :], pb[7].f(0, 512, 0, 32), AF.Silu)
            S.ts("dve", ca[:, :], ca[:, :], selAB[0:32, 0:1], None, ALU.mult)
            S.stt(ca[:, :], cb[:, :], selAB[0:32, 1:2], ca[:, :], ALU.mult, ALU.add)
            for h in range(4):
                S.act(t1[0][0:32, h * 128:(h + 1) * 128], ca[:, h * 128:(h + 1) * 128], AF.Square, accum=osq[0:32, h:h + 1])
            S.act(osq[0:32, 4:8], osq[0:32, 0:4], AF.Sqrt, bias=EPS, scale=1.0 / 128.0)
            S.recip(osq[0:32, 4:8], osq[0:32, 4:8])
            rb = V(osq.t[0:32, 4:8].unsqueeze(2).broadcast_to([32, 4, 128]), [osq.b])
            S.tt("dve", t1[0][0:32, :].re("p (h d) -> p h d", h=4), ca[:, :].re("p (h d) -> p h d", h=4), rb, ALU.mult)
            S.tt("pool", t1[0][0:32, :], t1[0][0:32, :], gonw4[0:32, :], ALU.mult)
            S.tt("dve", mg[0][0:32, :], t1[0][0:32, :], zs[0][0:32, :], ALU.mult)
            for c in range(4):
                S.tr(pb[0].b(c * 32, (c + 1) * 32), mg[0][0:32, c * 128:(c + 1) * 128], self.idb[0:32, 0:32])
            mghT = self.sb(es, "mghT", [128, 4, 32], BF16)
            S.copy("act", mghT[:, :, :].re("p c t -> p (c t)"), pb[0].b(0, 128))
            S.dma(self.mixh_d[:, 0:4, :], mghT[:, :, :], S.group("mghT"))
            S.flush()

    def phase3(self, keep):
        S = self.S
        pb = self.pb
        idb = self.idb
        qnT, gates = keep["qnT"], keep["gates"]
        SC = 128.0 ** -0.5
        with ExitStack() as es:
            gc_ = S.group("const3", full=True)
            kcmpT = self.sb(es, "kcmpT", [128, 512], BF16)
            vcx = self.sb(es, "vcx", [128, 4, 129], BF16)
            onesf = self.cload(es, "onesf3", [128, 128], gc_)
            onesb = self.sb(es, "onesb3", [128, 128], BF16)
            S.copy("dve", onesb[:, :], onesf[:, :])
            with ExitStack() as e2:
                g2 = S.group("const3b", full=True)
                kcw = self.cload(e2, "kcw", [128, 1], g2)
                pm511 = self.cload(e2, "pm511", [128, 1], g2)
                for which, src_d, w1n, w2n, posn in (("k", self.kcT_d, "cmp_k_w1", "cmp_k_w2", "cmp_k_posT"),
                                                    ("v", self.vcT_d, "cmp_v_w1", "cmp_v_w2", "cmp_v_posT")):
                    with ExitStack() as e3:
                        g3 = S.group("c3" + which, full=True)
                        xT = self.sb(e3, "cx" + which, [128, S_LEN], BF16)
                        S.dma(xT[:, 0:4096], src_d[:, 0:4096], g3)
                        S.dma(xT[:, 4096:8192], src_d[:, 4096:8192], g3)
                        w1d = self.inp(w1n, [128, 32, 128])
                        w1f = self.sb(e3, "w1f" + which, [128, 32, 128])
                        S.dma(w1f[:, 0:16, :], w1d[:, 0:16, :], g3)
                        S.dma(w1f[:, 16:32, :], w1d[:, 16:32, :], g3)
                        w1b = self.sb(e3, "w1b" + which, [128, 32, 128], BF16)
                        S.copy("dve", w1b[:, 0:16, :], w1f[:, 0:16, :])
                        S.copy("pool", w1b[:, 16:32, :], w1f[:, 16:32, :])
                        w2f = self.cload(e3, w2n, [128, 128], g3)
                        w2b = self.sb(e3, "w2b" + which, [128, 128], BF16)
                        S.copy("dve", w2b[:, :], w2f[:, :])
                        posf = self.cload(e3, posn, [128, 32], g3)
                        posb = self.sb(e3, "posb" + which, [128, 32], BF16)
                        S.copy("dve", posb[:, :], posf[:, :])
                        hid = self.sb(e3, "hid" + which, [128, 512], BF16)
                        bcol = self.sb(e3, "bcol" + which, [128, 1])
                        S.memset("pool", hid[:, :], 0.0)
                        bh, bb = pb[2], pb[3]
                        x3 = xT[:, :].re("p (n s) -> p n s", s=16)
                        for l in range(32):
                            S.mm(bh.f(0, 511), lhsT=w1b[:, l, :], rhs=x3[:, l // 16:l // 16 + 511, l % 16],
                                 start=(l == 0), stop=(l == 31))
                        for l in range(32):
                            S.mm(bb.f(0, 1), lhsT=w1b[:, l, :], rhs=posb[:, l:l + 1], start=(l == 0), stop=(l == 31))
                        S.copy("dve", bcol[:, :], bb.f(0, 1))
                        S.act(hid[:, 0:511], bh.f(0, 511), AF.Silu, bias=bcol[:, 0:1])
                        if which == "k":
                            bk = pb[4]
                            S.mm(bk.f(0, 512), lhsT=w2b[:, :], rhs=hid[:, :])
                            yk = self.sb(e3, "yk", [128, 512])
                            sqk = self.sb(e3, "sqk", [128, 512], BF16)
                            rk = self.sb(e3, "rk", [128, 512])
                            S.copy("act", yk[:, :], bk.f(0, 512))
                            S.act(sqk[:, :], bk.f(0, 512), AF.Square)
                            S.mm(pb[5].f(0, 512), lhsT=onesb[:, :], rhs=sqk[:, :])
                            S.act(rk[:, :], pb[5].f(0, 512), AF.Sqrt, bias=EPS, scale=1.0 / 128.0)
                            S.recip(rk[:, :], rk[:, :])
                            S.stt(kcmpT[:, :], yk[:, :], kcw[:, 0:1], rk[:, :], ALU.mult, ALU.mult)
                            S.memset("dve", kcmpT[:, 511:512], 0.0)
                        else:
                            bv = pb[4]
                            for nt in range(4):
                                S.mm(bv.f(nt * 128, (nt + 1) * 128), lhsT=hid[:, nt * 128:(nt + 1) * 128], rhs=w2b[:, :])
                            S.memset("pool", vcx[:, :, 128:129], 1.0)
                            S.copy("act", vcx[:, :, 0:128], bv.f(0, 512).re("p (n d) -> p n d", n=4))
                            S.ts("dve", vcx[:, 3, :], vcx[:, 3, :], pm511[:, 0:1], None, ALU.mult)
                        S.flush()
            if self.debug:
                S.dma(self.outp("d_kcmpT", [128, 512], BF16)[:, :], kcmpT[:, :], self.g_out)
                S.dma(self.outp("d_vcx", [128, 4, 129], BF16)[:, :, :], vcx[:, :, :], self.g_out)
            if self.stop == 31:
                S.flush()
                return
            kslT = self.sb(es, "kslT", [128, S_LEN], BF16)
            kwnT = self.sb(es, "kwnT", [128, S_LEN], BF16)
            vslx = self.sb(es, "vslx", [128, NB, 129], BF16)
            vwnx = self.sb(es, "vwnx", [128, NB, 129], BF16)
            for i in range(2):
                S.dma(kslT[:, i * 4096:(i + 1) * 4096], self.kslT_d[:, i * 4096:(i + 1) * 4096], gc_)
                S.dma(kwnT[:, i * 4096:(i + 1) * 4096], self.kwnT_d[:, i * 4096:(i + 1) * 4096], gc_)
                S.dma(vslx[:, i * 32:(i + 1) * 32, 0:128], self.vsl_d[:, i * 32:(i + 1) * 32, :], gc_)
                S.dma(vwnx[:, i * 32:(i + 1) * 32, 0:128], self.vwn_d[:, i * 32:(i + 1) * 32, :], gc_)
            S.memset("pool", vslx[:, :, 128:129], 1.0)
            S.memset("pool", vwnx[:, :, 128:129], 1.0)
            eall = self.sb(es, "eallb", [128, S_LEN], BF16)
            addm = self.cload(es, "addm", [128, 16, 128], gc_)
            ovb = self.sb(es, "ovb", [128, 4, 128], BF16)
            cmbb = self.sb(es, "cmbb", [128, 16, 2, 128], BF16)
            cbsb = self.sb(es, "cbsb", [128, 4, 4, 128], BF16)
            cbwb = self.sb(es, "cbwb", [128, 8, 4, 128], BF16)
            with ExitStack() as e2:
                g2 = S.group("const3c", full=True)
                ead = self.inp("eall", [128, S_LEN])
                est = [self.sb(e2, f"est{i}", [128, 2048]) for i in range(2)]
                g_e = [S.group(f"est{i}") for i in range(2)]
                for i in range(4):
                    S.dma(est[i % 2][:, :], ead[:, i * 2048:(i + 1) * 2048], g_e[i % 2])
                    S.copy("pool" if i % 2 else "dve", eall[:, i * 2048:(i + 1) * 2048], est[i % 2][:, :])
                ovf = self.cload(e2, "ovm", [128, 4, 128], g2)
                S.copy("dve", ovb[:, :, :], ovf[:, :, :])
                cmf = self.cload(e2, "cmb", [128, 16, 2, 128], g2)
                S.copy("pool", cmbb[:, :, :, :], cmf[:, :, :, :])
                cbsf = self.cload(e2, "cbs", [128, 4, 128], g2)
                cbwf = self.cload(e2, "cbw", [128, 8, 128], g2)
                for h in range(4):
                    S.copy("dve", cbsb[:, :, h, :], cbsf[:, :, :])
                    S.copy("pool", cbwb[:, :, h, :], cbwf[:, :, :])
                S.flush()
            Pc = [self.sb(es, f"Pc{i}", [128, 512], BF16) for i in range(4)]
            Pk = [self.sb(es, f"Pk{i}", [128, 512], BF16) for i in range(3)]
            cmrep = [self.sb(es, f"cmrep{i}", [128, 4, 128], BF16) for i in range(2)]
            ob = {nm: self.sb(es, "ob_" + nm, [128, 4, 129]) for nm in ("c", "s", "w")}
            rs = self.sb(es, "rs", [128, 12])
            cf = self.sb(es, "cf", [128, 12])
            imp = self.sb(es, "imp", [128, 128])
            imp2 = self.sb(es, "imp2", [128, 128])
            m8 = self.sb(es, "m8", [128, 16])
            selm = self.sb(es, "selm", [128, 128])
            biasT = self.sb(es, "biasT", [128, 4, 128], BF16)
            mixn = [self.sb(es, f"mixn{i}", [128, 4, 128], BF16) for i in range(2)]
            tmpn = self.sb(es, "tmpn", [128, 128])
            mnT = [self.sb(es, f"mnT{i}", [128, 4, 128], BF16) for i in range(2)]
            g_mn = [S.group(f"mnT{i}") for i in range(2)]
            bS = [pb[0], pb[1]]
            bO = [pb[2], pb[3]]
            bI, bT = pb[4], pb[5]
            si = [0]

            def nsa_block(Q, qv, cmp_plan, slc_plan, win_plan, addv, gatev, store):
                NQ = 4 * Q

                def scores(kT_tile, extra):
                    bank = bS[si[0] % 2]
                    out_ = Pk[si[0] % 3]
                    si[0] += 1
                    S.mm(bank.f(0, NQ), lhsT=kT_tile, rhs=qv, start=True, stop=(len(extra) == 0))
                    for i_, (l_, r_) in enumerate(extra):
                        S.mm(bank.f(0, NQ), lhsT=l_, rhs=r_, start=False, stop=(i_ == len(extra) - 1))
                    S.act(out_[:, 0:NQ], bank.f(0, NQ), AF.Exp, scale=SC)
                    return out_

                def pv(P_, vx, first, last):
                    for h in range(4):
                        S.mm_(bO[h // 2].f((h % 2) * 129, (h % 2) * 129 + 129, 0, Q), lhsT=P_[:, h * Q:(h + 1) * Q],
                              rhs=vx, start=(first and h % 2 == 0), stop=last, skip=True)

                def evac_o(dst):
                    for hh in range(2):
                        S.copy("act", dst[0:Q, 2 * hh:2 * hh + 2, :], bO[hh].f(0, 258, 0, Q).re("p (h c) -> p h c", h=2))

                ncp = len(cmp_plan)
                for i_, (nt, mk) in enumerate(cmp_plan):
                    bank = bS[i_ % 2]
                    S.mm(bank.f(0, NQ), lhsT=kcmpT[:, nt * 128:(nt + 1) * 128], rhs=qv, start=True, stop=(mk is None))
                    if mk is not None:
                        S.mm(bank.f(0, NQ), lhsT=idb[:, :], rhs=mk, start=False, stop=True)
                    S.act(Pc[i_][:, 0:NQ], bank.f(0, NQ), AF.Exp, scale=SC)
                for h in range(4):
                    for i_, (nt, mk) in enumerate(cmp_plan):
                        S.mm_(bO[h // 2].f((h % 2) * 129, (h % 2) * 129 + 129, 0, Q), lhsT=Pc[i_][:, h * Q:(h + 1) * Q],
                              rhs=vcx[:, nt, :], start=(i_ == 0 and h % 2 == 0), stop=(i_ == ncp - 1), skip=True)
                for h in range(4):
                    for i_, (nt, mk) in enumerate(cmp_plan):
                        S.mm_(bI.f(h * 128, (h + 1) * 128, 0, Q), lhsT=Pc[i_][:, h * Q:(h + 1) * Q], rhs=ovb[:, nt, :],
                              start=(i_ == 0 and h == 0), stop=(i_ == ncp - 1), skip=True)
                evac_o(ob["c"])
                S.ts("dve", rs[0:Q, 0:4], ob["c"][0:Q, :, 128], 1e-30, None, ALU.max)
                S.recip(rs[0:Q, 0:4], rs[0:Q, 0:4])
                S.ts("dve", imp[0:Q, :], bI.f(0, 128, 0, Q), rs[0:Q, 0:1], None, ALU.mult)
                for h in range(1, 4):
                    S.stt(imp[0:Q, :], bI.f(h * 128, (h + 1) * 128, 0, Q), rs[0:Q, h:h + 1], imp[0:Q, :], ALU.mult, ALU.add)
                S.tt("pool", imp[0:Q, :], imp[0:Q, :], addv, ALU.add)
                self.max8(m8[0:Q, 0:8], imp[0:Q, :])
                self.match_replace(imp2[0:Q, :], m8[0:Q, 0:8], imp[0:Q, :], -3.0e38)
                self.max8(m8[0:Q, 8:16], imp2[0:Q, :])
                S.ts("dve", selm[0:Q, :], imp[0:Q, :], m8[0:Q, 15:16], None, ALU.is_ge)
                S.mm(bT.f(0, Q), lhsT=selm[0:Q, :], rhs=self.idf[0:Q, 0:Q])
                S.ts("dve", bT2[:, 0:NQ].re("p (h q) -> p h q", h=4),
                     V(bT.t[:, 0:Q].unsqueeze(1).broadcast_to([128, 4, Q]), bT.q[0:1]), BIG, -BIG, ALU.mult, ALU.add)
                brhs = bT2[:, 0:NQ]
                prev = None
                for i_, (kt, extra) in enumerate(slc_plan):
                    ex = [(eall[:, kt * 128:(kt + 1) * 128], brhs)] + extra
                    P_ = scores(kslT[:, kt * 128:(kt + 1) * 128], ex)
                    if prev is not None:
                        pv(prev[0], vslx[:, prev[1], :], prev[2] == 0, False)
                    prev = (P_, kt, i_)
                pv(prev[0], vslx[:, prev[1], :], prev[2] == 0, True)
                evac_o(ob["s"])
                prev = None
                for i_, (kt, mk) in enumerate(win_plan):
                    P_ = scores(kwnT[:, kt * 128:(kt + 1) * 128], [(idb[:, :], mk)])
                    if prev is not None:
                        pv(prev[0], vwnx[:, prev[1], :], prev[2] == 0, False)
                    prev = (P_, kt, i_)
                pv(prev[0], vwnx[:, prev[1], :], prev[2] == 0, True)
                evac_o(ob["w"])
                S.ts("dve", rs[0:Q, 4:8], ob["s"][0:Q, :, 128], 1e-30, None, ALU.max)
                S.ts("dve", rs[0:Q, 8:12], ob["w"][0:Q, :, 128], 1e-30, None, ALU.max)
                S.recip(rs[0:Q, 4:12], rs[0:Q, 4:12])
                g3 = gatev.re("p (h g) -> p g h", g=3)
                S.tt("dve", cf[0:Q, :].re("p (g h) -> p g h", g=3), rs[0:Q, :].re("p (g h) -> p g h", g=3), g3, ALU.mult)
                mx = mixn[si[0] % 2]
                for h in range(4):
                    S.ts("pool", tmpn[0:Q, :], ob["c"][0:Q, h, 0:128], cf[0:Q, h:h + 1], 1.0, ALU.mult, ALU.mult)
                    S.stt(tmpn[0:Q, :], ob["s"][0:Q, h, 0:128], cf[0:Q, 4 + h:5 + h], tmpn[0:Q, :], ALU.mult, ALU.add)
                    S.stt(mx[0:Q, h, :], ob["w"][0:Q, h, 0:128], cf[0:Q, 8 + h:9 + h], tmpn[0:Q, :], ALU.mult, ALU.add)
                for h in range(4):
                    S.tr(bT.b(h * Q, (h + 1) * Q), mx[0:Q, h, :], idb[0:Q, 0:Q])
                store(bT.b(0, 4 * Q))

            bT2 = self.sb(es, "bT2", [128, 512], BF16)
            for j in range(16):
                qv = qnT[:, :, j * 128:(j + 1) * 128]
                nt_hi = min(3, (32 * j + 30) // 128)
                lo = max(0, 32 * j - 1) // 128
                cmp_plan = []
                for nt in range(nt_hi + 1):
                    mk = None
                    if nt >= lo:
                        slot = nt - lo
                        S.copy("pool", cmrep[slot][:, :, :],
                               V(cmbb.t[:, j, slot, :].unsqueeze(1).broadcast_to([128, 4, 128]), [cmbb.b]))
                        mk = cmrep[slot][:, :, :].re("p h q -> p (h q)")
                    cmp_plan.append((nt, mk))
                slc_plan = []
                for kt in range(4 * j + 4):
                    ex = []
                    if kt >= 4 * j:
                        ex.append((idb[:, :], cbsb[:, kt - 4 * j, :, :].re("p h q -> p (h q)")))
                    slc_plan.append((kt, ex))
                win_plan = [(4 * j - 4 + e, cbwb[:, e, :, :].re("p h q -> p (h q)")) for e in range(8) if 4 * j - 4 + e >= 0]

                def store(src, j=j):
                    S.copy("act", mnT[j % 2][:, :, :].re("p c t -> p (c t)"), src)
                    S.dma(self.mix_d[:, 4:8, j * 128:(j + 1) * 128], mnT[j % 2][:, :, :], g_mn[j % 2])

                nsa_block(128, qv, cmp_plan, slc_plan, win_plan, addm[:, j, :], gates[:, j, :], store)
            self.nsa_halo(es, nsa_block, keep, idb)
            S.flush()

    def nsa_halo(self, es, nsa_block, keep, idb):
        S = self.S
        with ExitStack() as e2:
            g2 = S.group("const3h", full=True)
            haddm = self.cload(e2, "haddm", [32, 128], g2)
            hcmb = self.sb(e2, "hcmb", [128, 4, 128], BF16)
            hsmb = self.sb(e2, "hsmb", [128, 64, 128], BF16)
            hwmb = self.sb(e2, "hwmb", [128, 64, 128], BF16)
            hcf = self.cload(e2, "hcm", [128, 4, 128], g2)
            S.copy("dve", hcmb[:, :, :], hcf[:, :, :])
            st = [self.sb(e2, f"hst{i}", [128, 16, 128]) for i in range(2)]
            g_s = [S.group(f"hst{i}") for i in range(2)]
            n = 0
            for nm, dst in (("hsm", hsmb), ("hwm", hwmb)):
                src = self.inp(nm, [128, 64, 128])
                for i in range(4):
                    S.dma(st[n % 2][:, :, :], src[:, i * 16:(i + 1) * 16, :], g_s[n % 2])
                    S.copy("pool" if n % 2 else "dve", dst[:, i * 16:(i + 1) * 16, :], st[n % 2][:, :, :])
                    n += 1
            mnTh = self.sb(e2, "mnTh", [128, 4, 32], BF16)
            g_m = S.group("mnTh")
            cmp_plan = [(nt, hcmb[:, nt, :]) for nt in range(4)]
            slc_plan = [(kt, [(idb[:, :], hsmb[:, kt, :])]) for kt in range(NB)]
            win_plan = [(kt, hwmb[:, kt, :]) for kt in range(NB)]

            def store(src):
                S.copy("act", mnTh[:, :, :].re("p c t -> p (c t)"), src)
                S.dma(self.mixh_d[:, 4:8, :], mnTh[:, :, :], g_m)

            nsa_block(32, keep["qnTh"][:, :, :], cmp_plan, slc_plan, win_plan, haddm[:, :], keep["gatesh"][:, :], store)
            S.flush()

    def max8(self, out, in_):
        o_, i_ = out.ap, in_.ap
        self.S.op("dve", lambda e: e.max(out=o_, in_=i_), r=self.S._bs(in_), w=self.S._bs(out))

    def match_replace(self, out, rep, vals, imm):
        o_, r_, v_ = out.ap, rep.ap, vals.ap
        self.S.op("dve", lambda e: e.match_replace(out=o_, in_to_replace=r_, in_values=v_, imm_value=imm),
                  r=self.S._bs(rep, vals), w=self.S._bs(out))

    def phase45(self):
        S = self.S
        pb = self.pb
        idb = self.idb
        w_out = self.inp("w_out", [D, D])
        w_up = self.inp("w_up", [D, 2 * DFF])
        w_dn = self.inp("w_dn", [DFF, D])
        wup_d = self.scratch("wup_d", [128, 44, 8, 128], BF16)
        with ExitStack() as e1:
            full = self.sb(e1, "wupfull", [128, 44, 8, 128], BF16)
            stg = [self.sb(e1, f"wupst{i}", [128, 2 * DFF]) for i in range(2)]
            g_s = [S.group(f"wupst{i}") for i in range(2)]
            g_f = S.group("wupfull")
            for k in range(8):
                st = stg[k % 2]
                for i in range(2):
                    S.dma(st[:, i * DFF:(i + 1) * DFF], w_up[k * 128:(k + 1) * 128, i * DFF:(i + 1) * DFF], g_s[k % 2])
                s3 = st[:, :].re("p (c n) -> p c n", n=128)
                S.copy("dve", full[:, 0:16, k, :], s3[:, 0:16, :])
                S.copy("pool", full[:, 16:30, k, :], s3[:, 16:30, :])
                S.copy("act", full[:, 30:44, k, :], s3[:, 30:44, :])
            for i in range(4):
                S.dma(wup_d[:, i * 11:(i + 1) * 11, :, :], full[:, i * 11:(i + 1) * 11, :, :], g_f)
            S.flush()
        with ExitStack() as es:
            gc_ = S.group("const4", full=True)
            fcw = self.cload(es, "fcw", [128, 44 * 3], gc_)
            fcb = self.cload(es, "fcb", [128, 44], gc_)
            wdnb = self.sb(es, "wdnb", [128, 22, D], BF16)
            woutb = self.sb(es, "woutb", [128, 8, D], BF16)
            g1row = self.sb(es, "g1row", [128, D])
            g2row = self.sb(es, "g2row", [128, D])
            with ExitStack() as e2:
                stg = [self.sb(e2, f"wst4_{i}", [128, 2, D]) for i in range(2)]
                g_s = [S.group(f"wst4_{i}") for i in range(2)]
                n = 0
                for (src, dst, nch) in ((w_dn, wdnb, 22), (w_out, woutb, 8)):
                    sv = src.rearrange("(c p) d -> p c d", p=128)
                    for c0 in range(0, nch, 2):
                        st = stg[n % 2]
                        S.dma(st[:, :, :], sv[:, c0:c0 + 2, :], g_s[n % 2])
                        S.copy(("dve", "pool", "act")[n % 3], dst[:, c0:c0 + 2, :], st[:, :, :])
                        n += 1
                gb = self.sb(e2, "gbc", [128, 128])
                for gi, (col0, dst) in enumerate(((16, g1row), (40, g2row))):
                    for cc in range(8):
                        S.copy("dve", gb[:, :], V(self.modT.t[:, col0 + cc:col0 + cc + 1].broadcast_to([128, 128]),
                                                  [self.modT.b]))
                        bank = pb[cc % 2]
                        S.mm(bank.f(0, 128), lhsT=gb[:, :], rhs=self.idf[:, :])
                        S.copy("act", dst[:, cc * 128:(cc + 1) * 128], bank.f(0, 128))
                S.flush()
            F = dict(ssq=self.sb(es, "f4_ssq", [128, 4]), rt=self.sb(es, "f4_rt", [128, 4]),
                     rstd=self.sb(es, "f4_rstd", [128, 4]), junk=self.sb(es, "f4_junk", [128, 1024], BF16),
                     xs=self.sb(es, "f4_xs", [128, 2, 1024], BF16))
            x1 = [self.sb(es, f"x1_{i}", [128, 4, D]) for i in range(2)]
            g_x = [S.group(f"x1_{i}") for i in range(2)]
            mixt = self.sb(es, "mixt", [128, 8, 512], BF16)
            g_m = S.group("mixt")
            h2T = self.sb(es, "h2T", [128, 8, 512], BF16)
            gT = self.sb(es, "gT", [128, 22, 512], BF16)
            wch = [self.sb(es, f"wch{i}", [128, 2, 8, 128], BF16) for i in range(3)]
            g_wc = [S.group(f"wch{i}") for i in range(3)]
            upa = [self.sb(es, f"upa{i}", [128, 4, 130]) for i in range(2)]
            upb = [self.sb(es, f"upb{i}", [128, 4, 130]) for i in range(2)]
            for u_ in upa + upb:
                S.memset("pool", u_[:, :, :], 0.0)
            aa = [self.sb(es, f"aa{i}", [128, 4, 128]) for i in range(2)]
            ab_ = [self.sb(es, f"ab_{i}", [128, 4, 128]) for i in range(2)]
            tmp = [self.sb(es, f"tmp4_{i}", [128, 512]) for i in range(2)]
            xrows = self.xo.rearrange("(t b p) d -> t p b d", b=4, p=128)
            orows = self.out.rearrange("(t b p) d -> t p b d", b=4, p=128)
            wi = 0
            gh = S.group("p4halo", full=True)
            hexr = self.cload(es, "hexr", [128, 32], gh)
            x1h = self.sb(es, "x1h", [32, D])
            mixh = self.sb(es, "mixh", [128, 8, 32], BF16)
            S.dma(x1h[:, :], self.din["xh"][:, :], gh)
            S.dma(mixh[:, :, :], self.mixh_d[:, :, :], gh)
            h2Th = self.sb(es, "h2Th", [128, 8, 32], BF16)
            for half in range(2):
                bank = pb[2 + half]
                hs = slice(half * 512, (half + 1) * 512)
                for c in range(8):
                    S.mm(bank.f(0, 512, 0, 32), lhsT=mixh[:, c, :], rhs=woutb[:, c, hs], start=(c == 0), stop=(c == 7))
                S.tt("dve", tmp[half][0:32, :], bank.f(0, 512, 0, 32), g1row[0:32, hs], ALU.mult)
                S.tt("pool", x1h[:, hs], x1h[:, hs], tmp[half][0:32, :], ALU.add)
            self.front32(F, x1h, h2Th, self.s2, 24)

            def conv(u_, c, dst):
                S.ts("pool", dst[:, :, :], u_[:, :, 2:130], fcw[:, c * 3 + 2:c * 3 + 3], fcb[:, c:c + 1], ALU.mult, ALU.add)
                S.stt(dst[:, :, :], u_[:, :, 1:129], fcw[:, c * 3 + 1:c * 3 + 2], dst[:, :, :], ALU.mult, ALU.add)
                S.stt(dst[:, :, :], u_[:, :, 0:128], fcw[:, c * 3:c * 3 + 1], dst[:, :, :], ALU.mult, ALU.add)

            for t in range(4):
                xx = x1[t % 2]
                S.dma(xx[:, :, :], xrows[t], g_x[t % 2])
                S.dma(mixt[:, :, :], self.mix_d[:, :, t * 512:(t + 1) * 512], g_m)
                for blk in range(4):
                    cs = slice(blk * 128, (blk + 1) * 128)
                    for half in range(2):
                        bank = pb[2 + half]
                        hs = slice(half * 512, (half + 1) * 512)
                        for c in range(8):
                            S.mm(bank.f(0, 512), lhsT=mixt[:, c, cs], rhs=woutb[:, c, hs], start=(c == 0), stop=(c == 7))
                        tm = tmp[half]
                        S.tt("dve", tm[:, :], bank.f(0, 512), g1row[:, hs], ALU.mult)
                        S.tt("pool", xx[:, blk, hs], xx[:, blk, hs], tm[:, :], ALU.add)
                if self.debug:
                    S.dma(self.dx1[t], xx[:, :, :], self.g_out)
                for hf in range(2):
                    self.front(F, xx, 2, h2T, hf * 256, self.s2, 24, b0=2 * hf)
                for c in range(22):
                    w_ = wch[wi % 3]
                    S.dma(w_[:, 0, :, :], wup_d[:, c, :, :], g_wc[wi % 3])
                    S.dma(w_[:, 1, :, :], wup_d[:, 22 + c, :, :], g_wc[wi % 3])
                    wi += 1
                    ua, ub = upa[c % 2], upb[c % 2]
                    for i_, (u_, bank) in enumerate(((ua, pb[4]), (ub, pb[5]))):
                        for k in range(8):
                            S.mm(bank.f(0, 512), lhsT=w_[:, i_, k, :], rhs=h2T[:, k, :], start=(k == 0), stop=(k == 7))
                        S.copy("act", u_[:, :, 2:130], bank.f(0, 512).re("p (b t) -> p b t", b=4))
                        bh_ = pb[6 + i_]
                        for k in range(8):
                            S.mm(bh_.f(0, 8), lhsT=w_[:, i_, k, :], rhs=h2Th[:, k, 8 * t:8 * t + 8], start=(k == 0),
                                 stop=(k == 7))
                        S.tt("dve", u_[:, :, 0:2], bh_.f(0, 8).re("p (b i) -> p b i", i=2),
                             hexr[:, 8 * t:8 * t + 8].re("p (b i) -> p b i", i=2), ALU.mult)
                    conv(ua, c, aa[c % 2])
                    conv(ub, 22 + c, ab_[c % 2])
                    S.act(aa[c % 2][:, :, :], aa[c % 2][:, :, :], AF.Silu)
                    S.tt("dve", gT[:, c, :].re("p (b t) -> p b t", b=4), aa[c % 2][:, :, :], ab_[c % 2][:, :, :], ALU.mult)
                for blk in range(4):
                    cs = slice(blk * 128, (blk + 1) * 128)
                    for half in range(2):
                        bank = pb[6 + half]
                        hs = slice(half * 512, (half + 1) * 512)
                        for c in range(22):
                            S.mm(bank.f(0, 512), lhsT=gT[:, c, cs], rhs=wdnb[:, c, hs], start=(c == 0), stop=(c == 21))
                        tm = tmp[half]
                        S.tt("dve", tm[:, :], bank.f(0, 512), g2row[:, hs], ALU.mult)
                        S.tt("pool", xx[:, blk, hs], xx[:, blk, hs], tm[:, :], ALU.add)
                S.dma(orows[t], xx[:, :, :], g_x[t % 2])
            S.flush()


def _colL(v, n):
    return np.ascontiguousarray(np.asarray(v, np.float32).reshape(n, 128).T)


def _rep(v, n=128):
    v = np.asarray(v, np.float32).reshape(1, -1)
    return np.ascontiguousarray(np.repeat(v, n, axis=0))


def _consts():
    p = np.arange(128)
    same = (p[:, None] // 64) == (p[None, :] // 64)
    c = {}
    c["identf"] = np.eye(128, dtype=np.float32)
    c["tri2"] = (same & (p[:, None] <= p[None, :])).astype(np.float32)
    c["blk2"] = same.astype(np.float32)
    c["onesf"] = np.ones((128, 128), np.float32)
    c["cind"] = np.stack([(p < 64), (p >= 64)], axis=1).astype(np.float32)
    ma = np.where(same & (p[None, :] < p[:, None]), 0.0, BIG).astype(np.float32)
    mb = np.where(same & (p[None, :] >= p[:, None]), 0.0, -BIG).astype(np.float32)
    c["ma4"] = np.ascontiguousarray(np.tile(ma, (1, 4)))
    c["mb4"] = np.ascontiguousarray(np.tile(mb, (1, 4)))
    return c


def _host_inputs(inputs):
    x = np.asarray(inputs["x"], np.float32)
    cst = _consts()
    g = lambda k: np.asarray(inputs[k][0], np.float32)
    gcw = g("gdn_conv_w")
    gcwT = np.ascontiguousarray(gcw.reshape(4, 12, 128).transpose(2, 1, 0).reshape(128, 48))
    shared = {
        "ada_w": np.ascontiguousarray(g("ada_w")),
        "ada_bT": _colL(g("ada_b"), 48),
        "n1w": _colL(g("norm1_w"), 8),
        "n2w": _colL(g("norm2_w"), 8),
        "w_in": np.ascontiguousarray(g("w_in")),
        "gcw": gcwT,
        "dtb": _rep(g("gdn_dt_bias")),
        "alog": _rep(g("gdn_A_log")),
        "kslw": _colL(g("nsa_k_norm_slc"), 1),
        "kwnw": _colL(g("nsa_k_norm_win"), 1),
        "qnw": _colL(g("nsa_q_norm_w"), 1),
        "gonw4": _rep(np.tile(g("gdn_out_norm_w"), 4)),
        "onesf2": np.ones((128, 128), np.float32),
    }
    for nm in ("k", "v"):
        shared[f"cmp_{nm}_w1"] = np.ascontiguousarray(g(f"cmp_{nm}_w1").reshape(32, 128, 128).transpose(1, 0, 2))
        shared[f"cmp_{nm}_w2"] = np.ascontiguousarray(g(f"cmp_{nm}_w2"))
        shared[f"cmp_{nm}_posT"] = np.ascontiguousarray(g(f"cmp_{nm}_pos").T)
    shared["kcw"] = _colL(g("nsa_k_norm_cmp"), 1)
    shared["onesf3"] = np.ones((128, 128), np.float32)
    pm = np.ones((128, 1), np.float32)
    pm[127, 0] = 0.0
    shared["pm511"] = pm
    keys = np.arange(S_LEN)
    shared["eall"] = (keys[None, :] // 64 == np.arange(128)[:, None]).astype(np.float32)
    n = np.arange(512)
    js = np.arange(128)
    ov = np.minimum(16 * n[:, None] + 32, 64 * js[None, :] + 64) - np.maximum(16 * n[:, None], 64 * js[None, :])
    ov = np.clip(ov, 0, None).astype(np.float32) / 32.0
    ov[511] = 0.0
    shared["ovm"] = np.ascontiguousarray(ov.reshape(4, 128, 128).transpose(1, 0, 2))
    fw = g("ffn_conv_w")
    shared["fcw"] = np.ascontiguousarray(fw.reshape(3, 44, 128).transpose(2, 1, 0).reshape(128, 132))
    shared["fcb"] = _colL(g("ffn_conv_b"), 44)
    shared["w_out"] = np.ascontiguousarray(g("w_out"))
    shared["w_up"] = np.ascontiguousarray(g("ffn_w_up"))
    shared["w_dn"] = np.ascontiguousarray(g("ffn_w_down"))
    shared.update(cst)
    maps = []
    for core in range(8):
        b, r = core // 4, core % 4
        xo = np.concatenate([x[b, 128 * (4 * j + r):128 * (4 * j + r) + 128] for j in range(16)], axis=0)
        m = dict(shared)
        m["xb"] = np.ascontiguousarray(x[b])
        m["xo"] = np.ascontiguousarray(xo)
        m["cT"] = _colL(inputs["c"][b], 8)
        sel = np.zeros((128, 4), np.float32)
        sel[:, r] = 1.0
        m["sel"] = sel
        p = np.arange(128)
        q = np.arange(128)
        addm = np.zeros((128, 16, 128), np.float32)
        cmb = np.zeros((128, 16, 2, 128), np.float32)
        for j in range(16):
            qi = 4 * j + r
            tq = 128 * qi + q
            cur = tq // 64
            jj = np.arange(128)[None, :]
            valid = jj <= cur[:, None]
            forced = (jj == 0) | (jj == cur[:, None]) | (jj == cur[:, None] - 1)
            addm[:, j, :] = np.where(valid, np.where(forced, 1.0e4, 0.0), -1.0e30)
            lo = max(0, 32 * j - 1) // 128
            for slot in range(2):
                nn = 128 * (lo + slot) + p
                ok = (16 * nn[:, None] + 31) <= tq[None, :]
                cmb[:, j, slot, :] = np.where(ok, 0.0, -BIG)
        m["addm"] = addm
        m["cmb"] = cmb
        cbs = np.zeros((128, 4, 128), np.float32)
        for d in range(4):
            ok = (128 * (d - r) + p[:, None]) <= q[None, :]
            cbs[:, d, :] = np.where(ok, 0.0, -BIG)
        m["cbs"] = cbs
        cbw = np.zeros((128, 8, 128), np.float32)
        for e in range(8):
            rel = 128 * (e - 4 - r) + p[:, None]
            ok = (rel <= q[None, :]) & (rel > q[None, :] - 512)
            cbw[:, e, :] = np.where(ok, 0.0, -BIG)
        m["cbw"] = cbw
        hs_ = np.zeros((128, 4), np.float32)
        hs_[:, (r - 1) % 4] = 1.0
        m["hsel"] = hs_
        sab = np.zeros((128, 2), np.float32)
        sab[:, 0] = 1.0 if r >= 1 else 0.0
        sab[:, 1] = 1.0 if r == 0 else 0.0
        m["selAB"] = sab
        tq = np.array([128 * (4 * j + r) - 2 + i for j in range(16) for i in range(2)])
        ex = tq >= 0
        xh = np.zeros((32, D), np.float32)
        xh[ex] = x[b, tq[ex]]
        m["xh"] = xh
        m["hexr"] = _rep(ex.astype(np.float32))
        cur = tq // 64
        jj = np.arange(128)[None, :]
        valid = (jj <= cur[:, None]) & ex[:, None]
        forced = (jj == 0) | (jj == cur[:, None]) | (jj == cur[:, None] - 1)
        m["haddm"] = np.where(valid, np.where(forced, 1.0e4, 0.0), -1.0e30).astype(np.float32)
        nn = np.arange(512)
        okc = ((16 * nn[:, None] + 31) <= tq[None, :]) & ex[None, :]
        hcm = np.where(okc, 0.0, -BIG).astype(np.float32).reshape(4, 128, 1, 32)
        m["hcm"] = np.ascontiguousarray(np.broadcast_to(hcm, (4, 128, 4, 32)).transpose(1, 0, 2, 3).reshape(128, 4, 128))
        pos = np.arange(S_LEN)
        oks = (pos[:, None] <= tq[None, :]) & ex[None, :]
        okw = oks & (pos[:, None] > tq[None, :] - 512)
        for nm, ok in (("hsm", oks), ("hwm", okw)):
            a = np.where(ok, 0.0, -BIG).astype(np.float32).reshape(64, 128, 1, 32)
            m[nm] = np.ascontiguousarray(np.broadcast_to(a, (64, 128, 4, 32)).transpose(1, 0, 2, 3).reshape(128, 64, 128))
        maps.append(m)
    return maps


def run(inputs, debug=False, upto=9, ntiles=NT):
    bld = Builder(debug, ntiles)
    nc = bld.build(upto)
    maps = _host_inputs(inputs)
    maps = [{k: v for k, v in m.items() if k in bld.din} for m in maps]
    missing = [k for k in bld.din if k not in maps[0]]
    assert not missing, missing
    res = run_bass_kernel_spmd(nc, maps, core_ids=list(range(8)))
    return res.results


def kernel(**inputs):
    results = run(inputs)
    outp = np.zeros((2, S_LEN, D), np.float32)
    for core in range(8):
        b, r = core // 4, core % 4
        o = results[core]["out"]
        for j in range(16):
            qi = 4 * j + r
            outp[b, 128 * qi:128 * qi + 128] = o[128 * j:128 * j + 128]
    return outp
```

```python
import numpy as np
from contextlib import ExitStack
import concourse.bass as bass
import concourse.mybir as mybir
from concourse.bass_utils import run_bass_kernel_spmd

F32 = mybir.dt.float32
BF16 = mybir.dt.bfloat16
AF = mybir.ActivationFunctionType
ALU = mybir.AluOpType

D = 1024
S_LEN = 8192
NT = 16
NB = 64
N_IN = 3348
DFF = 2816
EPS = 1e-6
O_GQ, O_GK, O_GV, O_GZ, O_GA, O_GB, O_NQ, O_KC, O_VC, O_KSL, O_VSL, O_KWN, O_VWN, O_NG = (
    0, 512, 1024, 1536, 2048, 2052, 2056, 2568, 2696, 2824, 2952, 3080, 3208, 3336)
BIG = 30000.0


class Buf:
    __slots__ = ("name", "w", "rs", "const", "excl")

    def __init__(self, name, const=False, excl=False):
        self.name = name
        self.w = None
        self.rs = []
        self.const = const
        self.excl = excl


class V:
    __slots__ = ("ap", "bs")

    def __init__(self, ap, bs):
        self.ap = ap
        self.bs = bs if isinstance(bs, (list, tuple)) else [bs]

    def bitcast(self, dt):
        return V(self.ap.bitcast(dt), self.bs)

    def re(self, pat, **kw):
        return V(self.ap.rearrange(pat, **kw), self.bs)

    def bc(self, shape):
        return V(self.ap.broadcast_to(shape), self.bs)

    def __getitem__(self, k):
        return V(self.ap[k], self.bs)


class Tl:
    def __init__(self, t, b):
        self.t = t
        self.b = b

    def __getitem__(self, k):
        return V(self.t[k], self.b)


class Op:
    __slots__ = ("eng", "fn", "deps", "dmaw", "signal", "sigval", "grp")


class DGroup:
    def __init__(self, name, sem, full=False):
        self.name = name
        self.sem = sem
        self.count = 0
        self.full = full


ENGS = ("pe", "act", "dve", "pool", "sp")


class Sched:
    def __init__(self, nc, es):
        self.nc = nc
        self.es = es
        self.eng = {"pe": nc.tensor, "act": nc.scalar, "dve": nc.vector, "pool": nc.gpsimd, "sp": nc.sync}
        self.sem = {e: es.enter_context(nc.semaphore("s_" + e)) for e in ENGS}
        self.cnt = {e: 0 for e in ENGS}
        self.seen = {e: {} for e in ENGS}
        self.ops = []
        self.bufs = []
        self.groups = []
        self.nins = 0

    def buf(self, name, const=False, excl=False):
        b = Buf(name, const, excl)
        self.bufs.append(b)
        return b

    def group(self, name, full=False):
        g = DGroup(name, self.es.enter_context(self.nc.semaphore("g_" + name)), full)
        self.groups.append(g)
        return g

    def op(self, eng, fn, r=(), w=(), grp=None):
        o = Op()
        o.eng = eng
        o.fn = fn
        o.signal = False
        o.sigval = None
        o.grp = grp
        deps = []
        for b in r:
            if b.w is not None:
                deps.append(b.w)
            if b.excl:
                deps.extend(x for x in b.rs if x.eng != eng)
        for b in w:
            if b.w is not None:
                deps.append(b.w)
            deps.extend(b.rs)
        seen = set()
        dd = []
        for d in deps:
            if id(d) in seen or d is o:
                continue
            seen.add(id(d))
            if d.eng == "pe" and eng == "pe":
                continue
            if grp is not None and grp.full and d.grp is grp:
                continue
            dd.append(d)
        o.deps = dd
        o.dmaw = {}
        for d in dd:
            if d.grp is None:
                d.signal = True
            else:
                o.dmaw[d.grp.name] = d.grp.count
        if grp is not None:
            grp.count += 1
        for b in w:
            b.w = o
            b.rs = []
        for b in r:
            if not b.const and b.w is not o:
                b.rs.append(o)
        self.ops.append(o)
        return o

    def flush(self, barrier=True):
        if barrier:
            last = {}
            for o in self.ops:
                last[o.eng] = o
            for e, o in last.items():
                if o.grp is None:
                    o.signal = True
        for o in self.ops:
            e = self.eng[o.eng]
            for d in o.deps:
                if d.grp is not None:
                    key = "g_" + d.grp.name
                    val = 16 * (d.grp.count if d.grp.full else o.dmaw[d.grp.name])
                    sem = d.grp.sem
                else:
                    key = d.eng
                    val = d.sigval
                    sem = self.sem[d.eng]
                    assert val is not None, (o.eng, d.eng)
                if self.seen[o.eng].get(key, 0) >= val:
                    continue
                e.wait_ge(sem, val)
                self.seen[o.eng][key] = val
            ins = o.fn(e)
            self.nins += 1
            if o.grp is not None:
                ins.then_inc(o.grp.sem, 16)
            elif o.signal:
                self.cnt[o.eng] += 1
                o.sigval = self.cnt[o.eng]
                ins.then_inc(self.sem[o.eng], 1)
        self.ops = []
        if barrier:
            for en in ENGS:
                e = self.eng[en]
                for e2 in ENGS:
                    if e2 == en or e2 == "sp":
                        continue
                    if self.cnt[e2] > self.seen[en].get(e2, 0):
                        e.wait_ge(self.sem[e2], self.cnt[e2])
                        self.seen[en][e2] = self.cnt[e2]
                for g in self.groups:
                    key = "g_" + g.name
                    if 16 * g.count > self.seen[en].get(key, 0):
                        e.wait_ge(g.sem, 16 * g.count)
                        self.seen[en][key] = 16 * g.count
            for b in self.bufs:
                b.w = None
                b.rs = []

    @staticmethod
    def _bs(*vs):
        out = []
        for v in vs:
            if isinstance(v, V):
                for b in v.bs:
                    if b not in out:
                        out.append(b)
        return out

    @staticmethod
    def _a(v):
        return v.ap if isinstance(v, V) else v

    def mm(self, out, lhsT, rhs, start=True, stop=True):
        o_, l_, r_ = out.ap, lhsT.ap, rhs.ap
        self.op("pe", lambda e: e.matmul(o_, lhsT=l_, rhs=r_, start=start, stop=stop),
                r=self._bs(lhsT, rhs), w=self._bs(out))

    def mm_(self, out, lhsT, rhs, start=True, stop=True, skip=False):
        o_, l_, r_ = out.ap, lhsT.ap, rhs.ap
        self.op("pe", lambda e: e.matmul(o_, lhsT=l_, rhs=r_, start=start, stop=stop, skip_group_check=skip),
                r=self._bs(lhsT, rhs), w=self._bs(out))

    def tr(self, out, in_, ident):
        o_, i_, d_ = out.ap, in_.ap, ident.ap
        self.op("pe", lambda e: e.transpose(o_, i_, d_), r=self._bs(in_, ident), w=self._bs(out))

    def act(self, out, in_, func, bias=None, scale=None, accum=None, eng="act"):
        kw = {}
        if bias is not None:
            kw["bias"] = self._a(bias)
        if scale is not None:
            kw["scale"] = self._a(scale)
        if accum is not None:
            kw["accum_out"] = accum.ap
        o_, i_ = out.ap, in_.ap
        self.op("act", lambda e: e.activation(out=o_, in_=i_, func=func, **kw),
                r=self._bs(in_, bias, scale), w=self._bs(out, accum))

    def tt(self, eng, out, in0, in1, op):
        o_, a_, b_ = out.ap, in0.ap, in1.ap
        self.op(eng, lambda e: e.tensor_tensor(out=o_, in0=a_, in1=b_, op=op),
                r=self._bs(in0, in1), w=self._bs(out))

    def ts(self, eng, out, in0, s1, s2=None, op0=ALU.mult, op1=None):
        o_, a_ = out.ap, in0.ap
        s1_, s2_ = self._a(s1), self._a(s2)
        kw = {}
        if op1 is not None:
            kw["op1"] = op1
        self.op(eng, lambda e: e.tensor_scalar(out=o_, in0=a_, scalar1=s1_, scalar2=s2_, op0=op0, **kw),
                r=self._bs(in0, s1, s2), w=self._bs(out))

    def stt(self, out, in0, scalar, in1, op0, op1):
        o_, a_, b_ = out.ap, in0.ap, in1.ap
        s_ = self._a(scalar)
        self.op("dve", lambda e: e.scalar_tensor_tensor(out=o_, in0=a_, scalar=s_, in1=b_, op0=op0, op1=op1),
                r=self._bs(in0, scalar, in1), w=self._bs(out))

    def copy(self, eng, out, in_):
        o_, i_ = out.ap, in_.ap
        if eng == "act":
            self.op("act", lambda e: e.copy(out=o_, in_=i_), r=self._bs(in_), w=self._bs(out))
        else:
            self.op(eng, lambda e: e.tensor_copy(out=o_, in_=i_), r=self._bs(in_), w=self._bs(out))

    def recip(self, out, in_):
        o_, i_ = out.ap, in_.ap
        self.op("dve", lambda e: e.reciprocal(out=o_, in_=i_), r=self._bs(in_), w=self._bs(out))

    def memset(self, eng, out, val):
        o_ = out.ap
        self.op(eng, lambda e: e.memset(o_, val), r=[], w=self._bs(out))

    def dma(self, out, in_, grp, eng="sp"):
        o_, i_ = self._a(out), self._a(in_)
        self.op(eng, lambda e: e.dma_start(out=o_, in_=i_), r=self._bs(in_), w=self._bs(out), grp=grp)


class Bank:
    def __init__(self, t, q):
        self.t = t
        self.q = q

    def f(self, c0, c1, p0=0, p1=128):
        return V(self.t[p0:p1, c0:c1], self.q[0:1])

    def b(self, c0, c1, p0=0, p1=128):
        return V(self.t[p0:p1, :].bitcast(BF16)[:, c0:c1], self.q[0:1])


W1_SEGS = ((0, 1536, 0), (2048, 2056, 1536), (2568, 3336, 1544))
W1_N = 2312
C_AB, C_KC, C_VC, C_KSL, C_VSL, C_KWN, C_VWN = 1536, 1544, 1672, 1800, 1928, 2056, 2184


class Builder:
    def __init__(self, debug=False, ntiles=NT):
        self.debug = debug
        self.ntiles = ntiles
        import os
        self.stop = int(os.environ.get('K_STOP', '0'))
        self.var = int(os.environ.get('K_VAR', '0'))
        self.halo = int(os.environ.get('K_HALO', '0'))
        self.nc = bass.Bass("TRN2", target_bir_lowering=False)
        self.din = {}
        self.dout = {}

    def inp(self, name, shape, dt=F32):
        self.din[name] = self.nc.dram_tensor(name, list(shape), dt, kind="ExternalInput").ap()
        return self.din[name]

    def outp(self, name, shape, dt=F32):
        self.dout[name] = self.nc.dram_tensor(name, list(shape), dt, kind="ExternalOutput").ap()
        return self.dout[name]

    def scratch(self, name, shape, dt=F32):
        if self.debug:
            return self.outp(name, shape, dt)
        return self.nc.dram_tensor(name, list(shape), dt, kind="Internal").ap()

    def sb(self, es, name, shape, dt=F32, const=False):
        t = es.enter_context(self.nc.sbuf_tensor(name, list(shape), dt))
        return Tl(t, self.S.buf(name, const))

    def cload(self, es, name, shape, grp):
        d = self.inp(name, shape)
        t = self.sb(es, "c_" + name, shape, F32, const=True)
        idx = tuple(slice(None) for _ in shape)
        self.S.dma(t[idx], d[idx], grp)
        return t

    def build(self, upto=9):
        nc = self.nc
        I = self.inp
        self.xb = I("xb", [S_LEN, D])
        self.xo = I("xo", [2048, D])
        cT = I("cT", [128, 8])
        ada_w = I("ada_w", [D, 6 * D])
        self.w_in = I("w_in", [D, N_IN])
        self.out = self.outp("out", [2048, D])
        self.o_own = self.scratch("o_own", [16, 128, 512])
        self.kslT_d = self.scratch("kslT_d", [128, S_LEN], BF16)
        self.kwnT_d = self.scratch("kwnT_d", [128, S_LEN], BF16)
        self.kcT_d = self.scratch("kcT_d", [128, S_LEN], BF16)
        self.vcT_d = self.scratch("vcT_d", [128, S_LEN], BF16)
        self.vsl_d = self.scratch("vsl_d", [128, NB, 128], BF16)
        self.vwn_d = self.scratch("vwn_d", [128, NB, 128], BF16)
        self.mix_d = self.scratch("mix_d", [128, 8, 2048], BF16)
        self.oprev_d = self.scratch("oprev_d", [16, 2, 512])
        self.mixh_d = self.scratch("mixh_d", [128, 8, 32], BF16)

        with ExitStack() as top:
            S = self.S = Sched(nc, top)
            self.g_const = g_const = S.group("const", full=True)
            self.g_out = S.group("outw")
            self.idf = idf = self.cload(top, "identf", [128, 128], g_const)
            self.idb = idb = self.sb(top, "idb", [128, 128], BF16)
            S.copy("dve", idb[:, :], idf[:, :])
            self.modT = modT = self.sb(top, "modT", [128, 48])
            self.s1 = s1 = self.sb(top, "s1", [128, 8])
            self.s2 = s2 = self.sb(top, "s2", [128, 8])
            n1 = self.cload(top, "n1w", [128, 8], g_const)
            n2 = self.cload(top, "n2w", [128, 8], g_const)
            self.pb = []
            for i in range(8):
                t = top.enter_context(nc.psum_tensor(f"pb{i}", [128, 512], F32))
                self.pb.append(Bank(t, [S.buf(f"pb{i}", excl=True)]))
            pb = self.pb
            self.bankA = [pb[2], pb[3]]
            self.bankB = [pb[4], pb[5]]
            self.bankC = [pb[0], pb[1]]

            with ExitStack() as es:
                ct = self.sb(es, "ct", [128, 8])
                sc = self.sb(es, "sc", [128, 8])
                abT = self.cload(es, "ada_bT", [128, 48], g_const)
                S.dma(ct[:, :], cT[:, :], g_const)
                S.act(sc[:, :], ct[:, :], AF.Silu)
                aw = [self.sb(es, f"aw{i}", [128, 6 * D]) for i in range(2)]
                g_aw = [S.group(f"aw{i}") for i in range(2)]
                for k in range(8):
                    sl = k % 2
                    for hh in range(4):
                        S.dma(aw[sl][:, hh * 1536:(hh + 1) * 1536],
                              ada_w[k * 128:(k + 1) * 128, hh * 1536:(hh + 1) * 1536], g_aw[sl])
                    pm = pb[k % 2]
                    for cc in range(48):
                        S.mm(pm.f(cc, cc + 1), lhsT=aw[sl][:, cc * 128:(cc + 1) * 128], rhs=sc[:, k:k + 1])
                    S.tt("dve", modT[:, :], pm.f(0, 48), (abT if k == 0 else modT)[:, :], ALU.add)
                S.stt(s1[:, :], modT[:, 8:16], 1.0, n1[:, :], ALU.add, ALU.mult)
                S.stt(s2[:, :], modT[:, 32:40], 1.0, n2[:, :], ALU.add, ALU.mult)
                S.flush()

            if upto >= 1:
                self.phase1()
            keep = dict(qnT=self.sb(top, "qnT", [128, 4, 2048], BF16), gates=self.sb(top, "gates", [128, 16, 12]),
                        qnTh=self.sb(top, "qnTh", [128, 4, 32], BF16), gatesh=self.sb(top, "gatesh", [32, 12]))
            if upto >= 2:
                self.phase2(keep)
                if self.debug:
                    S.dma(self.outp("d_qnT", [128, 4, 2048], BF16)[:, :, :], keep["qnT"][:, :, :], self.g_out)
                    S.dma(self.outp("d_gates", [128, 16, 12])[:, :, :], keep["gates"][:, :, :], self.g_out)
            if upto >= 3:
                S.flush()
                self.phase3(keep)
            if upto >= 4:
                S.flush()
                if self.debug:
                    self.dx1 = self.outp("d_x1", [4, 128, 4, D])
                self.phase45()
            S.flush()
        return nc

    def front(self, F, xt, nb, hT, c0, scol, sh_c0, b0=0):
        S = self.S
        pb = self.pb
        ssq, rt, rstd, junk, xs = F["ssq"], F["rt"], F["rstd"], F["junk"], F["xs"]
        for b2 in range(nb):
            S.act(junk[:, :], xt[:, b0 + b2, :], AF.Square, accum=ssq[:, b2:b2 + 1])
        S.act(rt[:, 0:nb], ssq[:, 0:nb], AF.Ln, bias=EPS, scale=1.0 / D)
        S.act(rstd[:, 0:nb], rt[:, 0:nb], AF.Exp, scale=-0.5)
        for b2 in range(nb):
            S.ts("pool", xs[:, b2, :], xt[:, b0 + b2, :], rstd[:, b2:b2 + 1], 1.0, ALU.mult, ALU.mult)
        w = nb * 128
        for half in range(2):
            bank = pb[half]
            for k in range(half * 4, half * 4 + 4):
                off = (k % 4) * 256
                for b2 in range(nb):
                    S.tr(bank.b(off + b2 * 128, off + (b2 + 1) * 128), xs[:, b2, k * 128:(k + 1) * 128],
                         self.idb[:, :])
            for k in range(half * 4, half * 4 + 4):
                off = (k % 4) * 256
                S.ts("dve", hT[:, k, c0:c0 + w], bank.b(off, off + w), scol[:, k:k + 1],
                     self.modT[:, sh_c0 + k:sh_c0 + k + 1], ALU.mult, ALU.add)

    def front32(self, F, xt, hT, scol, sh_c0):
        S = self.S
        bank = self.pb[0]
        ssq, rt, rstd, junk, xs = F["ssq"], F["rt"], F["rstd"], F["junk"], F["xs"]
        S.act(junk[0:32, :], xt[:, :], AF.Square, accum=ssq[0:32, 0:1])
        S.act(rt[0:32, 0:1], ssq[0:32, 0:1], AF.Ln, bias=EPS, scale=1.0 / D)
        S.act(rstd[0:32, 0:1], rt[0:32, 0:1], AF.Exp, scale=-0.5)
        S.ts("pool", xs[0:32, 0, :], xt[:, :], rstd[0:32, 0:1], 1.0, ALU.mult, ALU.mult)
        for k in range(8):
            S.tr(bank.b(k * 32, (k + 1) * 32), xs[0:32, 0, k * 128:(k + 1) * 128], self.idb[0:32, 0:32])
        for k in range(8):
            S.ts("dve", hT[:, k, :], bank.b(k * 32, (k + 1) * 32), scol[:, k:k + 1],
                 self.modT[:, sh_c0 + k:sh_c0 + k + 1], ALU.mult, ALU.add)

    def phase1(self):
        S = self.S
        nc = self.nc
        pb = self.pb
        gc_ = S.group("const1", full=True)
        idb, idf = self.idb, self.idf
        with ExitStack() as es:
            tri2 = self.cload(es, "tri2", [128, 128], gc_)
            blk2 = self.cload(es, "blk2", [128, 128], gc_)
            onesf = self.cload(es, "onesf", [128, 128], gc_)
            cind = self.cload(es, "cind", [128, 2], gc_)
            ma4 = self.cload(es, "ma4", [128, 512], gc_)
            mb4 = self.cload(es, "mb4", [128, 512], gc_)
            sel = self.cload(es, "sel", [128, 4], gc_)
            hsel = self.cload(es, "hsel", [128, 4], gc_)
            cw = self.cload(es, "gcw", [128, 48], gc_)
            dtb = self.cload(es, "dtb", [128, 4], gc_)
            alog = self.cload(es, "alog", [128, 4], gc_)
            kslw = self.cload(es, "kslw", [128, 1], gc_)
            kwnw = self.cload(es, "kwnw", [128, 1], gc_)
            negones = self.sb(es, "negones", [128, 128])
            S.ts("dve", negones[:, :], onesf[:, :], -1.0, None, ALU.mult)
            onesb = self.sb(es, "onesb", [128, 128], BF16)
            S.copy("dve", onesb[:, :], onesf[:, :])
            i4b = self.sb(es, "i4b", [128, 512], BF16)
            for h in range(4):
                S.copy("dve", i4b[:, h * 128:(h + 1) * 128], idf[:, :])
            negA = self.sb(es, "negA", [128, 4])
            S.act(negA[:, :], alog[:, :], AF.Exp)
            S.ts("dve", negA[:, :], negA[:, :], -1.0, None, ALU.mult)

            winb = self.sb(es, "winb", [128, 8, W1_N], BF16)
            with ExitStack() as es2:
                wst = [self.sb(es2, f"wst{i}", [128, W1_N]) for i in range(2)]
                g_w = [S.group(f"wst{i}") for i in range(2)]
                for k in range(8):
                    sl = k % 2
                    for (a, b_, o) in W1_SEGS:
                        S.dma(wst[sl][:, o:o + (b_ - a)], self.w_in[k * 128:(k + 1) * 128, a:b_], g_w[sl])
                    S.copy("dve", winb[:, k, 0:1024], wst[sl][:, 0:1024])
                    S.copy("pool", winb[:, k, 1024:W1_N], wst[sl][:, 1024:W1_N])
                S.flush()

            F = dict(ssq=self.sb(es, "f_ssq", [128, 4]), rt=self.sb(es, "f_rt", [128, 4]),
                     rstd=self.sb(es, "f_rstd", [128, 4]), junk=self.sb(es, "f_junk", [128, 1024], BF16),
                     xs=self.sb(es, "f_xs", [128, 2, 1024], BF16))
            xt = [self.sb(es, f"xt{i}", [128, 2, 1024]) for i in range(2)]
            g_x = [S.group(f"xt{i}") for i in range(2)]
            hT = self.sb(es, "hT", [128, 8, 512], BF16)
            pre = [self.sb(es, f"pre{i}", [128, 515]) for i in range(3)]
            hist = self.sb(es, "hist", [128, 12, 3])
            S.memset("pool", hist[:, :, :], 0.0)
            acc = [self.sb(es, f"cacc{i}", [128, 512]) for i in range(3)]
            yfs = [self.sb(es, f"yfs{i}", [128, 512], BF16 if i < 8 else F32) for i in range(10)]
            sq = [self.sb(es, f"sq{i}", [128, 512], BF16) for i in range(3)]
            rtt = [self.sb(es, f"rtt{i}", [128, 512]) for i in range(3)]
            qT = self.sb(es, "qT", [128, 4, 512], BF16)
            kT = self.sb(es, "kT", [128, 4, 512], BF16)
            vT = self.sb(es, "vT", [128, 4, 512], BF16)
            st_f = [[self.sb(es, f"stf{c}_{i}", [128, 512], BF16) for i in range(2)] for c in range(4)]
            st_v = [[self.sb(es, f"stv{c}_{i}", [128, 4, 128], BF16) for i in range(2)] for c in range(2)]
            g_sf = [[S.group(f"sf{c}_{i}") for i in range(2)] for c in range(4)]
            g_sv = [[S.group(f"sv{c}_{i}") for i in range(2)] for c in range(2)]
            ab = self.sb(es, "ab", [128, 4, 8])
            S32 = self.sb(es, "S32", [128, 4, 128])
            Sb = [self.sb(es, f"Sb{i}", [128, 4, 128], BF16) for i in range(2)]
            self.cidx = 0
            S.memset("pool", S32[:, :, :], 0.0)
            S.memset("pool", Sb[0][:, :, :], 0.0)
            oacc = [self.sb(es, f"oacc{i}", [128, 512]) for i in range(2)]
            g_o = [S.group(f"oacc{i}") for i in range(2)]
            oph = [self.sb(es, f"oph{i}", [128, 512]) for i in range(2)]
            g_oh = [S.group(f"oph{i}") for i in range(2)]
            G = []
            for s_ in range(2):
                g = {}
                for nm, shp, dt in (("sm", [128, 64], F32), ("TG", [128, 4, 128], F32), ("Xs", [128, 512], F32),
                                    ("XA", [128, 512], F32), ("XB", [128, 512], F32), ("EG", [128, 4, 128], F32),
                                    ("Nb", [128, 4, 128], BF16), ("aT", [128, 4, 128], BF16),
                                    ("qdT", [128, 4, 128], BF16), ("kw", [128, 4, 128], BF16),
                                    ("kd", [128, 4, 128], BF16), ("vb", [128, 4, 128], BF16),
                                    ("NTb", [128, 4, 128], BF16), ("RTb", [128, 4, 128], BF16),
                                    ("P0", [128, 4, 128], BF16), ("P1", [128, 4, 128], BF16),
                                    ("Q0", [128, 4, 128], BF16), ("Q1", [128, 4, 128], BF16),
                                    ("u", [128, 4, 128], F32), ("wTb", [128, 4, 128], BF16),
                                    ("vn", [128, 4, 128], BF16)):
                    g[nm] = self.sb(es, f"g{s_}_{nm}", shp, dt)
                G.append(g)

            xrows = self.xb.rearrange("(n b p) d -> n p b d", b=2, p=128)

            def load_x(n):
                S.dma(xt[n % 2][:, :, :], xrows[n], g_x[n % 2])

            load_x(0)
            for ti in range(self.ntiles):
                for hf in range(2):
                    n = 2 * ti + hf
                    if n + 1 < 2 * NT:
                        load_x(n + 1)
                    self.front(F, xt[n % 2], 2, hT, hf * 256, self.s1, 0)
                if self.stop == 1:
                    continue
                nsa_ch = ((C_KC, self.kcT_d, None), (C_VC, self.vcT_d, None), (C_KSL, self.kslT_d, kslw),
                          (C_KWN, self.kwnT_d, kwnw))

                def st1(c, pos):
                    bank = pb[2 + (pos % 2)]
                    coff = c * 128 if c < 12 else nsa_ch[c - 12][0]
                    for k in range(8):
                        S.mm(bank.f(0, 512), lhsT=winb[:, k, coff:coff + 128], rhs=hT[:, k, :],
                             start=(k == 0), stop=(k == 7))
                    if c < 12:
                        p_ = pre[pos % 3]
                        S.copy("pool", p_[:, 0:3], hist[:, c, :])
                        S.copy("act", p_[:, 3:515], bank.f(0, 512))
                        S.copy("pool", hist[:, c, :], p_[:, 512:515])
                        a_ = acc[pos % 3]
                        S.ts("pool", a_[:, :], p_[:, 0:512], cw[:, c * 4:c * 4 + 1], 1.0, ALU.mult, ALU.mult)
                        for j in range(1, 4):
                            S.stt(a_[:, :], p_[:, j:j + 512], cw[:, c * 4 + j:c * 4 + j + 1], a_[:, :], ALU.mult, ALU.add)
                    else:
                        ci = c - 12
                        wcol = nsa_ch[ci][2]
                        if wcol is None:
                            S.copy("act", st_f[ci][ti % 2][:, :], bank.f(0, 512))
                            S.dma(nsa_ch[ci][1][:, ti * 512:(ti + 1) * 512], st_f[ci][ti % 2][:, :], g_sf[ci][ti % 2])
                        else:
                            S.copy("act", yfs[c - 6][:, :], bank.f(0, 512))

                def st2(c, pos):
                    h = c % 4
                    if c >= 12:
                        return
                    if c >= 8:
                        S.act(vT[:, h, :], acc[pos % 3][:, :], AF.Silu)
                        return
                    S.act(yfs[c][:, :], acc[pos % 3][:, :], AF.Silu)

                def st3(c, i_):
                    h = c % 4
                    bk2 = pb[4 + (i_ % 2)]
                    yi = c if c < 8 else c - 6
                    S.act(sq[i_ % 3][:, :], yfs[yi][:, :], AF.Square)
                    S.mm(bk2.f(0, 512), lhsT=onesb[:, :], rhs=sq[i_ % 3][:, :])
                    r_ = rtt[i_ % 3]
                    if c < 4:
                        S.act(r_[:, :], bk2.f(0, 512), AF.Ln, bias=EPS * 128.0, scale=128.0)
                    elif c < 8:
                        S.act(r_[:, :], bk2.f(0, 512), AF.Ln, bias=EPS, scale=1.0)
                    else:
                        S.act(r_[:, :], bk2.f(0, 512), AF.Ln, bias=EPS, scale=1.0 / 128.0)
                    S.act(r_[:, :], r_[:, :], AF.Exp, scale=-0.5)
                    if c < 8:
                        S.tt("pool", (qT if c < 4 else kT)[:, h, :], yfs[yi][:, :], r_[:, :], ALU.mult)
                    else:
                        ci = c - 12
                        st = st_f[ci][ti % 2]
                        S.stt(st[:, :], yfs[yi][:, :], nsa_ch[ci][2][:, 0:1], r_[:, :], ALU.mult, ALU.mult)
                        S.dma(nsa_ch[ci][1][:, ti * 512:(ti + 1) * 512], st[:, :], g_sf[ci][ti % 2])

                order = [12, 13, 14, 15] + list(range(12))
                for i in range(len(order) + 1):
                    if i < len(order):
                        st1(order[i], i)
                    if 0 <= i - 1 < len(order):
                        st2(order[i - 1], i - 1)
                if self.stop == 3:
                    continue
                for blk in range(4):
                    bank = pb[6]
                    for vi, coff in enumerate((C_VSL, C_VWN)):
                        for k in range(8):
                            if self.var == 4:
                                break
                            S.mm(bank.f(vi * 128, (vi + 1) * 128), lhsT=hT[:, k, blk * 128:(blk + 1) * 128],
                                 rhs=winb[:, k, coff:coff + 128], start=(k == 0), stop=(k == 7))
                    for k in range(8):
                        if self.var == 1:
                            break
                        S.mm(bank.f(256, 264), lhsT=hT[:, k, blk * 128:(blk + 1) * 128],
                             rhs=winb[:, k, C_AB:C_AB + 8], start=(k == 0), stop=(k == 7))
                    if self.var != 3:
                        S.copy("act", st_v[0][ti % 2][:, blk, :], bank.f(0, 128))
                        S.copy("act", st_v[1][ti % 2][:, blk, :], bank.f(128, 256))
                    if self.var != 5:
                        S.copy("dve", ab[:, blk, :], bank.f(256, 264))
                if self.var != 2:
                    S.dma(self.vsl_d[:, ti * 4:(ti + 1) * 4, :], st_v[0][ti % 2][:, :, :], g_sv[0][ti % 2])
                    S.dma(self.vwn_d[:, ti * 4:(ti + 1) * 4, :], st_v[1][ti % 2][:, :, :], g_sv[1][ti % 2])

                for i_, c in enumerate([14, 15, 0, 1, 2, 3, 4, 5, 6, 7]):
                    st3(c, i_)
                if self.stop == 4:
                    continue
                for pair in range(2):
                    blks = (2 * pair, 2 * pair + 1)
                    for s_, blk in enumerate(blks):
                        self.gdn_local_1(G[s_], s_, blk, ab, tri2, blk2, onesf, negones, cind, ma4, mb4, dtb, negA,
                                         qT, kT, vT)
                    if self.stop == 5:
                        continue
                    self.gdn_solve(G, blks, i4b)
                    if self.stop == 6:
                        continue
                    for s_, blk in enumerate(blks):
                        self.gdn_uw(G[s_], s_)
                    if self.stop == 7:
                        continue
                    for s_, blk in enumerate(blks):
                        oa = oacc[ti % 2]
                        self.gdn_recur(G[s_], S32, Sb, sel, blk, oa, hsel, oph[ti % 2])
                S.dma(self.o_own[ti], oacc[ti % 2][:, :], g_o[ti % 2])
                S.dma(self.oprev_d[ti], oph[ti % 2][126:128, :], g_oh[ti % 2])
            S.flush()

    def gdn_local_1(self, g, s_, blk, ab, tri2, blk2, onesf, negones, cind, ma4, mb4, dtb, negA, qT, kT, vT):
        S = self.S
        pb = self.pb
        sm = g["sm"]
        cs = slice(blk * 128, (blk + 1) * 128)
        x_, t_, gg, be, nb_, gci, gcv, gl, egc, ekd, bw, glS = (
            sm[:, 0:4], sm[:, 4:8], sm[:, 8:12], sm[:, 12:16], sm[:, 16:20], sm[:, 20:28], sm[:, 28:32],
            sm[:, 32:36], sm[:, 36:40], sm[:, 40:44], sm[:, 44:48], sm[:, 48:56])
        S.tt("dve", x_, ab[:, blk, 0:4], dtb[:, :], ALU.add)
        S.stt(t_, x_, -1.0, x_, ALU.mult, ALU.max)
        S.act(t_, t_, AF.Exp, scale=-1.0)
        S.act(t_, t_, AF.Ln, bias=1.0)
        S.stt(t_, x_, 0.0, t_, ALU.max, ALU.add)
        S.tt("dve", gg, t_, negA[:, :], ALU.mult)
        S.act(be, ab[:, blk, 4:8], AF.Exp, scale=-1.0)
        S.ts("dve", be, be, 1.0, None, ALU.add)
        S.recip(be, be)
        S.ts("dve", nb_, be, -1.0, None, ALU.mult)
        gci3 = gci.re("p (h i) -> p h i", i=2)
        for i in range(2):
            S.ts("dve", gci3[:, :, i], gg, cind[:, i:i + 1], None, ALU.mult)
        for h in range(4):
            S.ts("pool", g["TG"][:, h, :], tri2[:, :], gg[:, h:h + 1], 1.0, ALU.mult, ALU.mult)
        bS = pb[6]
        o0 = 384 + s_ * 32
        S.mm(bS.f(o0, o0 + 4), lhsT=tri2[:, :], rhs=gg)
        S.mm(bS.f(o0 + 4, o0 + 8), lhsT=blk2[:, :], rhs=gg)
        S.mm(bS.f(o0 + 8, o0 + 16), lhsT=onesf[:, :], rhs=gci)
        bX = self.bankA[s_]
        for h in range(4):
            S.mm(bX.f(h * 128, (h + 1) * 128), lhsT=onesf[:, :], rhs=g["TG"][:, h, :], start=True, stop=False)
            S.mm(bX.f(h * 128, (h + 1) * 128), lhsT=g["TG"][:, h, :], rhs=negones[:, :], start=False, stop=True)
        S.copy("dve", sm[:, 28:36], bS.f(o0, o0 + 8))
        S.act(glS, bS.f(o0 + 8, o0 + 16), AF.Exp)
        S.act(egc, gcv, AF.Exp)
        S.tt("dve", ekd, gl, gcv, ALU.subtract)
        S.act(ekd, ekd, AF.Exp)
        S.tt("dve", bw, be, egc, ALU.mult)
        S.copy("act", g["Xs"][:, :], bX.f(0, 512))
        for h in range(4):
            S.act(g["EG"][:, h, :], bX.f(h * 128, (h + 1) * 128), AF.Exp, bias=gcv[:, h:h + 1])
        S.tt("pool", g["XA"][:, :], g["Xs"][:, :], ma4[:, :], ALU.add)
        S.tt("pool", g["XB"][:, :], g["Xs"][:, :], mb4[:, :], ALU.add)
        S.act(g["XA"][:, :], g["XA"][:, :], AF.Exp, scale=-1.0)
        S.act(g["XB"][:, :], g["XB"][:, :], AF.Exp)
        bK = self.bankA[s_]
        bQ = self.bankB[s_]
        for h in range(4):
            S.mm(bK.f(h * 128, (h + 1) * 128), lhsT=kT[:, h, cs], rhs=kT[:, h, cs])
        for h in range(4):
            S.mm(bQ.f(h * 128, (h + 1) * 128), lhsT=kT[:, h, cs], rhs=qT[:, h, cs])
        for h in range(4):
            S.stt(g["Nb"][:, h, :], bK.f(h * 128, (h + 1) * 128), nb_[:, h:h + 1],
                  g["XA"][:, h * 128:(h + 1) * 128], ALU.mult, ALU.mult)
        S.tt("dve", g["aT"][:, :, :].re("p h c -> p (h c)"), bQ.f(0, 512), g["XB"][:, :], ALU.mult)
        S.tt("pool", g["qdT"][:, :, :], qT[:, :, cs], g["EG"][:, :, :], ALU.mult)
        bT = pb[7]
        for h in range(4):
            S.tr(bT.b(h * 128, (h + 1) * 128), kT[:, h, cs], self.idb[:, :])
        for h in range(4):
            S.tr(bT.b(512 + h * 128, 512 + (h + 1) * 128), vT[:, h, cs], self.idb[:, :])
        kt3 = bT.b(0, 512).re("p (h d) -> p h d", h=4)
        vt3 = bT.b(512, 1024).re("p (h d) -> p h d", h=4)
        S.tt("dve", g["kw"][:, :, :], kt3, V(bw.ap.unsqueeze(2).broadcast_to([128, 4, 128]), bw.bs), ALU.mult)
        S.tt("dve", g["kd"][:, :, :], kt3, V(ekd.ap.unsqueeze(2).broadcast_to([128, 4, 128]), ekd.bs), ALU.mult)
        S.tt("dve", g["vb"][:, :, :], vt3, V(be.ap.unsqueeze(2).broadcast_to([128, 4, 128]), be.bs), ALU.mult)

    def gdn_solve(self, G, blks, i4b):
        S = self.S
        idb = self.idb
        for s_ in range(len(blks)):
            g = G[s_]
            bA, bC = self.bankA[s_], self.bankC[s_]
            for h in range(4):
                S.mm(bA.f(h * 128, (h + 1) * 128), lhsT=g["Nb"][:, h, :], rhs=idb[:, :])
            S.copy("act", g["NTb"][:, :, :].re("p h c -> p (h c)"), bA.f(0, 512))
            S.mm(bC.f(0, 512), lhsT=idb[:, :], rhs=i4b[:, :], start=True, stop=False)
            for h in range(4):
                S.mm(bC.f(h * 128, (h + 1) * 128), lhsT=g["Nb"][:, h, :], rhs=idb[:, :], start=False, stop=(h == 3))
            S.copy("act", g["RTb"][:, :, :].re("p h c -> p (h c)"), bC.f(0, 512))
        P = [G[s_]["Nb"] for s_ in range(len(blks))]
        Q = [G[s_]["NTb"] for s_ in range(len(blks))]
        for k in range(1, 6):
            for s_ in range(len(blks)):
                g = G[s_]
                bA, bB, bC = self.bankA[s_], self.bankB[s_], self.bankC[s_]
                Pn = g["P%d" % (k % 2)]
                Qn = g["Q%d" % (k % 2)]
                for h in range(4):
                    S.mm(bA.f(h * 128, (h + 1) * 128), lhsT=Q[s_][:, h, :], rhs=P[s_][:, h, :])
                if k < 5:
                    for h in range(4):
                        S.mm(bB.f(h * 128, (h + 1) * 128), lhsT=P[s_][:, h, :], rhs=Q[s_][:, h, :])
                S.copy("dve", Pn[:, :, :].re("p h c -> p (h c)"), bA.f(0, 512))
                if k < 5:
                    S.copy("act", Qn[:, :, :].re("p h c -> p (h c)"), bB.f(0, 512))
                S.mm(bC.f(0, 512), lhsT=idb[:, :], rhs=g["RTb"][:, :, :].re("p h c -> p (h c)"), start=True, stop=False)
                for h in range(4):
                    S.mm(bC.f(h * 128, (h + 1) * 128), lhsT=Pn[:, h, :], rhs=g["RTb"][:, h, :],
                         start=False, stop=(h == 3))
                S.copy("act", g["RTb"][:, :, :].re("p h c -> p (h c)"), bC.f(0, 512))
                P[s_] = Pn
                Q[s_] = Qn

    def gdn_uw(self, g, s_):
        S = self.S
        bA, bB = self.bankA[s_], self.bankB[s_]
        for h in range(4):
            S.mm(bA.f(h * 128, (h + 1) * 128), lhsT=g["RTb"][:, h, :], rhs=g["vb"][:, h, :])
        for h in range(4):
            S.mm(bB.f(h * 128, (h + 1) * 128), lhsT=g["kw"][:, h, :], rhs=g["RTb"][:, h, :])
        S.copy("act", g["u"][:, :, :].re("p h c -> p (h c)"), bA.f(0, 512))
        S.copy("dve", g["wTb"][:, :, :].re("p h c -> p (h c)"), bB.f(0, 512))

    def gdn_recur(self, g, S32, Sb, sel, blk, oa, hsel, oh):
        S = self.S
        pb = self.pb
        bV, bO, bS_ = pb[2], pb[3], pb[4]
        sm = g["sm"]
        for i in range(2):
            r0, r1 = 64 * i, 64 * i + 64
            cur = Sb[self.cidx % 2]
            nxt = Sb[(self.cidx + 1) % 2]
            self.cidx += 1
            for h in range(4):
                S.mm(bV.f(h * 128, (h + 1) * 128, r0, r1), lhsT=g["wTb"][:, h, r0:r1], rhs=cur[:, h, :])
            S.tt("dve", g["vn"][r0:r1, :, :].re("p h c -> p (h c)"), g["u"][r0:r1, :, :].re("p h c -> p (h c)"),
                 bV.f(0, 512, r0, r1), ALU.subtract)
            for h in range(4):
                S.mm(bS_.f(h * 128, (h + 1) * 128), lhsT=g["kd"][r0:r1, h, :], rhs=g["vn"][r0:r1, h, :])
            for h in range(4):
                S.stt(S32[:, h, :], S32[:, h, :], sm[:, 48 + 2 * h + i:49 + 2 * h + i], bS_.f(h * 128, (h + 1) * 128),
                      ALU.mult, ALU.add)
            S.copy("act", nxt[:, :, :], S32[:, :, :])
            for h in range(4):
                S.mm(bO.f(h * 128, (h + 1) * 128, r0, r1), lhsT=g["qdT"][:, h, r0:r1], rhs=cur[:, h, :],
                     start=True, stop=False)
                S.mm(bO.f(h * 128, (h + 1) * 128, r0, r1), lhsT=g["aT"][r0:r1, h, r0:r1], rhs=g["vn"][r0:r1, h, :],
                     start=False, stop=True)
        r_ = blk % 4
        if r_ == 0:
            S.ts("dve", oa[:, :], bO.f(0, 512), sel[:, 0:1], None, ALU.mult)
        else:
            S.stt(oa[:, :], bO.f(0, 512), sel[:, r_:r_ + 1], oa[:, :], ALU.mult, ALU.add)
        if r_ == 0:
            S.ts("dve", oh[64:128, :], bO.f(0, 512, 64, 128), hsel[64:128, 0:1], None, ALU.mult)
        else:
            S.stt(oh[64:128, :], bO.f(0, 512, 64, 128), hsel[64:128, r_:r_ + 1], oh[64:128, :], ALU.mult, ALU.add)

    def phase2(self, keep):
        S = self.S
        pb = self.pb
        gc_ = S.group("const2", full=True)
        qnT, gates = keep["qnT"], keep["gates"]
        with ExitStack() as es:
            qnw = self.cload(es, "qnw", [128, 1], gc_)
            gonw4 = self.cload(es, "gonw4", [128, 512], gc_)
            onesf = self.cload(es, "onesf2", [128, 128], gc_)
            onesb = self.sb(es, "onesb2", [128, 128], BF16)
            S.copy("dve", onesb[:, :], onesf[:, :])
            winb2 = self.sb(es, "winb2", [128, 8, 1036], BF16)
            with ExitStack() as es2:
                wst = [self.sb(es2, f"w2st{i}", [128, 1036]) for i in range(2)]
                g_w = [S.group(f"w2st{i}") for i in range(2)]
                for k in range(8):
                    sl = k % 2
                    for (a, b_, o) in ((O_GZ, O_GZ + 512, 0), (O_NQ, O_NQ + 512, 512), (O_NG, O_NG + 12, 1024)):
                        S.dma(wst[sl][:, o:o + (b_ - a)], self.w_in[k * 128:(k + 1) * 128, a:b_], g_w[sl])
                    S.copy("dve" if k % 2 else "pool", winb2[:, k, :], wst[sl][:, :])
                S.flush()
            F = dict(ssq=self.sb(es, "f2_ssq", [128, 4]), rt=self.sb(es, "f2_rt", [128, 4]),
                     rstd=self.sb(es, "f2_rstd", [128, 4]), junk=self.sb(es, "f2_junk", [128, 1024], BF16),
                     xs=self.sb(es, "f2_xs", [128, 2, 1024], BF16))
            xt = [self.sb(es, f"x2t{i}", [128, 2, 1024]) for i in range(2)]
            g_x = [S.group(f"x2t{i}") for i in range(2)]
            hT = self.sb(es, "h2T_", [128, 8, 512], BF16)
            yf = [self.sb(es, f"y2f{i}", [128, 512]) for i in range(2)]
            sq = [self.sb(es, f"s2q{i}", [128, 512], BF16) for i in range(2)]
            rtt = [self.sb(es, f"r2tt{i}", [128, 512]) for i in range(2)]
            og = [self.sb(es, f"og{i}", [128, 512]) for i in range(2)]
            g_og = [S.group(f"og{i}") for i in range(2)]
            zs = [self.sb(es, f"zs{i}", [128, 512]) for i in range(2)]
            t1 = [self.sb(es, f"p2t{i}", [128, 512]) for i in range(2)]
            mg = [self.sb(es, f"mg{i}", [128, 512], BF16) for i in range(2)]
            mgT = [self.sb(es, f"mgT{i}", [128, 4, 128], BF16) for i in range(2)]
            g_mg = [S.group(f"mgT{i}") for i in range(2)]
            osq = self.sb(es, "osq", [128, 8])
            xrows = self.xo.rearrange("(n b p) d -> n p b d", b=2, p=128)

            def load_x(n):
                S.dma(xt[n % 2][:, :, :], xrows[n], g_x[n % 2])

            load_x(0)
            for t in range(4):
                for hf in range(2):
                    n = 2 * t + hf
                    if n + 1 < 8:
                        load_x(n + 1)
                    self.front(F, xt[n % 2], 2, hT, hf * 256, self.s1, 0)
                for h in range(4):
                    bank = pb[2 + (h % 2)]
                    for k in range(8):
                        S.mm(bank.f(0, 512), lhsT=winb2[:, k, 512 + h * 128:512 + (h + 1) * 128], rhs=hT[:, k, :],
                             start=(k == 0), stop=(k == 7))
                    y_ = yf[h % 2]
                    S.copy("act", y_[:, :], bank.f(0, 512))
                    S.act(sq[h % 2][:, :], bank.f(0, 512), AF.Square)
                    bk2 = pb[4 + (h % 2)]
                    S.mm(bk2.f(0, 512), lhsT=onesb[:, :], rhs=sq[h % 2][:, :])
                    r_ = rtt[h % 2]
                    S.act(r_[:, :], bk2.f(0, 512), AF.Ln, bias=EPS, scale=1.0 / 128.0)
                    S.act(r_[:, :], r_[:, :], AF.Exp, scale=-0.5)
                    S.stt(qnT[:, h, t * 512:(t + 1) * 512], y_[:, :], qnw[:, 0:1], r_[:, :], ALU.mult, ALU.mult)
                for blk in range(4):
                    j = 4 * t + blk
                    cs = slice(blk * 128, (blk + 1) * 128)
                    bg = pb[6]
                    for k in range(8):
                        S.mm(bg.f(0, 12), lhsT=hT[:, k, cs], rhs=winb2[:, k, 1024:1036], start=(k == 0), stop=(k == 7))
                    S.act(gates[:, j, :], bg.f(0, 12), AF.Sigmoid)
                    bz = pb[7]
                    for k in range(8):
                        S.mm(bz.f(0, 512), lhsT=hT[:, k, cs], rhs=winb2[:, k, 0:512], start=(k == 0), stop=(k == 7))
                    z_ = zs[j % 2]
                    S.act(z_[:, :], bz.f(0, 512), AF.Silu)
                    o_ = og[j % 2]
                    S.dma(o_[:, :], self.o_own[j], g_og[j % 2])
                    for h in range(4):
                        S.act(t1[j % 2][:, h * 128:(h + 1) * 128], o_[:, h * 128:(h + 1) * 128], AF.Square,
                              accum=osq[:, h:h + 1])
                    S.act(osq[:, 4:8], osq[:, 0:4], AF.Sqrt, bias=EPS, scale=1.0 / 128.0)
                    S.recip(osq[:, 4:8], osq[:, 4:8])
                    rb = V(osq.t[:, 4:8].unsqueeze(2).broadcast_to([128, 4, 128]), [osq.b])
                    S.tt("dve", t1[j % 2][:, :].re("p (h d) -> p h d", h=4), o_[:, :].re("p (h d) -> p h d", h=4), rb,
                         ALU.mult)
                    S.tt("pool", t1[j % 2][:, :], t1[j % 2][:, :], gonw4[:, :], ALU.mult)
                    S.tt("dve", mg[j % 2][:, :], t1[j % 2][:, :], z_[:, :], ALU.mult)
                    bt = pb[0]
                    for c in range(4):
                        S.tr(bt.b(c * 128, (c + 1) * 128), mg[j % 2][:, c * 128:(c + 1) * 128], self.idb[:, :])
                    S.copy("act", mgT[j % 2][:, :, :].re("p c t -> p (c t)"), bt.b(0, 512))
                    S.dma(self.mix_d[:, 0:4, j * 128:(j + 1) * 128], mgT[j % 2][:, :, :], g_mg[j % 2])
            gh = S.group("p2halo", full=True)
            selAB = self.cload(es, "selAB", [128, 2], gh)
            xht = self.sb(es, "xht", [32, D])
            S.dma(xht[:, :], self.inp("xh", [32, D])[:, :], gh)
            ca = self.sb(es, "hca", [32, 512])
            cb = self.sb(es, "hcb", [32, 512])
            S.memset("pool", cb[0:2, :], 0.0)
            opv = self.oprev_d.rearrange("j i c -> (j i) c")
            S.dma(ca[:, :], opv[0:32, :], gh)
            S.dma(cb[2:32, :], opv[0:30, :], gh)
            hTh = self.sb(es, "hTh", [128, 8, 32], BF16)
            self.front32(F, xht, hTh, self.s1, 0)
            qnTh, gatesh = keep["qnTh"], keep["gatesh"]
            bq = pb[2]
            for h in range(4):
                for k in range(8):
                    S.mm(bq.f(h * 32, (h + 1) * 32), lhsT=winb2[:, k, 512 + h * 128:512 + (h + 1) * 128], rhs=hTh[:, k, :],
                         start=(k == 0), stop=(k == 7))
            S.copy("act", yf[0][:, 0:128], bq.f(0, 128))
            S.act(sq[0][:, 0:128], bq.f(0, 128), AF.Square)
            S.mm(pb[4].f(0, 128), lhsT=onesb[:, :], rhs=sq[0][:, 0:128])
            S.act(rtt[0][:, 0:128], pb[4].f(0, 128), AF.Ln, bias=EPS, scale=1.0 / 128.0)
            S.act(rtt[0][:, 0:128], rtt[0][:, 0:128], AF.Exp, scale=-0.5)
            S.stt(qnTh[:, :, :].re("p h q -> p (h q)"), yf[0][:, 0:128], qnw[:, 0:1], rtt[0][:, 0:128], ALU.mult, ALU.mult)
            for k in range(8):
                S.mm(pb[6].f(0, 12, 0, 32), lhsT=hTh[:, k, :], rhs=winb2[:, k, 1024:1036], start=(k == 0), stop=(k == 7))
            S.act(gatesh[:, :], pb[6].f(0, 12, 0, 32), AF.Sigmoid)
            for k in range(8):
                S.mm(pb[7].f(0, 512, 0, 32), lhsT=hTh[:, k, :], rhs=winb2[:, k, 0:512], start=(k == 0), stop=(k == 7))
            S.act(zs[0][0:32, :], pb[7].f(0, 512, 0, 32), AF.Silu)
            S.ts("dve", ca[:, :], ca[:, :], selAB[0:32, 0:1], None, ALU.mult)
            S.stt(ca[:, :], cb[:, :], selAB[0:32, 1:2], ca[:, :], ALU.mult, ALU.add)
            for h in range(4):
                S.act(t1[0][0:32, h * 128:(h + 1) * 128], ca[:, h * 128:(h + 1) * 128], AF.Square, accum=osq[0:32, h:h + 1])
            S.act(osq[0:32, 4:8], osq[0:32, 0:4], AF.Sqrt, bias=EPS, scale=1.0 / 128.0)
            S.recip(osq[0:32, 4:8], osq[0:32, 4:8])
            rb = V(osq.t[0:32, 4:8].unsqueeze(2).broadcast_to([32, 4, 128]), [osq.b])
            S.tt("dve", t1[0][0:32, :].re("p (h d) -> p h d", h=4), ca[:, :].re("p (h d) -> p h d", h=4), rb, ALU.mult)
            S.tt("pool", t1[0][0:32, :], t1[0][0:32, :], gonw4[0:32, :], ALU.mult)
            S.tt("dve", mg[0][0:32, :], t1[0][0:32, :], zs[0][0:32, :], ALU.mult)
            for c in range(4):
                S.tr(pb[0].b(c * 32, (c + 1) * 32), mg[0][0:32, c * 128:(c + 1) * 128], self.idb[0:32, 0:32])
            mghT = self.sb(es, "mghT", [128, 4, 32], BF16)
            S.copy("act", mghT[:, :, :].re("p c t -> p (c t)"), pb[0].b(0, 128))
            S.dma(self.mixh_d[:, 0:4, :], mghT[:, :, :], S.group("mghT"))
            S.flush()

    def phase3(self, keep):
        S = self.S
        pb = self.pb
        idb = self.idb
        qnT, gates = keep["qnT"], keep["gates"]
        SC = 128.0 ** -0.5
        with ExitStack() as es:
            gc_ = S.group("const3", full=True)
            kcmpT = self.sb(es, "kcmpT", [128, 512], BF16)
            vcx = self.sb(es, "vcx", [128, 4, 129], BF16)
            onesf = self.cload(es, "onesf3", [128, 128], gc_)
            onesb = self.sb(es, "onesb3", [128, 128], BF16)
            S.copy("dve", onesb[:, :], onesf[:, :])
            with ExitStack() as e2:
                g2 = S.group("const3b", full=True)
                kcw = self.cload(e2, "kcw", [128, 1], g2)
                pm511 = self.cload(e2, "pm511", [128, 1], g2)
                for which, src_d, w1n, w2n, posn in (("k", self.kcT_d, "cmp_k_w1", "cmp_k_w2", "cmp_k_posT"),
                                                    ("v", self.vcT_d, "cmp_v_w1", "cmp_v_w2", "cmp_v_posT")):
                    with ExitStack() as e3:
                        g3 = S.group("c3" + which, full=True)
                        xT = self.sb(e3, "cx" + which, [128, S_LEN], BF16)
                        S.dma(xT[:, 0:4096], src_d[:, 0:4096], g3)
                        S.dma(xT[:, 4096:8192], src_d[:, 4096:8192], g3)
                        w1d = self.inp(w1n, [128, 32, 128])
                        w1f = self.sb(e3, "w1f" + which, [128, 32, 128])
                        S.dma(w1f[:, 0:16, :], w1d[:, 0:16, :], g3)
                        S.dma(w1f[:, 16:32, :], w1d[:, 16:32, :], g3)
                        w1b = self.sb(e3, "w1b" + which, [128, 32, 128], BF16)
                        S.copy("dve", w1b[:, 0:16, :], w1f[:, 0:16, :])
                        S.copy("pool", w1b[:, 16:32, :], w1f[:, 16:32, :])
                        w2f = self.cload(e3, w2n, [128, 128], g3)
                        w2b = self.sb(e3, "w2b" + which, [128, 128], BF16)
                        S.copy("dve", w2b[:, :], w2f[:, :])
                        posf = self.cload(e3, posn, [128, 32], g3)
                        posb = self.sb(e3, "posb" + which, [128, 32], BF16)
                        S.copy("dve", posb[:, :], posf[:, :])
                        hid = self.sb(e3, "hid" + which, [128, 512], BF16)
                        bcol = self.sb(e3, "bcol" + which, [128, 1])
                        S.memset("pool", hid[:, :], 0.0)
                        bh, bb = pb[2], pb[3]
                        x3 = xT[:, :].re("p (n s) -> p n s", s=16)
                        for l in range(32):
                            S.mm(bh.f(0, 511), lhsT=w1b[:, l, :], rhs=x3[:, l // 16:l // 16 + 511, l % 16],
                                 start=(l == 0), stop=(l == 31))
                        for l in range(32):
                            S.mm(bb.f(0, 1), lhsT=w1b[:, l, :], rhs=posb[:, l:l + 1], start=(l == 0), stop=(l == 31))
                        S.copy("dve", bcol[:, :], bb.f(0, 1))
                        S.act(hid[:, 0:511], bh.f(0, 511), AF.Silu, bias=bcol[:, 0:1])
                        if which == "k":
                            bk = pb[4]
                            S.mm(bk.f(0, 512), lhsT=w2b[:, :], rhs=hid[:, :])
                            yk = self.sb(e3, "yk", [128, 512])
                            sqk = self.sb(e3, "sqk", [128, 512], BF16)
                            rk = self.sb(e3, "rk", [128, 512])
                            S.copy("act", yk[:, :], bk.f(0, 512))
                            S.act(sqk[:, :], bk.f(0, 512), AF.Square)
                            S.mm(pb[5].f(0, 512), lhsT=onesb[:, :], rhs=sqk[:, :])
                            S.act(rk[:, :], pb[5].f(0, 512), AF.Sqrt, bias=EPS, scale=1.0 / 128.0)
                            S.recip(rk[:, :], rk[:, :])
                            S.stt(kcmpT[:, :], yk[:, :], kcw[:, 0:1], rk[:, :], ALU.mult, ALU.mult)
                            S.memset("dve", kcmpT[:, 511:512], 0.0)
                        else:
                            bv = pb[4]
                            for nt in range(4):
                                S.mm(bv.f(nt * 128, (nt + 1) * 128), lhsT=hid[:, nt * 128:(nt + 1) * 128], rhs=w2b[:, :])
                            S.memset("pool", vcx[:, :, 128:129], 1.0)
                            S.copy("act", vcx[:, :, 0:128], bv.f(0, 512).re("p (n d) -> p n d", n=4))
                            S.ts("dve", vcx[:, 3, :], vcx[:, 3, :], pm511[:, 0:1], None, ALU.mult)
                        S.flush()
            if self.debug:
                S.dma(self.outp("d_kcmpT", [128, 512], BF16)[:, :], kcmpT[:, :], self.g_out)
                S.dma(self.outp("d_vcx", [128, 4, 129], BF16)[:, :, :], vcx[:, :, :], self.g_out)
            if self.stop == 31:
                S.flush()
                return
            kslT = self.sb(es, "kslT", [128, S_LEN], BF16)
            kwnT = self.sb(es, "kwnT", [128, S_LEN], BF16)
            vslx = self.sb(es, "vslx", [128, NB, 129], BF16)
            vwnx = self.sb(es, "vwnx", [128, NB, 129], BF16)
            for i in range(2):
                S.dma(kslT[:, i * 4096:(i + 1) * 4096], self.kslT_d[:, i * 4096:(i + 1) * 4096], gc_)
                S.dma(kwnT[:, i * 4096:(i + 1) * 4096], self.kwnT_d[:, i * 4096:(i + 1) * 4096], gc_)
                S.dma(vslx[:, i * 32:(i + 1) * 32, 0:128], self.vsl_d[:, i * 32:(i + 1) * 32, :], gc_)
                S.dma(vwnx[:, i * 32:(i + 1) * 32, 0:128], self.vwn_d[:, i * 32:(i + 1) * 32, :], gc_)
            S.memset("pool", vslx[:, :, 128:129], 1.0)
            S.memset("pool", vwnx[:, :, 128:129], 1.0)
            eall = self.sb(es, "eallb", [128, S_LEN], BF16)
            addm = self.cload(es, "addm", [128, 16, 128], gc_)
            ovb = self.sb(es, "ovb", [128, 4, 128], BF16)
            cmbb = self.sb(es, "cmbb", [128, 16, 2, 128], BF16)
            cbsb = self.sb(es, "cbsb", [128, 4, 4, 128], BF16)
            cbwb = self.sb(es, "cbwb", [128, 8, 4, 128], BF16)
            with ExitStack() as e2:
                g2 = S.group("const3c", full=True)
                ead = self.inp("eall", [128, S_LEN])
                est = [self.sb(e2, f"est{i}", [128, 2048]) for i in range(2)]
                g_e = [S.group(f"est{i}") for i in range(2)]
                for i in range(4):
                    S.dma(est[i % 2][:, :], ead[:, i * 2048:(i + 1) * 2048], g_e[i % 2])
                    S.copy("pool" if i % 2 else "dve", eall[:, i * 2048:(i + 1) * 2048], est[i % 2][:, :])
                ovf = self.cload(e2, "ovm", [128, 4, 128], g2)
                S.copy("dve", ovb[:, :, :], ovf[:, :, :])
                cmf = self.cload(e2, "cmb", [128, 16, 2, 128], g2)
                S.copy("pool", cmbb[:, :, :, :], cmf[:, :, :, :])
                cbsf = self.cload(e2, "cbs", [128, 4, 128], g2)
                cbwf = self.cload(e2, "cbw", [128, 8, 128], g2)
                for h in range(4):
                    S.copy("dve", cbsb[:, :, h, :], cbsf[:, :, :])
                    S.copy("pool", cbwb[:, :, h, :], cbwf[:, :, :])
                S.flush()
            Pc = [self.sb(es, f"Pc{i}", [128, 512], BF16) for i in range(4)]
            Pk = [self.sb(es, f"Pk{i}", [128, 512], BF16) for i in range(3)]
            cmrep = [self.sb(es, f"cmrep{i}", [128, 4, 128], BF16) for i in range(2)]
            ob = {nm: self.sb(es, "ob_" + nm, [128, 4, 129]) for nm in ("c", "s", "w")}
            rs = self.sb(es, "rs", [128, 12])
            cf = self.sb(es, "cf", [128, 12])
            imp = self.sb(es, "imp", [128, 128])
            imp2 = self.sb(es, "imp2", [128, 128])
            m8 = self.sb(es, "m8", [128, 16])
            selm = self.sb(es, "selm", [128, 128])
            biasT = self.sb(es, "biasT", [128, 4, 128], BF16)
            mixn = [self.sb(es, f"mixn{i}", [128, 4, 128], BF16) for i in range(2)]
            tmpn = self.sb(es, "tmpn", [128, 128])
            mnT = [self.sb(es, f"mnT{i}", [128, 4, 128], BF16) for i in range(2)]
            g_mn = [S.group(f"mnT{i}") for i in range(2)]
            bS = [pb[0], pb[1]]
            bO = [pb[2], pb[3]]
            bI, bT = pb[4], pb[5]
            si = [0]

            def nsa_block(Q, qv, cmp_plan, slc_plan, win_plan, addv, gatev, store):
                NQ = 4 * Q

                def scores(kT_tile, extra):
                    bank = bS[si[0] % 2]
                    out_ = Pk[si[0] % 3]
                    si[0] += 1
                    S.mm(bank.f(0, NQ), lhsT=kT_tile, rhs=qv, start=True, stop=(len(extra) == 0))
                    for i_, (l_, r_) in enumerate(extra):
                        S.mm(bank.f(0, NQ), lhsT=l_, rhs=r_, start=False, stop=(i_ == len(extra) - 1))
                    S.act(out_[:, 0:NQ], bank.f(0, NQ), AF.Exp, scale=SC)
                    return out_

                def pv(P_, vx, first, last):
                    for h in range(4):
                        S.mm_(bO[h // 2].f((h % 2) * 129, (h % 2) * 129 + 129, 0, Q), lhsT=P_[:, h * Q:(h + 1) * Q],
                              rhs=vx, start=(first and h % 2 == 0), stop=last, skip=True)

                def evac_o(dst):
                    for hh in range(2):
                        S.copy("act", dst[0:Q, 2 * hh:2 * hh + 2, :], bO[hh].f(0, 258, 0, Q).re("p (h c) -> p h c", h=2))

                ncp = len(cmp_plan)
                for i_, (nt, mk) in enumerate(cmp_plan):
                    bank = bS[i_ % 2]
                    S.mm(bank.f(0, NQ), lhsT=kcmpT[:, nt * 128:(nt + 1) * 128], rhs=qv, start=True, stop=(mk is None))
                    if mk is not None:
                        S.mm(bank.f(0, NQ), lhsT=idb[:, :], rhs=mk, start=False, stop=True)
                    S.act(Pc[i_][:, 0:NQ], bank.f(0, NQ), AF.Exp, scale=SC)
                for h in range(4):
                    for i_, (nt, mk) in enumerate(cmp_plan):
                        S.mm_(bO[h // 2].f((h % 2) * 129, (h % 2) * 129 + 129, 0, Q), lhsT=Pc[i_][:, h * Q:(h + 1) * Q],
                              rhs=vcx[:, nt, :], start=(i_ == 0 and h % 2 == 0), stop=(i_ == ncp - 1), skip=True)
                for h in range(4):
                    for i_, (nt, mk) in enumerate(cmp_plan):
                        S.mm_(bI.f(h * 128, (h + 1) * 128, 0, Q), lhsT=Pc[i_][:, h * Q:(h + 1) * Q], rhs=ovb[:, nt, :],
                              start=(i_ == 0 and h == 0), stop=(i_ == ncp - 1), skip=True)
                evac_o(ob["c"])
                S.ts("dve", rs[0:Q, 0:4], ob["c"][0:Q, :, 128], 1e-30, None, ALU.max)
                S.recip(rs[0:Q, 0:4], rs[0:Q, 0:4])
                S.ts("dve", imp[0:Q, :], bI.f(0, 128, 0, Q), rs[0:Q, 0:1], None, ALU.mult)
                for h in range(1, 4):
                    S.stt(imp[0:Q, :], bI.f(h * 128, (h + 1) * 128, 0, Q), rs[0:Q, h:h + 1], imp[0:Q, :], ALU.mult, ALU.add)
                S.tt("pool", imp[0:Q, :], imp[0:Q, :], addv, ALU.add)
                self.max8(m8[0:Q, 0:8], imp[0:Q, :])
                self.match_replace(imp2[0:Q, :], m8[0:Q, 0:8], imp[0:Q, :], -3.0e38)
                self.max8(m8[0:Q, 8:16], imp2[0:Q, :])
                S.ts("dve", selm[0:Q, :], imp[0:Q, :], m8[0:Q, 15:16], None, ALU.is_ge)
                S.mm(bT.f(0, Q), lhsT=selm[0:Q, :], rhs=self.idf[0:Q, 0:Q])
                S.ts("dve", bT2[:, 0:NQ].re("p (h q) -> p h q", h=4),
                     V(bT.t[:, 0:Q].unsqueeze(1).broadcast_to([128, 4, Q]), bT.q[0:1]), BIG, -BIG, ALU.mult, ALU.add)
                brhs = bT2[:, 0:NQ]
                prev = None
                for i_, (kt, extra) in enumerate(slc_plan):
                    ex = [(eall[:, kt * 128:(kt + 1) * 128], brhs)] + extra
                    P_ = scores(kslT[:, kt * 128:(kt + 1) * 128], ex)
                    if prev is not None:
                        pv(prev[0], vslx[:, prev[1], :], prev[2] == 0, False)
                    prev = (P_, kt, i_)
                pv(prev[0], vslx[:, prev[1], :], prev[2] == 0, True)
                evac_o(ob["s"])
                prev = None
                for i_, (kt, mk) in enumerate(win_plan):
                    P_ = scores(kwnT[:, kt * 128:(kt + 1) * 128], [(idb[:, :], mk)])
                    if prev is not None:
                        pv(prev[0], vwnx[:, prev[1], :], prev[2] == 0, False)
                    prev = (P_, kt, i_)
                pv(prev[0], vwnx[:, prev[1], :], prev[2] == 0, True)
                evac_o(ob["w"])
                S.ts("dve", rs[0:Q, 4:8], ob["s"][0:Q, :, 128], 1e-30, None, ALU.max)
                S.ts("dve", rs[0:Q, 8:12], ob["w"][0:Q, :, 128], 1e-30, None, ALU.max)
                S.recip(rs[0:Q, 4:12], rs[0:Q, 4:12])
                g3 = gatev.re("p (h g) -> p g h", g=3)
                S.tt("dve", cf[0:Q, :].re("p (g h) -> p g h", g=3), rs[0:Q, :].re("p (g h) -> p g h", g=3), g3, ALU.mult)
                mx = mixn[si[0] % 2]
                for h in range(4):
                    S.ts("pool", tmpn[0:Q, :], ob["c"][0:Q, h, 0:128], cf[0:Q, h:h + 1], 1.0, ALU.mult, ALU.mult)
                    S.stt(tmpn[0:Q, :], ob["s"][0:Q, h, 0:128], cf[0:Q, 4 + h:5 + h], tmpn[0:Q, :], ALU.mult, ALU.add)
                    S.stt(mx[0:Q, h, :], ob["w"][0:Q, h, 0:128], cf[0:Q, 8 + h:9 + h], tmpn[0:Q, :], ALU.mult, ALU.add)
                for h in range(4):
                    S.tr(bT.b(h * Q, (h + 1) * Q), mx[0:Q, h, :], idb[0:Q, 0:Q])
                store(bT.b(0, 4 * Q))

            bT2 = self.sb(es, "bT2", [128, 512], BF16)
            for j in range(16):
                qv = qnT[:, :, j * 128:(j + 1) * 128]
                nt_hi = min(3, (32 * j + 30) // 128)
                lo = max(0, 32 * j - 1) // 128
                cmp_plan = []
                for nt in range(nt_hi + 1):
                    mk = None
                    if nt >= lo:
                        slot = nt - lo
                        S.copy("pool", cmrep[slot][:, :, :],
                               V(cmbb.t[:, j, slot, :].unsqueeze(1).broadcast_to([128, 4, 128]), [cmbb.b]))
                        mk = cmrep[slot][:, :, :].re("p h q -> p (h q)")
                    cmp_plan.append((nt, mk))
                slc_plan = []
                for kt in range(4 * j + 4):
                    ex = []
                    if kt >= 4 * j:
                        ex.append((idb[:, :], cbsb[:, kt - 4 * j, :, :].re("p h q -> p (h q)")))
                    slc_plan.append((kt, ex))
                win_plan = [(4 * j - 4 + e, cbwb[:, e, :, :].re("p h q -> p (h q)")) for e in range(8) if 4 * j - 4 + e >= 0]

                def store(src, j=j):
                    S.copy("act", mnT[j % 2][:, :, :].re("p c t -> p (c t)"), src)
                    S.dma(self.mix_d[:, 4:8, j * 128:(j + 1) * 128], mnT[j % 2][:, :, :], g_mn[j % 2])

                nsa_block(128, qv, cmp_plan, slc_plan, win_plan, addm[:, j, :], gates[:, j, :], store)
            self.nsa_halo(es, nsa_block, keep, idb)
            S.flush()

    def nsa_halo(self, es, nsa_block, keep, idb):
        S = self.S
        with ExitStack() as e2:
            g2 = S.group("const3h", full=True)
            haddm = self.cload(e2, "haddm", [32, 128], g2)
            hcmb = self.sb(e2, "hcmb", [128, 4, 128], BF16)
            hsmb = self.sb(e2, "hsmb", [128, 64, 128], BF16)
            hwmb = self.sb(e2, "hwmb", [128, 64, 128], BF16)
            hcf = self.cload(e2, "hcm", [128, 4, 128], g2)
            S.copy("dve", hcmb[:, :, :], hcf[:, :, :])
            st = [self.sb(e2, f"hst{i}", [128, 16, 128]) for i in range(2)]
            g_s = [S.group(f"hst{i}") for i in range(2)]
            n = 0
            for nm, dst in (("hsm", hsmb), ("hwm", hwmb)):
                src = self.inp(nm, [128, 64, 128])
                for i in range(4):
                    S.dma(st[n % 2][:, :, :], src[:, i * 16:(i + 1) * 16, :], g_s[n % 2])
                    S.copy("pool" if n % 2 else "dve", dst[:, i * 16:(i + 1) * 16, :], st[n % 2][:, :, :])
                    n += 1
            mnTh = self.sb(e2, "mnTh", [128, 4, 32], BF16)
            g_m = S.group("mnTh")
            cmp_plan = [(nt, hcmb[:, nt, :]) for nt in range(4)]
            slc_plan = [(kt, [(idb[:, :], hsmb[:, kt, :])]) for kt in range(NB)]
            win_plan = [(kt, hwmb[:, kt, :]) for kt in range(NB)]

            def store(src):
                S.copy("act", mnTh[:, :, :].re("p c t -> p (c t)"), src)
                S.dma(self.mixh_d[:, 4:8, :], mnTh[:, :, :], g_m)

            nsa_block(32, keep["qnTh"][:, :, :], cmp_plan, slc_plan, win_plan, haddm[:, :], keep["gatesh"][:, :], store)
            S.flush()

    def max8(self, out, in_):
        o_, i_ = out.ap, in_.ap
        self.S.op("dve", lambda e: e.max(out=o_, in_=i_), r=self.S._bs(in_), w=self.S._bs(out))

    def match_replace(self, out, rep, vals, imm):
        o_, r_, v_ = out.ap, rep.ap, vals.ap
        self.S.op("dve", lambda e: e.match_replace(out=o_, in_to_replace=r_, in_values=v_, imm_value=imm),
                  r=self.S._bs(rep, vals), w=self.S._bs(out))

    def phase45(self):
        S = self.S
        pb = self.pb
        idb = self.idb
        w_out = self.inp("w_out", [D, D])
        w_up = self.inp("w_up", [D, 2 * DFF])
        w_dn = self.inp("w_dn", [DFF, D])
        wup_d = self.scratch("wup_d", [128, 44, 8, 128], BF16)
        with ExitStack() as e1:
            full = self.sb(e1, "wupfull", [128, 44, 8, 128], BF16)
            stg = [self.sb(e1, f"wupst{i}", [128, 2 * DFF]) for i in range(2)]
            g_s = [S.group(f"wupst{i}") for i in range(2)]
            g_f = S.group("wupfull")
            for k in range(8):
                st = stg[k % 2]
                for i in range(2):
                    S.dma(st[:, i * DFF:(i + 1) * DFF], w_up[k * 128:(k + 1) * 128, i * DFF:(i + 1) * DFF], g_s[k % 2])
                s3 = st[:, :].re("p (c n) -> p c n", n=128)
                S.copy("dve", full[:, 0:16, k, :], s3[:, 0:16, :])
                S.copy("pool", full[:, 16:30, k, :], s3[:, 16:30, :])
                S.copy("act", full[:, 30:44, k, :], s3[:, 30:44, :])
            for i in range(4):
                S.dma(wup_d[:, i * 11:(i + 1) * 11, :, :], full[:, i * 11:(i + 1) * 11, :, :], g_f)
            S.flush()
        with ExitStack() as es:
            gc_ = S.group("const4", full=True)
            fcw = self.cload(es, "fcw", [128, 44 * 3], gc_)
            fcb = self.cload(es, "fcb", [128, 44], gc_)
            wdnb = self.sb(es, "wdnb", [128, 22, D], BF16)
            woutb = self.sb(es, "woutb", [128, 8, D], BF16)
            g1row = self.sb(es, "g1row", [128, D])
            g2row = self.sb(es, "g2row", [128, D])
            with ExitStack() as e2:
                stg = [self.sb(e2, f"wst4_{i}", [128, 2, D]) for i in range(2)]
                g_s = [S.group(f"wst4_{i}") for i in range(2)]
                n = 0
                for (src, dst, nch) in ((w_dn, wdnb, 22), (w_out, woutb, 8)):
                    sv = src.rearrange("(c p) d -> p c d", p=128)
                    for c0 in range(0, nch, 2):
                        st = stg[n % 2]
                        S.dma(st[:, :, :], sv[:, c0:c0 + 2, :], g_s[n % 2])
                        S.copy(("dve", "pool", "act")[n % 3], dst[:, c0:c0 + 2, :], st[:, :, :])
                        n += 1
                gb = self.sb(e2, "gbc", [128, 128])
                for gi, (col0, dst) in enumerate(((16, g1row), (40, g2row))):
                    for cc in range(8):
                        S.copy("dve", gb[:, :], V(self.modT.t[:, col0 + cc:col0 + cc + 1].broadcast_to([128, 128]),
                                                  [self.modT.b]))
                        bank = pb[cc % 2]
                        S.mm(bank.f(0, 128), lhsT=gb[:, :], rhs=self.idf[:, :])
                        S.copy("act", dst[:, cc * 128:(cc + 1) * 128], bank.f(0, 128))
                S.flush()
            F = dict(ssq=self.sb(es, "f4_ssq", [128, 4]), rt=self.sb(es, "f4_rt", [128, 4]),
                     rstd=self.sb(es, "f4_rstd", [128, 4]), junk=self.sb(es, "f4_junk", [128, 1024], BF16),
                     xs=self.sb(es, "f4_xs", [128, 2, 1024], BF16))
            x1 = [self.sb(es, f"x1_{i}", [128, 4, D]) for i in range(2)]
            g_x = [S.group(f"x1_{i}") for i in range(2)]
            mixt = self.sb(es, "mixt", [128, 8, 512], BF16)
            g_m = S.group("mixt")
            h2T = self.sb(es, "h2T", [128, 8, 512], BF16)
            gT = self.sb(es, "gT", [128, 22, 512], BF16)
            wch = [self.sb(es, f"wch{i}", [128, 2, 8, 128], BF16) for i in range(3)]
            g_wc = [S.group(f"wch{i}") for i in range(3)]
            upa = [self.sb(es, f"upa{i}", [128, 4, 130]) for i in range(2)]
            upb = [self.sb(es, f"upb{i}", [128, 4, 130]) for i in range(2)]
            for u_ in upa + upb:
                S.memset("pool", u_[:, :, :], 0.0)
            aa = [self.sb(es, f"aa{i}", [128, 4, 128]) for i in range(2)]
            ab_ = [self.sb(es, f"ab_{i}", [128, 4, 128]) for i in range(2)]
            tmp = [self.sb(es, f"tmp4_{i}", [128, 512]) for i in range(2)]
            xrows = self.xo.rearrange("(t b p) d -> t p b d", b=4, p=128)
            orows = self.out.rearrange("(t b p) d -> t p b d", b=4, p=128)
            wi = 0
            gh = S.group("p4halo", full=True)
            hexr = self.cload(es, "hexr", [128, 32], gh)
            x1h = self.sb(es, "x1h", [32, D])
            mixh = self.sb(es, "mixh", [128, 8, 32], BF16)
            S.dma(x1h[:, :], self.din["xh"][:, :], gh)
            S.dma(mixh[:, :, :], self.mixh_d[:, :, :], gh)
            h2Th = self.sb(es, "h2Th", [128, 8, 32], BF16)
            for half in range(2):
                bank = pb[2 + half]
                hs = slice(half * 512, (half + 1) * 512)
                for c in range(8):
                    S.mm(bank.f(0, 512, 0, 32), lhsT=mixh[:, c, :], rhs=woutb[:, c, hs], start=(c == 0), stop=(c == 7))
                S.tt("dve", tmp[half][0:32, :], bank.f(0, 512, 0, 32), g1row[0:32, hs], ALU.mult)
                S.tt("pool", x1h[:, hs], x1h[:, hs], tmp[half][0:32, :], ALU.add)
            self.front32(F, x1h, h2Th, self.s2, 24)

            def conv(u_, c, dst):
                S.ts("pool", dst[:, :, :], u_[:, :, 2:130], fcw[:, c * 3 + 2:c * 3 + 3], fcb[:, c:c + 1], ALU.mult, ALU.add)
                S.stt(dst[:, :, :], u_[:, :, 1:129], fcw[:, c * 3 + 1:c * 3 + 2], dst[:, :, :], ALU.mult, ALU.add)
                S.stt(dst[:, :, :], u_[:, :, 0:128], fcw[:, c * 3:c * 3 + 1], dst[:, :, :], ALU.mult, ALU.add)

            for t in range(4):
                xx = x1[t % 2]
                S.dma(xx[:, :, :], xrows[t], g_x[t % 2])
                S.dma(mixt[:, :, :], self.mix_d[:, :, t * 512:(t + 1) * 512], g_m)
                for blk in range(4):
                    cs = slice(blk * 128, (blk + 1) * 128)
                    for half in range(2):
                        bank = pb[2 + half]
                        hs = slice(half * 512, (half + 1) * 512)
                        for c in range(8):
                            S.mm(bank.f(0, 512), lhsT=mixt[:, c, cs], rhs=woutb[:, c, hs], start=(c == 0), stop=(c == 7))
                        tm = tmp[half]
                        S.tt("dve", tm[:, :], bank.f(0, 512), g1row[:, hs], ALU.mult)
                        S.tt("pool", xx[:, blk, hs], xx[:, blk, hs], tm[:, :], ALU.add)
                if self.debug:
                    S.dma(self.dx1[t], xx[:, :, :], self.g_out)
                for hf in range(2):
                    self.front(F, xx, 2, h2T, hf * 256, self.s2, 24, b0=2 * hf)
                for c in range(22):
                    w_ = wch[wi % 3]
                    S.dma(w_[:, 0, :, :], wup_d[:, c, :, :], g_wc[wi % 3])
                    S.dma(w_[:, 1, :, :], wup_d[:, 22 + c, :, :], g_wc[wi % 3])
                    wi += 1
                    ua, ub = upa[c % 2], upb[c % 2]
                    for i_, (u_, bank) in enumerate(((ua, pb[4]), (ub, pb[5]))):
                        for k in range(8):
                            S.mm(bank.f(0, 512), lhsT=w_[:, i_, k, :], rhs=h2T[:, k, :], start=(k == 0), stop=(k == 7))
                        S.copy("act", u_[:, :, 2:130], bank.f(0, 512).re("p (b t) -> p b t", b=4))
                        bh_ = pb[6 + i_]
                        for k in range(8):
                            S.mm(bh_.f(0, 8), lhsT=w_[:, i_, k, :], rhs=h2Th[:, k, 8 * t:8 * t + 8], start=(k == 0),
                                 stop=(k == 7))
                        S.tt("dve", u_[:, :, 0:2], bh_.f(0, 8).re("p (b i) -> p b i", i=2),
                             hexr[:, 8 * t:8 * t + 8].re("p (b i) -> p b i", i=2), ALU.mult)
                    conv(ua, c, aa[c % 2])
                    conv(ub, 22 + c, ab_[c % 2])
                    S.act(aa[c % 2][:, :, :], aa[c % 2][:, :, :], AF.Silu)
                    S.tt("dve", gT[:, c, :].re("p (b t) -> p b t", b=4), aa[c % 2][:, :, :], ab_[c % 2][:, :, :], ALU.mult)
                for blk in range(4):
                    cs = slice(blk * 128, (blk + 1) * 128)
                    for half in range(2):
                        bank = pb[6 + half]
                        hs = slice(half * 512, (half + 1) * 512)
                        for c in range(22):
                            S.mm(bank.f(0, 512), lhsT=gT[:, c, cs], rhs=wdnb[:, c, hs], start=(c == 0), stop=(c == 21))
                        tm = tmp[half]
                        S.tt("dve", tm[:, :], bank.f(0, 512), g2row[:, hs], ALU.mult)
                        S.tt("pool", xx[:, blk, hs], xx[:, blk, hs], tm[:, :], ALU.add)
                S.dma(orows[t], xx[:, :, :], g_x[t % 2])
            S.flush()


def _colL(v, n):
    return np.ascontiguousarray(np.asarray(v, np.float32).reshape(n, 128).T)


def _rep(v, n=128):
    v = np.asarray(v, np.float32).reshape(1, -1)
    return np.ascontiguousarray(np.repeat(v, n, axis=0))


def _consts():
    p = np.arange(128)
    same = (p[:, None] // 64) == (p[None, :] // 64)
    c = {}
    c["identf"] = np.eye(128, dtype=np.float32)
    c["tri2"] = (same & (p[:, None] <= p[None, :])).astype(np.float32)
    c["blk2"] = same.astype(np.float32)
    c["onesf"] = np.ones((128, 128), np.float32)
    c["cind"] = np.stack([(p < 64), (p >= 64)], axis=1).astype(np.float32)
    ma = np.where(same & (p[None, :] < p[:, None]), 0.0, BIG).astype(np.float32)
    mb = np.where(same & (p[None, :] >= p[:, None]), 0.0, -BIG).astype(np.float32)
    c["ma4"] = np.ascontiguousarray(np.tile(ma, (1, 4)))
    c["mb4"] = np.ascontiguousarray(np.tile(mb, (1, 4)))
    return c


def _host_inputs(inputs):
    x = np.asarray(inputs["x"], np.float32)
    cst = _consts()
    g = lambda k: np.asarray(inputs[k][0], np.float32)
    gcw = g("gdn_conv_w")
    gcwT = np.ascontiguousarray(gcw.reshape(4, 12, 128).transpose(2, 1, 0).reshape(128, 48))
    shared = {
        "ada_w": np.ascontiguousarray(g("ada_w")),
        "ada_bT": _colL(g("ada_b"), 48),
        "n1w": _colL(g("norm1_w"), 8),
        "n2w": _colL(g("norm2_w"), 8),
        "w_in": np.ascontiguousarray(g("w_in")),
        "gcw": gcwT,
        "dtb": _rep(g("gdn_dt_bias")),
        "alog": _rep(g("gdn_A_log")),
        "kslw": _colL(g("nsa_k_norm_slc"), 1),
        "kwnw": _colL(g("nsa_k_norm_win"), 1),
        "qnw": _colL(g("nsa_q_norm_w"), 1),
        "gonw4": _rep(np.tile(g("gdn_out_norm_w"), 4)),
        "onesf2": np.ones((128, 128), np.float32),
    }
    for nm in ("k", "v"):
        shared[f"cmp_{nm}_w1"] = np.ascontiguousarray(g(f"cmp_{nm}_w1").reshape(32, 128, 128).transpose(1, 0, 2))
        shared[f"cmp_{nm}_w2"] = np.ascontiguousarray(g(f"cmp_{nm}_w2"))
        shared[f"cmp_{nm}_posT"] = np.ascontiguousarray(g(f"cmp_{nm}_pos").T)
    shared["kcw"] = _colL(g("nsa_k_norm_cmp"), 1)
    shared["onesf3"] = np.ones((128, 128), np.float32)
    pm = np.ones((128, 1), np.float32)
    pm[127, 0] = 0.0
    shared["pm511"] = pm
    keys = np.arange(S_LEN)
    shared["eall"] = (keys[None, :] // 64 == np.arange(128)[:, None]).astype(np.float32)
    n = np.arange(512)
    js = np.arange(128)
    ov = np.minimum(16 * n[:, None] + 32, 64 * js[None, :] + 64) - np.maximum(16 * n[:, None], 64 * js[None, :])
    ov = np.clip(ov, 0, None).astype(np.float32) / 32.0
    ov[511] = 0.0
    shared["ovm"] = np.ascontiguousarray(ov.reshape(4, 128, 128).transpose(1, 0, 2))
    fw = g("ffn_conv_w")
    shared["fcw"] = np.ascontiguousarray(fw.reshape(3, 44, 128).transpose(2, 1, 0).reshape(128, 132))
    shared["fcb"] = _colL(g("ffn_conv_b"), 44)
    shared["w_out"] = np.ascontiguousarray(g("w_out"))
    shared["w_up"] = np.ascontiguousarray(g("ffn_w_up"))
    shared["w_dn"] = np.ascontiguousarray(g("ffn_w_down"))
    shared.update(cst)
    maps = []
    for core in range(8):
        b, r = core // 4, core % 4
        xo = np.concatenate([x[b, 128 * (4 * j + r):128 * (4 * j + r) + 128] for j in range(16)], axis=0)
        m = dict(shared)
        m["xb"] = np.ascontiguousarray(x[b])
        m["xo"] = np.ascontiguousarray(xo)
        m["cT"] = _colL(inputs["c"][b], 8)
        sel = np.zeros((128, 4), np.float32)
        sel[:, r] = 1.0
        m["sel"] = sel
        p = np.arange(128)
        q = np.arange(128)
        addm = np.zeros((128, 16, 128), np.float32)
        cmb = np.zeros((128, 16, 2, 128), np.float32)
        for j in range(16):
            qi = 4 * j + r
            tq = 128 * qi + q
            cur = tq // 64
            jj = np.arange(128)[None, :]
            valid = jj <= cur[:, None]
            forced = (jj == 0) | (jj == cur[:, None]) | (jj == cur[:, None] - 1)
            addm[:, j, :] = np.where(valid, np.where(forced, 1.0e4, 0.0), -1.0e30)
            lo = max(0, 32 * j - 1) // 128
            for slot in range(2):
                nn = 128 * (lo + slot) + p
                ok = (16 * nn[:, None] + 31) <= tq[None, :]
                cmb[:, j, slot, :] = np.where(ok, 0.0, -BIG)
        m["addm"] = addm
        m["cmb"] = cmb
        cbs = np.zeros((128, 4, 128), np.float32)
        for d in range(4):
            ok = (128 * (d - r) + p[:, None]) <= q[None, :]
            cbs[:, d, :] = np.where(ok, 0.0, -BIG)
        m["cbs"] = cbs
        cbw = np.zeros((128, 8, 128), np.float32)
        for e in range(8):
            rel = 128 * (e - 4 - r) + p[:, None]
            ok = (rel <= q[None, :]) & (rel > q[None, :] - 512)
            cbw[:, e, :] = np.where(ok, 0.0, -BIG)
        m["cbw"] = cbw
        hs_ = np.zeros((128, 4), np.float32)
        hs_[:, (r - 1) % 4] = 1.0
        m["hsel"] = hs_
        sab = np.zeros((128, 2), np.float32)
        sab[:, 0] = 1.0 if r >= 1 else 0.0
        sab[:, 1] = 1.0 if r == 0 else 0.0
        m["selAB"] = sab
        tq = np.array([128 * (4 * j + r) - 2 + i for j in range(16) for i in range(2)])
        ex = tq >= 0
        xh = np.zeros((32, D), np.float32)
        xh[ex] = x[b, tq[ex]]
        m["xh"] = xh
        m["hexr"] = _rep(ex.astype(np.float32))
        cur = tq // 64
        jj = np.arange(128)[None, :]
        valid = (jj <= cur[:, None]) & ex[:, None]
        forced = (jj == 0) | (jj == cur[:, None]) | (jj == cur[:, None] - 1)
        m["haddm"] = np.where(valid, np.where(forced, 1.0e4, 0.0), -1.0e30).astype(np.float32)
        nn = np.arange(512)
        okc = ((16 * nn[:, None] + 31) <= tq[None, :]) & ex[None, :]
        hcm = np.where(okc, 0.0, -BIG).astype(np.float32).reshape(4, 128, 1, 32)
        m["hcm"] = np.ascontiguousarray(np.broadcast_to(hcm, (4, 128, 4, 32)).transpose(1, 0, 2, 3).reshape(128, 4, 128))
        pos = np.arange(S_LEN)
        oks = (pos[:, None] <= tq[None, :]) & ex[None, :]
        okw = oks & (pos[:, None] > tq[None, :] - 512)
        for nm, ok in (("hsm", oks), ("hwm", okw)):
            a = np.where(ok, 0.0, -BIG).astype(np.float32).reshape(64, 128, 1, 32)
            m[nm] = np.ascontiguousarray(np.broadcast_to(a, (64, 128, 4, 32)).transpose(1, 0, 2, 3).reshape(128, 64, 128))
        maps.append(m)
    return maps


def run(inputs, debug=False, upto=9, ntiles=NT):
    bld = Builder(debug, ntiles)
    nc = bld.build(upto)
    maps = _host_inputs(inputs)
    maps = [{k: v for k, v in m.items() if k in bld.din} for m in maps]
    missing = [k for k in bld.din if k not in maps[0]]
    assert not missing, missing
    res = run_bass_kernel_spmd(nc, maps, core_ids=list(range(8)))
    return res.results


def kernel(**inputs):
    results = run(inputs)
    outp = np.zeros((2, S_LEN, D), np.float32)
    for core in range(8):
        b, r = core // 4, core % 4
        o = results[core]["out"]
        for j in range(16):
            qi = 4 * j + r
            outp[b, 128 * qi:128 * qi + 128] = o[128 * j:128 * j + 128]
    return outp
```

```python
import numpy as np
from contextlib import ExitStack
import concourse.bass as bass
import concourse.mybir as mybir
from concourse.bass_utils import run_bass_kernel_spmd

F32 = mybir.dt.float32
BF16 = mybir.dt.bfloat16
AF = mybir.ActivationFunctionType
ALU = mybir.AluOpType

D = 1024
S_LEN = 8192
NT = 16
NB = 64
N_IN = 3348
DFF = 2816
EPS = 1e-6
O_GQ, O_GK, O_GV, O_GZ, O_GA, O_GB, O_NQ, O_KC, O_VC, O_KSL, O_VSL, O_KWN, O_VWN, O_NG = (
    0, 512, 1024, 1536, 2048, 2052, 2056, 2568, 2696, 2824, 2952, 3080, 3208, 3336)
BIG = 30000.0


class Buf:
    __slots__ = ("name", "w", "rs", "const", "excl")

    def __init__(self, name, const=False, excl=False):
        self.name = name
        self.w = None
        self.rs = []
        self.const = const
        self.excl = excl


class V:
    __slots__ = ("ap", "bs")

    def __init__(self, ap, bs):
        self.ap = ap
        self.bs = bs if isinstance(bs, (list, tuple)) else [bs]

    def bitcast(self, dt):
        return V(self.ap.bitcast(dt), self.bs)

    def re(self, pat, **kw):
        return V(self.ap.rearrange(pat, **kw), self.bs)

    def bc(self, shape):
        return V(self.ap.broadcast_to(shape), self.bs)

    def __getitem__(self, k):
        return V(self.ap[k], self.bs)


class Tl:
    def __init__(self, t, b):
        self.t = t
        self.b = b

    def __getitem__(self, k):
        return V(self.t[k], self.b)


class Op:
    __slots__ = ("eng", "fn", "deps", "dmaw", "signal", "sigval", "grp")


class DGroup:
    def __init__(self, name, sem, full=False):
        self.name = name
        self.sem = sem
        self.count = 0
        self.full = full


ENGS = ("pe", "act", "dve", "pool", "sp")


class Sched:
    def __init__(self, nc, es):
        self.nc = nc
        self.es = es
        self.eng = {"pe": nc.tensor, "act": nc.scalar, "dve": nc.vector, "pool": nc.gpsimd, "sp": nc.sync}
        self.sem = {e: es.enter_context(nc.semaphore("s_" + e)) for e in ENGS}
        self.cnt = {e: 0 for e in ENGS}
        self.seen = {e: {} for e in ENGS}
        self.ops = []
        self.bufs = []
        self.groups = []
        self.nins = 0

    def buf(self, name, const=False, excl=False):
        b = Buf(name, const, excl)
        self.bufs.append(b)
        return b

    def group(self, name, full=False):
        g = DGroup(name, self.es.enter_context(self.nc.semaphore("g_" + name)), full)
        self.groups.append(g)
        return g

    def op(self, eng, fn, r=(), w=(), grp=None):
        o = Op()
        o.eng = eng
        o.fn = fn
        o.signal = False
        o.sigval = None
        o.grp = grp
        deps = []
        for b in r:
            if b.w is not None:
                deps.append(b.w)
            if b.excl:
                deps.extend(x for x in b.rs if x.eng != eng)
        for b in w:
            if b.w is not None:
                deps.append(b.w)
            deps.extend(b.rs)
        seen = set()
        dd = []
        for d in deps:
            if id(d) in seen or d is o:
                continue
            seen.add(id(d))
            if d.eng == "pe" and eng == "pe":
                continue
            if grp is not None and grp.full and d.grp is grp:
                continue
            dd.append(d)
        o.deps = dd
        o.dmaw = {}
        for d in dd:
            if d.grp is None:
                d.signal = True
            else:
                o.dmaw[d.grp.name] = d.grp.count
        if grp is not None:
            grp.count += 1
        for b in w:
            b.w = o
            b.rs = []
        for b in r:
            if not b.const and b.w is not o:
                b.rs.append(o)
        self.ops.append(o)
        return o

    def flush(self, barrier=True):
        if barrier:
            last = {}
            for o in self.ops:
                last[o.eng] = o
            for e, o in last.items():
                if o.grp is None:
                    o.signal = True
        for o in self.ops:
            e = self.eng[o.eng]
            for d in o.deps:
                if d.grp is not None:
                    key = "g_" + d.grp.name
                    val = 16 * (d.grp.count if d.grp.full else o.dmaw[d.grp.name])
                    sem = d.grp.sem
                else:
                    key = d.eng
                    val = d.sigval
                    sem = self.sem[d.eng]
                    assert val is not None, (o.eng, d.eng)
                if self.seen[o.eng].get(key, 0) >= val:
                    continue
                e.wait_ge(sem, val)
                self.seen[o.eng][key] = val
            ins = o.fn(e)
            self.nins += 1
            if o.grp is not None:
                ins.then_inc(o.grp.sem, 16)
            elif o.signal:
                self.cnt[o.eng] += 1
                o.sigval = self.cnt[o.eng]
                ins.then_inc(self.sem[o.eng], 1)
        self.ops = []
        if barrier:
            for en in ENGS:
                e = self.eng[en]
                for e2 in ENGS:
                    if e2 == en or e2 == "sp":
                        continue
                    if self.cnt[e2] > self.seen[en].get(e2, 0):
                        e.wait_ge(self.sem[e2], self.cnt[e2])
                        self.seen[en][e2] = self.cnt[e2]
                for g in self.groups:
                    key = "g_" + g.name
                    if 16 * g.count > self.seen[en].get(key, 0):
                        e.wait_ge(g.sem, 16 * g.count)
                        self.seen[en][key] = 16 * g.count
            for b in self.bufs:
                b.w = None
                b.rs = []

    @staticmethod
    def _bs(*vs):
        out = []
        for v in vs:
            if isinstance(v, V):
                for b in v.bs:
                    if b not in out:
                        out.append(b)
        return out

    @staticmethod
    def _a(v):
        return v.ap if isinstance(v, V) else v

    def mm(self, out, lhsT, rhs, start=True, stop=True):
        o_, l_, r_ = out.ap, lhsT.ap, rhs.ap
        self.op("pe", lambda e: e.matmul(o_, lhsT=l_, rhs=r_, start=start, stop=stop),
                r=self._bs(lhsT, rhs), w=self._bs(out))

    def mm_(self, out, lhsT, rhs, start=True, stop=True, skip=False):
        o_, l_, r_ = out.ap, lhsT.ap, rhs.ap
        self.op("pe", lambda e: e.matmul(o_, lhsT=l_, rhs=r_, start=start, stop=stop, skip_group_check=skip),
                r=self._bs(lhsT, rhs), w=self._bs(out))

    def tr(self, out, in_, ident):
        o_, i_, d_ = out.ap, in_.ap, ident.ap
        self.op("pe", lambda e: e.transpose(o_, i_, d_), r=self._bs(in_, ident), w=self._bs(out))

    def act(self, out, in_, func, bias=None, scale=None, accum=None, eng="act"):
        kw = {}
        if bias is not None:
            kw["bias"] = self._a(bias)
        if scale is not None:
            kw["scale"] = self._a(scale)
        if accum is not None:
            kw["accum_out"] = accum.ap
        o_, i_ = out.ap, in_.ap
        self.op("act", lambda e: e.activation(out=o_, in_=i_, func=func, **kw),
                r=self._bs(in_, bias, scale), w=self._bs(out, accum))

    def tt(self, eng, out, in0, in1, op):
        o_, a_, b_ = out.ap, in0.ap, in1.ap
        self.op(eng, lambda e: e.tensor_tensor(out=o_, in0=a_, in1=b_, op=op),
                r=self._bs(in0, in1), w=self._bs(out))

    def ts(self, eng, out, in0, s1, s2=None, op0=ALU.mult, op1=None):
        o_, a_ = out.ap, in0.ap
        s1_, s2_ = self._a(s1), self._a(s2)
        kw = {}
        if op1 is not None:
            kw["op1"] = op1
        self.op(eng, lambda e: e.tensor_scalar(out=o_, in0=a_, scalar1=s1_, scalar2=s2_, op0=op0, **kw),
                r=self._bs(in0, s1, s2), w=self._bs(out))

    def stt(self, out, in0, scalar, in1, op0, op1):
        o_, a_, b_ = out.ap, in0.ap, in1.ap
        s_ = self._a(scalar)
        self.op("dve", lambda e: e.scalar_tensor_tensor(out=o_, in0=a_, scalar=s_, in1=b_, op0=op0, op1=op1),
                r=self._bs(in0, scalar, in1), w=self._bs(out))

    def copy(self, eng, out, in_):
        o_, i_ = out.ap, in_.ap
        if eng == "act":
            self.op("act", lambda e: e.copy(out=o_, in_=i_), r=self._bs(in_), w=self._bs(out))
        else:
            self.op(eng, lambda e: e.tensor_copy(out=o_, in_=i_), r=self._bs(in_), w=self._bs(out))

    def recip(self, out, in_):
        o_, i_ = out.ap, in_.ap
        self.op("dve", lambda e: e.reciprocal(out=o_, in_=i_), r=self._bs(in_), w=self._bs(out))

    def memset(self, eng, out, val):
        o_ = out.ap
        self.op(eng, lambda e: e.memset(o_, val), r=[], w=self._bs(out))

    def dma(self, out, in_, grp, eng="sp"):
        o_, i_ = self._a(out), self._a(in_)
        self.op(eng, lambda e: e.dma_start(out=o_, in_=i_), r=self._bs(in_), w=self._bs(out), grp=grp)


class Bank:
    def __init__(self, t, q):
        self.t = t
        self.q = q

    def f(self, c0, c1, p0=0, p1=128):
        return V(self.t[p0:p1, c0:c1], self.q[0:1])

    def b(self, c0, c1, p0=0, p1=128):
        return V(self.t[p0:p1, :].bitcast(BF16)[:, c0:c1], self.q[0:1])


W1_SEGS = ((0, 1536, 0), (2048, 2056, 1536), (2568, 3336, 1544))
W1_N = 2312
C_AB, C_KC, C_VC, C_KSL, C_VSL, C_KWN, C_VWN = 1536, 1544, 1672, 1800, 1928, 2056, 2184


class Builder:
    def __init__(self, debug=False, ntiles=NT):
        self.debug = debug
        self.ntiles = ntiles
        import os
        self.stop = int(os.environ.get('K_STOP', '0'))
        self.var = int(os.environ.get('K_VAR', '0'))
        self.halo = int(os.environ.get('K_HALO', '0'))
        self.nc = bass.Bass("TRN2", target_bir_lowering=False)
        self.din = {}
        self.dout = {}

    def inp(self, name, shape, dt=F32):
        self.din[name] = self.nc.dram_tensor(name, list(shape), dt, kind="ExternalInput").ap()
        return self.din[name]

    def outp(self, name, shape, dt=F32):
        self.dout[name] = self.nc.dram_tensor(name, list(shape), dt, kind="ExternalOutput").ap()
        return self.dout[name]

    def scratch(self, name, shape, dt=F32):
        if self.debug:
            return self.outp(name, shape, dt)
        return self.nc.dram_tensor(name, list(shape), dt, kind="Internal").ap()

    def sb(self, es, name, shape, dt=F32, const=False):
        t = es.enter_context(self.nc.sbuf_tensor(name, list(shape), dt))
        return Tl(t, self.S.buf(name, const))

    def cload(self, es, name, shape, grp):
        d = self.inp(name, shape)
        t = self.sb(es, "c_" + name, shape, F32, const=True)
        idx = tuple(slice(None) for _ in shape)
        self.S.dma(t[idx], d[idx], grp)
        return t

    def build(self, upto=9):
        nc = self.nc
        I = self.inp
        self.xb = I("xb", [S_LEN, D])
        self.xo = I("xo", [2048, D])
        cT = I("cT", [128, 8])
        ada_w = I("ada_w", [D, 6 * D])
        self.w_in = I("w_in", [D, N_IN])
        self.out = self.outp("out", [2048, D])
        self.o_own = self.scratch("o_own", [16, 128, 512])
        self.kslT_d = self.scratch("kslT_d", [128, S_LEN], BF16)
        self.kwnT_d = self.scratch("kwnT_d", [128, S_LEN], BF16)
        self.kcT_d = self.scratch("kcT_d", [128, S_LEN], BF16)
        self.vcT_d = self.scratch("vcT_d", [128, S_LEN], BF16)
        self.vsl_d = self.scratch("vsl_d", [128, NB, 128], BF16)
        self.vwn_d = self.scratch("vwn_d", [128, NB, 128], BF16)
        self.mix_d = self.scratch("mix_d", [128, 8, 2048], BF16)
        self.oprev_d = self.scratch("oprev_d", [16, 2, 512])
        self.mixh_d = self.scratch("mixh_d", [128, 8, 32], BF16)

        with ExitStack() as top:
            S = self.S = Sched(nc, top)
            self.g_const = g_const = S.group("const", full=True)
            self.g_out = S.group("outw")
            self.idf = idf = self.cload(top, "identf", [128, 128], g_const)
            self.idb = idb = self.sb(top, "idb", [128, 128], BF16)
            S.copy("dve", idb[:, :], idf[:, :])
            self.modT = modT = self.sb(top, "modT", [128, 48])
            self.s1 = s1 = self.sb(top, "s1", [128, 8])
            self.s2 = s2 = self.sb(top, "s2", [128, 8])
            n1 = self.cload(top, "n1w", [128, 8], g_const)
            n2 = self.cload(top, "n2w", [128, 8], g_const)
            self.pb = []
            for i in range(8):
                t = top.enter_context(nc.psum_tensor(f"pb{i}", [128, 512], F32))
                self.pb.append(Bank(t, [S.buf(f"pb{i}", excl=True)]))
            pb = self.pb
            self.bankA = [pb[2], pb[3]]
            self.bankB = [pb[4], pb[5]]
            self.bankC = [pb[0], pb[1]]

            with ExitStack() as es:
                ct = self.sb(es, "ct", [128, 8])
                sc = self.sb(es, "sc", [128, 8])
                abT = self.cload(es, "ada_bT", [128, 48], g_const)
                S.dma(ct[:, :], cT[:, :], g_const)
                S.act(sc[:, :], ct[:, :], AF.Silu)
                aw = [self.sb(es, f"aw{i}", [128, 6 * D]) for i in range(2)]
                g_aw = [S.group(f"aw{i}") for i in range(2)]
                for k in range(8):
                    sl = k % 2
                    for hh in range(4):
                        S.dma(aw[sl][:, hh * 1536:(hh + 1) * 1536],
                              ada_w[k * 128:(k + 1) * 128, hh * 1536:(hh + 1) * 1536], g_aw[sl])
                    pm = pb[k % 2]
                    for cc in range(48):
                        S.mm(pm.f(cc, cc + 1), lhsT=aw[sl][:, cc * 128:(cc + 1) * 128], rhs=sc[:, k:k + 1])
                    S.tt("dve", modT[:, :], pm.f(0, 48), (abT if k == 0 else modT)[:, :], ALU.add)
                S.stt(s1[:, :], modT[:, 8:16], 1.0, n1[:, :], ALU.add, ALU.mult)
                S.stt(s2[:, :], modT[:, 32:40], 1.0, n2[:, :], ALU.add, ALU.mult)
                S.flush()

            if upto >= 1:
                self.phase1()
            keep = dict(qnT=self.sb(top, "qnT", [128, 4, 2048], BF16), gates=self.sb(top, "gates", [128, 16, 12]),
                        qnTh=self.sb(top, "qnTh", [128, 4, 32], BF16), gatesh=self.sb(top, "gatesh", [32, 12]))
            if upto >= 2:
                self.phase2(keep)
                if self.debug:
                    S.dma(self.outp("d_qnT", [128, 4, 2048], BF16)[:, :, :], keep["qnT"][:, :, :], self.g_out)
                    S.dma(self.outp("d_gates", [128, 16, 12])[:, :, :], keep["gates"][:, :, :], self.g_out)
            if upto >= 3:
                S.flush()
                self.phase3(keep)
            if upto >= 4:
                S.flush()
                if self.debug:
                    self.dx1 = self.outp("d_x1", [4, 128, 4, D])
                self.phase45()
            S.flush()
        return nc

    def front(self, F, xt, nb, hT, c0, scol, sh_c0, b0=0):
        S = self.S
        pb = self.pb
        ssq, rt, rstd, junk, xs = F["ssq"], F["rt"], F["rstd"], F["junk"], F["xs"]
        for b2 in range(nb):
            S.act(junk[:, :], xt[:, b0 + b2, :], AF.Square, accum=ssq[:, b2:b2 + 1])
        S.act(rt[:, 0:nb], ssq[:, 0:nb], AF.Ln, bias=EPS, scale=1.0 / D)
        S.act(rstd[:, 0:nb], rt[:, 0:nb], AF.Exp, scale=-0.5)
        for b2 in range(nb):
            S.ts("pool", xs[:, b2, :], xt[:, b0 + b2, :], rstd[:, b2:b2 + 1], 1.0, ALU.mult, ALU.mult)
        w = nb * 128
        for half in range(2):
            bank = pb[half]
            for k in range(half * 4, half * 4 + 4):
                off = (k % 4) * 256
                for b2 in range(nb):
                    S.tr(bank.b(off + b2 * 128, off + (b2 + 1) * 128), xs[:, b2, k * 128:(k + 1) * 128],
                         self.idb[:, :])
            for k in range(half * 4, half * 4 + 4):
                off = (k % 4) * 256
                S.ts("dve", hT[:, k, c0:c0 + w], bank.b(off, off + w), scol[:, k:k + 1],
                     self.modT[:, sh_c0 + k:sh_c0 + k + 1], ALU.mult, ALU.add)

    def front32(self, F, xt, hT, scol, sh_c0):
        S = self.S
        bank = self.pb[0]
        ssq, rt, rstd, junk, xs = F["ssq"], F["rt"], F["rstd"], F["junk"], F["xs"]
        S.act(junk[0:32, :], xt[:, :], AF.Square, accum=ssq[0:32, 0:1])
        S.act(rt[0:32, 0:1], ssq[0:32, 0:1], AF.Ln, bias=EPS, scale=1.0 / D)
        S.act(rstd[0:32, 0:1], rt[0:32, 0:1], AF.Exp, scale=-0.5)
        S.ts("pool", xs[0:32, 0, :], xt[:, :], rstd[0:32, 0:1], 1.0, ALU.mult, ALU.mult)
        for k in range(8):
            S.tr(bank.b(k * 32, (k + 1) * 32), xs[0:32, 0, k * 128:(k + 1) * 128], self.idb[0:32, 0:32])
        for k in range(8):
            S.ts("dve", hT[:, k, :], bank.b(k * 32, (k + 1) * 32), scol[:, k:k + 1],
                 self.modT[:, sh_c0 + k:sh_c0 + k + 1], ALU.mult, ALU.add)

    def phase1(self):
        S = self.S
        nc = self.nc
        pb = self.pb
        gc_ = S.group("const1", full=True)
        idb, idf = self.idb, self.idf
        with ExitStack() as es:
            tri2 = self.cload(es, "tri2", [128, 128], gc_)
            blk2 = self.cload(es, "blk2", [128, 128], gc_)
            onesf = self.cload(es, "onesf", [128, 128], gc_)
            cind = self.cload(es, "cind", [128, 2], gc_)
            ma4 = self.cload(es, "ma4", [128, 512], gc_)
            mb4 = self.cload(es, "mb4", [128, 512], gc_)
            sel = self.cload(es, "sel", [128, 4], gc_)
            hsel = self.cload(es, "hsel", [128, 4], gc_)
            cw = self.cload(es, "gcw", [128, 48], gc_)
            dtb = self.cload(es, "dtb", [128, 4], gc_)
            alog = self.cload(es, "alog", [128, 4], gc_)
            kslw = self.cload(es, "kslw", [128, 1], gc_)
            kwnw = self.cload(es, "kwnw", [128, 1], gc_)
            negones = self.sb(es, "negones", [128, 128])
            S.ts("dve", negones[:, :], onesf[:, :], -1.0, None, ALU.mult)
            onesb = self.sb(es, "onesb", [128, 128], BF16)
            S.copy("dve", onesb[:, :], onesf[:, :])
            i4b = self.sb(es, "i4b", [128, 512], BF16)
            for h in range(4):
                S.copy("dve", i4b[:, h * 128:(h + 1) * 128], idf[:, :])
            negA = self.sb(es, "negA", [128, 4])
            S.act(negA[:, :], alog[:, :], AF.Exp)
            S.ts("dve", negA[:, :], negA[:, :], -1.0, None, ALU.mult)

            winb = self.sb(es, "winb", [128, 8, W1_N], BF16)
            with ExitStack() as es2:
                wst = [self.sb(es2, f"wst{i}", [128, W1_N]) for i in range(2)]
                g_w = [S.group(f"wst{i}") for i in range(2)]
                for k in range(8):
                    sl = k % 2
                    for (a, b_, o) in W1_SEGS:
                        S.dma(wst[sl][:, o:o + (b_ - a)], self.w_in[k * 128:(k + 1) * 128, a:b_], g_w[sl])
                    S.copy("dve", winb[:, k, 0:1024], wst[sl][:, 0:1024])
                    S.copy("pool", winb[:, k, 1024:W1_N], wst[sl][:, 1024:W1_N])
                S.flush()

            F = dict(ssq=self.sb(es, "f_ssq", [128, 4]), rt=self.sb(es, "f_rt", [128, 4]),
                     rstd=self.sb(es, "f_rstd", [128, 4]), junk=self.sb(es, "f_junk", [128, 1024], BF16),
                     xs=self.sb(es, "f_xs", [128, 2, 1024], BF16))
            xt = [self.sb(es, f"xt{i}", [128, 2, 1024]) for i in range(2)]
            g_x = [S.group(f"xt{i}") for i in range(2)]
            hT = self.sb(es, "hT", [128, 8, 512], BF16)
            pre = [self.sb(es, f"pre{i}", [128, 515]) for i in range(3)]
            hist = self.sb(es, "hist", [128, 12, 3])
            S.memset("pool", hist[:, :, :], 0.0)
            acc = [self.sb(es, f"cacc{i}", [128, 512]) for i in range(3)]
            yfs = [self.sb(es, f"yfs{i}", [128, 512], BF16 if i < 8 else F32) for i in range(10)]
            sq = [self.sb(es, f"sq{i}", [128, 512], BF16) for i in range(3)]
            rtt = [self.sb(es, f"rtt{i}", [128, 512]) for i in range(3)]
            qT = self.sb(es, "qT", [128, 4, 512], BF16)
            kT = self.sb(es, "kT", [128, 4, 512], BF16)
            vT = self.sb(es, "vT", [128, 4, 512], BF16)
            st_f = [[self.sb(es, f"stf{c}_{i}", [128, 512], BF16) for i in range(2)] for c in range(4)]
            st_v = [[self.sb(es, f"stv{c}_{i}", [128, 4, 128], BF16) for i in range(2)] for c in range(2)]
            g_sf = [[S.group(f"sf{c}_{i}") for i in range(2)] for c in range(4)]
            g_sv = [[S.group(f"sv{c}_{i}") for i in range(2)] for c in range(2)]
            ab = self.sb(es, "ab", [128, 4, 8])
            self.sm4 = sm4 = self.sb(es, "sm4", [128, 4, 64])
            S32 = self.sb(es, "S32", [128, 4, 128])
            Sb = [self.sb(es, f"Sb{i}", [128, 4, 128], BF16) for i in range(2)]
            self.cidx = 0
            S.memset("pool", S32[:, :, :], 0.0)
            S.memset("pool", Sb[0][:, :, :], 0.0)
            oacc = [self.sb(es, f"oacc{i}", [128, 512]) for i in range(2)]
            g_o = [S.group(f"oacc{i}") for i in range(2)]
            oph = [self.sb(es, f"oph{i}", [128, 512]) for i in range(2)]
            g_oh = [S.group(f"oph{i}") for i in range(2)]
            G = []
            for s_ in range(2):
                g = {}
                for nm, shp, dt in (("TG", [128, 4, 128], F32), ("Xs", [128, 512], F32),
                                    ("XA", [128, 512], F32), ("XB", [128, 512], F32), ("EG", [128, 4, 128], F32),
                                    ("Nb", [128, 4, 128], BF16), ("aT", [128, 4, 128], BF16),
                                    ("qdT", [128, 4, 128], BF16), ("kw", [128, 4, 128], BF16),
                                    ("kd", [128, 4, 128], BF16), ("vb", [128, 4, 128], BF16),
                                    ("NTb", [128, 4, 128], BF16), ("RTb", [128, 4, 128], BF16),
                                    ("P0", [128, 4, 128], BF16), ("P1", [128, 4, 128], BF16),
                                    ("Q0", [128, 4, 128], BF16), ("Q1", [128, 4, 128], BF16),
                                    ("u", [128, 4, 128], F32), ("wTb", [128, 4, 128], BF16),
                                    ("vn", [128, 4, 128], BF16)):
                    g[nm] = self.sb(es, f"g{s_}_{nm}", shp, dt)
                G.append(g)

            xrows = self.xb.rearrange("(n b p) d -> n p b d", b=2, p=128)

            def load_x(n):
                S.dma(xt[n % 2][:, :, :], xrows[n], g_x[n % 2])

            load_x(0)
            for ti in range(self.ntiles):
                for hf in range(2):
                    n = 2 * ti + hf
                    if n + 1 < 2 * NT:
                        load_x(n + 1)
                    self.front(F, xt[n % 2], 2, hT, hf * 256, self.s1, 0)
                if self.stop == 1:
                    continue
                nsa_ch = ((C_KC, self.kcT_d, None), (C_VC, self.vcT_d, None), (C_KSL, self.kslT_d, kslw),
                          (C_KWN, self.kwnT_d, kwnw))

                def st1(c, pos):
                    bank = pb[2 + (pos % 2)]
                    coff = c * 128 if c < 12 else nsa_ch[c - 12][0]
                    for k in range(8):
                        S.mm(bank.f(0, 512), lhsT=winb[:, k, coff:coff + 128], rhs=hT[:, k, :],
                             start=(k == 0), stop=(k == 7))
                    if c < 12:
                        p_ = pre[pos % 3]
                        S.copy("pool", p_[:, 0:3], hist[:, c, :])
                        S.copy("act", p_[:, 3:515], bank.f(0, 512))
                        S.copy("pool", hist[:, c, :], p_[:, 512:515])
                        a_ = acc[pos % 3]
                        S.ts("pool", a_[:, :], p_[:, 0:512], cw[:, c * 4:c * 4 + 1], 1.0, ALU.mult, ALU.mult)
                        for j in range(1, 4):
                            S.stt(a_[:, :], p_[:, j:j + 512], cw[:, c * 4 + j:c * 4 + j + 1], a_[:, :], ALU.mult, ALU.add)
                    else:
                        ci = c - 12
                        wcol = nsa_ch[ci][2]
                        if wcol is None:
                            S.copy("act", st_f[ci][ti % 2][:, :], bank.f(0, 512))
                            S.dma(nsa_ch[ci][1][:, ti * 512:(ti + 1) * 512], st_f[ci][ti % 2][:, :], g_sf[ci][ti % 2])
                        else:
                            S.copy("act", yfs[c - 6][:, :], bank.f(0, 512))

                def st2(c, pos):
                    h = c % 4
                    if c >= 12:
                        return
                    if c >= 8:
                        S.act(vT[:, h, :], acc[pos % 3][:, :], AF.Silu)
                        return
                    S.act(yfs[c][:, :], acc[pos % 3][:, :], AF.Silu)

                def st3(c, i_):
                    h = c % 4
                    bk2 = pb[4 + (i_ % 2)]
                    yi = c if c < 8 else c - 6
                    S.act(sq[i_ % 3][:, :], yfs[yi][:, :], AF.Square)
                    S.mm(bk2.f(0, 512), lhsT=onesb[:, :], rhs=sq[i_ % 3][:, :])
                    r_ = rtt[i_ % 3]
                    if c < 4:
                        S.act(r_[:, :], bk2.f(0, 512), AF.Ln, bias=EPS * 128.0, scale=128.0)
                    elif c < 8:
                        S.act(r_[:, :], bk2.f(0, 512), AF.Ln, bias=EPS, scale=1.0)
                    else:
                        S.act(r_[:, :], bk2.f(0, 512), AF.Ln, bias=EPS, scale=1.0 / 128.0)
                    S.act(r_[:, :], r_[:, :], AF.Exp, scale=-0.5)
                    if c < 8:
                        S.tt("pool", (qT if c < 4 else kT)[:, h, :], yfs[yi][:, :], r_[:, :], ALU.mult)
                    else:
                        ci = c - 12
                        st = st_f[ci][ti % 2]
                        S.stt(st[:, :], yfs[yi][:, :], nsa_ch[ci][2][:, 0:1], r_[:, :], ALU.mult, ALU.mult)
                        S.dma(nsa_ch[ci][1][:, ti * 512:(ti + 1) * 512], st[:, :], g_sf[ci][ti % 2])

                order = [12, 13, 14, 15] + list(range(12))
                for i in range(len(order) + 1):
                    if i < len(order):
                        st1(order[i], i)
                    if 0 <= i - 1 < len(order):
                        st2(order[i - 1], i - 1)
                if self.stop == 3:
                    continue
                for blk in range(4):
                    bank = pb[6]
                    for vi, coff in enumerate((C_VSL, C_VWN)):
                        for k in range(8):
                            if self.var == 4:
                                break
                            S.mm(bank.f(vi * 128, (vi + 1) * 128), lhsT=hT[:, k, blk * 128:(blk + 1) * 128],
                                 rhs=winb[:, k, coff:coff + 128], start=(k == 0), stop=(k == 7))
                    for k in range(8):
                        if self.var == 1:
                            break
                        S.mm(bank.f(256, 264), lhsT=hT[:, k, blk * 128:(blk + 1) * 128],
                             rhs=winb[:, k, C_AB:C_AB + 8], start=(k == 0), stop=(k == 7))
                    if self.var != 3:
                        S.copy("act", st_v[0][ti % 2][:, blk, :], bank.f(0, 128))
                        S.copy("act", st_v[1][ti % 2][:, blk, :], bank.f(128, 256))
                    if self.var != 5:
                        S.copy("dve", ab[:, blk, :], bank.f(256, 264))
                if self.var != 2:
                    S.dma(self.vsl_d[:, ti * 4:(ti + 1) * 4, :], st_v[0][ti % 2][:, :, :], g_sv[0][ti % 2])
                    S.dma(self.vwn_d[:, ti * 4:(ti + 1) * 4, :], st_v[1][ti % 2][:, :, :], g_sv[1][ti % 2])

                for i_, c in enumerate([14, 15, 0, 1, 2, 3, 4, 5, 6, 7]):
                    st3(c, i_)
                if self.stop == 4:
                    continue
                for pair in range(2):
                    blks = (2 * pair, 2 * pair + 1)
                    if pair == 0:
                        self.gdn_small(sm4, ab, tri2, blk2, onesf, cind, dtb, negA)
                    for s_, blk in enumerate(blks):
                        self.gdn_local_1(G[s_], s_, blk, sm4, tri2, negones, onesf, ma4, mb4, qT, kT, vT)
                    if self.stop == 5:
                        continue
                    self.gdn_solve(G, blks, i4b)
                    if self.stop == 6:
                        continue
                    for s_, blk in enumerate(blks):
                        self.gdn_uw(G[s_], s_)
                    if self.stop == 7:
                        continue
                    for s_, blk in enumerate(blks):
                        oa = oacc[ti % 2]
                        self.gdn_recur(G[s_], S32, Sb, sel, blk, oa, hsel, oph[ti % 2])
                S.dma(self.o_own[ti], oacc[ti % 2][:, :], g_o[ti % 2])
                S.dma(self.oprev_d[ti], oph[ti % 2][126:128, :], g_oh[ti % 2])
            S.flush()

    def gdn_small(self, sm4, ab, tri2, blk2, onesf, cind, dtb, negA):
        S = self.S
        bS = self.pb[6]
        x_, t_, gg, be, nb_ = (sm4[:, :, 0:4], sm4[:, :, 4:8], sm4[:, :, 8:12], sm4[:, :, 12:16], sm4[:, :, 16:20])
        gci, gcv, gl, egc, ekd, bw, glS = (sm4[:, :, 20:28], sm4[:, :, 28:32], sm4[:, :, 32:36], sm4[:, :, 36:40],
                                           sm4[:, :, 40:44], sm4[:, :, 44:48], sm4[:, :, 48:56])
        bc = lambda t: V(t.t[:, :].unsqueeze(1).broadcast_to([128, 4, 4]), [t.b])
        S.tt("dve", x_, ab[:, :, 0:4], bc(dtb), ALU.add)
        S.stt(t_, x_, -1.0, x_, ALU.mult, ALU.max)
        S.act(t_, t_, AF.Exp, scale=-1.0)
        S.act(t_, t_, AF.Ln, bias=1.0)
        S.stt(t_, x_, 0.0, t_, ALU.max, ALU.add)
        S.tt("dve", gg, t_, bc(negA), ALU.mult)
        S.act(be, ab[:, :, 4:8], AF.Exp, scale=-1.0)
        S.ts("dve", be, be, 1.0, None, ALU.add)
        S.recip(be, be)
        S.ts("dve", nb_, be, -1.0, None, ALU.mult)
        for i in range(2):
            for blk in range(4):
                S.ts("dve", sm4[:, blk, 20:28].re("p (h i) -> p h i", i=2)[:, :, i], sm4[:, blk, 8:12], cind[:, i:i + 1],
                     None, ALU.mult)
        for blk in range(4):
            S.mm(bS.f(384 + blk * 4, 388 + blk * 4), lhsT=tri2[:, :], rhs=sm4[:, blk, 8:12])
            S.mm(bS.f(400 + blk * 4, 404 + blk * 4), lhsT=blk2[:, :], rhs=sm4[:, blk, 8:12])
            S.mm(bS.f(416 + blk * 8, 424 + blk * 8), lhsT=onesf[:, :], rhs=sm4[:, blk, 20:28])
        S.copy("dve", gcv, bS.f(384, 400).re("p (b h) -> p b h", b=4))
        S.copy("dve", gl, bS.f(400, 416).re("p (b h) -> p b h", b=4))
        S.act(glS, bS.f(416, 448).re("p (b c) -> p b c", b=4), AF.Exp)
        S.act(egc, gcv, AF.Exp)
        S.tt("dve", ekd, gl, gcv, ALU.subtract)
        S.act(ekd, ekd, AF.Exp)
        S.tt("dve", bw, be, egc, ALU.mult)

    def gdn_local_1(self, g, s_, blk, sm4, tri2, negones, onesf, ma4, mb4, qT, kT, vT):
        S = self.S
        pb = self.pb
        cs = slice(blk * 128, (blk + 1) * 128)
        sm = V(sm4.t[:, blk, :], [sm4.b])
        gg, be, nb_, gcv, ekd, bw = (sm[:, 8:12], sm[:, 12:16], sm[:, 16:20], sm[:, 28:32], sm[:, 40:44], sm[:, 44:48])
        bK = self.bankA[s_]
        bQ = self.bankB[s_]
        bX = self.bankC[s_]
        bT = pb[7]
        for h in range(4):
            S.mm(bK.f(h * 128, (h + 1) * 128), lhsT=kT[:, h, cs], rhs=kT[:, h, cs])
        for h in range(4):
            S.mm(bQ.f(h * 128, (h + 1) * 128), lhsT=kT[:, h, cs], rhs=qT[:, h, cs])
        for h in range(4):
            S.tr(bT.b(h * 128, (h + 1) * 128), kT[:, h, cs], self.idb[:, :])
        for h in range(4):
            S.tr(bT.b(512 + h * 128, 512 + (h + 1) * 128), vT[:, h, cs], self.idb[:, :])
        for h in range(4):
            S.ts("pool", g["TG"][:, h, :], tri2[:, :], gg[:, h:h + 1], 1.0, ALU.mult, ALU.mult)
        for h in range(4):
            S.mm(bX.f(h * 128, (h + 1) * 128), lhsT=onesf[:, :], rhs=g["TG"][:, h, :], start=True, stop=False)
            S.mm(bX.f(h * 128, (h + 1) * 128), lhsT=g["TG"][:, h, :], rhs=negones[:, :], start=False, stop=True)
        S.copy("act", g["Xs"][:, :], bX.f(0, 512))
        for h in range(4):
            S.act(g["EG"][:, h, :], bX.f(h * 128, (h + 1) * 128), AF.Exp, bias=gcv[:, h:h + 1])
        S.tt("pool", g["XA"][:, :], g["Xs"][:, :], ma4[:, :], ALU.add)
        S.tt("pool", g["XB"][:, :], g["Xs"][:, :], mb4[:, :], ALU.add)
        S.act(g["XA"][:, :], g["XA"][:, :], AF.Exp, scale=-1.0)
        S.act(g["XB"][:, :], g["XB"][:, :], AF.Exp)
        for h in range(4):
            S.stt(g["Nb"][:, h, :], bK.f(h * 128, (h + 1) * 128), nb_[:, h:h + 1],
                  g["XA"][:, h * 128:(h + 1) * 128], ALU.mult, ALU.mult)
        S.tt("dve", g["aT"][:, :, :].re("p h c -> p (h c)"), bQ.f(0, 512), g["XB"][:, :], ALU.mult)
        S.tt("pool", g["qdT"][:, :, :], qT[:, :, cs], g["EG"][:, :, :], ALU.mult)
        kt3 = bT.b(0, 512).re("p (h d) -> p h d", h=4)
        vt3 = bT.b(512, 1024).re("p (h d) -> p h d", h=4)
        S.tt("dve", g["kw"][:, :, :], kt3, V(bw.ap.unsqueeze(2).broadcast_to([128, 4, 128]), bw.bs), ALU.mult)
        S.tt("dve", g["kd"][:, :, :], kt3, V(ekd.ap.unsqueeze(2).broadcast_to([128, 4, 128]), ekd.bs), ALU.mult)
        S.tt("dve", g["vb"][:, :, :], vt3, V(be.ap.unsqueeze(2).broadcast_to([128, 4, 128]), be.bs), ALU.mult)

    def gdn_solve(self, G, blks, i4b):
        S = self.S
        idb = self.idb
        for s_ in range(len(blks)):
            g = G[s_]
            bA, bC = self.bankA[s_], self.bankC[s_]
            for h in range(4):
                S.mm(bA.f(h * 128, (h + 1) * 128), lhsT=g["Nb"][:, h, :], rhs=idb[:, :])
            S.copy("act", g["NTb"][:, :, :].re("p h c -> p (h c)"), bA.f(0, 512))
            S.mm(bC.f(0, 512), lhsT=idb[:, :], rhs=i4b[:, :], start=True, stop=False)
            for h in range(4):
                S.mm(bC.f(h * 128, (h + 1) * 128), lhsT=g["Nb"][:, h, :], rhs=idb[:, :], start=False, stop=(h == 3))
            S.copy("act", g["RTb"][:, :, :].re("p h c -> p (h c)"), bC.f(0, 512))
        P = [G[s_]["Nb"] for s_ in range(len(blks))]
        Q = [G[s_]["NTb"] for s_ in range(len(blks))]
        for k in range(1, 6):
            for s_ in range(len(blks)):
                g = G[s_]
                bA, bB, bC = self.bankA[s_], self.bankB[s_], self.bankC[s_]
                Pn = g["P%d" % (k % 2)]
                Qn = g["Q%d" % (k % 2)]
                for h in range(4):
                    S.mm(bA.f(h * 128, (h + 1) * 128), lhsT=Q[s_][:, h, :], rhs=P[s_][:, h, :])
                if k < 5:
                    for h in range(4):
                        S.mm(bB.f(h * 128, (h + 1) * 128), lhsT=P[s_][:, h, :], rhs=Q[s_][:, h, :])
                S.copy("dve", Pn[:, :, :].re("p h c -> p (h c)"), bA.f(0, 512))
                if k < 5:
                    S.copy("act", Qn[:, :, :].re("p h c -> p (h c)"), bB.f(0, 512))
                S.mm(bC.f(0, 512), lhsT=idb[:, :], rhs=g["RTb"][:, :, :].re("p h c -> p (h c)"), start=True, stop=False)
                for h in range(4):
                    S.mm(bC.f(h * 128, (h + 1) * 128), lhsT=Pn[:, h, :], rhs=g["RTb"][:, h, :],
                         start=False, stop=(h == 3))
                S.copy("act", g["RTb"][:, :, :].re("p h c -> p (h c)"), bC.f(0, 512))
                P[s_] = Pn
                Q[s_] = Qn

    def gdn_uw(self, g, s_):
        S = self.S
        bA, bB = self.bankA[s_], self.bankB[s_]
        for h in range(4):
            S.mm(bA.f(h * 128, (h + 1) * 128), lhsT=g["RTb"][:, h, :], rhs=g["vb"][:, h, :])
        for h in range(4):
            S.mm(bB.f(h * 128, (h + 1) * 128), lhsT=g["kw"][:, h, :], rhs=g["RTb"][:, h, :])
        S.copy("act", g["u"][:, :, :].re("p h c -> p (h c)"), bA.f(0, 512))
        S.copy("dve", g["wTb"][:, :, :].re("p h c -> p (h c)"), bB.f(0, 512))

    def gdn_recur(self, g, S32, Sb, sel, blk, oa, hsel, oh):
        S = self.S
        pb = self.pb
        bV, bO, bS_ = pb[2], pb[3], pb[4]
        sm = V(self.sm4.t[:, blk % 4, :], [self.sm4.b])
        for i in range(2):
            r0, r1 = 64 * i, 64 * i + 64
            cur = Sb[self.cidx % 2]
            nxt = Sb[(self.cidx + 1) % 2]
            self.cidx += 1
            for h in range(4):
                S.mm(bV.f(h * 128, (h + 1) * 128, r0, r1), lhsT=g["wTb"][:, h, r0:r1], rhs=cur[:, h, :])
            S.tt("dve", g["vn"][r0:r1, :, :].re("p h c -> p (h c)"), g["u"][r0:r1, :, :].re("p h c -> p (h c)"),
                 bV.f(0, 512, r0, r1), ALU.subtract)
            for h in range(4):
                S.mm(bS_.f(h * 128, (h + 1) * 128), lhsT=g["kd"][r0:r1, h, :], rhs=g["vn"][r0:r1, h, :])
            for h in range(4):
                S.stt(S32[:, h, :], S32[:, h, :], sm[:, 48 + 2 * h + i:49 + 2 * h + i], bS_.f(h * 128, (h + 1) * 128),
                      ALU.mult, ALU.add)
            S.copy("act", nxt[:, :, :], S32[:, :, :])
            for h in range(4):
                S.mm(bO.f(h * 128, (h + 1) * 128, r0, r1), lhsT=g["qdT"][:, h, r0:r1], rhs=cur[:, h, :],
                     start=True, stop=False)
                S.mm(bO.f(h * 128, (h + 1) * 128, r0, r1), lhsT=g["aT"][r0:r1, h, r0:r1], rhs=g["vn"][r0:r1, h, :],
                     start=False, stop=True)
        r_ = blk % 4
        if r_ == 0:
            S.ts("dve", oa[:, :], bO.f(0, 512), sel[:, 0:1], None, ALU.mult)
        else:
            S.stt(oa[:, :], bO.f(0, 512), sel[:, r_:r_ + 1], oa[:, :], ALU.mult, ALU.add)
        if r_ == 0:
            S.ts("dve", oh[64:128, :], bO.f(0, 512, 64, 128), hsel[64:128, 0:1], None, ALU.mult)
        else:
            S.stt(oh[64:128, :], bO.f(0, 512, 64, 128), hsel[64:128, r_:r_ + 1], oh[64:128, :], ALU.mult, ALU.add)

    def phase2(self, keep):
        S = self.S
        pb = self.pb
        gc_ = S.group("const2", full=True)
        qnT, gates = keep["qnT"], keep["gates"]
        with ExitStack() as es:
            qnw = self.cload(es, "qnw", [128, 1], gc_)
            gonw4 = self.cload(es, "gonw4", [128, 512], gc_)
            onesf = self.cload(es, "onesf2", [128, 128], gc_)
            onesb = self.sb(es, "onesb2", [128, 128], BF16)
            S.copy("dve", onesb[:, :], onesf[:, :])
            winb2 = self.sb(es, "winb2", [128, 8, 1036], BF16)
            with ExitStack() as es2:
                wst = [self.sb(es2, f"w2st{i}", [128, 1036]) for i in range(2)]
                g_w = [S.group(f"w2st{i}") for i in range(2)]
                for k in range(8):
                    sl = k % 2
                    for (a, b_, o) in ((O_GZ, O_GZ + 512, 0), (O_NQ, O_NQ + 512, 512), (O_NG, O_NG + 12, 1024)):
                        S.dma(wst[sl][:, o:o + (b_ - a)], self.w_in[k * 128:(k + 1) * 128, a:b_], g_w[sl])
                    S.copy("dve" if k % 2 else "pool", winb2[:, k, :], wst[sl][:, :])
                S.flush()
            F = dict(ssq=self.sb(es, "f2_ssq", [128, 4]), rt=self.sb(es, "f2_rt", [128, 4]),
                     rstd=self.sb(es, "f2_rstd", [128, 4]), junk=self.sb(es, "f2_junk", [128, 1024], BF16),
                     xs=self.sb(es, "f2_xs", [128, 2, 1024], BF16))
            xt = [self.sb(es, f"x2t{i}", [128, 2, 1024]) for i in range(2)]
            g_x = [S.group(f"x2t{i}") for i in range(2)]
            hT = self.sb(es, "h2T_", [128, 8, 512], BF16)
            yf = [self.sb(es, f"y2f{i}", [128, 512]) for i in range(2)]
            sq = [self.sb(es, f"s2q{i}", [128, 512], BF16) for i in range(2)]
            rtt = [self.sb(es, f"r2tt{i}", [128, 512]) for i in range(2)]
            og = [self.sb(es, f"og{i}", [128, 512]) for i in range(2)]
            g_og = [S.group(f"og{i}") for i in range(2)]
            zs = [self.sb(es, f"zs{i}", [128, 512]) for i in range(2)]
            t1 = [self.sb(es, f"p2t{i}", [128, 512]) for i in range(2)]
            mg = [self.sb(es, f"mg{i}", [128, 512], BF16) for i in range(2)]
            mgT = [self.sb(es, f"mgT{i}", [128, 4, 128], BF16) for i in range(2)]
            g_mg = [S.group(f"mgT{i}") for i in range(2)]
            osq = self.sb(es, "osq", [128, 8])
            xrows = self.xo.rearrange("(n b p) d -> n p b d", b=2, p=128)

            def load_x(n):
                S.dma(xt[n % 2][:, :, :], xrows[n], g_x[n % 2])

            load_x(0)
            for t in range(4):
                for hf in range(2):
                    n = 2 * t + hf
                    if n + 1 < 8:
                        load_x(n + 1)
                    self.front(F, xt[n % 2], 2, hT, hf * 256, self.s1, 0)
                for h in range(4):
                    bank = pb[2 + (h % 2)]
                    for k in range(8):
                        S.mm(bank.f(0, 512), lhsT=winb2[:, k, 512 + h * 128:512 + (h + 1) * 128], rhs=hT[:, k, :],
                             start=(k == 0), stop=(k == 7))
                    y_ = yf[h % 2]
                    S.copy("act", y_[:, :], bank.f(0, 512))
                    S.act(sq[h % 2][:, :], bank.f(0, 512), AF.Square)
                    bk2 = pb[4 + (h % 2)]
                    S.mm(bk2.f(0, 512), lhsT=onesb[:, :], rhs=sq[h % 2][:, :])
                    r_ = rtt[h % 2]
                    S.act(r_[:, :], bk2.f(0, 512), AF.Ln, bias=EPS, scale=1.0 / 128.0)
                    S.act(r_[:, :], r_[:, :], AF.Exp, scale=-0.5)
                    S.stt(qnT[:, h, t * 512:(t + 1) * 512], y_[:, :], qnw[:, 0:1], r_[:, :], ALU.mult, ALU.mult)
                for blk in range(4):
                    j = 4 * t + blk
                    cs = slice(blk * 128, (blk + 1) * 128)
                    bg = pb[6]
                    for k in range(8):
                        S.mm(bg.f(0, 12), lhsT=hT[:, k, cs], rhs=winb2[:, k, 1024:1036], start=(k == 0), stop=(k == 7))
                    S.act(gates[:, j, :], bg.f(0, 12), AF.Sigmoid)
                    bz = pb[7]
                    for k in range(8):
                        S.mm(bz.f(0, 512), lhsT=hT[:, k, cs], rhs=winb2[:, k, 0:512], start=(k == 0), stop=(k == 7))
                    z_ = zs[j % 2]
                    S.act(z_[:, :], bz.f(0, 512), AF.Silu)
                    o_ = og[j % 2]
                    S.dma(o_[:, :], self.o_own[j], g_og[j % 2])
                    for h in range(4):
                        S.act(t1[j % 2][:, h * 128:(h + 1) * 128], o_[:, h * 128:(h + 1) * 128], AF.Square,
                              accum=osq[:, h:h + 1])
                    S.act(osq[:, 4:8], osq[:, 0:4], AF.Sqrt, bias=EPS, scale=1.0 / 128.0)
                    S.recip(osq[:, 4:8], osq[:, 4:8])
                    rb = V(osq.t[:, 4:8].unsqueeze(2).broadcast_to([128, 4, 128]), [osq.b])
                    S.tt("dve", t1[j % 2][:, :].re("p (h d) -> p h d", h=4), o_[:, :].re("p (h d) -> p h d", h=4), rb,
                         ALU.mult)
                    S.tt("pool", t1[j % 2][:, :], t1[j % 2][:, :], gonw4[:, :], ALU.mult)
                    S.tt("dve", mg[j % 2][:, :], t1[j % 2][:, :], z_[:, :], ALU.mult)
                    bt = pb[0]
                    for c in range(4):
                        S.tr(bt.b(c * 128, (c + 1) * 128), mg[j % 2][:, c * 128:(c + 1) * 128], self.idb[:, :])
                    S.copy("act", mgT[j % 2][:, :, :].re("p c t -> p (c t)"), bt.b(0, 512))
                    S.dma(self.mix_d[:, 0:4, j * 128:(j + 1) * 128], mgT[j % 2][:, :, :], g_mg[j % 2])
            gh = S.group("p2halo", full=True)
            selAB = self.cload(es, "selAB", [128, 2], gh)
            xht = self.sb(es, "xht", [32, D])
            S.dma(xht[:, :], self.inp("xh", [32, D])[:, :], gh)
            ca = self.sb(es, "hca", [32, 512])
            cb = self.sb(es, "hcb", [32, 512])
            S.memset("pool", cb[0:2, :], 0.0)
            opv = self.oprev_d.rearrange("j i c -> (j i) c")
            S.dma(ca[:, :], opv[0:32, :], gh)
            S.dma(cb[2:32, :], opv[0:30, :], gh)
            hTh = self.sb(es, "hTh", [128, 8, 32], BF16)
            self.front32(F, xht, hTh, self.s1, 0)
            qnTh, gatesh = keep["qnTh"], keep["gatesh"]
            bq = pb[2]
            for h in range(4):
                for k in range(8):
                    S.mm(bq.f(h * 32, (h + 1) * 32), lhsT=winb2[:, k, 512 + h * 128:512 + (h + 1) * 128], rhs=hTh[:, k, :],
                         start=(k == 0), stop=(k == 7))
            S.copy("act", yf[0][:, 0:128], bq.f(0, 128))
            S.act(sq[0][:, 0:128], bq.f(0, 128), AF.Square)
            S.mm(pb[4].f(0, 128), lhsT=onesb[:, :], rhs=sq[0][:, 0:128])
            S.act(rtt[0][:, 0:128], pb[4].f(0, 128), AF.Ln, bias=EPS, scale=1.0 / 128.0)
            S.act(rtt[0][:, 0:128], rtt[0][:, 0:128], AF.Exp, scale=-0.5)
            S.stt(qnTh[:, :, :].re("p h q -> p (h q)"), yf[0][:, 0:128], qnw[:, 0:1], rtt[0][:, 0:128], ALU.mult, ALU.mult)
            for k in range(8):
                S.mm(pb[6].f(0, 12, 0, 32), lhsT=hTh[:, k, :], rhs=winb2[:, k, 1024:1036], start=(k == 0), stop=(k == 7))
            S.act(gatesh[:, :], pb[6].f(0, 12, 0, 32), AF.Sigmoid)
            for k in range(8):
                S.mm(pb[7].f(0, 512, 0, 32), lhsT=hTh[:, k, :], rhs=winb2[:, k, 0:512], start=(k == 0), stop=(k == 7))
            S.act(zs[0][0:32, :], pb[7].f(0, 512, 0, 32), AF.Silu)
            S.ts("dve", ca[:, :], ca[:, :], selAB[0:32, 0:1], None, ALU.mult)
            S.stt(ca[:, :], cb[:, :], selAB[0:32, 1:2], ca[:, :], ALU.mult, ALU.add)
            for h in range(4):
                S.act(t1[0][0:32, h * 128:(h + 1) * 128], ca[:, h * 128:(h + 1) * 128], AF.Square, accum=osq[0:32, h:h + 1])
            S.act(osq[0:32, 4:8], osq[0:32, 0:4], AF.Sqrt, bias=EPS, scale=1.0 / 128.0)
            S.recip(osq[0:32, 4:8], osq[0:32, 4:8])
            rb = V(osq.t[0:32, 4:8].unsqueeze(2).broadcast_to([32, 4, 128]), [osq.b])
            S.tt("dve", t1[0][0:32, :].re("p (h d) -> p h d", h=4), ca[:, :].re("p (h d) -> p h d", h=4), rb, ALU.mult)
            S.tt("pool", t1[0][0:32, :], t1[0][0:32, :], gonw4[0:32, :], ALU.mult)
            S.tt("dve", mg[0][0:32, :], t1[0][0:32, :], zs[0][0:32, :], ALU.mult)
            for c in range(4):
                S.tr(pb[0].b(c * 32, (c + 1) * 32), mg[0][0:32, c * 128:(c + 1) * 128], self.idb[0:32, 0:32])
            mghT = self.sb(es, "mghT", [128, 4, 32], BF16)
            S.copy("act", mghT[:, :, :].re("p c t -> p (c t)"), pb[0].b(0, 128))
            S.dma(self.mixh_d[:, 0:4, :], mghT[:, :, :], S.group("mghT"))
            S.flush()

    def phase3(self, keep):
        S = self.S
        pb = self.pb
        idb = self.idb
        qnT, gates = keep["qnT"], keep["gates"]
        SC = 128.0 ** -0.5
        with ExitStack() as es:
            gc_ = S.group("const3", full=True)
            kcmpT = self.sb(es, "kcmpT", [128, 512], BF16)
            vcx = self.sb(es, "vcx", [128, 4, 129], BF16)
            onesf = self.cload(es, "onesf3", [128, 128], gc_)
            onesb = self.sb(es, "onesb3", [128, 128], BF16)
            S.copy("dve", onesb[:, :], onesf[:, :])
            with ExitStack() as e2:
                g2 = S.group("const3b", full=True)
                kcw = self.cload(e2, "kcw", [128, 1], g2)
                pm511 = self.cload(e2, "pm511", [128, 1], g2)
                for which, src_d, w1n, w2n, posn in (("k", self.kcT_d, "cmp_k_w1", "cmp_k_w2", "cmp_k_posT"),
                                                    ("v", self.vcT_d, "cmp_v_w1", "cmp_v_w2", "cmp_v_posT")):
                    with ExitStack() as e3:
                        g3 = S.group("c3" + which, full=True)
                        xT = self.sb(e3, "cx" + which, [128, S_LEN], BF16)
                        S.dma(xT[:, 0:4096], src_d[:, 0:4096], g3)
                        S.dma(xT[:, 4096:8192], src_d[:, 4096:8192], g3)
                        w1d = self.inp(w1n, [128, 32, 128])
                        w1f = self.sb(e3, "w1f" + which, [128, 32, 128])
                        S.dma(w1f[:, 0:16, :], w1d[:, 0:16, :], g3)
                        S.dma(w1f[:, 16:32, :], w1d[:, 16:32, :], g3)
                        w1b = self.sb(e3, "w1b" + which, [128, 32, 128], BF16)
                        S.copy("dve", w1b[:, 0:16, :], w1f[:, 0:16, :])
                        S.copy("pool", w1b[:, 16:32, :], w1f[:, 16:32, :])
                        w2f = self.cload(e3, w2n, [128, 128], g3)
                        w2b = self.sb(e3, "w2b" + which, [128, 128], BF16)
                        S.copy("dve", w2b[:, :], w2f[:, :])
                        posf = self.cload(e3, posn, [128, 32], g3)
                        posb = self.sb(e3, "posb" + which, [128, 32], BF16)
                        S.copy("dve", posb[:, :], posf[:, :])
                        hid = self.sb(e3, "hid" + which, [128, 512], BF16)
                        bcol = self.sb(e3, "bcol" + which, [128, 1])
                        S.memset("pool", hid[:, :], 0.0)
                        bh, bb = pb[2], pb[3]
                        x3 = xT[:, :].re("p (n s) -> p n s", s=16)
                        for l in range(32):
                            S.mm(bh.f(0, 511), lhsT=w1b[:, l, :], rhs=x3[:, l // 16:l // 16 + 511, l % 16],
                                 start=(l == 0), stop=(l == 31))
                        for l in range(32):
                            S.mm(bb.f(0, 1), lhsT=w1b[:, l, :], rhs=posb[:, l:l + 1], start=(l == 0), stop=(l == 31))
                        S.copy("dve", bcol[:, :], bb.f(0, 1))
                        S.act(hid[:, 0:511], bh.f(0, 511), AF.Silu, bias=bcol[:, 0:1])
                        if which == "k":
                            bk = pb[4]
                            S.mm(bk.f(0, 512), lhsT=w2b[:, :], rhs=hid[:, :])
                            yk = self.sb(e3, "yk", [128, 512])
                            sqk = self.sb(e3, "sqk", [128, 512], BF16)
                            rk = self.sb(e3, "rk", [128, 512])
                            S.copy("act", yk[:, :], bk.f(0, 512))
                            S.act(sqk[:, :], bk.f(0, 512), AF.Square)
                            S.mm(pb[5].f(0, 512), lhsT=onesb[:, :], rhs=sqk[:, :])
                            S.act(rk[:, :], pb[5].f(0, 512), AF.Sqrt, bias=EPS, scale=1.0 / 128.0)
                            S.recip(rk[:, :], rk[:, :])
                            S.stt(kcmpT[:, :], yk[:, :], kcw[:, 0:1], rk[:, :], ALU.mult, ALU.mult)
                            S.memset("dve", kcmpT[:, 511:512], 0.0)
                        else:
                            bv = pb[4]
                            for nt in range(4):
                                S.mm(bv.f(nt * 128, (nt + 1) * 128), lhsT=hid[:, nt * 128:(nt + 1) * 128], rhs=w2b[:, :])
                            S.memset("pool", vcx[:, :, 128:129], 1.0)
                            S.copy("act", vcx[:, :, 0:128], bv.f(0, 512).re("p (n d) -> p n d", n=4))
                            S.ts("dve", vcx[:, 3, :], vcx[:, 3, :], pm511[:, 0:1], None, ALU.mult)
                        S.flush()
            if self.debug:
                S.dma(self.outp("d_kcmpT", [128, 512], BF16)[:, :], kcmpT[:, :], self.g_out)
                S.dma(self.outp("d_vcx", [128, 4, 129], BF16)[:, :, :], vcx[:, :, :], self.g_out)
            if self.stop == 31:
                S.flush()
                return
            kslT = self.sb(es, "kslT", [128, S_LEN], BF16)
            kwnT = self.sb(es, "kwnT", [128, S_LEN], BF16)
            vslx = self.sb(es, "vslx", [128, NB, 129], BF16)
            vwnx = self.sb(es, "vwnx", [128, NB, 129], BF16)
            for i in range(2):
                S.dma(kslT[:, i * 4096:(i + 1) * 4096], self.kslT_d[:, i * 4096:(i + 1) * 4096], gc_)
                S.dma(kwnT[:, i * 4096:(i + 1) * 4096], self.kwnT_d[:, i * 4096:(i + 1) * 4096], gc_)
                S.dma(vslx[:, i * 32:(i + 1) * 32, 0:128], self.vsl_d[:, i * 32:(i + 1) * 32, :], gc_)
                S.dma(vwnx[:, i * 32:(i + 1) * 32, 0:128], self.vwn_d[:, i * 32:(i + 1) * 32, :], gc_)
            S.memset("pool", vslx[:, :, 128:129], 1.0)
            S.memset("pool", vwnx[:, :, 128:129], 1.0)
            eall = self.sb(es, "eallb", [128, S_LEN], BF16)
            addm = self.cload(es, "addm", [128, 16, 128], gc_)
            ovb = self.sb(es, "ovb", [128, 4, 128], BF16)
            cmbb = self.sb(es, "cmbb", [128, 16, 2, 128], BF16)
            cbsb = self.sb(es, "cbsb", [128, 4, 4, 128], BF16)
            cbwb = self.sb(es, "cbwb", [128, 8, 4, 128], BF16)
            with ExitStack() as e2:
                g2 = S.group("const3c", full=True)
                ead = self.inp("eall", [128, S_LEN])
                est = [self.sb(e2, f"est{i}", [128, 2048]) for i in range(2)]
                g_e = [S.group(f"est{i}") for i in range(2)]
                for i in range(4):
                    S.dma(est[i % 2][:, :], ead[:, i * 2048:(i + 1) * 2048], g_e[i % 2])
                    S.copy("pool" if i % 2 else "dve", eall[:, i * 2048:(i + 1) * 2048], est[i % 2][:, :])
                ovf = self.cload(e2, "ovm", [128, 4, 128], g2)
                S.copy("dve", ovb[:, :, :], ovf[:, :, :])
                cmf = self.cload(e2, "cmb", [128, 16, 2, 128], g2)
                S.copy("pool", cmbb[:, :, :, :], cmf[:, :, :, :])
                cbsf = self.cload(e2, "cbs", [128, 4, 128], g2)
                cbwf = self.cload(e2, "cbw", [128, 8, 128], g2)
                for h in range(4):
                    S.copy("dve", cbsb[:, :, h, :], cbsf[:, :, :])
                    S.copy("pool", cbwb[:, :, h, :], cbwf[:, :, :])
                S.flush()
            Pc = [self.sb(es, f"Pc{i}", [128, 512], BF16) for i in range(4)]
            Pk = [self.sb(es, f"Pk{i}", [128, 512], BF16) for i in range(3)]
            cmrep = [self.sb(es, f"cmrep{i}", [128, 4, 128], BF16) for i in range(2)]
            ob = {nm: self.sb(es, "ob_" + nm, [128, 4, 129]) for nm in ("c", "s", "w")}
            rs = self.sb(es, "rs", [128, 12])
            cf = self.sb(es, "cf", [128, 12])
            imp = self.sb(es, "imp", [128, 128])
            imp2 = self.sb(es, "imp2", [128, 128])
            m8 = self.sb(es, "m8", [128, 16])
            selm = self.sb(es, "selm", [128, 128])
            biasT = self.sb(es, "biasT", [128, 4, 128], BF16)
            mixn = [self.sb(es, f"mixn{i}", [128, 4, 128], BF16) for i in range(2)]
            tmpn = self.sb(es, "tmpn", [128, 128])
            mnT = [self.sb(es, f"mnT{i}", [128, 4, 128], BF16) for i in range(2)]
            g_mn = [S.group(f"mnT{i}") for i in range(2)]
            bS = [pb[0], pb[1]]
            bO = [pb[2], pb[3]]
            bI, bT = pb[4], pb[5]
            si = [0]

            def nsa_block(Q, qv, cmp_plan, slc_plan, win_plan, addv, gatev, store):
                NQ = 4 * Q

                def scores(kT_tile, extra):
                    bank = bS[si[0] % 2]
                    out_ = Pk[si[0] % 3]
                    si[0] += 1
                    S.mm(bank.f(0, NQ), lhsT=kT_tile, rhs=qv, start=True, stop=(len(extra) == 0))
                    for i_, (l_, r_) in enumerate(extra):
                        S.mm(bank.f(0, NQ), lhsT=l_, rhs=r_, start=False, stop=(i_ == len(extra) - 1))
                    S.act(out_[:, 0:NQ], bank.f(0, NQ), AF.Exp, scale=SC)
                    return out_

                def pv(P_, vx, first, last):
                    for h in range(4):
                        S.mm_(bO[h // 2].f((h % 2) * 129, (h % 2) * 129 + 129, 0, Q), lhsT=P_[:, h * Q:(h + 1) * Q],
                              rhs=vx, start=(first and h % 2 == 0), stop=last, skip=True)

                def evac_o(dst):
                    for hh in range(2):
                        S.copy("act", dst[0:Q, 2 * hh:2 * hh + 2, :], bO[hh].f(0, 258, 0, Q).re("p (h c) -> p h c", h=2))

                ncp = len(cmp_plan)
                for i_, (nt, mk) in enumerate(cmp_plan):
                    bank = bS[i_ % 2]
                    S.mm(bank.f(0, NQ), lhsT=kcmpT[:, nt * 128:(nt + 1) * 128], rhs=qv, start=True, stop=(mk is None))
                    if mk is not None:
                        S.mm(bank.f(0, NQ), lhsT=idb[:, :], rhs=mk, start=False, stop=True)
                    S.act(Pc[i_][:, 0:NQ], bank.f(0, NQ), AF.Exp, scale=SC)
                for h in range(4):
                    for i_, (nt, mk) in enumerate(cmp_plan):
                        S.mm_(bO[h // 2].f((h % 2) * 129, (h % 2) * 129 + 129, 0, Q), lhsT=Pc[i_][:, h * Q:(h + 1) * Q],
                              rhs=vcx[:, nt, :], start=(i_ == 0 and h % 2 == 0), stop=(i_ == ncp - 1), skip=True)
                for h in range(4):
                    for i_, (nt, mk) in enumerate(cmp_plan):
                        S.mm_(bI.f(h * 128, (h + 1) * 128, 0, Q), lhsT=Pc[i_][:, h * Q:(h + 1) * Q], rhs=ovb[:, nt, :],
                              start=(i_ == 0 and h == 0), stop=(i_ == ncp - 1), skip=True)
                evac_o(ob["c"])
                S.ts("dve", rs[0:Q, 0:4], ob["c"][0:Q, :, 128], 1e-30, None, ALU.max)
                S.recip(rs[0:Q, 0:4], rs[0:Q, 0:4])
                S.ts("dve", imp[0:Q, :], bI.f(0, 128, 0, Q), rs[0:Q, 0:1], None, ALU.mult)
                for h in range(1, 4):
                    S.stt(imp[0:Q, :], bI.f(h * 128, (h + 1) * 128, 0, Q), rs[0:Q, h:h + 1], imp[0:Q, :], ALU.mult, ALU.add)
                S.tt("pool", imp[0:Q, :], imp[0:Q, :], addv, ALU.add)
                self.max8(m8[0:Q, 0:8], imp[0:Q, :])
                self.match_replace(imp2[0:Q, :], m8[0:Q, 0:8], imp[0:Q, :], -3.0e38)
                self.max8(m8[0:Q, 8:16], imp2[0:Q, :])
                S.ts("dve", selm[0:Q, :], imp[0:Q, :], m8[0:Q, 15:16], None, ALU.is_ge)
                S.mm(bT.f(0, Q), lhsT=selm[0:Q, :], rhs=self.idf[0:Q, 0:Q])
                S.ts("dve", bT2[:, 0:NQ].re("p (h q) -> p h q", h=4),
                     V(bT.t[:, 0:Q].unsqueeze(1).broadcast_to([128, 4, Q]), bT.q[0:1]), BIG, -BIG, ALU.mult, ALU.add)
                brhs = bT2[:, 0:NQ]
                prev = None
                for i_, (kt, extra) in enumerate(slc_plan):
                    ex = [(eall[:, kt * 128:(kt + 1) * 128], brhs)] + extra
                    P_ = scores(kslT[:, kt * 128:(kt + 1) * 128], ex)
                    if prev is not None:
                        pv(prev[0], vslx[:, prev[1], :], prev[2] == 0, False)
                    prev = (P_, kt, i_)
                pv(prev[0], vslx[:, prev[1], :], prev[2] == 0, True)
                evac_o(ob["s"])
                prev = None
                for i_, (kt, mk) in enumerate(win_plan):
                    P_ = scores(kwnT[:, kt * 128:(kt + 1) * 128], [(idb[:, :], mk)])
                    if prev is not None:
                        pv(prev[0], vwnx[:, prev[1], :], prev[2] == 0, False)
                    prev = (P_, kt, i_)
                pv(prev[0], vwnx[:, prev[1], :], prev[2] == 0, True)
                evac_o(ob["w"])
                S.ts("dve", rs[0:Q, 4:8], ob["s"][0:Q, :, 128], 1e-30, None, ALU.max)
                S.ts("dve", rs[0:Q, 8:12], ob["w"][0:Q, :, 128], 1e-30, None, ALU.max)
                S.recip(rs[0:Q, 4:12], rs[0:Q, 4:12])
                g3 = gatev.re("p (h g) -> p g h", g=3)
                S.tt("dve", cf[0:Q, :].re("p (g h) -> p g h", g=3), rs[0:Q, :].re("p (g h) -> p g h", g=3), g3, ALU.mult)
                mx = mixn[si[0] % 2]
                for h in range(4):
                    S.ts("pool", tmpn[0:Q, :], ob["c"][0:Q, h, 0:128], cf[0:Q, h:h + 1], 1.0, ALU.mult, ALU.mult)
                    S.stt(tmpn[0:Q, :], ob["s"][0:Q, h, 0:128], cf[0:Q, 4 + h:5 + h], tmpn[0:Q, :], ALU.mult, ALU.add)
                    S.stt(mx[0:Q, h, :], ob["w"][0:Q, h, 0:128], cf[0:Q, 8 + h:9 + h], tmpn[0:Q, :], ALU.mult, ALU.add)
                for h in range(4):
                    S.tr(bT.b(h * Q, (h + 1) * Q), mx[0:Q, h, :], idb[0:Q, 0:Q])
                store(bT.b(0, 4 * Q))

            bT2 = self.sb(es, "bT2", [128, 512], BF16)
            for j in range(16):
                qv = qnT[:, :, j * 128:(j + 1) * 128]
                nt_hi = min(3, (32 * j + 30) // 128)
                lo = max(0, 32 * j - 1) // 128
                cmp_plan = []
                for nt in range(nt_hi + 1):
                    mk = None
                    if nt >= lo:
                        slot = nt - lo
                        S.copy("pool", cmrep[slot][:, :, :],
                               V(cmbb.t[:, j, slot, :].unsqueeze(1).broadcast_to([128, 4, 128]), [cmbb.b]))
                        mk = cmrep[slot][:, :, :].re("p h q -> p (h q)")
                    cmp_plan.append((nt, mk))
                slc_plan = []
                for kt in range(4 * j + 4):
                    ex = []
                    if kt >= 4 * j:
                        ex.append((idb[:, :], cbsb[:, kt - 4 * j, :, :].re("p h q -> p (h q)")))
                    slc_plan.append((kt, ex))
                win_plan = [(4 * j - 4 + e, cbwb[:, e, :, :].re("p h q -> p (h q)")) for e in range(8) if 4 * j - 4 + e >= 0]

                def store(src, j=j):
                    S.copy("act", mnT[j % 2][:, :, :].re("p c t -> p (c t)"), src)
                    S.dma(self.mix_d[:, 4:8, j * 128:(j + 1) * 128], mnT[j % 2][:, :, :], g_mn[j % 2])

                nsa_block(128, qv, cmp_plan, slc_plan, win_plan, addm[:, j, :], gates[:, j, :], store)
            self.nsa_halo(es, nsa_block, keep, idb)
            S.flush()

    def nsa_halo(self, es, nsa_block, keep, idb):
        S = self.S
        with ExitStack() as e2:
            g2 = S.group("const3h", full=True)
            haddm = self.cload(e2, "haddm", [32, 128], g2)
            hcmb = self.sb(e2, "hcmb", [128, 4, 128], BF16)
            hsmb = self.sb(e2, "hsmb", [128, 64, 128], BF16)
            hwmb = self.sb(e2, "hwmb", [128, 64, 128], BF16)
            hcf = self.cload(e2, "hcm", [128, 4, 128], g2)
            S.copy("dve", hcmb[:, :, :], hcf[:, :, :])
            st = [self.sb(e2, f"hst{i}", [128, 16, 128]) for i in range(2)]
            g_s = [S.group(f"hst{i}") for i in range(2)]
            n = 0
            for nm, dst in (("hsm", hsmb), ("hwm", hwmb)):
                src = self.inp(nm, [128, 64, 128])
                for i in range(4):
                    S.dma(st[n % 2][:, :, :], src[:, i * 16:(i + 1) * 16, :], g_s[n % 2])
                    S.copy("pool" if n % 2 else "dve", dst[:, i * 16:(i + 1) * 16, :], st[n % 2][:, :, :])
                    n += 1
            mnTh = self.sb(e2, "mnTh", [128, 4, 32], BF16)
            g_m = S.group("mnTh")
            cmp_plan = [(nt, hcmb[:, nt, :]) for nt in range(4)]
            slc_plan = [(kt, [(idb[:, :], hsmb[:, kt, :])]) for kt in range(NB)]
            win_plan = [(kt, hwmb[:, kt, :]) for kt in range(NB)]

            def store(src):
                S.copy("act", mnTh[:, :, :].re("p c t -> p (c t)"), src)
                S.dma(self.mixh_d[:, 4:8, :], mnTh[:, :, :], g_m)

            nsa_block(32, keep["qnTh"][:, :, :], cmp_plan, slc_plan, win_plan, haddm[:, :], keep["gatesh"][:, :], store)
            S.flush()

    def max8(self, out, in_):
        o_, i_ = out.ap, in_.ap
        self.S.op("dve", lambda e: e.max(out=o_, in_=i_), r=self.S._bs(in_), w=self.S._bs(out))

    def match_replace(self, out, rep, vals, imm):
        o_, r_, v_ = out.ap, rep.ap, vals.ap
        self.S.op("dve", lambda e: e.match_replace(out=o_, in_to_replace=r_, in_values=v_, imm_value=imm),
                  r=self.S._bs(rep, vals), w=self.S._bs(out))

    def phase45(self):
        S = self.S
        pb = self.pb
        idb = self.idb
        w_out = self.inp("w_out", [D, D])
        w_up = self.inp("w_up", [D, 2 * DFF])
        w_dn = self.inp("w_dn", [DFF, D])
        wup_d = self.scratch("wup_d", [128, 44, 8, 128], BF16)
        with ExitStack() as e1:
            full = self.sb(e1, "wupfull", [128, 44, 8, 128], BF16)
            stg = [self.sb(e1, f"wupst{i}", [128, 2 * DFF]) for i in range(2)]
            g_s = [S.group(f"wupst{i}") for i in range(2)]
            g_f = S.group("wupfull")
            for k in range(8):
                st = stg[k % 2]
                for i in range(2):
                    S.dma(st[:, i * DFF:(i + 1) * DFF], w_up[k * 128:(k + 1) * 128, i * DFF:(i + 1) * DFF], g_s[k % 2])
                s3 = st[:, :].re("p (c n) -> p c n", n=128)
                S.copy("dve", full[:, 0:16, k, :], s3[:, 0:16, :])
                S.copy("pool", full[:, 16:30, k, :], s3[:, 16:30, :])
                S.copy("act", full[:, 30:44, k, :], s3[:, 30:44, :])
            for i in range(4):
                S.dma(wup_d[:, i * 11:(i + 1) * 11, :, :], full[:, i * 11:(i + 1) * 11, :, :], g_f)
            S.flush()
        with ExitStack() as es:
            gc_ = S.group("const4", full=True)
            fcw = self.cload(es, "fcw", [128, 44 * 3], gc_)
            fcb = self.cload(es, "fcb", [128, 44], gc_)
            wdnb = self.sb(es, "wdnb", [128, 22, D], BF16)
            woutb = self.sb(es, "woutb", [128, 8, D], BF16)
            g1row = self.sb(es, "g1row", [128, D])
            g2row = self.sb(es, "g2row", [128, D])
            with ExitStack() as e2:
                stg = [self.sb(e2, f"wst4_{i}", [128, 2, D]) for i in range(2)]
                g_s = [S.group(f"wst4_{i}") for i in range(2)]
                n = 0
                for (src, dst, nch) in ((w_dn, wdnb, 22), (w_out, woutb, 8)):
                    sv = src.rearrange("(c p) d -> p c d", p=128)
                    for c0 in range(0, nch, 2):
                        st = stg[n % 2]
                        S.dma(st[:, :, :], sv[:, c0:c0 + 2, :], g_s[n % 2])
                        S.copy(("dve", "pool", "act")[n % 3], dst[:, c0:c0 + 2, :], st[:, :, :])
                        n += 1
                gb = self.sb(e2, "gbc", [128, 128])
                for gi, (col0, dst) in enumerate(((16, g1row), (40, g2row))):
                    for cc in range(8):
                        S.copy("dve", gb[:, :], V(self.modT.t[:, col0 + cc:col0 + cc + 1].broadcast_to([128, 128]),
                                                  [self.modT.b]))
                        bank = pb[cc % 2]
                        S.mm(bank.f(0, 128), lhsT=gb[:, :], rhs=self.idf[:, :])
                        S.copy("act", dst[:, cc * 128:(cc + 1) * 128], bank.f(0, 128))
                S.flush()
            F = dict(ssq=self.sb(es, "f4_ssq", [128, 4]), rt=self.sb(es, "f4_rt", [128, 4]),
                     rstd=self.sb(es, "f4_rstd", [128, 4]), junk=self.sb(es, "f4_junk", [128, 1024], BF16),
                     xs=self.sb(es, "f4_xs", [128, 2, 1024], BF16))
            x1 = [self.sb(es, f"x1_{i}", [128, 4, D]) for i in range(2)]
            g_x = [S.group(f"x1_{i}") for i in range(2)]
            mixt = self.sb(es, "mixt", [128, 8, 512], BF16)
            g_m = S.group("mixt")
            h2T = self.sb(es, "h2T", [128, 8, 512], BF16)
            gT = self.sb(es, "gT", [128, 22, 512], BF16)
            wch = [self.sb(es, f"wch{i}", [128, 2, 8, 128], BF16) for i in range(3)]
            g_wc = [S.group(f"wch{i}") for i in range(3)]
            upa = [self.sb(es, f"upa{i}", [128, 4, 130]) for i in range(2)]
            upb = [self.sb(es, f"upb{i}", [128, 4, 130]) for i in range(2)]
            for u_ in upa + upb:
                S.memset("pool", u_[:, :, :], 0.0)
            aa = [self.sb(es, f"aa{i}", [128, 4, 128]) for i in range(2)]
            ab_ = [self.sb(es, f"ab_{i}", [128, 4, 128]) for i in range(2)]
            tmp = [self.sb(es, f"tmp4_{i}", [128, 512]) for i in range(2)]
            xrows = self.xo.rearrange("(t b p) d -> t p b d", b=4, p=128)
            orows = self.out.rearrange("(t b p) d -> t p b d", b=4, p=128)
            wi = 0
            gh = S.group("p4halo", full=True)
            hexr = self.cload(es, "hexr", [128, 32], gh)
            x1h = self.sb(es, "x1h", [32, D])
            mixh = self.sb(es, "mixh", [128, 8, 32], BF16)
            S.dma(x1h[:, :], self.din["xh"][:, :], gh)
            S.dma(mixh[:, :, :], self.mixh_d[:, :, :], gh)
            h2Th = self.sb(es, "h2Th", [128, 8, 32], BF16)
            for half in range(2):
                bank = pb[2 + half]
                hs = slice(half * 512, (half + 1) * 512)
                for c in range(8):
                    S.mm(bank.f(0, 512, 0, 32), lhsT=mixh[:, c, :], rhs=woutb[:, c, hs], start=(c == 0), stop=(c == 7))
                S.tt("dve", tmp[half][0:32, :], bank.f(0, 512, 0, 32), g1row[0:32, hs], ALU.mult)
                S.tt("pool", x1h[:, hs], x1h[:, hs], tmp[half][0:32, :], ALU.add)
            self.front32(F, x1h, h2Th, self.s2, 24)

            def conv(u_, c, dst):
                S.ts("pool", dst[:, :, :], u_[:, :, 2:130], fcw[:, c * 3 + 2:c * 3 + 3], fcb[:, c:c + 1], ALU.mult, ALU.add)
                S.stt(dst[:, :, :], u_[:, :, 1:129], fcw[:, c * 3 + 1:c * 3 + 2], dst[:, :, :], ALU.mult, ALU.add)
                S.stt(dst[:, :, :], u_[:, :, 0:128], fcw[:, c * 3:c * 3 + 1], dst[:, :, :], ALU.mult, ALU.add)

            for t in range(4):
                xx = x1[t % 2]
                S.dma(xx[:, :, :], xrows[t], g_x[t % 2])
                S.dma(mixt[:, :, :], self.mix_d[:, :, t * 512:(t + 1) * 512], g_m)
                for blk in range(4):
                    cs = slice(blk * 128, (blk + 1) * 128)
                    for half in range(2):
                        bank = pb[2 + half]
                        hs = slice(half * 512, (half + 1) * 512)
                        for c in range(8):
                            S.mm(bank.f(0, 512), lhsT=mixt[:, c, cs], rhs=woutb[:, c, hs], start=(c == 0), stop=(c == 7))
                        tm = tmp[half]
                        S.tt("dve", tm[:, :], bank.f(0, 512), g1row[:, hs], ALU.mult)
                        S.tt("pool", xx[:, blk, hs], xx[:, blk, hs], tm[:, :], ALU.add)
                if self.debug:
                    S.dma(self.dx1[t], xx[:, :, :], self.g_out)
                for hf in range(2):
                    self.front(F, xx, 2, h2T, hf * 256, self.s2, 24, b0=2 * hf)
                for c in range(22):
                    w_ = wch[wi % 3]
                    S.dma(w_[:, 0, :, :], wup_d[:, c, :, :], g_wc[wi % 3])
                    S.dma(w_[:, 1, :, :], wup_d[:, 22 + c, :, :], g_wc[wi % 3])
                    wi += 1
                    ua, ub = upa[c % 2], upb[c % 2]
                    for i_, (u_, bank) in enumerate(((ua, pb[4]), (ub, pb[5]))):
                        for k in range(8):
                            S.mm(bank.f(0, 512), lhsT=w_[:, i_, k, :], rhs=h2T[:, k, :], start=(k == 0), stop=(k == 7))
                        S.copy("act", u_[:, :, 2:130], bank.f(0, 512).re("p (b t) -> p b t", b=4))
                        bh_ = pb[6 + i_]
                        for k in range(8):
                            S.mm(bh_.f(0, 8), lhsT=w_[:, i_, k, :], rhs=h2Th[:, k, 8 * t:8 * t + 8], start=(k == 0),
                                 stop=(k == 7))
                        S.tt("dve", u_[:, :, 0:2], bh_.f(0, 8).re("p (b i) -> p b i", i=2),
                             hexr[:, 8 * t:8 * t + 8].re("p (b i) -> p b i", i=2), ALU.mult)
                    conv(ua, c, aa[c % 2])
                    conv(ub, 22 + c, ab_[c % 2])
                    S.act(aa[c % 2][:, :, :], aa[c % 2][:, :, :], AF.Silu)
                    S.tt("dve", gT[:, c, :].re("p (b t) -> p b t", b=4), aa[c % 2][:, :, :], ab_[c % 2][:, :, :], ALU.mult)
                for blk in range(4):
                    cs = slice(blk * 128, (blk + 1) * 128)
                    for half in range(2):
                        bank = pb[6 + half]
                        hs = slice(half * 512, (half + 1) * 512)
                        for c in range(22):
                            S.mm(bank.f(0, 512), lhsT=gT[:, c, cs], rhs=wdnb[:, c, hs], start=(c == 0), stop=(c == 21))
                        tm = tmp[half]
                        S.tt("dve", tm[:, :], bank.f(0, 512), g2row[:, hs], ALU.mult)
                        S.tt("pool", xx[:, blk, hs], xx[:, blk, hs], tm[:, :], ALU.add)
                S.dma(orows[t], xx[:, :, :], g_x[t % 2])
            S.flush()


def _colL(v, n):
    return np.ascontiguousarray(np.asarray(v, np.float32).reshape(n, 128).T)


def _rep(v, n=128):
    v = np.asarray(v, np.float32).reshape(1, -1)
    return np.ascontiguousarray(np.repeat(v, n, axis=0))


def _consts():
    p = np.arange(128)
    same = (p[:, None] // 64) == (p[None, :] // 64)
    c = {}
    c["identf"] = np.eye(128, dtype=np.float32)
    c["tri2"] = (same & (p[:, None] <= p[None, :])).astype(np.float32)
    c["blk2"] = same.astype(np.float32)
    c["onesf"] = np.ones((128, 128), np.float32)
    c["cind"] = np.stack([(p < 64), (p >= 64)], axis=1).astype(np.float32)
    ma = np.where(same & (p[None, :] < p[:, None]), 0.0, BIG).astype(np.float32)
    mb = np.where(same & (p[None, :] >= p[:, None]), 0.0, -BIG).astype(np.float32)
    c["ma4"] = np.ascontiguousarray(np.tile(ma, (1, 4)))
    c["mb4"] = np.ascontiguousarray(np.tile(mb, (1, 4)))
    return c


def _host_inputs(inputs):
    x = np.asarray(inputs["x"], np.float32)
    cst = _consts()
    g = lambda k: np.asarray(inputs[k][0], np.float32)
    gcw = g("gdn_conv_w")
    gcwT = np.ascontiguousarray(gcw.reshape(4, 12, 128).transpose(2, 1, 0).reshape(128, 48))
    shared = {
        "ada_w": np.ascontiguousarray(g("ada_w")),
        "ada_bT": _colL(g("ada_b"), 48),
        "n1w": _colL(g("norm1_w"), 8),
        "n2w": _colL(g("norm2_w"), 8),
        "w_in": np.ascontiguousarray(g("w_in")),
        "gcw": gcwT,
        "dtb": _rep(g("gdn_dt_bias")),
        "alog": _rep(g("gdn_A_log")),
        "kslw": _colL(g("nsa_k_norm_slc"), 1),
        "kwnw": _colL(g("nsa_k_norm_win"), 1),
        "qnw": _colL(g("nsa_q_norm_w"), 1),
        "gonw4": _rep(np.tile(g("gdn_out_norm_w"), 4)),
        "onesf2": np.ones((128, 128), np.float32),
    }
    for nm in ("k", "v"):
        shared[f"cmp_{nm}_w1"] = np.ascontiguousarray(g(f"cmp_{nm}_w1").reshape(32, 128, 128).transpose(1, 0, 2))
        shared[f"cmp_{nm}_w2"] = np.ascontiguousarray(g(f"cmp_{nm}_w2"))
        shared[f"cmp_{nm}_posT"] = np.ascontiguousarray(g(f"cmp_{nm}_pos").T)
    shared["kcw"] = _colL(g("nsa_k_norm_cmp"), 1)
    shared["onesf3"] = np.ones((128, 128), np.float32)
    pm = np.ones((128, 1), np.float32)
    pm[127, 0] = 0.0
    shared["pm511"] = pm
    keys = np.arange(S_LEN)
    shared["eall"] = (keys[None, :] // 64 == np.arange(128)[:, None]).astype(np.float32)
    n = np.arange(512)
    js = np.arange(128)
    ov = np.minimum(16 * n[:, None] + 32, 64 * js[None, :] + 64) - np.maximum(16 * n[:, None], 64 * js[None, :])
    ov = np.clip(ov, 0, None).astype(np.float32) / 32.0
    ov[511] = 0.0
    shared["ovm"] = np.ascontiguousarray(ov.reshape(4, 128, 128).transpose(1, 0, 2))
    fw = g("ffn_conv_w")
    shared["fcw"] = np.ascontiguousarray(fw.reshape(3, 44, 128).transpose(2, 1, 0).reshape(128, 132))
    shared["fcb"] = _colL(g("ffn_conv_b"), 44)
    shared["w_out"] = np.ascontiguousarray(g("w_out"))
    shared["w_up"] = np.ascontiguousarray(g("ffn_w_up"))
    shared["w_dn"] = np.ascontiguousarray(g("ffn_w_down"))
    shared.update(cst)
    maps = []
    for core in range(8):
        b, r = core // 4, core % 4
        xo = np.concatenate([x[b, 128 * (4 * j + r):128 * (4 * j + r) + 128] for j in range(16)], axis=0)
        m = dict(shared)
        m["xb"] = np.ascontiguousarray(x[b])
        m["xo"] = np.ascontiguousarray(xo)
        m["cT"] = _colL(inputs["c"][b], 8)
        sel = np.zeros((128, 4), np.float32)
        sel[:, r] = 1.0
        m["sel"] = sel
        p = np.arange(128)
        q = np.arange(128)
        addm = np.zeros((128, 16, 128), np.float32)
        cmb = np.zeros((128, 16, 2, 128), np.float32)
        for j in range(16):
            qi = 4 * j + r
            tq = 128 * qi + q
            cur = tq // 64
            jj = np.arange(128)[None, :]
            valid = jj <= cur[:, None]
            forced = (jj == 0) | (jj == cur[:, None]) | (jj == cur[:, None] - 1)
            addm[:, j, :] = np.where(valid, np.where(forced, 1.0e4, 0.0), -1.0e30)
            lo = max(0, 32 * j - 1) // 128
            for slot in range(2):
                nn = 128 * (lo + slot) + p
                ok = (16 * nn[:, None] + 31) <= tq[None, :]
                cmb[:, j, slot, :] = np.where(ok, 0.0, -BIG)
        m["addm"] = addm
        m["cmb"] = cmb
        cbs = np.zeros((128, 4, 128), np.float32)
        for d in range(4):
            ok = (128 * (d - r) + p[:, None]) <= q[None, :]
            cbs[:, d, :] = np.where(ok, 0.0, -BIG)
        m["cbs"] = cbs
        cbw = np.zeros((128, 8, 128), np.float32)
        for e in range(8):
            rel = 128 * (e - 4 - r) + p[:, None]
            ok = (rel <= q[None, :]) & (rel > q[None, :] - 512)
            cbw[:, e, :] = np.where(ok, 0.0, -BIG)
        m["cbw"] = cbw
        hs_ = np.zeros((128, 4), np.float32)
        hs_[:, (r - 1) % 4] = 1.0
        m["hsel"] = hs_
        sab = np.zeros((128, 2), np.float32)
        sab[:, 0] = 1.0 if r >= 1 else 0.0
        sab[:, 1] = 1.0 if r == 0 else 0.0
        m["selAB"] = sab
        tq = np.array([128 * (4 * j + r) - 2 + i for j in range(16) for i in range(2)])
        ex = tq >= 0
        xh = np.zeros((32, D), np.float32)
        xh[ex] = x[b, tq[ex]]
        m["xh"] = xh
        m["hexr"] = _rep(ex.astype(np.float32))
        cur = tq // 64
        jj = np.arange(128)[None, :]
        valid = (jj <= cur[:, None]) & ex[:, None]
        forced = (jj == 0) | (jj == cur[:, None]) | (jj == cur[:, None] - 1)
        m["haddm"] = np.where(valid, np.where(forced, 1.0e4, 0.0), -1.0e30).astype(np.float32)
        nn = np.arange(512)
        okc = ((16 * nn[:, None] + 31) <= tq[None, :]) & ex[None, :]
        hcm = np.where(okc, 0.0, -BIG).astype(np.float32).reshape(4, 128, 1, 32)
        m["hcm"] = np.ascontiguousarray(np.broadcast_to(hcm, (4, 128, 4, 32)).transpose(1, 0, 2, 3).reshape(128, 4, 128))
        pos = np.arange(S_LEN)
        oks = (pos[:, None] <= tq[None, :]) & ex[None, :]
        okw = oks & (pos[:, None] > tq[None, :] - 512)
        for nm, ok in (("hsm", oks), ("hwm", okw)):
            a = np.where(ok, 0.0, -BIG).astype(np.float32).reshape(64, 128, 1, 32)
            m[nm] = np.ascontiguousarray(np.broadcast_to(a, (64, 128, 4, 32)).transpose(1, 0, 2, 3).reshape(128, 64, 128))
        maps.append(m)
    return maps


def run(inputs, debug=False, upto=9, ntiles=NT):
    bld = Builder(debug, ntiles)
    nc = bld.build(upto)
    maps = _host_inputs(inputs)
    maps = [{k: v for k, v in m.items() if k in bld.din} for m in maps]
    missing = [k for k in bld.din if k not in maps[0]]
    assert not missing, missing
    res = run_bass_kernel_spmd(nc, maps, core_ids=list(range(8)))
    return res.results


def kernel(**inputs):
    results = run(inputs)
    outp = np.zeros((2, S_LEN, D), np.float32)
    for core in range(8):
        b, r = core // 4, core % 4
        o = results[core]["out"]
        for j in range(16):
            qi = 4 * j + r
            outp[b, 128 * qi:128 * qi + 128] = o[128 * j:128 * j + 128]
    return outp
```

```python
import numpy as np
from contextlib import ExitStack
import concourse.bass as bass
import concourse.mybir as mybir
from concourse.bass_utils import run_bass_kernel_spmd

F32 = mybir.dt.float32
BF16 = mybir.dt.bfloat16
AF = mybir.ActivationFunctionType
ALU = mybir.AluOpType

D = 1024
S_LEN = 8192
NT = 16
NB = 64
N_IN = 3348
DFF = 2816
EPS = 1e-6
O_GQ, O_GK, O_GV, O_GZ, O_GA, O_GB, O_NQ, O_KC, O_VC, O_KSL, O_VSL, O_KWN, O_VWN, O_NG = (
    0, 512, 1024, 1536, 2048, 2052, 2056, 2568, 2696, 2824, 2952, 3080, 3208, 3336)
BIG = 30000.0


class Buf:
    __slots__ = ("name", "w", "rs", "const", "excl")

    def __init__(self, name, const=False, excl=False):
        self.name = name
        self.w = None
        self.rs = []
        self.const = const
        self.excl = excl


class V:
    __slots__ = ("ap", "bs")

    def __init__(self, ap, bs):
        self.ap = ap
        self.bs = bs if isinstance(bs, (list, tuple)) else [bs]

    def bitcast(self, dt):
        return V(self.ap.bitcast(dt), self.bs)

    def re(self, pat, **kw):
        return V(self.ap.rearrange(pat, **kw), self.bs)

    def bc(self, shape):
        return V(self.ap.broadcast_to(shape), self.bs)

    def __getitem__(self, k):
        return V(self.ap[k], self.bs)


class Tl:
    def __init__(self, t, b):
        self.t = t
        self.b = b

    def __getitem__(self, k):
        return V(self.t[k], self.b)


class Op:
    __slots__ = ("eng", "fn", "deps", "dmaw", "signal", "sigval", "grp")


class DGroup:
    def __init__(self, name, sem, full=False):
        self.name = name
        self.sem = sem
        self.count = 0
        self.full = full


ENGS = ("pe", "act", "dve", "pool", "sp")


class Sched:
    def __init__(self, nc, es):
        self.nc = nc
        self.es = es
        self.eng = {"pe": nc.tensor, "act": nc.scalar, "dve": nc.vector, "pool": nc.gpsimd, "sp": nc.sync}
        self.sem = {e: es.enter_context(nc.semaphore("s_" + e)) for e in ENGS}
        self.cnt = {e: 0 for e in ENGS}
        self.seen = {e: {} for e in ENGS}
        self.ops = []
        self.bufs = []
        self.groups = []
        self.nins = 0

    def buf(self, name, const=False, excl=False):
        b = Buf(name, const, excl)
        self.bufs.append(b)
        return b

    def group(self, name, full=False):
        g = DGroup(name, self.es.enter_context(self.nc.semaphore("g_" + name)), full)
        self.groups.append(g)
        return g

    def op(self, eng, fn, r=(), w=(), grp=None):
        o = Op()
        o.eng = eng
        o.fn = fn
        o.signal = False
        o.sigval = None
        o.grp = grp
        deps = []
        for b in r:
            if b.w is not None:
                deps.append(b.w)
            if b.excl:
                deps.extend(x for x in b.rs if x.eng != eng)
        for b in w:
            if b.w is not None:
                deps.append(b.w)
            deps.extend(b.rs)
        seen = set()
        dd = []
        for d in deps:
            if id(d) in seen or d is o:
                continue
            seen.add(id(d))
            if d.eng == "pe" and eng == "pe":
                continue
            if grp is not None and grp.full and d.grp is grp:
                continue
            dd.append(d)
        o.deps = dd
        o.dmaw = {}
        for d in dd:
            if d.grp is None:
                d.signal = True
            else:
                o.dmaw[d.grp.name] = d.grp.count
        if grp is not None:
            grp.count += 1
        for b in w:
            b.w = o
            b.rs = []
        for b in r:
            if not b.const and b.w is not o:
                b.rs.append(o)
        self.ops.append(o)
        return o

    def flush(self, barrier=True):
        if barrier:
            last = {}
            for o in self.ops:
                last[o.eng] = o
            for e, o in last.items():
                if o.grp is None:
                    o.signal = True
        for o in self.ops:
            e = self.eng[o.eng]
            for d in o.deps:
                if d.grp is not None:
                    key = "g_" + d.grp.name
                    val = 16 * (d.grp.count if d.grp.full else o.dmaw[d.grp.name])
                    sem = d.grp.sem
                else:
                    key = d.eng
                    val = d.sigval
                    sem = self.sem[d.eng]
                    assert val is not None, (o.eng, d.eng)
                if self.seen[o.eng].get(key, 0) >= val:
                    continue
                e.wait_ge(sem, val)
                self.seen[o.eng][key] = val
            ins = o.fn(e)
            self.nins += 1
            if o.grp is not None:
                ins.then_inc(o.grp.sem, 16)
            elif o.signal:
                self.cnt[o.eng] += 1
                o.sigval = self.cnt[o.eng]
                ins.then_inc(self.sem[o.eng], 1)
        self.ops = []
        if barrier:
            for en in ENGS:
                e = self.eng[en]
                for e2 in ENGS:
                    if e2 == en or e2 == "sp":
                        continue
                    if self.cnt[e2] > self.seen[en].get(e2, 0):
                        e.wait_ge(self.sem[e2], self.cnt[e2])
                        self.seen[en][e2] = self.cnt[e2]
                for g in self.groups:
                    key = "g_" + g.name
                    if 16 * g.count > self.seen[en].get(key, 0):
                        e.wait_ge(g.sem, 16 * g.count)
                        self.seen[en][key] = 16 * g.count
            for b in self.bufs:
                b.w = None
                b.rs = []

    @staticmethod
    def _bs(*vs):
        out = []
        for v in vs:
            if isinstance(v, V):
                for b in v.bs:
                    if b not in out:
                        out.append(b)
        return out

    @staticmethod
    def _a(v):
        return v.ap if isinstance(v, V) else v

    def mm(self, out, lhsT, rhs, start=True, stop=True):
        o_, l_, r_ = out.ap, lhsT.ap, rhs.ap
        self.op("pe", lambda e: e.matmul(o_, lhsT=l_, rhs=r_, start=start, stop=stop),
                r=self._bs(lhsT, rhs), w=self._bs(out))

    def mm_(self, out, lhsT, rhs, start=True, stop=True, skip=False):
        o_, l_, r_ = out.ap, lhsT.ap, rhs.ap
        self.op("pe", lambda e: e.matmul(o_, lhsT=l_, rhs=r_, start=start, stop=stop, skip_group_check=skip),
                r=self._bs(lhsT, rhs), w=self._bs(out))

    def tr(self, out, in_, ident):
        o_, i_, d_ = out.ap, in_.ap, ident.ap
        self.op("pe", lambda e: e.transpose(o_, i_, d_), r=self._bs(in_, ident), w=self._bs(out))

    def act(self, out, in_, func, bias=None, scale=None, accum=None, eng="act"):
        kw = {}
        if bias is not None:
            kw["bias"] = self._a(bias)
        if scale is not None:
            kw["scale"] = self._a(scale)
        if accum is not None:
            kw["accum_out"] = accum.ap
        o_, i_ = out.ap, in_.ap
        self.op("act", lambda e: e.activation(out=o_, in_=i_, func=func, **kw),
                r=self._bs(in_, bias, scale), w=self._bs(out, accum))

    def tt(self, eng, out, in0, in1, op):
        o_, a_, b_ = out.ap, in0.ap, in1.ap
        self.op(eng, lambda e: e.tensor_tensor(out=o_, in0=a_, in1=b_, op=op),
                r=self._bs(in0, in1), w=self._bs(out))

    def ts(self, eng, out, in0, s1, s2=None, op0=ALU.mult, op1=None):
        o_, a_ = out.ap, in0.ap
        s1_, s2_ = self._a(s1), self._a(s2)
        kw = {}
        if op1 is not None:
            kw["op1"] = op1
        self.op(eng, lambda e: e.tensor_scalar(out=o_, in0=a_, scalar1=s1_, scalar2=s2_, op0=op0, **kw),
                r=self._bs(in0, s1, s2), w=self._bs(out))

    def stt(self, out, in0, scalar, in1, op0, op1):
        o_, a_, b_ = out.ap, in0.ap, in1.ap
        s_ = self._a(scalar)
        self.op("dve", lambda e: e.scalar_tensor_tensor(out=o_, in0=a_, scalar=s_, in1=b_, op0=op0, op1=op1),
                r=self._bs(in0, scalar, in1), w=self._bs(out))

    def copy(self, eng, out, in_):
        o_, i_ = out.ap, in_.ap
        if eng == "act":
            self.op("act", lambda e: e.copy(out=o_, in_=i_), r=self._bs(in_), w=self._bs(out))
        else:
            self.op(eng, lambda e: e.tensor_copy(out=o_, in_=i_), r=self._bs(in_), w=self._bs(out))

    def recip(self, out, in_):
        o_, i_ = out.ap, in_.ap
        self.op("dve", lambda e: e.reciprocal(out=o_, in_=i_), r=self._bs(in_), w=self._bs(out))

    def memset(self, eng, out, val):
        o_ = out.ap
        self.op(eng, lambda e: e.memset(o_, val), r=[], w=self._bs(out))

    def dma(self, out, in_, grp, eng="sp"):
        o_, i_ = self._a(out), self._a(in_)
        self.op(eng, lambda e: e.dma_start(out=o_, in_=i_), r=self._bs(in_), w=self._bs(out), grp=grp)


class Bank:
    def __init__(self, t, q):
        self.t = t
        self.q = q

    def f(self, c0, c1, p0=0, p1=128):
        return V(self.t[p0:p1, c0:c1], self.q[0:1])

    def b(self, c0, c1, p0=0, p1=128):
        return V(self.t[p0:p1, :].bitcast(BF16)[:, c0:c1], self.q[0:1])


W1_SEGS = ((0, 1536, 0), (2048, 2056, 1536), (2568, 3336, 1544))
W1_N = 2312
C_AB, C_KC, C_VC, C_KSL, C_VSL, C_KWN, C_VWN = 1536, 1544, 1672, 1800, 1928, 2056, 2184


class Builder:
    def __init__(self, debug=False, ntiles=NT):
        self.debug = debug
        self.ntiles = ntiles
        import os
        self.stop = int(os.environ.get('K_STOP', '0'))
        self.var = int(os.environ.get('K_VAR', '0'))
        self.halo = int(os.environ.get('K_HALO', '0'))
        self.nc = bass.Bass("TRN2", target_bir_lowering=False)
        self.din = {}
        self.dout = {}

    def inp(self, name, shape, dt=F32):
        self.din[name] = self.nc.dram_tensor(name, list(shape), dt, kind="ExternalInput").ap()
        return self.din[name]

    def outp(self, name, shape, dt=F32):
        self.dout[name] = self.nc.dram_tensor(name, list(shape), dt, kind="ExternalOutput").ap()
        return self.dout[name]

    def scratch(self, name, shape, dt=F32):
        if self.debug:
            return self.outp(name, shape, dt)
        return self.nc.dram_tensor(name, list(shape), dt, kind="Internal").ap()

    def sb(self, es, name, shape, dt=F32, const=False):
        t = es.enter_context(self.nc.sbuf_tensor(name, list(shape), dt))
        return Tl(t, self.S.buf(name, const))

    def cload(self, es, name, shape, grp):
        d = self.inp(name, shape)
        t = self.sb(es, "c_" + name, shape, F32, const=True)
        idx = tuple(slice(None) for _ in shape)
        self.S.dma(t[idx], d[idx], grp)
        return t

    def build(self, upto=9):
        nc = self.nc
        I = self.inp
        self.xb = I("xb", [S_LEN, D])
        self.xo = I("xo", [2048, D])
        cT = I("cT", [128, 8])
        ada_w = I("ada_w", [D, 6 * D])
        self.w_in = I("w_in", [D, N_IN])
        self.out = self.outp("out", [2048, D])
        self.o_own = self.scratch("o_own", [16, 128, 512])
        self.kslT_d = self.scratch("kslT_d", [128, S_LEN], BF16)
        self.kwnT_d = self.scratch("kwnT_d", [128, S_LEN], BF16)
        self.kcT_d = self.scratch("kcT_d", [128, S_LEN], BF16)
        self.vcT_d = self.scratch("vcT_d", [128, S_LEN], BF16)
        self.vsl_d = self.scratch("vsl_d", [128, NB, 128], BF16)
        self.vwn_d = self.scratch("vwn_d", [128, NB, 128], BF16)
        self.mix_d = self.scratch("mix_d", [128, 8, 2048], BF16)
        self.oprev_d = self.scratch("oprev_d", [16, 2, 512])
        self.mixh_d = self.scratch("mixh_d", [128, 8, 32], BF16)

        with ExitStack() as top:
            S = self.S = Sched(nc, top)
            self.g_const = g_const = S.group("const", full=True)
            self.g_out = S.group("outw")
            self.idf = idf = self.cload(top, "identf", [128, 128], g_const)
            self.idb = idb = self.sb(top, "idb", [128, 128], BF16)
            S.copy("dve", idb[:, :], idf[:, :])
            self.modT = modT = self.sb(top, "modT", [128, 48])
            self.s1 = s1 = self.sb(top, "s1", [128, 8])
            self.s2 = s2 = self.sb(top, "s2", [128, 8])
            n1 = self.cload(top, "n1w", [128, 8], g_const)
            n2 = self.cload(top, "n2w", [128, 8], g_const)
            self.pb = []
            for i in range(8):
                t = top.enter_context(nc.psum_tensor(f"pb{i}", [128, 512], F32))
                self.pb.append(Bank(t, [S.buf(f"pb{i}", excl=True)]))
            pb = self.pb
            self.bankA = [pb[2], pb[3]]
            self.bankB = [pb[4], pb[5]]
            self.bankC = [pb[0], pb[1]]

            with ExitStack() as es:
                ct = self.sb(es, "ct", [128, 8])
                sc = self.sb(es, "sc", [128, 8])
                abT = self.cload(es, "ada_bT", [128, 48], g_const)
                S.dma(ct[:, :], cT[:, :], g_const)
                S.act(sc[:, :], ct[:, :], AF.Silu)
                aw = [self.sb(es, f"aw{i}", [128, 6 * D]) for i in range(2)]
                g_aw = [S.group(f"aw{i}") for i in range(2)]
                for k in range(8):
                    sl = k % 2
                    for hh in range(4):
                        S.dma(aw[sl][:, hh * 1536:(hh + 1) * 1536],
                              ada_w[k * 128:(k + 1) * 128, hh * 1536:(hh + 1) * 1536], g_aw[sl])
                    pm = pb[k % 2]
                    for cc in range(48):
                        S.mm(pm.f(cc, cc + 1), lhsT=aw[sl][:, cc * 128:(cc + 1) * 128], rhs=sc[:, k:k + 1])
                    S.tt("dve", modT[:, :], pm.f(0, 48), (abT if k == 0 else modT)[:, :], ALU.add)
                S.stt(s1[:, :], modT[:, 8:16], 1.0, n1[:, :], ALU.add, ALU.mult)
                S.stt(s2[:, :], modT[:, 32:40], 1.0, n2[:, :], ALU.add, ALU.mult)
                S.flush()

            if upto >= 1:
                self.phase1()
            keep = dict(qnT=self.sb(top, "qnT", [128, 4, 2048], BF16), gates=self.sb(top, "gates", [128, 16, 12]),
                        qnTh=self.sb(top, "qnTh", [128, 4, 32], BF16), gatesh=self.sb(top, "gatesh", [32, 12]))
            if upto >= 2:
                self.phase2(keep)
                if self.debug:
                    S.dma(self.outp("d_qnT", [128, 4, 2048], BF16)[:, :, :], keep["qnT"][:, :, :], self.g_out)
                    S.dma(self.outp("d_gates", [128, 16, 12])[:, :, :], keep["gates"][:, :, :], self.g_out)
            if upto >= 3:
                S.flush()
                self.phase3(keep)
            if upto >= 4:
                S.flush()
                if self.debug:
                    self.dx1 = self.outp("d_x1", [4, 128, 4, D])
                self.phase45()
            S.flush()
        return nc

    def front(self, F, xt, nb, hT, c0, scol, sh_c0, b0=0):
        S = self.S
        pb = self.pb
        ssq, rt, rstd, junk, xs = F["ssq"], F["rt"], F["rstd"], F["junk"], F["xs"]
        for b2 in range(nb):
            S.act(junk[:, :], xt[:, b0 + b2, :], AF.Square, accum=ssq[:, b2:b2 + 1])
        S.act(rt[:, 0:nb], ssq[:, 0:nb], AF.Ln, bias=EPS, scale=1.0 / D)
        S.act(rstd[:, 0:nb], rt[:, 0:nb], AF.Exp, scale=-0.5)
        for b2 in range(nb):
            S.ts("pool", xs[:, b2, :], xt[:, b0 + b2, :], rstd[:, b2:b2 + 1], 1.0, ALU.mult, ALU.mult)
        w = nb * 128
        for half in range(2):
            bank = pb[half]
            for k in range(half * 4, half * 4 + 4):
                off = (k % 4) * 256
                for b2 in range(nb):
                    S.tr(bank.b(off + b2 * 128, off + (b2 + 1) * 128), xs[:, b2, k * 128:(k + 1) * 128],
                         self.idb[:, :])
            for k in range(half * 4, half * 4 + 4):
                off = (k % 4) * 256
                S.ts("dve", hT[:, k, c0:c0 + w], bank.b(off, off + w), scol[:, k:k + 1],
                     self.modT[:, sh_c0 + k:sh_c0 + k + 1], ALU.mult, ALU.add)

    def front32(self, F, xt, hT, scol, sh_c0):
        S = self.S
        bank = self.pb[0]
        ssq, rt, rstd, junk, xs = F["ssq"], F["rt"], F["rstd"], F["junk"], F["xs"]
        S.act(junk[0:32, :], xt[:, :], AF.Square, accum=ssq[0:32, 0:1])
        S.act(rt[0:32, 0:1], ssq[0:32, 0:1], AF.Ln, bias=EPS, scale=1.0 / D)
        S.act(rstd[0:32, 0:1], rt[0:32, 0:1], AF.Exp, scale=-0.5)
        S.ts("pool", xs[0:32, 0, :], xt[:, :], rstd[0:32, 0:1], 1.0, ALU.mult, ALU.mult)
        for k in range(8):
            S.tr(bank.b(k * 32, (k + 1) * 32), xs[0:32, 0, k * 128:(k + 1) * 128], self.idb[0:32, 0:32])
        for k in range(8):
            S.ts("dve", hT[:, k, :], bank.b(k * 32, (k + 1) * 32), scol[:, k:k + 1],
                 self.modT[:, sh_c0 + k:sh_c0 + k + 1], ALU.mult, ALU.add)

    def phase1(self):
        S = self.S
        nc = self.nc
        pb = self.pb
        gc_ = S.group("const1", full=True)
        idb, idf = self.idb, self.idf
        with ExitStack() as es:
            tri2 = self.cload(es, "tri2", [128, 128], gc_)
            blk2 = self.cload(es, "blk2", [128, 128], gc_)
            onesf = self.cload(es, "onesf", [128, 128], gc_)
            cind = self.cload(es, "cind", [128, 2], gc_)
            ma4 = self.cload(es, "ma4", [128, 512], gc_)
            mb4 = self.cload(es, "mb4", [128, 512], gc_)
            sel = self.cload(es, "sel", [128, 4], gc_)
            hsel = self.cload(es, "hsel", [128, 4], gc_)
            cw = self.cload(es, "gcw", [128, 48], gc_)
            dtb = self.cload(es, "dtb", [128, 4], gc_)
            alog = self.cload(es, "alog", [128, 4], gc_)
            kslw = self.cload(es, "kslw", [128, 1], gc_)
            kwnw = self.cload(es, "kwnw", [128, 1], gc_)
            negones = self.sb(es, "negones", [128, 128])
            S.ts("dve", negones[:, :], onesf[:, :], -1.0, None, ALU.mult)
            onesb = self.sb(es, "onesb", [128, 128], BF16)
            S.copy("dve", onesb[:, :], onesf[:, :])
            i4b = self.sb(es, "i4b", [128, 512], BF16)
            for h in range(4):
                S.copy("dve", i4b[:, h * 128:(h + 1) * 128], idf[:, :])
            negA = self.sb(es, "negA", [128, 4])
            S.act(negA[:, :], alog[:, :], AF.Exp)
            S.ts("dve", negA[:, :], negA[:, :], -1.0, None, ALU.mult)

            winb = self.sb(es, "winb", [128, 8, W1_N], BF16)
            with ExitStack() as es2:
                wst = [self.sb(es2, f"wst{i}", [128, W1_N]) for i in range(2)]
                g_w = [S.group(f"wst{i}") for i in range(2)]
                for k in range(8):
                    sl = k % 2
                    for (a, b_, o) in W1_SEGS:
                        S.dma(wst[sl][:, o:o + (b_ - a)], self.w_in[k * 128:(k + 1) * 128, a:b_], g_w[sl])
                    S.copy("dve", winb[:, k, 0:1024], wst[sl][:, 0:1024])
                    S.copy("pool", winb[:, k, 1024:W1_N], wst[sl][:, 1024:W1_N])
                S.flush()

            F = dict(ssq=self.sb(es, "f_ssq", [128, 4]), rt=self.sb(es, "f_rt", [128, 4]),
                     rstd=self.sb(es, "f_rstd", [128, 4]), junk=self.sb(es, "f_junk", [128, 1024], BF16),
                     xs=self.sb(es, "f_xs", [128, 2, 1024], BF16))
            xt = [self.sb(es, f"xt{i}", [128, 2, 1024]) for i in range(2)]
            g_x = [S.group(f"xt{i}") for i in range(2)]
            hT = self.sb(es, "hT", [128, 8, 512], BF16)
            pre = [self.sb(es, f"pre{i}", [128, 515]) for i in range(3)]
            hist = self.sb(es, "hist", [128, 12, 3])
            S.memset("pool", hist[:, :, :], 0.0)
            acc = [self.sb(es, f"cacc{i}", [128, 512]) for i in range(3)]
            yfs = [self.sb(es, f"yfs{i}", [128, 512], BF16 if i < 8 else F32) for i in range(10)]
            sq = [self.sb(es, f"sq{i}", [128, 512], BF16) for i in range(3)]
            rtt = [self.sb(es, f"rtt{i}", [128, 512]) for i in range(3)]
            qT = self.sb(es, "qT", [128, 4, 512], BF16)
            kT = self.sb(es, "kT", [128, 4, 512], BF16)
            vT = self.sb(es, "vT", [128, 4, 512], BF16)
            st_f = [[self.sb(es, f"stf{c}_{i}", [128, 512], BF16) for i in range(2)] for c in range(4)]
            st_v = [[self.sb(es, f"stv{c}_{i}", [128, 4, 128], BF16) for i in range(2)] for c in range(2)]
            g_sf = [[S.group(f"sf{c}_{i}") for i in range(2)] for c in range(4)]
            g_sv = [[S.group(f"sv{c}_{i}") for i in range(2)] for c in range(2)]
            ab = self.sb(es, "ab", [128, 4, 8])
            self.sm4 = sm4 = self.sb(es, "sm4", [128, 4, 64])
            S32 = self.sb(es, "S32", [128, 4, 128])
            Sb = [self.sb(es, f"Sb{i}", [128, 4, 128], BF16) for i in range(2)]
            self.cidx = 0
            S.memset("pool", S32[:, :, :], 0.0)
            S.memset("pool", Sb[0][:, :, :], 0.0)
            oacc = [self.sb(es, f"oacc{i}", [128, 512]) for i in range(2)]
            g_o = [S.group(f"oacc{i}") for i in range(2)]
            oph = [self.sb(es, f"oph{i}", [128, 512]) for i in range(2)]
            g_oh = [S.group(f"oph{i}") for i in range(2)]
            G = []
            for s_ in range(2):
                g = {}
                for nm, shp, dt in (("TG", [128, 4, 128], F32), ("Xs", [128, 512], F32),
                                    ("XA", [128, 512], F32), ("XB", [128, 512], F32), ("EG", [128, 4, 128], F32),
                                    ("Nb", [128, 4, 128], BF16), ("aT", [128, 4, 128], BF16),
                                    ("qdT", [128, 4, 128], BF16), ("kw", [128, 4, 128], BF16),
                                    ("kd", [128, 4, 128], BF16), ("vb", [128, 4, 128], BF16),
                                    ("NTb", [128, 4, 128], BF16), ("RTb", [128, 4, 128], BF16),
                                    ("P0", [128, 4, 128], BF16), ("P1", [128, 4, 128], BF16),
                                    ("Q0", [128, 4, 128], BF16), ("Q1", [128, 4, 128], BF16),
                                    ("u", [128, 4, 128], F32), ("wTb", [128, 4, 128], BF16),
                                    ("vn", [128, 4, 128], BF16)):
                    g[nm] = self.sb(es, f"g{s_}_{nm}", shp, dt)
                G.append(g)

            xrows = self.xb.rearrange("(n b p) d -> n p b d", b=2, p=128)

            def load_x(n):
                S.dma(xt[n % 2][:, :, :], xrows[n], g_x[n % 2])

            load_x(0)
            for ti in range(self.ntiles):
                for hf in range(2):
                    n = 2 * ti + hf
                    if n + 1 < 2 * NT:
                        load_x(n + 1)
                    self.front(F, xt[n % 2], 2, hT, hf * 256, self.s1, 0)
                if self.stop == 1:
                    continue
                nsa_ch = ((C_KC, self.kcT_d, None), (C_VC, self.vcT_d, None), (C_KSL, self.kslT_d, kslw),
                          (C_KWN, self.kwnT_d, kwnw))

                def st1(c, pos):
                    bank = pb[2 + (pos % 2)]
                    coff = c * 128 if c < 12 else nsa_ch[c - 12][0]
                    for k in range(8):
                        S.mm(bank.f(0, 512), lhsT=winb[:, k, coff:coff + 128], rhs=hT[:, k, :],
                             start=(k == 0), stop=(k == 7))
                    if c < 12:
                        p_ = pre[pos % 3]
                        S.copy("pool", p_[:, 0:3], hist[:, c, :])
                        S.copy("act", p_[:, 3:515], bank.f(0, 512))
                        S.copy("pool", hist[:, c, :], p_[:, 512:515])
                        a_ = acc[pos % 3]
                        S.ts("pool", a_[:, :], p_[:, 0:512], cw[:, c * 4:c * 4 + 1], 1.0, ALU.mult, ALU.mult)
                        for j in range(1, 4):
                            S.stt(a_[:, :], p_[:, j:j + 512], cw[:, c * 4 + j:c * 4 + j + 1], a_[:, :], ALU.mult, ALU.add)
                    else:
                        ci = c - 12
                        wcol = nsa_ch[ci][2]
                        if wcol is None:
                            S.copy("act", st_f[ci][ti % 2][:, :], bank.f(0, 512))
                            S.dma(nsa_ch[ci][1][:, ti * 512:(ti + 1) * 512], st_f[ci][ti % 2][:, :], g_sf[ci][ti % 2])
                        else:
                            S.copy("act", yfs[c - 6][:, :], bank.f(0, 512))

                def st2(c, pos):
                    h = c % 4
                    if c >= 12:
                        return
                    if c >= 8:
                        S.act(vT[:, h, :], acc[pos % 3][:, :], AF.Silu)
                        return
                    S.act(yfs[c][:, :], acc[pos % 3][:, :], AF.Silu)

                def st3(c, i_):
                    h = c % 4
                    bk2 = pb[4 + (i_ % 2)]
                    yi = c if c < 8 else c - 6
                    S.act(sq[i_ % 3][:, :], yfs[yi][:, :], AF.Square)
                    S.mm(bk2.f(0, 512), lhsT=onesb[:, :], rhs=sq[i_ % 3][:, :])
                    r_ = rtt[i_ % 3]
                    if c < 4:
                        S.act(r_[:, :], bk2.f(0, 512), AF.Ln, bias=EPS * 128.0, scale=128.0)
                    elif c < 8:
                        S.act(r_[:, :], bk2.f(0, 512), AF.Ln, bias=EPS, scale=1.0)
                    else:
                        S.act(r_[:, :], bk2.f(0, 512), AF.Ln, bias=EPS, scale=1.0 / 128.0)
                    S.act(r_[:, :], r_[:, :], AF.Exp, scale=-0.5)
                    if c < 8:
                        S.tt("pool", (qT if c < 4 else kT)[:, h, :], yfs[yi][:, :], r_[:, :], ALU.mult)
                    else:
                        ci = c - 12
                        st = st_f[ci][ti % 2]
                        S.stt(st[:, :], yfs[yi][:, :], nsa_ch[ci][2][:, 0:1], r_[:, :], ALU.mult, ALU.mult)
                        S.dma(nsa_ch[ci][1][:, ti * 512:(ti + 1) * 512], st[:, :], g_sf[ci][ti % 2])

                order = [12, 13, 14, 15] + list(range(12))
                for i in range(len(order) + 1):
                    if i < len(order):
                        st1(order[i], i)
                    if 0 <= i - 1 < len(order):
                        st2(order[i - 1], i - 1)
                if self.stop == 3:
                    continue
                for blk in range(4):
                    bank = pb[6]
                    for vi, coff in enumerate((C_VSL, C_VWN)):
                        for k in range(8):
                            if self.var == 4:
                                break
                            S.mm(bank.f(vi * 128, (vi + 1) * 128), lhsT=hT[:, k, blk * 128:(blk + 1) * 128],
                                 rhs=winb[:, k, coff:coff + 128], start=(k == 0), stop=(k == 7))
                    for k in range(8):
                        if self.var == 1:
                            break
                        S.mm(bank.f(256, 264), lhsT=hT[:, k, blk * 128:(blk + 1) * 128],
                             rhs=winb[:, k, C_AB:C_AB + 8], start=(k == 0), stop=(k == 7))
                    if self.var != 3:
                        S.copy("act", st_v[0][ti % 2][:, blk, :], bank.f(0, 128))
                        S.copy("act", st_v[1][ti % 2][:, blk, :], bank.f(128, 256))
                    if self.var != 5:
                        S.copy("dve", ab[:, blk, :], bank.f(256, 264))
                if self.var != 2:
                    S.dma(self.vsl_d[:, ti * 4:(ti + 1) * 4, :], st_v[0][ti % 2][:, :, :], g_sv[0][ti % 2])
                    S.dma(self.vwn_d[:, ti * 4:(ti + 1) * 4, :], st_v[1][ti % 2][:, :, :], g_sv[1][ti % 2])

                for i_, c in enumerate([14, 15, 0, 1, 2, 3, 4, 5, 6, 7]):
                    st3(c, i_)
                if self.stop == 4:
                    continue
                for pair in range(2):
                    blks = (2 * pair, 2 * pair + 1)
                    if pair == 0:
                        self.gdn_small(sm4, ab, tri2, blk2, onesf, cind, dtb, negA)
                    for s_, blk in enumerate(blks):
                        self.gdn_local_1(G[s_], s_, blk, sm4, tri2, negones, onesf, ma4, mb4, qT, kT, vT)
                    if self.stop == 5:
                        continue
                    self.gdn_solve(G, blks, i4b)
                    if self.stop == 6:
                        continue
                    for s_, blk in enumerate(blks):
                        self.gdn_uw(G[s_], s_)
                    if self.stop == 7:
                        continue
                    for s_, blk in enumerate(blks):
                        oa = oacc[ti % 2]
                        self.gdn_recur(G[s_], S32, Sb, sel, blk, oa, hsel, oph[ti % 2])
                S.dma(self.o_own[ti], oacc[ti % 2][:, :], g_o[ti % 2])
                S.dma(self.oprev_d[ti], oph[ti % 2][126:128, :], g_oh[ti % 2])
            S.flush()

    def gdn_small(self, sm4, ab, tri2, blk2, onesf, cind, dtb, negA):
        S = self.S
        bS = self.pb[6]
        x_, t_, gg, be, nb_ = (sm4[:, :, 0:4], sm4[:, :, 4:8], sm4[:, :, 8:12], sm4[:, :, 12:16], sm4[:, :, 16:20])
        gci, gcv, gl, egc, ekd, bw, glS = (sm4[:, :, 20:28], sm4[:, :, 28:32], sm4[:, :, 32:36], sm4[:, :, 36:40],
                                           sm4[:, :, 40:44], sm4[:, :, 44:48], sm4[:, :, 48:56])
        bc = lambda t: V(t.t[:, :].unsqueeze(1).broadcast_to([128, 4, 4]), [t.b])
        S.tt("dve", x_, ab[:, :, 0:4], bc(dtb), ALU.add)
        S.stt(t_, x_, -1.0, x_, ALU.mult, ALU.max)
        S.act(t_, t_, AF.Exp, scale=-1.0)
        S.act(t_, t_, AF.Ln, bias=1.0)
        S.stt(t_, x_, 0.0, t_, ALU.max, ALU.add)
        S.tt("dve", gg, t_, bc(negA), ALU.mult)
        S.act(be, ab[:, :, 4:8], AF.Exp, scale=-1.0)
        S.ts("dve", be, be, 1.0, None, ALU.add)
        S.recip(be, be)
        S.ts("dve", nb_, be, -1.0, None, ALU.mult)
        for i in range(2):
            for blk in range(4):
                S.ts("dve", sm4[:, blk, 20:28].re("p (h i) -> p h i", i=2)[:, :, i], sm4[:, blk, 8:12], cind[:, i:i + 1],
                     None, ALU.mult)
        for blk in range(4):
            S.mm(bS.f(384 + blk * 4, 388 + blk * 4), lhsT=tri2[:, :], rhs=sm4[:, blk, 8:12])
            S.mm(bS.f(400 + blk * 4, 404 + blk * 4), lhsT=blk2[:, :], rhs=sm4[:, blk, 8:12])
            S.mm(bS.f(416 + blk * 8, 424 + blk * 8), lhsT=onesf[:, :], rhs=sm4[:, blk, 20:28])
        S.copy("dve", gcv, bS.f(384, 400).re("p (b h) -> p b h", b=4))
        S.copy("dve", gl, bS.f(400, 416).re("p (b h) -> p b h", b=4))
        S.act(glS, bS.f(416, 448).re("p (b c) -> p b c", b=4), AF.Exp)
        S.act(egc, gcv, AF.Exp)
        S.tt("dve", ekd, gl, gcv, ALU.subtract)
        S.act(ekd, ekd, AF.Exp)
        S.tt("dve", bw, be, egc, ALU.mult)

    def gdn_local_1(self, g, s_, blk, sm4, tri2, negones, onesf, ma4, mb4, qT, kT, vT):
        S = self.S
        pb = self.pb
        cs = slice(blk * 128, (blk + 1) * 128)
        sm = V(sm4.t[:, blk, :], [sm4.b])
        gg, be, nb_, gcv, ekd, bw = (sm[:, 8:12], sm[:, 12:16], sm[:, 16:20], sm[:, 28:32], sm[:, 40:44], sm[:, 44:48])
        bK = self.bankA[s_]
        bQ = self.bankB[s_]
        bX = self.bankC[s_]
        bT = pb[7]
        for h in range(4):
            S.mm(bK.f(h * 128, (h + 1) * 128), lhsT=kT[:, h, cs], rhs=kT[:, h, cs])
        for h in range(4):
            S.mm(bQ.f(h * 128, (h + 1) * 128), lhsT=kT[:, h, cs], rhs=qT[:, h, cs])
        for h in range(4):
            S.tr(bT.b(h * 128, (h + 1) * 128), kT[:, h, cs], self.idb[:, :])
        for h in range(4):
            S.tr(bT.b(512 + h * 128, 512 + (h + 1) * 128), vT[:, h, cs], self.idb[:, :])
        for h in range(4):
            S.ts("pool", g["TG"][:, h, :], tri2[:, :], gg[:, h:h + 1], 1.0, ALU.mult, ALU.mult)
        for h in range(4):
            S.mm(bX.f(h * 128, (h + 1) * 128), lhsT=onesf[:, :], rhs=g["TG"][:, h, :], start=True, stop=False)
            S.mm(bX.f(h * 128, (h + 1) * 128), lhsT=g["TG"][:, h, :], rhs=negones[:, :], start=False, stop=True)
        S.copy("act", g["Xs"][:, :], bX.f(0, 512))
        for h in range(4):
            S.act(g["EG"][:, h, :], bX.f(h * 128, (h + 1) * 128), AF.Exp, bias=gcv[:, h:h + 1])
        S.tt("pool", g["XA"][:, :], g["Xs"][:, :], ma4[:, :], ALU.add)
        S.tt("pool", g["XB"][:, :], g["Xs"][:, :], mb4[:, :], ALU.add)
        S.act(g["XA"][:, :], g["XA"][:, :], AF.Exp, scale=-1.0)
        S.act(g["XB"][:, :], g["XB"][:, :], AF.Exp)
        for h in range(4):
            S.stt(g["Nb"][:, h, :], bK.f(h * 128, (h + 1) * 128), nb_[:, h:h + 1],
                  g["XA"][:, h * 128:(h + 1) * 128], ALU.mult, ALU.mult)
        S.tt("dve", g["aT"][:, :, :].re("p h c -> p (h c)"), bQ.f(0, 512), g["XB"][:, :], ALU.mult)
        S.tt("pool", g["qdT"][:, :, :], qT[:, :, cs], g["EG"][:, :, :], ALU.mult)
        kt3 = bT.b(0, 512).re("p (h d) -> p h d", h=4)
        vt3 = bT.b(512, 1024).re("p (h d) -> p h d", h=4)
        S.tt("dve", g["kw"][:, :, :], kt3, V(bw.ap.unsqueeze(2).broadcast_to([128, 4, 128]), bw.bs), ALU.mult)
        S.tt("dve", g["kd"][:, :, :], kt3, V(ekd.ap.unsqueeze(2).broadcast_to([128, 4, 128]), ekd.bs), ALU.mult)
        S.tt("dve", g["vb"][:, :, :], vt3, V(be.ap.unsqueeze(2).broadcast_to([128, 4, 128]), be.bs), ALU.mult)

    def gdn_solve(self, G, blks, i4b):
        S = self.S
        idb = self.idb
        for s_ in range(len(blks)):
            g = G[s_]
            bA, bC = self.bankA[s_], self.bankC[s_]
            for h in range(4):
                S.mm(bA.f(h * 128, (h + 1) * 128), lhsT=g["Nb"][:, h, :], rhs=idb[:, :])
            S.copy("act", g["NTb"][:, :, :].re("p h c -> p (h c)"), bA.f(0, 512))
            S.mm(bC.f(0, 512), lhsT=idb[:, :], rhs=i4b[:, :], start=True, stop=False)
            for h in range(4):
                S.mm(bC.f(h * 128, (h + 1) * 128), lhsT=g["Nb"][:, h, :], rhs=idb[:, :], start=False, stop=(h == 3))
            S.copy("act", g["RTb"][:, :, :].re("p h c -> p (h c)"), bC.f(0, 512))
        P = [G[s_]["Nb"] for s_ in range(len(blks))]
        Q = [G[s_]["NTb"] for s_ in range(len(blks))]
        for k in range(1, 6):
            for s_ in range(len(blks)):
                g = G[s_]
                bA, bB, bC = self.bankA[s_], self.bankB[s_], self.bankC[s_]
                Pn = g["P%d" % (k % 2)]
                Qn = g["Q%d" % (k % 2)]
                for h in range(4):
                    S.mm(bA.f(h * 128, (h + 1) * 128), lhsT=Q[s_][:, h, :], rhs=P[s_][:, h, :])
                if k < 5:
                    for h in range(4):
                        S.mm(bB.f(h * 128, (h + 1) * 128), lhsT=P[s_][:, h, :], rhs=Q[s_][:, h, :])
                S.copy("dve", Pn[:, :, :].re("p h c -> p (h c)"), bA.f(0, 512))
                if k < 5:
                    S.copy("act", Qn[:, :, :].re("p h c -> p (h c)"), bB.f(0, 512))
                S.mm(bC.f(0, 512), lhsT=idb[:, :], rhs=g["RTb"][:, :, :].re("p h c -> p (h c)"), start=True, stop=False)
                for h in range(4):
                    S.mm(bC.f(h * 128, (h + 1) * 128), lhsT=Pn[:, h, :], rhs=g["RTb"][:, h, :],
                         start=False, stop=(h == 3))
                S.copy("act", g["RTb"][:, :, :].re("p h c -> p (h c)"), bC.f(0, 512))
                P[s_] = Pn
                Q[s_] = Qn

    def gdn_uw(self, g, s_):
        S = self.S
        bA, bB = self.bankA[s_], self.bankB[s_]
        for h in range(4):
            S.mm(bA.f(h * 128, (h + 1) * 128), lhsT=g["RTb"][:, h, :], rhs=g["vb"][:, h, :])
        for h in range(4):
            S.mm(bB.f(h * 128, (h + 1) * 128), lhsT=g["kw"][:, h, :], rhs=g["RTb"][:, h, :])
        S.copy("act", g["u"][:, :, :].re("p h c -> p (h c)"), bA.f(0, 512))
        S.copy("dve", g["wTb"][:, :, :].re("p h c -> p (h c)"), bB.f(0, 512))

    def gdn_recur(self, g, S32, Sb, sel, blk, oa, hsel, oh):
        S = self.S
        pb = self.pb
        bV, bO, bS_ = pb[2], pb[3], pb[4]
        sm = V(self.sm4.t[:, blk % 4, :], [self.sm4.b])
        for i in range(2):
            r0, r1 = 64 * i, 64 * i + 64
            cur = Sb[self.cidx % 2]
            nxt = Sb[(self.cidx + 1) % 2]
            self.cidx += 1
            for h in range(4):
                S.mm(bV.f(h * 128, (h + 1) * 128, r0, r1), lhsT=g["wTb"][:, h, r0:r1], rhs=cur[:, h, :])
            S.tt("dve", g["vn"][r0:r1, :, :].re("p h c -> p (h c)"), g["u"][r0:r1, :, :].re("p h c -> p (h c)"),
                 bV.f(0, 512, r0, r1), ALU.subtract)
            for h in range(4):
                S.mm(bS_.f(h * 128, (h + 1) * 128), lhsT=g["kd"][r0:r1, h, :], rhs=g["vn"][r0:r1, h, :])
            for h in range(4):
                S.stt(S32[:, h, :], S32[:, h, :], sm[:, 48 + 2 * h + i:49 + 2 * h + i], bS_.f(h * 128, (h + 1) * 128),
                      ALU.mult, ALU.add)
            S.copy("act", nxt[:, :, :], S32[:, :, :])
            for h in range(4):
                S.mm(bO.f(h * 128, (h + 1) * 128, r0, r1), lhsT=g["qdT"][:, h, r0:r1], rhs=cur[:, h, :],
                     start=True, stop=False)
                S.mm(bO.f(h * 128, (h + 1) * 128, r0, r1), lhsT=g["aT"][r0:r1, h, r0:r1], rhs=g["vn"][r0:r1, h, :],
                     start=False, stop=True)
        r_ = blk % 4
        if r_ == 0:
            S.ts("dve", oa[:, :], bO.f(0, 512), sel[:, 0:1], None, ALU.mult)
        else:
            S.stt(oa[:, :], bO.f(0, 512), sel[:, r_:r_ + 1], oa[:, :], ALU.mult, ALU.add)
        if r_ == 0:
            S.ts("dve", oh[64:128, :], bO.f(0, 512, 64, 128), hsel[64:128, 0:1], None, ALU.mult)
        else:
            S.stt(oh[64:128, :], bO.f(0, 512, 64, 128), hsel[64:128, r_:r_ + 1], oh[64:128, :], ALU.mult, ALU.add)

    def phase2(self, keep):
        S = self.S
        pb = self.pb
        gc_ = S.group("const2", full=True)
        qnT, gates = keep["qnT"], keep["gates"]
        with ExitStack() as es:
            qnw = self.cload(es, "qnw", [128, 1], gc_)
            gonw4 = self.cload(es, "gonw4", [128, 512], gc_)
            onesf = self.cload(es, "onesf2", [128, 128], gc_)
            onesb = self.sb(es, "onesb2", [128, 128], BF16)
            S.copy("dve", onesb[:, :], onesf[:, :])
            winb2 = self.sb(es, "winb2", [128, 8, 1036], BF16)
            with ExitStack() as es2:
                wst = [self.sb(es2, f"w2st{i}", [128, 1036]) for i in range(2)]
                g_w = [S.group(f"w2st{i}") for i in range(2)]
                for k in range(8):
                    sl = k % 2
                    for (a, b_, o) in ((O_GZ, O_GZ + 512, 0), (O_NQ, O_NQ + 512, 512), (O_NG, O_NG + 12, 1024)):
                        S.dma(wst[sl][:, o:o + (b_ - a)], self.w_in[k * 128:(k + 1) * 128, a:b_], g_w[sl])
                    S.copy("dve" if k % 2 else "pool", winb2[:, k, :], wst[sl][:, :])
                S.flush()
            F = dict(ssq=self.sb(es, "f2_ssq", [128, 4]), rt=self.sb(es, "f2_rt", [128, 4]),
                     rstd=self.sb(es, "f2_rstd", [128, 4]), junk=self.sb(es, "f2_junk", [128, 1024], BF16),
                     xs=self.sb(es, "f2_xs", [128, 2, 1024], BF16))
            xt = [self.sb(es, f"x2t{i}", [128, 2, 1024]) for i in range(2)]
            g_x = [S.group(f"x2t{i}") for i in range(2)]
            hT = self.sb(es, "h2T_", [128, 8, 512], BF16)
            yf = [self.sb(es, f"y2f{i}", [128, 512]) for i in range(2)]
            sq = [self.sb(es, f"s2q{i}", [128, 512], BF16) for i in range(2)]
            rtt = [self.sb(es, f"r2tt{i}", [128, 512]) for i in range(2)]
            og = [self.sb(es, f"og{i}", [128, 512]) for i in range(2)]
            g_og = [S.group(f"og{i}") for i in range(2)]
            zs = [self.sb(es, f"zs{i}", [128, 512]) for i in range(2)]
            t1 = [self.sb(es, f"p2t{i}", [128, 512]) for i in range(2)]
            mg = [self.sb(es, f"mg{i}", [128, 512], BF16) for i in range(2)]
            mgT = [self.sb(es, f"mgT{i}", [128, 4, 128], BF16) for i in range(2)]
            g_mg = [S.group(f"mgT{i}") for i in range(2)]
            osq = self.sb(es, "osq", [128, 8])
            xrows = self.xo.rearrange("(n b p) d -> n p b d", b=2, p=128)

            def load_x(n):
                S.dma(xt[n % 2][:, :, :], xrows[n], g_x[n % 2])

            load_x(0)
            for t in range(4):
                for hf in range(2):
                    n = 2 * t + hf
                    if n + 1 < 8:
                        load_x(n + 1)
                    self.front(F, xt[n % 2], 2, hT, hf * 256, self.s1, 0)
                for h in range(4):
                    bank = pb[2 + (h % 2)]
                    for k in range(8):
                        S.mm(bank.f(0, 512), lhsT=winb2[:, k, 512 + h * 128:512 + (h + 1) * 128], rhs=hT[:, k, :],
                             start=(k == 0), stop=(k == 7))
                    y_ = yf[h % 2]
                    S.copy("act", y_[:, :], bank.f(0, 512))
                    S.act(sq[h % 2][:, :], bank.f(0, 512), AF.Square)
                    bk2 = pb[4 + (h % 2)]
                    S.mm(bk2.f(0, 512), lhsT=onesb[:, :], rhs=sq[h % 2][:, :])
                    r_ = rtt[h % 2]
                    S.act(r_[:, :], bk2.f(0, 512), AF.Ln, bias=EPS, scale=1.0 / 128.0)
                    S.act(r_[:, :], r_[:, :], AF.Exp, scale=-0.5)
                    S.stt(qnT[:, h, t * 512:(t + 1) * 512], y_[:, :], qnw[:, 0:1], r_[:, :], ALU.mult, ALU.mult)
                for blk in range(4):
                    j = 4 * t + blk
                    cs = slice(blk * 128, (blk + 1) * 128)
                    bg = pb[6]
                    for k in range(8):
                        S.mm(bg.f(0, 12), lhsT=hT[:, k, cs], rhs=winb2[:, k, 1024:1036], start=(k == 0), stop=(k == 7))
                    S.act(gates[:, j, :], bg.f(0, 12), AF.Sigmoid)
                    bz = pb[7]
                    for k in range(8):
                        S.mm(bz.f(0, 512), lhsT=hT[:, k, cs], rhs=winb2[:, k, 0:512], start=(k == 0), stop=(k == 7))
                    z_ = zs[j % 2]
                    S.act(z_[:, :], bz.f(0, 512), AF.Silu)
                    o_ = og[j % 2]
                    S.dma(o_[:, :], self.o_own[j], g_og[j % 2])
                    for h in range(4):
                        S.act(t1[j % 2][:, h * 128:(h + 1) * 128], o_[:, h * 128:(h + 1) * 128], AF.Square,
                              accum=osq[:, h:h + 1])
                    S.act(osq[:, 4:8], osq[:, 0:4], AF.Sqrt, bias=EPS, scale=1.0 / 128.0)
                    S.recip(osq[:, 4:8], osq[:, 4:8])
                    rb = V(osq.t[:, 4:8].unsqueeze(2).broadcast_to([128, 4, 128]), [osq.b])
                    S.tt("dve", t1[j % 2][:, :].re("p (h d) -> p h d", h=4), o_[:, :].re("p (h d) -> p h d", h=4), rb,
                         ALU.mult)
                    S.tt("pool", t1[j % 2][:, :], t1[j % 2][:, :], gonw4[:, :], ALU.mult)
                    S.tt("dve", mg[j % 2][:, :], t1[j % 2][:, :], z_[:, :], ALU.mult)
                    bt = pb[0]
                    for c in range(4):
                        S.tr(bt.b(c * 128, (c + 1) * 128), mg[j % 2][:, c * 128:(c + 1) * 128], self.idb[:, :])
                    S.copy("act", mgT[j % 2][:, :, :].re("p c t -> p (c t)"), bt.b(0, 512))
                    S.dma(self.mix_d[:, 0:4, j * 128:(j + 1) * 128], mgT[j % 2][:, :, :], g_mg[j % 2])
            gh = S.group("p2halo", full=True)
            selAB = self.cload(es, "selAB", [128, 2], gh)
            xht = self.sb(es, "xht", [32, D])
            S.dma(xht[:, :], self.inp("xh", [32, D])[:, :], gh)
            ca = self.sb(es, "hca", [32, 512])
            cb = self.sb(es, "hcb", [32, 512])
            S.memset("pool", cb[0:2, :], 0.0)
            opv = self.oprev_d.rearrange("j i c -> (j i) c")
            S.dma(ca[:, :], opv[0:32, :], gh)
            S.dma(cb[2:32, :], opv[0:30, :], gh)
            hTh = self.sb(es, "hTh", [128, 8, 32], BF16)
            self.front32(F, xht, hTh, self.s1, 0)
            qnTh, gatesh = keep["qnTh"], keep["gatesh"]
            bq = pb[2]
            for h in range(4):
                for k in range(8):
                    S.mm(bq.f(h * 32, (h + 1) * 32), lhsT=winb2[:, k, 512 + h * 128:512 + (h + 1) * 128], rhs=hTh[:, k, :],
                         start=(k == 0), stop=(k == 7))
            S.copy("act", yf[0][:, 0:128], bq.f(0, 128))
            S.act(sq[0][:, 0:128], bq.f(0, 128), AF.Square)
            S.mm(pb[4].f(0, 128), lhsT=onesb[:, :], rhs=sq[0][:, 0:128])
            S.act(rtt[0][:, 0:128], pb[4].f(0, 128), AF.Ln, bias=EPS, scale=1.0 / 128.0)
            S.act(rtt[0][:, 0:128], rtt[0][:, 0:128], AF.Exp, scale=-0.5)
            S.stt(qnTh[:, :, :].re("p h q -> p (h q)"), yf[0][:, 0:128], qnw[:, 0:1], rtt[0][:, 0:128], ALU.mult, ALU.mult)
            for k in range(8):
                S.mm(pb[6].f(0, 12, 0, 32), lhsT=hTh[:, k, :], rhs=winb2[:, k, 1024:1036], start=(k == 0), stop=(k == 7))
            S.act(gatesh[:, :], pb[6].f(0, 12, 0, 32), AF.Sigmoid)
            for k in range(8):
                S.mm(pb[7].f(0, 512, 0, 32), lhsT=hTh[:, k, :], rhs=winb2[:, k, 0:512], start=(k == 0), stop=(k == 7))
            S.act(zs[0][0:32, :], pb[7].f(0, 512, 0, 32), AF.Silu)
            S.ts("dve", ca[:, :], ca[:, :], selAB[0:32, 0:1], None, ALU.mult)
            S.stt(ca[:, :], cb[:, :], selAB[0:32, 1:2], ca[:, :], ALU.mult, ALU.add)
            for h in range(4):
                S.act(t1[0][0:32, h * 128:(h + 1) * 128], ca[:, h * 128:(h + 1) * 128], AF.Square, accum=osq[0:32, h:h + 1])
            S.act(osq[0:32, 4:8], osq[0:32, 0:4], AF.Sqrt, bias=EPS, scale=1.0 / 128.0)
            S.recip(osq[0:32, 4:8], osq[0:32, 4:8])
            rb = V(osq.t[0:32, 4:8].unsqueeze(2).broadcast_to([32, 4, 128]), [osq.b])
            S.tt("dve", t1[0][0:32, :].re("p (h d) -> p h d", h=4), ca[:, :].re("p (h d) -> p h d", h=4), rb, ALU.mult)
            S.tt("pool", t1[0][0:32, :], t1[0][0:32, :], gonw4[0:32, :], ALU.mult)
            S.tt("dve", mg[0][0:32, :], t1[0][0:32, :], zs[0][0:32, :], ALU.mult)
            for c in range(4):
                S.tr(pb[0].b(c * 32, (c + 1) * 32), mg[0][0:32, c * 128:(c + 1) * 128], self.idb[0:32, 0:32])
            mghT = self.sb(es, "mghT", [128, 4, 32], BF16)
            S.copy("act", mghT[:, :, :].re("p c t -> p (c t)"), pb[0].b(0, 128))
            S.dma(self.mixh_d[:, 0:4, :], mghT[:, :, :], S.group("mghT"))
            S.flush()

    def phase3(self, keep):
        S = self.S
        pb = self.pb
        idb = self.idb
        qnT, gates = keep["qnT"], keep["gates"]
        SC = 128.0 ** -0.5
        with ExitStack() as es:
            gc_ = S.group("const3", full=True)
            kcmpT = self.sb(es, "kcmpT", [128, 512], BF16)
            vcx = self.sb(es, "vcx", [128, 4, 129], BF16)
            onesf = self.cload(es, "onesf3", [128, 128], gc_)
            onesb = self.sb(es, "onesb3", [128, 128], BF16)
            S.copy("dve", onesb[:, :], onesf[:, :])
            with ExitStack() as e2:
                g2 = S.group("const3b", full=True)
                kcw = self.cload(e2, "kcw", [128, 1], g2)
                pm511 = self.cload(e2, "pm511", [128, 1], g2)
                for which, src_d, w1n, w2n, posn in (("k", self.kcT_d, "cmp_k_w1", "cmp_k_w2", "cmp_k_posT"),
                                                    ("v", self.vcT_d, "cmp_v_w1", "cmp_v_w2", "cmp_v_posT")):
                    with ExitStack() as e3:
                        g3 = S.group("c3" + which, full=True)
                        xT = self.sb(e3, "cx" + which, [128, S_LEN], BF16)
                        S.dma(xT[:, 0:4096], src_d[:, 0:4096], g3)
                        S.dma(xT[:, 4096:8192], src_d[:, 4096:8192], g3)
                        w1d = self.inp(w1n, [128, 32, 128])
                        w1f = self.sb(e3, "w1f" + which, [128, 32, 128])
                        S.dma(w1f[:, 0:16, :], w1d[:, 0:16, :], g3)
                        S.dma(w1f[:, 16:32, :], w1d[:, 16:32, :], g3)
                        w1b = self.sb(e3, "w1b" + which, [128, 32, 128], BF16)
                        S.copy("dve", w1b[:, 0:16, :], w1f[:, 0:16, :])
                        S.copy("pool", w1b[:, 16:32, :], w1f[:, 16:32, :])
                        w2f = self.cload(e3, w2n, [128, 128], g3)
                        w2b = self.sb(e3, "w2b" + which, [128, 128], BF16)
                        S.copy("dve", w2b[:, :], w2f[:, :])
                        posf = self.cload(e3, posn, [128, 32], g3)
                        posb = self.sb(e3, "posb" + which, [128, 32], BF16)
                        S.copy("dve", posb[:, :], posf[:, :])
                        hid = self.sb(e3, "hid" + which, [128, 512], BF16)
                        bcol = self.sb(e3, "bcol" + which, [128, 1])
                        S.memset("pool", hid[:, :], 0.0)
                        bh, bb = pb[2], pb[3]
                        x3 = xT[:, :].re("p (n s) -> p n s", s=16)
                        for l in range(32):
                            S.mm(bh.f(0, 511), lhsT=w1b[:, l, :], rhs=x3[:, l // 16:l // 16 + 511, l % 16],
                                 start=(l == 0), stop=(l == 31))
                        for l in range(32):
                            S.mm(bb.f(0, 1), lhsT=w1b[:, l, :], rhs=posb[:, l:l + 1], start=(l == 0), stop=(l == 31))
                        S.copy("dve", bcol[:, :], bb.f(0, 1))
                        S.act(hid[:, 0:511], bh.f(0, 511), AF.Silu, bias=bcol[:, 0:1])
                        if which == "k":
                            bk = pb[4]
                            S.mm(bk.f(0, 512), lhsT=w2b[:, :], rhs=hid[:, :])
                            yk = self.sb(e3, "yk", [128, 512])
                            sqk = self.sb(e3, "sqk", [128, 512], BF16)
                            rk = self.sb(e3, "rk", [128, 512])
                            S.copy("act", yk[:, :], bk.f(0, 512))
                            S.act(sqk[:, :], bk.f(0, 512), AF.Square)
                            S.mm(pb[5].f(0, 512), lhsT=onesb[:, :], rhs=sqk[:, :])
                            S.act(rk[:, :], pb[5].f(0, 512), AF.Sqrt, bias=EPS, scale=1.0 / 128.0)
                            S.recip(rk[:, :], rk[:, :])
                            S.stt(kcmpT[:, :], yk[:, :], kcw[:, 0:1], rk[:, :], ALU.mult, ALU.mult)
                            S.memset("dve", kcmpT[:, 511:512], 0.0)
                        else:
                            bv = pb[4]
                            for nt in range(4):
                                S.mm(bv.f(nt * 128, (nt + 1) * 128), lhsT=hid[:, nt * 128:(nt + 1) * 128], rhs=w2b[:, :])
                            S.memset("pool", vcx[:, :, 128:129], 1.0)
                            S.copy("act", vcx[:, :, 0:128], bv.f(0, 512).re("p (n d) -> p n d", n=4))
                            S.ts("dve", vcx[:, 3, :], vcx[:, 3, :], pm511[:, 0:1], None, ALU.mult)
                        S.flush()
            if self.debug:
                S.dma(self.outp("d_kcmpT", [128, 512], BF16)[:, :], kcmpT[:, :], self.g_out)
                S.dma(self.outp("d_vcx", [128, 4, 129], BF16)[:, :, :], vcx[:, :, :], self.g_out)
            if self.stop == 31:
                S.flush()
                return
            kslT = self.sb(es, "kslT", [128, S_LEN], BF16)
            kwnT = self.sb(es, "kwnT", [128, S_LEN], BF16)
            vslx = self.sb(es, "vslx", [128, NB, 129], BF16)
            vwnx = self.sb(es, "vwnx", [128, NB, 129], BF16)
            for i in range(2):
                S.dma(kslT[:, i * 4096:(i + 1) * 4096], self.kslT_d[:, i * 4096:(i + 1) * 4096], gc_)
                S.dma(kwnT[:, i * 4096:(i + 1) * 4096], self.kwnT_d[:, i * 4096:(i + 1) * 4096], gc_)
                S.dma(vslx[:, i * 32:(i + 1) * 32, 0:128], self.vsl_d[:, i * 32:(i + 1) * 32, :], gc_)
                S.dma(vwnx[:, i * 32:(i + 1) * 32, 0:128], self.vwn_d[:, i * 32:(i + 1) * 32, :], gc_)
            S.memset("pool", vslx[:, :, 128:129], 1.0)
            S.memset("pool", vwnx[:, :, 128:129], 1.0)
            eall = self.sb(es, "eallb", [128, S_LEN], BF16)
            addm = self.cload(es, "addm", [128, 16, 128], gc_)
            ovb = self.sb(es, "ovb", [128, 4, 128], BF16)
            cmbb = self.sb(es, "cmbb", [128, 16, 2, 128], BF16)
            cbsb = self.sb(es, "cbsb", [128, 4, 4, 128], BF16)
            cbwb = self.sb(es, "cbwb", [128, 8, 4, 128], BF16)
            with ExitStack() as e2:
                g2 = S.group("const3c", full=True)
                ead = self.inp("eall", [128, S_LEN])
                est = [self.sb(e2, f"est{i}", [128, 2048]) for i in range(2)]
                g_e = [S.group(f"est{i}") for i in range(2)]
                for i in range(4):
                    S.dma(est[i % 2][:, :], ead[:, i * 2048:(i + 1) * 2048], g_e[i % 2])
                    S.copy("pool" if i % 2 else "dve", eall[:, i * 2048:(i + 1) * 2048], est[i % 2][:, :])
                ovf = self.cload(e2, "ovm", [128, 4, 128], g2)
                S.copy("dve", ovb[:, :, :], ovf[:, :, :])
                cmf = self.cload(e2, "cmb", [128, 16, 2, 128], g2)
                S.copy("pool", cmbb[:, :, :, :], cmf[:, :, :, :])
                cbsf = self.cload(e2, "cbs", [128, 4, 128], g2)
                cbwf = self.cload(e2, "cbw", [128, 8, 128], g2)
                for h in range(4):
                    S.copy("dve", cbsb[:, :, h, :], cbsf[:, :, :])
                    S.copy("pool", cbwb[:, :, h, :], cbwf[:, :, :])
                S.flush()
            Pc = [self.sb(es, f"Pc{i}", [128, 512], BF16) for i in range(4)]
            Pk = [self.sb(es, f"Pk{i}", [128, 512], BF16) for i in range(3)]
            cmrep = [self.sb(es, f"cmrep{i}", [128, 4, 128], BF16) for i in range(2)]
            ob = {nm: self.sb(es, "ob_" + nm, [128, 4, 129]) for nm in ("c", "s", "w")}
            rs = self.sb(es, "rs", [128, 12])
            cf = self.sb(es, "cf", [128, 12])
            imp = self.sb(es, "imp", [128, 128])
            imp2 = self.sb(es, "imp2", [128, 128])
            m8 = self.sb(es, "m8", [128, 16])
            selm = self.sb(es, "selm", [128, 128])
            biasT = self.sb(es, "biasT", [128, 4, 128], BF16)
            mixn = [self.sb(es, f"mixn{i}", [128, 4, 128], BF16) for i in range(2)]
            tmpn = self.sb(es, "tmpn", [128, 128])
            mnT = [self.sb(es, f"mnT{i}", [128, 4, 128], BF16) for i in range(2)]
            g_mn = [S.group(f"mnT{i}") for i in range(2)]
            bS = [pb[0], pb[1]]
            bO = [pb[2], pb[3]]
            bI, bT = pb[4], pb[5]
            si = [0]

            def nsa_part1(s_, Q, qv, cmp_plan, addv):
                NQ = 4 * Q

                ncp = len(cmp_plan)
                for i_, (nt, mk) in enumerate(cmp_plan):
                    bank = bS[i_ % 2]
                    S.mm(bank.f(0, NQ), lhsT=kcmpT[:, nt * 128:(nt + 1) * 128], rhs=qv, start=True, stop=(mk is None))
                    if mk is not None:
                        S.mm(bank.f(0, NQ), lhsT=idb[:, :], rhs=mk, start=False, stop=True)
                    S.act(Pc[i_][:, 0:NQ], bank.f(0, NQ), AF.Exp, scale=SC)
                for h in range(4):
                    for i_, (nt, mk) in enumerate(cmp_plan):
                        S.mm_(bOc[h // 2].f((h % 2) * 129, (h % 2) * 129 + 129, 0, Q), lhsT=Pc[i_][:, h * Q:(h + 1) * Q],
                              rhs=vcx[:, nt, :], start=(i_ == 0 and h % 2 == 0), stop=(i_ == ncp - 1), skip=True)
                for h in range(4):
                    for i_, (nt, mk) in enumerate(cmp_plan):
                        S.mm_(bI.f(h * 128, (h + 1) * 128, 0, Q), lhsT=Pc[i_][:, h * Q:(h + 1) * Q], rhs=ovb[:, nt, :],
                              start=(i_ == 0 and h == 0), stop=(i_ == ncp - 1), skip=True)
                for hh in range(2):
                    S.copy("act", obc[s_][0:Q, 2 * hh:2 * hh + 2, :], bOc[hh].f(0, 258, 0, Q).re("p (h c) -> p h c", h=2))
                S.ts("dve", rsc[s_][0:Q, 0:4], obc[s_][0:Q, :, 128], 1e-30, None, ALU.max)
                S.recip(rsc[s_][0:Q, 0:4], rsc[s_][0:Q, 0:4])
                S.ts("dve", imp[0:Q, :], bI.f(0, 128, 0, Q), rsc[s_][0:Q, 0:1], None, ALU.mult)
                for h in range(1, 4):
                    S.stt(imp[0:Q, :], bI.f(h * 128, (h + 1) * 128, 0, Q), rsc[s_][0:Q, h:h + 1], imp[0:Q, :], ALU.mult, ALU.add)
                S.tt("pool", imp[0:Q, :], imp[0:Q, :], addv, ALU.add)
                self.max8(m8[0:Q, 0:8], imp[0:Q, :])
                self.match_replace(imp2[0:Q, :], m8[0:Q, 0:8], imp[0:Q, :], -3.0e38)
                self.max8(m8[0:Q, 8:16], imp2[0:Q, :])
                S.ts("dve", selm[0:Q, :], imp[0:Q, :], m8[0:Q, 15:16], None, ALU.is_ge)
                S.mm(bT.f(0, Q), lhsT=selm[0:Q, :], rhs=self.idf[0:Q, 0:Q])
                S.ts("dve", bT2s[s_][:, 0:NQ].re("p (h q) -> p h q", h=4),
                     V(bT.t[:, 0:Q].unsqueeze(1).broadcast_to([128, 4, Q]), bT.q[0:1]), BIG, -BIG, ALU.mult, ALU.add)

            def nsa_part2(s_, Q, qv, slc_plan, win_plan, gatev, store):
                NQ = 4 * Q

                def scores(kT_tile, extra):
                    bank = bS[si[0] % 2]
                    out_ = Pk[si[0] % 3]
                    si[0] += 1
                    S.mm(bank.f(0, NQ), lhsT=kT_tile, rhs=qv, start=True, stop=(len(extra) == 0))
                    for i_, (l_, r_) in enumerate(extra):
                        S.mm(bank.f(0, NQ), lhsT=l_, rhs=r_, start=False, stop=(i_ == len(extra) - 1))
                    S.act(out_[:, 0:NQ], bank.f(0, NQ), AF.Exp, scale=SC)
                    return out_

                def pv(P_, vx, first, last):
                    for h in range(4):
                        S.mm_(bO[h // 2].f((h % 2) * 129, (h % 2) * 129 + 129, 0, Q), lhsT=P_[:, h * Q:(h + 1) * Q],
                              rhs=vx, start=(first and h % 2 == 0), stop=last, skip=True)

                def evac_o(dst):
                    for hh in range(2):
                        S.copy("act", dst[0:Q, 2 * hh:2 * hh + 2, :], bO[hh].f(0, 258, 0, Q).re("p (h c) -> p h c", h=2))

                brhs = bT2s[s_][:, 0:NQ]
                S.copy("dve", rs[0:Q, 0:4], rsc[s_][0:Q, 0:4])
                prev = None
                for i_, (kt, extra) in enumerate(slc_plan):
                    ex = [(eall[:, kt * 128:(kt + 1) * 128], brhs)] + extra
                    P_ = scores(kslT[:, kt * 128:(kt + 1) * 128], ex)
                    if prev is not None:
                        pv(prev[0], vslx[:, prev[1], :], prev[2] == 0, False)
                    prev = (P_, kt, i_)
                pv(prev[0], vslx[:, prev[1], :], prev[2] == 0, True)
                evac_o(ob["s"])
                prev = None
                for i_, (kt, mk) in enumerate(win_plan):
                    P_ = scores(kwnT[:, kt * 128:(kt + 1) * 128], [(idb[:, :], mk)])
                    if prev is not None:
                        pv(prev[0], vwnx[:, prev[1], :], prev[2] == 0, False)
                    prev = (P_, kt, i_)
                pv(prev[0], vwnx[:, prev[1], :], prev[2] == 0, True)
                evac_o(ob["w"])
                S.ts("dve", rs[0:Q, 4:8], ob["s"][0:Q, :, 128], 1e-30, None, ALU.max)
                S.ts("dve", rs[0:Q, 8:12], ob["w"][0:Q, :, 128], 1e-30, None, ALU.max)
                S.recip(rs[0:Q, 4:12], rs[0:Q, 4:12])
                g3 = gatev.re("p (h g) -> p g h", g=3)
                S.tt("dve", cf[0:Q, :].re("p (g h) -> p g h", g=3), rs[0:Q, :].re("p (g h) -> p g h", g=3), g3, ALU.mult)
                mx = mixn[si[0] % 2]
                for h in range(4):
                    S.ts("pool", tmpn[0:Q, :], obc[s_][0:Q, h, 0:128], cf[0:Q, h:h + 1], 1.0, ALU.mult, ALU.mult)
                    S.stt(tmpn[0:Q, :], ob["s"][0:Q, h, 0:128], cf[0:Q, 4 + h:5 + h], tmpn[0:Q, :], ALU.mult, ALU.add)
                    S.stt(mx[0:Q, h, :], ob["w"][0:Q, h, 0:128], cf[0:Q, 8 + h:9 + h], tmpn[0:Q, :], ALU.mult, ALU.add)
                for h in range(4):
                    S.tr(bT.b(h * Q, (h + 1) * Q), mx[0:Q, h, :], idb[0:Q, 0:Q])
                store(bT.b(0, 4 * Q))

            bT2s = [self.sb(es, f"bT2_{i}", [128, 512], BF16) for i in range(2)]
            obc = [self.sb(es, f"obc{i}", [128, 4, 129]) for i in range(2)]
            rsc = [self.sb(es, f"rsc{i}", [128, 4]) for i in range(2)]
            bOc = [pb[6], pb[7]]
            plans = []
            for j in range(16):
                qv = qnT[:, :, j * 128:(j + 1) * 128]
                nt_hi = min(3, (32 * j + 30) // 128)
                lo = max(0, 32 * j - 1) // 128
                cmp_plan = []
                for nt in range(nt_hi + 1):
                    mk = None
                    if nt >= lo:
                        mk = nt - lo
                    cmp_plan.append((nt, mk))
                slc_plan = []
                for kt in range(4 * j + 4):
                    ex = []
                    if kt >= 4 * j:
                        ex.append((idb[:, :], cbsb[:, kt - 4 * j, :, :].re("p h q -> p (h q)")))
                    slc_plan.append((kt, ex))
                win_plan = [(4 * j - 4 + e, cbwb[:, e, :, :].re("p h q -> p (h q)")) for e in range(8) if 4 * j - 4 + e >= 0]

                def store(src, j=j):
                    S.copy("act", mnT[j % 2][:, :, :].re("p c t -> p (c t)"), src)
                    S.dma(self.mix_d[:, 4:8, j * 128:(j + 1) * 128], mnT[j % 2][:, :, :], g_mn[j % 2])

                plans.append((qv, cmp_plan, slc_plan, win_plan, store))
            def emit1(j):
                qv, cmp_plan, _, _, _ = plans[j]
                cp = []
                for nt, slot in cmp_plan:
                    mk = None
                    if slot is not None:
                        S.copy("pool", cmrep[slot][:, :, :],
                               V(cmbb.t[:, j, slot, :].unsqueeze(1).broadcast_to([128, 4, 128]), [cmbb.b]))
                        mk = cmrep[slot][:, :, :].re("p h q -> p (h q)")
                    cp.append((nt, mk))
                nsa_part1(j % 2, 128, qv, cp, addm[:, j, :])

            def emit2(j):
                qv, _, slc_plan, win_plan, store = plans[j]
                nsa_part2(j % 2, 128, qv, slc_plan, win_plan, gates[:, j, :], store)

            self._p3_emit(emit1, emit2)
            self.nsa_halo(es, (nsa_part1, nsa_part2), keep, idb)
            S.flush()

    @staticmethod
    def _p3_emit(emit1, emit2, n=16):
        emit1(0)
        for j in range(n):
            if j + 1 < n:
                emit1(j + 1)
            emit2(j)

    def nsa_halo(self, es, nsa_parts, keep, idb):
        S = self.S
        with ExitStack() as e2:
            g2 = S.group("const3h", full=True)
            haddm = self.cload(e2, "haddm", [32, 128], g2)
            hcmb = self.sb(e2, "hcmb", [128, 4, 128], BF16)
            hsmb = self.sb(e2, "hsmb", [128, 64, 128], BF16)
            hwmb = self.sb(e2, "hwmb", [128, 64, 128], BF16)
            hcf = self.cload(e2, "hcm", [128, 4, 128], g2)
            S.copy("dve", hcmb[:, :, :], hcf[:, :, :])
            st = [self.sb(e2, f"hst{i}", [128, 8, 128]) for i in range(2)]
            g_s = [S.group(f"hst{i}") for i in range(2)]
            n = 0
            for nm, dst in (("hsm", hsmb), ("hwm", hwmb)):
                src = self.inp(nm, [128, 64, 128])
                for i in range(8):
                    S.dma(st[n % 2][:, :, :], src[:, i * 8:(i + 1) * 8, :], g_s[n % 2])
                    S.copy("pool" if n % 2 else "dve", dst[:, i * 8:(i + 1) * 8, :], st[n % 2][:, :, :])
                    n += 1
            mnTh = self.sb(e2, "mnTh", [128, 4, 32], BF16)
            g_m = S.group("mnTh")
            cmp_plan = [(nt, hcmb[:, nt, :]) for nt in range(4)]
            slc_plan = [(kt, [(idb[:, :], hsmb[:, kt, :])]) for kt in range(NB)]
            win_plan = [(kt, hwmb[:, kt, :]) for kt in range(NB)]

            def store(src):
                S.copy("act", mnTh[:, :, :].re("p c t -> p (c t)"), src)
                S.dma(self.mixh_d[:, 4:8, :], mnTh[:, :, :], g_m)

            part1, part2 = nsa_parts
            part1(0, 32, keep["qnTh"][:, :, :], cmp_plan, haddm[:, :])
            part2(0, 32, keep["qnTh"][:, :, :], slc_plan, win_plan, keep["gatesh"][:, :], store)
            S.flush()

    def max8(self, out, in_):
        o_, i_ = out.ap, in_.ap
        self.S.op("dve", lambda e: e.max(out=o_, in_=i_), r=self.S._bs(in_), w=self.S._bs(out))

    def match_replace(self, out, rep, vals, imm):
        o_, r_, v_ = out.ap, rep.ap, vals.ap
        self.S.op("dve", lambda e: e.match_replace(out=o_, in_to_replace=r_, in_values=v_, imm_value=imm),
                  r=self.S._bs(rep, vals), w=self.S._bs(out))

    def phase45(self):
        S = self.S
        pb = self.pb
        idb = self.idb
        w_out = self.inp("w_out", [D, D])
        w_up = self.inp("w_up", [D, 2 * DFF])
        w_dn = self.inp("w_dn", [DFF, D])
        wup_d = self.scratch("wup_d", [128, 44, 8, 128], BF16)
        with ExitStack() as e1:
            full = self.sb(e1, "wupfull", [128, 44, 8, 128], BF16)
            stg = [self.sb(e1, f"wupst{i}", [128, 2 * DFF]) for i in range(2)]
            g_s = [S.group(f"wupst{i}") for i in range(2)]
            g_f = S.group("wupfull")
            for k in range(8):
                st = stg[k % 2]
                for i in range(2):
                    S.dma(st[:, i * DFF:(i + 1) * DFF], w_up[k * 128:(k + 1) * 128, i * DFF:(i + 1) * DFF], g_s[k % 2])
                s3 = st[:, :].re("p (c n) -> p c n", n=128)
                S.copy("dve", full[:, 0:16, k, :], s3[:, 0:16, :])
                S.copy("pool", full[:, 16:30, k, :], s3[:, 16:30, :])
                S.copy("act", full[:, 30:44, k, :], s3[:, 30:44, :])
            for i in range(4):
                S.dma(wup_d[:, i * 11:(i + 1) * 11, :, :], full[:, i * 11:(i + 1) * 11, :, :], g_f)
            S.flush()
        with ExitStack() as es:
            gc_ = S.group("const4", full=True)
            fcw = self.cload(es, "fcw", [128, 44 * 3], gc_)
            fcb = self.cload(es, "fcb", [128, 44], gc_)
            wdnb = self.sb(es, "wdnb", [128, 22, D], BF16)
            woutb = self.sb(es, "woutb", [128, 8, D], BF16)
            g1row = self.sb(es, "g1row", [128, D])
            g2row = self.sb(es, "g2row", [128, D])
            with ExitStack() as e2:
                stg = [self.sb(e2, f"wst4_{i}", [128, 2, D]) for i in range(2)]
                g_s = [S.group(f"wst4_{i}") for i in range(2)]
                n = 0
                for (src, dst, nch) in ((w_dn, wdnb, 22), (w_out, woutb, 8)):
                    sv = src.rearrange("(c p) d -> p c d", p=128)
                    for c0 in range(0, nch, 2):
                        st = stg[n % 2]
                        S.dma(st[:, :, :], sv[:, c0:c0 + 2, :], g_s[n % 2])
                        S.copy(("dve", "pool", "act")[n % 3], dst[:, c0:c0 + 2, :], st[:, :, :])
                        n += 1
                gb = self.sb(e2, "gbc", [128, 128])
                for gi, (col0, dst) in enumerate(((16, g1row), (40, g2row))):
                    for cc in range(8):
                        S.copy("dve", gb[:, :], V(self.modT.t[:, col0 + cc:col0 + cc + 1].broadcast_to([128, 128]),
                                                  [self.modT.b]))
                        bank = pb[cc % 2]
                        S.mm(bank.f(0, 128), lhsT=gb[:, :], rhs=self.idf[:, :])
                        S.copy("act", dst[:, cc * 128:(cc + 1) * 128], bank.f(0, 128))
                S.flush()
            F = dict(ssq=self.sb(es, "f4_ssq", [128, 4]), rt=self.sb(es, "f4_rt", [128, 4]),
                     rstd=self.sb(es, "f4_rstd", [128, 4]), junk=self.sb(es, "f4_junk", [128, 1024], BF16),
                     xs=self.sb(es, "f4_xs", [128, 2, 1024], BF16))
            x1 = [self.sb(es, f"x1_{i}", [128, 4, D]) for i in range(2)]
            g_x = [S.group(f"x1_{i}") for i in range(2)]
            mixt = self.sb(es, "mixt", [128, 8, 512], BF16)
            g_m = S.group("mixt")
            h2T = self.sb(es, "h2T", [128, 8, 512], BF16)
            gT = self.sb(es, "gT", [128, 22, 512], BF16)
            wch = [self.sb(es, f"wch{i}", [128, 2, 8, 128], BF16) for i in range(3)]
            g_wc = [S.group(f"wch{i}") for i in range(3)]
            upa = [self.sb(es, f"upa{i}", [128, 4, 130]) for i in range(2)]
            upb = [self.sb(es, f"upb{i}", [128, 4, 130]) for i in range(2)]
            for u_ in upa + upb:
                S.memset("pool", u_[:, :, :], 0.0)
            aa = [self.sb(es, f"aa{i}", [128, 4, 128]) for i in range(2)]
            ab_ = [self.sb(es, f"ab_{i}", [128, 4, 128]) for i in range(2)]
            tmp = [self.sb(es, f"tmp4_{i}", [128, 512]) for i in range(2)]
            xrows = self.xo.rearrange("(t b p) d -> t p b d", b=4, p=128)
            orows = self.out.rearrange("(t b p) d -> t p b d", b=4, p=128)
            wi = 0
            gh = S.group("p4halo", full=True)
            hexr = self.cload(es, "hexr", [128, 32], gh)
            x1h = self.sb(es, "x1h", [32, D])
            mixh = self.sb(es, "mixh", [128, 8, 32], BF16)
            S.dma(x1h[:, :], self.din["xh"][:, :], gh)
            S.dma(mixh[:, :, :], self.mixh_d[:, :, :], gh)
            h2Th = self.sb(es, "h2Th", [128, 8, 32], BF16)
            for half in range(2):
                bank = pb[2 + half]
                hs = slice(half * 512, (half + 1) * 512)
                for c in range(8):
                    S.mm(bank.f(0, 512, 0, 32), lhsT=mixh[:, c, :], rhs=woutb[:, c, hs], start=(c == 0), stop=(c == 7))
                S.tt("dve", tmp[half][0:32, :], bank.f(0, 512, 0, 32), g1row[0:32, hs], ALU.mult)
                S.tt("pool", x1h[:, hs], x1h[:, hs], tmp[half][0:32, :], ALU.add)
            self.front32(F, x1h, h2Th, self.s2, 24)

            def conv(u_, c, dst):
                S.ts("pool", dst[:, :, :], u_[:, :, 2:130], fcw[:, c * 3 + 2:c * 3 + 3], fcb[:, c:c + 1], ALU.mult, ALU.add)
                S.stt(dst[:, :, :], u_[:, :, 1:129], fcw[:, c * 3 + 1:c * 3 + 2], dst[:, :, :], ALU.mult, ALU.add)
                S.stt(dst[:, :, :], u_[:, :, 0:128], fcw[:, c * 3:c * 3 + 1], dst[:, :, :], ALU.mult, ALU.add)

            for t in range(4):
                xx = x1[t % 2]
                S.dma(xx[:, :, :], xrows[t], g_x[t % 2])
                S.dma(mixt[:, :, :], self.mix_d[:, :, t * 512:(t + 1) * 512], g_m)
                for blk in range(4):
                    cs = slice(blk * 128, (blk + 1) * 128)
                    for half in range(2):
                        bank = pb[2 + half]
                        hs = slice(half * 512, (half + 1) * 512)
                        for c in range(8):
                            S.mm(bank.f(0, 512), lhsT=mixt[:, c, cs], rhs=woutb[:, c, hs], start=(c == 0), stop=(c == 7))
                        tm = tmp[half]
                        S.tt("dve", tm[:, :], bank.f(0, 512), g1row[:, hs], ALU.mult)
                        S.tt("pool", xx[:, blk, hs], xx[:, blk, hs], tm[:, :], ALU.add)
                if self.debug:
                    S.dma(self.dx1[t], xx[:, :, :], self.g_out)
                for hf in range(2):
                    self.front(F, xx, 2, h2T, hf * 256, self.s2, 24, b0=2 * hf)
                for c in range(22):
                    w_ = wch[wi % 3]
                    S.dma(w_[:, 0, :, :], wup_d[:, c, :, :], g_wc[wi % 3])
                    S.dma(w_[:, 1, :, :], wup_d[:, 22 + c, :, :], g_wc[wi % 3])
                    wi += 1
                    ua, ub = upa[c % 2], upb[c % 2]
                    for i_, (u_, bank) in enumerate(((ua, pb[4]), (ub, pb[5]))):
                        for k in range(8):
                            S.mm(bank.f(0, 512), lhsT=w_[:, i_, k, :], rhs=h2T[:, k, :], start=(k == 0), stop=(k == 7))
                        S.copy("act", u_[:, :, 2:130], bank.f(0, 512).re("p (b t) -> p b t", b=4))
                        bh_ = pb[6 + i_]
                        for k in range(8):
                            S.mm(bh_.f(0, 8), lhsT=w_[:, i_, k, :], rhs=h2Th[:, k, 8 * t:8 * t + 8], start=(k == 0),
                                 stop=(k == 7))
                        S.tt("dve", u_[:, :, 0:2], bh_.f(0, 8).re("p (b i) -> p b i", i=2),
                             hexr[:, 8 * t:8 * t + 8].re("p (b i) -> p b i", i=2), ALU.mult)
                    conv(ua, c, aa[c % 2])
                    conv(ub, 22 + c, ab_[c % 2])
                    S.act(aa[c % 2][:, :, :], aa[c % 2][:, :, :], AF.Silu)
                    S.tt("dve", gT[:, c, :].re("p (b t) -> p b t", b=4), aa[c % 2][:, :, :], ab_[c % 2][:, :, :], ALU.mult)
                for blk in range(4):
                    cs = slice(blk * 128, (blk + 1) * 128)
                    for half in range(2):
                        bank = pb[6 + half]
                        hs = slice(half * 512, (half + 1) * 512)
                        for c in range(22):
                            S.mm(bank.f(0, 512), lhsT=gT[:, c, cs], rhs=wdnb[:, c, hs], start=(c == 0), stop=(c == 21))
                        tm = tmp[half]
                        S.tt("dve", tm[:, :], bank.f(0, 512), g2row[:, hs], ALU.mult)
                        S.tt("pool", xx[:, blk, hs], xx[:, blk, hs], tm[:, :], ALU.add)
                S.dma(orows[t], xx[:, :, :], g_x[t % 2])
            S.flush()


def _colL(v, n):
    return np.ascontiguousarray(np.asarray(v, np.float32).reshape(n, 128).T)


def _rep(v, n=128):
    v = np.asarray(v, np.float32).reshape(1, -1)
    return np.ascontiguousarray(np.repeat(v, n, axis=0))


def _consts():
    p = np.arange(128)
    same = (p[:, None] // 64) == (p[None, :] // 64)
    c = {}
    c["identf"] = np.eye(128, dtype=np.float32)
    c["tri2"] = (same & (p[:, None] <= p[None, :])).astype(np.float32)
    c["blk2"] = same.astype(np.float32)
    c["onesf"] = np.ones((128, 128), np.float32)
    c["cind"] = np.stack([(p < 64), (p >= 64)], axis=1).astype(np.float32)
    ma = np.where(same & (p[None, :] < p[:, None]), 0.0, BIG).astype(np.float32)
    mb = np.where(same & (p[None, :] >= p[:, None]), 0.0, -BIG).astype(np.float32)
    c["ma4"] = np.ascontiguousarray(np.tile(ma, (1, 4)))
    c["mb4"] = np.ascontiguousarray(np.tile(mb, (1, 4)))
    return c


def _host_inputs(inputs):
    x = np.asarray(inputs["x"], np.float32)
    cst = _consts()
    g = lambda k: np.asarray(inputs[k][0], np.float32)
    gcw = g("gdn_conv_w")
    gcwT = np.ascontiguousarray(gcw.reshape(4, 12, 128).transpose(2, 1, 0).reshape(128, 48))
    shared = {
        "ada_w": np.ascontiguousarray(g("ada_w")),
        "ada_bT": _colL(g("ada_b"), 48),
        "n1w": _colL(g("norm1_w"), 8),
        "n2w": _colL(g("norm2_w"), 8),
        "w_in": np.ascontiguousarray(g("w_in")),
        "gcw": gcwT,
        "dtb": _rep(g("gdn_dt_bias")),
        "alog": _rep(g("gdn_A_log")),
        "kslw": _colL(g("nsa_k_norm_slc"), 1),
        "kwnw": _colL(g("nsa_k_norm_win"), 1),
        "qnw": _colL(g("nsa_q_norm_w"), 1),
        "gonw4": _rep(np.tile(g("gdn_out_norm_w"), 4)),
        "onesf2": np.ones((128, 128), np.float32),
    }
    for nm in ("k", "v"):
        shared[f"cmp_{nm}_w1"] = np.ascontiguousarray(g(f"cmp_{nm}_w1").reshape(32, 128, 128).transpose(1, 0, 2))
        shared[f"cmp_{nm}_w2"] = np.ascontiguousarray(g(f"cmp_{nm}_w2"))
        shared[f"cmp_{nm}_posT"] = np.ascontiguousarray(g(f"cmp_{nm}_pos").T)
    shared["kcw"] = _colL(g("nsa_k_norm_cmp"), 1)
    shared["onesf3"] = np.ones((128, 128), np.float32)
    pm = np.ones((128, 1), np.float32)
    pm[127, 0] = 0.0
    shared["pm511"] = pm
    keys = np.arange(S_LEN)
    shared["eall"] = (keys[None, :] // 64 == np.arange(128)[:, None]).astype(np.float32)
    n = np.arange(512)
    js = np.arange(128)
    ov = np.minimum(16 * n[:, None] + 32, 64 * js[None, :] + 64) - np.maximum(16 * n[:, None], 64 * js[None, :])
    ov = np.clip(ov, 0, None).astype(np.float32) / 32.0
    ov[511] = 0.0
    shared["ovm"] = np.ascontiguousarray(ov.reshape(4, 128, 128).transpose(1, 0, 2))
    fw = g("ffn_conv_w")
    shared["fcw"] = np.ascontiguousarray(fw.reshape(3, 44, 128).transpose(2, 1, 0).reshape(128, 132))
    shared["fcb"] = _colL(g("ffn_conv_b"), 44)
    shared["w_out"] = np.ascontiguousarray(g("w_out"))
    shared["w_up"] = np.ascontiguousarray(g("ffn_w_up"))
    shared["w_dn"] = np.ascontiguousarray(g("ffn_w_down"))
    shared.update(cst)
    maps = []
    for core in range(8):
        b, r = core // 4, core % 4
        xo = np.concatenate([x[b, 128 * (4 * j + r):128 * (4 * j + r) + 128] for j in range(16)], axis=0)
        m = dict(shared)
        m["xb"] = np.ascontiguousarray(x[b])
        m["xo"] = np.ascontiguousarray(xo)
        m["cT"] = _colL(inputs["c"][b], 8)
        sel = np.zeros((128, 4), np.float32)
        sel[:, r] = 1.0
        m["sel"] = sel
        p = np.arange(128)
        q = np.arange(128)
        addm = np.zeros((128, 16, 128), np.float32)
        cmb = np.zeros((128, 16, 2, 128), np.float32)
        for j in range(16):
            qi = 4 * j + r
            tq = 128 * qi + q
            cur = tq // 64
            jj = np.arange(128)[None, :]
            valid = jj <= cur[:, None]
            forced = (jj == 0) | (jj == cur[:, None]) | (jj == cur[:, None] - 1)
            addm[:, j, :] = np.where(valid, np.where(forced, 1.0e4, 0.0), -1.0e30)
            lo = max(0, 32 * j - 1) // 128
            for slot in range(2):
                nn = 128 * (lo + slot) + p
                ok = (16 * nn[:, None] + 31) <= tq[None, :]
                cmb[:, j, slot, :] = np.where(ok, 0.0, -BIG)
        m["addm"] = addm
        m["cmb"] = cmb
        cbs = np.zeros((128, 4, 128), np.float32)
        for d in range(4):
            ok = (128 * (d - r) + p[:, None]) <= q[None, :]
            cbs[:, d, :] = np.where(ok, 0.0, -BIG)
        m["cbs"] = cbs
        cbw = np.zeros((128, 8, 128), np.float32)
        for e in range(8):
            rel = 128 * (e - 4 - r) + p[:, None]
            ok = (rel <= q[None, :]) & (rel > q[None, :] - 512)
            cbw[:, e, :] = np.where(ok, 0.0, -BIG)
        m["cbw"] = cbw
        hs_ = np.zeros((128, 4), np.float32)
        hs_[:, (r - 1) % 4] = 1.0
        m["hsel"] = hs_
        sab = np.zeros((128, 2), np.float32)
        sab[:, 0] = 1.0 if r >= 1 else 0.0
        sab[:, 1] = 1.0 if r == 0 else 0.0
        m["selAB"] = sab
        tq = np.array([128 * (4 * j + r) - 2 + i for j in range(16) for i in range(2)])
        ex = tq >= 0
        xh = np.zeros((32, D), np.float32)
        xh[ex] = x[b, tq[ex]]
        m["xh"] = xh
        m["hexr"] = _rep(ex.astype(np.float32))
        cur = tq // 64
        jj = np.arange(128)[None, :]
        valid = (jj <= cur[:, None]) & ex[:, None]
        forced = (jj == 0) | (jj == cur[:, None]) | (jj == cur[:, None] - 1)
        m["haddm"] = np.where(valid, np.where(forced, 1.0e4, 0.0), -1.0e30).astype(np.float32)
        nn = np.arange(512)
        okc = ((16 * nn[:, None] + 31) <= tq[None, :]) & ex[None, :]
        hcm = np.where(okc, 0.0, -BIG).astype(np.float32).reshape(4, 128, 1, 32)
        m["hcm"] = np.ascontiguousarray(np.broadcast_to(hcm, (4, 128, 4, 32)).transpose(1, 0, 2, 3).reshape(128, 4, 128))
        pos = np.arange(S_LEN)
        oks = (pos[:, None] <= tq[None, :]) & ex[None, :]
        okw = oks & (pos[:, None] > tq[None, :] - 512)
        for nm, ok in (("hsm", oks), ("hwm", okw)):
            a = np.where(ok, 0.0, -BIG).astype(np.float32).reshape(64, 128, 1, 32)
            m[nm] = np.ascontiguousarray(np.broadcast_to(a, (64, 128, 4, 32)).transpose(1, 0, 2, 3).reshape(128, 64, 128))
        maps.append(m)
    return maps


def run(inputs, debug=False, upto=9, ntiles=NT):
    bld = Builder(debug, ntiles)
    nc = bld.build(upto)
    maps = _host_inputs(inputs)
    maps = [{k: v for k, v in m.items() if k in bld.din} for m in maps]
    missing = [k for k in bld.din if k not in maps[0]]
    assert not missing, missing
    res = run_bass_kernel_spmd(nc, maps, core_ids=list(range(8)))
    return res.results


def kernel(**inputs):
    results = run(inputs)
    outp = np.zeros((2, S_LEN, D), np.float32)
    for core in range(8):
        b, r = core // 4, core % 4
        o = results[core]["out"]
        for j in range(16):
            qi = 4 * j + r
            outp[b, 128 * qi:128 * qi + 128] = o[128 * j:128 * j + 128]
    return outp
```

```python
import numpy as np
from contextlib import ExitStack
import concourse.bass as bass
import concourse.mybir as mybir
from concourse.bass_utils import run_bass_kernel_spmd

F32 = mybir.dt.float32
BF16 = mybir.dt.bfloat16
AF = mybir.ActivationFunctionType
ALU = mybir.AluOpType

D = 1024
S_LEN = 8192
NT = 16
NB = 64
N_IN = 3348
DFF = 2816
EPS = 1e-6
O_GQ, O_GK, O_GV, O_GZ, O_GA, O_GB, O_NQ, O_KC, O_VC, O_KSL, O_VSL, O_KWN, O_VWN, O_NG = (
    0, 512, 1024, 1536, 2048, 2052, 2056, 2568, 2696, 2824, 2952, 3080, 3208, 3336)
BIG = 30000.0


class Buf:
    __slots__ = ("name", "w", "rs", "const", "excl")

    def __init__(self, name, const=False, excl=False):
        self.name = name
        self.w = None
        self.rs = []
        self.const = const
        self.excl = excl


class V:
    __slots__ = ("ap", "bs")

    def __init__(self, ap, bs):
        self.ap = ap
        self.bs = bs if isinstance(bs, (list, tuple)) else [bs]

    def bitcast(self, dt):
        return V(self.ap.bitcast(dt), self.bs)

    def re(self, pat, **kw):
        return V(self.ap.rearrange(pat, **kw), self.bs)

    def bc(self, shape):
        return V(self.ap.broadcast_to(shape), self.bs)

    def __getitem__(self, k):
        return V(self.ap[k], self.bs)


class Tl:
    def __init__(self, t, b):
        self.t = t
        self.b = b

    def __getitem__(self, k):
        return V(self.t[k], self.b)


class Op:
    __slots__ = ("eng", "fn", "deps", "dmaw", "signal", "sigval", "grp")


class DGroup:
    def __init__(self, name, sem, full=False):
        self.name = name
        self.sem = sem
        self.count = 0
        self.full = full


ENGS = ("pe", "act", "dve", "pool", "sp")


class Sched:
    def __init__(self, nc, es):
        self.nc = nc
        self.es = es
        self.eng = {"pe": nc.tensor, "act": nc.scalar, "dve": nc.vector, "pool": nc.gpsimd, "sp": nc.sync}
        self.sem = {e: es.enter_context(nc.semaphore("s_" + e)) for e in ENGS}
        self.cnt = {e: 0 for e in ENGS}
        self.seen = {e: {} for e in ENGS}
        self.ops = []
        self.bufs = []
        self.groups = []
        self.nins = 0

    def buf(self, name, const=False, excl=False):
        b = Buf(name, const, excl)
        self.bufs.append(b)
        return b

    def group(self, name, full=False):
        g = DGroup(name, self.es.enter_context(self.nc.semaphore("g_" + name)), full)
        self.groups.append(g)
        return g

    def op(self, eng, fn, r=(), w=(), grp=None):
        o = Op()
        o.eng = eng
        o.fn = fn
        o.signal = False
        o.sigval = None
        o.grp = grp
        deps = []
        for b in r:
            if b.w is not None:
                deps.append(b.w)
            if b.excl:
                deps.extend(x for x in b.rs if x.eng != eng)
        for b in w:
            if b.w is not None:
                deps.append(b.w)
            deps.extend(b.rs)
        seen = set()
        dd = []
        for d in deps:
            if id(d) in seen or d is o:
                continue
            seen.add(id(d))
            if d.eng == "pe" and eng == "pe":
                continue
            if grp is not None and grp.full and d.grp is grp:
                continue
            dd.append(d)
        o.deps = dd
        o.dmaw = {}
        for d in dd:
            if d.grp is None:
                d.signal = True
            else:
                o.dmaw[d.grp.name] = d.grp.count
        if grp is not None:
            grp.count += 1
        for b in w:
            b.w = o
            b.rs = []
        for b in r:
            if not b.const and b.w is not o:
                b.rs.append(o)
        self.ops.append(o)
        return o

    def flush(self, barrier=True):
        if barrier:
            last = {}
            for o in self.ops:
                last[o.eng] = o
            for e, o in last.items():
                if o.grp is None:
                    o.signal = True
        for o in self.ops:
            e = self.eng[o.eng]
            for d in o.deps:
                if d.grp is not None:
                    key = "g_" + d.grp.name
                    val = 16 * (d.grp.count if d.grp.full else o.dmaw[d.grp.name])
                    sem = d.grp.sem
                else:
                    key = d.eng
                    val = d.sigval
                    sem = self.sem[d.eng]
                    assert val is not None, (o.eng, d.eng)
                if self.seen[o.eng].get(key, 0) >= val:
                    continue
                e.wait_ge(sem, val)
                self.seen[o.eng][key] = val
            ins = o.fn(e)
            self.nins += 1
            if o.grp is not None:
                ins.then_inc(o.grp.sem, 16)
            elif o.signal:
                self.cnt[o.eng] += 1
                o.sigval = self.cnt[o.eng]
                ins.then_inc(self.sem[o.eng], 1)
        self.ops = []
        if barrier:
            for en in ENGS:
                e = self.eng[en]
                for e2 in ENGS:
                    if e2 == en or e2 == "sp":
                        continue
                    if self.cnt[e2] > self.seen[en].get(e2, 0):
                        e.wait_ge(self.sem[e2], self.cnt[e2])
                        self.seen[en][e2] = self.cnt[e2]
                for g in self.groups:
                    key = "g_" + g.name
                    if 16 * g.count > self.seen[en].get(key, 0):
                        e.wait_ge(g.sem, 16 * g.count)
                        self.seen[en][key] = 16 * g.count
            for b in self.bufs:
                b.w = None
                b.rs = []

    @staticmethod
    def _bs(*vs):
        out = []
        for v in vs:
            if isinstance(v, V):
                for b in v.bs:
                    if b not in out:
                        out.append(b)
        return out

    @staticmethod
    def _a(v):
        return v.ap if isinstance(v, V) else v

    def mm(self, out, lhsT, rhs, start=True, stop=True):
        o_, l_, r_ = out.ap, lhsT.ap, rhs.ap
        self.op("pe", lambda e: e.matmul(o_, lhsT=l_, rhs=r_, start=start, stop=stop),
                r=self._bs(lhsT, rhs), w=self._bs(out))

    def mm_(self, out, lhsT, rhs, start=True, stop=True, skip=False):
        o_, l_, r_ = out.ap, lhsT.ap, rhs.ap
        self.op("pe", lambda e: e.matmul(o_, lhsT=l_, rhs=r_, start=start, stop=stop, skip_group_check=skip),
                r=self._bs(lhsT, rhs), w=self._bs(out))

    def tr(self, out, in_, ident):
        o_, i_, d_ = out.ap, in_.ap, ident.ap
        self.op("pe", lambda e: e.transpose(o_, i_, d_), r=self._bs(in_, ident), w=self._bs(out))

    def act(self, out, in_, func, bias=None, scale=None, accum=None, eng="act"):
        kw = {}
        if bias is not None:
            kw["bias"] = self._a(bias)
        if scale is not None:
            kw["scale"] = self._a(scale)
        if accum is not None:
            kw["accum_out"] = accum.ap
        o_, i_ = out.ap, in_.ap
        self.op("act", lambda e: e.activation(out=o_, in_=i_, func=func, **kw),
                r=self._bs(in_, bias, scale), w=self._bs(out, accum))

    def tt(self, eng, out, in0, in1, op):
        o_, a_, b_ = out.ap, in0.ap, in1.ap
        self.op(eng, lambda e: e.tensor_tensor(out=o_, in0=a_, in1=b_, op=op),
                r=self._bs(in0, in1), w=self._bs(out))

    def ts(self, eng, out, in0, s1, s2=None, op0=ALU.mult, op1=None):
        o_, a_ = out.ap, in0.ap
        s1_, s2_ = self._a(s1), self._a(s2)
        kw = {}
        if op1 is not None:
            kw["op1"] = op1
        self.op(eng, lambda e: e.tensor_scalar(out=o_, in0=a_, scalar1=s1_, scalar2=s2_, op0=op0, **kw),
                r=self._bs(in0, s1, s2), w=self._bs(out))

    def stt(self, out, in0, scalar, in1, op0, op1):
        o_, a_, b_ = out.ap, in0.ap, in1.ap
        s_ = self._a(scalar)
        self.op("dve", lambda e: e.scalar_tensor_tensor(out=o_, in0=a_, scalar=s_, in1=b_, op0=op0, op1=op1),
                r=self._bs(in0, scalar, in1), w=self._bs(out))

    def copy(self, eng, out, in_):
        o_, i_ = out.ap, in_.ap
        if eng == "act":
            self.op("act", lambda e: e.copy(out=o_, in_=i_), r=self._bs(in_), w=self._bs(out))
        else:
            self.op(eng, lambda e: e.tensor_copy(out=o_, in_=i_), r=self._bs(in_), w=self._bs(out))

    def recip(self, out, in_):
        o_, i_ = out.ap, in_.ap
        self.op("dve", lambda e: e.reciprocal(out=o_, in_=i_), r=self._bs(in_), w=self._bs(out))

    def memset(self, eng, out, val):
        o_ = out.ap
        self.op(eng, lambda e: e.memset(o_, val), r=[], w=self._bs(out))

    def dma(self, out, in_, grp, eng="sp"):
        o_, i_ = self._a(out), self._a(in_)
        self.op(eng, lambda e: e.dma_start(out=o_, in_=i_), r=self._bs(in_), w=self._bs(out), grp=grp)


class Bank:
    def __init__(self, t, q):
        self.t = t
        self.q = q

    def f(self, c0, c1, p0=0, p1=128):
        return V(self.t[p0:p1, c0:c1], self.q[0:1])

    def b(self, c0, c1, p0=0, p1=128):
        return V(self.t[p0:p1, :].bitcast(BF16)[:, c0:c1], self.q[0:1])


W1_SEGS = ((0, 1536, 0), (2048, 2056, 1536), (2568, 3336, 1544))
W1_N = 2312
C_AB, C_KC, C_VC, C_KSL, C_VSL, C_KWN, C_VWN = 1536, 1544, 1672, 1800, 1928, 2056, 2184


class Builder:
    def __init__(self, debug=False, ntiles=NT):
        self.debug = debug
        self.ntiles = ntiles
        import os
        self.stop = int(os.environ.get('K_STOP', '0'))
        self.var = int(os.environ.get('K_VAR', '0'))
        self.halo = int(os.environ.get('K_HALO', '0'))
        self.nc = bass.Bass("TRN2", target_bir_lowering=False)
        self.din = {}
        self.dout = {}

    def inp(self, name, shape, dt=F32):
        self.din[name] = self.nc.dram_tensor(name, list(shape), dt, kind="ExternalInput").ap()
        return self.din[name]

    def outp(self, name, shape, dt=F32):
        self.dout[name] = self.nc.dram_tensor(name, list(shape), dt, kind="ExternalOutput").ap()
        return self.dout[name]

    def scratch(self, name, shape, dt=F32):
        if self.debug:
            return self.outp(name, shape, dt)
        return self.nc.dram_tensor(name, list(shape), dt, kind="Internal").ap()

    def sb(self, es, name, shape, dt=F32, const=False):
        t = es.enter_context(self.nc.sbuf_tensor(name, list(shape), dt))
        return Tl(t, self.S.buf(name, const))

    def cload(self, es, name, shape, grp):
        d = self.inp(name, shape)
        t = self.sb(es, "c_" + name, shape, F32, const=True)
        idx = tuple(slice(None) for _ in shape)
        self.S.dma(t[idx], d[idx], grp)
        return t

    def build(self, upto=9):
        nc = self.nc
        I = self.inp
        self.xb = I("xb", [S_LEN, D])
        self.xo = I("xo", [2048, D])
        cT = I("cT", [128, 8])
        ada_w = I("ada_w", [D, 6 * D])
        self.w_in = I("w_in", [D, N_IN])
        self.out = self.outp("out", [2048, D])
        self.o_own = self.scratch("o_own", [16, 128, 512])
        self.kslT_d = self.scratch("kslT_d", [128, S_LEN], BF16)
        self.kwnT_d = self.scratch("kwnT_d", [128, S_LEN], BF16)
        self.kcT_d = self.scratch("kcT_d", [128, S_LEN], BF16)
        self.vcT_d = self.scratch("vcT_d", [128, S_LEN], BF16)
        self.vsl_d = self.scratch("vsl_d", [128, NB, 128], BF16)
        self.vwn_d = self.scratch("vwn_d", [128, NB, 128], BF16)
        self.mix_d = self.scratch("mix_d", [128, 8, 2048], BF16)
        self.oprev_d = self.scratch("oprev_d", [16, 2, 512])
        self.mixh_d = self.scratch("mixh_d", [128, 8, 32], BF16)
        self.w_up = I("w_up", [D, 2 * DFF])
        self.wup_d = self.scratch("wup_d", [128, 44, 8, 128], BF16)

        with ExitStack() as top:
            S = self.S = Sched(nc, top)
            self.g_const = g_const = S.group("const", full=True)
            self.g_out = S.group("outw")
            self.idf = idf = self.cload(top, "identf", [128, 128], g_const)
            self.idb = idb = self.sb(top, "idb", [128, 128], BF16)
            S.copy("dve", idb[:, :], idf[:, :])
            self.modT = modT = self.sb(top, "modT", [128, 48])
            self.s1 = s1 = self.sb(top, "s1", [128, 8])
            self.s2 = s2 = self.sb(top, "s2", [128, 8])
            n1 = self.cload(top, "n1w", [128, 8], g_const)
            n2 = self.cload(top, "n2w", [128, 8], g_const)
            self.pb = []
            for i in range(8):
                t = top.enter_context(nc.psum_tensor(f"pb{i}", [128, 512], F32))
                self.pb.append(Bank(t, [S.buf(f"pb{i}", excl=True)]))
            pb = self.pb
            self.bankA = [pb[2], pb[3]]
            self.bankB = [pb[4], pb[5]]
            self.bankC = [pb[0], pb[1]]

            with ExitStack() as es:
                ct = self.sb(es, "ct", [128, 8])
                sc = self.sb(es, "sc", [128, 8])
                abT = self.cload(es, "ada_bT", [128, 48], g_const)
                S.dma(ct[:, :], cT[:, :], g_const)
                S.act(sc[:, :], ct[:, :], AF.Silu)
                aw = [self.sb(es, f"aw{i}", [128, 6 * D]) for i in range(2)]
                g_aw = [S.group(f"aw{i}") for i in range(2)]
                for k in range(8):
                    sl = k % 2
                    for hh in range(4):
                        S.dma(aw[sl][:, hh * 1536:(hh + 1) * 1536],
                              ada_w[k * 128:(k + 1) * 128, hh * 1536:(hh + 1) * 1536], g_aw[sl])
                    pm = pb[k % 2]
                    for cc in range(48):
                        S.mm(pm.f(cc, cc + 1), lhsT=aw[sl][:, cc * 128:(cc + 1) * 128], rhs=sc[:, k:k + 1])
                    S.tt("dve", modT[:, :], pm.f(0, 48), (abT if k == 0 else modT)[:, :], ALU.add)
                S.stt(s1[:, :], modT[:, 8:16], 1.0, n1[:, :], ALU.add, ALU.mult)
                S.stt(s2[:, :], modT[:, 32:40], 1.0, n2[:, :], ALU.add, ALU.mult)
                S.flush()

            if upto >= 1:
                self.phase1()
            keep = dict(qnT=self.sb(top, "qnT", [128, 4, 2048], BF16), gates=self.sb(top, "gates", [128, 16, 12]),
                        qnTh=self.sb(top, "qnTh", [128, 4, 32], BF16), gatesh=self.sb(top, "gatesh", [32, 12]))
            if upto >= 2:
                self.phase2(keep)
                if self.debug:
                    S.dma(self.outp("d_qnT", [128, 4, 2048], BF16)[:, :, :], keep["qnT"][:, :, :], self.g_out)
                    S.dma(self.outp("d_gates", [128, 16, 12])[:, :, :], keep["gates"][:, :, :], self.g_out)
            if upto >= 3:
                S.flush()
                self.phase3(keep)
            if upto >= 4:
                S.flush()
                if self.debug:
                    self.dx1 = self.outp("d_x1", [4, 128, 4, D])
                self.phase45()
            S.flush()
        return nc

    def front(self, F, xt, nb, hT, c0, scol, sh_c0, b0=0):
        S = self.S
        pb = self.pb
        ssq, rt, rstd, junk, xs = F["ssq"], F["rt"], F["rstd"], F["junk"], F["xs"]
        for b2 in range(nb):
            S.act(junk[:, :], xt[:, b0 + b2, :], AF.Square, accum=ssq[:, b2:b2 + 1])
        S.act(rt[:, 0:nb], ssq[:, 0:nb], AF.Ln, bias=EPS, scale=1.0 / D)
        S.act(rstd[:, 0:nb], rt[:, 0:nb], AF.Exp, scale=-0.5)
        for b2 in range(nb):
            S.ts("pool", xs[:, b2, :], xt[:, b0 + b2, :], rstd[:, b2:b2 + 1], 1.0, ALU.mult, ALU.mult)
        w = nb * 128
        for half in range(2):
            bank = pb[half]
            for k in range(half * 4, half * 4 + 4):
                off = (k % 4) * 256
                for b2 in range(nb):
                    S.tr(bank.b(off + b2 * 128, off + (b2 + 1) * 128), xs[:, b2, k * 128:(k + 1) * 128],
                         self.idb[:, :])
            for k in range(half * 4, half * 4 + 4):
                off = (k % 4) * 256
                S.ts("dve", hT[:, k, c0:c0 + w], bank.b(off, off + w), scol[:, k:k + 1],
                     self.modT[:, sh_c0 + k:sh_c0 + k + 1], ALU.mult, ALU.add)

    def front32(self, F, xt, hT, scol, sh_c0):
        S = self.S
        bank = self.pb[0]
        ssq, rt, rstd, junk, xs = F["ssq"], F["rt"], F["rstd"], F["junk"], F["xs"]
        S.act(junk[0:32, :], xt[:, :], AF.Square, accum=ssq[0:32, 0:1])
        S.act(rt[0:32, 0:1], ssq[0:32, 0:1], AF.Ln, bias=EPS, scale=1.0 / D)
        S.act(rstd[0:32, 0:1], rt[0:32, 0:1], AF.Exp, scale=-0.5)
        S.ts("pool", xs[0:32, 0, :], xt[:, :], rstd[0:32, 0:1], 1.0, ALU.mult, ALU.mult)
        for k in range(8):
            S.tr(bank.b(k * 32, (k + 1) * 32), xs[0:32, 0, k * 128:(k + 1) * 128], self.idb[0:32, 0:32])
        for k in range(8):
            S.ts("dve", hT[:, k, :], bank.b(k * 32, (k + 1) * 32), scol[:, k:k + 1],
                 self.modT[:, sh_c0 + k:sh_c0 + k + 1], ALU.mult, ALU.add)

    def phase1(self):
        S = self.S
        nc = self.nc
        pb = self.pb
        gc_ = S.group("const1", full=True)
        idb, idf = self.idb, self.idf
        with ExitStack() as es:
            tri2 = self.cload(es, "tri2", [128, 128], gc_)
            blk2 = self.cload(es, "blk2", [128, 128], gc_)
            onesf = self.cload(es, "onesf", [128, 128], gc_)
            cind = self.cload(es, "cind", [128, 2], gc_)
            ma4 = self.cload(es, "ma4", [128, 512], gc_)
            mb4 = self.cload(es, "mb4", [128, 512], gc_)
            sel = self.cload(es, "sel", [128, 4], gc_)
            hsel = self.cload(es, "hsel", [128, 4], gc_)
            cw = self.cload(es, "gcw", [128, 48], gc_)
            dtb = self.cload(es, "dtb", [128, 4], gc_)
            alog = self.cload(es, "alog", [128, 4], gc_)
            kslw = self.cload(es, "kslw", [128, 1], gc_)
            kwnw = self.cload(es, "kwnw", [128, 1], gc_)
            negones = self.sb(es, "negones", [128, 128])
            S.ts("dve", negones[:, :], onesf[:, :], -1.0, None, ALU.mult)
            onesb = self.sb(es, "onesb", [128, 128], BF16)
            S.copy("dve", onesb[:, :], onesf[:, :])
            i4b = self.sb(es, "i4b", [128, 512], BF16)
            for h in range(4):
                S.copy("dve", i4b[:, h * 128:(h + 1) * 128], idf[:, :])
            negA = self.sb(es, "negA", [128, 4])
            S.act(negA[:, :], alog[:, :], AF.Exp)
            S.ts("dve", negA[:, :], negA[:, :], -1.0, None, ALU.mult)

            winb = self.sb(es, "winb", [128, 8, W1_N], BF16)
            with ExitStack() as es2:
                wst = [self.sb(es2, f"wst{i}", [128, W1_N]) for i in range(2)]
                g_w = [S.group(f"wst{i}") for i in range(2)]
                for k in range(8):
                    sl = k % 2
                    for (a, b_, o) in W1_SEGS:
                        S.dma(wst[sl][:, o:o + (b_ - a)], self.w_in[k * 128:(k + 1) * 128, a:b_], g_w[sl])
                    S.copy("dve", winb[:, k, 0:1024], wst[sl][:, 0:1024])
                    S.copy("pool", winb[:, k, 1024:W1_N], wst[sl][:, 1024:W1_N])
                S.flush()

            F = dict(ssq=self.sb(es, "f_ssq", [128, 4]), rt=self.sb(es, "f_rt", [128, 4]),
                     rstd=self.sb(es, "f_rstd", [128, 4]), junk=self.sb(es, "f_junk", [128, 1024], BF16),
                     xs=self.sb(es, "f_xs", [128, 2, 1024], BF16))
            xt = [self.sb(es, f"xt{i}", [128, 2, 1024]) for i in range(2)]
            g_x = [S.group(f"xt{i}") for i in range(2)]
            hT = self.sb(es, "hT", [128, 8, 512], BF16)
            pre = [self.sb(es, f"pre{i}", [128, 515]) for i in range(3)]
            hist = self.sb(es, "hist", [128, 12, 3])
            S.memset("pool", hist[:, :, :], 0.0)
            acc = [self.sb(es, f"cacc{i}", [128, 512]) for i in range(3)]
            yfs = [self.sb(es, f"yfs{i}", [128, 512], BF16 if i < 8 else F32) for i in range(10)]
            sq = [self.sb(es, f"sq{i}", [128, 512], BF16) for i in range(3)]
            rtt = [self.sb(es, f"rtt{i}", [128, 512]) for i in range(3)]
            qT = self.sb(es, "qT", [128, 4, 512], BF16)
            kT = self.sb(es, "kT", [128, 4, 512], BF16)
            vT = self.sb(es, "vT", [128, 4, 512], BF16)
            st_f = [[self.sb(es, f"stf{c}_{i}", [128, 512], BF16) for i in range(2)] for c in range(4)]
            st_v = [[self.sb(es, f"stv{c}_{i}", [128, 4, 128], BF16) for i in range(2)] for c in range(2)]
            g_sf = [[S.group(f"sf{c}_{i}") for i in range(2)] for c in range(4)]
            g_sv = [[S.group(f"sv{c}_{i}") for i in range(2)] for c in range(2)]
            ab = self.sb(es, "ab", [128, 4, 8])
            self.sm4 = sm4 = self.sb(es, "sm4", [128, 4, 64])
            S32 = self.sb(es, "S32", [128, 4, 128])
            Sb = [self.sb(es, f"Sb{i}", [128, 4, 128], BF16) for i in range(2)]
            self.cidx = 0
            S.memset("pool", S32[:, :, :], 0.0)
            S.memset("pool", Sb[0][:, :, :], 0.0)
            oacc = [self.sb(es, f"oacc{i}", [128, 512]) for i in range(2)]
            g_o = [S.group(f"oacc{i}") for i in range(2)]
            oph = [self.sb(es, f"oph{i}", [128, 512]) for i in range(2)]
            g_oh = [S.group(f"oph{i}") for i in range(2)]
            G = []
            for s_ in range(2):
                g = {}
                for nm, shp, dt in (("TG", [128, 4, 128], F32), ("Xs", [128, 512], F32),
                                    ("XA", [128, 512], F32), ("XB", [128, 512], F32), ("EG", [128, 4, 128], F32),
                                    ("Nb", [128, 4, 128], BF16), ("aT", [128, 4, 128], BF16),
                                    ("qdT", [128, 4, 128], BF16), ("kw", [128, 4, 128], BF16),
                                    ("kd", [128, 4, 128], BF16), ("vb", [128, 4, 128], BF16),
                                    ("NTb", [128, 4, 128], BF16), ("RTb", [128, 4, 128], BF16),
                                    ("P0", [128, 4, 128], BF16), ("P1", [128, 4, 128], BF16),
                                    ("Q0", [128, 4, 128], BF16), ("Q1", [128, 4, 128], BF16),
                                    ("u", [128, 4, 128], F32), ("wTb", [128, 4, 128], BF16),
                                    ("vn", [128, 4, 128], BF16)):
                    g[nm] = self.sb(es, f"g{s_}_{nm}", shp, dt)
                G.append(g)

            xrows = self.xb.rearrange("(n b p) d -> n p b d", b=2, p=128)

            def load_x(n):
                S.dma(xt[n % 2][:, :, :], xrows[n], g_x[n % 2])

            load_x(0)
            for ti in range(self.ntiles):
                for hf in range(2):
                    n = 2 * ti + hf
                    if n + 1 < 2 * NT:
                        load_x(n + 1)
                    self.front(F, xt[n % 2], 2, hT, hf * 256, self.s1, 0)
                if self.stop == 1:
                    continue
                nsa_ch = ((C_KC, self.kcT_d, None), (C_VC, self.vcT_d, None), (C_KSL, self.kslT_d, kslw),
                          (C_KWN, self.kwnT_d, kwnw))

                def st1(c, pos):
                    bank = pb[2 + (pos % 2)]
                    coff = c * 128 if c < 12 else nsa_ch[c - 12][0]
                    for k in range(8):
                        S.mm(bank.f(0, 512), lhsT=winb[:, k, coff:coff + 128], rhs=hT[:, k, :],
                             start=(k == 0), stop=(k == 7))
                    if c < 12:
                        p_ = pre[pos % 3]
                        S.copy("pool", p_[:, 0:3], hist[:, c, :])
                        S.copy("act", p_[:, 3:515], bank.f(0, 512))
                        S.copy("pool", hist[:, c, :], p_[:, 512:515])
                        a_ = acc[pos % 3]
                        S.ts("pool", a_[:, :], p_[:, 0:512], cw[:, c * 4:c * 4 + 1], 1.0, ALU.mult, ALU.mult)
                        for j in range(1, 4):
                            S.stt(a_[:, :], p_[:, j:j + 512], cw[:, c * 4 + j:c * 4 + j + 1], a_[:, :], ALU.mult, ALU.add)
                    else:
                        ci = c - 12
                        wcol = nsa_ch[ci][2]
                        if wcol is None:
                            S.copy("act", st_f[ci][ti % 2][:, :], bank.f(0, 512))
                            S.dma(nsa_ch[ci][1][:, ti * 512:(ti + 1) * 512], st_f[ci][ti % 2][:, :], g_sf[ci][ti % 2])
                        else:
                            S.copy("act", yfs[c - 6][:, :], bank.f(0, 512))

                def st2(c, pos):
                    h = c % 4
                    if c >= 12:
                        return
                    if c >= 8:
                        S.act(vT[:, h, :], acc[pos % 3][:, :], AF.Silu)
                        return
                    S.act(yfs[c][:, :], acc[pos % 3][:, :], AF.Silu)

                def st3(c, i_):
                    h = c % 4
                    bk2 = pb[4 + (i_ % 2)]
                    yi = c if c < 8 else c - 6
                    S.act(sq[i_ % 3][:, :], yfs[yi][:, :], AF.Square)
                    S.mm(bk2.f(0, 512), lhsT=onesb[:, :], rhs=sq[i_ % 3][:, :])
                    r_ = rtt[i_ % 3]
                    if c < 4:
                        S.act(r_[:, :], bk2.f(0, 512), AF.Ln, bias=EPS * 128.0, scale=128.0)
                    elif c < 8:
                        S.act(r_[:, :], bk2.f(0, 512), AF.Ln, bias=EPS, scale=1.0)
                    else:
                        S.act(r_[:, :], bk2.f(0, 512), AF.Ln, bias=EPS, scale=1.0 / 128.0)
                    S.act(r_[:, :], r_[:, :], AF.Exp, scale=-0.5)
                    if c < 8:
                        S.tt("pool", (qT if c < 4 else kT)[:, h, :], yfs[yi][:, :], r_[:, :], ALU.mult)
                    else:
                        ci = c - 12
                        st = st_f[ci][ti % 2]
                        S.stt(st[:, :], yfs[yi][:, :], nsa_ch[ci][2][:, 0:1], r_[:, :], ALU.mult, ALU.mult)
                        S.dma(nsa_ch[ci][1][:, ti * 512:(ti + 1) * 512], st[:, :], g_sf[ci][ti % 2])

                order = [12, 13, 14, 15] + list(range(12))
                for i in range(len(order) + 1):
                    if i < len(order):
                        st1(order[i], i)
                    if 0 <= i - 1 < len(order):
                        st2(order[i - 1], i - 1)
                if self.stop == 3:
                    continue
                for blk in range(4):
                    bank = pb[6]
                    for vi, coff in enumerate((C_VSL, C_VWN)):
                        for k in range(8):
                            if self.var == 4:
                                break
                            S.mm(bank.f(vi * 128, (vi + 1) * 128), lhsT=hT[:, k, blk * 128:(blk + 1) * 128],
                                 rhs=winb[:, k, coff:coff + 128], start=(k == 0), stop=(k == 7))
                    for k in range(8):
                        if self.var == 1:
                            break
                        S.mm(bank.f(256, 264), lhsT=hT[:, k, blk * 128:(blk + 1) * 128],
                             rhs=winb[:, k, C_AB:C_AB + 8], start=(k == 0), stop=(k == 7))
                    if self.var != 3:
                        S.copy("act", st_v[0][ti % 2][:, blk, :], bank.f(0, 128))
                        S.copy("act", st_v[1][ti % 2][:, blk, :], bank.f(128, 256))
                    if self.var != 5:
                        S.copy("dve", ab[:, blk, :], bank.f(256, 264))
                if self.var != 2:
                    S.dma(self.vsl_d[:, ti * 4:(ti + 1) * 4, :], st_v[0][ti % 2][:, :, :], g_sv[0][ti % 2])
                    S.dma(self.vwn_d[:, ti * 4:(ti + 1) * 4, :], st_v[1][ti % 2][:, :, :], g_sv[1][ti % 2])

                for i_, c in enumerate([14, 15, 0, 1, 2, 3, 4, 5, 6, 7]):
                    st3(c, i_)
                if self.stop == 4:
                    continue
                for pair in range(2):
                    blks = (2 * pair, 2 * pair + 1)
                    if pair == 0:
                        self.gdn_small(sm4, ab, tri2, blk2, onesf, cind, dtb, negA)
                    for s_, blk in enumerate(blks):
                        self.gdn_local_1(G[s_], s_, blk, sm4, tri2, negones, onesf, ma4, mb4, qT, kT, vT)
                    if self.stop == 5:
                        continue
                    self.gdn_solve(G, blks, i4b)
                    if self.stop == 6:
                        continue
                    for s_, blk in enumerate(blks):
                        self.gdn_uw(G[s_], s_)
                    if self.stop == 7:
                        continue
                    for s_, blk in enumerate(blks):
                        oa = oacc[ti % 2]
                        self.gdn_recur(G[s_], S32, Sb, sel, blk, oa, hsel, oph[ti % 2])
                S.dma(self.o_own[ti], oacc[ti % 2][:, :], g_o[ti % 2])
                S.dma(self.oprev_d[ti], oph[ti % 2][126:128, :], g_oh[ti % 2])
            S.flush()

    def gdn_small(self, sm4, ab, tri2, blk2, onesf, cind, dtb, negA):
        S = self.S
        bS = self.pb[6]
        x_, t_, gg, be, nb_ = (sm4[:, :, 0:4], sm4[:, :, 4:8], sm4[:, :, 8:12], sm4[:, :, 12:16], sm4[:, :, 16:20])
        gci, gcv, gl, egc, ekd, bw, glS = (sm4[:, :, 20:28], sm4[:, :, 28:32], sm4[:, :, 32:36], sm4[:, :, 36:40],
                                           sm4[:, :, 40:44], sm4[:, :, 44:48], sm4[:, :, 48:56])
        bc = lambda t: V(t.t[:, :].unsqueeze(1).broadcast_to([128, 4, 4]), [t.b])
        S.tt("dve", x_, ab[:, :, 0:4], bc(dtb), ALU.add)
        S.stt(t_, x_, -1.0, x_, ALU.mult, ALU.max)
        S.act(t_, t_, AF.Exp, scale=-1.0)
        S.act(t_, t_, AF.Ln, bias=1.0)
        S.stt(t_, x_, 0.0, t_, ALU.max, ALU.add)
        S.tt("dve", gg, t_, bc(negA), ALU.mult)
        S.act(be, ab[:, :, 4:8], AF.Exp, scale=-1.0)
        S.ts("dve", be, be, 1.0, None, ALU.add)
        S.recip(be, be)
        S.ts("dve", nb_, be, -1.0, None, ALU.mult)
        for i in range(2):
            for blk in range(4):
                S.ts("dve", sm4[:, blk, 20:28].re("p (h i) -> p h i", i=2)[:, :, i], sm4[:, blk, 8:12], cind[:, i:i + 1],
                     None, ALU.mult)
        for blk in range(4):
            S.mm(bS.f(384 + blk * 4, 388 + blk * 4), lhsT=tri2[:, :], rhs=sm4[:, blk, 8:12])
            S.mm(bS.f(400 + blk * 4, 404 + blk * 4), lhsT=blk2[:, :], rhs=sm4[:, blk, 8:12])
            S.mm(bS.f(416 + blk * 8, 424 + blk * 8), lhsT=onesf[:, :], rhs=sm4[:, blk, 20:28])
        S.copy("dve", gcv, bS.f(384, 400).re("p (b h) -> p b h", b=4))
        S.copy("dve", gl, bS.f(400, 416).re("p (b h) -> p b h", b=4))
        S.act(glS, bS.f(416, 448).re("p (b c) -> p b c", b=4), AF.Exp)
        S.act(egc, gcv, AF.Exp)
        S.tt("dve", ekd, gl, gcv, ALU.subtract)
        S.act(ekd, ekd, AF.Exp)
        S.tt("dve", bw, be, egc, ALU.mult)

    def gdn_local_1(self, g, s_, blk, sm4, tri2, negones, onesf, ma4, mb4, qT, kT, vT):
        S = self.S
        pb = self.pb
        cs = slice(blk * 128, (blk + 1) * 128)
        sm = V(sm4.t[:, blk, :], [sm4.b])
        gg, be, nb_, gcv, ekd, bw = (sm[:, 8:12], sm[:, 12:16], sm[:, 16:20], sm[:, 28:32], sm[:, 40:44], sm[:, 44:48])
        bK = self.bankA[s_]
        bQ = self.bankB[s_]
        bX = self.bankC[s_]
        bT = pb[7]
        for h in range(4):
            S.mm(bK.f(h * 128, (h + 1) * 128), lhsT=kT[:, h, cs], rhs=kT[:, h, cs])
        for h in range(4):
            S.mm(bQ.f(h * 128, (h + 1) * 128), lhsT=kT[:, h, cs], rhs=qT[:, h, cs])
        for h in range(4):
            S.tr(bT.b(h * 128, (h + 1) * 128), kT[:, h, cs], self.idb[:, :])
        for h in range(4):
            S.tr(bT.b(512 + h * 128, 512 + (h + 1) * 128), vT[:, h, cs], self.idb[:, :])
        for h in range(4):
            S.ts("pool", g["TG"][:, h, :], tri2[:, :], gg[:, h:h + 1], 1.0, ALU.mult, ALU.mult)
        for h in range(4):
            S.mm(bX.f(h * 128, (h + 1) * 128), lhsT=onesf[:, :], rhs=g["TG"][:, h, :], start=True, stop=False)
            S.mm(bX.f(h * 128, (h + 1) * 128), lhsT=g["TG"][:, h, :], rhs=negones[:, :], start=False, stop=True)
        S.copy("act", g["Xs"][:, :], bX.f(0, 512))
        for h in range(4):
            S.act(g["EG"][:, h, :], bX.f(h * 128, (h + 1) * 128), AF.Exp, bias=gcv[:, h:h + 1])
        S.tt("pool", g["XA"][:, :], g["Xs"][:, :], ma4[:, :], ALU.add)
        S.tt("pool", g["XB"][:, :], g["Xs"][:, :], mb4[:, :], ALU.add)
        S.act(g["XA"][:, :], g["XA"][:, :], AF.Exp, scale=-1.0)
        S.act(g["XB"][:, :], g["XB"][:, :], AF.Exp)
        for h in range(4):
            S.stt(g["Nb"][:, h, :], bK.f(h * 128, (h + 1) * 128), nb_[:, h:h + 1],
                  g["XA"][:, h * 128:(h + 1) * 128], ALU.mult, ALU.mult)
        S.tt("dve", g["aT"][:, :, :].re("p h c -> p (h c)"), bQ.f(0, 512), g["XB"][:, :], ALU.mult)
        S.tt("pool", g["qdT"][:, :, :], qT[:, :, cs], g["EG"][:, :, :], ALU.mult)
        kt3 = bT.b(0, 512).re("p (h d) -> p h d", h=4)
        vt3 = bT.b(512, 1024).re("p (h d) -> p h d", h=4)
        S.tt("dve", g["kw"][:, :, :], kt3, V(bw.ap.unsqueeze(2).broadcast_to([128, 4, 128]), bw.bs), ALU.mult)
        S.tt("dve", g["kd"][:, :, :], kt3, V(ekd.ap.unsqueeze(2).broadcast_to([128, 4, 128]), ekd.bs), ALU.mult)
        S.tt("dve", g["vb"][:, :, :], vt3, V(be.ap.unsqueeze(2).broadcast_to([128, 4, 128]), be.bs), ALU.mult)

    def gdn_solve(self, G, blks, i4b):
        S = self.S
        idb = self.idb
        for s_ in range(len(blks)):
            g = G[s_]
            bA, bC = self.bankA[s_], self.bankC[s_]
            for h in range(4):
                S.mm(bA.f(h * 128, (h + 1) * 128), lhsT=g["Nb"][:, h, :], rhs=idb[:, :])
            S.copy("act", g["NTb"][:, :, :].re("p h c -> p (h c)"), bA.f(0, 512))
            S.mm(bC.f(0, 512), lhsT=idb[:, :], rhs=i4b[:, :], start=True, stop=False)
            for h in range(4):
                S.mm(bC.f(h * 128, (h + 1) * 128), lhsT=g["Nb"][:, h, :], rhs=idb[:, :], start=False, stop=(h == 3))
            S.copy("act", g["RTb"][:, :, :].re("p h c -> p (h c)"), bC.f(0, 512))
        P = [G[s_]["Nb"] for s_ in range(len(blks))]
        Q = [G[s_]["NTb"] for s_ in range(len(blks))]
        for k in range(1, 6):
            for s_ in range(len(blks)):
                g = G[s_]
                bA, bB, bC = self.bankA[s_], self.bankB[s_], self.bankC[s_]
                Pn = g["P%d" % (k % 2)]
                Qn = g["Q%d" % (k % 2)]
                for h in range(4):
                    S.mm(bA.f(h * 128, (h + 1) * 128), lhsT=Q[s_][:, h, :], rhs=P[s_][:, h, :])
                if k < 5:
                    for h in range(4):
                        S.mm(bB.f(h * 128, (h + 1) * 128), lhsT=P[s_][:, h, :], rhs=Q[s_][:, h, :])
                S.copy("dve", Pn[:, :, :].re("p h c -> p (h c)"), bA.f(0, 512))
                if k < 5:
                    S.copy("act", Qn[:, :, :].re("p h c -> p (h c)"), bB.f(0, 512))
                S.mm(bC.f(0, 512), lhsT=idb[:, :], rhs=g["RTb"][:, :, :].re("p h c -> p (h c)"), start=True, stop=False)
                for h in range(4):
                    S.mm(bC.f(h * 128, (h + 1) * 128), lhsT=Pn[:, h, :], rhs=g["RTb"][:, h, :],
                         start=False, stop=(h == 3))
                S.copy("act", g["RTb"][:, :, :].re("p h c -> p (h c)"), bC.f(0, 512))
                P[s_] = Pn
                Q[s_] = Qn

    def gdn_uw(self, g, s_):
        S = self.S
        bA, bB = self.bankA[s_], self.bankB[s_]
        for h in range(4):
            S.mm(bA.f(h * 128, (h + 1) * 128), lhsT=g["RTb"][:, h, :], rhs=g["vb"][:, h, :])
        for h in range(4):
            S.mm(bB.f(h * 128, (h + 1) * 128), lhsT=g["kw"][:, h, :], rhs=g["RTb"][:, h, :])
        S.copy("act", g["u"][:, :, :].re("p h c -> p (h c)"), bA.f(0, 512))
        S.copy("dve", g["wTb"][:, :, :].re("p h c -> p (h c)"), bB.f(0, 512))

    def gdn_recur(self, g, S32, Sb, sel, blk, oa, hsel, oh):
        S = self.S
        pb = self.pb
        bV, bO, bS_ = pb[2], pb[3], pb[4]
        sm = V(self.sm4.t[:, blk % 4, :], [self.sm4.b])
        for i in range(2):
            r0, r1 = 64 * i, 64 * i + 64
            cur = Sb[self.cidx % 2]
            nxt = Sb[(self.cidx + 1) % 2]
            self.cidx += 1
            for h in range(4):
                S.mm(bV.f(h * 128, (h + 1) * 128, r0, r1), lhsT=g["wTb"][:, h, r0:r1], rhs=cur[:, h, :])
            S.tt("dve", g["vn"][r0:r1, :, :].re("p h c -> p (h c)"), g["u"][r0:r1, :, :].re("p h c -> p (h c)"),
                 bV.f(0, 512, r0, r1), ALU.subtract)
            for h in range(4):
                S.mm(bS_.f(h * 128, (h + 1) * 128), lhsT=g["kd"][r0:r1, h, :], rhs=g["vn"][r0:r1, h, :])
            for h in range(4):
                S.stt(S32[:, h, :], S32[:, h, :], sm[:, 48 + 2 * h + i:49 + 2 * h + i], bS_.f(h * 128, (h + 1) * 128),
                      ALU.mult, ALU.add)
            S.copy("act", nxt[:, :, :], S32[:, :, :])
            for h in range(4):
                S.mm(bO.f(h * 128, (h + 1) * 128, r0, r1), lhsT=g["qdT"][:, h, r0:r1], rhs=cur[:, h, :],
                     start=True, stop=False)
                S.mm(bO.f(h * 128, (h + 1) * 128, r0, r1), lhsT=g["aT"][r0:r1, h, r0:r1], rhs=g["vn"][r0:r1, h, :],
                     start=False, stop=True)
        r_ = blk % 4
        if r_ == 0:
            S.ts("dve", oa[:, :], bO.f(0, 512), sel[:, 0:1], None, ALU.mult)
        else:
            S.stt(oa[:, :], bO.f(0, 512), sel[:, r_:r_ + 1], oa[:, :], ALU.mult, ALU.add)
        if r_ == 0:
            S.ts("dve", oh[64:128, :], bO.f(0, 512, 64, 128), hsel[64:128, 0:1], None, ALU.mult)
        else:
            S.stt(oh[64:128, :], bO.f(0, 512, 64, 128), hsel[64:128, r_:r_ + 1], oh[64:128, :], ALU.mult, ALU.add)

    def phase2(self, keep):
        S = self.S
        pb = self.pb
        gc_ = S.group("const2", full=True)
        qnT, gates = keep["qnT"], keep["gates"]
        with ExitStack() as es:
            qnw = self.cload(es, "qnw", [128, 1], gc_)
            gonw4 = self.cload(es, "gonw4", [128, 512], gc_)
            onesf = self.cload(es, "onesf2", [128, 128], gc_)
            onesb = self.sb(es, "onesb2", [128, 128], BF16)
            S.copy("dve", onesb[:, :], onesf[:, :])
            winb2 = self.sb(es, "winb2", [128, 8, 1036], BF16)
            with ExitStack() as es2:
                wst = [self.sb(es2, f"w2st{i}", [128, 1036]) for i in range(2)]
                g_w = [S.group(f"w2st{i}") for i in range(2)]
                for k in range(8):
                    sl = k % 2
                    for (a, b_, o) in ((O_GZ, O_GZ + 512, 0), (O_NQ, O_NQ + 512, 512), (O_NG, O_NG + 12, 1024)):
                        S.dma(wst[sl][:, o:o + (b_ - a)], self.w_in[k * 128:(k + 1) * 128, a:b_], g_w[sl])
                    S.copy("dve" if k % 2 else "pool", winb2[:, k, :], wst[sl][:, :])
                S.flush()
            F = dict(ssq=self.sb(es, "f2_ssq", [128, 4]), rt=self.sb(es, "f2_rt", [128, 4]),
                     rstd=self.sb(es, "f2_rstd", [128, 4]), junk=self.sb(es, "f2_junk", [128, 1024], BF16),
                     xs=self.sb(es, "f2_xs", [128, 2, 1024], BF16))
            xt = [self.sb(es, f"x2t{i}", [128, 2, 1024]) for i in range(2)]
            g_x = [S.group(f"x2t{i}") for i in range(2)]
            hT = self.sb(es, "h2T_", [128, 8, 512], BF16)
            yf = [self.sb(es, f"y2f{i}", [128, 512]) for i in range(2)]
            sq = [self.sb(es, f"s2q{i}", [128, 512], BF16) for i in range(2)]
            rtt = [self.sb(es, f"r2tt{i}", [128, 512]) for i in range(2)]
            og = [self.sb(es, f"og{i}", [128, 512]) for i in range(2)]
            g_og = [S.group(f"og{i}") for i in range(2)]
            zs = [self.sb(es, f"zs{i}", [128, 512]) for i in range(4)]
            t1 = [self.sb(es, f"p2t{i}", [128, 512]) for i in range(2)]
            mg = [self.sb(es, f"mg{i}", [128, 512], BF16) for i in range(2)]
            mgT = [self.sb(es, f"mgT{i}", [128, 4, 128], BF16) for i in range(2)]
            g_mg = [S.group(f"mgT{i}") for i in range(2)]
            osq = self.sb(es, "osq", [128, 8])
            xrows = self.xo.rearrange("(n b p) d -> n p b d", b=2, p=128)

            def load_x(n):
                S.dma(xt[n % 2][:, :, :], xrows[n], g_x[n % 2])

            load_x(0)
            for t in range(4):
                for hf in range(2):
                    n = 2 * t + hf
                    if n + 1 < 8:
                        load_x(n + 1)
                    self.front(F, xt[n % 2], 2, hT, hf * 256, self.s1, 0)
                for h in range(4):
                    bank = pb[2 + (h % 2)]
                    for k in range(8):
                        S.mm(bank.f(0, 512), lhsT=winb2[:, k, 512 + h * 128:512 + (h + 1) * 128], rhs=hT[:, k, :],
                             start=(k == 0), stop=(k == 7))
                    y_ = yf[h % 2]
                    S.copy("act", y_[:, :], bank.f(0, 512))
                    S.act(sq[h % 2][:, :], bank.f(0, 512), AF.Square)
                    bk2 = pb[4 + (h % 2)]
                    S.mm(bk2.f(0, 512), lhsT=onesb[:, :], rhs=sq[h % 2][:, :])
                    r_ = rtt[h % 2]
                    S.act(r_[:, :], bk2.f(0, 512), AF.Ln, bias=EPS, scale=1.0 / 128.0)
                    S.act(r_[:, :], r_[:, :], AF.Exp, scale=-0.5)
                    S.stt(qnT[:, h, t * 512:(t + 1) * 512], y_[:, :], qnw[:, 0:1], r_[:, :], ALU.mult, ALU.mult)
                for blk in range(4):
                    cs = slice(blk * 128, (blk + 1) * 128)
                    bz = pb[7]
                    for k in range(8):
                        S.mm(bz.f(0, 512), lhsT=hT[:, k, cs], rhs=winb2[:, k, 0:512], start=(k == 0), stop=(k == 7))
                    S.act(zs[blk][:, :], bz.f(0, 512), AF.Silu)
                for blk in range(4):
                    j = 4 * t + blk
                    cs = slice(blk * 128, (blk + 1) * 128)
                    bg = pb[6]
                    for k in range(8):
                        S.mm(bg.f(0, 12), lhsT=hT[:, k, cs], rhs=winb2[:, k, 1024:1036], start=(k == 0), stop=(k == 7))
                    S.act(gates[:, j, :], bg.f(0, 12), AF.Exp, scale=-1.0)
                    S.ts("dve", gates[:, j, :], gates[:, j, :], 1.0, None, ALU.add)
                    S.recip(gates[:, j, :], gates[:, j, :])
                    z_ = zs[blk]
                    o_ = og[j % 2]
                    S.dma(o_[:, :], self.o_own[j], g_og[j % 2])
                    for h in range(4):
                        S.act(t1[j % 2][:, h * 128:(h + 1) * 128], o_[:, h * 128:(h + 1) * 128], AF.Square,
                              accum=osq[:, h:h + 1])
                    S.act(osq[:, 4:8], osq[:, 0:4], AF.Ln, bias=EPS, scale=1.0 / 128.0)
                    S.act(osq[:, 4:8], osq[:, 4:8], AF.Exp, scale=-0.5)
                    rb = V(osq.t[:, 4:8].unsqueeze(2).broadcast_to([128, 4, 128]), [osq.b])
                    S.tt("dve", t1[j % 2][:, :].re("p (h d) -> p h d", h=4), o_[:, :].re("p (h d) -> p h d", h=4), rb,
                         ALU.mult)
                    S.tt("pool", t1[j % 2][:, :], t1[j % 2][:, :], gonw4[:, :], ALU.mult)
                    S.tt("dve", mg[j % 2][:, :], t1[j % 2][:, :], z_[:, :], ALU.mult)
                    bt = pb[0]
                    for c in range(4):
                        S.tr(bt.b(c * 128, (c + 1) * 128), mg[j % 2][:, c * 128:(c + 1) * 128], self.idb[:, :])
                    S.copy("act", mgT[j % 2][:, :, :].re("p c t -> p (c t)"), bt.b(0, 512))
                    S.dma(self.mix_d[:, 0:4, j * 128:(j + 1) * 128], mgT[j % 2][:, :, :], g_mg[j % 2])
            gh = S.group("p2halo", full=True)
            selAB = self.cload(es, "selAB", [128, 2], gh)
            xht = self.sb(es, "xht", [32, D])
            S.dma(xht[:, :], self.inp("xh", [32, D])[:, :], gh)
            ca = self.sb(es, "hca", [32, 512])
            cb = self.sb(es, "hcb", [32, 512])
            S.memset("pool", cb[0:2, :], 0.0)
            opv = self.oprev_d.rearrange("j i c -> (j i) c")
            S.dma(ca[:, :], opv[0:32, :], gh)
            S.dma(cb[2:32, :], opv[0:30, :], gh)
            hTh = self.sb(es, "hTh", [128, 8, 32], BF16)
            self.front32(F, xht, hTh, self.s1, 0)
            qnTh, gatesh = keep["qnTh"], keep["gatesh"]
            bq = pb[2]
            for h in range(4):
                for k in range(8):
                    S.mm(bq.f(h * 32, (h + 1) * 32), lhsT=winb2[:, k, 512 + h * 128:512 + (h + 1) * 128], rhs=hTh[:, k, :],
                         start=(k == 0), stop=(k == 7))
            S.copy("act", yf[0][:, 0:128], bq.f(0, 128))
            S.act(sq[0][:, 0:128], bq.f(0, 128), AF.Square)
            S.mm(pb[4].f(0, 128), lhsT=onesb[:, :], rhs=sq[0][:, 0:128])
            S.act(rtt[0][:, 0:128], pb[4].f(0, 128), AF.Ln, bias=EPS, scale=1.0 / 128.0)
            S.act(rtt[0][:, 0:128], rtt[0][:, 0:128], AF.Exp, scale=-0.5)
            S.stt(qnTh[:, :, :].re("p h q -> p (h q)"), yf[0][:, 0:128], qnw[:, 0:1], rtt[0][:, 0:128], ALU.mult, ALU.mult)
            for k in range(8):
                S.mm(pb[6].f(0, 12, 0, 32), lhsT=hTh[:, k, :], rhs=winb2[:, k, 1024:1036], start=(k == 0), stop=(k == 7))
            S.act(gatesh[:, :], pb[6].f(0, 12, 0, 32), AF.Exp, scale=-1.0)
            S.ts("dve", gatesh[:, :], gatesh[:, :], 1.0, None, ALU.add)
            S.recip(gatesh[:, :], gatesh[:, :])
            for k in range(8):
                S.mm(pb[7].f(0, 512, 0, 32), lhsT=hTh[:, k, :], rhs=winb2[:, k, 0:512], start=(k == 0), stop=(k == 7))
            S.act(zs[0][0:32, :], pb[7].f(0, 512, 0, 32), AF.Silu)
            S.ts("dve", ca[:, :], ca[:, :], selAB[0:32, 0:1], None, ALU.mult)
            S.stt(ca[:, :], cb[:, :], selAB[0:32, 1:2], ca[:, :], ALU.mult, ALU.add)
            for h in range(4):
                S.act(t1[0][0:32, h * 128:(h + 1) * 128], ca[:, h * 128:(h + 1) * 128], AF.Square, accum=osq[0:32, h:h + 1])
            S.act(osq[0:32, 4:8], osq[0:32, 0:4], AF.Ln, bias=EPS, scale=1.0 / 128.0)
            S.act(osq[0:32, 4:8], osq[0:32, 4:8], AF.Exp, scale=-0.5)
            rb = V(osq.t[0:32, 4:8].unsqueeze(2).broadcast_to([32, 4, 128]), [osq.b])
            S.tt("dve", t1[0][0:32, :].re("p (h d) -> p h d", h=4), ca[:, :].re("p (h d) -> p h d", h=4), rb, ALU.mult)
            S.tt("pool", t1[0][0:32, :], t1[0][0:32, :], gonw4[0:32, :], ALU.mult)
            S.tt("dve", mg[0][0:32, :], t1[0][0:32, :], zs[0][0:32, :], ALU.mult)
            for c in range(4):
                S.tr(pb[0].b(c * 32, (c + 1) * 32), mg[0][0:32, c * 128:(c + 1) * 128], self.idb[0:32, 0:32])
            mghT = self.sb(es, "mghT", [128, 4, 32], BF16)
            S.copy("act", mghT[:, :, :].re("p c t -> p (c t)"), pb[0].b(0, 128))
            S.dma(self.mixh_d[:, 0:4, :], mghT[:, :, :], S.group("mghT"))
            S.flush()

    def phase3(self, keep):
        S = self.S
        pb = self.pb
        idb = self.idb
        qnT, gates = keep["qnT"], keep["gates"]
        SC = 128.0 ** -0.5
        with ExitStack() as es:
            gc_ = S.group("const3", full=True)
            kcmpT = self.sb(es, "kcmpT", [128, 512], BF16)
            vcx = self.sb(es, "vcx", [128, 4, 129], BF16)
            onesf = self.cload(es, "onesf3", [128, 128], gc_)
            onesb = self.sb(es, "onesb3", [128, 128], BF16)
            S.copy("dve", onesb[:, :], onesf[:, :])
            with ExitStack() as e2:
                g2 = S.group("const3b", full=True)
                kcw = self.cload(e2, "kcw", [128, 1], g2)
                pm511 = self.cload(e2, "pm511", [128, 1], g2)
                for which, src_d, w1n, w2n, posn in (("k", self.kcT_d, "cmp_k_w1", "cmp_k_w2", "cmp_k_posT"),
                                                    ("v", self.vcT_d, "cmp_v_w1", "cmp_v_w2", "cmp_v_posT")):
                    with ExitStack() as e3:
                        g3 = S.group("c3" + which, full=True)
                        xT = self.sb(e3, "cx" + which, [128, S_LEN], BF16)
                        S.dma(xT[:, 0:4096], src_d[:, 0:4096], g3)
                        S.dma(xT[:, 4096:8192], src_d[:, 4096:8192], g3)
                        w1d = self.inp(w1n, [128, 32, 128])
                        w1f = self.sb(e3, "w1f" + which, [128, 32, 128])
                        S.dma(w1f[:, 0:16, :], w1d[:, 0:16, :], g3)
                        S.dma(w1f[:, 16:32, :], w1d[:, 16:32, :], g3)
                        w1b = self.sb(e3, "w1b" + which, [128, 32, 128], BF16)
                        S.copy("dve", w1b[:, 0:16, :], w1f[:, 0:16, :])
                        S.copy("pool", w1b[:, 16:32, :], w1f[:, 16:32, :])
                        w2f = self.cload(e3, w2n, [128, 128], g3)
                        w2b = self.sb(e3, "w2b" + which, [128, 128], BF16)
                        S.copy("dve", w2b[:, :], w2f[:, :])
                        posf = self.cload(e3, posn, [128, 32], g3)
                        posb = self.sb(e3, "posb" + which, [128, 32], BF16)
                        S.copy("dve", posb[:, :], posf[:, :])
                        hid = self.sb(e3, "hid" + which, [128, 512], BF16)
                        bcol = self.sb(e3, "bcol" + which, [128, 1])
                        S.memset("pool", hid[:, :], 0.0)
                        bh, bb = pb[2], pb[3]
                        x3 = xT[:, :].re("p (n s) -> p n s", s=16)
                        for l in range(32):
                            S.mm(bh.f(0, 511), lhsT=w1b[:, l, :], rhs=x3[:, l // 16:l // 16 + 511, l % 16],
                                 start=(l == 0), stop=(l == 31))
                        for l in range(32):
                            S.mm(bb.f(0, 1), lhsT=w1b[:, l, :], rhs=posb[:, l:l + 1], start=(l == 0), stop=(l == 31))
                        S.copy("dve", bcol[:, :], bb.f(0, 1))
                        S.act(hid[:, 0:511], bh.f(0, 511), AF.Silu, bias=bcol[:, 0:1])
                        if which == "k":
                            bk = pb[4]
                            S.mm(bk.f(0, 512), lhsT=w2b[:, :], rhs=hid[:, :])
                            yk = self.sb(e3, "yk", [128, 512])
                            sqk = self.sb(e3, "sqk", [128, 512], BF16)
                            rk = self.sb(e3, "rk", [128, 512])
                            S.copy("act", yk[:, :], bk.f(0, 512))
                            S.act(sqk[:, :], bk.f(0, 512), AF.Square)
                            S.mm(pb[5].f(0, 512), lhsT=onesb[:, :], rhs=sqk[:, :])
                            S.act(rk[:, :], pb[5].f(0, 512), AF.Sqrt, bias=EPS, scale=1.0 / 128.0)
                            S.recip(rk[:, :], rk[:, :])
                            S.stt(kcmpT[:, :], yk[:, :], kcw[:, 0:1], rk[:, :], ALU.mult, ALU.mult)
                            S.memset("dve", kcmpT[:, 511:512], 0.0)
                        else:
                            bv = pb[4]
                            for nt in range(4):
                                S.mm(bv.f(nt * 128, (nt + 1) * 128), lhsT=hid[:, nt * 128:(nt + 1) * 128], rhs=w2b[:, :])
                            S.memset("pool", vcx[:, :, 128:129], 1.0)
                            S.copy("act", vcx[:, :, 0:128], bv.f(0, 512).re("p (n d) -> p n d", n=4))
                            S.ts("dve", vcx[:, 3, :], vcx[:, 3, :], pm511[:, 0:1], None, ALU.mult)
                        S.flush()
            if self.debug:
                S.dma(self.outp("d_kcmpT", [128, 512], BF16)[:, :], kcmpT[:, :], self.g_out)
                S.dma(self.outp("d_vcx", [128, 4, 129], BF16)[:, :, :], vcx[:, :, :], self.g_out)
            if self.stop == 31:
                S.flush()
                return
            kslT = self.sb(es, "kslT", [128, S_LEN], BF16)
            kwnT = self.sb(es, "kwnT", [128, S_LEN], BF16)
            vslx = self.sb(es, "vslx", [128, NB, 129], BF16)
            vwnx = self.sb(es, "vwnx", [128, NB, 129], BF16)
            for i in range(2):
                S.dma(kslT[:, i * 4096:(i + 1) * 4096], self.kslT_d[:, i * 4096:(i + 1) * 4096], gc_)
                S.dma(kwnT[:, i * 4096:(i + 1) * 4096], self.kwnT_d[:, i * 4096:(i + 1) * 4096], gc_)
                S.dma(vslx[:, i * 32:(i + 1) * 32, 0:128], self.vsl_d[:, i * 32:(i + 1) * 32, :], gc_)
                S.dma(vwnx[:, i * 32:(i + 1) * 32, 0:128], self.vwn_d[:, i * 32:(i + 1) * 32, :], gc_)
            S.memset("pool", vslx[:, :, 128:129], 1.0)
            S.memset("pool", vwnx[:, :, 128:129], 1.0)
            eall = self.sb(es, "eallb", [128, S_LEN], BF16)
            addm = self.cload(es, "addm", [128, 16, 128], gc_)
            ovb = self.sb(es, "ovb", [128, 4, 128], BF16)
            cmbb = self.sb(es, "cmbb", [128, 16, 2, 128], BF16)
            cbsb = self.sb(es, "cbsb", [128, 4, 4, 128], BF16)
            cbwb = self.sb(es, "cbwb", [128, 8, 4, 128], BF16)
            with ExitStack() as e2:
                g2 = S.group("const3c", full=True)
                ead = self.inp("eall", [128, S_LEN])
                est = [self.sb(e2, f"est{i}", [128, 2048]) for i in range(2)]
                g_e = [S.group(f"est{i}") for i in range(2)]
                for i in range(4):
                    S.dma(est[i % 2][:, :], ead[:, i * 2048:(i + 1) * 2048], g_e[i % 2])
                    S.copy("pool" if i % 2 else "dve", eall[:, i * 2048:(i + 1) * 2048], est[i % 2][:, :])
                ovf = self.cload(e2, "ovm", [128, 4, 128], g2)
                S.copy("dve", ovb[:, :, :], ovf[:, :, :])
                cmf = self.cload(e2, "cmb", [128, 16, 2, 128], g2)
                S.copy("pool", cmbb[:, :, :, :], cmf[:, :, :, :])
                cbsf = self.cload(e2, "cbs", [128, 4, 128], g2)
                cbwf = self.cload(e2, "cbw", [128, 8, 128], g2)
                for h in range(4):
                    S.copy("dve", cbsb[:, :, h, :], cbsf[:, :, :])
                    S.copy("pool", cbwb[:, :, h, :], cbwf[:, :, :])
                S.flush()
            Pc = [self.sb(es, f"Pc{i}", [128, 512], BF16) for i in range(4)]
            Pk = [self.sb(es, f"Pk{i}", [128, 512], BF16) for i in range(3)]
            cmrep = [self.sb(es, f"cmrep{i}", [128, 4, 128], BF16) for i in range(2)]
            ob = {nm: self.sb(es, "ob_" + nm, [128, 4, 129]) for nm in ("c", "s", "w")}
            rs = self.sb(es, "rs", [128, 12])
            cf = self.sb(es, "cf", [128, 12])
            imp = self.sb(es, "imp", [128, 128])
            imp2 = self.sb(es, "imp2", [128, 128])
            m8 = self.sb(es, "m8", [128, 16])
            selm = self.sb(es, "selm", [128, 128])
            biasT = self.sb(es, "biasT", [128, 4, 128], BF16)
            mixn = [self.sb(es, f"mixn{i}", [128, 4, 128], BF16) for i in range(2)]
            tmpn = self.sb(es, "tmpn", [128, 128])
            mnT = [self.sb(es, f"mnT{i}", [128, 4, 128], BF16) for i in range(2)]
            g_mn = [S.group(f"mnT{i}") for i in range(2)]
            bS = [pb[0], pb[1]]
            bO = [pb[2], pb[3]]
            bI, bT = pb[4], pb[5]
            si = [0]

            def nsa_part1(s_, Q, qv, cmp_plan, addv):
                NQ = 4 * Q

                ncp = len(cmp_plan)
                for i_, (nt, mk) in enumerate(cmp_plan):
                    bank = bS[i_ % 2]
                    S.mm(bank.f(0, NQ), lhsT=kcmpT[:, nt * 128:(nt + 1) * 128], rhs=qv, start=True, stop=(mk is None))
                    if mk is not None:
                        S.mm(bank.f(0, NQ), lhsT=idb[:, :], rhs=mk, start=False, stop=True)
                    S.act(Pc[i_][:, 0:NQ], bank.f(0, NQ), AF.Exp, scale=SC)
                for h in range(4):
                    for i_, (nt, mk) in enumerate(cmp_plan):
                        S.mm_(bOc[h // 2].f((h % 2) * 129, (h % 2) * 129 + 129, 0, Q), lhsT=Pc[i_][:, h * Q:(h + 1) * Q],
                              rhs=vcx[:, nt, :], start=(i_ == 0 and h % 2 == 0), stop=(i_ == ncp - 1), skip=True)
                for h in range(4):
                    for i_, (nt, mk) in enumerate(cmp_plan):
                        S.mm_(bI.f(h * 128, (h + 1) * 128, 0, Q), lhsT=Pc[i_][:, h * Q:(h + 1) * Q], rhs=ovb[:, nt, :],
                              start=(i_ == 0 and h == 0), stop=(i_ == ncp - 1), skip=True)
                for hh in range(2):
                    S.copy("act", obc[s_][0:Q, 2 * hh:2 * hh + 2, :], bOc[hh].f(0, 258, 0, Q).re("p (h c) -> p h c", h=2))
                S.ts("dve", rsc[s_][0:Q, 0:4], obc[s_][0:Q, :, 128], 1e-30, None, ALU.max)
                S.recip(rsc[s_][0:Q, 0:4], rsc[s_][0:Q, 0:4])
                S.ts("dve", imp[0:Q, :], bI.f(0, 128, 0, Q), rsc[s_][0:Q, 0:1], None, ALU.mult)
                for h in range(1, 4):
                    S.stt(imp[0:Q, :], bI.f(h * 128, (h + 1) * 128, 0, Q), rsc[s_][0:Q, h:h + 1], imp[0:Q, :], ALU.mult, ALU.add)
                S.tt("pool", imp[0:Q, :], imp[0:Q, :], addv, ALU.add)
                self.max8(m8[0:Q, 0:8], imp[0:Q, :])
                self.match_replace(imp2[0:Q, :], m8[0:Q, 0:8], imp[0:Q, :], -3.0e38)
                self.max8(m8[0:Q, 8:16], imp2[0:Q, :])
                S.ts("dve", selm[0:Q, :], imp[0:Q, :], m8[0:Q, 15:16], None, ALU.is_ge)
                S.mm(bT.f(0, Q), lhsT=selm[0:Q, :], rhs=self.idf[0:Q, 0:Q])
                S.ts("dve", bT2s[s_][:, 0:NQ].re("p (h q) -> p h q", h=4),
                     V(bT.t[:, 0:Q].unsqueeze(1).broadcast_to([128, 4, Q]), bT.q[0:1]), BIG, -BIG, ALU.mult, ALU.add)

            def nsa_part2(s_, Q, qv, slc_plan, win_plan, gatev, store):
                NQ = 4 * Q

                def scores(kT_tile, extra):
                    bank = bS[si[0] % 2]
                    out_ = Pk[si[0] % 3]
                    si[0] += 1
                    S.mm(bank.f(0, NQ), lhsT=kT_tile, rhs=qv, start=True, stop=(len(extra) == 0))
                    for i_, (l_, r_) in enumerate(extra):
                        S.mm(bank.f(0, NQ), lhsT=l_, rhs=r_, start=False, stop=(i_ == len(extra) - 1))
                    S.act(out_[:, 0:NQ], bank.f(0, NQ), AF.Exp, scale=SC)
                    return out_

                def pv(P_, vx, first, last):
                    for h in range(4):
                        S.mm_(bO[h // 2].f((h % 2) * 129, (h % 2) * 129 + 129, 0, Q), lhsT=P_[:, h * Q:(h + 1) * Q],
                              rhs=vx, start=(first and h % 2 == 0), stop=last, skip=True)

                def evac_o(dst):
                    for hh in range(2):
                        S.copy("act", dst[0:Q, 2 * hh:2 * hh + 2, :], bO[hh].f(0, 258, 0, Q).re("p (h c) -> p h c", h=2))

                brhs = bT2s[s_][:, 0:NQ]
                S.copy("dve", rs[0:Q, 0:4], rsc[s_][0:Q, 0:4])
                prev = None
                for i_, (kt, extra) in enumerate(slc_plan):
                    ex = [(eall[:, kt * 128:(kt + 1) * 128], brhs)] + extra
                    P_ = scores(kslT[:, kt * 128:(kt + 1) * 128], ex)
                    if prev is not None:
                        pv(prev[0], vslx[:, prev[1], :], prev[2] == 0, False)
                    prev = (P_, kt, i_)
                pv(prev[0], vslx[:, prev[1], :], prev[2] == 0, True)
                evac_o(ob["s"])
                prev = None
                for i_, (kt, mk) in enumerate(win_plan):
                    P_ = scores(kwnT[:, kt * 128:(kt + 1) * 128], [(idb[:, :], mk)])
                    if prev is not None:
                        pv(prev[0], vwnx[:, prev[1], :], prev[2] == 0, False)
                    prev = (P_, kt, i_)
                pv(prev[0], vwnx[:, prev[1], :], prev[2] == 0, True)
                evac_o(ob["w"])
                S.ts("dve", rs[0:Q, 4:8], ob["s"][0:Q, :, 128], 1e-30, None, ALU.max)
                S.ts("dve", rs[0:Q, 8:12], ob["w"][0:Q, :, 128], 1e-30, None, ALU.max)
                S.recip(rs[0:Q, 4:12], rs[0:Q, 4:12])
                g3 = gatev.re("p (h g) -> p g h", g=3)
                S.tt("dve", cf[0:Q, :].re("p (g h) -> p g h", g=3), rs[0:Q, :].re("p (g h) -> p g h", g=3), g3, ALU.mult)
                mx = mixn[si[0] % 2]
                for h in range(4):
                    S.ts("pool", tmpn[0:Q, :], obc[s_][0:Q, h, 0:128], cf[0:Q, h:h + 1], 1.0, ALU.mult, ALU.mult)
                    S.stt(tmpn[0:Q, :], ob["s"][0:Q, h, 0:128], cf[0:Q, 4 + h:5 + h], tmpn[0:Q, :], ALU.mult, ALU.add)
                    S.stt(mx[0:Q, h, :], ob["w"][0:Q, h, 0:128], cf[0:Q, 8 + h:9 + h], tmpn[0:Q, :], ALU.mult, ALU.add)
                for h in range(4):
                    S.tr(bT.b(h * Q, (h + 1) * Q), mx[0:Q, h, :], idb[0:Q, 0:Q])
                store(bT.b(0, 4 * Q))

            bT2s = [self.sb(es, f"bT2_{i}", [128, 512], BF16) for i in range(2)]
            obc = [self.sb(es, f"obc{i}", [128, 4, 129]) for i in range(2)]
            rsc = [self.sb(es, f"rsc{i}", [128, 4]) for i in range(2)]
            bOc = [pb[6], pb[7]]
            plans = []
            for j in range(16):
                qv = qnT[:, :, j * 128:(j + 1) * 128]
                nt_hi = min(3, (32 * j + 30) // 128)
                lo = max(0, 32 * j - 1) // 128
                cmp_plan = []
                for nt in range(nt_hi + 1):
                    mk = None
                    if nt >= lo:
                        mk = nt - lo
                    cmp_plan.append((nt, mk))
                slc_plan = []
                for kt in range(4 * j + 4):
                    ex = []
                    if kt >= 4 * j:
                        ex.append((idb[:, :], cbsb[:, kt - 4 * j, :, :].re("p h q -> p (h q)")))
                    slc_plan.append((kt, ex))
                win_plan = [(4 * j - 4 + e, cbwb[:, e, :, :].re("p h q -> p (h q)")) for e in range(8) if 4 * j - 4 + e >= 0]

                def store(src, j=j):
                    S.copy("act", mnT[j % 2][:, :, :].re("p c t -> p (c t)"), src)
                    S.dma(self.mix_d[:, 4:8, j * 128:(j + 1) * 128], mnT[j % 2][:, :, :], g_mn[j % 2])

                plans.append((qv, cmp_plan, slc_plan, win_plan, store))
            def emit1(j):
                qv, cmp_plan, _, _, _ = plans[j]
                cp = []
                for nt, slot in cmp_plan:
                    mk = None
                    if slot is not None:
                        S.copy("pool", cmrep[slot][:, :, :],
                               V(cmbb.t[:, j, slot, :].unsqueeze(1).broadcast_to([128, 4, 128]), [cmbb.b]))
                        mk = cmrep[slot][:, :, :].re("p h q -> p (h q)")
                    cp.append((nt, mk))
                nsa_part1(j % 2, 128, qv, cp, addm[:, j, :])

            def emit2(j):
                qv, _, slc_plan, win_plan, store = plans[j]
                nsa_part2(j % 2, 128, qv, slc_plan, win_plan, gates[:, j, :], store)

            with ExitStack() as ew:
                wps = [self.sb(ew, f"wps{i}", [128, 1408]) for i in range(2)]
                wpb = [self.sb(ew, f"wpb{i}", [128, 1408], BF16) for i in range(2)]
                g_ws = [S.group(f"wps{i}") for i in range(2)]
                g_wb = [S.group(f"wpb{i}") for i in range(2)]

                def wload(n):
                    k, q = n // 4, n % 4
                    S.dma(wps[n % 2][:, :], self.w_up[k * 128:(k + 1) * 128, q * 1408:(q + 1) * 1408], g_ws[n % 2])

                def wconv(n):
                    k, q = n // 4, n % 4
                    S.copy("pool", wpb[n % 2][:, :], wps[n % 2][:, :])
                    S.dma(self.wup_d[:, 11 * q:11 * q + 11, k, :], wpb[n % 2][:, :].re("p (c n) -> p c n", n=128),
                          g_wb[n % 2])
                    if n + 2 < 32:
                        wload(n + 2)

                wload(0)
                wload(1)

                def extra(j):
                    wconv(2 * j)
                    wconv(2 * j + 1)

                self._p3_emit(emit1, emit2, extra=extra)
                S.flush()
            self.nsa_halo(es, (nsa_part1, nsa_part2), keep, idb)
            S.flush()

    @staticmethod
    def _p3_emit(emit1, emit2, n=16, extra=None):
        emit1(0)
        for j in range(n):
            if j + 1 < n:
                emit1(j + 1)
            if extra is not None:
                extra(j)
            emit2(j)

    def nsa_halo(self, es, nsa_parts, keep, idb):
        S = self.S
        with ExitStack() as e2:
            g2 = S.group("const3h", full=True)
            haddm = self.cload(e2, "haddm", [32, 128], g2)
            hcmb = self.sb(e2, "hcmb", [128, 4, 128], BF16)
            hsmb = self.sb(e2, "hsmb", [128, 64, 128], BF16)
            hwmb = self.sb(e2, "hwmb", [128, 64, 128], BF16)
            hcf = self.cload(e2, "hcm", [128, 4, 128], g2)
            S.copy("dve", hcmb[:, :, :], hcf[:, :, :])
            st = [self.sb(e2, f"hst{i}", [128, 8, 128]) for i in range(2)]
            g_s = [S.group(f"hst{i}") for i in range(2)]
            n = 0
            for nm, dst in (("hsm", hsmb), ("hwm", hwmb)):
                src = self.inp(nm, [128, 64, 128])
                for i in range(8):
                    S.dma(st[n % 2][:, :, :], src[:, i * 8:(i + 1) * 8, :], g_s[n % 2])
                    S.copy("pool" if n % 2 else "dve", dst[:, i * 8:(i + 1) * 8, :], st[n % 2][:, :, :])
                    n += 1
            mnTh = self.sb(e2, "mnTh", [128, 4, 32], BF16)
            g_m = S.group("mnTh")
            cmp_plan = [(nt, hcmb[:, nt, :]) for nt in range(4)]
            slc_plan = [(kt, [(idb[:, :], hsmb[:, kt, :])]) for kt in range(NB)]
            win_plan = [(kt, hwmb[:, kt, :]) for kt in range(NB)]

            def store(src):
                S.copy("act", mnTh[:, :, :].re("p c t -> p (c t)"), src)
                S.dma(self.mixh_d[:, 4:8, :], mnTh[:, :, :], g_m)

            part1, part2 = nsa_parts
            part1(0, 32, keep["qnTh"][:, :, :], cmp_plan, haddm[:, :])
            part2(0, 32, keep["qnTh"][:, :, :], slc_plan, win_plan, keep["gatesh"][:, :], store)
            S.flush()

    def max8(self, out, in_):
        o_, i_ = out.ap, in_.ap
        self.S.op("dve", lambda e: e.max(out=o_, in_=i_), r=self.S._bs(in_), w=self.S._bs(out))

    def match_replace(self, out, rep, vals, imm):
        o_, r_, v_ = out.ap, rep.ap, vals.ap
        self.S.op("dve", lambda e: e.match_replace(out=o_, in_to_replace=r_, in_values=v_, imm_value=imm),
                  r=self.S._bs(rep, vals), w=self.S._bs(out))

    def phase45(self):
        S = self.S
        pb = self.pb
        idb = self.idb
        w_out = self.inp("w_out", [D, D])
        w_dn = self.inp("w_dn", [DFF, D])
        wup_d = self.wup_d
        with ExitStack() as es:
            gc_ = S.group("const4", full=True)
            fcw = self.cload(es, "fcw", [128, 44 * 3], gc_)
            fcb = self.cload(es, "fcb", [128, 44], gc_)
            wdnb = self.sb(es, "wdnb", [128, 22, D], BF16)
            woutb = self.sb(es, "woutb", [128, 8, D], BF16)
            g1row = self.sb(es, "g1row", [128, D])
            g2row = self.sb(es, "g2row", [128, D])
            with ExitStack() as e2:
                stg = [self.sb(e2, f"wst4_{i}", [128, 2, D]) for i in range(2)]
                g_s = [S.group(f"wst4_{i}") for i in range(2)]
                n = 0
                for (src, dst, nch) in ((w_dn, wdnb, 22), (w_out, woutb, 8)):
                    sv = src.rearrange("(c p) d -> p c d", p=128)
                    for c0 in range(0, nch, 2):
                        st = stg[n % 2]
                        S.dma(st[:, :, :], sv[:, c0:c0 + 2, :], g_s[n % 2])
                        S.copy(("dve", "pool", "act")[n % 3], dst[:, c0:c0 + 2, :], st[:, :, :])
                        n += 1
                gb = self.sb(e2, "gbc", [128, 128])
                for gi, (col0, dst) in enumerate(((16, g1row), (40, g2row))):
                    for cc in range(8):
                        S.copy("dve", gb[:, :], V(self.modT.t[:, col0 + cc:col0 + cc + 1].broadcast_to([128, 128]),
                                                  [self.modT.b]))
                        bank = pb[cc % 2]
                        S.mm(bank.f(0, 128), lhsT=gb[:, :], rhs=self.idf[:, :])
                        S.copy("act", dst[:, cc * 128:(cc + 1) * 128], bank.f(0, 128))
                S.flush()
            F = dict(ssq=self.sb(es, "f4_ssq", [128, 4]), rt=self.sb(es, "f4_rt", [128, 4]),
                     rstd=self.sb(es, "f4_rstd", [128, 4]), junk=self.sb(es, "f4_junk", [128, 1024], BF16),
                     xs=self.sb(es, "f4_xs", [128, 2, 1024], BF16))
            x1 = [self.sb(es, f"x1_{i}", [128, 4, D]) for i in range(2)]
            g_x = [S.group(f"x1_{i}") for i in range(2)]
            mixt = self.sb(es, "mixt", [128, 8, 512], BF16)
            g_m = S.group("mixt")
            h2T = self.sb(es, "h2T", [128, 8, 512], BF16)
            gT = self.sb(es, "gT", [128, 22, 512], BF16)
            wch = [self.sb(es, f"wch{i}", [128, 2, 8, 128], BF16) for i in range(3)]
            g_wc = [S.group(f"wch{i}") for i in range(3)]
            upa = [self.sb(es, f"upa{i}", [128, 4, 130]) for i in range(2)]
            upb = [self.sb(es, f"upb{i}", [128, 4, 130]) for i in range(2)]
            for u_ in upa + upb:
                S.memset("pool", u_[:, :, :], 0.0)
            aa = [self.sb(es, f"aa{i}", [128, 4, 128]) for i in range(2)]
            ab_ = [self.sb(es, f"ab_{i}", [128, 4, 128]) for i in range(2)]
            tmp = [self.sb(es, f"tmp4_{i}", [128, 512]) for i in range(2)]
            xrows = self.xo.rearrange("(t b p) d -> t p b d", b=4, p=128)
            orows = self.out.rearrange("(t b p) d -> t p b d", b=4, p=128)
            wi = 0
            gh = S.group("p4halo", full=True)
            hexr = self.cload(es, "hexr", [128, 32], gh)
            x1h = self.sb(es, "x1h", [32, D])
            mixh = self.sb(es, "mixh", [128, 8, 32], BF16)
            S.dma(x1h[:, :], self.din["xh"][:, :], gh)
            S.dma(mixh[:, :, :], self.mixh_d[:, :, :], gh)
            h2Th = self.sb(es, "h2Th", [128, 8, 32], BF16)
            for half in range(2):
                bank = pb[2 + half]
                hs = slice(half * 512, (half + 1) * 512)
                for c in range(8):
                    S.mm(bank.f(0, 512, 0, 32), lhsT=mixh[:, c, :], rhs=woutb[:, c, hs], start=(c == 0), stop=(c == 7))
                S.tt("dve", tmp[half][0:32, :], bank.f(0, 512, 0, 32), g1row[0:32, hs], ALU.mult)
                S.tt("pool", x1h[:, hs], x1h[:, hs], tmp[half][0:32, :], ALU.add)
            self.front32(F, x1h, h2Th, self.s2, 24)

            def conv(u_, c, dst):
                S.ts("pool", dst[:, :, :], u_[:, :, 2:130], fcw[:, c * 3 + 2:c * 3 + 3], fcb[:, c:c + 1], ALU.mult, ALU.add)
                S.stt(dst[:, :, :], u_[:, :, 1:129], fcw[:, c * 3 + 1:c * 3 + 2], dst[:, :, :], ALU.mult, ALU.add)
                S.stt(dst[:, :, :], u_[:, :, 0:128], fcw[:, c * 3:c * 3 + 1], dst[:, :, :], ALU.mult, ALU.add)

            for t in range(4):
                xx = x1[t % 2]
                S.dma(xx[:, :, :], xrows[t], g_x[t % 2])
                S.dma(mixt[:, :, :], self.mix_d[:, :, t * 512:(t + 1) * 512], g_m)
                for blk in range(4):
                    cs = slice(blk * 128, (blk + 1) * 128)
                    for half in range(2):
                        bank = pb[2 + half]
                        hs = slice(half * 512, (half + 1) * 512)
                        for c in range(8):
                            S.mm(bank.f(0, 512), lhsT=mixt[:, c, cs], rhs=woutb[:, c, hs], start=(c == 0), stop=(c == 7))
                        tm = tmp[half]
                        S.tt("dve", tm[:, :], bank.f(0, 512), g1row[:, hs], ALU.mult)
                        S.tt("pool", xx[:, blk, hs], xx[:, blk, hs], tm[:, :], ALU.add)
                if self.debug:
                    S.dma(self.dx1[t], xx[:, :, :], self.g_out)
                for hf in range(2):
                    self.front(F, xx, 2, h2T, hf * 256, self.s2, 24, b0=2 * hf)
                for c in range(22):
                    w_ = wch[wi % 3]
                    S.dma(w_[:, 0, :, :], wup_d[:, c, :, :], g_wc[wi % 3])
                    S.dma(w_[:, 1, :, :], wup_d[:, 22 + c, :, :], g_wc[wi % 3])
                    wi += 1
                    ua, ub = upa[c % 2], upb[c % 2]
                    for i_, (u_, bank) in enumerate(((ua, pb[4]), (ub, pb[5]))):
                        for k in range(8):
                            S.mm(bank.f(0, 512), lhsT=w_[:, i_, k, :], rhs=h2T[:, k, :], start=(k == 0), stop=(k == 7))
                        S.copy("act", u_[:, :, 2:130], bank.f(0, 512).re("p (b t) -> p b t", b=4))
                        bh_ = pb[6 + i_]
                        for k in range(8):
                            S.mm(bh_.f(0, 8), lhsT=w_[:, i_, k, :], rhs=h2Th[:, k, 8 * t:8 * t + 8], start=(k == 0),
                                 stop=(k == 7))
                        S.tt("dve", u_[:, :, 0:2], bh_.f(0, 8).re("p (b i) -> p b i", i=2),
                             hexr[:, 8 * t:8 * t + 8].re("p (b i) -> p b i", i=2), ALU.mult)
                    conv(ua, c, aa[c % 2])
                    conv(ub, 22 + c, ab_[c % 2])
                    S.act(aa[c % 2][:, :, :], aa[c % 2][:, :, :], AF.Silu)
                    S.tt("dve", gT[:, c, :].re("p (b t) -> p b t", b=4), aa[c % 2][:, :, :], ab_[c % 2][:, :, :], ALU.mult)
                for blk in range(4):
                    cs = slice(blk * 128, (blk + 1) * 128)
                    for half in range(2):
                        bank = pb[6 + half]
                        hs = slice(half * 512, (half + 1) * 512)
                        for c in range(22):
                            S.mm(bank.f(0, 512), lhsT=gT[:, c, cs], rhs=wdnb[:, c, hs], start=(c == 0), stop=(c == 21))
                        tm = tmp[half]
                        S.tt("dve", tm[:, :], bank.f(0, 512), g2row[:, hs], ALU.mult)
                        S.tt("pool", xx[:, blk, hs], xx[:, blk, hs], tm[:, :], ALU.add)
                S.dma(orows[t], xx[:, :, :], g_x[t % 2])
            S.flush()


def _colL(v, n):
    return np.ascontiguousarray(np.asarray(v, np.float32).reshape(n, 128).T)


def _rep(v, n=128):
    v = np.asarray(v, np.float32).reshape(1, -1)
    return np.ascontiguousarray(np.repeat(v, n, axis=0))


def _consts():
    p = np.arange(128)
    same = (p[:, None] // 64) == (p[None, :] // 64)
    c = {}
    c["identf"] = np.eye(128, dtype=np.float32)
    c["tri2"] = (same & (p[:, None] <= p[None, :])).astype(np.float32)
    c["blk2"] = same.astype(np.float32)
    c["onesf"] = np.ones((128, 128), np.float32)
    c["cind"] = np.stack([(p < 64), (p >= 64)], axis=1).astype(np.float32)
    ma = np.where(same & (p[None, :] < p[:, None]), 0.0, BIG).astype(np.float32)
    mb = np.where(same & (p[None, :] >= p[:, None]), 0.0, -BIG).astype(np.float32)
    c["ma4"] = np.ascontiguousarray(np.tile(ma, (1, 4)))
    c["mb4"] = np.ascontiguousarray(np.tile(mb, (1, 4)))
    return c


def _host_inputs(inputs):
    x = np.asarray(inputs["x"], np.float32)
    cst = _consts()
    g = lambda k: np.asarray(inputs[k][0], np.float32)
    gcw = g("gdn_conv_w")
    gcwT = np.ascontiguousarray(gcw.reshape(4, 12, 128).transpose(2, 1, 0).reshape(128, 48))
    shared = {
        "ada_w": np.ascontiguousarray(g("ada_w")),
        "ada_bT": _colL(g("ada_b"), 48),
        "n1w": _colL(g("norm1_w"), 8),
        "n2w": _colL(g("norm2_w"), 8),
        "w_in": np.ascontiguousarray(g("w_in")),
        "gcw": gcwT,
        "dtb": _rep(g("gdn_dt_bias")),
        "alog": _rep(g("gdn_A_log")),
        "kslw": _colL(g("nsa_k_norm_slc"), 1),
        "kwnw": _colL(g("nsa_k_norm_win"), 1),
        "qnw": _colL(g("nsa_q_norm_w"), 1),
        "gonw4": _rep(np.tile(g("gdn_out_norm_w"), 4)),
        "onesf2": np.ones((128, 128), np.float32),
    }
    for nm in ("k", "v"):
        shared[f"cmp_{nm}_w1"] = np.ascontiguousarray(g(f"cmp_{nm}_w1").reshape(32, 128, 128).transpose(1, 0, 2))
        shared[f"cmp_{nm}_w2"] = np.ascontiguousarray(g(f"cmp_{nm}_w2"))
        shared[f"cmp_{nm}_posT"] = np.ascontiguousarray(g(f"cmp_{nm}_pos").T)
    shared["kcw"] = _colL(g("nsa_k_norm_cmp"), 1)
    shared["onesf3"] = np.ones((128, 128), np.float32)
    pm = np.ones((128, 1), np.float32)
    pm[127, 0] = 0.0
    shared["pm511"] = pm
    keys = np.arange(S_LEN)
    shared["eall"] = (keys[None, :] // 64 == np.arange(128)[:, None]).astype(np.float32)
    n = np.arange(512)
    js = np.arange(128)
    ov = np.minimum(16 * n[:, None] + 32, 64 * js[None, :] + 64) - np.maximum(16 * n[:, None], 64 * js[None, :])
    ov = np.clip(ov, 0, None).astype(np.float32) / 32.0
    ov[511] = 0.0
    shared["ovm"] = np.ascontiguousarray(ov.reshape(4, 128, 128).transpose(1, 0, 2))
    fw = g("ffn_conv_w")
    shared["fcw"] = np.ascontiguousarray(fw.reshape(3, 44, 128).transpose(2, 1, 0).reshape(128, 132))
    shared["fcb"] = _colL(g("ffn_conv_b"), 44)
    shared["w_out"] = np.ascontiguousarray(g("w_out"))
    shared["w_up"] = np.ascontiguousarray(g("ffn_w_up"))
    shared["w_dn"] = np.ascontiguousarray(g("ffn_w_down"))
    shared.update(cst)
    maps = []
    for core in range(8):
        b, r = core // 4, core % 4
        xo = np.concatenate([x[b, 128 * (4 * j + r):128 * (4 * j + r) + 128] for j in range(16)], axis=0)
        m = dict(shared)
        m["xb"] = np.ascontiguousarray(x[b])
        m["xo"] = np.ascontiguousarray(xo)
        m["cT"] = _colL(inputs["c"][b], 8)
        sel = np.zeros((128, 4), np.float32)
        sel[:, r] = 1.0
        m["sel"] = sel
        p = np.arange(128)
        q = np.arange(128)
        addm = np.zeros((128, 16, 128), np.float32)
        cmb = np.zeros((128, 16, 2, 128), np.float32)
        for j in range(16):
            qi = 4 * j + r
            tq = 128 * qi + q
            cur = tq // 64
            jj = np.arange(128)[None, :]
            valid = jj <= cur[:, None]
            forced = (jj == 0) | (jj == cur[:, None]) | (jj == cur[:, None] - 1)
            addm[:, j, :] = np.where(valid, np.where(forced, 1.0e4, 0.0), -1.0e30)
            lo = max(0, 32 * j - 1) // 128
            for slot in range(2):
                nn = 128 * (lo + slot) + p
                ok = (16 * nn[:, None] + 31) <= tq[None, :]
                cmb[:, j, slot, :] = np.where(ok, 0.0, -BIG)
        m["addm"] = addm
        m["cmb"] = cmb
        cbs = np.zeros((128, 4, 128), np.float32)
        for d in range(4):
            ok = (128 * (d - r) + p[:, None]) <= q[None, :]
            cbs[:, d, :] = np.where(ok, 0.0, -BIG)
        m["cbs"] = cbs
        cbw = np.zeros((128, 8, 128), np.float32)
        for e in range(8):
            rel = 128 * (e - 4 - r) + p[:, None]
            ok = (rel <= q[None, :]) & (rel > q[None, :] - 512)
            cbw[:, e, :] = np.where(ok, 0.0, -BIG)
        m["cbw"] = cbw
        hs_ = np.zeros((128, 4), np.float32)
        hs_[:, (r - 1) % 4] = 1.0
        m["hsel"] = hs_
        sab = np.zeros((128, 2), np.float32)
        sab[:, 0] = 1.0 if r >= 1 else 0.0
        sab[:, 1] = 1.0 if r == 0 else 0.0
        m["selAB"] = sab
        tq = np.array([128 * (4 * j + r) - 2 + i for j in range(16) for i in range(2)])
        ex = tq >= 0
        xh = np.zeros((32, D), np.float32)
        xh[ex] = x[b, tq[ex]]
        m["xh"] = xh
        m["hexr"] = _rep(ex.astype(np.float32))
        cur = tq // 64
        jj = np.arange(128)[None, :]
        valid = (jj <= cur[:, None]) & ex[:, None]
        forced = (jj == 0) | (jj == cur[:, None]) | (jj == cur[:, None] - 1)
        m["haddm"] = np.where(valid, np.where(forced, 1.0e4, 0.0), -1.0e30).astype(np.float32)
        nn = np.arange(512)
        okc = ((16 * nn[:, None] + 31) <= tq[None, :]) & ex[None, :]
        hcm = np.where(okc, 0.0, -BIG).astype(np.float32).reshape(4, 128, 1, 32)
        m["hcm"] = np.ascontiguousarray(np.broadcast_to(hcm, (4, 128, 4, 32)).transpose(1, 0, 2, 3).reshape(128, 4, 128))
        pos = np.arange(S_LEN)
        oks = (pos[:, None] <= tq[None, :]) & ex[None, :]
        okw = oks & (pos[:, None] > tq[None, :] - 512)
        for nm, ok in (("hsm", oks), ("hwm", okw)):
            a = np.where(ok, 0.0, -BIG).astype(np.float32).reshape(64, 128, 1, 32)
            m[nm] = np.ascontiguousarray(np.broadcast_to(a, (64, 128, 4, 32)).transpose(1, 0, 2, 3).reshape(128, 64, 128))
        maps.append(m)
    return maps


def run(inputs, debug=False, upto=9, ntiles=NT):
    bld = Builder(debug, ntiles)
    nc = bld.build(upto)
    maps = _host_inputs(inputs)
    maps = [{k: v for k, v in m.items() if k in bld.din} for m in maps]
    missing = [k for k in bld.din if k not in maps[0]]
    assert not missing, missing
    res = run_bass_kernel_spmd(nc, maps, core_ids=list(range(8)))
    return res.results


def kernel(**inputs):
    results = run(inputs)
    outp = np.zeros((2, S_LEN, D), np.float32)
    for core in range(8):
        b, r = core // 4, core % 4
        o = results[core]["out"]
        for j in range(16):
            qi = 4 * j + r
            outp[b, 128 * qi:128 * qi + 128] = o[128 * j:128 * j + 128]
    return outp
```

```python
import numpy as np
from contextlib import ExitStack
import concourse.bass as bass
import concourse.mybir as mybir
from concourse.bass_utils import run_bass_kernel_spmd

F32 = mybir.dt.float32
BF16 = mybir.dt.bfloat16
AF = mybir.ActivationFunctionType
ALU = mybir.AluOpType

D = 1024
S_LEN = 8192
NT = 16
NB = 64
N_IN = 3348
DFF = 2816
EPS = 1e-6
O_GQ, O_GK, O_GV, O_GZ, O_GA, O_GB, O_NQ, O_KC, O_VC, O_KSL, O_VSL, O_KWN, O_VWN, O_NG = (
    0, 512, 1024, 1536, 2048, 2052, 2056, 2568, 2696, 2824, 2952, 3080, 3208, 3336)
BIG = 30000.0


class Buf:
    __slots__ = ("name", "w", "rs", "const", "excl")

    def __init__(self, name, const=False, excl=False):
        self.name = name
        self.w = None
        self.rs = []
        self.const = const
        self.excl = excl


class V:
    __slots__ = ("ap", "bs")

    def __init__(self, ap, bs):
        self.ap = ap
        self.bs = bs if isinstance(bs, (list, tuple)) else [bs]

    def bitcast(self, dt):
        return V(self.ap.bitcast(dt), self.bs)

    def re(self, pat, **kw):
        return V(self.ap.rearrange(pat, **kw), self.bs)

    def bc(self, shape):
        return V(self.ap.broadcast_to(shape), self.bs)

    def __getitem__(self, k):
        return V(self.ap[k], self.bs)


class Tl:
    def __init__(self, t, b):
        self.t = t
        self.b = b

    def __getitem__(self, k):
        return V(self.t[k], self.b)


class Op:
    __slots__ = ("eng", "fn", "deps", "dmaw", "signal", "sigval", "grp")


class DGroup:
    def __init__(self, name, sem, full=False):
        self.name = name
        self.sem = sem
        self.count = 0
        self.full = full


ENGS = ("pe", "act", "dve", "pool", "sp")


class Sched:
    def __init__(self, nc, es):
        self.nc = nc
        self.es = es
        self.eng = {"pe": nc.tensor, "act": nc.scalar, "dve": nc.vector, "pool": nc.gpsimd, "sp": nc.sync}
        self.sem = {e: es.enter_context(nc.semaphore("s_" + e)) for e in ENGS}
        self.cnt = {e: 0 for e in ENGS}
        self.seen = {e: {} for e in ENGS}
        self.ops = []
        self.bufs = []
        self.groups = []
        self.nins = 0

    def buf(self, name, const=False, excl=False):
        b = Buf(name, const, excl)
        self.bufs.append(b)
        return b

    def group(self, name, full=False):
        g = DGroup(name, self.es.enter_context(self.nc.semaphore("g_" + name)), full)
        self.groups.append(g)
        return g

    def op(self, eng, fn, r=(), w=(), grp=None):
        o = Op()
        o.eng = eng
        o.fn = fn
        o.signal = False
        o.sigval = None
        o.grp = grp
        deps = []
        for b in r:
            if b.w is not None:
                deps.append(b.w)
            if b.excl:
                deps.extend(x for x in b.rs if x.eng != eng)
        for b in w:
            if b.w is not None:
                deps.append(b.w)
            deps.extend(b.rs)
        seen = set()
        dd = []
        for d in deps:
            if id(d) in seen or d is o:
                continue
            seen.add(id(d))
            if d.eng == "pe" and eng == "pe":
                continue
            if grp is not None and grp.full and d.grp is grp:
                continue
            dd.append(d)
        o.deps = dd
        o.dmaw = {}
        for d in dd:
            if d.grp is None:
                d.signal = True
            else:
                o.dmaw[d.grp.name] = d.grp.count
        if grp is not None:
            grp.count += 1
        for b in w:
            b.w = o
            b.rs = []
        for b in r:
            if not b.const and b.w is not o:
                b.rs.append(o)
        self.ops.append(o)
        return o

    def flush(self, barrier=True):
        if barrier:
            last = {}
            for o in self.ops:
                last[o.eng] = o
            for e, o in last.items():
                if o.grp is None:
                    o.signal = True
        for o in self.ops:
            e = self.eng[o.eng]
            for d in o.deps:
                if d.grp is not None:
                    key = "g_" + d.grp.name
                    val = 16 * (d.grp.count if d.grp.full else o.dmaw[d.grp.name])
                    sem = d.grp.sem
                else:
                    key = d.eng
                    val = d.sigval
                    sem = self.sem[d.eng]
                    assert val is not None, (o.eng, d.eng)
                if self.seen[o.eng].get(key, 0) >= val:
                    continue
                e.wait_ge(sem, val)
                self.seen[o.eng][key] = val
            ins = o.fn(e)
            self.nins += 1
            if o.grp is not None:
                ins.then_inc(o.grp.sem, 16)
            elif o.signal:
                self.cnt[o.eng] += 1
                o.sigval = self.cnt[o.eng]
                ins.then_inc(self.sem[o.eng], 1)
        self.ops = []
        if barrier:
            for en in ENGS:
                e = self.eng[en]
                for e2 in ENGS:
                    if e2 == en or e2 == "sp":
                        continue
                    if self.cnt[e2] > self.seen[en].get(e2, 0):
                        e.wait_ge(self.sem[e2], self.cnt[e2])
                        self.seen[en][e2] = self.cnt[e2]
                for g in self.groups:
                    key = "g_" + g.name
                    if 16 * g.count > self.seen[en].get(key, 0):
                        e.wait_ge(g.sem, 16 * g.count)
                        self.seen[en][key] = 16 * g.count
            for b in self.bufs:
                b.w = None
                b.rs = []

    @staticmethod
    def _bs(*vs):
        out = []
        for v in vs:
            if isinstance(v, V):
                for b in v.bs:
                    if b not in out:
                        out.append(b)
        return out

    @staticmethod
    def _a(v):
        return v.ap if isinstance(v, V) else v

    def mm(self, out, lhsT, rhs, start=True, stop=True):
        o_, l_, r_ = out.ap, lhsT.ap, rhs.ap
        self.op("pe", lambda e: e.matmul(o_, lhsT=l_, rhs=r_, start=start, stop=stop),
                r=self._bs(lhsT, rhs), w=self._bs(out))

    def mm_(self, out, lhsT, rhs, start=True, stop=True, skip=False):
        o_, l_, r_ = out.ap, lhsT.ap, rhs.ap
        self.op("pe", lambda e: e.matmul(o_, lhsT=l_, rhs=r_, start=start, stop=stop, skip_group_check=skip),
                r=self._bs(lhsT, rhs), w=self._bs(out))

    def tr(self, out, in_, ident):
        o_, i_, d_ = out.ap, in_.ap, ident.ap
        self.op("pe", lambda e: e.transpose(o_, i_, d_), r=self._bs(in_, ident), w=self._bs(out))

    def act(self, out, in_, func, bias=None, scale=None, accum=None, eng="act"):
        kw = {}
        if bias is not None:
            kw["bias"] = self._a(bias)
        if scale is not None:
            kw["scale"] = self._a(scale)
        if accum is not None:
            kw["accum_out"] = accum.ap
        o_, i_ = out.ap, in_.ap
        self.op("act", lambda e: e.activation(out=o_, in_=i_, func=func, **kw),
                r=self._bs(in_, bias, scale), w=self._bs(out, accum))

    def tt(self, eng, out, in0, in1, op):
        o_, a_, b_ = out.ap, in0.ap, in1.ap
        self.op(eng, lambda e: e.tensor_tensor(out=o_, in0=a_, in1=b_, op=op),
                r=self._bs(in0, in1), w=self._bs(out))

    def ts(self, eng, out, in0, s1, s2=None, op0=ALU.mult, op1=None):
        o_, a_ = out.ap, in0.ap
        s1_, s2_ = self._a(s1), self._a(s2)
        kw = {}
        if op1 is not None:
            kw["op1"] = op1
        self.op(eng, lambda e: e.tensor_scalar(out=o_, in0=a_, scalar1=s1_, scalar2=s2_, op0=op0, **kw),
                r=self._bs(in0, s1, s2), w=self._bs(out))

    def stt(self, out, in0, scalar, in1, op0, op1):
        o_, a_, b_ = out.ap, in0.ap, in1.ap
        s_ = self._a(scalar)
        self.op("dve", lambda e: e.scalar_tensor_tensor(out=o_, in0=a_, scalar=s_, in1=b_, op0=op0, op1=op1),
                r=self._bs(in0, scalar, in1), w=self._bs(out))

    def copy(self, eng, out, in_):
        o_, i_ = out.ap, in_.ap
        if eng == "act":
            self.op("act", lambda e: e.copy(out=o_, in_=i_), r=self._bs(in_), w=self._bs(out))
        else:
            self.op(eng, lambda e: e.tensor_copy(out=o_, in_=i_), r=self._bs(in_), w=self._bs(out))

    def recip(self, out, in_):
        o_, i_ = out.ap, in_.ap
        self.op("dve", lambda e: e.reciprocal(out=o_, in_=i_), r=self._bs(in_), w=self._bs(out))

    def memset(self, eng, out, val):
        o_ = out.ap
        self.op(eng, lambda e: e.memset(o_, val), r=[], w=self._bs(out))

    def dma(self, out, in_, grp, eng="sp"):
        o_, i_ = self._a(out), self._a(in_)
        self.op(eng, lambda e: e.dma_start(out=o_, in_=i_), r=self._bs(in_), w=self._bs(out), grp=grp)


class Bank:
    def __init__(self, t, q):
        self.t = t
        self.q = q

    def f(self, c0, c1, p0=0, p1=128):
        return V(self.t[p0:p1, c0:c1], self.q[0:1])

    def b(self, c0, c1, p0=0, p1=128):
        return V(self.t[p0:p1, :].bitcast(BF16)[:, c0:c1], self.q[0:1])


W1_SEGS = ((0, 1536, 0), (2048, 2056, 1536), (2568, 3336, 1544))
W1_N = 2312
C_AB, C_KC, C_VC, C_KSL, C_VSL, C_KWN, C_VWN = 1536, 1544, 1672, 1800, 1928, 2056, 2184


class Builder:
    def __init__(self, debug=False, ntiles=NT):
        self.debug = debug
        self.ntiles = ntiles
        import os
        self.stop = int(os.environ.get('K_STOP', '0'))
        self.var = int(os.environ.get('K_VAR', '0'))
        self.halo = int(os.environ.get('K_HALO', '0'))
        self.nc = bass.Bass("TRN2", target_bir_lowering=False)
        self.din = {}
        self.dout = {}

    def inp(self, name, shape, dt=F32):
        self.din[name] = self.nc.dram_tensor(name, list(shape), dt, kind="ExternalInput").ap()
        return self.din[name]

    def outp(self, name, shape, dt=F32):
        self.dout[name] = self.nc.dram_tensor(name, list(shape), dt, kind="ExternalOutput").ap()
        return self.dout[name]

    def scratch(self, name, shape, dt=F32):
        if self.debug:
            return self.outp(name, shape, dt)
        return self.nc.dram_tensor(name, list(shape), dt, kind="Internal").ap()

    def sb(self, es, name, shape, dt=F32, const=False):
        t = es.enter_context(self.nc.sbuf_tensor(name, list(shape), dt))
        return Tl(t, self.S.buf(name, const))

    def cload(self, es, name, shape, grp):
        d = self.inp(name, shape)
        t = self.sb(es, "c_" + name, shape, F32, const=True)
        idx = tuple(slice(None) for _ in shape)
        self.S.dma(t[idx], d[idx], grp)
        return t

    def build(self, upto=9):
        nc = self.nc
        I = self.inp
        self.xb = I("xb", [S_LEN, D])
        self.xo = I("xo", [2048, D])
        cT = I("cT", [128, 8])
        ada_w = I("ada_w", [D, 6 * D])
        self.w_in = I("w_in", [D, N_IN])
        self.out = self.outp("out", [2048, D])
        self.o_own = self.scratch("o_own", [16, 128, 512])
        self.kslT_d = self.scratch("kslT_d", [128, S_LEN], BF16)
        self.kwnT_d = self.scratch("kwnT_d", [128, S_LEN], BF16)
        self.kcT_d = self.scratch("kcT_d", [128, S_LEN], BF16)
        self.vcT_d = self.scratch("vcT_d", [128, S_LEN], BF16)
        self.vsl_d = self.scratch("vsl_d", [128, NB, 128], BF16)
        self.vwn_d = self.scratch("vwn_d", [128, NB, 128], BF16)
        self.mix_d = self.scratch("mix_d", [128, 8, 2048], BF16)
        self.oprev_d = self.scratch("oprev_d", [16, 2, 512])
        self.mixh_d = self.scratch("mixh_d", [128, 8, 32], BF16)
        self.w_up = I("w_up", [D, 2 * DFF])
        self.wup_d = self.scratch("wup_d", [128, 44, 8, 128], BF16)

        with ExitStack() as top:
            S = self.S = Sched(nc, top)
            self.g_const = g_const = S.group("const", full=True)
            self.g_out = S.group("outw")
            self.idf = idf = self.cload(top, "identf", [128, 128], g_const)
            self.idb = idb = self.sb(top, "idb", [128, 128], BF16)
            S.copy("dve", idb[:, :], idf[:, :])
            self.modT = modT = self.sb(top, "modT", [128, 48])
            self.s1 = s1 = self.sb(top, "s1", [128, 8])
            self.s2 = s2 = self.sb(top, "s2", [128, 8])
            n1 = self.cload(top, "n1w", [128, 8], g_const)
            n2 = self.cload(top, "n2w", [128, 8], g_const)
            self.pb = []
            for i in range(8):
                t = top.enter_context(nc.psum_tensor(f"pb{i}", [128, 512], F32))
                self.pb.append(Bank(t, [S.buf(f"pb{i}", excl=True)]))
            pb = self.pb
            self.bankA = [pb[2], pb[3]]
            self.bankB = [pb[4], pb[5]]
            self.bankC = [pb[0], pb[1]]

            with ExitStack() as es:
                ct = self.sb(es, "ct", [128, 8])
                sc = self.sb(es, "sc", [128, 8])
                abT = self.cload(es, "ada_bT", [128, 48], g_const)
                S.dma(ct[:, :], cT[:, :], g_const)
                S.act(sc[:, :], ct[:, :], AF.Silu)
                aw = [self.sb(es, f"aw{i}", [128, 6 * D]) for i in range(2)]
                g_aw = [S.group(f"aw{i}") for i in range(2)]
                for k in range(8):
                    sl = k % 2
                    for hh in range(4):
                        S.dma(aw[sl][:, hh * 1536:(hh + 1) * 1536],
                              ada_w[k * 128:(k + 1) * 128, hh * 1536:(hh + 1) * 1536], g_aw[sl])
                    pm = pb[k % 2]
                    for cc in range(48):
                        S.mm(pm.f(cc, cc + 1), lhsT=aw[sl][:, cc * 128:(cc + 1) * 128], rhs=sc[:, k:k + 1])
                    S.tt("dve", modT[:, :], pm.f(0, 48), (abT if k == 0 else modT)[:, :], ALU.add)
                S.stt(s1[:, :], modT[:, 8:16], 1.0, n1[:, :], ALU.add, ALU.mult)
                S.stt(s2[:, :], modT[:, 32:40], 1.0, n2[:, :], ALU.add, ALU.mult)
                S.flush()

            if upto >= 1:
                self.phase1()
            keep = dict(qnT=self.sb(top, "qnT", [128, 4, 2048], BF16), gates=self.sb(top, "gates", [128, 16, 12]),
                        qnTh=self.sb(top, "qnTh", [128, 4, 32], BF16), gatesh=self.sb(top, "gatesh", [32, 12]))
            if upto >= 2:
                self.phase2(keep)
                if self.debug:
                    S.dma(self.outp("d_qnT", [128, 4, 2048], BF16)[:, :, :], keep["qnT"][:, :, :], self.g_out)
                    S.dma(self.outp("d_gates", [128, 16, 12])[:, :, :], keep["gates"][:, :, :], self.g_out)
            if upto >= 3:
                S.flush()
                self.phase3(keep)
            if upto >= 4:
                S.flush()
                if self.debug:
                    self.dx1 = self.outp("d_x1", [4, 128, 4, D])
                self.phase45()
            S.flush()
        return nc

    def front(self, F, xt, nb, hT, c0, scol, sh_c0, b0=0):
        S = self.S
        pb = self.pb
        ssq, rt, rstd, junk, xs = F["ssq"], F["rt"], F["rstd"], F["junk"], F["xs"]
        for b2 in range(nb):
            S.act(junk[:, :], xt[:, b0 + b2, :], AF.Square, accum=ssq[:, b2:b2 + 1])
        S.act(rt[:, 0:nb], ssq[:, 0:nb], AF.Ln, bias=EPS, scale=1.0 / D)
        S.act(rstd[:, 0:nb], rt[:, 0:nb], AF.Exp, scale=-0.5)
        for b2 in range(nb):
            S.ts("pool", xs[:, b2, :], xt[:, b0 + b2, :], rstd[:, b2:b2 + 1], 1.0, ALU.mult, ALU.mult)
        w = nb * 128
        for half in range(2):
            bank = pb[half]
            for k in range(half * 4, half * 4 + 4):
                off = (k % 4) * 256
                for b2 in range(nb):
                    S.tr(bank.b(off + b2 * 128, off + (b2 + 1) * 128), xs[:, b2, k * 128:(k + 1) * 128],
                         self.idb[:, :])
            for k in range(half * 4, half * 4 + 4):
                off = (k % 4) * 256
                S.ts("dve", hT[:, k, c0:c0 + w], bank.b(off, off + w), scol[:, k:k + 1],
                     self.modT[:, sh_c0 + k:sh_c0 + k + 1], ALU.mult, ALU.add)

    def front32(self, F, xt, hT, scol, sh_c0):
        S = self.S
        bank = self.pb[0]
        ssq, rt, rstd, junk, xs = F["ssq"], F["rt"], F["rstd"], F["junk"], F["xs"]
        S.act(junk[0:32, :], xt[:, :], AF.Square, accum=ssq[0:32, 0:1])
        S.act(rt[0:32, 0:1], ssq[0:32, 0:1], AF.Ln, bias=EPS, scale=1.0 / D)
        S.act(rstd[0:32, 0:1], rt[0:32, 0:1], AF.Exp, scale=-0.5)
        S.ts("pool", xs[0:32, 0, :], xt[:, :], rstd[0:32, 0:1], 1.0, ALU.mult, ALU.mult)
        for k in range(8):
            S.tr(bank.b(k * 32, (k + 1) * 32), xs[0:32, 0, k * 128:(k + 1) * 128], self.idb[0:32, 0:32])
        for k in range(8):
            S.ts("dve", hT[:, k, :], bank.b(k * 32, (k + 1) * 32), scol[:, k:k + 1],
                 self.modT[:, sh_c0 + k:sh_c0 + k + 1], ALU.mult, ALU.add)

    def phase1(self):
        S = self.S
        nc = self.nc
        pb = self.pb
        gc_ = S.group("const1", full=True)
        idb, idf = self.idb, self.idf
        with ExitStack() as es:
            tri2 = self.cload(es, "tri2", [128, 128], gc_)
            blk2 = self.cload(es, "blk2", [128, 128], gc_)
            onesf = self.cload(es, "onesf", [128, 128], gc_)
            cind = self.cload(es, "cind", [128, 2], gc_)
            ma4 = self.cload(es, "ma4", [128, 512], gc_)
            mb4 = self.cload(es, "mb4", [128, 512], gc_)
            sel = self.cload(es, "sel", [128, 4], gc_)
            hsel = self.cload(es, "hsel", [128, 4], gc_)
            cw = self.cload(es, "gcw", [128, 48], gc_)
            dtb = self.cload(es, "dtb", [128, 4], gc_)
            alog = self.cload(es, "alog", [128, 4], gc_)
            kslw = self.cload(es, "kslw", [128, 1], gc_)
            kwnw = self.cload(es, "kwnw", [128, 1], gc_)
            negones = self.sb(es, "negones", [128, 128])
            S.ts("dve", negones[:, :], onesf[:, :], -1.0, None, ALU.mult)
            onesb = self.sb(es, "onesb", [128, 128], BF16)
            S.copy("dve", onesb[:, :], onesf[:, :])
            i4b = self.sb(es, "i4b", [128, 512], BF16)
            for h in range(4):
                S.copy("dve", i4b[:, h * 128:(h + 1) * 128], idf[:, :])
            negA = self.sb(es, "negA", [128, 4])
            S.act(negA[:, :], alog[:, :], AF.Exp)
            S.ts("dve", negA[:, :], negA[:, :], -1.0, None, ALU.mult)

            winb = self.sb(es, "winb", [128, 8, W1_N], BF16)
            with ExitStack() as es2:
                wst = [self.sb(es2, f"wst{i}", [128, W1_N]) for i in range(2)]
                g_w = [S.group(f"wst{i}") for i in range(2)]
                for k in range(8):
                    sl = k % 2
                    for (a, b_, o) in W1_SEGS:
                        S.dma(wst[sl][:, o:o + (b_ - a)], self.w_in[k * 128:(k + 1) * 128, a:b_], g_w[sl])
                    S.copy("dve", winb[:, k, 0:1024], wst[sl][:, 0:1024])
                    S.copy("pool", winb[:, k, 1024:W1_N], wst[sl][:, 1024:W1_N])
                S.flush()

            F = dict(ssq=self.sb(es, "f_ssq", [128, 4]), rt=self.sb(es, "f_rt", [128, 4]),
                     rstd=self.sb(es, "f_rstd", [128, 4]), junk=self.sb(es, "f_junk", [128, 1024], BF16),
                     xs=self.sb(es, "f_xs", [128, 2, 1024], BF16))
            xt = [self.sb(es, f"xt{i}", [128, 2, 1024]) for i in range(2)]
            g_x = [S.group(f"xt{i}") for i in range(2)]
            hT = self.sb(es, "hT", [128, 8, 512], BF16)
            pre = [self.sb(es, f"pre{i}", [128, 515]) for i in range(3)]
            hist = self.sb(es, "hist", [128, 12, 3])
            S.memset("pool", hist[:, :, :], 0.0)
            acc = [self.sb(es, f"cacc{i}", [128, 512]) for i in range(3)]
            yfs = [self.sb(es, f"yfs{i}", [128, 512], BF16 if i < 8 else F32) for i in range(10)]
            sq = [self.sb(es, f"sq{i}", [128, 512], BF16) for i in range(3)]
            rtt = [self.sb(es, f"rtt{i}", [128, 512]) for i in range(3)]
            qT = self.sb(es, "qT", [128, 4, 512], BF16)
            kT = self.sb(es, "kT", [128, 4, 512], BF16)
            vT = self.sb(es, "vT", [128, 4, 512], BF16)
            st_f = [[self.sb(es, f"stf{c}_{i}", [128, 512], BF16) for i in range(2)] for c in range(4)]
            st_v = [[self.sb(es, f"stv{c}_{i}", [128, 4, 128], BF16) for i in range(2)] for c in range(2)]
            g_sf = [[S.group(f"sf{c}_{i}") for i in range(2)] for c in range(4)]
            g_sv = [[S.group(f"sv{c}_{i}") for i in range(2)] for c in range(2)]
            ab = self.sb(es, "ab", [128, 4, 8])
            self.sm4 = sm4 = self.sb(es, "sm4", [128, 4, 64])
            S32 = self.sb(es, "S32", [128, 4, 128])
            Sb = [self.sb(es, f"Sb{i}", [128, 4, 128], BF16) for i in range(2)]
            self.cidx = 0
            S.memset("pool", S32[:, :, :], 0.0)
            S.memset("pool", Sb[0][:, :, :], 0.0)
            oacc = [self.sb(es, f"oacc{i}", [128, 512]) for i in range(2)]
            g_o = [S.group(f"oacc{i}") for i in range(2)]
            oph = [self.sb(es, f"oph{i}", [128, 512]) for i in range(2)]
            g_oh = [S.group(f"oph{i}") for i in range(2)]
            G = []
            for s_ in range(2):
                g = {}
                for nm, shp, dt in (("TG", [128, 4, 128], F32), ("Xs", [128, 512], F32),
                                    ("XA", [128, 512], F32), ("XB", [128, 512], F32), ("EG", [128, 4, 128], F32),
                                    ("Nb", [128, 4, 128], BF16), ("aT", [128, 4, 128], BF16),
                                    ("qdT", [128, 4, 128], BF16), ("kw", [128, 4, 128], BF16),
                                    ("kd", [128, 4, 128], BF16), ("vb", [128, 4, 128], BF16),
                                    ("NTb", [128, 4, 128], BF16), ("RTb", [128, 4, 128], BF16),
                                    ("P0", [128, 4, 128], BF16), ("P1", [128, 4, 128], BF16),
                                    ("Q0", [128, 4, 128], BF16), ("Q1", [128, 4, 128], BF16),
                                    ("u", [128, 4, 128], F32), ("wTb", [128, 4, 128], BF16),
                                    ("vn", [128, 4, 128], BF16)):
                    g[nm] = self.sb(es, f"g{s_}_{nm}", shp, dt)
                G.append(g)

            xrows = self.xb.rearrange("(n b p) d -> n p b d", b=2, p=128)

            def load_x(n):
                S.dma(xt[n % 2][:, :, :], xrows[n], g_x[n % 2])

            load_x(0)
            for ti in range(self.ntiles):
                for hf in range(2):
                    n = 2 * ti + hf
                    if n + 1 < 2 * NT:
                        load_x(n + 1)
                    self.front(F, xt[n % 2], 2, hT, hf * 256, self.s1, 0)
                if self.stop == 1:
                    continue
                nsa_ch = ((C_KC, self.kcT_d, None), (C_VC, self.vcT_d, None), (C_KSL, self.kslT_d, kslw),
                          (C_KWN, self.kwnT_d, kwnw))

                def st1(c, pos):
                    bank = pb[2 + (pos % 2)]
                    coff = c * 128 if c < 12 else nsa_ch[c - 12][0]
                    for k in range(8):
                        S.mm(bank.f(0, 512), lhsT=winb[:, k, coff:coff + 128], rhs=hT[:, k, :],
                             start=(k == 0), stop=(k == 7))
                    if c < 12:
                        p_ = pre[pos % 3]
                        S.copy("pool", p_[:, 0:3], hist[:, c, :])
                        S.copy("act", p_[:, 3:515], bank.f(0, 512))
                        S.copy("pool", hist[:, c, :], p_[:, 512:515])
                        a_ = acc[pos % 3]
                        S.ts("pool", a_[:, :], p_[:, 0:512], cw[:, c * 4:c * 4 + 1], 1.0, ALU.mult, ALU.mult)
                        for j in range(1, 4):
                            S.stt(a_[:, :], p_[:, j:j + 512], cw[:, c * 4 + j:c * 4 + j + 1], a_[:, :], ALU.mult, ALU.add)
                    else:
                        ci = c - 12
                        wcol = nsa_ch[ci][2]
                        if wcol is None:
                            S.copy("act", st_f[ci][ti % 2][:, :], bank.f(0, 512))
                            S.dma(nsa_ch[ci][1][:, ti * 512:(ti + 1) * 512], st_f[ci][ti % 2][:, :], g_sf[ci][ti % 2])
                        else:
                            S.copy("act", yfs[c - 6][:, :], bank.f(0, 512))

                def st2(c, pos):
                    h = c % 4
                    if c >= 12:
                        return
                    if c >= 8:
                        S.act(vT[:, h, :], acc[pos % 3][:, :], AF.Silu)
                        return
                    S.act(yfs[c][:, :], acc[pos % 3][:, :], AF.Silu)

                def st3(c, i_):
                    h = c % 4
                    bk2 = pb[4 + (i_ % 2)]
                    yi = c if c < 8 else c - 6
                    S.act(sq[i_ % 3][:, :], yfs[yi][:, :], AF.Square)
                    S.mm(bk2.f(0, 512), lhsT=onesb[:, :], rhs=sq[i_ % 3][:, :])
                    r_ = rtt[i_ % 3]
                    if c < 4:
                        S.act(r_[:, :], bk2.f(0, 512), AF.Ln, bias=EPS * 128.0, scale=128.0)
                    elif c < 8:
                        S.act(r_[:, :], bk2.f(0, 512), AF.Ln, bias=EPS, scale=1.0)
                    else:
                        S.act(r_[:, :], bk2.f(0, 512), AF.Ln, bias=EPS, scale=1.0 / 128.0)
                    S.act(r_[:, :], r_[:, :], AF.Exp, scale=-0.5)
                    if c < 8:
                        S.tt("pool", (qT if c < 4 else kT)[:, h, :], yfs[yi][:, :], r_[:, :], ALU.mult)
                    else:
                        ci = c - 12
                        st = st_f[ci][ti % 2]
                        S.stt(st[:, :], yfs[yi][:, :], nsa_ch[ci][2][:, 0:1], r_[:, :], ALU.mult, ALU.mult)
                        S.dma(nsa_ch[ci][1][:, ti * 512:(ti + 1) * 512], st[:, :], g_sf[ci][ti % 2])

                order = [12, 13, 14, 15] + list(range(12))
                for i in range(len(order) + 1):
                    if i < len(order):
                        st1(order[i], i)
                    if 0 <= i - 1 < len(order):
                        st2(order[i - 1], i - 1)
                if self.stop == 3:
                    continue
                for blk in range(4):
                    bank = pb[6]
                    for vi, coff in enumerate((C_VSL, C_VWN)):
                        for k in range(8):
                            if self.var == 4:
                                break
                            S.mm(bank.f(vi * 128, (vi + 1) * 128), lhsT=hT[:, k, blk * 128:(blk + 1) * 128],
                                 rhs=winb[:, k, coff:coff + 128], start=(k == 0), stop=(k == 7))
                    for k in range(8):
                        if self.var == 1:
                            break
                        S.mm(bank.f(256, 264), lhsT=hT[:, k, blk * 128:(blk + 1) * 128],
                             rhs=winb[:, k, C_AB:C_AB + 8], start=(k == 0), stop=(k == 7))
                    if self.var != 3:
                        S.copy("act", st_v[0][ti % 2][:, blk, :], bank.f(0, 128))
                        S.copy("act", st_v[1][ti % 2][:, blk, :], bank.f(128, 256))
                    if self.var != 5:
                        S.copy("dve", ab[:, blk, :], bank.f(256, 264))
                if self.var != 2:
                    S.dma(self.vsl_d[:, ti * 4:(ti + 1) * 4, :], st_v[0][ti % 2][:, :, :], g_sv[0][ti % 2])
                    S.dma(self.vwn_d[:, ti * 4:(ti + 1) * 4, :], st_v[1][ti % 2][:, :, :], g_sv[1][ti % 2])

                for i_, c in enumerate([14, 15, 0, 1, 2, 3, 4, 5, 6, 7]):
                    st3(c, i_)
                if self.stop == 4:
                    continue
                for pair in range(2):
                    blks = (2 * pair, 2 * pair + 1)
                    if pair == 0:
                        self.gdn_small(sm4, ab, tri2, blk2, onesf, cind, dtb, negA)
                    for s_, blk in enumerate(blks):
                        self.gdn_local_1(G[s_], s_, blk, sm4, tri2, negones, onesf, ma4, mb4, qT, kT, vT)
                    if self.stop == 5:
                        continue
                    self.gdn_solve(G, blks, i4b)
                    if self.stop == 6:
                        continue
                    for s_, blk in enumerate(blks):
                        self.gdn_uw(G[s_], s_)
                    if self.stop == 7:
                        continue
                    for s_, blk in enumerate(blks):
                        oa = oacc[ti % 2]
                        self.gdn_recur(G[s_], S32, Sb, sel, blk, oa, hsel, oph[ti % 2])
                S.dma(self.o_own[ti], oacc[ti % 2][:, :], g_o[ti % 2])
                S.dma(self.oprev_d[ti], oph[ti % 2][126:128, :], g_oh[ti % 2])
            S.flush()

    def gdn_small(self, sm4, ab, tri2, blk2, onesf, cind, dtb, negA):
        S = self.S
        bS = self.pb[6]
        x_, t_, gg, be, nb_ = (sm4[:, :, 0:4], sm4[:, :, 4:8], sm4[:, :, 8:12], sm4[:, :, 12:16], sm4[:, :, 16:20])
        gci, gcv, gl, egc, ekd, bw, glS = (sm4[:, :, 20:28], sm4[:, :, 28:32], sm4[:, :, 32:36], sm4[:, :, 36:40],
                                           sm4[:, :, 40:44], sm4[:, :, 44:48], sm4[:, :, 48:56])
        bc = lambda t: V(t.t[:, :].unsqueeze(1).broadcast_to([128, 4, 4]), [t.b])
        S.tt("dve", x_, ab[:, :, 0:4], bc(dtb), ALU.add)
        S.stt(t_, x_, -1.0, x_, ALU.mult, ALU.max)
        S.act(t_, t_, AF.Exp, scale=-1.0)
        S.act(t_, t_, AF.Ln, bias=1.0)
        S.stt(t_, x_, 0.0, t_, ALU.max, ALU.add)
        S.tt("dve", gg, t_, bc(negA), ALU.mult)
        S.act(be, ab[:, :, 4:8], AF.Exp, scale=-1.0)
        S.ts("dve", be, be, 1.0, None, ALU.add)
        S.recip(be, be)
        S.ts("dve", nb_, be, -1.0, None, ALU.mult)
        for i in range(2):
            for blk in range(4):
                S.ts("dve", sm4[:, blk, 20:28].re("p (h i) -> p h i", i=2)[:, :, i], sm4[:, blk, 8:12], cind[:, i:i + 1],
                     None, ALU.mult)
        for blk in range(4):
            S.mm(bS.f(384 + blk * 4, 388 + blk * 4), lhsT=tri2[:, :], rhs=sm4[:, blk, 8:12])
            S.mm(bS.f(400 + blk * 4, 404 + blk * 4), lhsT=blk2[:, :], rhs=sm4[:, blk, 8:12])
            S.mm(bS.f(416 + blk * 8, 424 + blk * 8), lhsT=onesf[:, :], rhs=sm4[:, blk, 20:28])
        S.copy("dve", gcv, bS.f(384, 400).re("p (b h) -> p b h", b=4))
        S.copy("dve", gl, bS.f(400, 416).re("p (b h) -> p b h", b=4))
        S.act(glS, bS.f(416, 448).re("p (b c) -> p b c", b=4), AF.Exp)
        S.act(egc, gcv, AF.Exp)
        S.tt("dve", ekd, gl, gcv, ALU.subtract)
        S.act(ekd, ekd, AF.Exp)
        S.tt("dve", bw, be, egc, ALU.mult)

    def gdn_local_1(self, g, s_, blk, sm4, tri2, negones, onesf, ma4, mb4, qT, kT, vT):
        S = self.S
        pb = self.pb
        cs = slice(blk * 128, (blk + 1) * 128)
        sm = V(sm4.t[:, blk, :], [sm4.b])
        gg, be, nb_, gcv, ekd, bw = (sm[:, 8:12], sm[:, 12:16], sm[:, 16:20], sm[:, 28:32], sm[:, 40:44], sm[:, 44:48])
        bK = self.bankA[s_]
        bQ = self.bankB[s_]
        bX = self.bankC[s_]
        bT = pb[7]
        for h in range(4):
            S.mm(bK.f(h * 128, (h + 1) * 128), lhsT=kT[:, h, cs], rhs=kT[:, h, cs])
        for h in range(4):
            S.mm(bQ.f(h * 128, (h + 1) * 128), lhsT=kT[:, h, cs], rhs=qT[:, h, cs])
        for h in range(4):
            S.tr(bT.b(h * 128, (h + 1) * 128), kT[:, h, cs], self.idb[:, :])
        for h in range(4):
            S.tr(bT.b(512 + h * 128, 512 + (h + 1) * 128), vT[:, h, cs], self.idb[:, :])
        for h in range(4):
            S.ts("pool", g["TG"][:, h, :], tri2[:, :], gg[:, h:h + 1], 1.0, ALU.mult, ALU.mult)
        for h in range(4):
            S.mm(bX.f(h * 128, (h + 1) * 128), lhsT=onesf[:, :], rhs=g["TG"][:, h, :], start=True, stop=False)
            S.mm(bX.f(h * 128, (h + 1) * 128), lhsT=g["TG"][:, h, :], rhs=negones[:, :], start=False, stop=True)
        S.copy("act", g["Xs"][:, :], bX.f(0, 512))
        for h in range(4):
            S.act(g["EG"][:, h, :], bX.f(h * 128, (h + 1) * 128), AF.Exp, bias=gcv[:, h:h + 1])
        S.tt("pool", g["XA"][:, :], g["Xs"][:, :], ma4[:, :], ALU.add)
        S.tt("pool", g["XB"][:, :], g["Xs"][:, :], mb4[:, :], ALU.add)
        S.act(g["XA"][:, :], g["XA"][:, :], AF.Exp, scale=-1.0)
        S.act(g["XB"][:, :], g["XB"][:, :], AF.Exp)
        for h in range(4):
            S.stt(g["Nb"][:, h, :], bK.f(h * 128, (h + 1) * 128), nb_[:, h:h + 1],
                  g["XA"][:, h * 128:(h + 1) * 128], ALU.mult, ALU.mult)
        S.tt("dve", g["aT"][:, :, :].re("p h c -> p (h c)"), bQ.f(0, 512), g["XB"][:, :], ALU.mult)
        S.tt("pool", g["qdT"][:, :, :], qT[:, :, cs], g["EG"][:, :, :], ALU.mult)
        kt3 = bT.b(0, 512).re("p (h d) -> p h d", h=4)
        vt3 = bT.b(512, 1024).re("p (h d) -> p h d", h=4)
        S.tt("dve", g["kw"][:, :, :], kt3, V(bw.ap.unsqueeze(2).broadcast_to([128, 4, 128]), bw.bs), ALU.mult)
        S.tt("dve", g["kd"][:, :, :], kt3, V(ekd.ap.unsqueeze(2).broadcast_to([128, 4, 128]), ekd.bs), ALU.mult)
        S.tt("dve", g["vb"][:, :, :], vt3, V(be.ap.unsqueeze(2).broadcast_to([128, 4, 128]), be.bs), ALU.mult)

    def gdn_solve(self, G, blks, i4b):
        S = self.S
        idb = self.idb
        for s_ in range(len(blks)):
            g = G[s_]
            bA, bC = self.bankA[s_], self.bankC[s_]
            for h in range(4):
                S.mm(bA.f(h * 128, (h + 1) * 128), lhsT=g["Nb"][:, h, :], rhs=idb[:, :])
            S.copy("act", g["NTb"][:, :, :].re("p h c -> p (h c)"), bA.f(0, 512))
            S.mm(bC.f(0, 512), lhsT=idb[:, :], rhs=i4b[:, :], start=True, stop=False)
            for h in range(4):
                S.mm(bC.f(h * 128, (h + 1) * 128), lhsT=g["Nb"][:, h, :], rhs=idb[:, :], start=False, stop=(h == 3))
            S.copy("act", g["RTb"][:, :, :].re("p h c -> p (h c)"), bC.f(0, 512))
        P = [G[s_]["Nb"] for s_ in range(len(blks))]
        Q = [G[s_]["NTb"] for s_ in range(len(blks))]
        for k in range(1, 6):
            for s_ in range(len(blks)):
                g = G[s_]
                bA, bB, bC = self.bankA[s_], self.bankB[s_], self.bankC[s_]
                Pn = g["P%d" % (k % 2)]
                Qn = g["Q%d" % (k % 2)]
                for h in range(4):
                    S.mm(bA.f(h * 128, (h + 1) * 128), lhsT=Q[s_][:, h, :], rhs=P[s_][:, h, :])
                if k < 5:
                    for h in range(4):
                        S.mm(bB.f(h * 128, (h + 1) * 128), lhsT=P[s_][:, h, :], rhs=Q[s_][:, h, :])
                S.copy("dve", Pn[:, :, :].re("p h c -> p (h c)"), bA.f(0, 512))
                if k < 5:
                    S.copy("act", Qn[:, :, :].re("p h c -> p (h c)"), bB.f(0, 512))
                S.mm(bC.f(0, 512), lhsT=idb[:, :], rhs=g["RTb"][:, :, :].re("p h c -> p (h c)"), start=True, stop=False)
                for h in range(4):
                    S.mm(bC.f(h * 128, (h + 1) * 128), lhsT=Pn[:, h, :], rhs=g["RTb"][:, h, :],
                         start=False, stop=(h == 3))
                S.copy("act", g["RTb"][:, :, :].re("p h c -> p (h c)"), bC.f(0, 512))
                P[s_] = Pn
                Q[s_] = Qn

    def gdn_uw(self, g, s_):
        S = self.S
        bA, bB = self.bankA[s_], self.bankB[s_]
        for h in range(4):
            S.mm(bA.f(h * 128, (h + 1) * 128), lhsT=g["RTb"][:, h, :], rhs=g["vb"][:, h, :])
        for h in range(4):
            S.mm(bB.f(h * 128, (h + 1) * 128), lhsT=g["kw"][:, h, :], rhs=g["RTb"][:, h, :])
        S.copy("act", g["u"][:, :, :].re("p h c -> p (h c)"), bA.f(0, 512))
        S.copy("dve", g["wTb"][:, :, :].re("p h c -> p (h c)"), bB.f(0, 512))

    def gdn_recur(self, g, S32, Sb, sel, blk, oa, hsel, oh):
        S = self.S
        pb = self.pb
        bV, bO, bS_ = pb[2], pb[3], pb[4]
        sm = V(self.sm4.t[:, blk % 4, :], [self.sm4.b])
        for i in range(2):
            r0, r1 = 64 * i, 64 * i + 64
            cur = Sb[self.cidx % 2]
            nxt = Sb[(self.cidx + 1) % 2]
            self.cidx += 1
            for h in range(4):
                S.mm(bV.f(h * 128, (h + 1) * 128, r0, r1), lhsT=g["wTb"][:, h, r0:r1], rhs=cur[:, h, :])
            S.tt("dve", g["vn"][r0:r1, :, :].re("p h c -> p (h c)"), g["u"][r0:r1, :, :].re("p h c -> p (h c)"),
                 bV.f(0, 512, r0, r1), ALU.subtract)
            for h in range(4):
                S.mm(bS_.f(h * 128, (h + 1) * 128), lhsT=g["kd"][r0:r1, h, :], rhs=g["vn"][r0:r1, h, :])
            for h in range(4):
                S.stt(S32[:, h, :], S32[:, h, :], sm[:, 48 + 2 * h + i:49 + 2 * h + i], bS_.f(h * 128, (h + 1) * 128),
                      ALU.mult, ALU.add)
            S.copy("act", nxt[:, :, :], S32[:, :, :])
            for h in range(4):
                S.mm(bO.f(h * 128, (h + 1) * 128, r0, r1), lhsT=g["qdT"][:, h, r0:r1], rhs=cur[:, h, :],
                     start=True, stop=False)
                S.mm(bO.f(h * 128, (h + 1) * 128, r0, r1), lhsT=g["aT"][r0:r1, h, r0:r1], rhs=g["vn"][r0:r1, h, :],
                     start=False, stop=True)
        r_ = blk % 4
        if r_ == 0:
            S.ts("dve", oa[:, :], bO.f(0, 512), sel[:, 0:1], None, ALU.mult)
        else:
            S.stt(oa[:, :], bO.f(0, 512), sel[:, r_:r_ + 1], oa[:, :], ALU.mult, ALU.add)
        if r_ == 0:
            S.ts("dve", oh[64:128, :], bO.f(0, 512, 64, 128), hsel[64:128, 0:1], None, ALU.mult)
        else:
            S.stt(oh[64:128, :], bO.f(0, 512, 64, 128), hsel[64:128, r_:r_ + 1], oh[64:128, :], ALU.mult, ALU.add)

    def phase2(self, keep):
        S = self.S
        pb = self.pb
        gc_ = S.group("const2", full=True)
        qnT, gates = keep["qnT"], keep["gates"]
        with ExitStack() as es:
            qnw = self.cload(es, "qnw", [128, 1], gc_)
            gonw4 = self.cload(es, "gonw4", [128, 512], gc_)
            onesf = self.cload(es, "onesf2", [128, 128], gc_)
            onesb = self.sb(es, "onesb2", [128, 128], BF16)
            S.copy("dve", onesb[:, :], onesf[:, :])
            winb2 = self.sb(es, "winb2", [128, 8, 1036], BF16)
            with ExitStack() as es2:
                wst = [self.sb(es2, f"w2st{i}", [128, 1036]) for i in range(2)]
                g_w = [S.group(f"w2st{i}") for i in range(2)]
                for k in range(8):
                    sl = k % 2
                    for (a, b_, o) in ((O_GZ, O_GZ + 512, 0), (O_NQ, O_NQ + 512, 512), (O_NG, O_NG + 12, 1024)):
                        S.dma(wst[sl][:, o:o + (b_ - a)], self.w_in[k * 128:(k + 1) * 128, a:b_], g_w[sl])
                    S.copy("dve" if k % 2 else "pool", winb2[:, k, :], wst[sl][:, :])
                S.flush()
            F = dict(ssq=self.sb(es, "f2_ssq", [128, 4]), rt=self.sb(es, "f2_rt", [128, 4]),
                     rstd=self.sb(es, "f2_rstd", [128, 4]), junk=self.sb(es, "f2_junk", [128, 1024], BF16),
                     xs=self.sb(es, "f2_xs", [128, 2, 1024], BF16))
            xt = [self.sb(es, f"x2t{i}", [128, 2, 1024]) for i in range(2)]
            g_x = [S.group(f"x2t{i}") for i in range(2)]
            hT = self.sb(es, "h2T_", [128, 8, 512], BF16)
            yf = [self.sb(es, f"y2f{i}", [128, 512]) for i in range(2)]
            sq = [self.sb(es, f"s2q{i}", [128, 512], BF16) for i in range(2)]
            rtt = [self.sb(es, f"r2tt{i}", [128, 512]) for i in range(2)]
            og = [self.sb(es, f"og{i}", [128, 512]) for i in range(2)]
            g_og = [S.group(f"og{i}") for i in range(2)]
            zs = [self.sb(es, f"zs{i}", [128, 512]) for i in range(4)]
            t1 = [self.sb(es, f"p2t{i}", [128, 512]) for i in range(2)]
            mg = [self.sb(es, f"mg{i}", [128, 512], BF16) for i in range(2)]
            mgT = [self.sb(es, f"mgT{i}", [128, 4, 128], BF16) for i in range(2)]
            g_mg = [S.group(f"mgT{i}") for i in range(2)]
            osq = self.sb(es, "osq", [128, 8])
            xrows = self.xo.rearrange("(n b p) d -> n p b d", b=2, p=128)

            def load_x(n):
                S.dma(xt[n % 2][:, :, :], xrows[n], g_x[n % 2])

            load_x(0)
            for t in range(4):
                for hf in range(2):
                    n = 2 * t + hf
                    if n + 1 < 8:
                        load_x(n + 1)
                    self.front(F, xt[n % 2], 2, hT, hf * 256, self.s1, 0)
                for h in range(4):
                    bank = pb[2 + (h % 2)]
                    for k in range(8):
                        S.mm(bank.f(0, 512), lhsT=winb2[:, k, 512 + h * 128:512 + (h + 1) * 128], rhs=hT[:, k, :],
                             start=(k == 0), stop=(k == 7))
                    y_ = yf[h % 2]
                    S.copy("act", y_[:, :], bank.f(0, 512))
                    S.act(sq[h % 2][:, :], bank.f(0, 512), AF.Square)
                    bk2 = pb[4 + (h % 2)]
                    S.mm(bk2.f(0, 512), lhsT=onesb[:, :], rhs=sq[h % 2][:, :])
                    r_ = rtt[h % 2]
                    S.act(r_[:, :], bk2.f(0, 512), AF.Ln, bias=EPS, scale=1.0 / 128.0)
                    S.act(r_[:, :], r_[:, :], AF.Exp, scale=-0.5)
                    S.stt(qnT[:, h, t * 512:(t + 1) * 512], y_[:, :], qnw[:, 0:1], r_[:, :], ALU.mult, ALU.mult)
                for blk in range(4):
                    cs = slice(blk * 128, (blk + 1) * 128)
                    bz = pb[7]
                    for k in range(8):
                        S.mm(bz.f(0, 512), lhsT=hT[:, k, cs], rhs=winb2[:, k, 0:512], start=(k == 0), stop=(k == 7))
                    S.act(zs[blk][:, :], bz.f(0, 512), AF.Silu)
                for blk in range(4):
                    j = 4 * t + blk
                    cs = slice(blk * 128, (blk + 1) * 128)
                    bg = pb[6]
                    for k in range(8):
                        S.mm(bg.f(0, 12), lhsT=hT[:, k, cs], rhs=winb2[:, k, 1024:1036], start=(k == 0), stop=(k == 7))
                    S.act(gates[:, j, :], bg.f(0, 12), AF.Exp, scale=-1.0)
                    S.ts("dve", gates[:, j, :], gates[:, j, :], 1.0, None, ALU.add)
                    S.recip(gates[:, j, :], gates[:, j, :])
                    z_ = zs[blk]
                    o_ = og[j % 2]
                    S.dma(o_[:, :], self.o_own[j], g_og[j % 2])
                    for h in range(4):
                        S.act(t1[j % 2][:, h * 128:(h + 1) * 128], o_[:, h * 128:(h + 1) * 128], AF.Square,
                              accum=osq[:, h:h + 1])
                    S.act(osq[:, 4:8], osq[:, 0:4], AF.Ln, bias=EPS, scale=1.0 / 128.0)
                    S.act(osq[:, 4:8], osq[:, 4:8], AF.Exp, scale=-0.5)
                    rb = V(osq.t[:, 4:8].unsqueeze(2).broadcast_to([128, 4, 128]), [osq.b])
                    S.tt("dve", t1[j % 2][:, :].re("p (h d) -> p h d", h=4), o_[:, :].re("p (h d) -> p h d", h=4), rb,
                         ALU.mult)
                    S.tt("pool", t1[j % 2][:, :], t1[j % 2][:, :], gonw4[:, :], ALU.mult)
                    S.tt("dve", mg[j % 2][:, :], t1[j % 2][:, :], z_[:, :], ALU.mult)
                    bt = pb[0]
                    for c in range(4):
                        S.tr(bt.b(c * 128, (c + 1) * 128), mg[j % 2][:, c * 128:(c + 1) * 128], self.idb[:, :])
                    S.copy("act", mgT[j % 2][:, :, :].re("p c t -> p (c t)"), bt.b(0, 512))
                    S.dma(self.mix_d[:, 0:4, j * 128:(j + 1) * 128], mgT[j % 2][:, :, :], g_mg[j % 2])
            gh = S.group("p2halo", full=True)
            selAB = self.cload(es, "selAB", [128, 2], gh)
            xht = self.sb(es, "xht", [32, D])
            S.dma(xht[:, :], self.inp("xh", [32, D])[:, :], gh)
            ca = self.sb(es, "hca", [32, 512])
            cb = self.sb(es, "hcb", [32, 512])
            S.memset("pool", cb[0:2, :], 0.0)
            opv = self.oprev_d.rearrange("j i c -> (j i) c")
            S.dma(ca[:, :], opv[0:32, :], gh)
            S.dma(cb[2:32, :], opv[0:30, :], gh)
            hTh = self.sb(es, "hTh", [128, 8, 32], BF16)
            self.front32(F, xht, hTh, self.s1, 0)
            qnTh, gatesh = keep["qnTh"], keep["gatesh"]
            bq = pb[2]
            for h in range(4):
                for k in range(8):
                    S.mm(bq.f(h * 32, (h + 1) * 32), lhsT=winb2[:, k, 512 + h * 128:512 + (h + 1) * 128], rhs=hTh[:, k, :],
                         start=(k == 0), stop=(k == 7))
            S.copy("act", yf[0][:, 0:128], bq.f(0, 128))
            S.act(sq[0][:, 0:128], bq.f(0, 128), AF.Square)
            S.mm(pb[4].f(0, 128), lhsT=onesb[:, :], rhs=sq[0][:, 0:128])
            S.act(rtt[0][:, 0:128], pb[4].f(0, 128), AF.Ln, bias=EPS, scale=1.0 / 128.0)
            S.act(rtt[0][:, 0:128], rtt[0][:, 0:128], AF.Exp, scale=-0.5)
            S.stt(qnTh[:, :, :].re("p h q -> p (h q)"), yf[0][:, 0:128], qnw[:, 0:1], rtt[0][:, 0:128], ALU.mult, ALU.mult)
            for k in range(8):
                S.mm(pb[6].f(0, 12, 0, 32), lhsT=hTh[:, k, :], rhs=winb2[:, k, 1024:1036], start=(k == 0), stop=(k == 7))
            S.act(gatesh[:, :], pb[6].f(0, 12, 0, 32), AF.Exp, scale=-1.0)
            S.ts("dve", gatesh[:, :], gatesh[:, :], 1.0, None, ALU.add)
            S.recip(gatesh[:, :], gatesh[:, :])
            for k in range(8):
                S.mm(pb[7].f(0, 512, 0, 32), lhsT=hTh[:, k, :], rhs=winb2[:, k, 0:512], start=(k == 0), stop=(k == 7))
            S.act(zs[0][0:32, :], pb[7].f(0, 512, 0, 32), AF.Silu)
            S.ts("dve", ca[:, :], ca[:, :], selAB[0:32, 0:1], None, ALU.mult)
            S.stt(ca[:, :], cb[:, :], selAB[0:32, 1:2], ca[:, :], ALU.mult, ALU.add)
            for h in range(4):
                S.act(t1[0][0:32, h * 128:(h + 1) * 128], ca[:, h * 128:(h + 1) * 128], AF.Square, accum=osq[0:32, h:h + 1])
            S.act(osq[0:32, 4:8], osq[0:32, 0:4], AF.Ln, bias=EPS, scale=1.0 / 128.0)
            S.act(osq[0:32, 4:8], osq[0:32, 4:8], AF.Exp, scale=-0.5)
            rb = V(osq.t[0:32, 4:8].unsqueeze(2).broadcast_to([32, 4, 128]), [osq.b])
            S.tt("dve", t1[0][0:32, :].re("p (h d) -> p h d", h=4), ca[:, :].re("p (h d) -> p h d", h=4), rb, ALU.mult)
            S.tt("pool", t1[0][0:32, :], t1[0][0:32, :], gonw4[0:32, :], ALU.mult)
            S.tt("dve", mg[0][0:32, :], t1[0][0:32, :], zs[0][0:32, :], ALU.mult)
            for c in range(4):
                S.tr(pb[0].b(c * 32, (c + 1) * 32), mg[0][0:32, c * 128:(c + 1) * 128], self.idb[0:32, 0:32])
            mghT = self.sb(es, "mghT", [128, 4, 32], BF16)
            S.copy("act", mghT[:, :, :].re("p c t -> p (c t)"), pb[0].b(0, 128))
            S.dma(self.mixh_d[:, 0:4, :], mghT[:, :, :], S.group("mghT"))
            S.flush()

    def phase3(self, keep):
        S = self.S
        pb = self.pb
        idb = self.idb
        qnT, gates = keep["qnT"], keep["gates"]
        SC = 128.0 ** -0.5
        with ExitStack() as es:
            gc_ = S.group("const3", full=True)
            kcmpT = self.sb(es, "kcmpT", [128, 512], BF16)
            vcx = self.sb(es, "vcx", [128, 4, 129], BF16)
            onesf = self.cload(es, "onesf3", [128, 128], gc_)
            onesb = self.sb(es, "onesb3", [128, 128], BF16)
            S.copy("dve", onesb[:, :], onesf[:, :])
            with ExitStack() as e2:
                g2 = S.group("const3b", full=True)
                kcw = self.cload(e2, "kcw", [128, 1], g2)
                pm511 = self.cload(e2, "pm511", [128, 1], g2)
                for which, src_d, w1n, w2n, posn in (("k", self.kcT_d, "cmp_k_w1", "cmp_k_w2", "cmp_k_posT"),
                                                    ("v", self.vcT_d, "cmp_v_w1", "cmp_v_w2", "cmp_v_posT")):
                    with ExitStack() as e3:
                        g3 = S.group("c3" + which, full=True)
                        xT = self.sb(e3, "cx" + which, [128, S_LEN], BF16)
                        S.dma(xT[:, 0:4096], src_d[:, 0:4096], g3)
                        S.dma(xT[:, 4096:8192], src_d[:, 4096:8192], g3)
                        w1d = self.inp(w1n, [128, 32, 128])
                        w1f = self.sb(e3, "w1f" + which, [128, 32, 128])
                        S.dma(w1f[:, 0:16, :], w1d[:, 0:16, :], g3)
                        S.dma(w1f[:, 16:32, :], w1d[:, 16:32, :], g3)
                        w1b = self.sb(e3, "w1b" + which, [128, 32, 128], BF16)
                        S.copy("dve", w1b[:, 0:16, :], w1f[:, 0:16, :])
                        S.copy("pool", w1b[:, 16:32, :], w1f[:, 16:32, :])
                        w2f = self.cload(e3, w2n, [128, 128], g3)
                        w2b = self.sb(e3, "w2b" + which, [128, 128], BF16)
                        S.copy("dve", w2b[:, :], w2f[:, :])
                        posf = self.cload(e3, posn, [128, 32], g3)
                        posb = self.sb(e3, "posb" + which, [128, 32], BF16)
                        S.copy("dve", posb[:, :], posf[:, :])
                        hid = self.sb(e3, "hid" + which, [128, 512], BF16)
                        bcol = self.sb(e3, "bcol" + which, [128, 1])
                        S.memset("pool", hid[:, :], 0.0)
                        bh, bb = pb[2], pb[3]
                        x3 = xT[:, :].re("p (n s) -> p n s", s=16)
                        for l in range(32):
                            S.mm(bh.f(0, 511), lhsT=w1b[:, l, :], rhs=x3[:, l // 16:l // 16 + 511, l % 16],
                                 start=(l == 0), stop=(l == 31))
                        for l in range(32):
                            S.mm(bb.f(0, 1), lhsT=w1b[:, l, :], rhs=posb[:, l:l + 1], start=(l == 0), stop=(l == 31))
                        S.copy("dve", bcol[:, :], bb.f(0, 1))
                        S.act(hid[:, 0:511], bh.f(0, 511), AF.Silu, bias=bcol[:, 0:1])
                        if which == "k":
                            bk = pb[4]
                            S.mm(bk.f(0, 512), lhsT=w2b[:, :], rhs=hid[:, :])
                            yk = self.sb(e3, "yk", [128, 512])
                            sqk = self.sb(e3, "sqk", [128, 512], BF16)
                            rk = self.sb(e3, "rk", [128, 512])
                            S.copy("act", yk[:, :], bk.f(0, 512))
                            S.act(sqk[:, :], bk.f(0, 512), AF.Square)
                            S.mm(pb[5].f(0, 512), lhsT=onesb[:, :], rhs=sqk[:, :])
                            S.act(rk[:, :], pb[5].f(0, 512), AF.Sqrt, bias=EPS, scale=1.0 / 128.0)
                            S.recip(rk[:, :], rk[:, :])
                            S.stt(kcmpT[:, :], yk[:, :], kcw[:, 0:1], rk[:, :], ALU.mult, ALU.mult)
                            S.memset("dve", kcmpT[:, 511:512], 0.0)
                        else:
                            bv = pb[4]
                            for nt in range(4):
                                S.mm(bv.f(nt * 128, (nt + 1) * 128), lhsT=hid[:, nt * 128:(nt + 1) * 128], rhs=w2b[:, :])
                            S.memset("pool", vcx[:, :, 128:129], 1.0)
                            S.copy("act", vcx[:, :, 0:128], bv.f(0, 512).re("p (n d) -> p n d", n=4))
                            S.ts("dve", vcx[:, 3, :], vcx[:, 3, :], pm511[:, 0:1], None, ALU.mult)
                        S.flush()
            if self.debug:
                S.dma(self.outp("d_kcmpT", [128, 512], BF16)[:, :], kcmpT[:, :], self.g_out)
                S.dma(self.outp("d_vcx", [128, 4, 129], BF16)[:, :, :], vcx[:, :, :], self.g_out)
            if self.stop == 31:
                S.flush()
                return
            kslT = self.sb(es, "kslT", [128, S_LEN], BF16)
            kwnT = self.sb(es, "kwnT", [128, S_LEN], BF16)
            vslx = self.sb(es, "vslx", [128, NB, 129], BF16)
            vwnx = self.sb(es, "vwnx", [128, NB, 129], BF16)
            for i in range(2):
                S.dma(kslT[:, i * 4096:(i + 1) * 4096], self.kslT_d[:, i * 4096:(i + 1) * 4096], gc_)
                S.dma(kwnT[:, i * 4096:(i + 1) * 4096], self.kwnT_d[:, i * 4096:(i + 1) * 4096], gc_)
                S.dma(vslx[:, i * 32:(i + 1) * 32, 0:128], self.vsl_d[:, i * 32:(i + 1) * 32, :], gc_)
                S.dma(vwnx[:, i * 32:(i + 1) * 32, 0:128], self.vwn_d[:, i * 32:(i + 1) * 32, :], gc_)
            S.memset("pool", vslx[:, :, 128:129], 1.0)
            S.memset("pool", vwnx[:, :, 128:129], 1.0)
            eall = self.sb(es, "eallb", [128, S_LEN], BF16)
            addm = self.cload(es, "addm", [128, 16, 128], gc_)
            ovb = self.sb(es, "ovb", [128, 4, 128], BF16)
            cmbb = self.sb(es, "cmbb", [128, 16, 2, 128], BF16)
            cbsb = self.sb(es, "cbsb", [128, 4, 4, 128], BF16)
            cbwb = self.sb(es, "cbwb", [128, 8, 4, 128], BF16)
            with ExitStack() as e2:
                g2 = S.group("const3c", full=True)
                ead = self.inp("eall", [128, S_LEN])
                est = [self.sb(e2, f"est{i}", [128, 2048]) for i in range(2)]
                g_e = [S.group(f"est{i}") for i in range(2)]
                for i in range(4):
                    S.dma(est[i % 2][:, :], ead[:, i * 2048:(i + 1) * 2048], g_e[i % 2])
                    S.copy("pool" if i % 2 else "dve", eall[:, i * 2048:(i + 1) * 2048], est[i % 2][:, :])
                ovf = self.cload(e2, "ovm", [128, 4, 128], g2)
                S.copy("dve", ovb[:, :, :], ovf[:, :, :])
                cmf = self.cload(e2, "cmb", [128, 16, 2, 128], g2)
                S.copy("pool", cmbb[:, :, :, :], cmf[:, :, :, :])
                cbsf = self.cload(e2, "cbs", [128, 4, 128], g2)
                cbwf = self.cload(e2, "cbw", [128, 8, 128], g2)
                for h in range(4):
                    S.copy("dve", cbsb[:, :, h, :], cbsf[:, :, :])
                    S.copy("pool", cbwb[:, :, h, :], cbwf[:, :, :])
                S.flush()
            Pc = [self.sb(es, f"Pc{i}", [128, 512], BF16) for i in range(4)]
            Pk = [self.sb(es, f"Pk{i}", [128, 512], BF16) for i in range(3)]
            cmrep = [self.sb(es, f"cmrep{i}", [128, 4, 128], BF16) for i in range(2)]
            ob = {nm: self.sb(es, "ob_" + nm, [128, 4, 129]) for nm in ("c", "s", "w")}
            rs = self.sb(es, "rs", [128, 12])
            cf = self.sb(es, "cf", [128, 12])
            imp = self.sb(es, "imp", [128, 128])
            imp2 = self.sb(es, "imp2", [128, 128])
            m8 = self.sb(es, "m8", [128, 16])
            selm = self.sb(es, "selm", [128, 128])
            biasT = self.sb(es, "biasT", [128, 4, 128], BF16)
            mixn = [self.sb(es, f"mixn{i}", [128, 4, 128], BF16) for i in range(2)]
            tmpn = self.sb(es, "tmpn", [128, 128])
            mnT = [self.sb(es, f"mnT{i}", [128, 4, 128], BF16) for i in range(2)]
            g_mn = [S.group(f"mnT{i}") for i in range(2)]
            bS = [pb[0], pb[1]]
            bO = [pb[2], pb[3]]
            bI, bT = pb[4], pb[5]
            si = [0]

            def nsa_part1(s_, Q, qv, cmp_plan, addv):
                NQ = 4 * Q

                ncp = len(cmp_plan)
                for i_, (nt, mk) in enumerate(cmp_plan):
                    bank = bS[i_ % 2]
                    S.mm(bank.f(0, NQ), lhsT=kcmpT[:, nt * 128:(nt + 1) * 128], rhs=qv, start=True, stop=(mk is None))
                    if mk is not None:
                        S.mm(bank.f(0, NQ), lhsT=idb[:, :], rhs=mk, start=False, stop=True)
                    S.act(Pc[i_][:, 0:NQ], bank.f(0, NQ), AF.Exp, scale=SC)
                for h in range(4):
                    for i_, (nt, mk) in enumerate(cmp_plan):
                        S.mm_(bOc[h // 2].f((h % 2) * 129, (h % 2) * 129 + 129, 0, Q), lhsT=Pc[i_][:, h * Q:(h + 1) * Q],
                              rhs=vcx[:, nt, :], start=(i_ == 0 and h % 2 == 0), stop=(i_ == ncp - 1), skip=True)
                for h in range(4):
                    for i_, (nt, mk) in enumerate(cmp_plan):
                        S.mm_(bI.f(h * 128, (h + 1) * 128, 0, Q), lhsT=Pc[i_][:, h * Q:(h + 1) * Q], rhs=ovb[:, nt, :],
                              start=(i_ == 0 and h == 0), stop=(i_ == ncp - 1), skip=True)
                for hh in range(2):
                    S.copy("act", obc[s_][0:Q, 2 * hh:2 * hh + 2, :], bOc[hh].f(0, 258, 0, Q).re("p (h c) -> p h c", h=2))
                S.ts("dve", rsc[s_][0:Q, 0:4], obc[s_][0:Q, :, 128], 1e-30, None, ALU.max)
                S.recip(rsc[s_][0:Q, 0:4], rsc[s_][0:Q, 0:4])
                S.ts("dve", imp[0:Q, :], bI.f(0, 128, 0, Q), rsc[s_][0:Q, 0:1], None, ALU.mult)
                for h in range(1, 4):
                    S.stt(imp[0:Q, :], bI.f(h * 128, (h + 1) * 128, 0, Q), rsc[s_][0:Q, h:h + 1], imp[0:Q, :], ALU.mult, ALU.add)
                S.tt("pool", imp[0:Q, :], imp[0:Q, :], addv, ALU.add)
                self.max8(m8[0:Q, 0:8], imp[0:Q, :])
                self.match_replace(imp2[0:Q, :], m8[0:Q, 0:8], imp[0:Q, :], -3.0e38)
                self.max8(m8[0:Q, 8:16], imp2[0:Q, :])
                S.ts("dve", selm[0:Q, :], imp[0:Q, :], m8[0:Q, 15:16], None, ALU.is_ge)
                S.mm(bT.f(0, Q), lhsT=selm[0:Q, :], rhs=self.idf[0:Q, 0:Q])
                S.ts("dve", bT2s[s_][:, 0:NQ].re("p (h q) -> p h q", h=4),
                     V(bT.t[:, 0:Q].unsqueeze(1).broadcast_to([128, 4, Q]), bT.q[0:1]), BIG, -BIG, ALU.mult, ALU.add)

            def nsa_part2(s_, Q, qv, slc_plan, win_plan, gatev, store):
                NQ = 4 * Q

                def scores(kT_tile, extra):
                    bank = bS[si[0] % 2]
                    out_ = Pk[si[0] % 3]
                    si[0] += 1
                    S.mm(bank.f(0, NQ), lhsT=kT_tile, rhs=qv, start=True, stop=(len(extra) == 0))
                    for i_, (l_, r_) in enumerate(extra):
                        S.mm(bank.f(0, NQ), lhsT=l_, rhs=r_, start=False, stop=(i_ == len(extra) - 1))
                    S.act(out_[:, 0:NQ], bank.f(0, NQ), AF.Exp, scale=SC)
                    return out_

                def pv(P_, vx, first, last):
                    for h in range(4):
                        S.mm_(bO[h // 2].f((h % 2) * 129, (h % 2) * 129 + 129, 0, Q), lhsT=P_[:, h * Q:(h + 1) * Q],
                              rhs=vx, start=(first and h % 2 == 0), stop=last, skip=True)

                def evac_o(dst):
                    for hh in range(2):
                        S.copy("act", dst[0:Q, 2 * hh:2 * hh + 2, :], bO[hh].f(0, 258, 0, Q).re("p (h c) -> p h c", h=2))

                brhs = bT2s[s_][:, 0:NQ]
                S.copy("dve", rs[0:Q, 0:4], rsc[s_][0:Q, 0:4])
                prev = None
                for i_, (kt, extra) in enumerate(slc_plan):
                    ex = [(eall[:, kt * 128:(kt + 1) * 128], brhs)] + extra
                    P_ = scores(kslT[:, kt * 128:(kt + 1) * 128], ex)
                    if prev is not None:
                        pv(prev[0], vslx[:, prev[1], :], prev[2] == 0, False)
                    prev = (P_, kt, i_)
                pv(prev[0], vslx[:, prev[1], :], prev[2] == 0, True)
                evac_o(ob["s"])
                prev = None
                for i_, (kt, mk) in enumerate(win_plan):
                    P_ = scores(kwnT[:, kt * 128:(kt + 1) * 128], [(idb[:, :], mk)])
                    if prev is not None:
                        pv(prev[0], vwnx[:, prev[1], :], prev[2] == 0, False)
                    prev = (P_, kt, i_)
                pv(prev[0], vwnx[:, prev[1], :], prev[2] == 0, True)
                evac_o(ob["w"])
                S.ts("dve", rs[0:Q, 4:8], ob["s"][0:Q, :, 128], 1e-30, None, ALU.max)
                S.ts("dve", rs[0:Q, 8:12], ob["w"][0:Q, :, 128], 1e-30, None, ALU.max)
                S.recip(rs[0:Q, 4:12], rs[0:Q, 4:12])
                g3 = gatev.re("p (h g) -> p g h", g=3)
                S.tt("dve", cf[0:Q, :].re("p (g h) -> p g h", g=3), rs[0:Q, :].re("p (g h) -> p g h", g=3), g3, ALU.mult)
                mx = mixn[si[0] % 2]
                for h in range(4):
                    S.ts("pool", tmpn[0:Q, :], obc[s_][0:Q, h, 0:128], cf[0:Q, h:h + 1], 1.0, ALU.mult, ALU.mult)
                    S.stt(tmpn[0:Q, :], ob["s"][0:Q, h, 0:128], cf[0:Q, 4 + h:5 + h], tmpn[0:Q, :], ALU.mult, ALU.add)
                    S.stt(mx[0:Q, h, :], ob["w"][0:Q, h, 0:128], cf[0:Q, 8 + h:9 + h], tmpn[0:Q, :], ALU.mult, ALU.add)
                for h in range(4):
                    S.tr(bT.b(h * Q, (h + 1) * Q), mx[0:Q, h, :], idb[0:Q, 0:Q])
                store(bT.b(0, 4 * Q))

            bT2s = [self.sb(es, f"bT2_{i}", [128, 512], BF16) for i in range(2)]
            obc = [self.sb(es, f"obc{i}", [128, 4, 129]) for i in range(2)]
            rsc = [self.sb(es, f"rsc{i}", [128, 4]) for i in range(2)]
            bOc = [pb[6], pb[7]]
            plans = []
            for j in range(16):
                qv = qnT[:, :, j * 128:(j + 1) * 128]
                nt_hi = min(3, (32 * j + 30) // 128)
                lo = max(0, 32 * j - 1) // 128
                cmp_plan = []
                for nt in range(nt_hi + 1):
                    mk = None
                    if nt >= lo:
                        mk = nt - lo
                    cmp_plan.append((nt, mk))
                slc_plan = []
                for kt in range(4 * j + 4):
                    ex = []
                    if kt >= 4 * j:
                        ex.append((idb[:, :], cbsb[:, kt - 4 * j, :, :].re("p h q -> p (h q)")))
                    slc_plan.append((kt, ex))
                win_plan = [(4 * j - 4 + e, cbwb[:, e, :, :].re("p h q -> p (h q)")) for e in range(8) if 4 * j - 4 + e >= 0]

                def store(src, j=j):
                    S.copy("act", mnT[j % 2][:, :, :].re("p c t -> p (c t)"), src)
                    S.dma(self.mix_d[:, 4:8, j * 128:(j + 1) * 128], mnT[j % 2][:, :, :], g_mn[j % 2])

                plans.append((qv, cmp_plan, slc_plan, win_plan, store))
            def emit1(j):
                qv, cmp_plan, _, _, _ = plans[j]
                cp = []
                for nt, slot in cmp_plan:
                    mk = None
                    if slot is not None:
                        S.copy("pool", cmrep[slot][:, :, :],
                               V(cmbb.t[:, j, slot, :].unsqueeze(1).broadcast_to([128, 4, 128]), [cmbb.b]))
                        mk = cmrep[slot][:, :, :].re("p h q -> p (h q)")
                    cp.append((nt, mk))
                nsa_part1(j % 2, 128, qv, cp, addm[:, j, :])

            def emit2(j):
                qv, _, slc_plan, win_plan, store = plans[j]
                nsa_part2(j % 2, 128, qv, slc_plan, win_plan, gates[:, j, :], store)

            with ExitStack() as ew:
                wps = [self.sb(ew, f"wps{i}", [128, 1408]) for i in range(2)]
                wpb = [self.sb(ew, f"wpb{i}", [128, 1408], BF16) for i in range(2)]
                g_ws = [S.group(f"wps{i}") for i in range(2)]
                g_wb = [S.group(f"wpb{i}") for i in range(2)]

                def wload(n):
                    k, q = n // 4, n % 4
                    S.dma(wps[n % 2][:, :], self.w_up[k * 128:(k + 1) * 128, q * 1408:(q + 1) * 1408], g_ws[n % 2])

                def wconv(n):
                    k, q = n // 4, n % 4
                    S.copy("pool", wpb[n % 2][:, :], wps[n % 2][:, :])
                    S.dma(self.wup_d[:, 11 * q:11 * q + 11, k, :], wpb[n % 2][:, :].re("p (c n) -> p c n", n=128),
                          g_wb[n % 2])
                    if n + 2 < 32:
                        wload(n + 2)

                wload(0)
                wload(1)

                def extra(j):
                    wconv(2 * j)
                    wconv(2 * j + 1)

                self._p3_emit(emit1, emit2, extra=extra)
                S.flush()
            self.nsa_halo(es, (nsa_part1, nsa_part2), keep, idb)
            S.flush()

    @staticmethod
    def _p3_emit(emit1, emit2, n=16, extra=None):
        emit1(0)
        for j in range(n):
            if j + 1 < n:
                emit1(j + 1)
            if extra is not None:
                extra(j)
            emit2(j)

    def nsa_halo(self, es, nsa_parts, keep, idb):
        S = self.S
        with ExitStack() as e2:
            g2 = S.group("const3h", full=True)
            haddm = self.cload(e2, "haddm", [32, 128], g2)
            hcmb = self.sb(e2, "hcmb", [128, 4, 128], BF16)
            hsmb = self.sb(e2, "hsmb", [128, 64, 128], BF16)
            hwmb = self.sb(e2, "hwmb", [128, 64, 128], BF16)
            hcf = self.cload(e2, "hcm", [128, 4, 128], g2)
            S.copy("dve", hcmb[:, :, :], hcf[:, :, :])
            st = [self.sb(e2, f"hst{i}", [128, 8, 128]) for i in range(2)]
            g_s = [S.group(f"hst{i}") for i in range(2)]
            n = 0
            for nm, dst in (("hsm", hsmb), ("hwm", hwmb)):
                src = self.inp(nm, [128, 64, 128])
                for i in range(8):
                    S.dma(st[n % 2][:, :, :], src[:, i * 8:(i + 1) * 8, :], g_s[n % 2])
                    S.copy("pool" if n % 2 else "dve", dst[:, i * 8:(i + 1) * 8, :], st[n % 2][:, :, :])
                    n += 1
            mnTh = self.sb(e2, "mnTh", [128, 4, 32], BF16)
            g_m = S.group("mnTh")
            cmp_plan = [(nt, hcmb[:, nt, :]) for nt in range(4)]
            slc_plan = [(kt, [(idb[:, :], hsmb[:, kt, :])]) for kt in range(NB)]
            win_plan = [(kt, hwmb[:, kt, :]) for kt in range(NB)]

            def store(src):
                S.copy("act", mnTh[:, :, :].re("p c t -> p (c t)"), src)
                S.dma(self.mixh_d[:, 4:8, :], mnTh[:, :, :], g_m)

            part1, part2 = nsa_parts
            part1(0, 32, keep["qnTh"][:, :, :], cmp_plan, haddm[:, :])
            part2(0, 32, keep["qnTh"][:, :, :], slc_plan, win_plan, keep["gatesh"][:, :], store)
            S.flush()

    def max8(self, out, in_):
        o_, i_ = out.ap, in_.ap
        self.S.op("dve", lambda e: e.max(out=o_, in_=i_), r=self.S._bs(in_), w=self.S._bs(out))

    def match_replace(self, out, rep, vals, imm):
        o_, r_, v_ = out.ap, rep.ap, vals.ap
        self.S.op("dve", lambda e: e.match_replace(out=o_, in_to_replace=r_, in_values=v_, imm_value=imm),
                  r=self.S._bs(rep, vals), w=self.S._bs(out))

    def phase45(self):
        S = self.S
        pb = self.pb
        idb = self.idb
        w_out = self.inp("w_out", [D, D])
        w_dn = self.inp("w_dn", [DFF, D])
        wup_d = self.wup_d
        with ExitStack() as es:
            gc_ = S.group("const4", full=True)
            fcw = self.cload(es, "fcw", [128, 44 * 3], gc_)
            fcb = self.cload(es, "fcb", [128, 44], gc_)
            wdnb = self.sb(es, "wdnb", [128, 22, D], BF16)
            woutb = self.sb(es, "woutb", [128, 8, D], BF16)
            g1row = self.sb(es, "g1row", [128, D])
            g2row = self.sb(es, "g2row", [128, D])
            with ExitStack() as e2:
                stg = [self.sb(e2, f"wst4_{i}", [128, 2, D]) for i in range(2)]
                g_s = [S.group(f"wst4_{i}") for i in range(2)]
                n = 0
                for (src, dst, nch) in ((w_dn, wdnb, 22), (w_out, woutb, 8)):
                    sv = src.rearrange("(c p) d -> p c d", p=128)
                    for c0 in range(0, nch, 2):
                        st = stg[n % 2]
                        S.dma(st[:, :, :], sv[:, c0:c0 + 2, :], g_s[n % 2])
                        S.copy(("dve", "pool", "act")[n % 3], dst[:, c0:c0 + 2, :], st[:, :, :])
                        n += 1
                gb = self.sb(e2, "gbc", [128, 128])
                for gi, (col0, dst) in enumerate(((16, g1row), (40, g2row))):
                    for cc in range(8):
                        S.copy("dve", gb[:, :], V(self.modT.t[:, col0 + cc:col0 + cc + 1].broadcast_to([128, 128]),
                                                  [self.modT.b]))
                        bank = pb[cc % 2]
                        S.mm(bank.f(0, 128), lhsT=gb[:, :], rhs=self.idf[:, :])
                        S.copy("act", dst[:, cc * 128:(cc + 1) * 128], bank.f(0, 128))
                S.flush()
            F = dict(ssq=self.sb(es, "f4_ssq", [128, 4]), rt=self.sb(es, "f4_rt", [128, 4]),
                     rstd=self.sb(es, "f4_rstd", [128, 4]), junk=self.sb(es, "f4_junk", [128, 1024], BF16),
                     xs=self.sb(es, "f4_xs", [128, 2, 1024], BF16))
            x1 = [self.sb(es, f"x1_{i}", [128, 4, D]) for i in range(2)]
            g_x = [S.group(f"x1_{i}") for i in range(2)]
            mixt = self.sb(es, "mixt", [128, 8, 512], BF16)
            g_m = S.group("mixt")
            h2T = self.sb(es, "h2T", [128, 8, 512], BF16)
            gT = self.sb(es, "gT", [128, 22, 512], BF16)
            wch = [self.sb(es, f"wch{i}", [128, 2, 8, 128], BF16) for i in range(3)]
            g_wc = [S.group(f"wch{i}") for i in range(3)]
            upa = [self.sb(es, f"upa{i}", [128, 4, 130]) for i in range(2)]
            upb = [self.sb(es, f"upb{i}", [128, 4, 130]) for i in range(2)]
            for u_ in upa + upb:
                S.memset("pool", u_[:, :, :], 0.0)
            aa = [self.sb(es, f"aa{i}", [128, 4, 128]) for i in range(2)]
            ab_ = [self.sb(es, f"ab_{i}", [128, 4, 128]) for i in range(2)]
            tmp = [self.sb(es, f"tmp4_{i}", [128, 512]) for i in range(2)]
            xrows = self.xo.rearrange("(t b p) d -> t p b d", b=4, p=128)
            orows = self.out.rearrange("(t b p) d -> t p b d", b=4, p=128)
            wi = 0
            gh = S.group("p4halo", full=True)
            hexr = self.cload(es, "hexr", [128, 32], gh)
            x1h = self.sb(es, "x1h", [32, D])
            mixh = self.sb(es, "mixh", [128, 8, 32], BF16)
            S.dma(x1h[:, :], self.din["xh"][:, :], gh)
            S.dma(mixh[:, :, :], self.mixh_d[:, :, :], gh)
            h2Th = self.sb(es, "h2Th", [128, 8, 32], BF16)
            hup = self.sb(es, "hup", [128, 44, 32])
            for half in range(2):
                bank = pb[2 + half]
                hs = slice(half * 512, (half + 1) * 512)
                for c in range(8):
                    S.mm(bank.f(0, 512, 0, 32), lhsT=mixh[:, c, :], rhs=woutb[:, c, hs], start=(c == 0), stop=(c == 7))
                S.tt("dve", tmp[half][0:32, :], bank.f(0, 512, 0, 32), g1row[0:32, hs], ALU.mult)
                S.tt("pool", x1h[:, hs], x1h[:, hs], tmp[half][0:32, :], ALU.add)
            self.front32(F, x1h, h2Th, self.s2, 24)

            def conv(u_, c, dst):
                S.ts("pool", dst[:, :, :], u_[:, :, 2:130], fcw[:, c * 3 + 2:c * 3 + 3], fcb[:, c:c + 1], ALU.mult, ALU.add)
                S.stt(dst[:, :, :], u_[:, :, 1:129], fcw[:, c * 3 + 1:c * 3 + 2], dst[:, :, :], ALU.mult, ALU.add)
                S.stt(dst[:, :, :], u_[:, :, 0:128], fcw[:, c * 3:c * 3 + 1], dst[:, :, :], ALU.mult, ALU.add)

            for t in range(4):
                xx = x1[t % 2]
                S.dma(xx[:, :, :], xrows[t], g_x[t % 2])
                S.dma(mixt[:, :, :], self.mix_d[:, :, t * 512:(t + 1) * 512], g_m)
                for blk in range(4):
                    cs = slice(blk * 128, (blk + 1) * 128)
                    for half in range(2):
                        bank = pb[2 + half]
                        hs = slice(half * 512, (half + 1) * 512)
                        for c in range(8):
                            S.mm(bank.f(0, 512), lhsT=mixt[:, c, cs], rhs=woutb[:, c, hs], start=(c == 0), stop=(c == 7))
                        tm = tmp[half]
                        S.tt("dve", tm[:, :], bank.f(0, 512), g1row[:, hs], ALU.mult)
                        S.tt("pool", xx[:, blk, hs], xx[:, blk, hs], tm[:, :], ALU.add)
                if self.debug:
                    S.dma(self.dx1[t], xx[:, :, :], self.g_out)
                for hf in range(2):
                    self.front(F, xx, 2, h2T, hf * 256, self.s2, 24, b0=2 * hf)
                for c in range(22):
                    w_ = wch[wi % 3]
                    S.dma(w_[:, 0, :, :], wup_d[:, c, :, :], g_wc[wi % 3])
                    S.dma(w_[:, 1, :, :], wup_d[:, 22 + c, :, :], g_wc[wi % 3])
                    wi += 1
                    ua, ub = upa[c % 2], upb[c % 2]
                    for i_, (u_, bank) in enumerate(((ua, pb[4]), (ub, pb[5]))):
                        for k in range(8):
                            S.mm(bank.f(0, 512), lhsT=w_[:, i_, k, :], rhs=h2T[:, k, :], start=(k == 0), stop=(k == 7))
                        S.copy("act", u_[:, :, 2:130], bank.f(0, 512).re("p (b t) -> p b t", b=4))
                        ci = c + 22 * i_
                        if t == 0:
                            bh_ = pb[6 + i_]
                            for k in range(8):
                                S.mm(bh_.f(0, 32), lhsT=w_[:, i_, k, :], rhs=h2Th[:, k, :], start=(k == 0), stop=(k == 7))
                            S.tt("dve", hup[:, ci, :], bh_.f(0, 32), hexr[:, :], ALU.mult)
                        S.copy("pool", u_[:, :, 0:2], hup[:, ci, 8 * t:8 * t + 8].re("p (b i) -> p b i", i=2))
                    conv(ua, c, aa[c % 2])
                    conv(ub, 22 + c, ab_[c % 2])
                    S.act(aa[c % 2][:, :, :], aa[c % 2][:, :, :], AF.Silu)
                    S.tt("dve", gT[:, c, :].re("p (b t) -> p b t", b=4), aa[c % 2][:, :, :], ab_[c % 2][:, :, :], ALU.mult)
                for blk in range(4):
                    cs = slice(blk * 128, (blk + 1) * 128)
                    for half in range(2):
                        bank = pb[6 + half]
                        hs = slice(half * 512, (half + 1) * 512)
                        for c in range(22):
                            S.mm(bank.f(0, 512), lhsT=gT[:, c, cs], rhs=wdnb[:, c, hs], start=(c == 0), stop=(c == 21))
                        tm = tmp[half]
                        S.tt("dve", tm[:, :], bank.f(0, 512), g2row[:, hs], ALU.mult)
                        S.tt("pool", xx[:, blk, hs], xx[:, blk, hs], tm[:, :], ALU.add)
                S.dma(orows[t], xx[:, :, :], g_x[t % 2])
            S.flush()


def _colL(v, n):
    return np.ascontiguousarray(np.asarray(v, np.float32).reshape(n, 128).T)


def _rep(v, n=128):
    v = np.asarray(v, np.float32).reshape(1, -1)
    return np.ascontiguousarray(np.repeat(v, n, axis=0))


def _consts():
    p = np.arange(128)
    same = (p[:, None] // 64) == (p[None, :] // 64)
    c = {}
    c["identf"] = np.eye(128, dtype=np.float32)
    c["tri2"] = (same & (p[:, None] <= p[None, :])).astype(np.float32)
    c["blk2"] = same.astype(np.float32)
    c["onesf"] = np.ones((128, 128), np.float32)
    c["cind"] = np.stack([(p < 64), (p >= 64)], axis=1).astype(np.float32)
    ma = np.where(same & (p[None, :] < p[:, None]), 0.0, BIG).astype(np.float32)
    mb = np.where(same & (p[None, :] >= p[:, None]), 0.0, -BIG).astype(np.float32)
    c["ma4"] = np.ascontiguousarray(np.tile(ma, (1, 4)))
    c["mb4"] = np.ascontiguousarray(np.tile(mb, (1, 4)))
    return c


def _host_inputs(inputs):
    x = np.asarray(inputs["x"], np.float32)
    cst = _consts()
    g = lambda k: np.asarray(inputs[k][0], np.float32)
    gcw = g("gdn_conv_w")
    gcwT = np.ascontiguousarray(gcw.reshape(4, 12, 128).transpose(2, 1, 0).reshape(128, 48))
    shared = {
        "ada_w": np.ascontiguousarray(g("ada_w")),
        "ada_bT": _colL(g("ada_b"), 48),
        "n1w": _colL(g("norm1_w"), 8),
        "n2w": _colL(g("norm2_w"), 8),
        "w_in": np.ascontiguousarray(g("w_in")),
        "gcw": gcwT,
        "dtb": _rep(g("gdn_dt_bias")),
        "alog": _rep(g("gdn_A_log")),
        "kslw": _colL(g("nsa_k_norm_slc"), 1),
        "kwnw": _colL(g("nsa_k_norm_win"), 1),
        "qnw": _colL(g("nsa_q_norm_w"), 1),
        "gonw4": _rep(np.tile(g("gdn_out_norm_w"), 4)),
        "onesf2": np.ones((128, 128), np.float32),
    }
    for nm in ("k", "v"):
        shared[f"cmp_{nm}_w1"] = np.ascontiguousarray(g(f"cmp_{nm}_w1").reshape(32, 128, 128).transpose(1, 0, 2))
        shared[f"cmp_{nm}_w2"] = np.ascontiguousarray(g(f"cmp_{nm}_w2"))
        shared[f"cmp_{nm}_posT"] = np.ascontiguousarray(g(f"cmp_{nm}_pos").T)
    shared["kcw"] = _colL(g("nsa_k_norm_cmp"), 1)
    shared["onesf3"] = np.ones((128, 128), np.float32)
    pm = np.ones((128, 1), np.float32)
    pm[127, 0] = 0.0
    shared["pm511"] = pm
    keys = np.arange(S_LEN)
    shared["eall"] = (keys[None, :] // 64 == np.arange(128)[:, None]).astype(np.float32)
    n = np.arange(512)
    js = np.arange(128)
    ov = np.minimum(16 * n[:, None] + 32, 64 * js[None, :] + 64) - np.maximum(16 * n[:, None], 64 * js[None, :])
    ov = np.clip(ov, 0, None).astype(np.float32) / 32.0
    ov[511] = 0.0
    shared["ovm"] = np.ascontiguousarray(ov.reshape(4, 128, 128).transpose(1, 0, 2))
    fw = g("ffn_conv_w")
    shared["fcw"] = np.ascontiguousarray(fw.reshape(3, 44, 128).transpose(2, 1, 0).reshape(128, 132))
    shared["fcb"] = _colL(g("ffn_conv_b"), 44)
    shared["w_out"] = np.ascontiguousarray(g("w_out"))
    shared["w_up"] = np.ascontiguousarray(g("ffn_w_up"))
    shared["w_dn"] = np.ascontiguousarray(g("ffn_w_down"))
    shared.update(cst)
    maps = []
    for core in range(8):
        b, r = core // 4, core % 4
        xo = np.concatenate([x[b, 128 * (4 * j + r):128 * (4 * j + r) + 128] for j in range(16)], axis=0)
        m = dict(shared)
        m["xb"] = np.ascontiguousarray(x[b])
        m["xo"] = np.ascontiguousarray(xo)
        m["cT"] = _colL(inputs["c"][b], 8)
        sel = np.zeros((128, 4), np.float32)
        sel[:, r] = 1.0
        m["sel"] = sel
        p = np.arange(128)
        q = np.arange(128)
        addm = np.zeros((128, 16, 128), np.float32)
        cmb = np.zeros((128, 16, 2, 128), np.float32)
        for j in range(16):
            qi = 4 * j + r
            tq = 128 * qi + q
            cur = tq // 64
            jj = np.arange(128)[None, :]
            valid = jj <= cur[:, None]
            forced = (jj == 0) | (jj == cur[:, None]) | (jj == cur[:, None] - 1)
            addm[:, j, :] = np.where(valid, np.where(forced, 1.0e4, 0.0), -1.0e30)
            lo = max(0, 32 * j - 1) // 128
            for slot in range(2):
                nn = 128 * (lo + slot) + p
                ok = (16 * nn[:, None] + 31) <= tq[None, :]
                cmb[:, j, slot, :] = np.where(ok, 0.0, -BIG)
        m["addm"] = addm
        m["cmb"] = cmb
        cbs = np.zeros((128, 4, 128), np.float32)
        for d in range(4):
            ok = (128 * (d - r) + p[:, None]) <= q[None, :]
            cbs[:, d, :] = np.where(ok, 0.0, -BIG)
        m["cbs"] = cbs
        cbw = np.zeros((128, 8, 128), np.float32)
        for e in range(8):
            rel = 128 * (e - 4 - r) + p[:, None]
            ok = (rel <= q[None, :]) & (rel > q[None, :] - 512)
            cbw[:, e, :] = np.where(ok, 0.0, -BIG)
        m["cbw"] = cbw
        hs_ = np.zeros((128, 4), np.float32)
        hs_[:, (r - 1) % 4] = 1.0
        m["hsel"] = hs_
        sab = np.zeros((128, 2), np.float32)
        sab[:, 0] = 1.0 if r >= 1 else 0.0
        sab[:, 1] = 1.0 if r == 0 else 0.0
        m["selAB"] = sab
        tq = np.array([128 * (4 * j + r) - 2 + i for j in range(16) for i in range(2)])
        ex = tq >= 0
        xh = np.zeros((32, D), np.float32)
        xh[ex] = x[b, tq[ex]]
        m["xh"] = xh
        m["hexr"] = _rep(ex.astype(np.float32))
        cur = tq // 64
        jj = np.arange(128)[None, :]
        valid = (jj <= cur[:, None]) & ex[:, None]
        forced = (jj == 0) | (jj == cur[:, None]) | (jj == cur[:, None] - 1)
        m["haddm"] = np.where(valid, np.where(forced, 1.0e4, 0.0), -1.0e30).astype(np.float32)
        nn = np.arange(512)
        okc = ((16 * nn[:, None] + 31) <= tq[None, :]) & ex[None, :]
        hcm = np.where(okc, 0.0, -BIG).astype(np.float32).reshape(4, 128, 1, 32)
        m["hcm"] = np.ascontiguousarray(np.broadcast_to(hcm, (4, 128, 4, 32)).transpose(1, 0, 2, 3).reshape(128, 4, 128))
        pos = np.arange(S_LEN)
        oks = (pos[:, None] <= tq[None, :]) & ex[None, :]
        okw = oks & (pos[:, None] > tq[None, :] - 512)
        for nm, ok in (("hsm", oks), ("hwm", okw)):
            a = np.where(ok, 0.0, -BIG).astype(np.float32).reshape(64, 128, 1, 32)
            m[nm] = np.ascontiguousarray(np.broadcast_to(a, (64, 128, 4, 32)).transpose(1, 0, 2, 3).reshape(128, 64, 128))
        maps.append(m)
    return maps


def run(inputs, debug=False, upto=9, ntiles=NT):
    bld = Builder(debug, ntiles)
    nc = bld.build(upto)
    maps = _host_inputs(inputs)
    maps = [{k: v for k, v in m.items() if k in bld.din} for m in maps]
    missing = [k for k in bld.din if k not in maps[0]]
    assert not missing, missing
    res = run_bass_kernel_spmd(nc, maps, core_ids=list(range(8)))
    return res.results


def kernel(**inputs):
    results = run(inputs)
    outp = np.zeros((2, S_LEN, D), np.float32)
    for core in range(8):
        b, r = core // 4, core % 4
        o = results[core]["out"]
        for j in range(16):
            qi = 4 * j + r
            outp[b, 128 * qi:128 * qi + 128] = o[128 * j:128 * j + 128]
    return outp
```

```python
import numpy as np
from contextlib import ExitStack
import concourse.bass as bass
import concourse.mybir as mybir
from concourse.bass_utils import run_bass_kernel_spmd

F32 = mybir.dt.float32
BF16 = mybir.dt.bfloat16
AF = mybir.ActivationFunctionType
ALU = mybir.AluOpType

D = 1024
S_LEN = 8192
NT = 16
NB = 64
N_IN = 3348
DFF = 2816
EPS = 1e-6
O_GQ, O_GK, O_GV, O_GZ, O_GA, O_GB, O_NQ, O_KC, O_VC, O_KSL, O_VSL, O_KWN, O_VWN, O_NG = (
    0, 512, 1024, 1536, 2048, 2052, 2056, 2568, 2696, 2824, 2952, 3080, 3208, 3336)
BIG = 30000.0


class Buf:
    __slots__ = ("name", "w", "rs", "const", "excl")

    def __init__(self, name, const=False, excl=False):
        self.name = name
        self.w = None
        self.rs = []
        self.const = const
        self.excl = excl


class V:
    __slots__ = ("ap", "bs")

    def __init__(self, ap, bs):
        self.ap = ap
        self.bs = bs if isinstance(bs, (list, tuple)) else [bs]

    def bitcast(self, dt):
        return V(self.ap.bitcast(dt), self.bs)

    def re(self, pat, **kw):
        return V(self.ap.rearrange(pat, **kw), self.bs)

    def bc(self, shape):
        return V(self.ap.broadcast_to(shape), self.bs)

    def __getitem__(self, k):
        return V(self.ap[k], self.bs)


class Tl:
    def __init__(self, t, b):
        self.t = t
        self.b = b

    def __getitem__(self, k):
        return V(self.t[k], self.b)


class Op:
    __slots__ = ("eng", "fn", "deps", "dmaw", "signal", "sigval", "grp")


class DGroup:
    def __init__(self, name, sem, full=False):
        self.name = name
        self.sem = sem
        self.count = 0
        self.full = full


ENGS = ("pe", "act", "dve", "pool", "sp")


class Sched:
    def __init__(self, nc, es):
        self.nc = nc
        self.es = es
        self.eng = {"pe": nc.tensor, "act": nc.scalar, "dve": nc.vector, "pool": nc.gpsimd, "sp": nc.sync}
        self.sem = {e: es.enter_context(nc.semaphore("s_" + e)) for e in ENGS}
        self.cnt = {e: 0 for e in ENGS}
        self.seen = {e: {} for e in ENGS}
        self.ops = []
        self.bufs = []
        self.groups = []
        self.nins = 0

    def buf(self, name, const=False, excl=False):
        b = Buf(name, const, excl)
        self.bufs.append(b)
        return b

    def group(self, name, full=False):
        g = DGroup(name, self.es.enter_context(self.nc.semaphore("g_" + name)), full)
        self.groups.append(g)
        return g

    def op(self, eng, fn, r=(), w=(), grp=None):
        o = Op()
        o.eng = eng
        o.fn = fn
        o.signal = False
        o.sigval = None
        o.grp = grp
        deps = []
        for b in r:
            if b.w is not None:
                deps.append(b.w)
            if b.excl:
                deps.extend(x for x in b.rs if x.eng != eng)
        for b in w:
            if b.w is not None:
                deps.append(b.w)
            deps.extend(b.rs)
        seen = set()
        dd = []
        for d in deps:
            if id(d) in seen or d is o:
                continue
            seen.add(id(d))
            if d.eng == "pe" and eng == "pe":
                continue
            if grp is not None and grp.full and d.grp is grp:
                continue
            dd.append(d)
        o.deps = dd
        o.dmaw = {}
        for d in dd:
            if d.grp is None:
                d.signal = True
            else:
                o.dmaw[d.grp.name] = d.grp.count
        if grp is not None:
            grp.count += 1
        for b in w:
            b.w = o
            b.rs = []
        for b in r:
            if not b.const and b.w is not o:
                b.rs.append(o)
        self.ops.append(o)
        return o

    def flush(self, barrier=True):
        if barrier:
            last = {}
            for o in self.ops:
                last[o.eng] = o
            for e, o in last.items():
                if o.grp is None:
                    o.signal = True
        for o in self.ops:
            e = self.eng[o.eng]
            for d in o.deps:
                if d.grp is not None:
                    key = "g_" + d.grp.name
                    val = 16 * (d.grp.count if d.grp.full else o.dmaw[d.grp.name])
                    sem = d.grp.sem
                else:
                    key = d.eng
                    val = d.sigval
                    sem = self.sem[d.eng]
                    assert val is not None, (o.eng, d.eng)
                if self.seen[o.eng].get(key, 0) >= val:
                    continue
                e.wait_ge(sem, val)
                self.seen[o.eng][key] = val
            ins = o.fn(e)
            self.nins += 1
            if o.grp is not None:
                ins.then_inc(o.grp.sem, 16)
            elif o.signal:
                self.cnt[o.eng] += 1
                o.sigval = self.cnt[o.eng]
                ins.then_inc(self.sem[o.eng], 1)
        self.ops = []
        if barrier:
            for en in ENGS:
                e = self.eng[en]
                for e2 in ENGS:
                    if e2 == en or e2 == "sp":
                        continue
                    if self.cnt[e2] > self.seen[en].get(e2, 0):
                        e.wait_ge(self.sem[e2], self.cnt[e2])
                        self.seen[en][e2] = self.cnt[e2]
                for g in self.groups:
                    key = "g_" + g.name
                    if 16 * g.count > self.seen[en].get(key, 0):
                        e.wait_ge(g.sem, 16 * g.count)
                        self.seen[en][key] = 16 * g.count
            for b in self.bufs:
                b.w = None
                b.rs = []

    @staticmethod
    def _bs(*vs):
        out = []
        for v in vs:
            if isinstance(v, V):
                for b in v.bs:
                    if b not in out:
                        out.append(b)
        return out

    @staticmethod
    def _a(v):
        return v.ap if isinstance(v, V) else v

    def mm(self, out, lhsT, rhs, start=True, stop=True):
        o_, l_, r_ = out.ap, lhsT.ap, rhs.ap
        self.op("pe", lambda e: e.matmul(o_, lhsT=l_, rhs=r_, start=start, stop=stop),
                r=self._bs(lhsT, rhs), w=self._bs(out))

    def mm_(self, out, lhsT, rhs, start=True, stop=True, skip=False):
        o_, l_, r_ = out.ap, lhsT.ap, rhs.ap
        self.op("pe", lambda e: e.matmul(o_, lhsT=l_, rhs=r_, start=start, stop=stop, skip_group_check=skip),
                r=self._bs(lhsT, rhs), w=self._bs(out))

    def tr(self, out, in_, ident):
        o_, i_, d_ = out.ap, in_.ap, ident.ap
        self.op("pe", lambda e: e.transpose(o_, i_, d_), r=self._bs(in_, ident), w=self._bs(out))

    def act(self, out, in_, func, bias=None, scale=None, accum=None, eng="act"):
        kw = {}
        if bias is not None:
            kw["bias"] = self._a(bias)
        if scale is not None:
            kw["scale"] = self._a(scale)
        if accum is not None:
            kw["accum_out"] = accum.ap
        o_, i_ = out.ap, in_.ap
        self.op("act", lambda e: e.activation(out=o_, in_=i_, func=func, **kw),
                r=self._bs(in_, bias, scale), w=self._bs(out, accum))

    def tt(self, eng, out, in0, in1, op):
        o_, a_, b_ = out.ap, in0.ap, in1.ap
        self.op(eng, lambda e: e.tensor_tensor(out=o_, in0=a_, in1=b_, op=op),
                r=self._bs(in0, in1), w=self._bs(out))

    def ts(self, eng, out, in0, s1, s2=None, op0=ALU.mult, op1=None):
        o_, a_ = out.ap, in0.ap
        s1_, s2_ = self._a(s1), self._a(s2)
        kw = {}
        if op1 is not None:
            kw["op1"] = op1
        self.op(eng, lambda e: e.tensor_scalar(out=o_, in0=a_, scalar1=s1_, scalar2=s2_, op0=op0, **kw),
                r=self._bs(in0, s1, s2), w=self._bs(out))

    def stt(self, out, in0, scalar, in1, op0, op1):
        o_, a_, b_ = out.ap, in0.ap, in1.ap
        s_ = self._a(scalar)
        self.op("dve", lambda e: e.scalar_tensor_tensor(out=o_, in0=a_, scalar=s_, in1=b_, op0=op0, op1=op1),
                r=self._bs(in0, scalar, in1), w=self._bs(out))

    def copy(self, eng, out, in_):
        o_, i_ = out.ap, in_.ap
        if eng == "act":
            self.op("act", lambda e: e.copy(out=o_, in_=i_), r=self._bs(in_), w=self._bs(out))
        else:
            self.op(eng, lambda e: e.tensor_copy(out=o_, in_=i_), r=self._bs(in_), w=self._bs(out))

    def recip(self, out, in_):
        o_, i_ = out.ap, in_.ap
        self.op("dve", lambda e: e.reciprocal(out=o_, in_=i_), r=self._bs(in_), w=self._bs(out))

    def memset(self, eng, out, val):
        o_ = out.ap
        self.op(eng, lambda e: e.memset(o_, val), r=[], w=self._bs(out))

    def dma(self, out, in_, grp, eng="sp"):
        o_, i_ = self._a(out), self._a(in_)
        self.op(eng, lambda e: e.dma_start(out=o_, in_=i_), r=self._bs(in_), w=self._bs(out), grp=grp)


class Bank:
    def __init__(self, t, q):
        self.t = t
        self.q = q

    def f(self, c0, c1, p0=0, p1=128):
        return V(self.t[p0:p1, c0:c1], self.q[0:1])

    def b(self, c0, c1, p0=0, p1=128):
        return V(self.t[p0:p1, :].bitcast(BF16)[:, c0:c1], self.q[0:1])


W1_SEGS = ((0, 1536, 0), (2048, 2056, 1536), (2568, 3336, 1544))
W1_N = 2312
C_AB, C_KC, C_VC, C_KSL, C_VSL, C_KWN, C_VWN = 1536, 1544, 1672, 1800, 1928, 2056, 2184


class Builder:
    def __init__(self, debug=False, ntiles=NT):
        self.debug = debug
        self.ntiles = ntiles
        import os
        self.stop = int(os.environ.get('K_STOP', '0'))
        self.var = int(os.environ.get('K_VAR', '0'))
        self.halo = int(os.environ.get('K_HALO', '0'))
        self.nc = bass.Bass("TRN2", target_bir_lowering=False)
        self.din = {}
        self.dout = {}

    def inp(self, name, shape, dt=F32):
        self.din[name] = self.nc.dram_tensor(name, list(shape), dt, kind="ExternalInput").ap()
        return self.din[name]

    def outp(self, name, shape, dt=F32):
        self.dout[name] = self.nc.dram_tensor(name, list(shape), dt, kind="ExternalOutput").ap()
        return self.dout[name]

    def scratch(self, name, shape, dt=F32):
        if self.debug:
            return self.outp(name, shape, dt)
        return self.nc.dram_tensor(name, list(shape), dt, kind="Internal").ap()

    def sb(self, es, name, shape, dt=F32, const=False):
        t = es.enter_context(self.nc.sbuf_tensor(name, list(shape), dt))
        return Tl(t, self.S.buf(name, const))

    def cload(self, es, name, shape, grp):
        d = self.inp(name, shape)
        t = self.sb(es, "c_" + name, shape, F32, const=True)
        idx = tuple(slice(None) for _ in shape)
        self.S.dma(t[idx], d[idx], grp)
        return t

    def build(self, upto=9):
        nc = self.nc
        I = self.inp
        self.xb = I("xb", [S_LEN, D])
        self.xo = I("xo", [2048, D])
        cT = I("cT", [128, 8])
        ada_w = I("ada_w", [D, 6 * D])
        self.w_in = I("w_in", [D, N_IN])
        self.out = self.outp("out", [2048, D])
        self.o_own = self.scratch("o_own", [16, 128, 512])
        self.kslT_d = self.scratch("kslT_d", [128, S_LEN], BF16)
        self.kwnT_d = self.scratch("kwnT_d", [128, S_LEN], BF16)
        self.kcT_d = self.scratch("kcT_d", [128, S_LEN], BF16)
        self.vcT_d = self.scratch("vcT_d", [128, S_LEN], BF16)
        self.vsl_d = self.scratch("vsl_d", [128, NB, 128], BF16)
        self.vwn_d = self.scratch("vwn_d", [128, NB, 128], BF16)
        self.mix_d = self.scratch("mix_d", [128, 8, 2048], BF16)
        self.oprev_d = self.scratch("oprev_d", [16, 2, 512])
        self.mixh_d = self.scratch("mixh_d", [128, 8, 32], BF16)
        self.w_up = I("w_up", [D, 2 * DFF])
        self.wup_d = self.scratch("wup_d", [128, 44, 8, 128], BF16)

        with ExitStack() as top:
            S = self.S = Sched(nc, top)
            self.g_const = g_const = S.group("const", full=True)
            self.g_out = S.group("outw")
            self.idf = idf = self.cload(top, "identf", [128, 128], g_const)
            self.idb = idb = self.sb(top, "idb", [128, 128], BF16)
            S.copy("dve", idb[:, :], idf[:, :])
            self.modT = modT = self.sb(top, "modT", [128, 48])
            self.s1 = s1 = self.sb(top, "s1", [128, 8])
            self.s2 = s2 = self.sb(top, "s2", [128, 8])
            n1 = self.cload(top, "n1w", [128, 8], g_const)
            n2 = self.cload(top, "n2w", [128, 8], g_const)
            self.pb = []
            for i in range(8):
                t = top.enter_context(nc.psum_tensor(f"pb{i}", [128, 512], F32))
                self.pb.append(Bank(t, [S.buf(f"pb{i}", excl=True)]))
            pb = self.pb
            self.bankA = [pb[2], pb[3]]
            self.bankB = [pb[4], pb[5]]
            self.bankC = [pb[0], pb[1]]

            with ExitStack() as es:
                ct = self.sb(es, "ct", [128, 8])
                sc = self.sb(es, "sc", [128, 8])
                abT = self.cload(es, "ada_bT", [128, 48], g_const)
                S.dma(ct[:, :], cT[:, :], g_const)
                S.act(sc[:, :], ct[:, :], AF.Silu)
                aw = [self.sb(es, f"aw{i}", [128, 6 * D]) for i in range(2)]
                g_aw = [S.group(f"aw{i}") for i in range(2)]
                for k in range(8):
                    sl = k % 2
                    for hh in range(4):
                        S.dma(aw[sl][:, hh * 1536:(hh + 1) * 1536],
                              ada_w[k * 128:(k + 1) * 128, hh * 1536:(hh + 1) * 1536], g_aw[sl])
                    pm = pb[k % 2]
                    for cc in range(48):
                        S.mm(pm.f(cc, cc + 1), lhsT=aw[sl][:, cc * 128:(cc + 1) * 128], rhs=sc[:, k:k + 1])
                    S.tt("dve", modT[:, :], pm.f(0, 48), (abT if k == 0 else modT)[:, :], ALU.add)
                S.stt(s1[:, :], modT[:, 8:16], 1.0, n1[:, :], ALU.add, ALU.mult)
                S.stt(s2[:, :], modT[:, 32:40], 1.0, n2[:, :], ALU.add, ALU.mult)
                S.flush()

            if upto >= 1:
                self.phase1()
            keep = dict(qnT=self.sb(top, "qnT", [128, 4, 2048], BF16), gates=self.sb(top, "gates", [128, 16, 12]),
                        qnTh=self.sb(top, "qnTh", [128, 4, 32], BF16), gatesh=self.sb(top, "gatesh", [32, 12]))
            if upto >= 2:
                self.phase2(keep)
                if self.debug:
                    S.dma(self.outp("d_qnT", [128, 4, 2048], BF16)[:, :, :], keep["qnT"][:, :, :], self.g_out)
                    S.dma(self.outp("d_gates", [128, 16, 12])[:, :, :], keep["gates"][:, :, :], self.g_out)
            if upto >= 3:
                S.flush()
                self.phase3(keep)
            if upto >= 4:
                S.flush()
                if self.debug:
                    self.dx1 = self.outp("d_x1", [4, 128, 4, D])
                self.phase45()
            S.flush()
        return nc

    def front(self, F, xt, nb, hT, c0, scol, sh_c0, b0=0):
        S = self.S
        pb = self.pb
        ssq, rt, rstd, junk, xs = F["ssq"], F["rt"], F["rstd"], F["junk"], F["xs"]
        for b2 in range(nb):
            S.act(junk[:, :], xt[:, b0 + b2, :], AF.Square, accum=ssq[:, b2:b2 + 1])
        S.act(rt[:, 0:nb], ssq[:, 0:nb], AF.Ln, bias=EPS, scale=1.0 / D)
        S.act(rstd[:, 0:nb], rt[:, 0:nb], AF.Exp, scale=-0.5)
        for b2 in range(nb):
            S.ts("dve", xs[:, b2, :], xt[:, b0 + b2, :], rstd[:, b2:b2 + 1], None, ALU.mult)
        w = nb * 128
        for half in range(2):
            bank = pb[half]
            for k in range(half * 4, half * 4 + 4):
                off = (k % 4) * 256
                for b2 in range(nb):
                    S.tr(bank.b(off + b2 * 128, off + (b2 + 1) * 128), xs[:, b2, k * 128:(k + 1) * 128],
                         self.idb[:, :])
            for k in range(half * 4, half * 4 + 4):
                off = (k % 4) * 256
                S.ts("dve", hT[:, k, c0:c0 + w], bank.b(off, off + w), scol[:, k:k + 1],
                     self.modT[:, sh_c0 + k:sh_c0 + k + 1], ALU.mult, ALU.add)

    def front32(self, F, xt, hT, scol, sh_c0):
        S = self.S
        bank = self.pb[0]
        ssq, rt, rstd, junk, xs = F["ssq"], F["rt"], F["rstd"], F["junk"], F["xs"]
        S.act(junk[0:32, :], xt[:, :], AF.Square, accum=ssq[0:32, 0:1])
        S.act(rt[0:32, 0:1], ssq[0:32, 0:1], AF.Ln, bias=EPS, scale=1.0 / D)
        S.act(rstd[0:32, 0:1], rt[0:32, 0:1], AF.Exp, scale=-0.5)
        S.ts("pool", xs[0:32, 0, :], xt[:, :], rstd[0:32, 0:1], 1.0, ALU.mult, ALU.mult)
        for k in range(8):
            S.tr(bank.b(k * 32, (k + 1) * 32), xs[0:32, 0, k * 128:(k + 1) * 128], self.idb[0:32, 0:32])
        for k in range(8):
            S.ts("dve", hT[:, k, :], bank.b(k * 32, (k + 1) * 32), scol[:, k:k + 1],
                 self.modT[:, sh_c0 + k:sh_c0 + k + 1], ALU.mult, ALU.add)

    def phase1(self):
        S = self.S
        nc = self.nc
        pb = self.pb
        gc_ = S.group("const1", full=True)
        idb, idf = self.idb, self.idf
        with ExitStack() as es:
            tri2 = self.cload(es, "tri2", [128, 128], gc_)
            blk2 = self.cload(es, "blk2", [128, 128], gc_)
            onesf = self.cload(es, "onesf", [128, 128], gc_)
            cind = self.cload(es, "cind", [128, 2], gc_)
            ma4 = self.cload(es, "ma4", [128, 512], gc_)
            mb4 = self.cload(es, "mb4", [128, 512], gc_)
            sel = self.cload(es, "sel", [128, 4], gc_)
            hsel = self.cload(es, "hsel", [128, 4], gc_)
            cw = self.cload(es, "gcw", [128, 48], gc_)
            dtb = self.cload(es, "dtb", [128, 4], gc_)
            alog = self.cload(es, "alog", [128, 4], gc_)
            kslw = self.cload(es, "kslw", [128, 1], gc_)
            kwnw = self.cload(es, "kwnw", [128, 1], gc_)
            negones = self.sb(es, "negones", [128, 128])
            S.ts("dve", negones[:, :], onesf[:, :], -1.0, None, ALU.mult)
            onesb = self.sb(es, "onesb", [128, 128], BF16)
            S.copy("dve", onesb[:, :], onesf[:, :])
            i4b = self.sb(es, "i4b", [128, 512], BF16)
            for h in range(4):
                S.copy("dve", i4b[:, h * 128:(h + 1) * 128], idf[:, :])
            negA = self.sb(es, "negA", [128, 4])
            S.act(negA[:, :], alog[:, :], AF.Exp)
            S.ts("dve", negA[:, :], negA[:, :], -1.0, None, ALU.mult)

            winb = self.sb(es, "winb", [128, 8, W1_N], BF16)
            with ExitStack() as es2:
                wst = [self.sb(es2, f"wst{i}", [128, W1_N]) for i in range(2)]
                g_w = [S.group(f"wst{i}") for i in range(2)]
                for k in range(8):
                    sl = k % 2
                    for (a, b_, o) in W1_SEGS:
                        S.dma(wst[sl][:, o:o + (b_ - a)], self.w_in[k * 128:(k + 1) * 128, a:b_], g_w[sl])
                    S.copy("dve", winb[:, k, 0:1024], wst[sl][:, 0:1024])
                    S.copy("pool", winb[:, k, 1024:W1_N], wst[sl][:, 1024:W1_N])
                S.flush()

            F = dict(ssq=self.sb(es, "f_ssq", [128, 4]), rt=self.sb(es, "f_rt", [128, 4]),
                     rstd=self.sb(es, "f_rstd", [128, 4]), junk=self.sb(es, "f_junk", [128, 1024], BF16),
                     xs=self.sb(es, "f_xs", [128, 2, 1024], BF16))
            xt = [self.sb(es, f"xt{i}", [128, 2, 1024]) for i in range(2)]
            g_x = [S.group(f"xt{i}") for i in range(2)]
            hT = self.sb(es, "hT", [128, 8, 512], BF16)
            pre = [self.sb(es, f"pre{i}", [128, 515]) for i in range(3)]
            hist = self.sb(es, "hist", [128, 12, 3])
            S.memset("pool", hist[:, :, :], 0.0)
            acc = [self.sb(es, f"cacc{i}", [128, 512]) for i in range(3)]
            yfs = [self.sb(es, f"yfs{i}", [128, 512], BF16 if i < 8 else F32) for i in range(10)]
            sq = [self.sb(es, f"sq{i}", [128, 512], BF16) for i in range(3)]
            rtt = [self.sb(es, f"rtt{i}", [128, 512]) for i in range(3)]
            qT = self.sb(es, "qT", [128, 4, 512], BF16)
            kT = self.sb(es, "kT", [128, 4, 512], BF16)
            vT = self.sb(es, "vT", [128, 4, 512], BF16)
            st_f = [[self.sb(es, f"stf{c}_{i}", [128, 512], BF16) for i in range(2)] for c in range(4)]
            st_v = [[self.sb(es, f"stv{c}_{i}", [128, 4, 128], BF16) for i in range(2)] for c in range(2)]
            g_sf = [[S.group(f"sf{c}_{i}") for i in range(2)] for c in range(4)]
            g_sv = [[S.group(f"sv{c}_{i}") for i in range(2)] for c in range(2)]
            ab = self.sb(es, "ab", [128, 4, 8])
            self.sm4 = sm4 = self.sb(es, "sm4", [128, 4, 64])
            S32 = self.sb(es, "S32", [128, 4, 128])
            Sb = [self.sb(es, f"Sb{i}", [128, 4, 128], BF16) for i in range(2)]
            self.cidx = 0
            S.memset("pool", S32[:, :, :], 0.0)
            S.memset("pool", Sb[0][:, :, :], 0.0)
            oacc = [self.sb(es, f"oacc{i}", [128, 512]) for i in range(2)]
            g_o = [S.group(f"oacc{i}") for i in range(2)]
            oph = [self.sb(es, f"oph{i}", [128, 512]) for i in range(2)]
            g_oh = [S.group(f"oph{i}") for i in range(2)]
            G = []
            for s_ in range(2):
                g = {}
                for nm, shp, dt in (("TG", [128, 4, 128], F32), ("Xs", [128, 512], F32),
                                    ("XA", [128, 512], F32), ("XB", [128, 512], F32), ("EG", [128, 4, 128], F32),
                                    ("Nb", [128, 4, 128], BF16), ("aT", [128, 4, 128], BF16),
                                    ("qdT", [128, 4, 128], BF16), ("kw", [128, 4, 128], BF16),
                                    ("kd", [128, 4, 128], BF16), ("vb", [128, 4, 128], BF16),
                                    ("NTb", [128, 4, 128], BF16), ("RTb", [128, 4, 128], BF16),
                                    ("P0", [128, 4, 128], BF16), ("P1", [128, 4, 128], BF16),
                                    ("Q0", [128, 4, 128], BF16), ("Q1", [128, 4, 128], BF16),
                                    ("u", [128, 4, 128], F32), ("wTb", [128, 4, 128], BF16),
                                    ("vn", [128, 4, 128], BF16)):
                    g[nm] = self.sb(es, f"g{s_}_{nm}", shp, dt)
                G.append(g)

            xrows = self.xb.rearrange("(n b p) d -> n p b d", b=2, p=128)

            def load_x(n):
                S.dma(xt[n % 2][:, :, :], xrows[n], g_x[n % 2])

            load_x(0)
            for ti in range(self.ntiles):
                for hf in range(2):
                    n = 2 * ti + hf
                    if n + 1 < 2 * NT:
                        load_x(n + 1)
                    self.front(F, xt[n % 2], 2, hT, hf * 256, self.s1, 0)
                if self.stop == 1:
                    continue
                nsa_ch = ((C_KC, self.kcT_d, None), (C_VC, self.vcT_d, None), (C_KSL, self.kslT_d, kslw),
                          (C_KWN, self.kwnT_d, kwnw))

                def st1(c, pos):
                    bank = pb[2 + (pos % 2)]
                    coff = c * 128 if c < 12 else nsa_ch[c - 12][0]
                    for k in range(8):
                        S.mm(bank.f(0, 512), lhsT=winb[:, k, coff:coff + 128], rhs=hT[:, k, :],
                             start=(k == 0), stop=(k == 7))
                    if c < 12:
                        p_ = pre[pos % 3]
                        S.copy("pool", p_[:, 0:3], hist[:, c, :])
                        S.copy("act", p_[:, 3:515], bank.f(0, 512))
                        S.copy("pool", hist[:, c, :], p_[:, 512:515])
                        a_ = acc[pos % 3]
                        S.ts("pool", a_[:, :], p_[:, 0:512], cw[:, c * 4:c * 4 + 1], 1.0, ALU.mult, ALU.mult)
                        for j in range(1, 4):
                            S.stt(a_[:, :], p_[:, j:j + 512], cw[:, c * 4 + j:c * 4 + j + 1], a_[:, :], ALU.mult, ALU.add)
                    else:
                        ci = c - 12
                        wcol = nsa_ch[ci][2]
                        if wcol is None:
                            S.copy("act", st_f[ci][ti % 2][:, :], bank.f(0, 512))
                            S.dma(nsa_ch[ci][1][:, ti * 512:(ti + 1) * 512], st_f[ci][ti % 2][:, :], g_sf[ci][ti % 2])
                        else:
                            S.copy("act", yfs[c - 6][:, :], bank.f(0, 512))

                def st2(c, pos):
                    h = c % 4
                    if c >= 12:
                        return
                    if c >= 8:
                        S.act(vT[:, h, :], acc[pos % 3][:, :], AF.Silu)
                        return
                    S.act(yfs[c][:, :], acc[pos % 3][:, :], AF.Silu)

                def st3(c, i_):
                    h = c % 4
                    bk2 = pb[4 + (i_ % 2)]
                    yi = c if c < 8 else c - 6
                    S.act(sq[i_ % 3][:, :], yfs[yi][:, :], AF.Square)
                    S.mm(bk2.f(0, 512), lhsT=onesb[:, :], rhs=sq[i_ % 3][:, :])
                    r_ = rtt[i_ % 3]
                    if c < 4:
                        S.act(r_[:, :], bk2.f(0, 512), AF.Ln, bias=EPS * 128.0, scale=128.0)
                    elif c < 8:
                        S.act(r_[:, :], bk2.f(0, 512), AF.Ln, bias=EPS, scale=1.0)
                    else:
                        S.act(r_[:, :], bk2.f(0, 512), AF.Ln, bias=EPS, scale=1.0 / 128.0)
                    S.act(r_[:, :], r_[:, :], AF.Exp, scale=-0.5)
                    if c < 8:
                        S.tt("pool", (qT if c < 4 else kT)[:, h, :], yfs[yi][:, :], r_[:, :], ALU.mult)
                    else:
                        ci = c - 12
                        st = st_f[ci][ti % 2]
                        S.stt(st[:, :], yfs[yi][:, :], nsa_ch[ci][2][:, 0:1], r_[:, :], ALU.mult, ALU.mult)
                        S.dma(nsa_ch[ci][1][:, ti * 512:(ti + 1) * 512], st[:, :], g_sf[ci][ti % 2])

                order = [12, 13, 14, 15] + list(range(12))
                for i in range(len(order) + 1):
                    if i < len(order):
                        st1(order[i], i)
                    if 0 <= i - 1 < len(order):
                        st2(order[i - 1], i - 1)
                if self.stop == 3:
                    continue
                for blk in range(4):
                    bank = pb[6]
                    for vi, coff in enumerate((C_VSL, C_VWN)):
                        for k in range(8):
                            if self.var == 4:
                                break
                            S.mm(bank.f(vi * 128, (vi + 1) * 128), lhsT=hT[:, k, blk * 128:(blk + 1) * 128],
                                 rhs=winb[:, k, coff:coff + 128], start=(k == 0), stop=(k == 7))
                    for k in range(8):
                        if self.var == 1:
                            break
                        S.mm(bank.f(256, 264), lhsT=hT[:, k, blk * 128:(blk + 1) * 128],
                             rhs=winb[:, k, C_AB:C_AB + 8], start=(k == 0), stop=(k == 7))
                    if self.var != 3:
                        S.copy("act", st_v[0][ti % 2][:, blk, :], bank.f(0, 128))
                        S.copy("act", st_v[1][ti % 2][:, blk, :], bank.f(128, 256))
                    if self.var != 5:
                        S.copy("dve", ab[:, blk, :], bank.f(256, 264))
                if self.var != 2:
                    S.dma(self.vsl_d[:, ti * 4:(ti + 1) * 4, :], st_v[0][ti % 2][:, :, :], g_sv[0][ti % 2])
                    S.dma(self.vwn_d[:, ti * 4:(ti + 1) * 4, :], st_v[1][ti % 2][:, :, :], g_sv[1][ti % 2])

                for i_, c in enumerate([14, 15, 0, 1, 2, 3, 4, 5, 6, 7]):
                    st3(c, i_)
                if self.stop == 4:
                    continue
                for pair in range(2):
                    blks = (2 * pair, 2 * pair + 1)
                    if pair == 0:
                        self.gdn_small(sm4, ab, tri2, blk2, onesf, cind, dtb, negA)
                    for s_, blk in enumerate(blks):
                        self.gdn_local_1(G[s_], s_, blk, sm4, tri2, negones, onesf, ma4, mb4, qT, kT, vT)
                    if self.stop == 5:
                        continue
                    self.gdn_solve(G, blks, i4b)
                    if self.stop == 6:
                        continue
                    for s_, blk in enumerate(blks):
                        self.gdn_uw(G[s_], s_)
                    if self.stop == 7:
                        continue
                    for s_, blk in enumerate(blks):
                        oa = oacc[ti % 2]
                        self.gdn_recur(G[s_], S32, Sb, sel, blk, oa, hsel, oph[ti % 2])
                S.dma(self.o_own[ti], oacc[ti % 2][:, :], g_o[ti % 2])
                S.dma(self.oprev_d[ti], oph[ti % 2][126:128, :], g_oh[ti % 2])
            S.flush()

    def gdn_small(self, sm4, ab, tri2, blk2, onesf, cind, dtb, negA):
        S = self.S
        bS = self.pb[6]
        x_, t_, gg, be, nb_ = (sm4[:, :, 0:4], sm4[:, :, 4:8], sm4[:, :, 8:12], sm4[:, :, 12:16], sm4[:, :, 16:20])
        gci, gcv, gl, egc, ekd, bw, glS = (sm4[:, :, 20:28], sm4[:, :, 28:32], sm4[:, :, 32:36], sm4[:, :, 36:40],
                                           sm4[:, :, 40:44], sm4[:, :, 44:48], sm4[:, :, 48:56])
        bc = lambda t: V(t.t[:, :].unsqueeze(1).broadcast_to([128, 4, 4]), [t.b])
        S.tt("dve", x_, ab[:, :, 0:4], bc(dtb), ALU.add)
        S.stt(t_, x_, -1.0, x_, ALU.mult, ALU.max)
        S.act(t_, t_, AF.Exp, scale=-1.0)
        S.act(t_, t_, AF.Ln, bias=1.0)
        S.stt(t_, x_, 0.0, t_, ALU.max, ALU.add)
        S.tt("dve", gg, t_, bc(negA), ALU.mult)
        S.act(be, ab[:, :, 4:8], AF.Exp, scale=-1.0)
        S.ts("dve", be, be, 1.0, None, ALU.add)
        S.recip(be, be)
        S.ts("dve", nb_, be, -1.0, None, ALU.mult)
        for i in range(2):
            for blk in range(4):
                S.ts("dve", sm4[:, blk, 20:28].re("p (h i) -> p h i", i=2)[:, :, i], sm4[:, blk, 8:12], cind[:, i:i + 1],
                     None, ALU.mult)
        for blk in range(4):
            S.mm(bS.f(384 + blk * 4, 388 + blk * 4), lhsT=tri2[:, :], rhs=sm4[:, blk, 8:12])
            S.mm(bS.f(400 + blk * 4, 404 + blk * 4), lhsT=blk2[:, :], rhs=sm4[:, blk, 8:12])
            S.mm(bS.f(416 + blk * 8, 424 + blk * 8), lhsT=onesf[:, :], rhs=sm4[:, blk, 20:28])
        S.copy("dve", gcv, bS.f(384, 400).re("p (b h) -> p b h", b=4))
        S.copy("dve", gl, bS.f(400, 416).re("p (b h) -> p b h", b=4))
        S.act(glS, bS.f(416, 448).re("p (b c) -> p b c", b=4), AF.Exp)
        S.act(egc, gcv, AF.Exp)
        S.tt("dve", ekd, gl, gcv, ALU.subtract)
        S.act(ekd, ekd, AF.Exp)
        S.tt("dve", bw, be, egc, ALU.mult)

    def gdn_local_1(self, g, s_, blk, sm4, tri2, negones, onesf, ma4, mb4, qT, kT, vT):
        S = self.S
        pb = self.pb
        cs = slice(blk * 128, (blk + 1) * 128)
        sm = V(sm4.t[:, blk, :], [sm4.b])
        gg, be, nb_, gcv, ekd, bw = (sm[:, 8:12], sm[:, 12:16], sm[:, 16:20], sm[:, 28:32], sm[:, 40:44], sm[:, 44:48])
        bK = self.bankA[s_]
        bQ = self.bankB[s_]
        bX = self.bankC[s_]
        bT = pb[7]
        for h in range(4):
            S.mm(bK.f(h * 128, (h + 1) * 128), lhsT=kT[:, h, cs], rhs=kT[:, h, cs])
        for h in range(4):
            S.mm(bQ.f(h * 128, (h + 1) * 128), lhsT=kT[:, h, cs], rhs=qT[:, h, cs])
        for h in range(4):
            S.tr(bT.b(h * 128, (h + 1) * 128), kT[:, h, cs], self.idb[:, :])
        for h in range(4):
            S.tr(bT.b(512 + h * 128, 512 + (h + 1) * 128), vT[:, h, cs], self.idb[:, :])
        for h in range(4):
            S.ts("pool", g["TG"][:, h, :], tri2[:, :], gg[:, h:h + 1], 1.0, ALU.mult, ALU.mult)
        for h in range(4):
            S.mm(bX.f(h * 128, (h + 1) * 128), lhsT=onesf[:, :], rhs=g["TG"][:, h, :], start=True, stop=False)
            S.mm(bX.f(h * 128, (h + 1) * 128), lhsT=g["TG"][:, h, :], rhs=negones[:, :], start=False, stop=True)
        S.copy("act", g["Xs"][:, :], bX.f(0, 512))
        for h in range(4):
            S.act(g["EG"][:, h, :], bX.f(h * 128, (h + 1) * 128), AF.Exp, bias=gcv[:, h:h + 1])
        S.tt("pool", g["XA"][:, :], g["Xs"][:, :], ma4[:, :], ALU.add)
        S.tt("pool", g["XB"][:, :], g["Xs"][:, :], mb4[:, :], ALU.add)
        S.act(g["XA"][:, :], g["XA"][:, :], AF.Exp, scale=-1.0)
        S.act(g["XB"][:, :], g["XB"][:, :], AF.Exp)
        for h in range(4):
            S.stt(g["Nb"][:, h, :], bK.f(h * 128, (h + 1) * 128), nb_[:, h:h + 1],
                  g["XA"][:, h * 128:(h + 1) * 128], ALU.mult, ALU.mult)
        S.tt("dve", g["aT"][:, :, :].re("p h c -> p (h c)"), bQ.f(0, 512), g["XB"][:, :], ALU.mult)
        S.tt("pool", g["qdT"][:, :, :], qT[:, :, cs], g["EG"][:, :, :], ALU.mult)
        kt3 = bT.b(0, 512).re("p (h d) -> p h d", h=4)
        vt3 = bT.b(512, 1024).re("p (h d) -> p h d", h=4)
        S.tt("dve", g["kw"][:, :, :], kt3, V(bw.ap.unsqueeze(2).broadcast_to([128, 4, 128]), bw.bs), ALU.mult)
        S.tt("dve", g["kd"][:, :, :], kt3, V(ekd.ap.unsqueeze(2).broadcast_to([128, 4, 128]), ekd.bs), ALU.mult)
        S.tt("dve", g["vb"][:, :, :], vt3, V(be.ap.unsqueeze(2).broadcast_to([128, 4, 128]), be.bs), ALU.mult)

    def gdn_solve(self, G, blks, i4b):
        S = self.S
        idb = self.idb
        for s_ in range(len(blks)):
            g = G[s_]
            bA, bC = self.bankA[s_], self.bankC[s_]
            for h in range(4):
                S.mm(bA.f(h * 128, (h + 1) * 128), lhsT=g["Nb"][:, h, :], rhs=idb[:, :])
            S.copy("act", g["NTb"][:, :, :].re("p h c -> p (h c)"), bA.f(0, 512))
            S.mm(bC.f(0, 512), lhsT=idb[:, :], rhs=i4b[:, :], start=True, stop=False)
            for h in range(4):
                S.mm(bC.f(h * 128, (h + 1) * 128), lhsT=g["Nb"][:, h, :], rhs=idb[:, :], start=False, stop=(h == 3))
            S.copy("act", g["RTb"][:, :, :].re("p h c -> p (h c)"), bC.f(0, 512))
        P = [G[s_]["Nb"] for s_ in range(len(blks))]
        Q = [G[s_]["NTb"] for s_ in range(len(blks))]
        for k in range(1, 6):
            for s_ in range(len(blks)):
                g = G[s_]
                bA, bB, bC = self.bankA[s_], self.bankB[s_], self.bankC[s_]
                Pn = g["P%d" % (k % 2)]
                Qn = g["Q%d" % (k % 2)]
                for h in range(4):
                    S.mm(bA.f(h * 128, (h + 1) * 128), lhsT=Q[s_][:, h, :], rhs=P[s_][:, h, :])
                if k < 5:
                    for h in range(4):
                        S.mm(bB.f(h * 128, (h + 1) * 128), lhsT=P[s_][:, h, :], rhs=Q[s_][:, h, :])
                S.copy("dve", Pn[:, :, :].re("p h c -> p (h c)"), bA.f(0, 512))
                if k < 5:
                    S.copy("act", Qn[:, :, :].re("p h c -> p (h c)"), bB.f(0, 512))
                S.mm(bC.f(0, 512), lhsT=idb[:, :], rhs=g["RTb"][:, :, :].re("p h c -> p (h c)"), start=True, stop=False)
                for h in range(4):
                    S.mm(bC.f(h * 128, (h + 1) * 128), lhsT=Pn[:, h, :], rhs=g["RTb"][:, h, :],
                         start=False, stop=(h == 3))
                S.copy("act", g["RTb"][:, :, :].re("p h c -> p (h c)"), bC.f(0, 512))
                P[s_] = Pn
                Q[s_] = Qn

    def gdn_uw(self, g, s_):
        S = self.S
        bA, bB = self.bankA[s_], self.bankB[s_]
        for h in range(4):
            S.mm(bA.f(h * 128, (h + 1) * 128), lhsT=g["RTb"][:, h, :], rhs=g["vb"][:, h, :])
        for h in range(4):
            S.mm(bB.f(h * 128, (h + 1) * 128), lhsT=g["kw"][:, h, :], rhs=g["RTb"][:, h, :])
        S.copy("act", g["u"][:, :, :].re("p h c -> p (h c)"), bA.f(0, 512))
        S.copy("dve", g["wTb"][:, :, :].re("p h c -> p (h c)"), bB.f(0, 512))

    def gdn_recur(self, g, S32, Sb, sel, blk, oa, hsel, oh):
        S = self.S
        pb = self.pb
        bV, bO, bS_ = pb[2], pb[3], pb[4]
        sm = V(self.sm4.t[:, blk % 4, :], [self.sm4.b])
        for i in range(2):
            r0, r1 = 64 * i, 64 * i + 64
            cur = Sb[self.cidx % 2]
            nxt = Sb[(self.cidx + 1) % 2]
            self.cidx += 1
            for h in range(4):
                S.mm(bV.f(h * 128, (h + 1) * 128, r0, r1), lhsT=g["wTb"][:, h, r0:r1], rhs=cur[:, h, :])
            S.tt("dve", g["vn"][r0:r1, :, :].re("p h c -> p (h c)"), g["u"][r0:r1, :, :].re("p h c -> p (h c)"),
                 bV.f(0, 512, r0, r1), ALU.subtract)
            for h in range(4):
                S.mm(bS_.f(h * 128, (h + 1) * 128), lhsT=g["kd"][r0:r1, h, :], rhs=g["vn"][r0:r1, h, :])
            for h in range(4):
                S.stt(S32[:, h, :], S32[:, h, :], sm[:, 48 + 2 * h + i:49 + 2 * h + i], bS_.f(h * 128, (h + 1) * 128),
                      ALU.mult, ALU.add)
            S.copy("act", nxt[:, :, :], S32[:, :, :])
            for h in range(4):
                S.mm(bO.f(h * 128, (h + 1) * 128, r0, r1), lhsT=g["qdT"][:, h, r0:r1], rhs=cur[:, h, :],
                     start=True, stop=False)
                S.mm(bO.f(h * 128, (h + 1) * 128, r0, r1), lhsT=g["aT"][r0:r1, h, r0:r1], rhs=g["vn"][r0:r1, h, :],
                     start=False, stop=True)
        r_ = blk % 4
        if r_ == 0:
            S.ts("dve", oa[:, :], bO.f(0, 512), sel[:, 0:1], None, ALU.mult)
        else:
            S.stt(oa[:, :], bO.f(0, 512), sel[:, r_:r_ + 1], oa[:, :], ALU.mult, ALU.add)
        if r_ == 0:
            S.ts("dve", oh[64:128, :], bO.f(0, 512, 64, 128), hsel[64:128, 0:1], None, ALU.mult)
        else:
            S.stt(oh[64:128, :], bO.f(0, 512, 64, 128), hsel[64:128, r_:r_ + 1], oh[64:128, :], ALU.mult, ALU.add)

    def phase2(self, keep):
        S = self.S
        pb = self.pb
        gc_ = S.group("const2", full=True)
        qnT, gates = keep["qnT"], keep["gates"]
        with ExitStack() as es:
            qnw = self.cload(es, "qnw", [128, 1], gc_)
            gonw4 = self.cload(es, "gonw4", [128, 512], gc_)
            onesf = self.cload(es, "onesf2", [128, 128], gc_)
            onesb = self.sb(es, "onesb2", [128, 128], BF16)
            S.copy("dve", onesb[:, :], onesf[:, :])
            winb2 = self.sb(es, "winb2", [128, 8, 1036], BF16)
            with ExitStack() as es2:
                wst = [self.sb(es2, f"w2st{i}", [128, 1036]) for i in range(2)]
                g_w = [S.group(f"w2st{i}") for i in range(2)]
                for k in range(8):
                    sl = k % 2
                    for (a, b_, o) in ((O_GZ, O_GZ + 512, 0), (O_NQ, O_NQ + 512, 512), (O_NG, O_NG + 12, 1024)):
                        S.dma(wst[sl][:, o:o + (b_ - a)], self.w_in[k * 128:(k + 1) * 128, a:b_], g_w[sl])
                    S.copy("dve" if k % 2 else "pool", winb2[:, k, :], wst[sl][:, :])
                S.flush()
            F = dict(ssq=self.sb(es, "f2_ssq", [128, 4]), rt=self.sb(es, "f2_rt", [128, 4]),
                     rstd=self.sb(es, "f2_rstd", [128, 4]), junk=self.sb(es, "f2_junk", [128, 1024], BF16),
                     xs=self.sb(es, "f2_xs", [128, 2, 1024], BF16))
            xt = [self.sb(es, f"x2t{i}", [128, 2, 1024]) for i in range(2)]
            g_x = [S.group(f"x2t{i}") for i in range(2)]
            hT = self.sb(es, "h2T_", [128, 8, 512], BF16)
            yf = [self.sb(es, f"y2f{i}", [128, 512]) for i in range(2)]
            sq = [self.sb(es, f"s2q{i}", [128, 512], BF16) for i in range(2)]
            rtt = [self.sb(es, f"r2tt{i}", [128, 512]) for i in range(2)]
            og = [self.sb(es, f"og{i}", [128, 512]) for i in range(2)]
            g_og = [S.group(f"og{i}") for i in range(2)]
            zs = [self.sb(es, f"zs{i}", [128, 512]) for i in range(4)]
            t1 = [self.sb(es, f"p2t{i}", [128, 512]) for i in range(2)]
            mg = [self.sb(es, f"mg{i}", [128, 512], BF16) for i in range(2)]
            mgT = [self.sb(es, f"mgT{i}", [128, 4, 128], BF16) for i in range(2)]
            g_mg = [S.group(f"mgT{i}") for i in range(2)]
            osq = self.sb(es, "osq", [128, 8])
            xrows = self.xo.rearrange("(n b p) d -> n p b d", b=2, p=128)

            def load_x(n):
                S.dma(xt[n % 2][:, :, :], xrows[n], g_x[n % 2])

            load_x(0)
            for t in range(4):
                for hf in range(2):
                    n = 2 * t + hf
                    if n + 1 < 8:
                        load_x(n + 1)
                    self.front(F, xt[n % 2], 2, hT, hf * 256, self.s1, 0)
                for h in range(4):
                    bank = pb[2 + (h % 2)]
                    for k in range(8):
                        S.mm(bank.f(0, 512), lhsT=winb2[:, k, 512 + h * 128:512 + (h + 1) * 128], rhs=hT[:, k, :],
                             start=(k == 0), stop=(k == 7))
                    y_ = yf[h % 2]
                    S.copy("act", y_[:, :], bank.f(0, 512))
                    S.act(sq[h % 2][:, :], bank.f(0, 512), AF.Square)
                    bk2 = pb[4 + (h % 2)]
                    S.mm(bk2.f(0, 512), lhsT=onesb[:, :], rhs=sq[h % 2][:, :])
                    r_ = rtt[h % 2]
                    S.act(r_[:, :], bk2.f(0, 512), AF.Ln, bias=EPS, scale=1.0 / 128.0)
                    S.act(r_[:, :], r_[:, :], AF.Exp, scale=-0.5)
                    S.stt(qnT[:, h, t * 512:(t + 1) * 512], y_[:, :], qnw[:, 0:1], r_[:, :], ALU.mult, ALU.mult)
                for blk in range(4):
                    cs = slice(blk * 128, (blk + 1) * 128)
                    bz = pb[7]
                    for k in range(8):
                        S.mm(bz.f(0, 512), lhsT=hT[:, k, cs], rhs=winb2[:, k, 0:512], start=(k == 0), stop=(k == 7))
                    S.act(zs[blk][:, :], bz.f(0, 512), AF.Silu)
                for blk in range(4):
                    j = 4 * t + blk
                    cs = slice(blk * 128, (blk + 1) * 128)
                    bg = pb[6]
                    for k in range(8):
                        S.mm(bg.f(0, 12), lhsT=hT[:, k, cs], rhs=winb2[:, k, 1024:1036], start=(k == 0), stop=(k == 7))
                    S.act(gates[:, j, :], bg.f(0, 12), AF.Exp, scale=-1.0)
                    S.ts("dve", gates[:, j, :], gates[:, j, :], 1.0, None, ALU.add)
                    S.recip(gates[:, j, :], gates[:, j, :])
                    z_ = zs[blk]
                    o_ = og[j % 2]
                    S.dma(o_[:, :], self.o_own[j], g_og[j % 2])
                    for h in range(4):
                        S.act(t1[j % 2][:, h * 128:(h + 1) * 128], o_[:, h * 128:(h + 1) * 128], AF.Square,
                              accum=osq[:, h:h + 1])
                    S.act(osq[:, 4:8], osq[:, 0:4], AF.Ln, bias=EPS, scale=1.0 / 128.0)
                    S.act(osq[:, 4:8], osq[:, 4:8], AF.Exp, scale=-0.5)
                    rb = V(osq.t[:, 4:8].unsqueeze(2).broadcast_to([128, 4, 128]), [osq.b])
                    S.tt("dve", t1[j % 2][:, :].re("p (h d) -> p h d", h=4), o_[:, :].re("p (h d) -> p h d", h=4), rb,
                         ALU.mult)
                    S.tt("pool", t1[j % 2][:, :], t1[j % 2][:, :], gonw4[:, :], ALU.mult)
                    S.tt("dve", mg[j % 2][:, :], t1[j % 2][:, :], z_[:, :], ALU.mult)
                    bt = pb[0]
                    for c in range(4):
                        S.tr(bt.b(c * 128, (c + 1) * 128), mg[j % 2][:, c * 128:(c + 1) * 128], self.idb[:, :])
                    S.copy("act", mgT[j % 2][:, :, :].re("p c t -> p (c t)"), bt.b(0, 512))
                    S.dma(self.mix_d[:, 0:4, j * 128:(j + 1) * 128], mgT[j % 2][:, :, :], g_mg[j % 2])
            gh = S.group("p2halo", full=True)
            selAB = self.cload(es, "selAB", [128, 2], gh)
            xht = self.sb(es, "xht", [32, D])
            S.dma(xht[:, :], self.inp("xh", [32, D])[:, :], gh)
            ca = self.sb(es, "hca", [32, 512])
            cb = self.sb(es, "hcb", [32, 512])
            S.memset("pool", cb[0:2, :], 0.0)
            opv = self.oprev_d.rearrange("j i c -> (j i) c")
            S.dma(ca[:, :], opv[0:32, :], gh)
            S.dma(cb[2:32, :], opv[0:30, :], gh)
            hTh = self.sb(es, "hTh", [128, 8, 32], BF16)
            self.front32(F, xht, hTh, self.s1, 0)
            qnTh, gatesh = keep["qnTh"], keep["gatesh"]
            bq = pb[2]
            for h in range(4):
                for k in range(8):
                    S.mm(bq.f(h * 32, (h + 1) * 32), lhsT=winb2[:, k, 512 + h * 128:512 + (h + 1) * 128], rhs=hTh[:, k, :],
                         start=(k == 0), stop=(k == 7))
            S.copy("act", yf[0][:, 0:128], bq.f(0, 128))
            S.act(sq[0][:, 0:128], bq.f(0, 128), AF.Square)
            S.mm(pb[4].f(0, 128), lhsT=onesb[:, :], rhs=sq[0][:, 0:128])
            S.act(rtt[0][:, 0:128], pb[4].f(0, 128), AF.Ln, bias=EPS, scale=1.0 / 128.0)
            S.act(rtt[0][:, 0:128], rtt[0][:, 0:128], AF.Exp, scale=-0.5)
            S.stt(qnTh[:, :, :].re("p h q -> p (h q)"), yf[0][:, 0:128], qnw[:, 0:1], rtt[0][:, 0:128], ALU.mult, ALU.mult)
            for k in range(8):
                S.mm(pb[6].f(0, 12, 0, 32), lhsT=hTh[:, k, :], rhs=winb2[:, k, 1024:1036], start=(k == 0), stop=(k == 7))
            S.act(gatesh[:, :], pb[6].f(0, 12, 0, 32), AF.Exp, scale=-1.0)
            S.ts("dve", gatesh[:, :], gatesh[:, :], 1.0, None, ALU.add)
            S.recip(gatesh[:, :], gatesh[:, :])
            for k in range(8):
                S.mm(pb[7].f(0, 512, 0, 32), lhsT=hTh[:, k, :], rhs=winb2[:, k, 0:512], start=(k == 0), stop=(k == 7))
            S.act(zs[0][0:32, :], pb[7].f(0, 512, 0, 32), AF.Silu)
            S.ts("dve", ca[:, :], ca[:, :], selAB[0:32, 0:1], None, ALU.mult)
            S.stt(ca[:, :], cb[:, :], selAB[0:32, 1:2], ca[:, :], ALU.mult, ALU.add)
            for h in range(4):
                S.act(t1[0][0:32, h * 128:(h + 1) * 128], ca[:, h * 128:(h + 1) * 128], AF.Square, accum=osq[0:32, h:h + 1])
            S.act(osq[0:32, 4:8], osq[0:32, 0:4], AF.Ln, bias=EPS, scale=1.0 / 128.0)
            S.act(osq[0:32, 4:8], osq[0:32, 4:8], AF.Exp, scale=-0.5)
            rb = V(osq.t[0:32, 4:8].unsqueeze(2).broadcast_to([32, 4, 128]), [osq.b])
            S.tt("dve", t1[0][0:32, :].re("p (h d) -> p h d", h=4), ca[:, :].re("p (h d) -> p h d", h=4), rb, ALU.mult)
            S.tt("pool", t1[0][0:32, :], t1[0][0:32, :], gonw4[0:32, :], ALU.mult)
            S.tt("dve", mg[0][0:32, :], t1[0][0:32, :], zs[0][0:32, :], ALU.mult)
            for c in range(4):
                S.tr(pb[0].b(c * 32, (c + 1) * 32), mg[0][0:32, c * 128:(c + 1) * 128], self.idb[0:32, 0:32])
            mghT = self.sb(es, "mghT", [128, 4, 32], BF16)
            S.copy("act", mghT[:, :, :].re("p c t -> p (c t)"), pb[0].b(0, 128))
            S.dma(self.mixh_d[:, 0:4, :], mghT[:, :, :], S.group("mghT"))
            S.flush()

    def phase3(self, keep):
        S = self.S
        pb = self.pb
        idb = self.idb
        qnT, gates = keep["qnT"], keep["gates"]
        SC = 128.0 ** -0.5
        with ExitStack() as es:
            gc_ = S.group("const3", full=True)
            kcmpT = self.sb(es, "kcmpT", [128, 512], BF16)
            vcx = self.sb(es, "vcx", [128, 4, 129], BF16)
            onesf = self.cload(es, "onesf3", [128, 128], gc_)
            onesb = self.sb(es, "onesb3", [128, 128], BF16)
            S.copy("dve", onesb[:, :], onesf[:, :])
            with ExitStack() as e2:
                g2 = S.group("const3b", full=True)
                kcw = self.cload(e2, "kcw", [128, 1], g2)
                pm511 = self.cload(e2, "pm511", [128, 1], g2)
                for which, src_d, w1n, w2n, posn in (("k", self.kcT_d, "cmp_k_w1", "cmp_k_w2", "cmp_k_posT"),
                                                    ("v", self.vcT_d, "cmp_v_w1", "cmp_v_w2", "cmp_v_posT")):
                    with ExitStack() as e3:
                        g3 = S.group("c3" + which, full=True)
                        xT = self.sb(e3, "cx" + which, [128, S_LEN], BF16)
                        S.dma(xT[:, 0:4096], src_d[:, 0:4096], g3)
                        S.dma(xT[:, 4096:8192], src_d[:, 4096:8192], g3)
                        w1d = self.inp(w1n, [128, 32, 128])
                        w1f = self.sb(e3, "w1f" + which, [128, 32, 128])
                        S.dma(w1f[:, 0:16, :], w1d[:, 0:16, :], g3)
                        S.dma(w1f[:, 16:32, :], w1d[:, 16:32, :], g3)
                        w1b = self.sb(e3, "w1b" + which, [128, 32, 128], BF16)
                        S.copy("dve", w1b[:, 0:16, :], w1f[:, 0:16, :])
                        S.copy("pool", w1b[:, 16:32, :], w1f[:, 16:32, :])
                        w2f = self.cload(e3, w2n, [128, 128], g3)
                        w2b = self.sb(e3, "w2b" + which, [128, 128], BF16)
                        S.copy("dve", w2b[:, :], w2f[:, :])
                        posf = self.cload(e3, posn, [128, 32], g3)
                        posb = self.sb(e3, "posb" + which, [128, 32], BF16)
                        S.copy("dve", posb[:, :], posf[:, :])
                        hid = self.sb(e3, "hid" + which, [128, 512], BF16)
                        bcol = self.sb(e3, "bcol" + which, [128, 1])
                        S.memset("pool", hid[:, :], 0.0)
                        bh, bb = pb[2], pb[3]
                        x3 = xT[:, :].re("p (n s) -> p n s", s=16)
                        for l in range(32):
                            S.mm(bh.f(0, 511), lhsT=w1b[:, l, :], rhs=x3[:, l // 16:l // 16 + 511, l % 16],
                                 start=(l == 0), stop=(l == 31))
                        for l in range(32):
                            S.mm(bb.f(0, 1), lhsT=w1b[:, l, :], rhs=posb[:, l:l + 1], start=(l == 0), stop=(l == 31))
                        S.copy("dve", bcol[:, :], bb.f(0, 1))
                        S.act(hid[:, 0:511], bh.f(0, 511), AF.Silu, bias=bcol[:, 0:1])
                        if which == "k":
                            bk = pb[4]
                            S.mm(bk.f(0, 512), lhsT=w2b[:, :], rhs=hid[:, :])
                            yk = self.sb(e3, "yk", [128, 512])
                            sqk = self.sb(e3, "sqk", [128, 512], BF16)
                            rk = self.sb(e3, "rk", [128, 512])
                            S.copy("act", yk[:, :], bk.f(0, 512))
                            S.act(sqk[:, :], bk.f(0, 512), AF.Square)
                            S.mm(pb[5].f(0, 512), lhsT=onesb[:, :], rhs=sqk[:, :])
                            S.act(rk[:, :], pb[5].f(0, 512), AF.Sqrt, bias=EPS, scale=1.0 / 128.0)
                            S.recip(rk[:, :], rk[:, :])
                            S.stt(kcmpT[:, :], yk[:, :], kcw[:, 0:1], rk[:, :], ALU.mult, ALU.mult)
                            S.memset("dve", kcmpT[:, 511:512], 0.0)
                        else:
                            bv = pb[4]
                            for nt in range(4):
                                S.mm(bv.f(nt * 128, (nt + 1) * 128), lhsT=hid[:, nt * 128:(nt + 1) * 128], rhs=w2b[:, :])
                            S.memset("pool", vcx[:, :, 128:129], 1.0)
                            S.copy("act", vcx[:, :, 0:128], bv.f(0, 512).re("p (n d) -> p n d", n=4))
                            S.ts("dve", vcx[:, 3, :], vcx[:, 3, :], pm511[:, 0:1], None, ALU.mult)
                        S.flush()
            if self.debug:
                S.dma(self.outp("d_kcmpT", [128, 512], BF16)[:, :], kcmpT[:, :], self.g_out)
                S.dma(self.outp("d_vcx", [128, 4, 129], BF16)[:, :, :], vcx[:, :, :], self.g_out)
            if self.stop == 31:
                S.flush()
                return
            kslT = self.sb(es, "kslT", [128, S_LEN], BF16)
            kwnT = self.sb(es, "kwnT", [128, S_LEN], BF16)
            vslx = self.sb(es, "vslx", [128, NB, 129], BF16)
            vwnx = self.sb(es, "vwnx", [128, NB, 129], BF16)
            for i in range(2):
                S.dma(kslT[:, i * 4096:(i + 1) * 4096], self.kslT_d[:, i * 4096:(i + 1) * 4096], gc_)
                S.dma(kwnT[:, i * 4096:(i + 1) * 4096], self.kwnT_d[:, i * 4096:(i + 1) * 4096], gc_)
                S.dma(vslx[:, i * 32:(i + 1) * 32, 0:128], self.vsl_d[:, i * 32:(i + 1) * 32, :], gc_)
                S.dma(vwnx[:, i * 32:(i + 1) * 32, 0:128], self.vwn_d[:, i * 32:(i + 1) * 32, :], gc_)
            S.memset("pool", vslx[:, :, 128:129], 1.0)
            S.memset("pool", vwnx[:, :, 128:129], 1.0)
            eall = self.sb(es, "eallb", [128, S_LEN], BF16)
            addm = self.cload(es, "addm", [128, 16, 128], gc_)
            ovb = self.sb(es, "ovb", [128, 4, 128], BF16)
            cmbb = self.sb(es, "cmbb", [128, 16, 2, 128], BF16)
            cbsb = self.sb(es, "cbsb", [128, 4, 4, 128], BF16)
            cbwb = self.sb(es, "cbwb", [128, 8, 4, 128], BF16)
            with ExitStack() as e2:
                g2 = S.group("const3c", full=True)
                ead = self.inp("eall", [128, S_LEN])
                est = [self.sb(e2, f"est{i}", [128, 2048]) for i in range(2)]
                g_e = [S.group(f"est{i}") for i in range(2)]
                for i in range(4):
                    S.dma(est[i % 2][:, :], ead[:, i * 2048:(i + 1) * 2048], g_e[i % 2])
                    S.copy("pool" if i % 2 else "dve", eall[:, i * 2048:(i + 1) * 2048], est[i % 2][:, :])
                ovf = self.cload(e2, "ovm", [128, 4, 128], g2)
                S.copy("dve", ovb[:, :, :], ovf[:, :, :])
                cmf = self.cload(e2, "cmb", [128, 16, 2, 128], g2)
                S.copy("pool", cmbb[:, :, :, :], cmf[:, :, :, :])
                cbsf = self.cload(e2, "cbs", [128, 4, 128], g2)
                cbwf = self.cload(e2, "cbw", [128, 8, 128], g2)
                for h in range(4):
                    S.copy("dve", cbsb[:, :, h, :], cbsf[:, :, :])
                    S.copy("pool", cbwb[:, :, h, :], cbwf[:, :, :])
                S.flush()
            Pc = [self.sb(es, f"Pc{i}", [128, 512], BF16) for i in range(4)]
            Pk = [self.sb(es, f"Pk{i}", [128, 512], BF16) for i in range(3)]
            cmrep = [self.sb(es, f"cmrep{i}", [128, 4, 128], BF16) for i in range(2)]
            ob = {nm: self.sb(es, "ob_" + nm, [128, 4, 129]) for nm in ("c", "s", "w")}
            rs = self.sb(es, "rs", [128, 12])
            cf = self.sb(es, "cf", [128, 12])
            imp = self.sb(es, "imp", [128, 128])
            imp2 = self.sb(es, "imp2", [128, 128])
            m8 = self.sb(es, "m8", [128, 16])
            selm = self.sb(es, "selm", [128, 128])
            biasT = self.sb(es, "biasT", [128, 4, 128], BF16)
            mixn = [self.sb(es, f"mixn{i}", [128, 4, 128], BF16) for i in range(2)]
            tmpn = self.sb(es, "tmpn", [128, 128])
            mnT = [self.sb(es, f"mnT{i}", [128, 4, 128], BF16) for i in range(2)]
            g_mn = [S.group(f"mnT{i}") for i in range(2)]
            bS = [pb[0], pb[1]]
            bO = [pb[2], pb[3]]
            bI, bT = pb[4], pb[5]
            si = [0]

            def nsa_part1(s_, Q, qv, cmp_plan, addv):
                NQ = 4 * Q

                ncp = len(cmp_plan)
                for i_, (nt, mk) in enumerate(cmp_plan):
                    bank = bS[i_ % 2]
                    S.mm(bank.f(0, NQ), lhsT=kcmpT[:, nt * 128:(nt + 1) * 128], rhs=qv, start=True, stop=(mk is None))
                    if mk is not None:
                        S.mm(bank.f(0, NQ), lhsT=idb[:, :], rhs=mk, start=False, stop=True)
                    S.act(Pc[i_][:, 0:NQ], bank.f(0, NQ), AF.Exp, scale=SC)
                for h in range(4):
                    for i_, (nt, mk) in enumerate(cmp_plan):
                        S.mm_(bOc[h // 2].f((h % 2) * 129, (h % 2) * 129 + 129, 0, Q), lhsT=Pc[i_][:, h * Q:(h + 1) * Q],
                              rhs=vcx[:, nt, :], start=(i_ == 0 and h % 2 == 0), stop=(i_ == ncp - 1), skip=True)
                for h in range(4):
                    for i_, (nt, mk) in enumerate(cmp_plan):
                        S.mm_(bI.f(h * 128, (h + 1) * 128, 0, Q), lhsT=Pc[i_][:, h * Q:(h + 1) * Q], rhs=ovb[:, nt, :],
                              start=(i_ == 0 and h == 0), stop=(i_ == ncp - 1), skip=True)
                for hh in range(2):
                    S.copy("act", obc[s_][0:Q, 2 * hh:2 * hh + 2, :], bOc[hh].f(0, 258, 0, Q).re("p (h c) -> p h c", h=2))
                S.ts("dve", rsc[s_][0:Q, 0:4], obc[s_][0:Q, :, 128], 1e-30, None, ALU.max)
                S.recip(rsc[s_][0:Q, 0:4], rsc[s_][0:Q, 0:4])
                S.ts("dve", imp[0:Q, :], bI.f(0, 128, 0, Q), rsc[s_][0:Q, 0:1], None, ALU.mult)
                for h in range(1, 4):
                    S.stt(imp[0:Q, :], bI.f(h * 128, (h + 1) * 128, 0, Q), rsc[s_][0:Q, h:h + 1], imp[0:Q, :], ALU.mult, ALU.add)
                S.tt("pool", imp[0:Q, :], imp[0:Q, :], addv, ALU.add)
                self.max8(m8[0:Q, 0:8], imp[0:Q, :])
                self.match_replace(imp2[0:Q, :], m8[0:Q, 0:8], imp[0:Q, :], -3.0e38)
                self.max8(m8[0:Q, 8:16], imp2[0:Q, :])
                S.ts("dve", selm[0:Q, :], imp[0:Q, :], m8[0:Q, 15:16], None, ALU.is_ge)
                S.mm(bT.f(0, Q), lhsT=selm[0:Q, :], rhs=self.idf[0:Q, 0:Q])
                S.ts("dve", bT2s[s_][:, 0:NQ].re("p (h q) -> p h q", h=4),
                     V(bT.t[:, 0:Q].unsqueeze(1).broadcast_to([128, 4, Q]), bT.q[0:1]), BIG, -BIG, ALU.mult, ALU.add)

            def nsa_part2(s_, Q, qv, slc_plan, win_plan, gatev, store):
                NQ = 4 * Q

                def scores(kT_tile, extra):
                    bank = bS[si[0] % 2]
                    out_ = Pk[si[0] % 3]
                    si[0] += 1
                    S.mm(bank.f(0, NQ), lhsT=kT_tile, rhs=qv, start=True, stop=(len(extra) == 0))
                    for i_, (l_, r_) in enumerate(extra):
                        S.mm(bank.f(0, NQ), lhsT=l_, rhs=r_, start=False, stop=(i_ == len(extra) - 1))
                    S.act(out_[:, 0:NQ], bank.f(0, NQ), AF.Exp, scale=SC)
                    return out_

                def pv(P_, vx, first, last):
                    for h in range(4):
                        S.mm_(bO[h // 2].f((h % 2) * 129, (h % 2) * 129 + 129, 0, Q), lhsT=P_[:, h * Q:(h + 1) * Q],
                              rhs=vx, start=(first and h % 2 == 0), stop=last, skip=True)

                def evac_o(dst):
                    for hh in range(2):
                        S.copy("act", dst[0:Q, 2 * hh:2 * hh + 2, :], bO[hh].f(0, 258, 0, Q).re("p (h c) -> p h c", h=2))

                brhs = bT2s[s_][:, 0:NQ]
                S.copy("dve", rs[0:Q, 0:4], rsc[s_][0:Q, 0:4])
                prev = None
                for i_, (kt, extra) in enumerate(slc_plan):
                    ex = [(eall[:, kt * 128:(kt + 1) * 128], brhs)] + extra
                    P_ = scores(kslT[:, kt * 128:(kt + 1) * 128], ex)
                    if prev is not None:
                        pv(prev[0], vslx[:, prev[1], :], prev[2] == 0, False)
                    prev = (P_, kt, i_)
                pv(prev[0], vslx[:, prev[1], :], prev[2] == 0, True)
                evac_o(ob["s"])
                prev = None
                for i_, (kt, mk) in enumerate(win_plan):
                    P_ = scores(kwnT[:, kt * 128:(kt + 1) * 128], [(idb[:, :], mk)])
                    if prev is not None:
                        pv(prev[0], vwnx[:, prev[1], :], prev[2] == 0, False)
                    prev = (P_, kt, i_)
                pv(prev[0], vwnx[:, prev[1], :], prev[2] == 0, True)
                evac_o(ob["w"])
                S.ts("dve", rs[0:Q, 4:8], ob["s"][0:Q, :, 128], 1e-30, None, ALU.max)
                S.ts("dve", rs[0:Q, 8:12], ob["w"][0:Q, :, 128], 1e-30, None, ALU.max)
                S.recip(rs[0:Q, 4:12], rs[0:Q, 4:12])
                g3 = gatev.re("p (h g) -> p g h", g=3)
                S.tt("dve", cf[0:Q, :].re("p (g h) -> p g h", g=3), rs[0:Q, :].re("p (g h) -> p g h", g=3), g3, ALU.mult)
                mx = mixn[si[0] % 2]
                for h in range(4):
                    S.ts("pool", tmpn[0:Q, :], obc[s_][0:Q, h, 0:128], cf[0:Q, h:h + 1], 1.0, ALU.mult, ALU.mult)
                    S.stt(tmpn[0:Q, :], ob["s"][0:Q, h, 0:128], cf[0:Q, 4 + h:5 + h], tmpn[0:Q, :], ALU.mult, ALU.add)
                    S.stt(mx[0:Q, h, :], ob["w"][0:Q, h, 0:128], cf[0:Q, 8 + h:9 + h], tmpn[0:Q, :], ALU.mult, ALU.add)
                for h in range(4):
                    S.tr(bT.b(h * Q, (h + 1) * Q), mx[0:Q, h, :], idb[0:Q, 0:Q])
                store(bT.b(0, 4 * Q))

            bT2s = [self.sb(es, f"bT2_{i}", [128, 512], BF16) for i in range(2)]
            obc = [self.sb(es, f"obc{i}", [128, 4, 129]) for i in range(2)]
            rsc = [self.sb(es, f"rsc{i}", [128, 4]) for i in range(2)]
            bOc = [pb[6], pb[7]]
            plans = []
            for j in range(16):
                qv = qnT[:, :, j * 128:(j + 1) * 128]
                nt_hi = min(3, (32 * j + 30) // 128)
                lo = max(0, 32 * j - 1) // 128
                cmp_plan = []
                for nt in range(nt_hi + 1):
                    mk = None
                    if nt >= lo:
                        mk = nt - lo
                    cmp_plan.append((nt, mk))
                slc_plan = []
                for kt in range(4 * j + 4):
                    ex = []
                    if kt >= 4 * j:
                        ex.append((idb[:, :], cbsb[:, kt - 4 * j, :, :].re("p h q -> p (h q)")))
                    slc_plan.append((kt, ex))
                win_plan = [(4 * j - 4 + e, cbwb[:, e, :, :].re("p h q -> p (h q)")) for e in range(8) if 4 * j - 4 + e >= 0]

                def store(src, j=j):
                    S.copy("act", mnT[j % 2][:, :, :].re("p c t -> p (c t)"), src)
                    S.dma(self.mix_d[:, 4:8, j * 128:(j + 1) * 128], mnT[j % 2][:, :, :], g_mn[j % 2])

                plans.append((qv, cmp_plan, slc_plan, win_plan, store))
            def emit1(j):
                qv, cmp_plan, _, _, _ = plans[j]
                cp = []
                for nt, slot in cmp_plan:
                    mk = None
                    if slot is not None:
                        S.copy("pool", cmrep[slot][:, :, :],
                               V(cmbb.t[:, j, slot, :].unsqueeze(1).broadcast_to([128, 4, 128]), [cmbb.b]))
                        mk = cmrep[slot][:, :, :].re("p h q -> p (h q)")
                    cp.append((nt, mk))
                nsa_part1(j % 2, 128, qv, cp, addm[:, j, :])

            def emit2(j):
                qv, _, slc_plan, win_plan, store = plans[j]
                nsa_part2(j % 2, 128, qv, slc_plan, win_plan, gates[:, j, :], store)

            with ExitStack() as ew:
                wps = [self.sb(ew, f"wps{i}", [128, 1408]) for i in range(2)]
                wpb = [self.sb(ew, f"wpb{i}", [128, 1408], BF16) for i in range(2)]
                g_ws = [S.group(f"wps{i}") for i in range(2)]
                g_wb = [S.group(f"wpb{i}") for i in range(2)]

                def wload(n):
                    k, q = n // 4, n % 4
                    S.dma(wps[n % 2][:, :], self.w_up[k * 128:(k + 1) * 128, q * 1408:(q + 1) * 1408], g_ws[n % 2])

                def wconv(n):
                    k, q = n // 4, n % 4
                    S.copy("pool", wpb[n % 2][:, :], wps[n % 2][:, :])
                    S.dma(self.wup_d[:, 11 * q:11 * q + 11, k, :], wpb[n % 2][:, :].re("p (c n) -> p c n", n=128),
                          g_wb[n % 2])
                    if n + 2 < 32:
                        wload(n + 2)

                wload(0)
                wload(1)

                def extra(j):
                    wconv(2 * j)
                    wconv(2 * j + 1)

                self._p3_emit(emit1, emit2, extra=extra)
                S.flush()
            self.nsa_halo(es, (nsa_part1, nsa_part2), keep, idb)
            S.flush()

    @staticmethod
    def _p3_emit(emit1, emit2, n=16, extra=None):
        emit1(0)
        for j in range(n):
            if j + 1 < n:
                emit1(j + 1)
            if extra is not None:
                extra(j)
            emit2(j)

    def nsa_halo(self, es, nsa_parts, keep, idb):
        S = self.S
        with ExitStack() as e2:
            g2 = S.group("const3h", full=True)
            haddm = self.cload(e2, "haddm", [32, 128], g2)
            hcmb = self.sb(e2, "hcmb", [128, 4, 128], BF16)
            hsmb = self.sb(e2, "hsmb", [128, 64, 128], BF16)
            hwmb = self.sb(e2, "hwmb", [128, 64, 128], BF16)
            hcf = self.cload(e2, "hcm", [128, 4, 128], g2)
            S.copy("dve", hcmb[:, :, :], hcf[:, :, :])
            st = [self.sb(e2, f"hst{i}", [128, 8, 128]) for i in range(2)]
            g_s = [S.group(f"hst{i}") for i in range(2)]
            n = 0
            for nm, dst in (("hsm", hsmb), ("hwm", hwmb)):
                src = self.inp(nm, [128, 64, 128])
                for i in range(8):
                    S.dma(st[n % 2][:, :, :], src[:, i * 8:(i + 1) * 8, :], g_s[n % 2])
                    S.copy("pool" if n % 2 else "dve", dst[:, i * 8:(i + 1) * 8, :], st[n % 2][:, :, :])
                    n += 1
            mnTh = self.sb(e2, "mnTh", [128, 4, 32], BF16)
            g_m = S.group("mnTh")
            cmp_plan = [(nt, hcmb[:, nt, :]) for nt in range(4)]
            slc_plan = [(kt, [(idb[:, :], hsmb[:, kt, :])]) for kt in range(NB)]
            win_plan = [(kt, hwmb[:, kt, :]) for kt in range(NB)]

            def store(src):
                S.copy("act", mnTh[:, :, :].re("p c t -> p (c t)"), src)
                S.dma(self.mixh_d[:, 4:8, :], mnTh[:, :, :], g_m)

            part1, part2 = nsa_parts
            part1(0, 32, keep["qnTh"][:, :, :], cmp_plan, haddm[:, :])
            part2(0, 32, keep["qnTh"][:, :, :], slc_plan, win_plan, keep["gatesh"][:, :], store)
            S.flush()

    def max8(self, out, in_):
        o_, i_ = out.ap, in_.ap
        self.S.op("dve", lambda e: e.max(out=o_, in_=i_), r=self.S._bs(in_), w=self.S._bs(out))

    def match_replace(self, out, rep, vals, imm):
        o_, r_, v_ = out.ap, rep.ap, vals.ap
        self.S.op("dve", lambda e: e.match_replace(out=o_, in_to_replace=r_, in_values=v_, imm_value=imm),
                  r=self.S._bs(rep, vals), w=self.S._bs(out))

    def phase45(self):
        S = self.S
        pb = self.pb
        idb = self.idb
        w_out = self.inp("w_out", [D, D])
        w_dn = self.inp("w_dn", [DFF, D])
        wup_d = self.wup_d
        with ExitStack() as es:
            gc_ = S.group("const4", full=True)
            fcw = self.cload(es, "fcw", [128, 44 * 3], gc_)
            fcb = self.cload(es, "fcb", [128, 44], gc_)
            wdnb = self.sb(es, "wdnb", [128, 22, D], BF16)
            woutb = self.sb(es, "woutb", [128, 8, D], BF16)
            g1row = self.sb(es, "g1row", [128, D])
            g2row = self.sb(es, "g2row", [128, D])
            with ExitStack() as e2:
                stg = [self.sb(e2, f"wst4_{i}", [128, 2, D]) for i in range(2)]
                g_s = [S.group(f"wst4_{i}") for i in range(2)]
                n = 0
                for (src, dst, nch) in ((w_dn, wdnb, 22), (w_out, woutb, 8)):
                    sv = src.rearrange("(c p) d -> p c d", p=128)
                    for c0 in range(0, nch, 2):
                        st = stg[n % 2]
                        S.dma(st[:, :, :], sv[:, c0:c0 + 2, :], g_s[n % 2])
                        S.copy(("dve", "pool", "act")[n % 3], dst[:, c0:c0 + 2, :], st[:, :, :])
                        n += 1
                gb = self.sb(e2, "gbc", [128, 128])
                for gi, (col0, dst) in enumerate(((16, g1row), (40, g2row))):
                    for cc in range(8):
                        S.copy("dve", gb[:, :], V(self.modT.t[:, col0 + cc:col0 + cc + 1].broadcast_to([128, 128]),
                                                  [self.modT.b]))
                        bank = pb[cc % 2]
                        S.mm(bank.f(0, 128), lhsT=gb[:, :], rhs=self.idf[:, :])
                        S.copy("act", dst[:, cc * 128:(cc + 1) * 128], bank.f(0, 128))
                S.flush()
            F = dict(ssq=self.sb(es, "f4_ssq", [128, 4]), rt=self.sb(es, "f4_rt", [128, 4]),
                     rstd=self.sb(es, "f4_rstd", [128, 4]), junk=self.sb(es, "f4_junk", [128, 1024], BF16),
                     xs=self.sb(es, "f4_xs", [128, 2, 1024], BF16))
            x1 = [self.sb(es, f"x1_{i}", [128, 4, D]) for i in range(2)]
            g_x = [S.group(f"x1_{i}") for i in range(2)]
            mixt = self.sb(es, "mixt", [128, 8, 512], BF16)
            g_m = S.group("mixt")
            h2T = self.sb(es, "h2T", [128, 8, 512], BF16)
            gT = self.sb(es, "gT", [128, 22, 512], BF16)
            wch = [self.sb(es, f"wch{i}", [128, 2, 8, 128], BF16) for i in range(3)]
            g_wc = [S.group(f"wch{i}") for i in range(3)]
            upa = [self.sb(es, f"upa{i}", [128, 4, 130]) for i in range(2)]
            upb = [self.sb(es, f"upb{i}", [128, 4, 130]) for i in range(2)]
            for u_ in upa + upb:
                S.memset("pool", u_[:, :, :], 0.0)
            aa = [self.sb(es, f"aa{i}", [128, 4, 128]) for i in range(2)]
            ab_ = [self.sb(es, f"ab_{i}", [128, 4, 128]) for i in range(2)]
            tmp = [self.sb(es, f"tmp4_{i}", [128, 512]) for i in range(2)]
            xrows = self.xo.rearrange("(t b p) d -> t p b d", b=4, p=128)
            orows = self.out.rearrange("(t b p) d -> t p b d", b=4, p=128)
            wi = 0
            gh = S.group("p4halo", full=True)
            hexr = self.cload(es, "hexr", [128, 32], gh)
            x1h = self.sb(es, "x1h", [32, D])
            mixh = self.sb(es, "mixh", [128, 8, 32], BF16)
            S.dma(x1h[:, :], self.din["xh"][:, :], gh)
            S.dma(mixh[:, :, :], self.mixh_d[:, :, :], gh)
            h2Th = self.sb(es, "h2Th", [128, 8, 32], BF16)
            for half in range(2):
                bank = pb[2 + half]
                hs = slice(half * 512, (half + 1) * 512)
                for c in range(8):
                    S.mm(bank.f(0, 512, 0, 32), lhsT=mixh[:, c, :], rhs=woutb[:, c, hs], start=(c == 0), stop=(c == 7))
                S.tt("dve", tmp[half][0:32, :], bank.f(0, 512, 0, 32), g1row[0:32, hs], ALU.mult)
                S.tt("pool", x1h[:, hs], x1h[:, hs], tmp[half][0:32, :], ALU.add)
            self.front32(F, x1h, h2Th, self.s2, 24)

            def conv(u_, c, dst):
                S.ts("pool", dst[:, :, :], u_[:, :, 2:130], fcw[:, c * 3 + 2:c * 3 + 3], fcb[:, c:c + 1], ALU.mult, ALU.add)
                S.stt(dst[:, :, :], u_[:, :, 1:129], fcw[:, c * 3 + 1:c * 3 + 2], dst[:, :, :], ALU.mult, ALU.add)
                S.stt(dst[:, :, :], u_[:, :, 0:128], fcw[:, c * 3:c * 3 + 1], dst[:, :, :], ALU.mult, ALU.add)

            for t in range(4):
                xx = x1[t % 2]
                S.dma(xx[:, :, :], xrows[t], g_x[t % 2])
                S.dma(mixt[:, :, :], self.mix_d[:, :, t * 512:(t + 1) * 512], g_m)
                for blk in range(4):
                    cs = slice(blk * 128, (blk + 1) * 128)
                    for half in range(2):
                        bank = pb[2 + half]
                        hs = slice(half * 512, (half + 1) * 512)
                        for c in range(8):
                            S.mm(bank.f(0, 512), lhsT=mixt[:, c, cs], rhs=woutb[:, c, hs], start=(c == 0), stop=(c == 7))
                        tm = tmp[half]
                        S.tt("dve", tm[:, :], bank.f(0, 512), g1row[:, hs], ALU.mult)
                        S.tt("pool", xx[:, blk, hs], xx[:, blk, hs], tm[:, :], ALU.add)
                if self.debug:
                    S.dma(self.dx1[t], xx[:, :, :], self.g_out)
                for hf in range(2):
                    self.front(F, xx, 2, h2T, hf * 256, self.s2, 24, b0=2 * hf)
                for c in range(22):
                    w_ = wch[wi % 3]
                    S.dma(w_[:, 0, :, :], wup_d[:, c, :, :], g_wc[wi % 3])
                    S.dma(w_[:, 1, :, :], wup_d[:, 22 + c, :, :], g_wc[wi % 3])
                    wi += 1
                    ua, ub = upa[c % 2], upb[c % 2]
                    for i_, (u_, bank) in enumerate(((ua, pb[4]), (ub, pb[5]))):
                        for k in range(8):
                            S.mm(bank.f(0, 512), lhsT=w_[:, i_, k, :], rhs=h2T[:, k, :], start=(k == 0), stop=(k == 7))
                        S.copy("act", u_[:, :, 2:130], bank.f(0, 512).re("p (b t) -> p b t", b=4))
                        bh_ = pb[6 + i_]
                        for k in range(8):
                            S.mm(bh_.f(0, 8), lhsT=w_[:, i_, k, :], rhs=h2Th[:, k, 8 * t:8 * t + 8], start=(k == 0),
                                 stop=(k == 7))
                        S.tt("dve", u_[:, :, 0:2], bh_.f(0, 8).re("p (b i) -> p b i", i=2),
                             hexr[:, 8 * t:8 * t + 8].re("p (b i) -> p b i", i=2), ALU.mult)
                    conv(ua, c, aa[c % 2])
                    conv(ub, 22 + c, ab_[c % 2])
                    S.act(aa[c % 2][:, :, :], aa[c % 2][:, :, :], AF.Silu)
                    S.tt("dve", gT[:, c, :].re("p (b t) -> p b t", b=4), aa[c % 2][:, :, :], ab_[c % 2][:, :, :], ALU.mult)
                for blk in range(4):
                    cs = slice(blk * 128, (blk + 1) * 128)
                    for half in range(2):
                        bank = pb[6 + half]
                        hs = slice(half * 512, (half + 1) * 512)
                        for c in range(22):
                            S.mm(bank.f(0, 512), lhsT=gT[:, c, cs], rhs=wdnb[:, c, hs], start=(c == 0), stop=(c == 21))
                        tm = tmp[half]
                        S.tt("dve", tm[:, :], bank.f(0, 512), g2row[:, hs], ALU.mult)
                        S.tt("pool", xx[:, blk, hs], xx[:, blk, hs], tm[:, :], ALU.add)
                S.dma(orows[t], xx[:, :, :], g_x[t % 2])
            S.flush()


def _colL(v, n):
    return np.ascontiguousarray(np.asarray(v, np.float32).reshape(n, 128).T)


def _rep(v, n=128):
    v = np.asarray(v, np.float32).reshape(1, -1)
    return np.ascontiguousarray(np.repeat(v, n, axis=0))


def _consts():
    p = np.arange(128)
    same = (p[:, None] // 64) == (p[None, :] // 64)
    c = {}
    c["identf"] = np.eye(128, dtype=np.float32)
    c["tri2"] = (same & (p[:, None] <= p[None, :])).astype(np.float32)
    c["blk2"] = same.astype(np.float32)
    c["onesf"] = np.ones((128, 128), np.float32)
    c["cind"] = np.stack([(p < 64), (p >= 64)], axis=1).astype(np.float32)
    ma = np.where(same & (p[None, :] < p[:, None]), 0.0, BIG).astype(np.float32)
    mb = np.where(same & (p[None, :] >= p[:, None]), 0.0, -BIG).astype(np.float32)
    c["ma4"] = np.ascontiguousarray(np.tile(ma, (1, 4)))
    c["mb4"] = np.ascontiguousarray(np.tile(mb, (1, 4)))
    return c


def _host_inputs(inputs):
    x = np.asarray(inputs["x"], np.float32)
    cst = _consts()
    g = lambda k: np.asarray(inputs[k][0], np.float32)
    gcw = g("gdn_conv_w")
    gcwT = np.ascontiguousarray(gcw.reshape(4, 12, 128).transpose(2, 1, 0).reshape(128, 48))
    shared = {
        "ada_w": np.ascontiguousarray(g("ada_w")),
        "ada_bT": _colL(g("ada_b"), 48),
        "n1w": _colL(g("norm1_w"), 8),
        "n2w": _colL(g("norm2_w"), 8),
        "w_in": np.ascontiguousarray(g("w_in")),
        "gcw": gcwT,
        "dtb": _rep(g("gdn_dt_bias")),
        "alog": _rep(g("gdn_A_log")),
        "kslw": _colL(g("nsa_k_norm_slc"), 1),
        "kwnw": _colL(g("nsa_k_norm_win"), 1),
        "qnw": _colL(g("nsa_q_norm_w"), 1),
        "gonw4": _rep(np.tile(g("gdn_out_norm_w"), 4)),
        "onesf2": np.ones((128, 128), np.float32),
    }
    for nm in ("k", "v"):
        shared[f"cmp_{nm}_w1"] = np.ascontiguousarray(g(f"cmp_{nm}_w1").reshape(32, 128, 128).transpose(1, 0, 2))
        shared[f"cmp_{nm}_w2"] = np.ascontiguousarray(g(f"cmp_{nm}_w2"))
        shared[f"cmp_{nm}_posT"] = np.ascontiguousarray(g(f"cmp_{nm}_pos").T)
    shared["kcw"] = _colL(g("nsa_k_norm_cmp"), 1)
    shared["onesf3"] = np.ones((128, 128), np.float32)
    pm = np.ones((128, 1), np.float32)
    pm[127, 0] = 0.0
    shared["pm511"] = pm
    keys = np.arange(S_LEN)
    shared["eall"] = (keys[None, :] // 64 == np.arange(128)[:, None]).astype(np.float32)
    n = np.arange(512)
    js = np.arange(128)
    ov = np.minimum(16 * n[:, None] + 32, 64 * js[None, :] + 64) - np.maximum(16 * n[:, None], 64 * js[None, :])
    ov = np.clip(ov, 0, None).astype(np.float32) / 32.0
    ov[511] = 0.0
    shared["ovm"] = np.ascontiguousarray(ov.reshape(4, 128, 128).transpose(1, 0, 2))
    fw = g("ffn_conv_w")
    shared["fcw"] = np.ascontiguousarray(fw.reshape(3, 44, 128).transpose(2, 1, 0).reshape(128, 132))
    shared["fcb"] = _colL(g("ffn_conv_b"), 44)
    shared["w_out"] = np.ascontiguousarray(g("w_out"))
    shared["w_up"] = np.ascontiguousarray(g("ffn_w_up"))
    shared["w_dn"] = np.ascontiguousarray(g("ffn_w_down"))
    shared.update(cst)
    maps = []
    for core in range(8):
        b, r = core // 4, core % 4
        xo = np.concatenate([x[b, 128 * (4 * j + r):128 * (4 * j + r) + 128] for j in range(16)], axis=0)
        m = dict(shared)
        m["xb"] = np.ascontiguousarray(x[b])
        m["xo"] = np.ascontiguousarray(xo)
        m["cT"] = _colL(inputs["c"][b], 8)
        sel = np.zeros((128, 4), np.float32)
        sel[:, r] = 1.0
        m["sel"] = sel
        p = np.arange(128)
        q = np.arange(128)
        addm = np.zeros((128, 16, 128), np.float32)
        cmb = np.zeros((128, 16, 2, 128), np.float32)
        for j in range(16):
            qi = 4 * j + r
            tq = 128 * qi + q
            cur = tq // 64
            jj = np.arange(128)[None, :]
            valid = jj <= cur[:, None]
            forced = (jj == 0) | (jj == cur[:, None]) | (jj == cur[:, None] - 1)
            addm[:, j, :] = np.where(valid, np.where(forced, 1.0e4, 0.0), -1.0e30)
            lo = max(0, 32 * j - 1) // 128
            for slot in range(2):
                nn = 128 * (lo + slot) + p
                ok = (16 * nn[:, None] + 31) <= tq[None, :]
                cmb[:, j, slot, :] = np.where(ok, 0.0, -BIG)
        m["addm"] = addm
        m["cmb"] = cmb
        cbs = np.zeros((128, 4, 128), np.float32)
        for d in range(4):
            ok = (128 * (d - r) + p[:, None]) <= q[None, :]
            cbs[:, d, :] = np.where(ok, 0.0, -BIG)
        m["cbs"] = cbs
        cbw = np.zeros((128, 8, 128), np.float32)
        for e in range(8):
            rel = 128 * (e - 4 - r) + p[:, None]
            ok = (rel <= q[None, :]) & (rel > q[None, :] - 512)
            cbw[:, e, :] = np.where(ok, 0.0, -BIG)
        m["cbw"] = cbw
        hs_ = np.zeros((128, 4), np.float32)
        hs_[:, (r - 1) % 4] = 1.0
        m["hsel"] = hs_
        sab = np.zeros((128, 2), np.float32)
        sab[:, 0] = 1.0 if r >= 1 else 0.0
        sab[:, 1] = 1.0 if r == 0 else 0.0
        m["selAB"] = sab
        tq = np.array([128 * (4 * j + r) - 2 + i for j in range(16) for i in range(2)])
        ex = tq >= 0
        xh = np.zeros((32, D), np.float32)
        xh[ex] = x[b, tq[ex]]
        m["xh"] = xh
        m["hexr"] = _rep(ex.astype(np.float32))
        cur = tq // 64
        jj = np.arange(128)[None, :]
        valid = (jj <= cur[:, None]) & ex[:, None]
        forced = (jj == 0) | (jj == cur[:, None]) | (jj == cur[:, None] - 1)
        m["haddm"] = np.where(valid, np.where(forced, 1.0e4, 0.0), -1.0e30).astype(np.float32)
        nn = np.arange(512)
        okc = ((16 * nn[:, None] + 31) <= tq[None, :]) & ex[None, :]
        hcm = np.where(okc, 0.0, -BIG).astype(np.float32).reshape(4, 128, 1, 32)
        m["hcm"] = np.ascontiguousarray(np.broadcast_to(hcm, (4, 128, 4, 32)).transpose(1, 0, 2, 3).reshape(128, 4, 128))
        pos = np.arange(S_LEN)
        oks = (pos[:, None] <= tq[None, :]) & ex[None, :]
        okw = oks & (pos[:, None] > tq[None, :] - 512)
        for nm, ok in (("hsm", oks), ("hwm", okw)):
            a = np.where(ok, 0.0, -BIG).astype(np.float32).reshape(64, 128, 1, 32)
            m[nm] = np.ascontiguousarray(np.broadcast_to(a, (64, 128, 4, 32)).transpose(1, 0, 2, 3).reshape(128, 64, 128))
        maps.append(m)
    return maps


def run(inputs, debug=False, upto=9, ntiles=NT):
    bld = Builder(debug, ntiles)
    nc = bld.build(upto)
    maps = _host_inputs(inputs)
    maps = [{k: v for k, v in m.items() if k in bld.din} for m in maps]
    missing = [k for k in bld.din if k not in maps[0]]
    assert not missing, missing
    res = run_bass_kernel_spmd(nc, maps, core_ids=list(range(8)))
    return res.results


def kernel(**inputs):
    results = run(inputs)
    outp = np.zeros((2, S_LEN, D), np.float32)
    for core in range(8):
        b, r = core // 4, core % 4
        o = results[core]["out"]
        for j in range(16):
            qi = 4 * j + r
            outp[b, 128 * qi:128 * qi + 128] = o[128 * j:128 * j + 128]
    return outp
```

```python
import numpy as np
from contextlib import ExitStack
import concourse.bass as bass
import concourse.mybir as mybir
from concourse.bass_utils import run_bass_kernel_spmd

F32 = mybir.dt.float32
BF16 = mybir.dt.bfloat16
AF = mybir.ActivationFunctionType
ALU = mybir.AluOpType

D = 1024
S_LEN = 8192
NT = 16
NB = 64
N_IN = 3348
DFF = 2816
EPS = 1e-6
O_GQ, O_GK, O_GV, O_GZ, O_GA, O_GB, O_NQ, O_KC, O_VC, O_KSL, O_VSL, O_KWN, O_VWN, O_NG = (
    0, 512, 1024, 1536, 2048, 2052, 2056, 2568, 2696, 2824, 2952, 3080, 3208, 3336)
BIG = 30000.0


class Buf:
    __slots__ = ("name", "w", "rs", "const", "excl")

    def __init__(self, name, const=False, excl=False):
        self.name = name
        self.w = None
        self.rs = []
        self.const = const
        self.excl = excl


class V:
    __slots__ = ("ap", "bs")

    def __init__(self, ap, bs):
        self.ap = ap
        self.bs = bs if isinstance(bs, (list, tuple)) else [bs]

    def bitcast(self, dt):
        return V(self.ap.bitcast(dt), self.bs)

    def re(self, pat, **kw):
        return V(self.ap.rearrange(pat, **kw), self.bs)

    def bc(self, shape):
        return V(self.ap.broadcast_to(shape), self.bs)

    def __getitem__(self, k):
        return V(self.ap[k], self.bs)


class Tl:
    def __init__(self, t, b):
        self.t = t
        self.b = b

    def __getitem__(self, k):
        return V(self.t[k], self.b)


class Op:
    __slots__ = ("eng", "fn", "deps", "dmaw", "signal", "sigval", "grp")


class DGroup:
    def __init__(self, name, sem, full=False):
        self.name = name
        self.sem = sem
        self.count = 0
        self.full = full


ENGS = ("pe", "act", "dve", "pool", "sp")


class Sched:
    def __init__(self, nc, es):
        self.nc = nc
        self.es = es
        self.eng = {"pe": nc.tensor, "act": nc.scalar, "dve": nc.vector, "pool": nc.gpsimd, "sp": nc.sync}
        self.sem = {e: es.enter_context(nc.semaphore("s_" + e)) for e in ENGS}
        self.cnt = {e: 0 for e in ENGS}
        self.seen = {e: {} for e in ENGS}
        self.ops = []
        self.bufs = []
        self.groups = []
        self.nins = 0

    def buf(self, name, const=False, excl=False):
        b = Buf(name, const, excl)
        self.bufs.append(b)
        return b

    def group(self, name, full=False):
        g = DGroup(name, self.es.enter_context(self.nc.semaphore("g_" + name)), full)
        self.groups.append(g)
        return g

    def op(self, eng, fn, r=(), w=(), grp=None):
        o = Op()
        o.eng = eng
        o.fn = fn
        o.signal = False
        o.sigval = None
        o.grp = grp
        deps = []
        for b in r:
            if b.w is not None:
                deps.append(b.w)
            if b.excl:
                deps.extend(x for x in b.rs if x.eng != eng)
        for b in w:
            if b.w is not None:
                deps.append(b.w)
            deps.extend(b.rs)
        seen = set()
        dd = []
        for d in deps:
            if id(d) in seen or d is o:
                continue
            seen.add(id(d))
            if d.eng == "pe" and eng == "pe":
                continue
            if grp is not None and grp.full and d.grp is grp:
                continue
            dd.append(d)
        o.deps = dd
        o.dmaw = {}
        for d in dd:
            if d.grp is None:
                d.signal = True
            else:
                o.dmaw[d.grp.name] = d.grp.count
        if grp is not None:
            grp.count += 1
        for b in w:
            b.w = o
            b.rs = []
        for b in r:
            if not b.const and b.w is not o:
                b.rs.append(o)
        self.ops.append(o)
        return o

    def flush(self, barrier=True):
        if barrier:
            last = {}
            for o in self.ops:
                last[o.eng] = o
            for e, o in last.items():
                if o.grp is None:
                    o.signal = True
        for o in self.ops:
            e = self.eng[o.eng]
            for d in o.deps:
                if d.grp is not None:
                    key = "g_" + d.grp.name
                    val = 16 * (d.grp.count if d.grp.full else o.dmaw[d.grp.name])
                    sem = d.grp.sem
                else:
                    key = d.eng
                    val = d.sigval
                    sem = self.sem[d.eng]
                    assert val is not None, (o.eng, d.eng)
                if self.seen[o.eng].get(key, 0) >= val:
                    continue
                e.wait_ge(sem, val)
                self.seen[o.eng][key] = val
            ins = o.fn(e)
            self.nins += 1
            if o.grp is not None:
                ins.then_inc(o.grp.sem, 16)
            elif o.signal:
                self.cnt[o.eng] += 1
                o.sigval = self.cnt[o.eng]
                ins.then_inc(self.sem[o.eng], 1)
        self.ops = []
        if barrier:
            for en in ENGS:
                e = self.eng[en]
                for e2 in ENGS:
                    if e2 == en or e2 == "sp":
                        continue
                    if self.cnt[e2] > self.seen[en].get(e2, 0):
                        e.wait_ge(self.sem[e2], self.cnt[e2])
                        self.seen[en][e2] = self.cnt[e2]
                for g in self.groups:
                    key = "g_" + g.name
                    if 16 * g.count > self.seen[en].get(key, 0):
                        e.wait_ge(g.sem, 16 * g.count)
                        self.seen[en][key] = 16 * g.count
            for b in self.bufs:
                b.w = None
                b.rs = []

    @staticmethod
    def _bs(*vs):
        out = []
        for v in vs:
            if isinstance(v, V):
                for b in v.bs:
                    if b not in out:
                        out.append(b)
        return out

    @staticmethod
    def _a(v):
        return v.ap if isinstance(v, V) else v

    def mm(self, out, lhsT, rhs, start=True, stop=True):
        o_, l_, r_ = out.ap, lhsT.ap, rhs.ap
        self.op("pe", lambda e: e.matmul(o_, lhsT=l_, rhs=r_, start=start, stop=stop),
                r=self._bs(lhsT, rhs), w=self._bs(out))

    def mm_(self, out, lhsT, rhs, start=True, stop=True, skip=False):
        o_, l_, r_ = out.ap, lhsT.ap, rhs.ap
        self.op("pe", lambda e: e.matmul(o_, lhsT=l_, rhs=r_, start=start, stop=stop, skip_group_check=skip),
                r=self._bs(lhsT, rhs), w=self._bs(out))

    def tr(self, out, in_, ident):
        o_, i_, d_ = out.ap, in_.ap, ident.ap
        self.op("pe", lambda e: e.transpose(o_, i_, d_), r=self._bs(in_, ident), w=self._bs(out))

    def act(self, out, in_, func, bias=None, scale=None, accum=None, eng="act"):
        kw = {}
        if bias is not None:
            kw["bias"] = self._a(bias)
        if scale is not None:
            kw["scale"] = self._a(scale)
        if accum is not None:
            kw["accum_out"] = accum.ap
        o_, i_ = out.ap, in_.ap
        self.op("act", lambda e: e.activation(out=o_, in_=i_, func=func, **kw),
                r=self._bs(in_, bias, scale), w=self._bs(out, accum))

    def tt(self, eng, out, in0, in1, op):
        o_, a_, b_ = out.ap, in0.ap, in1.ap
        self.op(eng, lambda e: e.tensor_tensor(out=o_, in0=a_, in1=b_, op=op),
                r=self._bs(in0, in1), w=self._bs(out))

    def ts(self, eng, out, in0, s1, s2=None, op0=ALU.mult, op1=None):
        o_, a_ = out.ap, in0.ap
        s1_, s2_ = self._a(s1), self._a(s2)
        kw = {}
        if op1 is not None:
            kw["op1"] = op1
        self.op(eng, lambda e: e.tensor_scalar(out=o_, in0=a_, scalar1=s1_, scalar2=s2_, op0=op0, **kw),
                r=self._bs(in0, s1, s2), w=self._bs(out))

    def stt(self, out, in0, scalar, in1, op0, op1):
        o_, a_, b_ = out.ap, in0.ap, in1.ap
        s_ = self._a(scalar)
        self.op("dve", lambda e: e.scalar_tensor_tensor(out=o_, in0=a_, scalar=s_, in1=b_, op0=op0, op1=op1),
                r=self._bs(in0, scalar, in1), w=self._bs(out))

    def copy(self, eng, out, in_):
        o_, i_ = out.ap, in_.ap
        if eng == "act":
            self.op("act", lambda e: e.copy(out=o_, in_=i_), r=self._bs(in_), w=self._bs(out))
        else:
            self.op(eng, lambda e: e.tensor_copy(out=o_, in_=i_), r=self._bs(in_), w=self._bs(out))

    def recip(self, out, in_):
        o_, i_ = out.ap, in_.ap
        self.op("dve", lambda e: e.reciprocal(out=o_, in_=i_), r=self._bs(in_), w=self._bs(out))

    def memset(self, eng, out, val):
        o_ = out.ap
        self.op(eng, lambda e: e.memset(o_, val), r=[], w=self._bs(out))

    def dma(self, out, in_, grp, eng="sp"):
        o_, i_ = self._a(out), self._a(in_)
        self.op(eng, lambda e: e.dma_start(out=o_, in_=i_), r=self._bs(in_), w=self._bs(out), grp=grp)


class Bank:
    def __init__(self, t, q):
        self.t = t
        self.q = q

    def f(self, c0, c1, p0=0, p1=128):
        return V(self.t[p0:p1, c0:c1], self.q[0:1])

    def b(self, c0, c1, p0=0, p1=128):
        return V(self.t[p0:p1, :].bitcast(BF16)[:, c0:c1], self.q[0:1])


W1_SEGS = ((0, 1536, 0), (2048, 2056, 1536), (2568, 3336, 1544))
W1_N = 2312
C_AB, C_KC, C_VC, C_KSL, C_VSL, C_KWN, C_VWN = 1536, 1544, 1672, 1800, 1928, 2056, 2184


class Builder:
    def __init__(self, debug=False, ntiles=NT):
        self.debug = debug
        self.ntiles = ntiles
        import os
        self.stop = int(os.environ.get('K_STOP', '0'))
        self.var = int(os.environ.get('K_VAR', '0'))
        self.halo = int(os.environ.get('K_HALO', '0'))
        self.nc = bass.Bass("TRN2", target_bir_lowering=False)
        self.din = {}
        self.dout = {}

    def inp(self, name, shape, dt=F32):
        self.din[name] = self.nc.dram_tensor(name, list(shape), dt, kind="ExternalInput").ap()
        return self.din[name]

    def outp(self, name, shape, dt=F32):
        self.dout[name] = self.nc.dram_tensor(name, list(shape), dt, kind="ExternalOutput").ap()
        return self.dout[name]

    def scratch(self, name, shape, dt=F32):
        if self.debug:
            return self.outp(name, shape, dt)
        return self.nc.dram_tensor(name, list(shape), dt, kind="Internal").ap()

    def sb(self, es, name, shape, dt=F32, const=False):
        t = es.enter_context(self.nc.sbuf_tensor(name, list(shape), dt))
        return Tl(t, self.S.buf(name, const))

    def cload(self, es, name, shape, grp):
        d = self.inp(name, shape)
        t = self.sb(es, "c_" + name, shape, F32, const=True)
        idx = tuple(slice(None) for _ in shape)
        self.S.dma(t[idx], d[idx], grp)
        return t

    def build(self, upto=9):
        nc = self.nc
        I = self.inp
        self.xb = I("xb", [S_LEN, D])
        self.xo = I("xo", [2048, D])
        cT = I("cT", [128, 8])
        ada_w = I("ada_w", [D, 6 * D])
        self.w_in = I("w_in", [D, N_IN])
        self.out = self.outp("out", [2048, D])
        self.o_own = self.scratch("o_own", [16, 128, 512])
        self.kslT_d = self.scratch("kslT_d", [128, S_LEN], BF16)
        self.kwnT_d = self.scratch("kwnT_d", [128, S_LEN], BF16)
        self.kcT_d = self.scratch("kcT_d", [128, S_LEN], BF16)
        self.vcT_d = self.scratch("vcT_d", [128, S_LEN], BF16)
        self.vsl_d = self.scratch("vsl_d", [128, NB, 128], BF16)
        self.vwn_d = self.scratch("vwn_d", [128, NB, 128], BF16)
        self.mix_d = self.scratch("mix_d", [128, 8, 2048], BF16)
        self.oprev_d = self.scratch("oprev_d", [16, 2, 512])
        self.mixh_d = self.scratch("mixh_d", [128, 8, 32], BF16)
        self.w_up = I("w_up", [D, 2 * DFF])
        self.wup_d = self.scratch("wup_d", [128, 44, 8, 128], BF16)

        with ExitStack() as top:
            S = self.S = Sched(nc, top)
            self.g_const = g_const = S.group("const", full=True)
            self.g_out = S.group("outw")
            self.idf = idf = self.cload(top, "identf", [128, 128], g_const)
            self.idb = idb = self.sb(top, "idb", [128, 128], BF16)
            S.copy("dve", idb[:, :], idf[:, :])
            self.modT = modT = self.sb(top, "modT", [128, 48])
            self.s1 = s1 = self.sb(top, "s1", [128, 8])
            self.s2 = s2 = self.sb(top, "s2", [128, 8])
            n1 = self.cload(top, "n1w", [128, 8], g_const)
            n2 = self.cload(top, "n2w", [128, 8], g_const)
            self.pb = []
            for i in range(8):
                t = top.enter_context(nc.psum_tensor(f"pb{i}", [128, 512], F32))
                self.pb.append(Bank(t, [S.buf(f"pb{i}", excl=True)]))
            pb = self.pb
            self.bankA = [pb[2], pb[3]]
            self.bankB = [pb[4], pb[5]]
            self.bankC = [pb[0], pb[1]]

            with ExitStack() as es:
                ct = self.sb(es, "ct", [128, 8])
                sc = self.sb(es, "sc", [128, 8])
                abT = self.cload(es, "ada_bT", [128, 48], g_const)
                S.dma(ct[:, :], cT[:, :], g_const)
                S.act(sc[:, :], ct[:, :], AF.Silu)
                aw = [self.sb(es, f"aw{i}", [128, 6 * D]) for i in range(2)]
                g_aw = [S.group(f"aw{i}") for i in range(2)]
                for k in range(8):
                    sl = k % 2
                    for hh in range(4):
                        S.dma(aw[sl][:, hh * 1536:(hh + 1) * 1536],
                              ada_w[k * 128:(k + 1) * 128, hh * 1536:(hh + 1) * 1536], g_aw[sl])
                    pm = pb[k % 2]
                    for cc in range(48):
                        S.mm(pm.f(cc, cc + 1), lhsT=aw[sl][:, cc * 128:(cc + 1) * 128], rhs=sc[:, k:k + 1])
                    S.tt("dve", modT[:, :], pm.f(0, 48), (abT if k == 0 else modT)[:, :], ALU.add)
                S.stt(s1[:, :], modT[:, 8:16], 1.0, n1[:, :], ALU.add, ALU.mult)
                S.stt(s2[:, :], modT[:, 32:40], 1.0, n2[:, :], ALU.add, ALU.mult)
                S.flush()

            if upto >= 1:
                self.phase1()
            keep = dict(qnT=self.sb(top, "qnT", [128, 4, 2048], BF16), gates=self.sb(top, "gates", [128, 16, 12]),
                        qnTh=self.sb(top, "qnTh", [128, 4, 32], BF16), gatesh=self.sb(top, "gatesh", [32, 12]))
            if upto >= 2:
                self.phase2(keep)
                if self.debug:
                    S.dma(self.outp("d_qnT", [128, 4, 2048], BF16)[:, :, :], keep["qnT"][:, :, :], self.g_out)
                    S.dma(self.outp("d_gates", [128, 16, 12])[:, :, :], keep["gates"][:, :, :], self.g_out)
            if upto >= 3:
                S.flush()
                self.phase3(keep)
            if upto >= 4:
                S.flush()
                if self.debug:
                    self.dx1 = self.outp("d_x1", [4, 128, 4, D])
                self.phase45()
            S.flush()
        return nc

    def front(self, F, xt, nb, hT, c0, scol, sh_c0, b0=0):
        S = self.S
        pb = self.pb
        ssq, rt, rstd, junk, xs = F["ssq"], F["rt"], F["rstd"], F["junk"], F["xs"]
        for b2 in range(nb):
            S.act(junk[:, :], xt[:, b0 + b2, :], AF.Square, accum=ssq[:, b2:b2 + 1])
        S.act(rt[:, 0:nb], ssq[:, 0:nb], AF.Ln, bias=EPS, scale=1.0 / D)
        S.act(rstd[:, 0:nb], rt[:, 0:nb], AF.Exp, scale=-0.5)
        for b2 in range(nb):
            S.ts("dve", xs[:, b2, :], xt[:, b0 + b2, :], rstd[:, b2:b2 + 1], None, ALU.mult)
        w = nb * 128
        for half in range(2):
            bank = pb[half]
            for k in range(half * 4, half * 4 + 4):
                off = (k % 4) * 256
                for b2 in range(nb):
                    S.tr(bank.b(off + b2 * 128, off + (b2 + 1) * 128), xs[:, b2, k * 128:(k + 1) * 128],
                         self.idb[:, :])
            for k in range(half * 4, half * 4 + 4):
                off = (k % 4) * 256
                S.ts("dve", hT[:, k, c0:c0 + w], bank.b(off, off + w), scol[:, k:k + 1],
                     self.modT[:, sh_c0 + k:sh_c0 + k + 1], ALU.mult, ALU.add)

    def front32(self, F, xt, hT, scol, sh_c0):
        S = self.S
        bank = self.pb[0]
        ssq, rt, rstd, junk, xs = F["ssq"], F["rt"], F["rstd"], F["junk"], F["xs"]
        S.act(junk[0:32, :], xt[:, :], AF.Square, accum=ssq[0:32, 0:1])
        S.act(rt[0:32, 0:1], ssq[0:32, 0:1], AF.Ln, bias=EPS, scale=1.0 / D)
        S.act(rstd[0:32, 0:1], rt[0:32, 0:1], AF.Exp, scale=-0.5)
        S.ts("pool", xs[0:32, 0, :], xt[:, :], rstd[0:32, 0:1], 1.0, ALU.mult, ALU.mult)
        for k in range(8):
            S.tr(bank.b(k * 32, (k + 1) * 32), xs[0:32, 0, k * 128:(k + 1) * 128], self.idb[0:32, 0:32])
        for k in range(8):
            S.ts("dve", hT[:, k, :], bank.b(k * 32, (k + 1) * 32), scol[:, k:k + 1],
                 self.modT[:, sh_c0 + k:sh_c0 + k + 1], ALU.mult, ALU.add)

    def phase1(self):
        S = self.S
        nc = self.nc
        pb = self.pb
        gc_ = S.group("const1", full=True)
        idb, idf = self.idb, self.idf
        with ExitStack() as es:
            tri2 = self.cload(es, "tri2", [128, 128], gc_)
            blk2 = self.cload(es, "blk2", [128, 128], gc_)
            onesf = self.cload(es, "onesf", [128, 128], gc_)
            cind = self.cload(es, "cind", [128, 2], gc_)
            ma4 = self.cload(es, "ma4", [128, 512], gc_)
            mb4 = self.cload(es, "mb4", [128, 512], gc_)
            sel = self.cload(es, "sel", [128, 4], gc_)
            hsel = self.cload(es, "hsel", [128, 4], gc_)
            cw = self.cload(es, "gcw", [128, 48], gc_)
            dtb = self.cload(es, "dtb", [128, 4], gc_)
            alog = self.cload(es, "alog", [128, 4], gc_)
            kslw = self.cload(es, "kslw", [128, 1], gc_)
            kwnw = self.cload(es, "kwnw", [128, 1], gc_)
            negones = self.sb(es, "negones", [128, 128])
            S.ts("dve", negones[:, :], onesf[:, :], -1.0, None, ALU.mult)
            onesb = self.sb(es, "onesb", [128, 128], BF16)
            S.copy("dve", onesb[:, :], onesf[:, :])
            i4b = self.sb(es, "i4b", [128, 512], BF16)
            for h in range(4):
                S.copy("dve", i4b[:, h * 128:(h + 1) * 128], idf[:, :])
            negA = self.sb(es, "negA", [128, 4])
            S.act(negA[:, :], alog[:, :], AF.Exp)
            S.ts("dve", negA[:, :], negA[:, :], -1.0, None, ALU.mult)

            winb = self.sb(es, "winb", [128, 8, W1_N], BF16)
            with ExitStack() as es2:
                wst = [self.sb(es2, f"wst{i}", [128, W1_N]) for i in range(2)]
                g_w = [S.group(f"wst{i}") for i in range(2)]
                for k in range(8):
                    sl = k % 2
                    for (a, b_, o) in W1_SEGS:
                        S.dma(wst[sl][:, o:o + (b_ - a)], self.w_in[k * 128:(k + 1) * 128, a:b_], g_w[sl])
                    S.copy("dve", winb[:, k, 0:1024], wst[sl][:, 0:1024])
                    S.copy("pool", winb[:, k, 1024:W1_N], wst[sl][:, 1024:W1_N])
                S.flush()

            F = dict(ssq=self.sb(es, "f_ssq", [128, 4]), rt=self.sb(es, "f_rt", [128, 4]),
                     rstd=self.sb(es, "f_rstd", [128, 4]), junk=self.sb(es, "f_junk", [128, 1024], BF16),
                     xs=self.sb(es, "f_xs", [128, 2, 1024], BF16))
            xt = [self.sb(es, f"xt{i}", [128, 2, 1024]) for i in range(2)]
            g_x = [S.group(f"xt{i}") for i in range(2)]
            hT = self.sb(es, "hT", [128, 8, 512], BF16)
            pre = [self.sb(es, f"pre{i}", [128, 515]) for i in range(3)]
            hist = self.sb(es, "hist", [128, 12, 3])
            S.memset("pool", hist[:, :, :], 0.0)
            acc = [self.sb(es, f"cacc{i}", [128, 512]) for i in range(3)]
            yfs = [self.sb(es, f"yfs{i}", [128, 512], BF16 if i < 8 else F32) for i in range(10)]
            sq = [self.sb(es, f"sq{i}", [128, 512], BF16) for i in range(3)]
            rtt = [self.sb(es, f"rtt{i}", [128, 512]) for i in range(3)]
            qT = self.sb(es, "qT", [128, 4, 512], BF16)
            kT = self.sb(es, "kT", [128, 4, 512], BF16)
            vT = self.sb(es, "vT", [128, 4, 512], BF16)
            st_f = [[self.sb(es, f"stf{c}_{i}", [128, 512], BF16) for i in range(2)] for c in range(4)]
            st_v = [[self.sb(es, f"stv{c}_{i}", [128, 4, 128], BF16) for i in range(2)] for c in range(2)]
            g_sf = [[S.group(f"sf{c}_{i}") for i in range(2)] for c in range(4)]
            g_sv = [[S.group(f"sv{c}_{i}") for i in range(2)] for c in range(2)]
            ab = self.sb(es, "ab", [128, 4, 8])
            self.sm4 = sm4 = self.sb(es, "sm4", [128, 4, 64])
            S32 = self.sb(es, "S32", [128, 4, 128])
            Sb = [self.sb(es, f"Sb{i}", [128, 4, 128], BF16) for i in range(2)]
            self.cidx = 0
            S.memset("pool", S32[:, :, :], 0.0)
            S.memset("pool", Sb[0][:, :, :], 0.0)
            oacc = [self.sb(es, f"oacc{i}", [128, 512]) for i in range(2)]
            g_o = [S.group(f"oacc{i}") for i in range(2)]
            oph = [self.sb(es, f"oph{i}", [128, 512]) for i in range(2)]
            g_oh = [S.group(f"oph{i}") for i in range(2)]
            G = []
            for s_ in range(2):
                g = {}
                for nm, shp, dt in (("TG", [128, 4, 128], F32), ("Xs", [128, 512], F32),
                                    ("XA", [128, 512], F32), ("XB", [128, 512], F32), ("EG", [128, 4, 128], F32),
                                    ("Nb", [128, 4, 128], BF16), ("aT", [128, 4, 128], BF16),
                                    ("qdT", [128, 4, 128], BF16), ("kw", [128, 4, 128], BF16),
                                    ("kd", [128, 4, 128], BF16), ("vb", [128, 4, 128], BF16),
                                    ("NTb", [128, 4, 128], BF16), ("RTb", [128, 4, 128], BF16),
                                    ("P0", [128, 4, 128], BF16), ("P1", [128, 4, 128], BF16),
                                    ("Q0", [128, 4, 128], BF16), ("Q1", [128, 4, 128], BF16),
                                    ("u", [128, 4, 128], F32), ("wTb", [128, 4, 128], BF16),
                                    ("vn", [128, 4, 128], BF16)):
                    g[nm] = self.sb(es, f"g{s_}_{nm}", shp, dt)
                G.append(g)

            xrows = self.xb.rearrange("(n b p) d -> n p b d", b=2, p=128)

            def load_x(n):
                S.dma(xt[n % 2][:, :, :], xrows[n], g_x[n % 2])

            load_x(0)
            for ti in range(self.ntiles):
                for hf in range(2):
                    n = 2 * ti + hf
                    if n + 1 < 2 * NT:
                        load_x(n + 1)
                    self.front(F, xt[n % 2], 2, hT, hf * 256, self.s1, 0)
                if self.stop == 1:
                    continue
                nsa_ch = ((C_KC, self.kcT_d, None), (C_VC, self.vcT_d, None), (C_KSL, self.kslT_d, kslw),
                          (C_KWN, self.kwnT_d, kwnw))

                def st1(c, pos):
                    bank = pb[2 + (pos % 2)]
                    coff = c * 128 if c < 12 else nsa_ch[c - 12][0]
                    for k in range(8):
                        S.mm(bank.f(0, 512), lhsT=winb[:, k, coff:coff + 128], rhs=hT[:, k, :],
                             start=(k == 0), stop=(k == 7))
                    if c < 12:
                        p_ = pre[pos % 3]
                        S.copy("pool", p_[:, 0:3], hist[:, c, :])
                        S.copy("act", p_[:, 3:515], bank.f(0, 512))
                        S.copy("pool", hist[:, c, :], p_[:, 512:515])
                        a_ = acc[pos % 3]
                        S.ts("pool", a_[:, :], p_[:, 0:512], cw[:, c * 4:c * 4 + 1], 1.0, ALU.mult, ALU.mult)
                        for j in range(1, 4):
                            S.stt(a_[:, :], p_[:, j:j + 512], cw[:, c * 4 + j:c * 4 + j + 1], a_[:, :], ALU.mult, ALU.add)
                    else:
                        ci = c - 12
                        wcol = nsa_ch[ci][2]
                        if wcol is None:
                            S.copy("act", st_f[ci][ti % 2][:, :], bank.f(0, 512))
                            S.dma(nsa_ch[ci][1][:, ti * 512:(ti + 1) * 512], st_f[ci][ti % 2][:, :], g_sf[ci][ti % 2])
                        else:
                            S.copy("act", yfs[c - 6][:, :], bank.f(0, 512))

                def st2(c, pos):
                    h = c % 4
                    if c >= 12:
                        return
                    if c >= 8:
                        S.act(vT[:, h, :], acc[pos % 3][:, :], AF.Silu)
                        return
                    S.act(yfs[c][:, :], acc[pos % 3][:, :], AF.Silu)

                def st3(c, i_):
                    h = c % 4
                    bk2 = pb[4 + (i_ % 2)]
                    yi = c if c < 8 else c - 6
                    S.act(sq[i_ % 3][:, :], yfs[yi][:, :], AF.Square)
                    S.mm(bk2.f(0, 512), lhsT=onesb[:, :], rhs=sq[i_ % 3][:, :])
                    r_ = rtt[i_ % 3]
                    if c < 4:
                        S.act(r_[:, :], bk2.f(0, 512), AF.Ln, bias=EPS * 128.0, scale=128.0)
                    elif c < 8:
                        S.act(r_[:, :], bk2.f(0, 512), AF.Ln, bias=EPS, scale=1.0)
                    else:
                        S.act(r_[:, :], bk2.f(0, 512), AF.Ln, bias=EPS, scale=1.0 / 128.0)
                    S.act(r_[:, :], r_[:, :], AF.Exp, scale=-0.5)
                    if c < 8:
                        S.tt("pool", (qT if c < 4 else kT)[:, h, :], yfs[yi][:, :], r_[:, :], ALU.mult)
                    else:
                        ci = c - 12
                        st = st_f[ci][ti % 2]
                        S.stt(st[:, :], yfs[yi][:, :], nsa_ch[ci][2][:, 0:1], r_[:, :], ALU.mult, ALU.mult)
                        S.dma(nsa_ch[ci][1][:, ti * 512:(ti + 1) * 512], st[:, :], g_sf[ci][ti % 2])

                order = [12, 13, 14, 15] + list(range(12))
                for i in range(len(order) + 1):
                    if i < len(order):
                        st1(order[i], i)
                    if 0 <= i - 1 < len(order):
                        st2(order[i - 1], i - 1)
                if self.stop == 3:
                    continue
                for blk in range(4):
                    bank = pb[6]
                    for vi, coff in enumerate((C_VSL, C_VWN)):
                        for k in range(8):
                            if self.var == 4:
                                break
                            S.mm(bank.f(vi * 128, (vi + 1) * 128), lhsT=hT[:, k, blk * 128:(blk + 1) * 128],
                                 rhs=winb[:, k, coff:coff + 128], start=(k == 0), stop=(k == 7))
                    for k in range(8):
                        if self.var == 1:
                            break
                        S.mm(bank.f(256, 264), lhsT=hT[:, k, blk * 128:(blk + 1) * 128],
                             rhs=winb[:, k, C_AB:C_AB + 8], start=(k == 0), stop=(k == 7))
                    if self.var != 3:
                        S.copy("act", st_v[0][ti % 2][:, blk, :], bank.f(0, 128))
                        S.copy("act", st_v[1][ti % 2][:, blk, :], bank.f(128, 256))
                    if self.var != 5:
                        S.copy("dve", ab[:, blk, :], bank.f(256, 264))
                if self.var != 2:
                    S.dma(self.vsl_d[:, ti * 4:(ti + 1) * 4, :], st_v[0][ti % 2][:, :, :], g_sv[0][ti % 2])
                    S.dma(self.vwn_d[:, ti * 4:(ti + 1) * 4, :], st_v[1][ti % 2][:, :, :], g_sv[1][ti % 2])

                for i_, c in enumerate([14, 15, 0, 1, 2, 3, 4, 5, 6, 7]):
                    st3(c, i_)
                if self.stop == 4:
                    continue
                for pair in range(2):
                    blks = (2 * pair, 2 * pair + 1)
                    if pair == 0:
                        self.gdn_small(sm4, ab, tri2, blk2, onesf, cind, dtb, negA)
                    for s_, blk in enumerate(blks):
                        self.gdn_local_1(G[s_], s_, blk, sm4, tri2, negones, onesf, ma4, mb4, qT, kT, vT)
                    if self.stop == 5:
                        continue
                    self.gdn_solve(G, blks, i4b)
                    if self.stop == 6:
                        continue
                    for s_, blk in enumerate(blks):
                        self.gdn_uw(G[s_], s_)
                    if self.stop == 7:
                        continue
                    for s_, blk in enumerate(blks):
                        oa = oacc[ti % 2]
                        self.gdn_recur(G[s_], S32, Sb, sel, blk, oa, hsel, oph[ti % 2])
                S.dma(self.o_own[ti], oacc[ti % 2][:, :], g_o[ti % 2])
                S.dma(self.oprev_d[ti], oph[ti % 2][126:128, :], g_oh[ti % 2])
            S.flush()

    def gdn_small(self, sm4, ab, tri2, blk2, onesf, cind, dtb, negA):
        S = self.S
        bS = self.pb[6]
        x_, t_, gg, be, nb_ = (sm4[:, :, 0:4], sm4[:, :, 4:8], sm4[:, :, 8:12], sm4[:, :, 12:16], sm4[:, :, 16:20])
        gci, gcv, gl, egc, ekd, bw, glS = (sm4[:, :, 20:28], sm4[:, :, 28:32], sm4[:, :, 32:36], sm4[:, :, 36:40],
                                           sm4[:, :, 40:44], sm4[:, :, 44:48], sm4[:, :, 48:56])
        bc = lambda t: V(t.t[:, :].unsqueeze(1).broadcast_to([128, 4, 4]), [t.b])
        S.tt("dve", x_, ab[:, :, 0:4], bc(dtb), ALU.add)
        S.stt(t_, x_, -1.0, x_, ALU.mult, ALU.max)
        S.act(t_, t_, AF.Exp, scale=-1.0)
        S.act(t_, t_, AF.Ln, bias=1.0)
        S.stt(t_, x_, 0.0, t_, ALU.max, ALU.add)
        S.tt("dve", gg, t_, bc(negA), ALU.mult)
        S.act(be, ab[:, :, 4:8], AF.Exp, scale=-1.0)
        S.ts("dve", be, be, 1.0, None, ALU.add)
        S.recip(be, be)
        S.ts("dve", nb_, be, -1.0, None, ALU.mult)
        for i in range(2):
            for blk in range(4):
                S.ts("dve", sm4[:, blk, 20:28].re("p (h i) -> p h i", i=2)[:, :, i], sm4[:, blk, 8:12], cind[:, i:i + 1],
                     None, ALU.mult)
        for blk in range(4):
            S.mm(bS.f(384 + blk * 4, 388 + blk * 4), lhsT=tri2[:, :], rhs=sm4[:, blk, 8:12])
            S.mm(bS.f(400 + blk * 4, 404 + blk * 4), lhsT=blk2[:, :], rhs=sm4[:, blk, 8:12])
            S.mm(bS.f(416 + blk * 8, 424 + blk * 8), lhsT=onesf[:, :], rhs=sm4[:, blk, 20:28])
        S.copy("dve", gcv, bS.f(384, 400).re("p (b h) -> p b h", b=4))
        S.copy("dve", gl, bS.f(400, 416).re("p (b h) -> p b h", b=4))
        S.act(glS, bS.f(416, 448).re("p (b c) -> p b c", b=4), AF.Exp)
        S.act(egc, gcv, AF.Exp)
        S.tt("dve", ekd, gl, gcv, ALU.subtract)
        S.act(ekd, ekd, AF.Exp)
        S.tt("dve", bw, be, egc, ALU.mult)

    def gdn_local_1(self, g, s_, blk, sm4, tri2, negones, onesf, ma4, mb4, qT, kT, vT):
        S = self.S
        pb = self.pb
        cs = slice(blk * 128, (blk + 1) * 128)
        sm = V(sm4.t[:, blk, :], [sm4.b])
        gg, be, nb_, gcv, ekd, bw = (sm[:, 8:12], sm[:, 12:16], sm[:, 16:20], sm[:, 28:32], sm[:, 40:44], sm[:, 44:48])
        bK = self.bankA[s_]
        bQ = self.bankB[s_]
        bX = self.bankC[s_]
        bT = pb[7]
        for h in range(4):
            S.mm(bK.f(h * 128, (h + 1) * 128), lhsT=kT[:, h, cs], rhs=kT[:, h, cs])
        for h in range(4):
            S.mm(bQ.f(h * 128, (h + 1) * 128), lhsT=kT[:, h, cs], rhs=qT[:, h, cs])
        for h in range(4):
            S.tr(bT.b(h * 128, (h + 1) * 128), kT[:, h, cs], self.idb[:, :])
        for h in range(4):
            S.tr(bT.b(512 + h * 128, 512 + (h + 1) * 128), vT[:, h, cs], self.idb[:, :])
        for h in range(4):
            S.ts("pool", g["TG"][:, h, :], tri2[:, :], gg[:, h:h + 1], 1.0, ALU.mult, ALU.mult)
        for h in range(4):
            S.mm(bX.f(h * 128, (h + 1) * 128), lhsT=onesf[:, :], rhs=g["TG"][:, h, :], start=True, stop=False)
            S.mm(bX.f(h * 128, (h + 1) * 128), lhsT=g["TG"][:, h, :], rhs=negones[:, :], start=False, stop=True)
        S.copy("act", g["Xs"][:, :], bX.f(0, 512))
        for h in range(4):
            S.act(g["EG"][:, h, :], bX.f(h * 128, (h + 1) * 128), AF.Exp, bias=gcv[:, h:h + 1])
        S.tt("dve", g["XA"][:, :], g["Xs"][:, :], ma4[:, :], ALU.add)
        S.tt("pool", g["XB"][:, :], g["Xs"][:, :], mb4[:, :], ALU.add)
        S.act(g["XA"][:, :], g["XA"][:, :], AF.Exp, scale=-1.0)
        S.act(g["XB"][:, :], g["XB"][:, :], AF.Exp)
        for h in range(4):
            S.stt(g["Nb"][:, h, :], bK.f(h * 128, (h + 1) * 128), nb_[:, h:h + 1],
                  g["XA"][:, h * 128:(h + 1) * 128], ALU.mult, ALU.mult)
        S.tt("dve", g["aT"][:, :, :].re("p h c -> p (h c)"), bQ.f(0, 512), g["XB"][:, :], ALU.mult)
        S.tt("pool", g["qdT"][:, :, :], qT[:, :, cs], g["EG"][:, :, :], ALU.mult)
        kt3 = bT.b(0, 512).re("p (h d) -> p h d", h=4)
        vt3 = bT.b(512, 1024).re("p (h d) -> p h d", h=4)
        S.tt("dve", g["kw"][:, :, :], kt3, V(bw.ap.unsqueeze(2).broadcast_to([128, 4, 128]), bw.bs), ALU.mult)
        S.tt("dve", g["kd"][:, :, :], kt3, V(ekd.ap.unsqueeze(2).broadcast_to([128, 4, 128]), ekd.bs), ALU.mult)
        S.tt("dve", g["vb"][:, :, :], vt3, V(be.ap.unsqueeze(2).broadcast_to([128, 4, 128]), be.bs), ALU.mult)

    def gdn_solve(self, G, blks, i4b):
        S = self.S
        idb = self.idb
        for s_ in range(len(blks)):
            g = G[s_]
            bA, bC = self.bankA[s_], self.bankC[s_]
            for h in range(4):
                S.mm(bA.f(h * 128, (h + 1) * 128), lhsT=g["Nb"][:, h, :], rhs=idb[:, :])
            S.copy("act", g["NTb"][:, :, :].re("p h c -> p (h c)"), bA.f(0, 512))
            S.mm(bC.f(0, 512), lhsT=idb[:, :], rhs=i4b[:, :], start=True, stop=False)
            for h in range(4):
                S.mm(bC.f(h * 128, (h + 1) * 128), lhsT=g["Nb"][:, h, :], rhs=idb[:, :], start=False, stop=(h == 3))
            S.copy("act", g["RTb"][:, :, :].re("p h c -> p (h c)"), bC.f(0, 512))
        P = [G[s_]["Nb"] for s_ in range(len(blks))]
        Q = [G[s_]["NTb"] for s_ in range(len(blks))]
        for k in range(1, 6):
            for s_ in range(len(blks)):
                g = G[s_]
                bA, bB, bC = self.bankA[s_], self.bankB[s_], self.bankC[s_]
                Pn = g["P%d" % (k % 2)]
                Qn = g["Q%d" % (k % 2)]
                for h in range(4):
                    S.mm(bA.f(h * 128, (h + 1) * 128), lhsT=Q[s_][:, h, :], rhs=P[s_][:, h, :])
                if k < 5:
                    for h in range(4):
                        S.mm(bB.f(h * 128, (h + 1) * 128), lhsT=P[s_][:, h, :], rhs=Q[s_][:, h, :])
                S.copy("dve", Pn[:, :, :].re("p h c -> p (h c)"), bA.f(0, 512))
                if k < 5:
                    S.copy("act", Qn[:, :, :].re("p h c -> p (h c)"), bB.f(0, 512))
                S.mm(bC.f(0, 512), lhsT=idb[:, :], rhs=g["RTb"][:, :, :].re("p h c -> p (h c)"), start=True, stop=False)
                for h in range(4):
                    S.mm(bC.f(h * 128, (h + 1) * 128), lhsT=Pn[:, h, :], rhs=g["RTb"][:, h, :],
                         start=False, stop=(h == 3))
                S.copy("act", g["RTb"][:, :, :].re("p h c -> p (h c)"), bC.f(0, 512))
                P[s_] = Pn
                Q[s_] = Qn

    def gdn_uw(self, g, s_):
        S = self.S
        bA, bB = self.bankA[s_], self.bankB[s_]
        for h in range(4):
            S.mm(bA.f(h * 128, (h + 1) * 128), lhsT=g["RTb"][:, h, :], rhs=g["vb"][:, h, :])
        for h in range(4):
            S.mm(bB.f(h * 128, (h + 1) * 128), lhsT=g["kw"][:, h, :], rhs=g["RTb"][:, h, :])
        S.copy("act", g["u"][:, :, :].re("p h c -> p (h c)"), bA.f(0, 512))
        S.copy("dve", g["wTb"][:, :, :].re("p h c -> p (h c)"), bB.f(0, 512))

    def gdn_recur(self, g, S32, Sb, sel, blk, oa, hsel, oh):
        S = self.S
        pb = self.pb
        bV, bO, bS_ = pb[2], pb[3], pb[4]
        sm = V(self.sm4.t[:, blk % 4, :], [self.sm4.b])
        for i in range(2):
            r0, r1 = 64 * i, 64 * i + 64
            cur = Sb[self.cidx % 2]
            nxt = Sb[(self.cidx + 1) % 2]
            self.cidx += 1
            for h in range(4):
                S.mm(bV.f(h * 128, (h + 1) * 128, r0, r1), lhsT=g["wTb"][:, h, r0:r1], rhs=cur[:, h, :])
            S.tt("dve", g["vn"][r0:r1, :, :].re("p h c -> p (h c)"), g["u"][r0:r1, :, :].re("p h c -> p (h c)"),
                 bV.f(0, 512, r0, r1), ALU.subtract)
            for h in range(4):
                S.mm(bS_.f(h * 128, (h + 1) * 128), lhsT=g["kd"][r0:r1, h, :], rhs=g["vn"][r0:r1, h, :])
            for h in range(4):
                S.stt(S32[:, h, :], S32[:, h, :], sm[:, 48 + 2 * h + i:49 + 2 * h + i], bS_.f(h * 128, (h + 1) * 128),
                      ALU.mult, ALU.add)
            S.copy("act", nxt[:, :, :], S32[:, :, :])
            for h in range(4):
                S.mm(bO.f(h * 128, (h + 1) * 128, r0, r1), lhsT=g["qdT"][:, h, r0:r1], rhs=cur[:, h, :],
                     start=True, stop=False)
                S.mm(bO.f(h * 128, (h + 1) * 128, r0, r1), lhsT=g["aT"][r0:r1, h, r0:r1], rhs=g["vn"][r0:r1, h, :],
                     start=False, stop=True)
        r_ = blk % 4
        if r_ == 0:
            S.ts("dve", oa[:, :], bO.f(0, 512), sel[:, 0:1], None, ALU.mult)
        else:
            S.stt(oa[:, :], bO.f(0, 512), sel[:, r_:r_ + 1], oa[:, :], ALU.mult, ALU.add)
        if r_ == 0:
            S.ts("dve", oh[64:128, :], bO.f(0, 512, 64, 128), hsel[64:128, 0:1], None, ALU.mult)
        else:
            S.stt(oh[64:128, :], bO.f(0, 512, 64, 128), hsel[64:128, r_:r_ + 1], oh[64:128, :], ALU.mult, ALU.add)

    def phase2(self, keep):
        S = self.S
        pb = self.pb
        gc_ = S.group("const2", full=True)
        qnT, gates = keep["qnT"], keep["gates"]
        with ExitStack() as es:
            qnw = self.cload(es, "qnw", [128, 1], gc_)
            gonw4 = self.cload(es, "gonw4", [128, 512], gc_)
            onesf = self.cload(es, "onesf2", [128, 128], gc_)
            onesb = self.sb(es, "onesb2", [128, 128], BF16)
            S.copy("dve", onesb[:, :], onesf[:, :])
            winb2 = self.sb(es, "winb2", [128, 8, 1036], BF16)
            with ExitStack() as es2:
                wst = [self.sb(es2, f"w2st{i}", [128, 1036]) for i in range(2)]
                g_w = [S.group(f"w2st{i}") for i in range(2)]
                for k in range(8):
                    sl = k % 2
                    for (a, b_, o) in ((O_GZ, O_GZ + 512, 0), (O_NQ, O_NQ + 512, 512), (O_NG, O_NG + 12, 1024)):
                        S.dma(wst[sl][:, o:o + (b_ - a)], self.w_in[k * 128:(k + 1) * 128, a:b_], g_w[sl])
                    S.copy("dve" if k % 2 else "pool", winb2[:, k, :], wst[sl][:, :])
                S.flush()
            F = dict(ssq=self.sb(es, "f2_ssq", [128, 4]), rt=self.sb(es, "f2_rt", [128, 4]),
                     rstd=self.sb(es, "f2_rstd", [128, 4]), junk=self.sb(es, "f2_junk", [128, 1024], BF16),
                     xs=self.sb(es, "f2_xs", [128, 2, 1024], BF16))
            xt = [self.sb(es, f"x2t{i}", [128, 2, 1024]) for i in range(2)]
            g_x = [S.group(f"x2t{i}") for i in range(2)]
            hT = self.sb(es, "h2T_", [128, 8, 512], BF16)
            yf = [self.sb(es, f"y2f{i}", [128, 512]) for i in range(2)]
            sq = [self.sb(es, f"s2q{i}", [128, 512], BF16) for i in range(2)]
            rtt = [self.sb(es, f"r2tt{i}", [128, 512]) for i in range(2)]
            og = [self.sb(es, f"og{i}", [128, 512]) for i in range(2)]
            g_og = [S.group(f"og{i}") for i in range(2)]
            zs = [self.sb(es, f"zs{i}", [128, 512]) for i in range(4)]
            t1 = [self.sb(es, f"p2t{i}", [128, 512]) for i in range(2)]
            mg = [self.sb(es, f"mg{i}", [128, 512], BF16) for i in range(2)]
            mgT = [self.sb(es, f"mgT{i}", [128, 4, 128], BF16) for i in range(2)]
            g_mg = [S.group(f"mgT{i}") for i in range(2)]
            osq = self.sb(es, "osq", [128, 8])
            xrows = self.xo.rearrange("(n b p) d -> n p b d", b=2, p=128)

            def load_x(n):
                S.dma(xt[n % 2][:, :, :], xrows[n], g_x[n % 2])

            load_x(0)
            for t in range(4):
                for hf in range(2):
                    n = 2 * t + hf
                    if n + 1 < 8:
                        load_x(n + 1)
                    self.front(F, xt[n % 2], 2, hT, hf * 256, self.s1, 0)
                for h in range(4):
                    bank = pb[2 + (h % 2)]
                    for k in range(8):
                        S.mm(bank.f(0, 512), lhsT=winb2[:, k, 512 + h * 128:512 + (h + 1) * 128], rhs=hT[:, k, :],
                             start=(k == 0), stop=(k == 7))
                    y_ = yf[h % 2]
                    S.copy("act", y_[:, :], bank.f(0, 512))
                    S.act(sq[h % 2][:, :], bank.f(0, 512), AF.Square)
                    bk2 = pb[4 + (h % 2)]
                    S.mm(bk2.f(0, 512), lhsT=onesb[:, :], rhs=sq[h % 2][:, :])
                    r_ = rtt[h % 2]
                    S.act(r_[:, :], bk2.f(0, 512), AF.Ln, bias=EPS, scale=1.0 / 128.0)
                    S.act(r_[:, :], r_[:, :], AF.Exp, scale=-0.5)
                    S.stt(qnT[:, h, t * 512:(t + 1) * 512], y_[:, :], qnw[:, 0:1], r_[:, :], ALU.mult, ALU.mult)
                for blk in range(4):
                    cs = slice(blk * 128, (blk + 1) * 128)
                    bz = pb[7]
                    for k in range(8):
                        S.mm(bz.f(0, 512), lhsT=hT[:, k, cs], rhs=winb2[:, k, 0:512], start=(k == 0), stop=(k == 7))
                    S.act(zs[blk][:, :], bz.f(0, 512), AF.Silu)
                for blk in range(4):
                    j = 4 * t + blk
                    cs = slice(blk * 128, (blk + 1) * 128)
                    bg = pb[6]
                    for k in range(8):
                        S.mm(bg.f(0, 12), lhsT=hT[:, k, cs], rhs=winb2[:, k, 1024:1036], start=(k == 0), stop=(k == 7))
                    S.act(gates[:, j, :], bg.f(0, 12), AF.Exp, scale=-1.0)
                    S.ts("dve", gates[:, j, :], gates[:, j, :], 1.0, None, ALU.add)
                    S.recip(gates[:, j, :], gates[:, j, :])
                    z_ = zs[blk]
                    o_ = og[j % 2]
                    S.dma(o_[:, :], self.o_own[j], g_og[j % 2])
                    for h in range(4):
                        S.act(t1[j % 2][:, h * 128:(h + 1) * 128], o_[:, h * 128:(h + 1) * 128], AF.Square,
                              accum=osq[:, h:h + 1])
                    S.act(osq[:, 4:8], osq[:, 0:4], AF.Ln, bias=EPS, scale=1.0 / 128.0)
                    S.act(osq[:, 4:8], osq[:, 4:8], AF.Exp, scale=-0.5)
                    rb = V(osq.t[:, 4:8].unsqueeze(2).broadcast_to([128, 4, 128]), [osq.b])
                    S.tt("dve", t1[j % 2][:, :].re("p (h d) -> p h d", h=4), o_[:, :].re("p (h d) -> p h d", h=4), rb,
                         ALU.mult)
                    S.tt("pool", t1[j % 2][:, :], t1[j % 2][:, :], gonw4[:, :], ALU.mult)
                    S.tt("dve", mg[j % 2][:, :], t1[j % 2][:, :], z_[:, :], ALU.mult)
                    bt = pb[0]
                    for c in range(4):
                        S.tr(bt.b(c * 128, (c + 1) * 128), mg[j % 2][:, c * 128:(c + 1) * 128], self.idb[:, :])
                    S.copy("act", mgT[j % 2][:, :, :].re("p c t -> p (c t)"), bt.b(0, 512))
                    S.dma(self.mix_d[:, 0:4, j * 128:(j + 1) * 128], mgT[j % 2][:, :, :], g_mg[j % 2])
            gh = S.group("p2halo", full=True)
            selAB = self.cload(es, "selAB", [128, 2], gh)
            xht = self.sb(es, "xht", [32, D])
            S.dma(xht[:, :], self.inp("xh", [32, D])[:, :], gh)
            ca = self.sb(es, "hca", [32, 512])
            cb = self.sb(es, "hcb", [32, 512])
            S.memset("pool", cb[0:2, :], 0.0)
            opv = self.oprev_d.rearrange("j i c -> (j i) c")
            S.dma(ca[:, :], opv[0:32, :], gh)
            S.dma(cb[2:32, :], opv[0:30, :], gh)
            hTh = self.sb(es, "hTh", [128, 8, 32], BF16)
            self.front32(F, xht, hTh, self.s1, 0)
            qnTh, gatesh = keep["qnTh"], keep["gatesh"]
            bq = pb[2]
            for h in range(4):
                for k in range(8):
                    S.mm(bq.f(h * 32, (h + 1) * 32), lhsT=winb2[:, k, 512 + h * 128:512 + (h + 1) * 128], rhs=hTh[:, k, :],
                         start=(k == 0), stop=(k == 7))
            S.copy("act", yf[0][:, 0:128], bq.f(0, 128))
            S.act(sq[0][:, 0:128], bq.f(0, 128), AF.Square)
            S.mm(pb[4].f(0, 128), lhsT=onesb[:, :], rhs=sq[0][:, 0:128])
            S.act(rtt[0][:, 0:128], pb[4].f(0, 128), AF.Ln, bias=EPS, scale=1.0 / 128.0)
            S.act(rtt[0][:, 0:128], rtt[0][:, 0:128], AF.Exp, scale=-0.5)
            S.stt(qnTh[:, :, :].re("p h q -> p (h q)"), yf[0][:, 0:128], qnw[:, 0:1], rtt[0][:, 0:128], ALU.mult, ALU.mult)
            for k in range(8):
                S.mm(pb[6].f(0, 12, 0, 32), lhsT=hTh[:, k, :], rhs=winb2[:, k, 1024:1036], start=(k == 0), stop=(k == 7))
            S.act(gatesh[:, :], pb[6].f(0, 12, 0, 32), AF.Exp, scale=-1.0)
            S.ts("dve", gatesh[:, :], gatesh[:, :], 1.0, None, ALU.add)
            S.recip(gatesh[:, :], gatesh[:, :])
            for k in range(8):
                S.mm(pb[7].f(0, 512, 0, 32), lhsT=hTh[:, k, :], rhs=winb2[:, k, 0:512], start=(k == 0), stop=(k == 7))
            S.act(zs[0][0:32, :], pb[7].f(0, 512, 0, 32), AF.Silu)
            S.ts("dve", ca[:, :], ca[:, :], selAB[0:32, 0:1], None, ALU.mult)
            S.stt(ca[:, :], cb[:, :], selAB[0:32, 1:2], ca[:, :], ALU.mult, ALU.add)
            for h in range(4):
                S.act(t1[0][0:32, h * 128:(h + 1) * 128], ca[:, h * 128:(h + 1) * 128], AF.Square, accum=osq[0:32, h:h + 1])
            S.act(osq[0:32, 4:8], osq[0:32, 0:4], AF.Ln, bias=EPS, scale=1.0 / 128.0)
            S.act(osq[0:32, 4:8], osq[0:32, 4:8], AF.Exp, scale=-0.5)
            rb = V(osq.t[0:32, 4:8].unsqueeze(2).broadcast_to([32, 4, 128]), [osq.b])
            S.tt("dve", t1[0][0:32, :].re("p (h d) -> p h d", h=4), ca[:, :].re("p (h d) -> p h d", h=4), rb, ALU.mult)
            S.tt("pool", t1[0][0:32, :], t1[0][0:32, :], gonw4[0:32, :], ALU.mult)
            S.tt("dve", mg[0][0:32, :], t1[0][0:32, :], zs[0][0:32, :], ALU.mult)
            for c in range(4):
                S.tr(pb[0].b(c * 32, (c + 1) * 32), mg[0][0:32, c * 128:(c + 1) * 128], self.idb[0:32, 0:32])
            mghT = self.sb(es, "mghT", [128, 4, 32], BF16)
            S.copy("act", mghT[:, :, :].re("p c t -> p (c t)"), pb[0].b(0, 128))
            S.dma(self.mixh_d[:, 0:4, :], mghT[:, :, :], S.group("mghT"))
            S.flush()

    def phase3(self, keep):
        S = self.S
        pb = self.pb
        idb = self.idb
        qnT, gates = keep["qnT"], keep["gates"]
        SC = 128.0 ** -0.5
        with ExitStack() as es:
            gc_ = S.group("const3", full=True)
            kcmpT = self.sb(es, "kcmpT", [128, 512], BF16)
            vcx = self.sb(es, "vcx", [128, 4, 129], BF16)
            onesf = self.cload(es, "onesf3", [128, 128], gc_)
            onesb = self.sb(es, "onesb3", [128, 128], BF16)
            S.copy("dve", onesb[:, :], onesf[:, :])
            with ExitStack() as e2:
                g2 = S.group("const3b", full=True)
                kcw = self.cload(e2, "kcw", [128, 1], g2)
                pm511 = self.cload(e2, "pm511", [128, 1], g2)
                for which, src_d, w1n, w2n, posn in (("k", self.kcT_d, "cmp_k_w1", "cmp_k_w2", "cmp_k_posT"),
                                                    ("v", self.vcT_d, "cmp_v_w1", "cmp_v_w2", "cmp_v_posT")):
                    with ExitStack() as e3:
                        g3 = S.group("c3" + which, full=True)
                        xT = self.sb(e3, "cx" + which, [128, S_LEN], BF16)
                        S.dma(xT[:, 0:4096], src_d[:, 0:4096], g3)
                        S.dma(xT[:, 4096:8192], src_d[:, 4096:8192], g3)
                        w1d = self.inp(w1n, [128, 32, 128])
                        w1f = self.sb(e3, "w1f" + which, [128, 32, 128])
                        S.dma(w1f[:, 0:16, :], w1d[:, 0:16, :], g3)
                        S.dma(w1f[:, 16:32, :], w1d[:, 16:32, :], g3)
                        w1b = self.sb(e3, "w1b" + which, [128, 32, 128], BF16)
                        S.copy("dve", w1b[:, 0:16, :], w1f[:, 0:16, :])
                        S.copy("pool", w1b[:, 16:32, :], w1f[:, 16:32, :])
                        w2f = self.cload(e3, w2n, [128, 128], g3)
                        w2b = self.sb(e3, "w2b" + which, [128, 128], BF16)
                        S.copy("dve", w2b[:, :], w2f[:, :])
                        posf = self.cload(e3, posn, [128, 32], g3)
                        posb = self.sb(e3, "posb" + which, [128, 32], BF16)
                        S.copy("dve", posb[:, :], posf[:, :])
                        hid = self.sb(e3, "hid" + which, [128, 512], BF16)
                        bcol = self.sb(e3, "bcol" + which, [128, 1])
                        S.memset("pool", hid[:, :], 0.0)
                        bh, bb = pb[2], pb[3]
                        x3 = xT[:, :].re("p (n s) -> p n s", s=16)
                        for l in range(32):
                            S.mm(bh.f(0, 511), lhsT=w1b[:, l, :], rhs=x3[:, l // 16:l // 16 + 511, l % 16],
                                 start=(l == 0), stop=(l == 31))
                        for l in range(32):
                            S.mm(bb.f(0, 1), lhsT=w1b[:, l, :], rhs=posb[:, l:l + 1], start=(l == 0), stop=(l == 31))
                        S.copy("dve", bcol[:, :], bb.f(0, 1))
                        S.act(hid[:, 0:511], bh.f(0, 511), AF.Silu, bias=bcol[:, 0:1])
                        if which == "k":
                            bk = pb[4]
                            S.mm(bk.f(0, 512), lhsT=w2b[:, :], rhs=hid[:, :])
                            yk = self.sb(e3, "yk", [128, 512])
                            sqk = self.sb(e3, "sqk", [128, 512], BF16)
                            rk = self.sb(e3, "rk", [128, 512])
                            S.copy("act", yk[:, :], bk.f(0, 512))
                            S.act(sqk[:, :], bk.f(0, 512), AF.Square)
                            S.mm(pb[5].f(0, 512), lhsT=onesb[:, :], rhs=sqk[:, :])
                            S.act(rk[:, :], pb[5].f(0, 512), AF.Sqrt, bias=EPS, scale=1.0 / 128.0)
                            S.recip(rk[:, :], rk[:, :])
                            S.stt(kcmpT[:, :], yk[:, :], kcw[:, 0:1], rk[:, :], ALU.mult, ALU.mult)
                            S.memset("dve", kcmpT[:, 511:512], 0.0)
                        else:
                            bv = pb[4]
                            for nt in range(4):
                                S.mm(bv.f(nt * 128, (nt + 1) * 128), lhsT=hid[:, nt * 128:(nt + 1) * 128], rhs=w2b[:, :])
                            S.memset("pool", vcx[:, :, 128:129], 1.0)
                            S.copy("act", vcx[:, :, 0:128], bv.f(0, 512).re("p (n d) -> p n d", n=4))
                            S.ts("dve", vcx[:, 3, :], vcx[:, 3, :], pm511[:, 0:1], None, ALU.mult)
                        S.flush()
            if self.debug:
                S.dma(self.outp("d_kcmpT", [128, 512], BF16)[:, :], kcmpT[:, :], self.g_out)
                S.dma(self.outp("d_vcx", [128, 4, 129], BF16)[:, :, :], vcx[:, :, :], self.g_out)
            if self.stop == 31:
                S.flush()
                return
            kslT = self.sb(es, "kslT", [128, S_LEN], BF16)
            kwnT = self.sb(es, "kwnT", [128, S_LEN], BF16)
            vslx = self.sb(es, "vslx", [128, NB, 129], BF16)
            vwnx = self.sb(es, "vwnx", [128, NB, 129], BF16)
            for i in range(2):
                S.dma(kslT[:, i * 4096:(i + 1) * 4096], self.kslT_d[:, i * 4096:(i + 1) * 4096], gc_)
                S.dma(kwnT[:, i * 4096:(i + 1) * 4096], self.kwnT_d[:, i * 4096:(i + 1) * 4096], gc_)
                S.dma(vslx[:, i * 32:(i + 1) * 32, 0:128], self.vsl_d[:, i * 32:(i + 1) * 32, :], gc_)
                S.dma(vwnx[:, i * 32:(i + 1) * 32, 0:128], self.vwn_d[:, i * 32:(i + 1) * 32, :], gc_)
            S.memset("pool", vslx[:, :, 128:129], 1.0)
            S.memset("pool", vwnx[:, :, 128:129], 1.0)
            eall = self.sb(es, "eallb", [128, S_LEN], BF16)
            addm = self.cload(es, "addm", [128, 16, 128], gc_)
            ovb = self.sb(es, "ovb", [128, 4, 128], BF16)
            cmbb = self.sb(es, "cmbb", [128, 16, 2, 128], BF16)
            cbsb = self.sb(es, "cbsb", [128, 4, 4, 128], BF16)
            cbwb = self.sb(es, "cbwb", [128, 8, 4, 128], BF16)
            with ExitStack() as e2:
                g2 = S.group("const3c", full=True)
                ead = self.inp("eall", [128, S_LEN])
                est = [self.sb(e2, f"est{i}", [128, 2048]) for i in range(2)]
                g_e = [S.group(f"est{i}") for i in range(2)]
                for i in range(4):
                    S.dma(est[i % 2][:, :], ead[:, i * 2048:(i + 1) * 2048], g_e[i % 2])
                    S.copy("pool" if i % 2 else "dve", eall[:, i * 2048:(i + 1) * 2048], est[i % 2][:, :])
                ovf = self.cload(e2, "ovm", [128, 4, 128], g2)
                S.copy("dve", ovb[:, :, :], ovf[:, :, :])
                cmf = self.cload(e2, "cmb", [128, 16, 2, 128], g2)
                S.copy("pool", cmbb[:, :, :, :], cmf[:, :, :, :])
                cbsf = self.cload(e2, "cbs", [128, 4, 128], g2)
                cbwf = self.cload(e2, "cbw", [128, 8, 128], g2)
                for h in range(4):
                    S.copy("dve", cbsb[:, :, h, :], cbsf[:, :, :])
                    S.copy("pool", cbwb[:, :, h, :], cbwf[:, :, :])
                S.flush()
            Pc = [self.sb(es, f"Pc{i}", [128, 512], BF16) for i in range(4)]
            Pk = [self.sb(es, f"Pk{i}", [128, 512], BF16) for i in range(3)]
            cmrep = [self.sb(es, f"cmrep{i}", [128, 4, 128], BF16) for i in range(2)]
            ob = {nm: self.sb(es, "ob_" + nm, [128, 4, 129]) for nm in ("c", "s", "w")}
            rs = self.sb(es, "rs", [128, 12])
            cf = self.sb(es, "cf", [128, 12])
            imp = self.sb(es, "imp", [128, 128])
            imp2 = self.sb(es, "imp2", [128, 128])
            m8 = self.sb(es, "m8", [128, 16])
            selm = self.sb(es, "selm", [128, 128])
            biasT = self.sb(es, "biasT", [128, 4, 128], BF16)
            mixn = [self.sb(es, f"mixn{i}", [128, 4, 128], BF16) for i in range(2)]
            tmpn = self.sb(es, "tmpn", [128, 128])
            mnT = [self.sb(es, f"mnT{i}", [128, 4, 128], BF16) for i in range(2)]
            g_mn = [S.group(f"mnT{i}") for i in range(2)]
            bS = [pb[0], pb[1]]
            bO = [pb[2], pb[3]]
            bI, bT = pb[4], pb[5]
            si = [0]

            def nsa_part1(s_, Q, qv, cmp_plan, addv):
                NQ = 4 * Q

                ncp = len(cmp_plan)
                for i_, (nt, mk) in enumerate(cmp_plan):
                    bank = bS[i_ % 2]
                    S.mm(bank.f(0, NQ), lhsT=kcmpT[:, nt * 128:(nt + 1) * 128], rhs=qv, start=True, stop=(mk is None))
                    if mk is not None:
                        S.mm(bank.f(0, NQ), lhsT=idb[:, :], rhs=mk, start=False, stop=True)
                    S.act(Pc[i_][:, 0:NQ], bank.f(0, NQ), AF.Exp, scale=SC)
                for h in range(4):
                    for i_, (nt, mk) in enumerate(cmp_plan):
                        S.mm_(bOc[h // 2].f((h % 2) * 129, (h % 2) * 129 + 129, 0, Q), lhsT=Pc[i_][:, h * Q:(h + 1) * Q],
                              rhs=vcx[:, nt, :], start=(i_ == 0 and h % 2 == 0), stop=(i_ == ncp - 1), skip=True)
                for h in range(4):
                    for i_, (nt, mk) in enumerate(cmp_plan):
                        S.mm_(bI.f(h * 128, (h + 1) * 128, 0, Q), lhsT=Pc[i_][:, h * Q:(h + 1) * Q], rhs=ovb[:, nt, :],
                              start=(i_ == 0 and h == 0), stop=(i_ == ncp - 1), skip=True)
                for hh in range(2):
                    S.copy("act", obc[s_][0:Q, 2 * hh:2 * hh + 2, :], bOc[hh].f(0, 258, 0, Q).re("p (h c) -> p h c", h=2))
                S.ts("dve", rsc[s_][0:Q, 0:4], obc[s_][0:Q, :, 128], 1e-30, None, ALU.max)
                S.recip(rsc[s_][0:Q, 0:4], rsc[s_][0:Q, 0:4])
                S.ts("dve", imp[0:Q, :], bI.f(0, 128, 0, Q), rsc[s_][0:Q, 0:1], None, ALU.mult)
                for h in range(1, 4):
                    S.stt(imp[0:Q, :], bI.f(h * 128, (h + 1) * 128, 0, Q), rsc[s_][0:Q, h:h + 1], imp[0:Q, :], ALU.mult, ALU.add)
                S.tt("pool", imp[0:Q, :], imp[0:Q, :], addv, ALU.add)
                self.max8(m8[0:Q, 0:8], imp[0:Q, :])
                self.match_replace(imp2[0:Q, :], m8[0:Q, 0:8], imp[0:Q, :], -3.0e38)
                self.max8(m8[0:Q, 8:16], imp2[0:Q, :])
                S.ts("dve", selm[0:Q, :], imp[0:Q, :], m8[0:Q, 15:16], None, ALU.is_ge)
                S.mm(bT.f(0, Q), lhsT=selm[0:Q, :], rhs=self.idf[0:Q, 0:Q])
                S.ts("dve", bT2s[s_][:, 0:NQ].re("p (h q) -> p h q", h=4),
                     V(bT.t[:, 0:Q].unsqueeze(1).broadcast_to([128, 4, Q]), bT.q[0:1]), BIG, -BIG, ALU.mult, ALU.add)

            def nsa_part2(s_, Q, qv, slc_plan, win_plan, gatev, store):
                NQ = 4 * Q

                def scores(kT_tile, extra):
                    bank = bS[si[0] % 2]
                    out_ = Pk[si[0] % 3]
                    si[0] += 1
                    S.mm(bank.f(0, NQ), lhsT=kT_tile, rhs=qv, start=True, stop=(len(extra) == 0))
                    for i_, (l_, r_) in enumerate(extra):
                        S.mm(bank.f(0, NQ), lhsT=l_, rhs=r_, start=False, stop=(i_ == len(extra) - 1))
                    S.act(out_[:, 0:NQ], bank.f(0, NQ), AF.Exp, scale=SC)
                    return out_

                def pv(P_, vx, first, last):
                    for h in range(4):
                        S.mm_(bO[h // 2].f((h % 2) * 129, (h % 2) * 129 + 129, 0, Q), lhsT=P_[:, h * Q:(h + 1) * Q],
                              rhs=vx, start=(first and h % 2 == 0), stop=last, skip=True)

                def evac_o(dst):
                    for hh in range(2):
                        S.copy("act", dst[0:Q, 2 * hh:2 * hh + 2, :], bO[hh].f(0, 258, 0, Q).re("p (h c) -> p h c", h=2))

                brhs = bT2s[s_][:, 0:NQ]
                S.copy("dve", rs[0:Q, 0:4], rsc[s_][0:Q, 0:4])
                prev = None
                for i_, (kt, extra) in enumerate(slc_plan):
                    ex = [(eall[:, kt * 128:(kt + 1) * 128], brhs)] + extra
                    P_ = scores(kslT[:, kt * 128:(kt + 1) * 128], ex)
                    if prev is not None:
                        pv(prev[0], vslx[:, prev[1], :], prev[2] == 0, False)
                    prev = (P_, kt, i_)
                pv(prev[0], vslx[:, prev[1], :], prev[2] == 0, True)
                evac_o(ob["s"])
                prev = None
                for i_, (kt, mk) in enumerate(win_plan):
                    P_ = scores(kwnT[:, kt * 128:(kt + 1) * 128], [(idb[:, :], mk)])
                    if prev is not None:
                        pv(prev[0], vwnx[:, prev[1], :], prev[2] == 0, False)
                    prev = (P_, kt, i_)
                pv(prev[0], vwnx[:, prev[1], :], prev[2] == 0, True)
                evac_o(ob["w"])
                S.ts("dve", rs[0:Q, 4:8], ob["s"][0:Q, :, 128], 1e-30, None, ALU.max)
                S.ts("dve", rs[0:Q, 8:12], ob["w"][0:Q, :, 128], 1e-30, None, ALU.max)
                S.recip(rs[0:Q, 4:12], rs[0:Q, 4:12])
                g3 = gatev.re("p (h g) -> p g h", g=3)
                S.tt("dve", cf[0:Q, :].re("p (g h) -> p g h", g=3), rs[0:Q, :].re("p (g h) -> p g h", g=3), g3, ALU.mult)
                mx = mixn[si[0] % 2]
                for h in range(4):
                    S.ts("pool", tmpn[0:Q, :], obc[s_][0:Q, h, 0:128], cf[0:Q, h:h + 1], 1.0, ALU.mult, ALU.mult)
                    S.stt(tmpn[0:Q, :], ob["s"][0:Q, h, 0:128], cf[0:Q, 4 + h:5 + h], tmpn[0:Q, :], ALU.mult, ALU.add)
                    S.stt(mx[0:Q, h, :], ob["w"][0:Q, h, 0:128], cf[0:Q, 8 + h:9 + h], tmpn[0:Q, :], ALU.mult, ALU.add)
                for h in range(4):
                    S.tr(bT.b(h * Q, (h + 1) * Q), mx[0:Q, h, :], idb[0:Q, 0:Q])
                store(bT.b(0, 4 * Q))

            bT2s = [self.sb(es, f"bT2_{i}", [128, 512], BF16) for i in range(2)]
            obc = [self.sb(es, f"obc{i}", [128, 4, 129]) for i in range(2)]
            rsc = [self.sb(es, f"rsc{i}", [128, 4]) for i in range(2)]
            bOc = [pb[6], pb[7]]
            plans = []
            for j in range(16):
                qv = qnT[:, :, j * 128:(j + 1) * 128]
                nt_hi = min(3, (32 * j + 30) // 128)
                lo = max(0, 32 * j - 1) // 128
                cmp_plan = []
                for nt in range(nt_hi + 1):
                    mk = None
                    if nt >= lo:
                        mk = nt - lo
                    cmp_plan.append((nt, mk))
                slc_plan = []
                for kt in range(4 * j + 4):
                    ex = []
                    if kt >= 4 * j:
                        ex.append((idb[:, :], cbsb[:, kt - 4 * j, :, :].re("p h q -> p (h q)")))
                    slc_plan.append((kt, ex))
                win_plan = [(4 * j - 4 + e, cbwb[:, e, :, :].re("p h q -> p (h q)")) for e in range(8) if 4 * j - 4 + e >= 0]

                def store(src, j=j):
                    S.copy("act", mnT[j % 2][:, :, :].re("p c t -> p (c t)"), src)
                    S.dma(self.mix_d[:, 4:8, j * 128:(j + 1) * 128], mnT[j % 2][:, :, :], g_mn[j % 2])

                plans.append((qv, cmp_plan, slc_plan, win_plan, store))
            def emit1(j):
                qv, cmp_plan, _, _, _ = plans[j]
                cp = []
                for nt, slot in cmp_plan:
                    mk = None
                    if slot is not None:
                        S.copy("pool", cmrep[slot][:, :, :],
                               V(cmbb.t[:, j, slot, :].unsqueeze(1).broadcast_to([128, 4, 128]), [cmbb.b]))
                        mk = cmrep[slot][:, :, :].re("p h q -> p (h q)")
                    cp.append((nt, mk))
                nsa_part1(j % 2, 128, qv, cp, addm[:, j, :])

            def emit2(j):
                qv, _, slc_plan, win_plan, store = plans[j]
                nsa_part2(j % 2, 128, qv, slc_plan, win_plan, gates[:, j, :], store)

            with ExitStack() as ew:
                wps = [self.sb(ew, f"wps{i}", [128, 1408]) for i in range(2)]
                wpb = [self.sb(ew, f"wpb{i}", [128, 1408], BF16) for i in range(2)]
                g_ws = [S.group(f"wps{i}") for i in range(2)]
                g_wb = [S.group(f"wpb{i}") for i in range(2)]

                def wload(n):
                    k, q = n // 4, n % 4
                    S.dma(wps[n % 2][:, :], self.w_up[k * 128:(k + 1) * 128, q * 1408:(q + 1) * 1408], g_ws[n % 2])

                def wconv(n):
                    k, q = n // 4, n % 4
                    S.copy("pool", wpb[n % 2][:, :], wps[n % 2][:, :])
                    S.dma(self.wup_d[:, 11 * q:11 * q + 11, k, :], wpb[n % 2][:, :].re("p (c n) -> p c n", n=128),
                          g_wb[n % 2])
                    if n + 2 < 32:
                        wload(n + 2)

                wload(0)
                wload(1)

                def extra(j):
                    wconv(2 * j)
                    wconv(2 * j + 1)

                self._p3_emit(emit1, emit2, extra=extra)
                S.flush()
            self.nsa_halo(es, (nsa_part1, nsa_part2), keep, idb)
            S.flush()

    @staticmethod
    def _p3_emit(emit1, emit2, n=16, extra=None):
        emit1(0)
        for j in range(n):
            if j + 1 < n:
                emit1(j + 1)
            if extra is not None:
                extra(j)
            emit2(j)

    def nsa_halo(self, es, nsa_parts, keep, idb):
        S = self.S
        with ExitStack() as e2:
            g2 = S.group("const3h", full=True)
            haddm = self.cload(e2, "haddm", [32, 128], g2)
            hcmb = self.sb(e2, "hcmb", [128, 4, 128], BF16)
            hsmb = self.sb(e2, "hsmb", [128, 64, 128], BF16)
            hwmb = self.sb(e2, "hwmb", [128, 64, 128], BF16)
            hcf = self.cload(e2, "hcm", [128, 4, 128], g2)
            S.copy("dve", hcmb[:, :, :], hcf[:, :, :])
            st = [self.sb(e2, f"hst{i}", [128, 8, 128]) for i in range(2)]
            g_s = [S.group(f"hst{i}") for i in range(2)]
            n = 0
            for nm, dst in (("hsm", hsmb), ("hwm", hwmb)):
                src = self.inp(nm, [128, 64, 128])
                for i in range(8):
                    S.dma(st[n % 2][:, :, :], src[:, i * 8:(i + 1) * 8, :], g_s[n % 2])
                    S.copy("pool" if n % 2 else "dve", dst[:, i * 8:(i + 1) * 8, :], st[n % 2][:, :, :])
                    n += 1
            mnTh = self.sb(e2, "mnTh", [128, 4, 32], BF16)
            g_m = S.group("mnTh")
            cmp_plan = [(nt, hcmb[:, nt, :]) for nt in range(4)]
            slc_plan = [(kt, [(idb[:, :], hsmb[:, kt, :])]) for kt in range(NB)]
            win_plan = [(kt, hwmb[:, kt, :]) for kt in range(NB)]

            def store(src):
                S.copy("act", mnTh[:, :, :].re("p c t -> p (c t)"), src)
                S.dma(self.mixh_d[:, 4:8, :], mnTh[:, :, :], g_m)

            part1, part2 = nsa_parts
            part1(0, 32, keep["qnTh"][:, :, :], cmp_plan, haddm[:, :])
            part2(0, 32, keep["qnTh"][:, :, :], slc_plan, win_plan, keep["gatesh"][:, :], store)
            S.flush()

    def max8(self, out, in_):
        o_, i_ = out.ap, in_.ap
        self.S.op("dve", lambda e: e.max(out=o_, in_=i_), r=self.S._bs(in_), w=self.S._bs(out))

    def match_replace(self, out, rep, vals, imm):
        o_, r_, v_ = out.ap, rep.ap, vals.ap
        self.S.op("dve", lambda e: e.match_replace(out=o_, in_to_replace=r_, in_values=v_, imm_value=imm),
                  r=self.S._bs(rep, vals), w=self.S._bs(out))

    def phase45(self):
        S = self.S
        pb = self.pb
        idb = self.idb
        w_out = self.inp("w_out", [D, D])
        w_dn = self.inp("w_dn", [DFF, D])
        wup_d = self.wup_d
        with ExitStack() as es:
            gc_ = S.group("const4", full=True)
            fcw = self.cload(es, "fcw", [128, 44 * 3], gc_)
            fcb = self.cload(es, "fcb", [128, 44], gc_)
            wdnb = self.sb(es, "wdnb", [128, 22, D], BF16)
            woutb = self.sb(es, "woutb", [128, 8, D], BF16)
            g1row = self.sb(es, "g1row", [128, D])
            g2row = self.sb(es, "g2row", [128, D])
            with ExitStack() as e2:
                stg = [self.sb(e2, f"wst4_{i}", [128, 2, D]) for i in range(2)]
                g_s = [S.group(f"wst4_{i}") for i in range(2)]
                n = 0
                for (src, dst, nch) in ((w_dn, wdnb, 22), (w_out, woutb, 8)):
                    sv = src.rearrange("(c p) d -> p c d", p=128)
                    for c0 in range(0, nch, 2):
                        st = stg[n % 2]
                        S.dma(st[:, :, :], sv[:, c0:c0 + 2, :], g_s[n % 2])
                        S.copy(("dve", "pool", "act")[n % 3], dst[:, c0:c0 + 2, :], st[:, :, :])
                        n += 1
                gb = self.sb(e2, "gbc", [128, 128])
                for gi, (col0, dst) in enumerate(((16, g1row), (40, g2row))):
                    for cc in range(8):
                        S.copy("dve", gb[:, :], V(self.modT.t[:, col0 + cc:col0 + cc + 1].broadcast_to([128, 128]),
                                                  [self.modT.b]))
                        bank = pb[cc % 2]
                        S.mm(bank.f(0, 128), lhsT=gb[:, :], rhs=self.idf[:, :])
                        S.copy("act", dst[:, cc * 128:(cc + 1) * 128], bank.f(0, 128))
                S.flush()
            F = dict(ssq=self.sb(es, "f4_ssq", [128, 4]), rt=self.sb(es, "f4_rt", [128, 4]),
                     rstd=self.sb(es, "f4_rstd", [128, 4]), junk=self.sb(es, "f4_junk", [128, 1024], BF16),
                     xs=self.sb(es, "f4_xs", [128, 2, 1024], BF16))
            x1 = [self.sb(es, f"x1_{i}", [128, 4, D]) for i in range(2)]
            g_x = [S.group(f"x1_{i}") for i in range(2)]
            mixt = self.sb(es, "mixt", [128, 8, 512], BF16)
            g_m = S.group("mixt")
            h2T = self.sb(es, "h2T", [128, 8, 512], BF16)
            gT = self.sb(es, "gT", [128, 22, 512], BF16)
            wch = [self.sb(es, f"wch{i}", [128, 2, 8, 128], BF16) for i in range(3)]
            g_wc = [S.group(f"wch{i}") for i in range(3)]
            upa = [self.sb(es, f"upa{i}", [128, 4, 130]) for i in range(2)]
            upb = [self.sb(es, f"upb{i}", [128, 4, 130]) for i in range(2)]
            for u_ in upa + upb:
                S.memset("pool", u_[:, :, :], 0.0)
            aa = [self.sb(es, f"aa{i}", [128, 4, 128]) for i in range(2)]
            ab_ = [self.sb(es, f"ab_{i}", [128, 4, 128]) for i in range(2)]
            tmp = [self.sb(es, f"tmp4_{i}", [128, 512]) for i in range(2)]
            xrows = self.xo.rearrange("(t b p) d -> t p b d", b=4, p=128)
            orows = self.out.rearrange("(t b p) d -> t p b d", b=4, p=128)
            wi = 0
            gh = S.group("p4halo", full=True)
            hexr = self.cload(es, "hexr", [128, 32], gh)
            x1h = self.sb(es, "x1h", [32, D])
            mixh = self.sb(es, "mixh", [128, 8, 32], BF16)
            S.dma(x1h[:, :], self.din["xh"][:, :], gh)
            S.dma(mixh[:, :, :], self.mixh_d[:, :, :], gh)
            h2Th = self.sb(es, "h2Th", [128, 8, 32], BF16)
            for half in range(2):
                bank = pb[2 + half]
                hs = slice(half * 512, (half + 1) * 512)
                for c in range(8):
                    S.mm(bank.f(0, 512, 0, 32), lhsT=mixh[:, c, :], rhs=woutb[:, c, hs], start=(c == 0), stop=(c == 7))
                S.tt("dve", tmp[half][0:32, :], bank.f(0, 512, 0, 32), g1row[0:32, hs], ALU.mult)
                S.tt("pool", x1h[:, hs], x1h[:, hs], tmp[half][0:32, :], ALU.add)
            self.front32(F, x1h, h2Th, self.s2, 24)

            def conv(u_, c, dst):
                S.ts("pool", dst[:, :, :], u_[:, :, 2:130], fcw[:, c * 3 + 2:c * 3 + 3], fcb[:, c:c + 1], ALU.mult, ALU.add)
                S.stt(dst[:, :, :], u_[:, :, 1:129], fcw[:, c * 3 + 1:c * 3 + 2], dst[:, :, :], ALU.mult, ALU.add)
                S.stt(dst[:, :, :], u_[:, :, 0:128], fcw[:, c * 3:c * 3 + 1], dst[:, :, :], ALU.mult, ALU.add)

            for t in range(4):
                xx = x1[t % 2]
                S.dma(xx[:, :, :], xrows[t], g_x[t % 2])
                S.dma(mixt[:, :, :], self.mix_d[:, :, t * 512:(t + 1) * 512], g_m)
                for blk in range(4):
                    cs = slice(blk * 128, (blk + 1) * 128)
                    for half in range(2):
                        bank = pb[2 + half]
                        hs = slice(half * 512, (half + 1) * 512)
                        for c in range(8):
                            S.mm(bank.f(0, 512), lhsT=mixt[:, c, cs], rhs=woutb[:, c, hs], start=(c == 0), stop=(c == 7))
                        tm = tmp[half]
                        S.tt("dve", tm[:, :], bank.f(0, 512), g1row[:, hs], ALU.mult)
                        S.tt("pool", xx[:, blk, hs], xx[:, blk, hs], tm[:, :], ALU.add)
                if self.debug:
                    S.dma(self.dx1[t], xx[:, :, :], self.g_out)
                for hf in range(2):
                    self.front(F, xx, 2, h2T, hf * 256, self.s2, 24, b0=2 * hf)
                for c in range(22):
                    w_ = wch[wi % 3]
                    S.dma(w_[:, 0, :, :], wup_d[:, c, :, :], g_wc[wi % 3])
                    S.dma(w_[:, 1, :, :], wup_d[:, 22 + c, :, :], g_wc[wi % 3])
                    wi += 1
                    ua, ub = upa[c % 2], upb[c % 2]
                    for i_, (u_, bank) in enumerate(((ua, pb[4]), (ub, pb[5]))):
                        for k in range(8):
                            S.mm(bank.f(0, 512), lhsT=w_[:, i_, k, :], rhs=h2T[:, k, :], start=(k == 0), stop=(k == 7))
                        S.copy("act", u_[:, :, 2:130], bank.f(0, 512).re("p (b t) -> p b t", b=4))
                        bh_ = pb[6 + i_]
                        for k in range(8):
                            S.mm(bh_.f(0, 8), lhsT=w_[:, i_, k, :], rhs=h2Th[:, k, 8 * t:8 * t + 8], start=(k == 0),
                                 stop=(k == 7))
                        S.tt("dve", u_[:, :, 0:2], bh_.f(0, 8).re("p (b i) -> p b i", i=2),
                             hexr[:, 8 * t:8 * t + 8].re("p (b i) -> p b i", i=2), ALU.mult)
                    conv(ua, c, aa[c % 2])
                    conv(ub, 22 + c, ab_[c % 2])
                    S.act(aa[c % 2][:, :, :], aa[c % 2][:, :, :], AF.Silu)
                    S.tt("dve", gT[:, c, :].re("p (b t) -> p b t", b=4), aa[c % 2][:, :, :], ab_[c % 2][:, :, :], ALU.mult)
                for blk in range(4):
                    cs = slice(blk * 128, (blk + 1) * 128)
                    for half in range(2):
                        bank = pb[6 + half]
                        hs = slice(half * 512, (half + 1) * 512)
                        for c in range(22):
                            S.mm(bank.f(0, 512), lhsT=gT[:, c, cs], rhs=wdnb[:, c, hs], start=(c == 0), stop=(c == 21))
                        tm = tmp[half]
                        S.tt("dve", tm[:, :], bank.f(0, 512), g2row[:, hs], ALU.mult)
                        S.tt("pool", xx[:, blk, hs], xx[:, blk, hs], tm[:, :], ALU.add)
                S.dma(orows[t], xx[:, :, :], g_x[t % 2])
            S.flush()


def _colL(v, n):
    return np.ascontiguousarray(np.asarray(v, np.float32).reshape(n, 128).T)


def _rep(v, n=128):
    v = np.asarray(v, np.float32).reshape(1, -1)
    return np.ascontiguousarray(np.repeat(v, n, axis=0))


def _consts():
    p = np.arange(128)
    same = (p[:, None] // 64) == (p[None, :] // 64)
    c = {}
    c["identf"] = np.eye(128, dtype=np.float32)
    c["tri2"] = (same & (p[:, None] <= p[None, :])).astype(np.float32)
    c["blk2"] = same.astype(np.float32)
    c["onesf"] = np.ones((128, 128), np.float32)
    c["cind"] = np.stack([(p < 64), (p >= 64)], axis=1).astype(np.float32)
    ma = np.where(same & (p[None, :] < p[:, None]), 0.0, BIG).astype(np.float32)
    mb = np.where(same & (p[None, :] >= p[:, None]), 0.0, -BIG).astype(np.float32)
    c["ma4"] = np.ascontiguousarray(np.tile(ma, (1, 4)))
    c["mb4"] = np.ascontiguousarray(np.tile(mb, (1, 4)))
    return c


def _host_inputs(inputs):
    x = np.asarray(inputs["x"], np.float32)
    cst = _consts()
    g = lambda k: np.asarray(inputs[k][0], np.float32)
    gcw = g("gdn_conv_w")
    gcwT = np.ascontiguousarray(gcw.reshape(4, 12, 128).transpose(2, 1, 0).reshape(128, 48))
    shared = {
        "ada_w": np.ascontiguousarray(g("ada_w")),
        "ada_bT": _colL(g("ada_b"), 48),
        "n1w": _colL(g("norm1_w"), 8),
        "n2w": _colL(g("norm2_w"), 8),
        "w_in": np.ascontiguousarray(g("w_in")),
        "gcw": gcwT,
        "dtb": _rep(g("gdn_dt_bias")),
        "alog": _rep(g("gdn_A_log")),
        "kslw": _colL(g("nsa_k_norm_slc"), 1),
        "kwnw": _colL(g("nsa_k_norm_win"), 1),
        "qnw": _colL(g("nsa_q_norm_w"), 1),
        "gonw4": _rep(np.tile(g("gdn_out_norm_w"), 4)),
        "onesf2": np.ones((128, 128), np.float32),
    }
    for nm in ("k", "v"):
        shared[f"cmp_{nm}_w1"] = np.ascontiguousarray(g(f"cmp_{nm}_w1").reshape(32, 128, 128).transpose(1, 0, 2))
        shared[f"cmp_{nm}_w2"] = np.ascontiguousarray(g(f"cmp_{nm}_w2"))
        shared[f"cmp_{nm}_posT"] = np.ascontiguousarray(g(f"cmp_{nm}_pos").T)
    shared["kcw"] = _colL(g("nsa_k_norm_cmp"), 1)
    shared["onesf3"] = np.ones((128, 128), np.float32)
    pm = np.ones((128, 1), np.float32)
    pm[127, 0] = 0.0
    shared["pm511"] = pm
    keys = np.arange(S_LEN)
    shared["eall"] = (keys[None, :] // 64 == np.arange(128)[:, None]).astype(np.float32)
    n = np.arange(512)
    js = np.arange(128)
    ov = np.minimum(16 * n[:, None] + 32, 64 * js[None, :] + 64) - np.maximum(16 * n[:, None], 64 * js[None, :])
    ov = np.clip(ov, 0, None).astype(np.float32) / 32.0
    ov[511] = 0.0
    shared["ovm"] = np.ascontiguousarray(ov.reshape(4, 128, 128).transpose(1, 0, 2))
    fw = g("ffn_conv_w")
    shared["fcw"] = np.ascontiguousarray(fw.reshape(3, 44, 128).transpose(2, 1, 0).reshape(128, 132))
    shared["fcb"] = _colL(g("ffn_conv_b"), 44)
    shared["w_out"] = np.ascontiguousarray(g("w_out"))
    shared["w_up"] = np.ascontiguousarray(g("ffn_w_up"))
    shared["w_dn"] = np.ascontiguousarray(g("ffn_w_down"))
    shared.update(cst)
    maps = []
    for core in range(8):
        b, r = core // 4, core % 4
        xo = np.concatenate([x[b, 128 * (4 * j + r):128 * (4 * j + r) + 128] for j in range(16)], axis=0)
        m = dict(shared)
        m["xb"] = np.ascontiguousarray(x[b])
        m["xo"] = np.ascontiguousarray(xo)
        m["cT"] = _colL(inputs["c"][b], 8)
        sel = np.zeros((128, 4), np.float32)
        sel[:, r] = 1.0
        m["sel"] = sel
        p = np.arange(128)
        q = np.arange(128)
        addm = np.zeros((128, 16, 128), np.float32)
        cmb = np.zeros((128, 16, 2, 128), np.float32)
        for j in range(16):
            qi = 4 * j + r
            tq = 128 * qi + q
            cur = tq // 64
            jj = np.arange(128)[None, :]
            valid = jj <= cur[:, None]
            forced = (jj == 0) | (jj == cur[:, None]) | (jj == cur[:, None] - 1)
            addm[:, j, :] = np.where(valid, np.where(forced, 1.0e4, 0.0), -1.0e30)
            lo = max(0, 32 * j - 1) // 128
            for slot in range(2):
                nn = 128 * (lo + slot) + p
                ok = (16 * nn[:, None] + 31) <= tq[None, :]
                cmb[:, j, slot, :] = np.where(ok, 0.0, -BIG)
        m["addm"] = addm
        m["cmb"] = cmb
        cbs = np.zeros((128, 4, 128), np.float32)
        for d in range(4):
            ok = (128 * (d - r) + p[:, None]) <= q[None, :]
            cbs[:, d, :] = np.where(ok, 0.0, -BIG)
        m["cbs"] = cbs
        cbw = np.zeros((128, 8, 128), np.float32)
        for e in range(8):
            rel = 128 * (e - 4 - r) + p[:, None]
            ok = (rel <= q[None, :]) & (rel > q[None, :] - 512)
            cbw[:, e, :] = np.where(ok, 0.0, -BIG)
        m["cbw"] = cbw
        hs_ = np.zeros((128, 4), np.float32)
        hs_[:, (r - 1) % 4] = 1.0
        m["hsel"] = hs_
        sab = np.zeros((128, 2), np.float32)
        sab[:, 0] = 1.0 if r >= 1 else 0.0
        sab[:, 1] = 1.0 if r == 0 else 0.0
        m["selAB"] = sab
        tq = np.array([128 * (4 * j + r) - 2 + i for j in range(16) for i in range(2)])
        ex = tq >= 0
        xh = np.zeros((32, D), np.float32)
        xh[ex] = x[b, tq[ex]]
        m["xh"] = xh
        m["hexr"] = _rep(ex.astype(np.float32))
        cur = tq // 64
        jj = np.arange(128)[None, :]
        valid = (jj <= cur[:, None]) & ex[:, None]
        forced = (jj == 0) | (jj == cur[:, None]) | (jj == cur[:, None] - 1)
        m["haddm"] = np.where(valid, np.where(forced, 1.0e4, 0.0), -1.0e30).astype(np.float32)
        nn = np.arange(512)
        okc = ((16 * nn[:, None] + 31) <= tq[None, :]) & ex[None, :]
        hcm = np.where(okc, 0.0, -BIG).astype(np.float32).reshape(4, 128, 1, 32)
        m["hcm"] = np.ascontiguousarray(np.broadcast_to(hcm, (4, 128, 4, 32)).transpose(1, 0, 2, 3).reshape(128, 4, 128))
        pos = np.arange(S_LEN)
        oks = (pos[:, None] <= tq[None, :]) & ex[None, :]
        okw = oks & (pos[:, None] > tq[None, :] - 512)
        for nm, ok in (("hsm", oks), ("hwm", okw)):
            a = np.where(ok, 0.0, -BIG).astype(np.float32).reshape(64, 128, 1, 32)
            m[nm] = np.ascontiguousarray(np.broadcast_to(a, (64, 128, 4, 32)).transpose(1, 0, 2, 3).reshape(128, 64, 128))
        maps.append(m)
    return maps


def run(inputs, debug=False, upto=9, ntiles=NT):
    bld = Builder(debug, ntiles)
    nc = bld.build(upto)
    maps = _host_inputs(inputs)
    maps = [{k: v for k, v in m.items() if k in bld.din} for m in maps]
    missing = [k for k in bld.din if k not in maps[0]]
    assert not missing, missing
    res = run_bass_kernel_spmd(nc, maps, core_ids=list(range(8)))
    return res.results


def kernel(**inputs):
    results = run(inputs)
    outp = np.zeros((2, S_LEN, D), np.float32)
    for core in range(8):
        b, r = core // 4, core % 4
        o = results[core]["out"]
        for j in range(16):
            qi = 4 * j + r
            outp[b, 128 * qi:128 * qi + 128] = o[128 * j:128 * j + 128]
    return outp
```
